# Optimizing a Trainium2 kernel written in Bass

```python
import math
import jax, jax.numpy as jnp
from jax import lax
import numpy as np

D_MODEL = 1024
BATCH = 8
SEQ = 2048
DEPTH = 1
DEC_BATCH = 128
DEC_SEQ = 1
PAST_LEN = 16384
PAGE_SIZE = 128

RW_HEADS = 8
RW_HEAD_DIM = 64
RW_WIDTH = RW_HEADS * RW_HEAD_DIM
LORA_W = 64
LORA_A = 64
LORA_G = 128
SHIFT_WIDTH = 3 * RW_WIDTH + LORA_W + LORA_A + LORA_G
SSM_WIDTH = 512
SSM_GROUP = 16
SSM_GROUPS = SSM_WIDTH // SSM_GROUP
SSM_STATE = 64
IN_WIDTH = SHIFT_WIDTH + SSM_WIDTH + 2 * D_MODEL
D_FF = 2816
CONV_W = 3
PLE_DIM = 256
EPS = 1e-6
GN_EPS = 64e-5

kernel_name = 'rwkv7_s5_gated_hybrid_step'


def rmsnorm(x, g):
    xf = x.astype(jnp.float32)
    y = xf * lax.rsqrt(jnp.mean(xf * xf, axis=-1, keepdims=True) + EPS)
    return (y * g.astype(jnp.float32)).astype(x.dtype)


def wkv_recurrence(r, w, k, v, kk, a, s0):
    def step(S, inp):
        r_t, w_t, k_t, v_t, kk_t, a_t = inp
        sa = jnp.einsum('bhvk,bhk->bhv', S, -kk_t)
        S = (S * w_t[:, :, None, :]
             + sa[..., None] * (kk_t * a_t)[:, :, None, :]
             + v_t[..., None] * k_t[:, :, None, :])
        y_t = jnp.einsum('bhvk,bhk->bhv', S, r_t)
        return S, y_t
    xs = tuple(jnp.swapaxes(t.astype(jnp.float32), 0, 1) for t in (r, w, k, v, kk, a))
    S, ys = lax.scan(step, s0.astype(jnp.float32), xs)
    return jnp.swapaxes(ys, 0, 1), S


def s5_discretise(A_re, A_im, log_dt, B_re, B_im):
    dt = jnp.exp(log_dt.astype(jnp.float32))[:, None]
    lr, li = A_re.astype(jnp.float32), A_im.astype(jnp.float32)
    mag = jnp.exp(lr * dt)
    ar, ai = mag * jnp.cos(li * dt), mag * jnp.sin(li * dt)
    den = lr * lr + li * li
    fr = ((ar - 1.0) * lr + ai * li) / den
    fi = (ai * lr - (ar - 1.0) * li) / den
    Br, Bi = B_re.astype(jnp.float32), B_im.astype(jnp.float32)
    br = fr[..., None] * Br - fi[..., None] * Bi
    bi = fr[..., None] * Bi + fi[..., None] * Br
    return ar, ai, br, bi


def complex_affine_combine(e1, e2):
    a1r, a1i, b1r, b1i = e1
    a2r, a2i, b2r, b2i = e2
    return (a2r * a1r - a2i * a1i,
            a2r * a1i + a2i * a1r,
            a2r * b1r - a2i * b1i + b2r,
            a2r * b1i + a2i * b1r + b2i)


def layer(x, p, st_shift, st_wkv, st_re, st_im, st_conv, L):
    B, T, _ = x.shape
    f32 = jnp.float32
    h = rmsnorm(x, L['ln1_g'])
    proj = h @ L['w_in']
    p_rw = proj[..., :SHIFT_WIDTH]
    u = proj[..., SHIFT_WIDTH:SHIFT_WIDTH + SSM_WIDTH]
    gate_logits = proj[..., SHIFT_WIDTH + SSM_WIDTH:]

    prev = jnp.concatenate([st_shift[:, None].astype(p_rw.dtype), p_rw[:, :-1]], axis=1)
    xs = p_rw + (prev - p_rw) * L['mu_shift']
    new_shift = p_rw[:, -1]
    r, k, v, xw, xa, xg = jnp.split(
        xs, [RW_WIDTH, 2 * RW_WIDTH, 3 * RW_WIDTH, 3 * RW_WIDTH + LORA_W,
             3 * RW_WIDTH + LORA_W + LORA_A], axis=-1)
    wlog = -jax.nn.softplus(-(L['w0'] + jnp.tanh(xw) @ L['w2'])) - 0.5
    decay = jnp.exp(-jnp.exp(wlog.astype(f32)))
    a = jax.nn.sigmoid(L['a0'] + xa @ L['a2'])
    g = jax.nn.sigmoid(xg) @ L['g2']
    heads = lambda t: t.reshape(B, T, RW_HEADS, RW_HEAD_DIM)
    kk = heads(k * L['k_k']).astype(f32)
    kk = kk / jnp.maximum(jnp.sqrt(jnp.sum(kk * kk, axis=-1, keepdims=True)), 1e-12)
    k = k * (1.0 + (a - 1.0) * L['k_a'])
    rh, kh, vh, ah = heads(r), heads(k), heads(v), heads(a)
    y, S_new = wkv_recurrence(rh, heads(decay), kh, vh, kk, ah, st_wkv)
    mu_ = jnp.mean(y, axis=-1, keepdims=True)
    var = jnp.mean(jnp.square(y - mu_), axis=-1, keepdims=True)
    y = ((y - mu_) * lax.rsqrt(var + GN_EPS)).reshape(B, T, RW_WIDTH)
    y = y * L['lnx_g'].astype(f32) + L['lnx_b'].astype(f32)
    bonus = jnp.sum((rh * kh * L['r_k']).astype(f32), axis=-1, keepdims=True) * vh.astype(f32)
    y = ((y + bonus.reshape(B, T, RW_WIDTH)) * g.astype(f32)).astype(x.dtype)
    rw_out = y @ L['w_rw_out']

    ar, ai, br, bi = s5_discretise(L['A_re'], L['A_im'], L['log_dt'], L['B_re'], L['B_im'])
    ug = u.reshape(B, T, SSM_GROUPS, SSM_GROUP).astype(f32)
    bu_r = jnp.einsum('btgc,gnc->btgn', ug, br)
    bu_i = jnp.einsum('btgc,gnc->btgn', ug, bi)
    x0r, x0i = st_re.astype(f32), st_im.astype(f32)
    bu_r = bu_r.at[:, 0].add(ar * x0r - ai * x0i)
    bu_i = bu_i.at[:, 0].add(ar * x0i + ai * x0r)
    Ar = jnp.broadcast_to(ar, bu_r.shape)
    Ai = jnp.broadcast_to(ai, bu_i.shape)
    _, _, sr, si = lax.associative_scan(complex_affine_combine, (Ar, Ai, bu_r, bu_i), axis=1)
    yc = (jnp.einsum('gcn,btgn->btgc', L['C_re'].astype(f32), sr)
          - jnp.einsum('gcn,btgn->btgc', L['C_im'].astype(f32), si))
    ys = yc.reshape(B, T, SSM_WIDTH) + L['D_skip'].astype(f32) * u.astype(f32)
    z = jax.nn.gelu(ys).astype(x.dtype)
    zz = z @ L['w_glu']
    s5_out = zz[..., :D_MODEL] * jax.nn.sigmoid(zz[..., D_MODEL:])

    g_rw = jax.nn.sigmoid(gate_logits[..., :D_MODEL])
    g_s5 = jax.nn.sigmoid(gate_logits[..., D_MODEL:])
    x = x + ((g_rw * rw_out + g_s5 * s5_out) @ L['w_out']).astype(x.dtype)

    h2 = rmsnorm(x, L['ln2_g'])
    ab = h2 @ L['w_ffn_in']
    a_up, b_up = ab[..., :D_FF], ab[..., D_FF:]
    a_ext = jnp.concatenate([st_conv.astype(a_up.dtype), a_up], axis=1)
    cw = L['conv_w']
    a_conv = (cw[0] * a_ext[:, :T] + cw[1] * a_ext[:, 1:T + 1] + cw[2] * a_ext[:, 2:T + 2]
              + L['conv_b'])
    new_conv = a_ext[:, T:]
    x = x + ((jax.nn.gelu(a_conv) * b_up) @ L['w_ffn_out']).astype(x.dtype)

    pg = jax.nn.sigmoid(rmsnorm(x, L['ln3_g']) @ L['w_ple_gate'])
    x = x + (pg * (p @ L['w_ple'])).astype(x.dtype)
    return x, (new_shift, S_new, sr[:, -1], si[:, -1], new_conv)


def setup_inputs(seed: int = 0) -> dict:
    key = jax.random.key(seed)
    ks = iter(jax.random.split(key, 48))
    f32 = jnp.float32
    nrm = lambda shape, s: jax.random.normal(next(ks), shape, f32) * s
    uni = lambda shape, lo, hi: jax.random.uniform(next(ks), shape, f32, lo, hi)
    Ld = DEPTH
    n_idx = jnp.arange(SSM_STATE, dtype=f32)
    return {
        'x_prompt': nrm((BATCH, SEQ, D_MODEL), 1.0),
        'x_sample': nrm((DEC_BATCH, DEC_SEQ, D_MODEL), 1.0),
        'p_prompt': nrm((DEPTH, BATCH, SEQ, PLE_DIM), 1.0),
        'p_sample': nrm((DEPTH, DEC_BATCH, DEC_SEQ, PLE_DIM), 1.0),
        'state_shift': nrm((DEPTH, DEC_BATCH, SHIFT_WIDTH), 1.0),
        'state_wkv': nrm((DEPTH, DEC_BATCH, RW_HEADS, RW_HEAD_DIM, RW_HEAD_DIM), 0.5),
        'state_ssm_re': nrm((DEPTH, DEC_BATCH, SSM_GROUPS, SSM_STATE), 0.5),
        'state_ssm_im': nrm((DEPTH, DEC_BATCH, SSM_GROUPS, SSM_STATE), 0.5),
        'state_conv': nrm((DEPTH, DEC_BATCH, CONV_W - 1, D_FF), 1.0),
        'ln1_g': 1.0 + nrm((Ld, D_MODEL), 0.02),
        'w_in': nrm((Ld, D_MODEL, IN_WIDTH), D_MODEL ** -0.5),
        'mu_shift': uni((Ld, SHIFT_WIDTH), 0.0, 1.0),
        'w0': uni((Ld, RW_WIDTH), -6.0, 1.0),
        'w2': nrm((Ld, LORA_W, RW_WIDTH), 0.1 * LORA_W ** -0.5),
        'a0': nrm((Ld, RW_WIDTH), 0.1),
        'a2': nrm((Ld, LORA_A, RW_WIDTH), 0.1 * LORA_A ** -0.5),
        'g2': nrm((Ld, LORA_G, RW_WIDTH), LORA_G ** -0.5),
        'k_k': 0.85 + nrm((Ld, RW_WIDTH), 0.02),
        'k_a': 1.0 + nrm((Ld, RW_WIDTH), 0.02),
        'r_k': nrm((Ld, RW_HEADS, RW_HEAD_DIM), 0.1),
        'lnx_g': 1.0 + nrm((Ld, RW_WIDTH), 0.02),
        'lnx_b': nrm((Ld, RW_WIDTH), 0.02),
        'w_rw_out': nrm((Ld, RW_WIDTH, D_MODEL), RW_WIDTH ** -0.5),
        'A_re': -0.5 + nrm((Ld, SSM_GROUPS, SSM_STATE), 0.01),
        'A_im': jnp.pi * n_idx + nrm((Ld, SSM_GROUPS, SSM_STATE), 0.01),
        'log_dt': uni((Ld, SSM_GROUPS), math.log(1e-3), math.log(1e-1)),
        'B_re': nrm((Ld, SSM_GROUPS, SSM_STATE, SSM_GROUP), (2 * SSM_GROUP) ** -0.5),
        'B_im': nrm((Ld, SSM_GROUPS, SSM_STATE, SSM_GROUP), (2 * SSM_GROUP) ** -0.5),
        'C_re': nrm((Ld, SSM_GROUPS, SSM_GROUP, SSM_STATE), (2 * SSM_STATE) ** -0.5),
        'C_im': nrm((Ld, SSM_GROUPS, SSM_GROUP, SSM_STATE), (2 * SSM_STATE) ** -0.5),
        'D_skip': nrm((Ld, SSM_WIDTH), 1.0),
        'w_glu': nrm((Ld, SSM_WIDTH, 2 * D_MODEL), SSM_WIDTH ** -0.5),
        'w_out': nrm((Ld, D_MODEL, D_MODEL), D_MODEL ** -0.5),
        'ln2_g': 1.0 + nrm((Ld, D_MODEL), 0.02),
        'w_ffn_in': nrm((Ld, D_MODEL, 2 * D_FF), D_MODEL ** -0.5),
        'conv_w': nrm((Ld, CONV_W, D_FF), CONV_W ** -0.5),
        'conv_b': nrm((Ld, D_FF), 0.02),
        'w_ffn_out': nrm((Ld, D_FF, D_MODEL), D_FF ** -0.5),
        'ln3_g': 1.0 + nrm((Ld, D_MODEL), 0.02),
        'w_ple_gate': nrm((Ld, D_MODEL, D_MODEL), D_MODEL ** -0.5),
        'w_ple': nrm((Ld, PLE_DIM, D_MODEL), PLE_DIM ** -0.5),
        'final_g': 1.0 + nrm((D_MODEL,), 0.02),
    }


def reference(x_prompt, x_sample, p_prompt, p_sample, state_shift, state_wkv, state_ssm_re,
              state_ssm_im, state_conv, ln1_g, w_in, mu_shift, w0, w2, a0, a2, g2, k_k, k_a,
              r_k, lnx_g, lnx_b, w_rw_out, A_re, A_im, log_dt, B_re, B_im, C_re, C_im, D_skip,
              w_glu, w_out, ln2_g, w_ffn_in, conv_w, conv_b, w_ffn_out, ln3_g, w_ple_gate,
              w_ple, final_g):
    xp, xs = x_prompt, x_sample
    Bp = x_prompt.shape[0]
    pst = [[] for _ in range(5)]
    sst = [[] for _ in range(5)]
    for i in range(DEPTH):
        L = dict(ln1_g=ln1_g[i], w_in=w_in[i], mu_shift=mu_shift[i], w0=w0[i], w2=w2[i],
                 a0=a0[i], a2=a2[i], g2=g2[i], k_k=k_k[i], k_a=k_a[i], r_k=r_k[i],
                 lnx_g=lnx_g[i], lnx_b=lnx_b[i], w_rw_out=w_rw_out[i], A_re=A_re[i],
                 A_im=A_im[i], log_dt=log_dt[i], B_re=B_re[i], B_im=B_im[i], C_re=C_re[i],
                 C_im=C_im[i], D_skip=D_skip[i], w_glu=w_glu[i], w_out=w_out[i],
                 ln2_g=ln2_g[i], w_ffn_in=w_ffn_in[i], conv_w=conv_w[i], conv_b=conv_b[i],
                 w_ffn_out=w_ffn_out[i], ln3_g=ln3_g[i], w_ple_gate=w_ple_gate[i],
                 w_ple=w_ple[i])
        z_shift = jnp.zeros((Bp, SHIFT_WIDTH), xp.dtype)
        z_wkv = jnp.zeros((Bp, RW_HEADS, RW_HEAD_DIM, RW_HEAD_DIM), jnp.float32)
        z_ssm = jnp.zeros((Bp, SSM_GROUPS, SSM_STATE), jnp.float32)
        z_conv = jnp.zeros((Bp, CONV_W - 1, D_FF), xp.dtype)
        xp, sp = layer(xp, p_prompt[i], z_shift, z_wkv, z_ssm, z_ssm, z_conv, L)
        xs, ss = layer(xs, p_sample[i], state_shift[i], state_wkv[i], state_ssm_re[i],
                       state_ssm_im[i], state_conv[i], L)
        for j in range(5):
            pst[j].append(sp[j])
            sst[j].append(ss[j])
    y_prompt = rmsnorm(xp, final_g)
    y_sample = rmsnorm(xs, final_g)
    return (y_prompt, y_sample,
            jnp.stack(pst[0]), jnp.stack(pst[1]), jnp.stack(pst[2]), jnp.stack(pst[3]), jnp.stack(pst[4]),
            jnp.stack(sst[0]), jnp.stack(sst[1]), jnp.stack(sst[2]), jnp.stack(sst[3]), jnp.stack(sst[4]))
```

```python
import contextlib
import math
import numpy as np
import concourse.bass as bass
import concourse.mybir as mybir
from concourse.bass_utils import run_bass_kernel_spmd

F32 = mybir.dt.float32
BF16 = mybir.dt.bfloat16
AF = mybir.ActivationFunctionType
ALU = mybir.AluOpType
AX = mybir.AxisListType

T = 2048
NS = 16
NT = T + NS
D = 1024
CS = 8
C1 = math.exp(-0.5)
BLOCKS = [(0, 512), (512, 512), (1024, 512), (1536, 512), (2048, 16)]

ENGS = ("pe", "act", "dve", "pool", "sp")
NDSEM = 12


class _Op:
    __slots__ = ("eng", "fn", "deps", "dma", "observed", "tok", "idx", "dslot")

    def __init__(self, eng, fn, dma):
        self.eng, self.fn, self.dma = eng, fn, dma
        self.deps = set()
        self.observed = False
        self.tok = None
        self.dslot = None


class Sched:
    def __init__(self, nc):
        self.nc = nc
        self.ops = []
        self.last_w = {}
        self.readers = {}
        self.dma_rr = {e: 0 for e in ENGS}
        self.dma_prev = {}
        self.excl = set()

    def _add(self, eng, fn, reads, writes, dma):
        op = _Op(eng, fn, dma)
        op.idx = len(self.ops)
        if self.excl:
            ex = tuple(b for b in reads if b in self.excl)
            if ex:
                writes = tuple(writes) + ex
        for b in reads:
            w = self.last_w.get(b)
            if w is not None:
                op.deps.add(w)
        for b in writes:
            w = self.last_w.get(b)
            if w is not None:
                op.deps.add(w)
            for r in self.readers.get(b, ()):
                op.deps.add(r)
        if dma:
            slot = (eng, self.dma_rr[eng] % NDSEM)
            self.dma_rr[eng] += 1
            op.dslot = slot
            prev = self.dma_prev.get(slot)
            if prev is not None:
                op.deps.add(prev)
            self.dma_prev[slot] = op.idx
        op.deps.discard(op.idx)
        self.ops.append(op)
        for b in writes:
            self.last_w[b] = op.idx
            self.readers[b] = []
        for b in reads:
            if b not in writes:
                self.readers.setdefault(b, []).append(op.idx)
        return op.idx

    def emit(self):
        nc = self.nc
        ops = self.ops
        need = []
        for op in ops:
            nd = []
            for d in op.deps:
                p = ops[d]
                if (not p.dma) and (not op.dma) and p.eng == op.eng == "pe":
                    continue
                nd.append(d)
                p.observed = True
            need.append(nd)
        last = {}
        for op in ops:
            key = op.dslot if op.dma else op.eng
            last[key] = op.idx
        for i in last.values():
            ops[i].observed = True
        g = getattr(nc, "_gsem", None)
        if g is None:
            g = {"sems": {}, "cnt": {e: 0 for e in ENGS}, "dcnt": {}}
            nc._gsem = g
        cnt = g["cnt"]
        dcnt = g["dcnt"]
        for op in ops:
            if op.dma:
                dcnt[op.dslot] = dcnt.get(op.dslot, 0) + 16
                op.tok = (op.dslot, dcnt[op.dslot])
            elif op.observed:
                cnt[op.eng] += 1
                op.tok = (op.eng, cnt[op.eng])
        sems = g["sems"]
        for k in list(ENGS) + sorted(set(o.dslot for o in ops if o.dma)):
            if k not in sems:
                nm = k if isinstance(k, str) else "d_%s_%d" % k
                sems[k] = nc.alloc_semaphore(name="s_" + nm)
        with contextlib.ExitStack() as st:
            block = st.enter_context(nc.Block())
            per = {e: [o for o in ops if o.eng == e] for e in ENGS}
            hw = {"pe": block.tensor, "act": block.scalar, "dve": block.vector,
                  "pool": block.gpsimd, "sp": block.sync}

            def make(e):
                def body(eng):
                    seen = {}
                    for op in per[e]:
                        waits = {}
                        for d in need[op.idx]:
                            k, v = ops[d].tok
                            if v > waits.get(k, 0):
                                waits[k] = v
                        for k, v in waits.items():
                            if seen.get(k, 0) >= v:
                                continue
                            seen[k] = v
                            eng.wait_ge(sems[k], v)
                        ins = op.fn(eng)
                        if op.dma:
                            ins.then_inc(sems[op.tok[0]], 16)
                        elif op.observed:
                            ins.then_inc(sems[e], 1)
                    if e == "sp":
                        for key, i in last.items():
                            k, v = ops[i].tok
                            if seen.get(k, 0) < v:
                                eng.wait_ge(sems[k], v)
                return body

            for e in ENGS:
                hw[e](make(e))


def _L(x):
    if x is None:
        return ()
    if isinstance(x, str):
        return (x,)
    return tuple(x)


class Ph:
    _uid = [0]

    def __init__(self, nc, tag):
        self.nc = nc
        self.tag = tag
        self.st = contextlib.ExitStack()
        self.S = Sched(nc)

    def sb(self, name, shape, dt):
        return self.st.enter_context(self.nc.sbuf_tensor(self.tag + "_" + name, list(shape), dt))

    def ps(self, name, shape, dt):
        self.S.excl.add(name)
        return self.st.enter_context(self.nc.psum_tensor(self.tag + "_" + name, list(shape), dt))

    def finish(self):
        self.S.emit()
        self.st.close()

    def dbg(self, name, ap, shape, key, dt=F32):
        import os
        if os.environ.get("K_DBG_DUMP", "") == "":
            return
        t = self.nc.dram_tensor("dbg_" + name, list(shape), dt, kind="ExternalOutput").ap()
        self.dma("sp", t, ap, R=key)

    def op(self, eng, fn, R=None, W=None):
        self.S._add(eng, fn, _L(R), _L(W), False)

    def dma(self, q, out, in_, R=None, W=None, slow=False):
        if slow:
            fn = lambda e: e.dma_start(out=out, in_=in_, allow_slow_non_contiguous=True)
        else:
            fn = lambda e: e.dma_start(out=out, in_=in_)
        self.S._add(q, fn, _L(R), _L(W), True)

    def tt(self, eng, out, in0, in1, op, R=None, W=None):
        self.op(eng, lambda e: e.tensor_tensor(out=out, in0=in0, in1=in1, op=op), R, W)

    def ts(self, eng, out, in0, s1, op0, s2=None, op1=None, R=None, W=None):
        if op1 is None:
            self.op(eng, lambda e: e.tensor_scalar(out=out, in0=in0, scalar1=s1, scalar2=None, op0=op0), R, W)
        else:
            self.op(eng, lambda e: e.tensor_scalar(out=out, in0=in0, scalar1=s1, scalar2=s2, op0=op0, op1=op1), R, W)

    def stt(self, out, in0, scalar, in1, op0, op1, R=None, W=None):
        self.op("dve", lambda e: e.scalar_tensor_tensor(out=out, in0=in0, scalar=scalar, in1=in1, op0=op0, op1=op1), R, W)

    def act(self, out, in_, func, R=None, W=None, bias=None, scale=1.0, accum=None):
        kw = {}
        if bias is not None:
            kw["bias"] = bias
        if accum is not None:
            kw["accum_out"] = accum
        self.op("act", lambda e: e.activation(out=out, in_=in_, func=func, scale=scale, **kw), R, W)

    def cp(self, eng, out, in_, R=None, W=None):
        if eng == "act":
            self.op("act", lambda e: e.activation(out=out, in_=in_, func=AF.Copy), R, W)
        else:
            self.op(eng, lambda e: e.tensor_copy(out=out, in_=in_), R, W)

    def mm(self, out, lhsT, rhs, start, stop, R=None, W=None, tp=None):
        if tp is None:
            self.op("pe", lambda e: e.matmul(out, lhsT=lhsT, rhs=rhs, start=start, stop=stop), R, W)
        else:
            self.op("pe", lambda e: e.matmul(out, lhsT=lhsT, rhs=rhs, start=start, stop=stop, tile_position=tp), R, W)

    def tr(self, out, in_, ident, R=None, W=None):
        self.op("pe", lambda e: e.transpose(out, in_, ident), R, W)

    def memset(self, eng, ap, v, W=None):
        self.op(eng, lambda e: e.memset(ap, v), None, W)


def bc(ap, shape):
    return ap.to_broadcast(list(shape))


def rms_to_hT(ph, G, xt, P, gcol, hT, c0, tag):
    sq, ss, xn, pT = G["sq"], G["ss"], G["xn"], G["pT"]
    ph.act(sq[:P, :], xt[:P, :], AF.Square, R="xt" + tag, W=["sq", "ss"], accum=ss[:P, 0:1])
    ph.act(ss[:P, 1:2], ss[:P, 0:1], AF.Sqrt, R="ss", W="ss", bias=G["eps"][:P, 0:1], scale=1.0 / D)
    ph.op("dve", lambda e: e.reciprocal(out=ss[:P, 2:3], in_=ss[:P, 1:2]), R="ss", W="ss")
    ph.ts("dve", xn[:P, :], xt[:P, :], ss[:P, 2:3], ALU.mult, R=["xt" + tag, "ss"], W="xn")
    for k in range(8):
        ph.tr(pT[:, k, :P], xn[:P, k * 128:(k + 1) * 128], G["identb"][:P, :P], R="xn", W="pT")
    ph.tt("dve", hT[:, :, c0:c0 + P], pT[:, :, :P], bc(gcol[:, :].unsqueeze(2), [128, 8, P]), ALU.mult,
          R="pT", W="hT")


def load_col(ph, dst, src1d, n, key):
    ph.dma("sp", dst, src1d.rearrange("(k p) -> p k", p=128), W=key, slow=True)


def norm_scratch(ph, G0):
    G = dict(G0)
    G["sq"] = ph.sb("sq", [128, D], F32)
    G["ss"] = ph.sb("ss", [128, 4], F32)
    G["xn"] = ph.sb("xn", [128, D], BF16)
    G["pT"] = ph.ps("pT", [128, 8, 128], BF16)
    G["eps"] = ph.sb("eps", [128, 1], F32)
    ph.memset("dve", G["eps"][:], 1e-6, W="eps")
    return G


def build_program(upto=9, debug=False):
    nc = bass.Bass("TRN2", target_bir_lowering=False)
    I = {}

    def inp(name, shape, dt=F32):
        I[name] = nc.dram_tensor(name, list(shape), dt, kind="ExternalInput").ap()

    def outp(name, shape):
        I[name] = nc.dram_tensor(name, list(shape), F32, kind="ExternalOutput").ap()

    def scratch(name, shape, dt):
        if debug:
            I[name] = nc.dram_tensor(name, list(shape), dt, kind="ExternalOutput").ap()
        else:
            I[name] = nc.dram_tensor(name, list(shape), dt).ap()
    if debug:
        scratch("d_BwT", [128, 4 * CS * 2 * 128], BF16); scratch("d_Kmat", [128, 4 * CS * 128], BF16)
        scratch("d_CwT", [128, CS * 2 * 16 * 32], BF16); scratch("d_Abar", [128, 64], F32)

    inp("xall", [NT, D]); inp("pall", [NT, 256])
    inp("st_shift", [NS, 1792]); inp("st_wkv", [128, 4096]); inp("st_re", [NS, 2048]); inp("st_im", [NS, 2048])
    inp("st_conv", [NS, 2, 2816])
    inp("ln1_g", [D]); inp("w_in", [D, 4352]); inp("mu_shift", [1792]); inp("w0", [512]); inp("w2", [64, 512])
    inp("a0", [512]); inp("a2", [64, 512]); inp("g2", [128, 512]); inp("k_k", [512]); inp("k_a", [512])
    inp("r_k", [512]); inp("lnx_g", [512]); inp("lnx_b", [512]); inp("w_rw_out", [512, D])
    inp("A_re", [32, 64]); inp("A_im", [32, 64]); inp("log_dt", [32]); inp("B_re", [32, 64, 16]); inp("B_im", [32, 64, 16])
    inp("C_re", [512, 64]); inp("C_im", [512, 64]); inp("D_skip", [512]); inp("w_glu", [512, 2048]); inp("w_out", [D, D])
    inp("ln2_g", [D]); inp("w_ffn_in", [D, 5632]); inp("conv_w", [3, 2816]); inp("conv_b", [2816]); inp("w_ffn_out", [2816, D])
    inp("ln3_g", [D]); inp("w_ple_gate", [D, D]); inp("w_ple", [256, D]); inp("final_g", [D])
    inp("c_ident", [128, 128]); inp("c_msl", [128, 128]); inp("c_msu", [128, 128]); inp("c_mui", [128, 128])
    inp("c_blk64", [128, 128]); inp("c_blk32", [128, 128]); inp("c_rowgp", [128, 128])
    outp("y", [NT, D]); outp("p_shift", [1792]); outp("p_wkv", [512, 64]); outp("p_re", [2048]); outp("p_im", [2048])
    outp("p_conv", [2, 2816]); outp("s_shift", [NS, 1792]); outp("s_wkv", [128, 4096]); outp("s_re", [NS, 2048])
    outp("s_im", [NS, 2048]); outp("s_conv", [NS, 2, 2816])
    scratch("PRW", [1792, NT], F32); scratch("UU", [512, NT], F32); scratch("GT", [2048, NT], BF16)
    scratch("YF", [512, NT], BF16); scratch("ZZ", [512, NT], BF16); scratch("X1", [NT, D], F32); scratch("X2", [NT, D], F32)
    scratch("SW", [6, NS, 512], F32); scratch("SY", [128, 64], F32)

    with contextlib.ExitStack() as gst:
        def gsb(name, shape, dt):
            return gst.enter_context(nc.sbuf_tensor("g_" + name, list(shape), dt))
        G0 = {}
        G0["identb"] = gsb("identb", [128, 128], BF16)
        G0["identf"] = gsb("identf", [128, 128], F32)
        with contextlib.ExitStack() as g2:
            def g2sb(name, shape, dt):
                return g2.enter_context(nc.sbuf_tensor("g_" + name, list(shape), dt))
            G0["BwT"] = g2sb("BwT", [128, 4, CS, 2, 128], BF16)
            G0["Kmat"] = g2sb("Kmat", [128, 4, CS, 128], BF16)
            G0["CwT"] = g2sb("CwT", [128, CS, 2, 16, 32], BF16)
            G0["Abar"] = g2sb("Abar", [128, 2, 2, 16], F32)
            phase0(nc, I, G0, debug)
            if upto >= 1:
                phase1(nc, I, G0)
            if upto >= 2:
                phase2(nc, I, G0, True)
            if upto >= 2.5:
                phase2(nc, I, G0, False)
        if upto >= 3:
            phase3(nc, I, G0)
        if upto >= 4:
            phase4(nc, I, G0)
        if upto >= 5:
            phase5(nc, I, G0)
    return nc


def phase0(nc, I, G0, debug=False):
    ph = Ph(nc, "p0")
    ph.dma("pool", G0["identb"][:], I["c_ident"], W="identb")
    ph.dma("sp", G0["identf"][:], I["c_ident"], W="identf")
    sb = ph.sb
    lr = sb("lr", [128, 16], F32); li = sb("li", [128, 16], F32); dtl = sb("dtl", [128, 16], F32)
    Bre = sb("Bre", [128, 16, 16], F32); Bim = sb("Bim", [128, 16, 16], F32)
    ph.dma("sp", lr[:], I["A_re"].rearrange("(P gp) n -> (gp n) P", gp=2), W="lr", slow=True)
    ph.dma("sp", li[:], I["A_im"].rearrange("(P gp) n -> (gp n) P", gp=2), W="li", slow=True)
    ldt2 = I["log_dt"].rearrange("(P gp) -> gp P", gp=2)
    for gp in range(2):
        ph.dma("sp", dtl[64 * gp:64 * gp + 64, :], ldt2[gp].partition_broadcast(64), W="dtl", slow=True)
    ph.dma("sp", Bre[:], I["B_re"].rearrange("(P gp) n c -> (gp n) P c", gp=2), W="Bre")
    ph.dma("sp", Bim[:], I["B_im"].rearrange("(P gp) n c -> (gp n) P c", gp=2), W="Bim")
    rowgp = sb("rowgp", [128, 128], F32); blk32 = sb("blk32", [128, 128], F32)
    ph.dma("sp", rowgp[:], I["c_rowgp"], W="rowgp"); ph.dma("sp", blk32[:], I["c_blk32"], W="blk32")
    CT = [sb("CTr", [128, 4, 128], F32), sb("CTi", [128, 4, 128], F32)]
    c2 = sb("c2", [128, 128], F32)
    pA = ph.ps("pA", [128, 4, 128], F32)
    for ri, nm in enumerate(("C_re", "C_im")):
        for k in range(4):
            src = I[nm][k * 128:(k + 1) * 128, :]
            ph.dma("sp", c2[:, 0:64], src, W="c2"); ph.dma("sp", c2[:, 64:128], src, W="c2")
            ph.tt("dve", c2[:], c2[:], rowgp[:], ALU.mult, R=["c2", "rowgp"], W="c2")
            ph.tr(pA[:, k, :], c2[:], G0["identf"][:], R=["c2", "identf"], W="pA")
        ph.cp("dve", CT[ri][:], pA[:], R="pA", W="CT%d" % ri)
    t = {n: sb(n, [128, 16], F32) for n in ("dt", "e1", "mag", "ang", "sa", "ca", "sinv", "cosv", "ar", "ai", "den",
                                             "rden", "am1", "fr", "fi", "t1", "t2")}
    V = "dve"
    K = lambda *n: list(n)
    hpi = sb("hpi", [128, 1], F32)
    ph.memset(V, hpi[:], math.pi / 2, W="hpi")
    ph.act(t["dt"][:], dtl[:], AF.Exp, R="dtl", W="dt")
    ph.tt(V, t["e1"][:], lr[:], t["dt"][:], ALU.mult, R=K("lr", "dt"), W="e1")
    ph.act(t["mag"][:], t["e1"][:], AF.Exp, R="e1", W="mag")
    ph.tt(V, t["ang"][:], li[:], t["dt"][:], ALU.mult, R=K("li", "dt"), W="ang")
    ph.ts(V, t["sa"][:], t["ang"][:], 1.0 / 64, ALU.mult, R="ang", W="sa")
    ph.act(t["sinv"][:], t["sa"][:], AF.Sin, R="sa", W="sinv")
    ph.act(t["cosv"][:], t["sa"][:], AF.Sin, R="sa", W="cosv", bias=hpi[:, 0:1])
    for _ in range(6):
        ph.tt(V, t["t1"][:], t["cosv"][:], t["cosv"][:], ALU.mult, R="cosv", W="t1")
        ph.tt(V, t["t2"][:], t["sinv"][:], t["sinv"][:], ALU.mult, R="sinv", W="t2")
        ph.stt(t["sinv"][:], t["cosv"][:], 2.0, t["sinv"][:], ALU.mult, ALU.mult, R=["cosv", "sinv", "t2"], W="sinv")
        ph.tt(V, t["cosv"][:], t["t1"][:], t["t2"][:], ALU.subtract, R=["t1", "t2", "sinv"], W="cosv")
    ph.tt(V, t["ar"][:], t["mag"][:], t["cosv"][:], ALU.mult, R=K("mag", "cosv"), W="ar")
    ph.tt(V, t["ai"][:], t["mag"][:], t["sinv"][:], ALU.mult, R=K("mag", "sinv"), W="ai")
    ph.tt(V, t["den"][:], lr[:], lr[:], ALU.mult, R="lr", W="den")
    ph.tt(V, t["t1"][:], li[:], li[:], ALU.mult, R="li", W="t1")
    ph.tt(V, t["den"][:], t["den"][:], t["t1"][:], ALU.add, R=K("den", "t1"), W="den")
    ph.op(V, lambda e: e.reciprocal(out=t["rden"][:], in_=t["den"][:]), R="den", W="rden")
    ph.ts(V, t["am1"][:], t["ar"][:], -1.0, ALU.add, R="ar", W="am1")
    ph.tt(V, t["t1"][:], t["am1"][:], lr[:], ALU.mult, R=K("am1", "lr", "den"), W="t1")
    ph.tt(V, t["t2"][:], t["ai"][:], li[:], ALU.mult, R=K("ai", "li"), W="t2")
    ph.tt(V, t["t1"][:], t["t1"][:], t["t2"][:], ALU.add, R=K("t1", "t2"), W="t1")
    ph.tt(V, t["fr"][:], t["t1"][:], t["rden"][:], ALU.mult, R=K("t1", "rden"), W="fr")
    ph.tt(V, t["t1"][:], t["ai"][:], lr[:], ALU.mult, R=K("ai", "lr", "fr"), W="t1")
    ph.tt(V, t["t2"][:], t["am1"][:], li[:], ALU.mult, R=K("am1", "li"), W="t2")
    ph.tt(V, t["t1"][:], t["t1"][:], t["t2"][:], ALU.subtract, R=K("t1", "t2"), W="t1")
    ph.tt(V, t["fi"][:], t["t1"][:], t["rden"][:], ALU.mult, R=K("t1", "rden"), W="fi")
    pwr = sb("pwr", [128, CS + 1, 16], F32); pwi = sb("pwi", [128, CS + 1, 16], F32)
    ph.memset(V, pwr[:, 0, :], 1.0, W="pw"); ph.memset(V, pwi[:, 0, :], 0.0, W="pw")
    for e in range(CS):
        ph.tt(V, t["t1"][:], pwr[:, e, :], t["ar"][:], ALU.mult, R=K("pw", "ar", "fi"), W="t1")
        ph.tt(V, t["t2"][:], pwi[:, e, :], t["ai"][:], ALU.mult, R=K("pw", "ai"), W="t2")
        ph.tt(V, pwr[:, e + 1, :], t["t1"][:], t["t2"][:], ALU.subtract, R=K("t1", "t2"), W="pw")
        ph.tt(V, t["t1"][:], pwr[:, e, :], t["ai"][:], ALU.mult, R=K("pw", "ai"), W="t1")
        ph.tt(V, t["t2"][:], pwi[:, e, :], t["ar"][:], ALU.mult, R=K("pw", "ar"), W="t2")
        ph.tt(V, pwi[:, e + 1, :], t["t1"][:], t["t2"][:], ALU.add, R=K("t1", "t2"), W="pw")
    Ab = G0["Abar"]
    ph.cp(V, Ab[:, 0, 0, :], pwr[:, CS, :], R="pw", W="Abar"); ph.cp(V, Ab[:, 0, 1, :], pwi[:, CS, :], R="pw", W="Abar")
    ph.cp(V, Ab[:, 1, 0, :], pwr[:, 1, :], R="pw", W="Abar"); ph.cp(V, Ab[:, 1, 1, :], pwi[:, 1, :], R="pw", W="Abar")
    bbr = sb("bbr", [128, 16, 16], F32); bbi = sb("bbi", [128, 16, 16], F32)
    u1 = sb("u1", [128, 16, 16], F32); u2 = sb("u2", [128, 16, 16], F32)
    frb = bc(t["fr"][:, :].unsqueeze(2), [128, 16, 16]); fib = bc(t["fi"][:, :].unsqueeze(2), [128, 16, 16])
    ph.tt(V, u1[:], Bre[:], frb, ALU.mult, R=K("Bre", "fr"), W="u1")
    ph.tt(V, u2[:], Bim[:], fib, ALU.mult, R=K("Bim", "fi"), W="u2")
    ph.tt(V, bbr[:], u1[:], u2[:], ALU.subtract, R=K("u1", "u2"), W="bbr")
    ph.tt(V, u1[:], Bim[:], frb, ALU.mult, R=K("Bim", "fr", "bbr"), W="u1")
    ph.tt(V, u2[:], Bre[:], fib, ALU.mult, R=K("Bre", "fi", "bbr"), W="u2")
    ph.tt(V, bbi[:], u1[:], u2[:], ALU.add, R=K("u1", "u2"), W="bbi")
    Ew = sb("Ew", [128, CS, 2, 16, 2, 16], F32)
    ph.memset(V, Ew[:].rearrange("p a b c d e -> p (a b c d e)"), 0.0, W="Ew")
    for e in range(CS):
        pr = bc(pwr[:, e, :].unsqueeze(2), [128, 16, 16]); pi = bc(pwi[:, e, :].unsqueeze(2), [128, 16, 16])
        ph.tt(V, u1[:], bbr[:], pr, ALU.mult, R=K("bbr", "pw", "Ew"), W="u1")
        ph.tt(V, u2[:], bbi[:], pi, ALU.mult, R=K("bbi", "pw", "Ew"), W="u2")
        ph.tt(V, u1[:], u1[:], u2[:], ALU.subtract, R=K("u1", "u2"), W="u1")
        for gp in range(2):
            ph.cp(V, Ew[64 * gp:64 * gp + 64, e, 0, :, gp, :], u1[64 * gp:64 * gp + 64, :, :], R="u1", W="Ew")
        ph.tt(V, u1[:], bbr[:], pi, ALU.mult, R=K("bbr", "pw", "Ew"), W="u1")
        ph.tt(V, u2[:], bbi[:], pr, ALU.mult, R=K("bbi", "pw", "Ew"), W="u2")
        ph.tt(V, u1[:], u1[:], u2[:], ALU.add, R=K("u1", "u2"), W="u1")
        for gp in range(2):
            ph.cp(V, Ew[64 * gp:64 * gp + 64, e, 1, :, gp, :], u1[64 * gp:64 * gp + 64, :, :], R="u1", W="Ew")
    CTin = sb("CTin", [128, 4, 128], F32)
    ph.ts(V, CTin[:], CT[1][:], -1.0, ALU.mult, R="CT1", W="CTin")
    pB = [ph.ps("pB%d" % i, [128, 4, 128], F32) for i in range(2)]
    n = 0
    for j in range(CS):
        e = CS - 1 - j
        for ri in range(2):
            pb = pB[n % 2]; n += 1
            for k in range(4):
                src = Ew[:, e, ri, 4 * k:4 * k + 4, :, :].rearrange("p a b c -> p (a b c)")
                ph.tr(pb[:, k, :], src, G0["identf"][:], R=["Ew", "identf"], W="pB%d" % ((n - 1) % 2))
            ph.cp("act" if n % 2 else "dve", G0["BwT"][:, :, j, ri, :], pb[:], R="pB%d" % ((n - 1) % 2), W="BwT")
    for tau in range(CS):
        pb = pB[n % 2]; key = "pB%d" % (n % 2); n += 1
        for k in range(4):
            lr_ = Ew[:, tau, 0, 4 * k:4 * k + 4, :, :].rearrange("p a b c -> p (a b c)")
            li_ = Ew[:, tau, 1, 4 * k:4 * k + 4, :, :].rearrange("p a b c -> p (a b c)")
            ph.mm(pb[:, k, :], lr_, CT[0][:, k, :], True, False, R=["Ew", "CT0"], W=key)
            ph.mm(pb[:, k, :], li_, CTin[:, k, :], False, True, R=["Ew", "CTin"], W=key)
        ph.tt(V, G0["Kmat"][:, :, tau, :], pb[:], bc(blk32[:, :].unsqueeze(1), [128, 4, 128]), ALU.mult,
              R=[key, "blk32"], W="Kmat")
    w1 = sb("w1", [128, 16, 32], F32); w2_ = sb("w2", [128, 16, 32], F32)
    CTr3 = CT[0][:].rearrange("p k (a b) -> p (k a) b", a=4); CTi3 = CT[1][:].rearrange("p k (a b) -> p (k a) b", a=4)
    for i in range(CS):
        pr = bc(pwr[:, i + 1, :].unsqueeze(2), [128, 16, 32]); pi = bc(pwi[:, i + 1, :].unsqueeze(2), [128, 16, 32])
        ph.tt(V, w1[:], CTr3, pr, ALU.mult, R=K("CT0", "pw", "CwT"), W="w1")
        ph.tt(V, w2_[:], CTi3, pi, ALU.mult, R=K("CT1", "pw", "CwT"), W="w2")
        ph.tt(V, G0["CwT"][:, i, 0, :, :], w1[:], w2_[:], ALU.subtract, R=K("w1", "w2"), W="CwT")
        ph.tt(V, w1[:], CTr3, pi, ALU.mult, R=K("CT0", "pw", "CwT"), W="w1")
        ph.tt(V, w2_[:], CTi3, pr, ALU.mult, R=K("CT1", "pw", "CwT"), W="w2")
        ph.tt(V, w1[:], w1[:], w2_[:], ALU.add, R=K("w1", "w2"), W="w1")
        ph.ts(V, G0["CwT"][:, i, 1, :, :], w1[:], -1.0, ALU.mult, R="w1", W="CwT")
    if debug:
        ph.dma("sp", I["d_BwT"], G0["BwT"][:].rearrange("p a b c d -> p (a b c d)"), R="BwT")
        ph.dma("sp", I["d_Kmat"], G0["Kmat"][:].rearrange("p a b c -> p (a b c)"), R="Kmat")
        ph.dma("sp", I["d_CwT"], G0["CwT"][:].rearrange("p a b c d -> p (a b c d)"), R="CwT")
        ph.dma("sp", I["d_Abar"], G0["Abar"][:].rearrange("p a b c -> p (a b c)"), R="Abar")
    ph.finish()


def phase1(nc, I, G0):
    ph = Ph(nc, "p1")
    G = norm_scratch(ph, G0)
    win = ph.sb("win", [128, 8, 4352], BF16)
    for k in range(8):
        ph.dma("pool", win[:, k, :], I["w_in"][k * 128:(k + 1) * 128, :], W="win%d" % k)
    g1c = ph.sb("g1c", [128, 8], F32)
    load_col(ph, g1c[:], I["ln1_g"], 8, "g1c")
    hT = ph.sb("hT", [128, 8, 512], BF16)
    xts = [ph.sb("xt%d" % i, [128, D], F32) for i in range(2)]
    pm = [ph.ps("pm%d" % i, [128, 512], F32) for i in range(4)]
    stf = [ph.sb("stf%d" % i, [128, 512], F32) for i in range(4)]
    stb = [ph.sb("stb%d" % i, [128, 512], BF16) for i in range(3)]
    WK = ["win%d" % k for k in range(8)]
    nx = nf = nb = npm = 0
    for (t0, nt) in BLOCKS:
        P = min(128, nt)
        for s in range((nt + 127) // 128):
            xt = xts[nx % 2]; tg = str(nx % 2); nx += 1
            ph.dma("sp", xt[:P, :], I["xall"][t0 + s * 128:t0 + s * 128 + P, :], W="xt" + tg)
            rms_to_hT(ph, G, xt, P, g1c, hT, s * 128, tg)
        for m in range(34):
            pb = pm[npm % 4]; pk = "pm%d" % (npm % 4); npm += 1
            for k in range(8):
                ph.mm(pb[:, :nt], win[:, k, m * 128:(m + 1) * 128], hT[:, k, :nt], k == 0, k == 7,
                      R=["win%d" % k, "hT"], W=pk)
            if m < 18:
                sf = stf[nf % 4]; sk = "stf%d" % (nf % 4); nf += 1
                ph.cp("dve" if m % 2 else "act", sf[:, :nt], pb[:, :nt], R=pk, W=sk)
                if m < 14:
                    ph.dma("sp", I["PRW"][m * 128:(m + 1) * 128, t0:t0 + nt], sf[:, :nt], R=sk)
                else:
                    ph.dma("sp", I["UU"][(m - 14) * 128:(m - 13) * 128, t0:t0 + nt], sf[:, :nt], R=sk)
            else:
                sbf = stb[nb % 3]; sk = "stb%d" % (nb % 3); nb += 1
                ph.act(sbf[:, :nt], pb[:, :nt], AF.Sigmoid, R=pk, W=sk)
                ph.dma("act", I["GT"][(m - 18) * 128:(m - 17) * 128, t0:t0 + nt], sbf[:, :nt], R=sk)
    ph.finish()


def phase2(nc, I, G0, prompt):
    ph = Ph(nc, "p2a" if prompt else "p2b")
    sb, ps = ph.sb, ph.ps
    V = "dve"
    ph._s5tmp = [sb("s5a", [128, 2, 16], F32), sb("s5b", [128, 2, 16], F32)]
    ph._s5xb = sb("Xb", [128, 2, 16, 64], BF16)
    ph._s5du = sb("s5du", [128, 512], F32)
    if prompt:
        msl = sb("msl", [128, 128], BF16); msu = sb("msu", [128, 128], BF16); mui = sb("mui", [128, 128], BF16)
        ph.dma("pool", msl[:], I["c_msl"], W="msl"); ph.dma("pool", msu[:], I["c_msu"], W="msu")
        ph.dma("pool", mui[:], I["c_mui"], W="mui")
    blk64 = sb("blk64", [128, 128], F32); ph.dma("sp", blk64[:], I["c_blk64"], W="blk64")
    w2a2 = sb("w2a2", [128, 512], BF16); g2b = sb("g2b", [128, 512], BF16)
    ph.dma("pool", w2a2[0:64, :], I["w2"], W="w2a2"); ph.dma("pool", w2a2[64:128, :], I["a2"], W="w2a2")
    ph.dma("pool", g2b[:], I["g2"], W="g2b")
    pc = {}
    for nm, n in (("mu_shift", 14), ("w0", 4), ("a0", 4), ("k_k", 4), ("k_a", 4), ("r_k", 4), ("lnx_g", 4),
                  ("lnx_b", 4), ("D_skip", 4)):
        pc[nm] = sb("c_" + nm, [128, n], F32)
        load_col(ph, pc[nm][:], I[nm], n, "c_" + nm)
    PK = ["c_" + k for k in pc]
    scm = sb("scm", [128, 4, 128], F32)
    ph.memset(V, scm[:].rearrange("p a b -> p (a b)"), 1.0, W="scm"); ph.memset(V, scm[:, :, 0:1], 0.0, W="scm")
    eps_gn = sb("eps_gn", [128, 1], F32); ph.memset(V, eps_gn[:], 64e-5, W="eps_gn")
    if prompt:
        Sst = sb("Sst", [128, 4, 64], F32); Sbd = sb("Sbd", [128, 4, 128], BF16)
        ph.memset(V, Sst[:].rearrange("p a b -> p (a b)"), 0.0, W="Sst")
        ph.memset(V, Sbd[:].rearrange("p a b -> p (a b)"), 0.0, W="Sbd")
        Xs = sb("Xs", [128, 2, 16, 65], F32)
        ph.memset(V, Xs[:].rearrange("p a b c -> p (a b c)"), 0.0, W="Xs")
        Pf = sb("Pf", [128, 14, 513], F32)
        ph.memset(V, Pf[:, :, 0:1], 0.0, W="Pf")
    uf = sb("uf", [128, 4, 512], F32); ub = sb("ub", [128, 4, 512], BF16)
    YFb = sb("YFb", [128, 4, 512], BF16); ZZb = sb("ZZb", [128, 4, 512], BF16)
    f4 = lambda n: sb(n, [128, 4, 128], F32)
    b4 = lambda n: sb(n, [128, 4, 128], BF16)
    XS = sb("XS", [128, 14, 128], F32); dd = sb("dd", [128, 14, 128], F32)
    lin = sb("lin", [128, 128], BF16); sgx = sb("sgx", [128, 128], BF16)
    sig = f4("sig"); aa = f4("aa"); gg = f4("gg"); kk0 = f4("kk0"); tq = f4("tq"); rn = f4("rn"); kkn = f4("kkn")
    bb = f4("bb"); kmod = f4("kmod"); bon = f4("bon"); cs = f4("cs"); ex1 = f4("ex1"); ex2 = f4("ex2"); ex3 = f4("ex3")
    nbias = sb("nbias", [128, 4], F32); PCt = sb("PCt", [128, 4], F32)
    if prompt:
        rT = b4("rT"); kT = b4("kT"); bT = b4("bT"); aT = b4("aT"); khT = b4("khT"); bhT = b4("bhT"); vT = b4("vT")
        Vtok = sb("Vtok", [128, 512], BF16); Khtok = sb("Khtok", [128, 512], BF16); Bhtok = sb("Bhtok", [128, 512], BF16)
        h8 = lambda n: sb(n, [128, 8, 128], BF16)
        Nb = [h8("Nb0"), h8("Nb1")]; Lb = [h8("Lb0"), h8("Lb1")]; Mt = [h8("Mt0"), h8("Mt1")]
        LKb = h8("LKb"); Arb = h8("Arb"); Ark = h8("Ark")
        Wbf = sb("Wbf", [128, 512], BF16); Ubf = sb("Ubf", [128, 512], BF16)
        tS = sb("tS", [128, 4, 64], F32)
    Ysb = sb("Ysb", [128, 8, 64], F32); Ysq = sb("Ysq", [128, 8, 64], F32); ynb = sb("ynb", [128, 8, 64], BF16)
    gn = sb("gn", [128, 6, 8], F32)
    pF = [ps("pF%d" % i, [128, 4, 128], F32) for i in range(6)]
    pT = [ps("pTb%d" % i, [128, 8, 128], BF16) for i in range(2)]
    cnt = {"f": 0, "t": 0}

    def getF():
        i = cnt["f"] % 6; cnt["f"] += 1
        return pF[i], "pF%d" % i

    def getT():
        i = cnt["t"] % 2; cnt["t"] += 1
        return pT[i], "pTb%d" % i

    ib = G0["identb"]

    if not prompt:
        sample_mixer(ph, I, G0, locals())
        ph.finish()
        return
    import os
    _nblk = int(os.environ.get("K_DBG_NBLK", "4"))
    _parts = os.environ.get("K_DBG_PARTS", "s5,rwkv")
    _nch = int(os.environ.get("K_DBG_NCH", "4"))
    for bi, (t0, nt) in enumerate(BLOCKS[:_nblk]):
        if bi > 0:
            ph.cp(V, Pf[:, :, 0:1], Pf[:, :, 512:513], R="Pf", W="Pf")
        ph.dma("sp", Pf[:, :, 1:513], I["PRW"][:, t0:t0 + nt].rearrange("(m p) t -> p m t", p=128), W="Pf")
        ph.dma("act", uf[:], I["UU"][:, t0:t0 + nt].rearrange("(m p) t -> p m t", p=128), W="uf")
        ph.cp("act", ub[:].rearrange("p a b -> p (a b)"), uf[:].rearrange("p a b -> p (a b)"), R="uf", W="ub")
        if bi == 3:
            ph.dma("sp", I["p_shift"].rearrange("(m p) -> p m", p=128), Pf[:, :, 512], R="Pf", slow=True)
        if "s5" in _parts:
            s5_block(ph, I, G0, pc, Xs, ub, ZZb, getF, nchunk=64, which=0, ncol=512)
        ph.dma("act", I["ZZ"][:, t0:t0 + nt].rearrange("(m p) t -> p m t", p=128), ZZb[:], R="ZZb")
        for c in range(_nch if "rwkv" in _parts else 0):
            c0 = c * 128
            ph.tt(V, dd[:], Pf[:, :, c0:c0 + 128], Pf[:, :, c0 + 1:c0 + 129], ALU.subtract, R="Pf", W="dd")
            ph.tt(V, dd[:], dd[:], bc(pc["mu_shift"][:, :].unsqueeze(2), [128, 14, 128]), ALU.mult,
                  R=["dd", "c_mu_shift"], W="dd")
            ph.tt(V, XS[:], dd[:], Pf[:, :, c0 + 1:c0 + 129], ALU.add, R=["dd", "Pf"], W="XS")
            rwkv_prep_and_core(ph, locals(), c, c0)
        ph.dma("sp", I["YF"][:, t0:t0 + nt].rearrange("(m p) t -> p m t", p=128), YFb[:], R="YFb")
    ph.dma("sp", I["p_wkv"].rearrange("(m p) v -> p m v", p=128), Sst[:], R="Sst")
    ph.dma("sp", I["p_re"].rearrange("(P p) -> p P", p=128), Xs[:, 0, :, 0], R="Xs", slow=True)
    ph.dma("sp", I["p_im"].rearrange("(P p) -> p P", p=128), Xs[:, 1, :, 0], R="Xs", slow=True)
    ph.finish()


def rwkv_prep_and_core(ph, L, c, c0):
    V = "dve"
    pc = L["pc"]; XS = L["XS"]; getF = L["getF"]; getT = L["getT"]; ib = L["ib"]
    sig, aa, gg, kk0, tq, rn, kkn = L["sig"], L["aa"], L["gg"], L["kk0"], L["tq"], L["rn"], L["kkn"]
    bb, kmod, bon, cs, ex1, ex2, ex3 = L["bb"], L["kmod"], L["bon"], L["cs"], L["ex1"], L["ex2"], L["ex3"]
    rT, kT, bT, aT, khT, bhT, vT = L["rT"], L["kT"], L["bT"], L["aT"], L["khT"], L["bhT"], L["vT"]
    lin, sgx, w2a2, g2b, blk64 = L["lin"], L["sgx"], L["w2a2"], L["g2b"], L["blk64"]
    nbias, PCt, scm = L["nbias"], L["PCt"], L["scm"]
    r_ = XS[:, 0:4, :]; k_ = XS[:, 4:8, :]; v_ = XS[:, 8:12, :]
    B4 = lambda t: bc(t[:, :].unsqueeze(2), [128, 4, 128])
    fl = lambda t: t[:].rearrange("p a b -> p (a b)")
    ph.act(lin[0:64, :], XS[0:64, 12, :], AF.Tanh, R="XS", W="lin")
    ph.cp("act", lin[64:128, :], XS[64:128, 12, :], R="XS", W="lin")
    ph.act(sgx[:], XS[:, 13, :], AF.Sigmoid, R="XS", W="sgx")
    pw_, kw_ = getF(); pa_, ka_ = getF(); pg_, kg_ = getF()
    for m in range(4):
        ph.mm(pw_[:, m, :], w2a2[0:64, m * 128:(m + 1) * 128], lin[0:64, :], True, True, R=["w2a2", "lin"], W=kw_)
    for m in range(4):
        ph.mm(pa_[:, m, :], w2a2[64:128, m * 128:(m + 1) * 128], lin[64:128, :], True, True, R=["w2a2", "lin"], W=ka_)
    for m in range(4):
        ph.mm(pg_[:, m, :], g2b[:, m * 128:(m + 1) * 128], sgx[:], True, True, R=["g2b", "sgx"], W=kg_)
    for m in range(4):
        ph.act(sig[:, m, :], pw_[:, m, :], AF.Sigmoid, R=[kw_, "c_w0"], W="sig", bias=pc["w0"][:, m:m + 1])
    for m in range(4):
        ph.act(aa[:, m, :], pa_[:, m, :], AF.Sigmoid, R=[ka_, "c_a0"], W="aa", bias=pc["a0"][:, m:m + 1])
    ph.cp("act", gg[:], pg_[:], R=kg_, W="gg")
    ph.tt(V, kk0[:], k_, B4(pc["k_k"]), ALU.mult, R=["XS", "c_k_k"], W="kk0")
    ph.tt(V, tq[:], kk0[:], kk0[:], ALU.mult, R="kk0", W="tq")
    pq, kq = getF()
    for m in range(4):
        ph.mm(pq[:, m, :], blk64[:], tq[:, m, :], True, True, R=["blk64", "tq"], W=kq)
    ph.act(rn[:], pq[:], AF.Sqrt, R=kq, W="rn")
    ph.ts(V, rn[:], rn[:], 1e-12, ALU.max, R="rn", W="rn")
    ph.op(V, lambda e: e.reciprocal(out=fl(rn), in_=fl(rn)), R="rn", W="rn")
    ph.tt(V, kkn[:], kk0[:], rn[:], ALU.mult, R=["kk0", "rn"], W="kkn")
    ph.tt(V, bb[:], kkn[:], aa[:], ALU.mult, R=["kkn", "aa"], W="bb")
    ph.tt(V, tq[:], aa[:], B4(pc["k_a"]), ALU.mult, R=["aa", "c_k_a", kq], W="tq")
    ph.tt(V, tq[:], tq[:], B4(pc["k_a"]), ALU.subtract, R=["tq", "c_k_a"], W="tq")
    ph.stt(kmod[:], tq[:], 1.0, k_, ALU.add, ALU.mult, R=["tq", "XS"], W="kmod")
    ph.tt(V, tq[:], r_, kmod[:], ALU.mult, R=["XS", "kmod"], W="tq")
    ph.tt(V, tq[:], tq[:], B4(pc["r_k"]), ALU.mult, R=["tq", "c_r_k"], W="tq")
    pq2, kq2 = getF()
    for m in range(4):
        ph.mm(pq2[:, m, :], blk64[:], tq[:, m, :], True, True, R=["blk64", "tq"], W=kq2)
    ph.tt(V, bon[:], pq2[:], v_, ALU.mult, R=[kq2, "XS"], W="bon")
    ph.op(V, lambda e: e.tensor_tensor_scan(out=fl(cs), data0=fl(scm), data1=fl(sig), initial=0.0, op0=ALU.mult,
                                             op1=ALU.add), R=["scm", "sig"], W="cs")
    ph.ts(V, nbias[:], cs[:, :, 127], -C1, ALU.mult, R="cs", W="nbias")
    ph.act(PCt[:], nbias[:], AF.Exp, R="nbias", W="PCt")
    ph.act(ex1[:], cs[:], AF.Exp, R="cs", W="ex1", scale=-C1)
    ph.tt(V, rT[:], r_, ex1[:], ALU.mult, R=["XS", "ex1"], W="rT")
    ph.act(ex2[:], cs[:], AF.Exp, R="cs", W="ex2", scale=C1)
    ph.tt(V, kT[:], kmod[:], ex2[:], ALU.mult, R=["kmod", "ex2"], W="kT")
    ph.tt(V, bT[:], bb[:], ex2[:], ALU.mult, R=["bb", "ex2"], W="bT")
    ph.tt(V, ex3[:], cs[:], sig[:], ALU.subtract, R=["cs", "sig"], W="ex3")
    ph.act(ex3[:], ex3[:], AF.Exp, R="ex3", W="ex3", scale=-C1)
    ph.stt(aT[:], kkn[:], -1.0, ex3[:], ALU.mult, ALU.mult, R=["kkn", "ex3"], W="aT")
    for m in range(4):
        ph.act(ex1[:, m, :], cs[:, m, :], AF.Exp, R=["cs", "nbias", "rT"], W="ex1", bias=nbias[:, m:m + 1], scale=C1)
    ph.tt(V, khT[:], kmod[:], ex1[:], ALU.mult, R=["kmod", "ex1"], W="khT")
    ph.tt(V, bhT[:], bb[:], ex1[:], ALU.mult, R=["bb", "ex1"], W="bhT")
    ph.cp("act", vT[:], v_, R="XS", W="vT")
    import os
    if int(os.environ.get("K_DBG_RW", "9")) >= 3:
        wkv_core(ph, L, c, c0)


def wkv_core(ph, L, c, c0):
    V = "dve"
    getF = L["getF"]; getT = L["getT"]; ib = L["ib"]
    rT, kT, bT, aT, khT, bhT, vT = L["rT"], L["kT"], L["bT"], L["aT"], L["khT"], L["bhT"], L["vT"]
    Vtok, Khtok, Bhtok = L["Vtok"], L["Khtok"], L["Bhtok"]
    Nb, Lb, Mt, LKb, Arb, Ark = L["Nb"], L["Lb"], L["Mt"], L["LKb"], L["Arb"], L["Ark"]
    msl, msu, mui = L["msl"], L["msu"], L["mui"]
    Wbf, Ubf, Ysb, Ysq, ynb, gn = L["Wbf"], L["Ubf"], L["Ysb"], L["Ysq"], L["ynb"], L["gn"]
    Sst, Sbd, PCt, tS = L["Sst"], L["Sbd"], L["PCt"], L["tS"]
    pc = L["pc"]; bon, gg, YFb = L["bon"], L["gg"], L["YFb"]
    M4 = lambda m_: bc(m_[:, :].unsqueeze(1), [128, 4, 128])
    pt, kt = getT()
    for m in range(4):
        ph.tr(pt[:, m, :], vT[:, m, :], ib[:], R="vT", W=kt)
    for m in range(4):
        ph.tr(pt[:, 4 + m, :], khT[:, m, :], ib[:], R="khT", W=kt)
    ph.cp("act", Vtok[:], pt[:, 0:4, :].rearrange("p a b -> p (a b)"), R=kt, W="Vtok")
    ph.cp(V, Khtok[:], pt[:, 4:8, :].rearrange("p a b -> p (a b)"), R=kt, W="Khtok")
    pt2, kt2 = getT()
    for m in range(4):
        ph.tr(pt2[:, m, :], bhT[:, m, :], ib[:], R="bhT", W=kt2)
    ph.cp("act", Bhtok[:], pt2[:, 0:4, :].rearrange("p a b -> p (a b)"), R=kt2, W="Bhtok")

    import os
    if os.environ.get("K_DBG_RW3", "") == "a":
        return

    def hsl(t, h):
        return t[64 * (h % 2):64 * (h % 2) + 64, h // 2, :]

    def amat(dst, dkey, lhs, lkey, rhs, rkey, mask, mkey):
        for par in range(2):
            pb, pk = getF()
            for q in range(4):
                h = 2 * q + par
                ph.mm(pb[:, q, :], hsl(lhs, h), hsl(rhs, h), True, True, R=[lkey, rkey], W=pk)
            ph.tt(V, dst[:, par:8:2, :], pb[:], M4(mask), ALU.mult, R=[pk, mkey], W=dkey)

    amat(Nb[0], "Nb0", aT, "aT", bT, "bT", msl, "msl")
    amat(Lb[0], "Lb0", bT, "bT", aT, "aT", msu, "msu")
    amat(LKb, "LKb", kT, "kT", aT, "aT", msu, "msu")
    amat(Arb, "Arb", bT, "bT", rT, "rT", mui, "mui")
    amat(Ark, "Ark", kT, "kT", rT, "rT", mui, "mui")
    for half in range(2):
        ph.tt(V, Mt[0][:, half * 4:half * 4 + 4, :], Lb[0][:, half * 4:half * 4 + 4, :], M4(ib), ALU.add,
              R=["Lb0", "identb"], W="Mt0")
    import os
    _rw = int(os.environ.get("K_DBG_RW", "9"))
    if _rw < 4:
        return
    cur = 0
    for lvl in range(6):
        nxt = 1 - cur
        for half in range(2):
            pb, pk = getF()
            for q in range(4):
                h = half * 4 + q
                ph.mm(pb[:, q, :], Lb[cur][:, h, :], Nb[cur][:, h, :], True, True, R=["Lb%d" % cur, "Nb%d" % cur], W=pk)
            ph.cp("act", Nb[nxt][:, half * 4:half * 4 + 4, :], pb[:], R=pk, W="Nb%d" % nxt)
        if lvl < 5:
            for half in range(2):
                pb, pk = getF()
                for q in range(4):
                    h = half * 4 + q
                    ph.mm(pb[:, q, :], Nb[cur][:, h, :], Lb[cur][:, h, :], True, True,
                          R=["Lb%d" % cur, "Nb%d" % cur], W=pk)
                ph.cp("act", Lb[nxt][:, half * 4:half * 4 + 4, :], pb[:], R=pk, W="Lb%d" % nxt)
        for half in range(2):
            pb, pk = getF()
            for q in range(4):
                h = half * 4 + q
                ph.mm(pb[:, q, :], Nb[nxt][:, h, :], Mt[cur][:, h, :], True, True, R=["Nb%d" % nxt, "Mt%d" % cur], W=pk)
            ph.tt(V, Mt[nxt][:, half * 4:half * 4 + 4, :], pb[:], Mt[cur][:, half * 4:half * 4 + 4, :], ALU.add,
                  R=[pk, "Mt%d" % cur], W="Mt%d" % nxt)
        cur = nxt
    MtF = Mt[cur]; mk = "Mt%d" % cur
    if _rw < 5:
        return
    def hcols(pb, h):
        return pb[:].rearrange("p a b -> p (a b)")[:, h * 64:h * 64 + 64]

    def pcols(pb, m):
        return pb[:].rearrange("p a b -> p (a b)")[:, m * 128:m * 128 + 128]

    pb, pk = getF()
    for m in range(4):
        ph.mm(pcols(pb, m), aT[:, m, :], Sbd[:, m, :], True, False, R=["aT", "Sbd"], W=pk)
        for hh in range(2):
            h = 2 * m + hh
            ph.mm(hcols(pb, h), LKb[:, h, :], Vtok[:, h * 64:h * 64 + 64], False, hh == 1, R=["LKb", "Vtok"], W=pk)
    ph.cp("act", Wbf[:], pb[:].rearrange("p a b -> p (a b)"), R=pk, W="Wbf")
    pb, pk = getF()
    for h in range(8):
        ph.mm(hcols(pb, h), MtF[:, h, :], Wbf[:, h * 64:h * 64 + 64], True, True, R=[mk, "Wbf"], W=pk)
    ph.cp("act", Ubf[:], pb[:].rearrange("p a b -> p (a b)"), R=pk, W="Ubf")
    pb, pk = getF()
    for m in range(4):
        ph.mm(pcols(pb, m), rT[:, m, :], Sbd[:, m, :], True, False, R=["rT", "Sbd"], W=pk)
        for hh in range(2):
            h = 2 * m + hh
            ph.mm(hcols(pb, h), Arb[:, h, :], Ubf[:, h * 64:h * 64 + 64], False, False, R=["Arb", "Ubf"], W=pk)
            ph.mm(hcols(pb, h), Ark[:, h, :], Vtok[:, h * 64:h * 64 + 64], False, hh == 1, R=["Ark", "Vtok"], W=pk)
    ph.cp("act", Ysb[:].rearrange("p a b -> p (a b)"), pb[:].rearrange("p a b -> p (a b)"), R=pk, W="Ysb")
    pS, kS = getF()
    for m in range(4):
        ph.mm(pS[:, m, :], Bhtok[:, m * 128:(m + 1) * 128], Ubf[:, m * 128:(m + 1) * 128], True, False,
              R=["Bhtok", "Ubf"], W=kS)
        ph.mm(pS[:, m, :], Khtok[:, m * 128:(m + 1) * 128], Vtok[:, m * 128:(m + 1) * 128], False, True,
              R=["Khtok", "Vtok"], W=kS)
    ph.tt(V, tS[:], Sst[:], bc(PCt[:, :].unsqueeze(2), [128, 4, 64]), ALU.mult, R=["Sst", "PCt"], W="tS")
    for hh in range(2):
        rs = slice(64 * hh, 64 * hh + 64)
        ph.tt(V, Sst[rs, :, :], tS[rs, :, :], pS[rs, :, 64 * hh:64 * hh + 64], ALU.add, R=["tS", kS], W="Sst")
        ph.cp(V, Sbd[rs, :, 64 * hh:64 * hh + 64], Sst[rs, :, :], R="Sst", W="Sbd")
    groupnorm_out(ph, L, c0, 128)


def groupnorm_out(ph, L, c0, P):
    V = "dve"
    Ysb, Ysq, ynb, gn = L["Ysb"], L["Ysq"], L["ynb"], L["gn"]
    pc = L["pc"]; bon, gg, YFb = L["bon"], L["gg"], L["YFb"]; getT = L["getT"]; ib = L["ib"]
    eps_gn = L["eps_gn"]; ex2 = L["ex2"]
    ph.op(V, lambda e: e.tensor_reduce(out=gn[:P, 0, :], in_=Ysb[:P], axis=AX.X, op=ALU.add), R="Ysb", W="gn")
    ph.act(Ysq[:P].rearrange("p a b -> p (a b)"), Ysb[:P].rearrange("p a b -> p (a b)"), AF.Square, R="Ysb", W="Ysq")
    ph.op(V, lambda e: e.tensor_reduce(out=gn[:P, 1, :], in_=Ysq[:P], axis=AX.X, op=ALU.add), R="Ysq", W="gn")
    ph.ts(V, gn[:P, 2, :], gn[:P, 0, :], 1.0 / 64, ALU.mult, R="gn", W="gn")
    ph.tt(V, gn[:P, 3, :], gn[:P, 2, :], gn[:P, 2, :], ALU.mult, R="gn", W="gn")
    ph.stt(gn[:P, 4, :], gn[:P, 1, :], 1.0 / 64, gn[:P, 3, :], ALU.mult, ALU.subtract, R="gn", W="gn")
    ph.act(gn[:P, 4, :], gn[:P, 4, :], AF.Sqrt, R=["gn", "eps_gn"], W="gn", bias=eps_gn[:P, 0:1])
    ph.op(V, lambda e: e.reciprocal(out=gn[:P, 5, :], in_=gn[:P, 4, :]), R="gn", W="gn")
    ph.tt(V, Ysq[:P], Ysb[:P], bc(gn[:P, 2, :].unsqueeze(2), [P, 8, 64]), ALU.subtract, R=["Ysb", "gn"], W="Ysq")
    ph.tt(V, ynb[:P], Ysq[:P], bc(gn[:P, 5, :].unsqueeze(2), [P, 8, 64]), ALU.mult, R=["Ysq", "gn"], W="ynb")
    pt, kt = getT()
    for m in range(4):
        ph.tr(pt[:, m, :P], ynb[:P, 2 * m:2 * m + 2, :].rearrange("p a b -> p (a b)"), ib[:P, :P], R="ynb", W=kt)
    B4 = lambda t: bc(t[:, :].unsqueeze(2), [128, 4, P])
    t1 = ex2
    ph.tt(V, t1[:, :, :P], pt[:, 0:4, :P], B4(pc["lnx_g"]), ALU.mult, R=[kt, "c_lnx_g", "kT", "bT"], W="ex2")
    ph.tt(V, t1[:, :, :P], t1[:, :, :P], B4(pc["lnx_b"]), ALU.add, R=["ex2", "c_lnx_b"], W="ex2")
    ph.tt(V, t1[:, :, :P], t1[:, :, :P], bon[:, :, :P], ALU.add, R=["ex2", "bon"], W="ex2")
    ph.tt(V, YFb[:, :, c0:c0 + P], t1[:, :, :P], gg[:, :, :P], ALU.mult, R=["ex2", "gg"], W="YFb")


def s5_block(ph, I, G0, pc, Xs, ub, ZZb, getF, nchunk, which, ncol, step=CS, npos=CS):
    V = "dve"
    BwT, Kmat, CwT, Ab = G0["BwT"], G0["Kmat"], G0["CwT"], G0["Abar"]
    nm = nchunk
    assert nm * 8 <= 512
    for Pl in range(4):
        pb, pk = getF()
        flat = pb[:].rearrange("p a b -> p (a b)")
        for ri in range(2):
            for k in range(4):
                q = ri * 4 + k
                dst = flat[:, q * nm:(q + 1) * nm]
                for j in range(npos):
                    jj = (CS - npos) + j
                    rhs = ub[32 * Pl:32 * Pl + 32, k, j:j + (nm - 1) * step + 1:step]
                    ph.mm(dst, BwT[32 * Pl:32 * Pl + 32, k, jj, ri, :], rhs, j == 0, j == npos - 1,
                          R=["BwT", "ub"], W=pk, tp=((96, 0) if Pl == 3 else None))
        for ri in range(2):
            ph.cp(V, Xs[:, ri, Pl:16:4, 1:1 + nm],
                  flat[:, ri * 4 * nm:(ri + 1) * 4 * nm].rearrange("p (q m) -> p q m", m=nm), R=[pk], W="Xs")
    import os
    _lvl = int(os.environ.get("K_DBG_S5", "3"))
    if _lvl < 2:
        return
    A_r = bc(Ab[:, which, 0, :].unsqueeze(1), [128, 2, 16]); A_i = bc(Ab[:, which, 1, :].unsqueeze(1), [128, 2, 16])
    tmpa = ph._s5tmp[0]; tmpb = ph._s5tmp[1]
    for m in range(nm):
        ph.tt(V, tmpa[:], Xs[:, :, :, m], A_r, ALU.mult, R=["Xs", "Abar"], W="s5a")
        ph.tt(V, tmpb[:], Xs[:, :, :, m], A_i, ALU.mult, R=["Xs", "Abar"], W="s5b")
        ph.tt(V, Xs[:, :, :, m + 1], Xs[:, :, :, m + 1], tmpa[:], ALU.add, R=["Xs", "s5a"], W="Xs")
        ph.tt(V, Xs[:, 0, :, m + 1], Xs[:, 0, :, m + 1], tmpb[:, 1, :], ALU.subtract, R=["Xs", "s5b"], W="Xs")
        ph.tt(V, Xs[:, 1, :, m + 1], Xs[:, 1, :, m + 1], tmpb[:, 0, :], ALU.add, R=["Xs", "s5b"], W="Xs")
    if _lvl < 3:
        return
    Xb = ph._s5xb
    ph.cp("act", Xb[:, :, :, 0:nm], Xs[:, :, :, 0:nm], R="Xs", W="Xb")
    for k in range(4):
        pb, pk = getF()
        flat = pb[:].rearrange("p a b -> p (a b)")
        for i in range(npos):
            dst = flat[:, i * nm:(i + 1) * nm]
            for tau in range(i + 1):
                rhs = ub[:, k, (i - tau):(i - tau) + (nm - 1) * step + 1:step]
                ph.mm(dst, Kmat[:, k, tau, :], rhs, tau == 0, False, R=["Kmat", "ub"], W=pk)
            for Pl in range(4):
                P_ = 4 * k + Pl
                for ri in range(2):
                    ph.mm(flat[32 * Pl:32 * Pl + 32, i * nm:(i + 1) * nm], CwT[:, i, ri, P_, :], Xb[:, ri, P_, 0:nm],
                          False, (Pl == 3 and ri == 1), R=["CwT", "Xb"], W=pk, tp=(0, 32 * Pl))
        du = ph._s5du
        ph.ts(V, du[:, 0:ncol], ub[:, k, 0:ncol], pc["D_skip"][:, k:k + 1], ALU.mult, R=["ub", "c_D_skip", "s5z"], W="s5du")
        if npos == 1:
            ph.tt(V, du[:, 0:ncol], du[:, 0:ncol], flat[:, 0:nm], ALU.add, R=["s5du", pk], W="s5du")
        else:
            ph.tt(V, du[:, 0:ncol].rearrange("p (m i) -> p m i", i=npos), du[:, 0:ncol].rearrange("p (m i) -> p m i", i=npos),
                  flat[:, 0:npos * nm].rearrange("p (i m) -> p m i", m=nm), ALU.add, R=["s5du", pk], W="s5du")
        ph.act(ZZb[:, k, 0:ncol], du[:, 0:ncol], AF.Gelu_apprx_tanh, R="s5du", W=["ZZb", "s5z"])
    ph.cp(V, Xs[:, :, :, 0], Xs[:, :, :, nm], R="Xs", W="Xs")


def sample_mixer(ph, I, G0, L):
    V = "dve"
    sb = ph.sb
    pc = L["pc"]; getF, getT, ib = L["getF"], L["getT"], L["ib"]
    identf = G0["identf"]
    XS = L["XS"]; dd = L["dd"]
    t0 = T
    n = NS
    cur = sb("s_cur", [128, 14, NS], F32); prv = sb("s_prv", [128, 14, NS], F32)
    ph.dma("sp", cur[:], I["PRW"][:, t0:t0 + n].rearrange("(m p) t -> p m t", p=128), W="s_cur")
    sst = sb("s_sst", [NS, 1792], F32)
    ph.dma("sp", sst[:], I["st_shift"], W="s_sst")
    for half in range(4):
        pb, pk = getF()
        flat = pb[:].rearrange("p a b -> p (a b)")
        ms = list(range(half * 4, min(14, half * 4 + 4)))
        for q, m in enumerate(ms):
            ph.tr(flat[:, q * NS:(q + 1) * NS], sst[:, m * 128:(m + 1) * 128], identf[:NS, :NS], R=["s_sst"], W=pk)
        ph.cp(V, prv[:, ms[0]:ms[-1] + 1, :], flat[:, 0:len(ms) * NS].rearrange("p (a b) -> p a b", b=NS), R=pk, W="s_prv")
    ph.dbg("cur", cur[:], [128, 14, NS], "s_cur")
    ph.dbg("prv", prv[:], [128, 14, NS], "s_prv")
    so = sst
    for half in range(4):
        pb, pk = getF()
        flat = pb[:].rearrange("p a b -> p (a b)")
        ms = list(range(half * 4, min(14, half * 4 + 4)))
        for q, m in enumerate(ms):
            ph.tr(flat[:NS, q * 128:(q + 1) * 128], cur[:, m, :], identf[:], R=["s_cur"], W=pk)
        ph.cp(V, so[:, ms[0] * 128:(ms[-1] + 1) * 128], flat[:NS, 0:len(ms) * 128], R=pk, W="s_sst")
    ph.dma("sp", I["s_shift"], so[:], R="s_sst")
    xs = XS[:, :, 0:NS]
    ph.tt(V, dd[:, :, 0:NS], prv[:], cur[:], ALU.subtract, R=["s_prv", "s_cur"], W="dd")
    ph.tt(V, dd[:, :, 0:NS], dd[:, :, 0:NS], bc(pc["mu_shift"][:, :].unsqueeze(2), [128, 14, NS]), ALU.mult,
          R=["dd", "c_mu_shift"], W="dd")
    ph.tt(V, xs, dd[:, :, 0:NS], cur[:], ALU.add, R=["dd", "s_cur"], W="XS")
    uf = L["uf"]; ub = L["ub"]; ZZb = L["ZZb"]
    ph.dma("act", uf[:, :, 0:NS], I["UU"][:, t0:t0 + n].rearrange("(m p) t -> p m t", p=128), W="uf")
    ph.cp("act", ub[:, :, 0:NS], uf[:, :, 0:NS], R="uf", W="ub")
    stx = [sb("s_stre", [NS, 2048], F32), sb("s_stim", [NS, 2048], F32)]
    ph.dma("sp", stx[0][:], I["st_re"], W="s_stx0"); ph.dma("sp", stx[1][:], I["st_im"], W="s_stx1")
    Xsm = sb("s_Xsm", [128, 2, 16, NS], F32)
    for ri in range(2):
        for q4 in range(4):
            pb, pk = getF()
            flat = pb[:].rearrange("p a b -> p (a b)")
            for q in range(4):
                P_ = q4 * 4 + q
                ph.tr(flat[:, q * NS:(q + 1) * NS], stx[ri][:, P_ * 128:(P_ + 1) * 128], identf[:NS, :NS],
                      R="s_stx%d" % ri, W=pk)
            ph.cp(V, Xsm[:, ri, q4 * 4:q4 * 4 + 4, :], flat[:, 0:4 * NS].rearrange("p (a b) -> p a b", b=NS), R=pk, W="s_Xsm")
    s5_sample(ph, I, G0, pc, Xsm, ub, ZZb, getF, stx)
    ph.dma("act", I["ZZ"][:, t0:t0 + n].rearrange("(m p) t -> p m t", p=128), ZZb[:, :, 0:NS], R="ZZb")
    rwkv_sample(ph, I, G0, L)
    ph.dma("sp", I["YF"][:, t0:t0 + n].rearrange("(m p) t -> p m t", p=128), L["YFb"][:, :, 0:NS], R="YFb")


def s5_sample(ph, I, G0, pc, Xsm, ub, ZZb, getF, stx):
    V = "dve"
    BwT, Kmat, CwT, Ab = G0["BwT"], G0["Kmat"], G0["CwT"], G0["Abar"]
    identf = G0["identf"]
    Xb = ph._s5xb
    ph.cp("act", Xb[:, :, :, 0:NS], Xsm[:], R="s_Xsm", W="Xb")
    du = ph._s5du
    for k in range(4):
        pb, pk = getF()
        flat = pb[:].rearrange("p a b -> p (a b)")
        ph.mm(flat[:, 0:NS], Kmat[:, k, 0, :], ub[:, k, 0:NS], True, False, R=["Kmat", "ub"], W=pk)
        for Pl in range(4):
            P_ = 4 * k + Pl
            for ri in range(2):
                ph.mm(flat[32 * Pl:32 * Pl + 32, 0:NS], CwT[:, 0, ri, P_, :], Xb[:, ri, P_, 0:NS], False,
                      (Pl == 3 and ri == 1), R=["CwT", "Xb"], W=pk, tp=(0, 32 * Pl))
        ph.ts(V, du[:, 0:NS], ub[:, k, 0:NS], pc["D_skip"][:, k:k + 1], ALU.mult, R=["ub", "c_D_skip", "s5z"], W="s5du")
        ph.tt(V, du[:, 0:NS], du[:, 0:NS], flat[:, 0:NS], ALU.add, R=["s5du", pk], W="s5du")
        ph.act(ZZb[:, k, 0:NS], du[:, 0:NS], AF.Gelu_apprx_tanh, R="s5du", W=["ZZb", "s5z"])
    Gs = ph.sb("s_Gs", [128, 2, 16, NS], F32)
    for Pl in range(4):
        pb, pk = getF()
        flat = pb[:].rearrange("p a b -> p (a b)")
        for ri in range(2):
            for k in range(4):
                q = ri * 4 + k
                ph.mm(flat[:, q * NS:(q + 1) * NS], BwT[32 * Pl:32 * Pl + 32, k, CS - 1, ri, :],
                      ub[32 * Pl:32 * Pl + 32, k, 0:NS], True, True, R=["BwT", "ub"], W=pk,
                      tp=((96, 0) if Pl == 3 else None))
        for ri in range(2):
            ph.cp(V, Gs[:, ri, Pl:16:4, :], flat[:, ri * 4 * NS:(ri + 1) * 4 * NS].rearrange("p (q m) -> p q m", m=NS),
                  R=pk, W="s_Gs")
    A_r = bc(Ab[:, 1, 0, :].unsqueeze(2), [128, 16, NS]); A_i = bc(Ab[:, 1, 1, :].unsqueeze(2), [128, 16, NS])
    ta = ph.sb("s_ta", [128, 16, NS], F32)
    ph.tt(V, ta[:], Xsm[:, 0], A_r, ALU.mult, R=["s_Xsm", "Abar"], W="s_ta")
    ph.tt(V, Gs[:, 0], Gs[:, 0], ta[:], ALU.add, R=["s_Gs", "s_ta"], W="s_Gs")
    ph.tt(V, ta[:], Xsm[:, 1], A_i, ALU.mult, R=["s_Xsm", "Abar", "s_Gs"], W="s_ta")
    ph.tt(V, Gs[:, 0], Gs[:, 0], ta[:], ALU.subtract, R=["s_Gs", "s_ta"], W="s_Gs")
    ph.tt(V, ta[:], Xsm[:, 1], A_r, ALU.mult, R=["s_Xsm", "Abar", "s_Gs"], W="s_ta")
    ph.tt(V, Gs[:, 1], Gs[:, 1], ta[:], ALU.add, R=["s_Gs", "s_ta"], W="s_Gs")
    ph.tt(V, ta[:], Xsm[:, 0], A_i, ALU.mult, R=["s_Xsm", "Abar", "s_Gs"], W="s_ta")
    ph.tt(V, Gs[:, 1], Gs[:, 1], ta[:], ALU.add, R=["s_Gs", "s_ta"], W="s_Gs")
    for ri, nm in enumerate(("s_re", "s_im")):
        xo = stx[ri]
        for q4 in range(4):
            pb, pk = getF()
            flat = pb[:].rearrange("p a b -> p (a b)")
            for q in range(4):
                P_ = q4 * 4 + q
                ph.tr(flat[:NS, q * 128:(q + 1) * 128], Gs[:, ri, P_, :], identf[:], R="s_Gs", W=pk)
            ph.cp(V, xo[:, q4 * 512:(q4 + 1) * 512], flat[:NS, 0:512], R=pk, W="s_stx%d" % ri)
        ph.dma("sp", I[nm], xo[:], R="s_stx%d" % ri)


def rwkv_sample(ph, I, G0, L):
    V = "dve"
    sb = ph.sb
    pc = L["pc"]; getF, getT, ib = L["getF"], L["getT"], L["ib"]
    identf = G0["identf"]
    XS = L["XS"]
    sig, aa, gg, kk0, tq, rn, kkn = L["sig"], L["aa"], L["gg"], L["kk0"], L["tq"], L["rn"], L["kkn"]
    bb, kmod, bon = L["bb"], L["kmod"], L["bon"]
    lin, sgx, w2a2, g2b, blk64 = L["lin"], L["sgx"], L["w2a2"], L["g2b"], L["blk64"]
    n = NS
    r_ = XS[:, 0:4, 0:n]; k_ = XS[:, 4:8, 0:n]; v_ = XS[:, 8:12, 0:n]
    B4 = lambda t: bc(t[:, :].unsqueeze(2), [128, 4, n])
    S4 = lambda t: t[:, :, 0:n]
    ph.act(lin[0:64, 0:n], XS[0:64, 12, 0:n], AF.Tanh, R="XS", W="lin")
    ph.cp("act", lin[64:128, 0:n], XS[64:128, 12, 0:n], R="XS", W="lin")
    ph.act(sgx[:, 0:n], XS[:, 13, 0:n], AF.Sigmoid, R="XS", W="sgx")
    pw_, kw_ = getF(); pa_, ka_ = getF(); pg_, kg_ = getF()
    for m in range(4):
        ph.mm(pw_[:, m, 0:n], w2a2[0:64, m * 128:(m + 1) * 128], lin[0:64, 0:n], True, True, R=["w2a2", "lin"], W=kw_)
        ph.mm(pa_[:, m, 0:n], w2a2[64:128, m * 128:(m + 1) * 128], lin[64:128, 0:n], True, True, R=["w2a2", "lin"], W=ka_)
        ph.mm(pg_[:, m, 0:n], g2b[:, m * 128:(m + 1) * 128], sgx[:, 0:n], True, True, R=["g2b", "sgx"], W=kg_)
    for m in range(4):
        ph.act(sig[:, m, 0:n], pw_[:, m, 0:n], AF.Sigmoid, R=[kw_, "c_w0"], W="sig", bias=pc["w0"][:, m:m + 1])
        ph.act(aa[:, m, 0:n], pa_[:, m, 0:n], AF.Sigmoid, R=[ka_, "c_a0"], W="aa", bias=pc["a0"][:, m:m + 1])
    ph.cp("act", S4(gg), pg_[:, :, 0:n], R=kg_, W="gg")
    ph.tt(V, S4(kk0), k_, B4(pc["k_k"]), ALU.mult, R=["XS", "c_k_k"], W="kk0")
    ph.tt(V, S4(tq), S4(kk0), S4(kk0), ALU.mult, R="kk0", W="tq")
    pq, kq = getF()
    for m in range(4):
        ph.mm(pq[:, m, 0:n], blk64[:], tq[:, m, 0:n], True, True, R=["blk64", "tq"], W=kq)
    ph.act(S4(rn), pq[:, :, 0:n], AF.Sqrt, R=kq, W="rn")
    ph.ts(V, S4(rn), S4(rn), 1e-12, ALU.max, R="rn", W="rn")
    ph.op(V, lambda e: e.reciprocal(out=S4(rn), in_=S4(rn)), R="rn", W="rn")
    ph.tt(V, S4(kkn), S4(kk0), S4(rn), ALU.mult, R=["kk0", "rn"], W="kkn")
    ph.tt(V, S4(bb), S4(kkn), S4(aa), ALU.mult, R=["kkn", "aa"], W="bb")
    ph.tt(V, S4(tq), S4(aa), B4(pc["k_a"]), ALU.mult, R=["aa", "c_k_a", kq], W="tq")
    ph.tt(V, S4(tq), S4(tq), B4(pc["k_a"]), ALU.subtract, R=["tq", "c_k_a"], W="tq")
    ph.stt(S4(kmod), S4(tq), 1.0, k_, ALU.add, ALU.mult, R=["tq", "XS"], W="kmod")
    ph.tt(V, S4(tq), r_, S4(kmod), ALU.mult, R=["XS", "kmod"], W="tq")
    ph.tt(V, S4(tq), S4(tq), B4(pc["r_k"]), ALU.mult, R=["tq", "c_r_k"], W="tq")
    pq2, kq2 = getF()
    for m in range(4):
        ph.mm(pq2[:, m, 0:n], blk64[:], tq[:, m, 0:n], True, True, R=["blk64", "tq"], W=kq2)
    ph.tt(V, S4(bon), pq2[:, :, 0:n], v_, ALU.mult, R=[kq2, "XS"], W="bon")
    wdec = L["ex1"]
    ph.act(S4(wdec), S4(sig), AF.Exp, R="sig", W="ex1", scale=-C1)
    srcs = [r_, S4(wdec), S4(kmod), v_, S4(kkn), S4(bb)]
    keys = ["XS", "ex1", "kmod", "XS", "kkn", "bb"]
    tok = sb("s_tok", [NS, 6, 512], F32)
    for i, (src, kkey) in enumerate(zip(srcs, keys)):
        pb, pk = getF()
        flat = pb[:].rearrange("p a b -> p (a b)")
        for m in range(4):
            ph.tr(flat[:NS, m * 128:(m + 1) * 128], src[:, m, :], identf[:], R=kkey, W=pk)
        ph.cp(V if i % 2 else "act", tok[:, i, :], flat[:NS, 0:512], R=pk, W="s_tok")
    ph.dma("sp", I["SW"].rearrange("i b f -> b i f"), tok[:], R="s_tok", W="SWd")
    vec = sb("s_vec", [128, 6, 64], F32)
    ph.dma("sp", vec[:], I["SW"].rearrange("i b (h k) -> (b h) i k", h=8), R="SWd", W="s_vec")
    S0 = sb("s_S0", [128, 64, 64], F32)
    ph.dma("act", S0[:].rearrange("p a b -> p (a b)"), I["st_wkv"], W="s_S0")
    tmp = sb("s_tmp", [128, 64, 64], F32)
    sa = sb("s_sa", [128, 64], F32); yv = sb("s_yv", [128, 64], F32); kka = sb("s_kka", [128, 64], F32)
    kB = lambda i: bc(vec[:, i, :].unsqueeze(1), [128, 64, 64])
    ph.tt(V, tmp[:], S0[:], kB(4), ALU.mult, R=["s_S0", "s_vec"], W="s_tmp")
    ph.op(V, lambda e: e.tensor_reduce(out=sa[:], in_=tmp[:], axis=AX.X, op=ALU.add), R="s_tmp", W="s_sa")
    ph.tt(V, S0[:], S0[:], kB(1), ALU.mult, R=["s_S0", "s_vec", "s_tmp"], W="s_S0")
    ph.tt(V, tmp[:], bc(sa[:, :].unsqueeze(2), [128, 64, 64]), kB(5), ALU.mult, R=["s_sa", "s_vec"], W="s_tmp")
    ph.tt(V, S0[:], S0[:], tmp[:], ALU.subtract, R=["s_S0", "s_tmp"], W="s_S0")
    ph.tt(V, tmp[:], bc(vec[:, 3, :].unsqueeze(2), [128, 64, 64]), kB(2), ALU.mult, R=["s_vec", "s_S0"], W="s_tmp")
    ph.tt(V, S0[:], S0[:], tmp[:], ALU.add, R=["s_S0", "s_tmp"], W="s_S0")
    ph.dma("act", I["s_wkv"], S0[:].rearrange("p a b -> p (a b)"), R="s_S0")
    ph.tt(V, tmp[:], S0[:], kB(0), ALU.mult, R=["s_S0", "s_vec"], W="s_tmp")
    ph.op(V, lambda e: e.tensor_reduce(out=yv[:], in_=tmp[:], axis=AX.X, op=ALU.add), R="s_tmp", W="s_yv")
    ph.dma("sp", I["SY"], yv[:], R="s_yv", W="SYd")
    Ysb = L["Ysb"]
    ph.dma("sp", Ysb[:NS].rearrange("p a b -> p (a b)"), I["SY"].rearrange("(b h) v -> b (h v)", h=8), R="SYd", W="Ysb")
    groupnorm_out(ph, L, 0, NS)


def phase3(nc, I, G0):
    ph = Ph(nc, "p3")
    V = "dve"
    rwo = ph.sb("rwo", [128, 4, D], BF16); glu = ph.sb("glu", [128, 4, 2048], BF16); wo = ph.sb("wo", [128, 8, D], BF16)
    for k in range(4):
        ph.dma("pool", rwo[:, k, :], I["w_rw_out"][k * 128:(k + 1) * 128, :], W="rwo")
        ph.dma("pool", glu[:, k, :], I["w_glu"][k * 128:(k + 1) * 128, :], W="glu")
    for k in range(8):
        ph.dma("pool", wo[:, k, :], I["w_out"][k * 128:(k + 1) * 128, :], W="wo")
    yf = ph.sb("yf", [128, 4, 512], BF16); zz = ph.sb("zz", [128, 4, 512], BF16); gt = ph.sb("gt", [128, 16, 512], BF16)
    trw = ph.sb("trw", [128, 8, 512], F32); mg = ph.sb("mg", [128, 8, 512], BF16)
    sgb = [ph.sb("sgb%d" % i, [128, 512], F32) for i in range(2)]
    s5t = [ph.sb("s5t%d" % i, [128, 512], F32) for i in range(2)]
    xts = [ph.sb("xt%d" % i, [128, D], F32) for i in range(2)]
    pm = [ph.ps("pm%d" % i, [128, 512], F32) for i in range(6)]
    npm = nx = ns = 0
    for (t0, nt) in BLOCKS:
        P = min(128, nt)
        r3 = lambda name: I[name][:, t0:t0 + nt].rearrange("(m p) t -> p m t", p=128)
        ph.dma("sp", yf[:, :, :nt], r3("YF"), W="yf"); ph.dma("sp", zz[:, :, :nt], r3("ZZ"), W="zz")
        ph.dma("act", gt[:, :, :nt], r3("GT"), W="gt")
        for m in range(8):
            pb = pm[npm % 6]; pk = "pm%d" % (npm % 6); npm += 1
            for k in range(4):
                ph.mm(pb[:, :nt], rwo[:, k, m * 128:(m + 1) * 128], yf[:, k, :nt], k == 0, k == 3, R=["rwo", "yf"], W=pk)
            ph.tt(V, trw[:, m, :nt], pb[:, :nt], gt[:, m, :nt], ALU.mult, R=[pk, "gt"], W="trw%d" % m)
        for m in range(8):
            pa = pm[npm % 6]; pka = "pm%d" % (npm % 6); npm += 1
            pb = pm[npm % 6]; pkb = "pm%d" % (npm % 6); npm += 1
            for k in range(4):
                ph.mm(pa[:, :nt], glu[:, k, m * 128:(m + 1) * 128], zz[:, k, :nt], k == 0, k == 3, R=["glu", "zz"], W=pka)
            for k in range(4):
                ph.mm(pb[:, :nt], glu[:, k, D + m * 128:D + (m + 1) * 128], zz[:, k, :nt], k == 0, k == 3,
                      R=["glu", "zz"], W=pkb)
            sg = sgb[ns % 2]; sk = "sgb%d" % (ns % 2); s5 = s5t[ns % 2]; s5k = "s5t%d" % (ns % 2); ns += 1
            ph.act(sg[:, :nt], pb[:, :nt], AF.Sigmoid, R=pkb, W=sk)
            ph.tt(V, s5[:, :nt], pa[:, :nt], sg[:, :nt], ALU.mult, R=[pka, sk], W=s5k)
            ph.tt(V, s5[:, :nt], s5[:, :nt], gt[:, 8 + m, :nt], ALU.mult, R=[s5k, "gt"], W=s5k)
            ph.tt(V, mg[:, m, :nt], s5[:, :nt], trw[:, m, :nt], ALU.add, R=[s5k, "trw%d" % m], W="mg")
        for s in range((nt + 127) // 128):
            xt = xts[nx % 2]; xk = "xt%d" % (nx % 2); nx += 1
            rows = slice(t0 + s * 128, t0 + s * 128 + P)
            ph.dma("sp", xt[:P, :], I["xall"][rows, :], W=xk)
            for half in range(2):
                pb = pm[npm % 6]; pk = "pm%d" % (npm % 6); npm += 1
                for k in range(8):
                    ph.mm(pb[:P, :], mg[:, k, s * 128:s * 128 + P], wo[:, k, half * 512:(half + 1) * 512], k == 0, k == 7,
                          R=["mg", "wo"], W=pk)
                ph.tt(V, xt[:P, half * 512:(half + 1) * 512], xt[:P, half * 512:(half + 1) * 512], pb[:P, :], ALU.add,
                      R=[pk, xk], W=xk)
            ph.dma("sp", I["X1"][rows, :], xt[:P, :], R=xk)
    ph.finish()


def phase4(nc, I, G0):
    ph = Ph(nc, "p4")
    V = "dve"
    G = norm_scratch(ph, G0)
    identf = G0["identf"]
    wfi = ph.sb("wfi", [128, 8, 5632], BF16); wfo = ph.sb("wfo", [128, 22, D], BF16)
    for k in range(8):
        ph.dma("pool", wfi[:, k, :], I["w_ffn_in"][k * 128:(k + 1) * 128, :], W="wfi")
    for k in range(22):
        ph.dma("pool", wfo[:, k, :], I["w_ffn_out"][k * 128:(k + 1) * 128, :], W="wfo")
    g2c = ph.sb("g2c", [128, 8], F32); load_col(ph, g2c[:], I["ln2_g"], 8, "g2c")
    cw = ph.sb("cw", [128, 3, 22], F32); cb = ph.sb("cb", [128, 22], F32)
    ph.dma("sp", cw[:], I["conv_w"].rearrange("t (f p) -> p t f", p=128), W="cw", slow=True)
    load_col(ph, cb[:], I["conv_b"], 22, "cb")
    hT = ph.sb("hT", [128, 8, 512], BF16)
    hid = ph.sb("hid", [128, 22, 512], BF16)
    xts = [ph.sb("xt%d" % i, [128, D], F32) for i in range(2)]
    At = [ph.sb("At%d" % i, [128, 514], F32) for i in range(2)]
    acc = [ph.sb("acc%d" % i, [128, 512], F32) for i in range(2)]
    cc = ph.sb("cc", [128, 22, 2], F32)
    ph.memset(V, cc[:].rearrange("p a b -> p (a b)"), 0.0, W="cc")
    pm = [ph.ps("pm%d" % i, [128, 512], F32) for i in range(6)]
    scs = ph.sb("scs", [NS, 2816], F32)
    scT = ph.sb("scT", [128, 22, 2, NS], F32)
    aout = scs
    npm = na = 0
    for (t0, nt) in BLOCKS:
        P = min(128, nt)
        sample = nt < 128
        nsub = (nt + 127) // 128
        for s in range(nsub):
            rows = slice(t0 + s * 128, t0 + s * 128 + P)
            ph.dma("sp", xts[s % 2][:P, :], I["X1"][rows, :], W="xt%d" % (s % 2))
            rms_to_hT(ph, G, xts[s % 2], P, g2c, hT, s * 128, str(s % 2))
        if sample:
            for tt_ in range(2):
                ph.dma("sp", scs[:], I["st_conv"][:, tt_, :], W="scs")
                for q in range(6):
                    pb = pm[npm % 6]; pk = "pm%d" % (npm % 6); npm += 1
                    fs = list(range(q * 4, min(22, q * 4 + 4)))
                    for j, f_ in enumerate(fs):
                        ph.tr(pb[:, j * NS:(j + 1) * NS], scs[:, f_ * 128:(f_ + 1) * 128], identf[:NS, :NS], R="scs", W=pk)
                    ph.cp(V, scT[:, fs[0]:fs[-1] + 1, tt_, :], pb[:, 0:len(fs) * NS].rearrange("p (a b) -> p a b", b=NS),
                          R=pk, W="scT")
        for f in range(22):
            pa = pm[npm % 6]; pka = "pm%d" % (npm % 6); npm += 1
            pb = pm[npm % 6]; pkb = "pm%d" % (npm % 6); npm += 1
            for k in range(8):
                ph.mm(pa[:, :nt], wfi[:, k, f * 128:(f + 1) * 128], hT[:, k, :nt], k == 0, k == 7, R=["wfi", "hT"], W=pka)
            for k in range(8):
                ph.mm(pb[:, :nt], wfi[:, k, 2816 + f * 128:2816 + (f + 1) * 128], hT[:, k, :nt], k == 0, k == 7,
                      R=["wfi", "hT"], W=pkb)
            A = At[na % 2]; ak = "At%d" % (na % 2); ac = acc[na % 2]; ck = "acc%d" % (na % 2); na += 1
            ph.cp("act", A[:, 2:2 + nt], pa[:, :nt], R=pka, W=ak)
            if not sample:
                ph.cp(V, A[:, 0:2], cc[:, f, :], R="cc", W=ak)
                a0, a1, a2 = A[:, 0:nt], A[:, 1:1 + nt], A[:, 2:2 + nt]
            else:
                a0, a1, a2 = scT[:, f, 0, :], scT[:, f, 1, :], A[:, 2:2 + nt]
            ph.ts(V, ac[:, :nt], a0, cw[:, 0, f:f + 1], ALU.mult, cb[:, f:f + 1], ALU.add, R=[ak, "scT", "cw", "cb"], W=ck)
            ph.stt(ac[:, :nt], a1, cw[:, 1, f:f + 1], ac[:, :nt], ALU.mult, ALU.add, R=[ak, "scT", "cw", ck], W=ck)
            ph.stt(ac[:, :nt], a2, cw[:, 2, f:f + 1], ac[:, :nt], ALU.mult, ALU.add, R=[ak, "cw", ck], W=ck)
            ph.act(ac[:, :nt], ac[:, :nt], AF.Gelu_apprx_tanh, R=ck, W=ck)
            ph.tt(V, hid[:, f, :nt], ac[:, :nt], pb[:, :nt], ALU.mult, R=[ck, pkb], W="hid")
            if not sample:
                ph.cp(V, cc[:, f, :], A[:, nt:nt + 2], R=ak, W="cc")
            else:
                po = pm[npm % 6]; pko = "pm%d" % (npm % 6); npm += 1
                ph.tr(po[:NS, 0:128], A[:, 2:2 + NS], identf[:], R=ak, W=pko)
                ph.cp(V, aout[:, f * 128:(f + 1) * 128], po[:NS, 0:128], R=pko, W="scs")
        if t0 + nt == T:
            for tt_ in range(2):
                ph.dma("sp", I["p_conv"][tt_].rearrange("(f p) -> p f", p=128), cc[:, :, tt_], R="cc", slow=True)
        if sample:
            ph.dma("sp", I["s_conv"][:, 1, :], aout[:], R="scs")
            ph.dma("act", I["s_conv"][:, 0, :], I["st_conv"][:, 1, :])
        for s in range(nsub):
            rows = slice(t0 + s * 128, t0 + s * 128 + P)
            xt = xts[s % 2]; xk = "xt%d" % (s % 2)
            ph.dma("sp", xt[:P, :], I["X1"][rows, :], W=xk)
            for half in range(2):
                pb = pm[npm % 6]; pk = "pm%d" % (npm % 6); npm += 1
                for f in range(22):
                    ph.mm(pb[:P, :], hid[:, f, s * 128:s * 128 + P], wfo[:, f, half * 512:(half + 1) * 512], f == 0, f == 21,
                          R=["hid", "wfo"], W=pk)
                ph.tt(V, xt[:P, half * 512:(half + 1) * 512], xt[:P, half * 512:(half + 1) * 512], pb[:P, :],
                      ALU.add, R=[pk, xk], W=xk)
            ph.dma("sp", I["X2"][rows, :], xt[:P, :], R=xk)
    ph.finish()


def phase5(nc, I, G0):
    ph = Ph(nc, "p5")
    V = "dve"
    G = norm_scratch(ph, G0)
    wpg = ph.sb("wpg", [128, 8, D], BF16); wpl = ph.sb("wpl", [128, 2, D], BF16)
    for k in range(8):
        ph.dma("pool", wpg[:, k, :], I["w_ple_gate"][k * 128:(k + 1) * 128, :], W="wpg")
    for k in range(2):
        ph.dma("pool", wpl[:, k, :], I["w_ple"][k * 128:(k + 1) * 128, :], W="wpl")
    g3c = ph.sb("g3c", [128, 8], F32); load_col(ph, g3c[:], I["ln3_g"], 8, "g3c")
    fg = ph.sb("fg", [128, D], F32)
    ph.dma("sp", fg[:], I["final_g"].partition_broadcast(128), W="fg")
    hT = ph.sb("hT", [128, 8, 128], BF16)
    xts = [ph.sb("xt%d" % i, [128, D], F32) for i in range(2)]
    pbs = [ph.sb("pb%d" % i, [128, 256], BF16) for i in range(2)]
    pTs = ph.sb("pTs", [128, 2, 128], BF16)
    sg = [ph.sb("sg%d" % i, [128, 512], F32) for i in range(2)]
    yo = [ph.sb("yo%d" % i, [128, D], F32) for i in range(2)]
    pm = [ph.ps("pm%d" % i, [128, 512], F32) for i in range(4)]
    pq = ph.ps("pq", [128, 8, 128], BF16)
    npm = nx = nsg = 0
    for (t0, nt) in BLOCKS:
        P = min(128, nt)
        for s in range((nt + 127) // 128):
            rows = slice(t0 + s * 128, t0 + s * 128 + P)
            i2 = nx % 2; nx += 1
            xt = xts[i2]; xk = "xt%d" % i2; pbt = pbs[i2]; pbk = "pb%d" % i2
            ph.dma("sp", xt[:P, :], I["X2"][rows, :], W=xk)
            ph.dma("pool", pbt[:P, :], I["pall"][rows, :], W=pbk)
            rms_to_hT(ph, G, xt, P, g3c, hT, 0, str(i2))
            for k in range(2):
                ph.tr(pq[:, k, :P], pbt[:P, k * 128:(k + 1) * 128], G0["identb"][:P, :P], R=pbk, W="pq")
            ph.cp("act", pTs[:, :, :P], pq[:, 0:2, :P], R="pq", W="pTs")
            for half in range(2):
                cs_ = slice(half * 512, (half + 1) * 512)
                pg = pm[npm % 4]; pgk = "pm%d" % (npm % 4); npm += 1
                pe = pm[npm % 4]; pek = "pm%d" % (npm % 4); npm += 1
                for k in range(8):
                    ph.mm(pg[:P, :], hT[:, k, :P], wpg[:, k, cs_], k == 0, k == 7, R=["hT", "wpg"], W=pgk)
                for k in range(2):
                    ph.mm(pe[:P, :], pTs[:, k, :P], wpl[:, k, cs_], k == 0, k == 1, R=["pTs", "wpl"], W=pek)
                sgt = sg[nsg % 2]; sgk = "sg%d" % (nsg % 2); nsg += 1
                ph.act(sgt[:P, :], pg[:P, :], AF.Sigmoid, R=pgk, W=sgk)
                ph.tt(V, sgt[:P, :], sgt[:P, :], pe[:P, :], ALU.mult, R=[sgk, pek], W=sgk)
                ph.tt(V, xt[:P, cs_], xt[:P, cs_], sgt[:P, :], ALU.add, R=[sgk, xk, "xn"], W=xk)
            ss = G["ss"]; sq = G["sq"]
            ph.act(sq[:P, :], xt[:P, :], AF.Square, R=xk, W=["sq", "ss"], accum=ss[:P, 0:1])
            ph.act(ss[:P, 1:2], ss[:P, 0:1], AF.Sqrt, R="ss", W="ss", bias=G["eps"][:P, 0:1], scale=1.0 / D)
            ph.op(V, lambda e, ss=ss, P=P: e.reciprocal(out=ss[:P, 3:4], in_=ss[:P, 1:2]), R="ss", W="ss3")
            y = yo[i2]; yk = "yo%d" % i2
            ph.stt(y[:P, :], xt[:P, :], ss[:P, 3:4], fg[:P, :], ALU.mult, ALU.mult, R=[xk, "ss3", "fg"], W=yk)
            ph.dma("sp", I["y"][rows, :], y[:P, :], R=yk)
    ph.finish()


_CACHE = {}


def _consts():
    i = np.arange(128)
    c = {}
    c["c_ident"] = np.eye(128, dtype=np.float32)
    c["c_msl"] = (i[None, :] < i[:, None]).astype(np.float32)
    c["c_msu"] = (i[:, None] < i[None, :]).astype(np.float32)
    c["c_mui"] = (i[:, None] <= i[None, :]).astype(np.float32)
    c["c_blk64"] = ((i[:, None] // 64) == (i[None, :] // 64)).astype(np.float32)
    c["c_blk32"] = ((i[:, None] // 32) == (i[None, :] // 32)).astype(np.float32)
    c["c_rowgp"] = (((i[:, None] // 16) % 2) == (i[None, :] // 64)).astype(np.float32)
    return c


def make_in_maps(inp):
    f = lambda a: np.ascontiguousarray(np.asarray(a, dtype=np.float32))
    cst = _consts()
    shared = {}
    for k in ("ln1_g", "w_in", "mu_shift", "w0", "w2", "a0", "a2", "g2", "k_k", "k_a", "lnx_g", "lnx_b", "w_rw_out",
              "A_re", "A_im", "log_dt", "B_re", "B_im", "D_skip", "w_glu", "w_out", "ln2_g", "w_ffn_in", "conv_w",
              "conv_b", "w_ffn_out", "ln3_g", "w_ple_gate", "w_ple"):
        shared[k] = f(inp[k])[0]
    shared["r_k"] = f(inp["r_k"])[0].reshape(512)
    shared["C_re"] = f(inp["C_re"])[0].reshape(512, 64)
    shared["C_im"] = f(inp["C_im"])[0].reshape(512, 64)
    shared["final_g"] = f(inp["final_g"])
    shared.update(cst)
    xp, xs = f(inp["x_prompt"]), f(inp["x_sample"])
    pp, psm = f(inp["p_prompt"])[0], f(inp["p_sample"])[0]
    in_maps = []
    for c in range(8):
        sl = slice(NS * c, NS * c + NS)
        m = dict(shared)
        m["xall"] = np.concatenate([xp[c], xs[sl, 0]], 0)
        m["pall"] = np.concatenate([pp[c], psm[sl, 0]], 0)
        m["st_shift"] = f(inp["state_shift"])[0, sl]
        m["st_wkv"] = f(inp["state_wkv"])[0, sl].reshape(128, 4096)
        m["st_re"] = f(inp["state_ssm_re"])[0, sl].reshape(NS, 2048)
        m["st_im"] = f(inp["state_ssm_im"])[0, sl].reshape(NS, 2048)
        m["st_conv"] = f(inp["state_conv"])[0, sl]
        in_maps.append({k: np.ascontiguousarray(v) for k, v in m.items()})
    return in_maps


def kernel(**inp):
    f = lambda a: np.ascontiguousarray(np.asarray(a, dtype=np.float32))
    if "nc" not in _CACHE:
        _CACHE["nc"] = build_program()
    nc = _CACHE["nc"]
    in_maps = make_in_maps(inp)
    res = run_bass_kernel_spmd(nc, in_maps, core_ids=list(range(8)))
    R = res.results
    cat = lambda fn: np.stack([fn(r) for r in R], 0)
    y_prompt = cat(lambda r: r["y"][:T])
    y_sample = np.concatenate([r["y"][T:] for r in R], 0)[:, None, :]
    p_shift = cat(lambda r: r["p_shift"])[None]
    p_wkv = cat(lambda r: r["p_wkv"].reshape(8, 64, 64).transpose(0, 2, 1))[None]
    p_re = cat(lambda r: r["p_re"].reshape(32, 64))[None]
    p_im = cat(lambda r: r["p_im"].reshape(32, 64))[None]
    p_conv = cat(lambda r: r["p_conv"])[None]
    s_shift = np.concatenate([r["s_shift"] for r in R], 0)[None]
    s_wkv = np.concatenate([r["s_wkv"].reshape(NS, 8, 64, 64) for r in R], 0)[None]
    s_re = np.concatenate([r["s_re"].reshape(NS, 32, 64) for r in R], 0)[None]
    s_im = np.concatenate([r["s_im"].reshape(NS, 32, 64) for r in R], 0)[None]
    s_conv = np.concatenate([r["s_conv"] for r in R], 0)[None]
    outs = (y_prompt, y_sample, p_shift, p_wkv, p_re, p_im, p_conv, s_shift, s_wkv, s_re, s_im, s_conv)
    return tuple(np.ascontiguousarray(o.astype(np.float32)) for o in outs)
```

```python
import contextlib
import math
import numpy as np
import concourse.bass as bass
import concourse.mybir as mybir
from concourse.bass_utils import run_bass_kernel_spmd

F32 = mybir.dt.float32
BF16 = mybir.dt.bfloat16
AF = mybir.ActivationFunctionType
ALU = mybir.AluOpType
AX = mybir.AxisListType

T = 2048
NS = 16
NT = T + NS
D = 1024
CS = 8
C1 = math.exp(-0.5)
BLOCKS = [(0, 512), (512, 512), (1024, 512), (1536, 512), (2048, 16)]

ENGS = ("pe", "act", "dve", "pool", "sp")
NDSEM = 12


class _Op:
    __slots__ = ("eng", "fn", "deps", "dma", "observed", "tok", "idx", "dslot")

    def __init__(self, eng, fn, dma):
        self.eng, self.fn, self.dma = eng, fn, dma
        self.deps = set()
        self.observed = False
        self.tok = None
        self.dslot = None


class Sched:
    def __init__(self, nc):
        self.nc = nc
        self.ops = []
        self.last_w = {}
        self.readers = {}
        self.dma_rr = {e: 0 for e in ENGS}
        self.dma_prev = {}
        self.excl = set()

    def _add(self, eng, fn, reads, writes, dma):
        op = _Op(eng, fn, dma)
        op.idx = len(self.ops)
        if self.excl:
            ex = tuple(b for b in reads if b in self.excl)
            if ex:
                writes = tuple(writes) + ex
        for b in reads:
            w = self.last_w.get(b)
            if w is not None:
                op.deps.add(w)
        for b in writes:
            w = self.last_w.get(b)
            if w is not None:
                op.deps.add(w)
            for r in self.readers.get(b, ()):
                op.deps.add(r)
        if dma:
            slot = (eng, self.dma_rr[eng] % NDSEM)
            self.dma_rr[eng] += 1
            op.dslot = slot
            prev = self.dma_prev.get(slot)
            if prev is not None:
                op.deps.add(prev)
            self.dma_prev[slot] = op.idx
        op.deps.discard(op.idx)
        self.ops.append(op)
        for b in writes:
            self.last_w[b] = op.idx
            self.readers[b] = []
        for b in reads:
            if b not in writes:
                self.readers.setdefault(b, []).append(op.idx)
        return op.idx

    def emit(self):
        nc = self.nc
        ops = self.ops
        need = []
        for op in ops:
            nd = []
            for d in op.deps:
                p = ops[d]
                if (not p.dma) and (not op.dma) and p.eng == op.eng == "pe":
                    continue
                nd.append(d)
                p.observed = True
            need.append(nd)
        last = {}
        for op in ops:
            key = op.dslot if op.dma else op.eng
            last[key] = op.idx
        for i in last.values():
            ops[i].observed = True
        g = getattr(nc, "_gsem", None)
        if g is None:
            g = {"sems": {}, "cnt": {e: 0 for e in ENGS}, "dcnt": {}}
            nc._gsem = g
        cnt = g["cnt"]
        dcnt = g["dcnt"]
        for op in ops:
            if op.dma:
                dcnt[op.dslot] = dcnt.get(op.dslot, 0) + 16
                op.tok = (op.dslot, dcnt[op.dslot])
            elif op.observed:
                cnt[op.eng] += 1
                op.tok = (op.eng, cnt[op.eng])
        sems = g["sems"]
        for k in list(ENGS) + sorted(set(o.dslot for o in ops if o.dma)):
            if k not in sems:
                nm = k if isinstance(k, str) else "d_%s_%d" % k
                sems[k] = nc.alloc_semaphore(name="s_" + nm)
        with contextlib.ExitStack() as st:
            block = st.enter_context(nc.Block())
            per = {e: [o for o in ops if o.eng == e] for e in ENGS}
            hw = {"pe": block.tensor, "act": block.scalar, "dve": block.vector,
                  "pool": block.gpsimd, "sp": block.sync}

            def make(e):
                def body(eng):
                    seen = {}
                    for op in per[e]:
                        waits = {}
                        for d in need[op.idx]:
                            k, v = ops[d].tok
                            if v > waits.get(k, 0):
                                waits[k] = v
                        for k, v in waits.items():
                            if seen.get(k, 0) >= v:
                                continue
                            seen[k] = v
                            eng.wait_ge(sems[k], v)
                        ins = op.fn(eng)
                        if op.dma:
                            ins.then_inc(sems[op.tok[0]], 16)
                        elif op.observed:
                            ins.then_inc(sems[e], 1)
                    if e == "sp":
                        for key, i in last.items():
                            k, v = ops[i].tok
                            if seen.get(k, 0) < v:
                                eng.wait_ge(sems[k], v)
                return body

            for e in ENGS:
                hw[e](make(e))


def _L(x):
    if x is None:
        return ()
    if isinstance(x, str):
        return (x,)
    return tuple(x)


class Ph:
    _uid = [0]

    def __init__(self, nc, tag):
        self.nc = nc
        self.tag = tag
        self.st = contextlib.ExitStack()
        self.S = Sched(nc)

    def sb(self, name, shape, dt):
        return self.st.enter_context(self.nc.sbuf_tensor(self.tag + "_" + name, list(shape), dt))

    def ps(self, name, shape, dt):
        self.S.excl.add(name)
        return self.st.enter_context(self.nc.psum_tensor(self.tag + "_" + name, list(shape), dt))

    def finish(self):
        self.S.emit()
        self.st.close()

    def dbg(self, name, ap, shape, key, dt=F32):
        import os
        if os.environ.get("K_DBG_DUMP", "") == "":
            return
        t = self.nc.dram_tensor("dbg_" + name, list(shape), dt, kind="ExternalOutput").ap()
        self.dma("sp", t, ap, R=key)

    _rec = None

    def rec_begin(self):
        self._rec = []

    def rec_end(self):
        r, self._rec = self._rec, None
        return r

    def play(self, *streams):
        streams = [st_ for st_ in streams if st_]
        pos = [0] * len(streams)
        while True:
            best, bi = None, -1
            for i, st_ in enumerate(streams):
                if pos[i] < len(st_):
                    f = (pos[i] + 1.0) / len(st_)
                    if best is None or f < best:
                        best, bi = f, i
            if bi < 0:
                break
            eng, fn, R, W, dma = streams[bi][pos[bi]]
            pos[bi] += 1
            self.S._add(eng, fn, R, W, dma)

    def op(self, eng, fn, R=None, W=None):
        if self._rec is not None:
            self._rec.append((eng, fn, _L(R), _L(W), False))
        else:
            self.S._add(eng, fn, _L(R), _L(W), False)

    def dma(self, q, out, in_, R=None, W=None, slow=False):
        if slow:
            fn = lambda e: e.dma_start(out=out, in_=in_, allow_slow_non_contiguous=True)
        else:
            fn = lambda e: e.dma_start(out=out, in_=in_)
        if self._rec is not None:
            self._rec.append((q, fn, _L(R), _L(W), True))
        else:
            self.S._add(q, fn, _L(R), _L(W), True)

    def tt(self, eng, out, in0, in1, op, R=None, W=None):
        self.op(eng, lambda e: e.tensor_tensor(out=out, in0=in0, in1=in1, op=op), R, W)

    def ts(self, eng, out, in0, s1, op0, s2=None, op1=None, R=None, W=None):
        if op1 is None:
            self.op(eng, lambda e: e.tensor_scalar(out=out, in0=in0, scalar1=s1, scalar2=None, op0=op0), R, W)
        else:
            self.op(eng, lambda e: e.tensor_scalar(out=out, in0=in0, scalar1=s1, scalar2=s2, op0=op0, op1=op1), R, W)

    def stt(self, out, in0, scalar, in1, op0, op1, R=None, W=None):
        self.op("dve", lambda e: e.scalar_tensor_tensor(out=out, in0=in0, scalar=scalar, in1=in1, op0=op0, op1=op1), R, W)

    def act(self, out, in_, func, R=None, W=None, bias=None, scale=1.0, accum=None):
        kw = {}
        if bias is not None:
            kw["bias"] = bias
        if accum is not None:
            kw["accum_out"] = accum
        self.op("act", lambda e: e.activation(out=out, in_=in_, func=func, scale=scale, **kw), R, W)

    def cp(self, eng, out, in_, R=None, W=None):
        if eng == "act":
            self.op("act", lambda e: e.activation(out=out, in_=in_, func=AF.Copy), R, W)
        else:
            self.op(eng, lambda e: e.tensor_copy(out=out, in_=in_), R, W)

    def mm(self, out, lhsT, rhs, start, stop, R=None, W=None, tp=None):
        if tp is None:
            self.op("pe", lambda e: e.matmul(out, lhsT=lhsT, rhs=rhs, start=start, stop=stop), R, W)
        else:
            self.op("pe", lambda e: e.matmul(out, lhsT=lhsT, rhs=rhs, start=start, stop=stop, tile_position=tp), R, W)

    def tr(self, out, in_, ident, R=None, W=None):
        self.op("pe", lambda e: e.transpose(out, in_, ident), R, W)

    def memset(self, eng, ap, v, W=None):
        self.op(eng, lambda e: e.memset(ap, v), None, W)


def bc(ap, shape):
    return ap.to_broadcast(list(shape))


def rms_to_hT(ph, G, xt, P, gcol, hT, c0, tag, gkey):
    sq, ss, xn, pT = G["sq"], G["ss"], G["xn"], G["pT"]
    ph.act(sq[:P, :], xt[:P, :], AF.Square, R="xt" + tag, W=["sq", "ss"], accum=ss[:P, 0:1])
    ph.act(ss[:P, 1:2], ss[:P, 0:1], AF.Sqrt, R=["ss", "eps"], W="ss", bias=G["eps"][:P, 0:1], scale=1.0 / D)
    ph.op("dve", lambda e: e.reciprocal(out=ss[:P, 2:3], in_=ss[:P, 1:2]), R="ss", W="ss")
    ph.ts("dve", xn[:P, :], xt[:P, :], ss[:P, 2:3], ALU.mult, R=["xt" + tag, "ss"], W="xn")
    for k in range(8):
        ph.tr(pT[:, k, :P], xn[:P, k * 128:(k + 1) * 128], G["identb"][:P, :P], R=["xn", "identb"], W="pT")
    ph.tt("dve", hT[:, :, c0:c0 + P], pT[:, :, :P], bc(gcol[:, :].unsqueeze(2), [128, 8, P]), ALU.mult,
          R=["pT", gkey], W="hT")


def load_col(ph, dst, src1d, n, key):
    ph.dma("sp", dst, src1d.rearrange("(k p) -> p k", p=128), W=key, slow=True)


def norm_scratch(ph, G0):
    G = dict(G0)
    G["sq"] = ph.sb("sq", [128, D], F32)
    G["ss"] = ph.sb("ss", [128, 4], F32)
    G["xn"] = ph.sb("xn", [128, D], BF16)
    G["pT"] = ph.ps("pT", [128, 8, 128], BF16)
    G["eps"] = ph.sb("eps", [128, 1], F32)
    ph.memset("dve", G["eps"][:], 1e-6, W="eps")
    return G


def build_program(upto=9, debug=False):
    nc = bass.Bass("TRN2", target_bir_lowering=False)
    I = {}

    def inp(name, shape, dt=F32):
        I[name] = nc.dram_tensor(name, list(shape), dt, kind="ExternalInput").ap()

    def outp(name, shape):
        I[name] = nc.dram_tensor(name, list(shape), F32, kind="ExternalOutput").ap()

    def scratch(name, shape, dt):
        if debug:
            I[name] = nc.dram_tensor(name, list(shape), dt, kind="ExternalOutput").ap()
        else:
            I[name] = nc.dram_tensor(name, list(shape), dt).ap()
    if debug:
        scratch("d_BwT", [128, 4 * CS * 2 * 128], BF16); scratch("d_Kmat", [128, 4 * CS * 128], BF16)
        scratch("d_CwT", [128, CS * 2 * 16 * 32], BF16); scratch("d_Abar", [128, 64], F32)

    inp("xall", [NT, D]); inp("pall", [NT, 256])
    inp("st_shift", [NS, 1792]); inp("st_wkv", [128, 4096]); inp("st_re", [NS, 2048]); inp("st_im", [NS, 2048])
    inp("st_conv", [NS, 2, 2816])
    inp("ln1_g", [D]); inp("w_in", [D, 4352]); inp("mu_shift", [1792]); inp("w0", [512]); inp("w2", [64, 512])
    inp("a0", [512]); inp("a2", [64, 512]); inp("g2", [128, 512]); inp("k_k", [512]); inp("k_a", [512])
    inp("r_k", [512]); inp("lnx_g", [512]); inp("lnx_b", [512]); inp("w_rw_out", [512, D])
    inp("A_re", [32, 64]); inp("A_im", [32, 64]); inp("log_dt", [32]); inp("B_re", [32, 64, 16]); inp("B_im", [32, 64, 16])
    inp("C_re", [512, 64]); inp("C_im", [512, 64]); inp("D_skip", [512]); inp("w_glu", [512, 2048]); inp("w_out", [D, D])
    inp("ln2_g", [D]); inp("w_ffn_in", [D, 5632]); inp("conv_w", [3, 2816]); inp("conv_b", [2816]); inp("w_ffn_out", [2816, D])
    inp("ln3_g", [D]); inp("w_ple_gate", [D, D]); inp("w_ple", [256, D]); inp("final_g", [D])
    inp("c_ident", [128, 128]); inp("c_msl", [128, 128]); inp("c_msu", [128, 128]); inp("c_mui", [128, 128])
    inp("c_blk64", [128, 128]); inp("c_blk32", [128, 128]); inp("c_rowgp", [128, 128])
    outp("y", [NT, D]); outp("p_shift", [1792]); outp("p_wkv", [512, 64]); outp("p_re", [2048]); outp("p_im", [2048])
    outp("p_conv", [2, 2816]); outp("s_shift", [NS, 1792]); outp("s_wkv", [128, 4096]); outp("s_re", [NS, 2048])
    outp("s_im", [NS, 2048]); outp("s_conv", [NS, 2, 2816])
    scratch("PRW", [1792, NT], F32); scratch("UU", [512, NT], F32); scratch("GT", [2048, NT], BF16)
    scratch("YF", [512, NT], BF16); scratch("ZZ", [512, NT], BF16); scratch("X1", [NT, D], F32); scratch("X2", [NT, D], F32)
    scratch("SW", [6, NS, 512], F32); scratch("SY", [128, 64], F32)

    with contextlib.ExitStack() as gst:
        def gsb(name, shape, dt):
            return gst.enter_context(nc.sbuf_tensor("g_" + name, list(shape), dt))
        G0 = {}
        G0["identb"] = gsb("identb", [128, 128], BF16)
        G0["identf"] = gsb("identf", [128, 128], F32)
        with contextlib.ExitStack() as g2:
            def g2sb(name, shape, dt):
                return g2.enter_context(nc.sbuf_tensor("g_" + name, list(shape), dt))
            G0["BwT"] = g2sb("BwT", [128, 4, CS, 2, 128], BF16)
            G0["Kmat"] = g2sb("Kmat", [128, 4, CS, 128], BF16)
            G0["CwT"] = g2sb("CwT", [128, CS, 2, 16, 32], BF16)
            G0["Abar"] = g2sb("Abar", [128, 2, 2, 16], F32)
            if upto >= 1:
                phase1(nc, I, G0, debug)
            else:
                phase0(nc, I, G0, debug)
            if upto >= 2:
                phase2(nc, I, G0, True)
            if upto >= 2.5:
                phase2(nc, I, G0, False)
        g4 = contextlib.ExitStack()
        WFI = g4.enter_context(nc.sbuf_tensor("g_wfi", [128, 8, 5632], BF16))
        if upto >= 3:
            phase3(nc, I, G0, None, WFI)
        if upto >= 4:
            phase4(nc, I, G0, WFI)
        g4.close()
        if upto >= 5:
            phase5(nc, I, G0)
    return nc


def phase0(nc, I, G0, debug=False, ph=None):
    own = ph is None
    if own:
        ph = Ph(nc, "p0")
        ph.dma("pool", G0["identb"][:], I["c_ident"], W="identb")
        ph.dma("sp", G0["identf"][:], I["c_ident"], W="identf")
    sb = ph.sb
    lr = sb("lr", [128, 16], F32); li = sb("li", [128, 16], F32); dtl = sb("dtl", [128, 16], F32)
    Bre = sb("Bre", [128, 16, 16], F32); Bim = sb("Bim", [128, 16, 16], F32)
    ph.dma("sp", lr[:], I["A_re"].rearrange("(P gp) n -> (gp n) P", gp=2), W="lr", slow=True)
    ph.dma("sp", li[:], I["A_im"].rearrange("(P gp) n -> (gp n) P", gp=2), W="li", slow=True)
    ldt2 = I["log_dt"].rearrange("(P gp) -> gp P", gp=2)
    for gp in range(2):
        ph.dma("sp", dtl[64 * gp:64 * gp + 64, :], ldt2[gp].partition_broadcast(64), W="dtl", slow=True)
    ph.dma("sp", Bre[:], I["B_re"].rearrange("(P gp) n c -> (gp n) P c", gp=2), W="Bre")
    ph.dma("sp", Bim[:], I["B_im"].rearrange("(P gp) n c -> (gp n) P c", gp=2), W="Bim")
    rowgp = sb("rowgp", [128, 128], F32); blk32 = sb("blk32", [128, 128], F32)
    ph.dma("sp", rowgp[:], I["c_rowgp"], W="rowgp"); ph.dma("sp", blk32[:], I["c_blk32"], W="blk32")
    CT = [sb("CTr", [128, 4, 128], F32), sb("CTi", [128, 4, 128], F32)]
    c2 = sb("c2", [128, 128], F32)
    pA = ph.ps("pA", [128, 4, 128], F32)
    for ri, nm in enumerate(("C_re", "C_im")):
        for k in range(4):
            src = I[nm][k * 128:(k + 1) * 128, :]
            ph.dma("sp", c2[:, 0:64], src, W="c2"); ph.dma("sp", c2[:, 64:128], src, W="c2")
            ph.tt("dve", c2[:], c2[:], rowgp[:], ALU.mult, R=["c2", "rowgp"], W="c2")
            ph.tr(pA[:, k, :], c2[:], G0["identf"][:], R=["c2", "identf"], W="pA")
        ph.cp("dve", CT[ri][:], pA[:], R="pA", W="CT%d" % ri)
    t = {n: sb(n, [128, 16], F32) for n in ("dt", "e1", "mag", "ang", "sa", "ca", "sinv", "cosv", "ar", "ai", "den",
                                             "rden", "am1", "fr", "fi", "t1", "t2")}
    V = "dve"
    K = lambda *n: list(n)
    hpi = sb("hpi", [128, 1], F32)
    ph.memset(V, hpi[:], math.pi / 2, W="hpi")
    ph.act(t["dt"][:], dtl[:], AF.Exp, R="dtl", W="dt")
    ph.tt(V, t["e1"][:], lr[:], t["dt"][:], ALU.mult, R=K("lr", "dt"), W="e1")
    ph.act(t["mag"][:], t["e1"][:], AF.Exp, R="e1", W="mag")
    ph.tt(V, t["ang"][:], li[:], t["dt"][:], ALU.mult, R=K("li", "dt"), W="ang")
    ph.ts(V, t["sa"][:], t["ang"][:], 1.0 / 64, ALU.mult, R="ang", W="sa")
    ph.act(t["sinv"][:], t["sa"][:], AF.Sin, R="sa", W="sinv")
    ph.act(t["cosv"][:], t["sa"][:], AF.Sin, R=["sa", "hpi"], W="cosv", bias=hpi[:, 0:1])
    for _ in range(6):
        ph.tt(V, t["t1"][:], t["cosv"][:], t["cosv"][:], ALU.mult, R="cosv", W="t1")
        ph.tt(V, t["t2"][:], t["sinv"][:], t["sinv"][:], ALU.mult, R="sinv", W="t2")
        ph.stt(t["sinv"][:], t["cosv"][:], 2.0, t["sinv"][:], ALU.mult, ALU.mult, R=["cosv", "sinv", "t2"], W="sinv")
        ph.tt(V, t["cosv"][:], t["t1"][:], t["t2"][:], ALU.subtract, R=["t1", "t2", "sinv"], W="cosv")
    ph.tt(V, t["ar"][:], t["mag"][:], t["cosv"][:], ALU.mult, R=K("mag", "cosv"), W="ar")
    ph.tt(V, t["ai"][:], t["mag"][:], t["sinv"][:], ALU.mult, R=K("mag", "sinv"), W="ai")
    ph.tt(V, t["den"][:], lr[:], lr[:], ALU.mult, R="lr", W="den")
    ph.tt(V, t["t1"][:], li[:], li[:], ALU.mult, R="li", W="t1")
    ph.tt(V, t["den"][:], t["den"][:], t["t1"][:], ALU.add, R=K("den", "t1"), W="den")
    ph.op(V, lambda e: e.reciprocal(out=t["rden"][:], in_=t["den"][:]), R="den", W="rden")
    ph.ts(V, t["am1"][:], t["ar"][:], -1.0, ALU.add, R="ar", W="am1")
    ph.tt(V, t["t1"][:], t["am1"][:], lr[:], ALU.mult, R=K("am1", "lr", "den"), W="t1")
    ph.tt(V, t["t2"][:], t["ai"][:], li[:], ALU.mult, R=K("ai", "li"), W="t2")
    ph.tt(V, t["t1"][:], t["t1"][:], t["t2"][:], ALU.add, R=K("t1", "t2"), W="t1")
    ph.tt(V, t["fr"][:], t["t1"][:], t["rden"][:], ALU.mult, R=K("t1", "rden"), W="fr")
    ph.tt(V, t["t1"][:], t["ai"][:], lr[:], ALU.mult, R=K("ai", "lr", "fr"), W="t1")
    ph.tt(V, t["t2"][:], t["am1"][:], li[:], ALU.mult, R=K("am1", "li"), W="t2")
    ph.tt(V, t["t1"][:], t["t1"][:], t["t2"][:], ALU.subtract, R=K("t1", "t2"), W="t1")
    ph.tt(V, t["fi"][:], t["t1"][:], t["rden"][:], ALU.mult, R=K("t1", "rden"), W="fi")
    pwr = sb("pwr", [128, CS + 1, 16], F32); pwi = sb("pwi", [128, CS + 1, 16], F32)
    ph.memset(V, pwr[:, 0, :], 1.0, W="pw"); ph.memset(V, pwi[:, 0, :], 0.0, W="pw")
    for e in range(CS):
        ph.tt(V, t["t1"][:], pwr[:, e, :], t["ar"][:], ALU.mult, R=K("pw", "ar", "fi"), W="t1")
        ph.tt(V, t["t2"][:], pwi[:, e, :], t["ai"][:], ALU.mult, R=K("pw", "ai"), W="t2")
        ph.tt(V, pwr[:, e + 1, :], t["t1"][:], t["t2"][:], ALU.subtract, R=K("t1", "t2"), W="pw")
        ph.tt(V, t["t1"][:], pwr[:, e, :], t["ai"][:], ALU.mult, R=K("pw", "ai"), W="t1")
        ph.tt(V, t["t2"][:], pwi[:, e, :], t["ar"][:], ALU.mult, R=K("pw", "ar"), W="t2")
        ph.tt(V, pwi[:, e + 1, :], t["t1"][:], t["t2"][:], ALU.add, R=K("t1", "t2"), W="pw")
    Ab = G0["Abar"]
    ph.cp(V, Ab[:, 0, 0, :], pwr[:, CS, :], R="pw", W="Abar"); ph.cp(V, Ab[:, 0, 1, :], pwi[:, CS, :], R="pw", W="Abar")
    ph.cp(V, Ab[:, 1, 0, :], pwr[:, 1, :], R="pw", W="Abar"); ph.cp(V, Ab[:, 1, 1, :], pwi[:, 1, :], R="pw", W="Abar")
    bbr = sb("bbr", [128, 16, 16], F32); bbi = sb("bbi", [128, 16, 16], F32)
    u1 = sb("u1", [128, 16, 16], F32); u2 = sb("u2", [128, 16, 16], F32)
    frb = bc(t["fr"][:, :].unsqueeze(2), [128, 16, 16]); fib = bc(t["fi"][:, :].unsqueeze(2), [128, 16, 16])
    ph.tt(V, u1[:], Bre[:], frb, ALU.mult, R=K("Bre", "fr"), W="u1")
    ph.tt(V, u2[:], Bim[:], fib, ALU.mult, R=K("Bim", "fi"), W="u2")
    ph.tt(V, bbr[:], u1[:], u2[:], ALU.subtract, R=K("u1", "u2"), W="bbr")
    ph.tt(V, u1[:], Bim[:], frb, ALU.mult, R=K("Bim", "fr", "bbr"), W="u1")
    ph.tt(V, u2[:], Bre[:], fib, ALU.mult, R=K("Bre", "fi", "bbr"), W="u2")
    ph.tt(V, bbi[:], u1[:], u2[:], ALU.add, R=K("u1", "u2"), W="bbi")
    Ew = sb("Ew", [128, CS, 2, 16, 2, 16], F32)
    ph.memset(V, Ew[:].rearrange("p a b c d e -> p (a b c d e)"), 0.0, W="Ew")
    for e in range(CS):
        pr = bc(pwr[:, e, :].unsqueeze(2), [128, 16, 16]); pi = bc(pwi[:, e, :].unsqueeze(2), [128, 16, 16])
        ph.tt(V, u1[:], bbr[:], pr, ALU.mult, R=K("bbr", "pw", "Ew"), W="u1")
        ph.tt(V, u2[:], bbi[:], pi, ALU.mult, R=K("bbi", "pw", "Ew"), W="u2")
        ph.tt(V, u1[:], u1[:], u2[:], ALU.subtract, R=K("u1", "u2"), W="u1")
        for gp in range(2):
            ph.cp(V, Ew[64 * gp:64 * gp + 64, e, 0, :, gp, :], u1[64 * gp:64 * gp + 64, :, :], R="u1", W="Ew")
        ph.tt(V, u1[:], bbr[:], pi, ALU.mult, R=K("bbr", "pw", "Ew"), W="u1")
        ph.tt(V, u2[:], bbi[:], pr, ALU.mult, R=K("bbi", "pw", "Ew"), W="u2")
        ph.tt(V, u1[:], u1[:], u2[:], ALU.add, R=K("u1", "u2"), W="u1")
        for gp in range(2):
            ph.cp(V, Ew[64 * gp:64 * gp + 64, e, 1, :, gp, :], u1[64 * gp:64 * gp + 64, :, :], R="u1", W="Ew")
    CTin = sb("CTin", [128, 4, 128], F32)
    ph.ts(V, CTin[:], CT[1][:], -1.0, ALU.mult, R="CT1", W="CTin")
    pB = [ph.ps("pB%d" % i, [128, 4, 128], F32) for i in range(2)]
    n = 0
    for j in range(CS):
        e = CS - 1 - j
        for ri in range(2):
            pb = pB[n % 2]; n += 1
            for k in range(4):
                src = Ew[:, e, ri, 4 * k:4 * k + 4, :, :].rearrange("p a b c -> p (a b c)")
                ph.tr(pb[:, k, :], src, G0["identf"][:], R=["Ew", "identf"], W="pB%d" % ((n - 1) % 2))
            ph.cp("act" if n % 2 else "dve", G0["BwT"][:, :, j, ri, :], pb[:], R="pB%d" % ((n - 1) % 2), W="BwT")
    for tau in range(CS):
        pb = pB[n % 2]; key = "pB%d" % (n % 2); n += 1
        for k in range(4):
            lr_ = Ew[:, tau, 0, 4 * k:4 * k + 4, :, :].rearrange("p a b c -> p (a b c)")
            li_ = Ew[:, tau, 1, 4 * k:4 * k + 4, :, :].rearrange("p a b c -> p (a b c)")
            ph.mm(pb[:, k, :], lr_, CT[0][:, k, :], True, False, R=["Ew", "CT0"], W=key)
            ph.mm(pb[:, k, :], li_, CTin[:, k, :], False, True, R=["Ew", "CTin"], W=key)
        ph.tt(V, G0["Kmat"][:, :, tau, :], pb[:], bc(blk32[:, :].unsqueeze(1), [128, 4, 128]), ALU.mult,
              R=[key, "blk32"], W="Kmat")
    w1 = sb("w1", [128, 16, 32], F32); w2_ = sb("w2", [128, 16, 32], F32)
    CTr3 = CT[0][:].rearrange("p k (a b) -> p (k a) b", a=4); CTi3 = CT[1][:].rearrange("p k (a b) -> p (k a) b", a=4)
    for i in range(CS):
        pr = bc(pwr[:, i + 1, :].unsqueeze(2), [128, 16, 32]); pi = bc(pwi[:, i + 1, :].unsqueeze(2), [128, 16, 32])
        ph.tt(V, w1[:], CTr3, pr, ALU.mult, R=K("CT0", "pw", "CwT"), W="w1")
        ph.tt(V, w2_[:], CTi3, pi, ALU.mult, R=K("CT1", "pw", "CwT"), W="w2")
        ph.tt(V, G0["CwT"][:, i, 0, :, :], w1[:], w2_[:], ALU.subtract, R=K("w1", "w2"), W="CwT")
        ph.tt(V, w1[:], CTr3, pi, ALU.mult, R=K("CT0", "pw", "CwT"), W="w1")
        ph.tt(V, w2_[:], CTi3, pr, ALU.mult, R=K("CT1", "pw", "CwT"), W="w2")
        ph.tt(V, w1[:], w1[:], w2_[:], ALU.add, R=K("w1", "w2"), W="w1")
        ph.ts(V, G0["CwT"][:, i, 1, :, :], w1[:], -1.0, ALU.mult, R="w1", W="CwT")
    if debug:
        ph.dma("sp", I["d_BwT"], G0["BwT"][:].rearrange("p a b c d -> p (a b c d)"), R="BwT")
        ph.dma("sp", I["d_Kmat"], G0["Kmat"][:].rearrange("p a b c -> p (a b c)"), R="Kmat")
        ph.dma("sp", I["d_CwT"], G0["CwT"][:].rearrange("p a b c d -> p (a b c d)"), R="CwT")
        ph.dma("sp", I["d_Abar"], G0["Abar"][:].rearrange("p a b c -> p (a b c)"), R="Abar")
    if own:
        ph.finish()


def phase1(nc, I, G0, debug=False):
    ph = Ph(nc, "p1")
    win = ph.sb("win", [128, 8, 4352], BF16)
    for k in range(8):
        ph.dma("pool", win[:, k, :], I["w_in"][k * 128:(k + 1) * 128, :], W="win%d" % k)
    ph.dma("pool", G0["identb"][:], I["c_ident"], W="identb")
    ph.dma("sp", G0["identf"][:], I["c_ident"], W="identf")
    ph.rec_begin()
    phase0(nc, I, G0, debug, ph=ph)
    s0 = ph.rec_end()
    ph.rec_begin()
    G = norm_scratch(ph, G0)
    g1c = ph.sb("g1c", [128, 8], F32)
    load_col(ph, g1c[:], I["ln1_g"], 8, "g1c")
    hT = ph.sb("hT", [128, 8, 512], BF16)
    xts = [ph.sb("xt%d" % i, [128, D], F32) for i in range(2)]
    pm = [ph.ps("pm%d" % i, [128, 512], F32) for i in range(4)]
    stf = [ph.sb("stf%d" % i, [128, 512], F32) for i in range(4)]
    stb = [ph.sb("stb%d" % i, [128, 512], BF16) for i in range(3)]
    WK = ["win%d" % k for k in range(8)]
    nx = nf = nb = npm = 0
    for (t0, nt) in BLOCKS:
        P = min(128, nt)
        for s in range((nt + 127) // 128):
            xt = xts[nx % 2]; tg = str(nx % 2); nx += 1
            ph.dma("sp", xt[:P, :], I["xall"][t0 + s * 128:t0 + s * 128 + P, :], W="xt" + tg)
            rms_to_hT(ph, G, xt, P, g1c, hT, s * 128, tg, "g1c")
        for m in range(34):
            pb = pm[npm % 4]; pk = "pm%d" % (npm % 4); npm += 1
            for k in range(8):
                ph.mm(pb[:, :nt], win[:, k, m * 128:(m + 1) * 128], hT[:, k, :nt], k == 0, k == 7,
                      R=["win%d" % k, "hT"], W=pk)
            if m < 18:
                sf = stf[nf % 4]; sk = "stf%d" % (nf % 4); nf += 1
                ph.cp("dve" if m % 2 else "act", sf[:, :nt], pb[:, :nt], R=pk, W=sk)
                if m < 14:
                    ph.dma("sp", I["PRW"][m * 128:(m + 1) * 128, t0:t0 + nt], sf[:, :nt], R=sk)
                else:
                    ph.dma("sp", I["UU"][(m - 14) * 128:(m - 13) * 128, t0:t0 + nt], sf[:, :nt], R=sk)
            else:
                sbf = stb[nb % 3]; sk = "stb%d" % (nb % 3); nb += 1
                ph.act(sbf[:, :nt], pb[:, :nt], AF.Sigmoid, R=pk, W=sk)
                ph.dma("act", I["GT"][(m - 18) * 128:(m - 17) * 128, t0:t0 + nt], sbf[:, :nt], R=sk)
    s1 = ph.rec_end()
    ph.play(s1, s0)
    ph.finish()


def alloc_w3(nc, st):
    t = lambda n, shp: st.enter_context(nc.sbuf_tensor("w3_" + n, shp, BF16))
    return {"rwo": t("rwo", [128, 4, D]), "glu": t("glu", [128, 4, 2048]), "wo": t("wo", [128, 8, D])}


def load_w3(ph, I, W3):
    for k in range(4):
        ph.dma("pool", W3["rwo"][:, k, :], I["w_rw_out"][k * 128:(k + 1) * 128, :], W="rwo")
        ph.dma("pool", W3["glu"][:, k, :], I["w_glu"][k * 128:(k + 1) * 128, :], W="glu")
    for k in range(8):
        ph.dma("pool", W3["wo"][:, k, :], I["w_out"][k * 128:(k + 1) * 128, :], W="wo")


def phase2(nc, I, G0, prompt, W3=None):
    ph = Ph(nc, "p2a" if prompt else "p2b")
    sb, ps = ph.sb, ph.ps
    V = "dve"
    if W3 is not None:
        load_w3(ph, I, W3)
    ph._s5tmp = [sb("s5a", [128, 2, 16], F32), sb("s5b", [128, 2, 16], F32)]
    ph._s5xb = sb("Xb", [128, 2, 16, 64], BF16)
    ph._s5du = sb("s5du", [128, 512], F32)
    if prompt:
        msl = sb("msl", [128, 128], BF16); msu = sb("msu", [128, 128], BF16); mui = sb("mui", [128, 128], BF16)
        ph.dma("pool", msl[:], I["c_msl"], W="msl"); ph.dma("pool", msu[:], I["c_msu"], W="msu")
        ph.dma("pool", mui[:], I["c_mui"], W="mui")
    blk64 = sb("blk64", [128, 128], F32); ph.dma("sp", blk64[:], I["c_blk64"], W="blk64")
    w2a2 = sb("w2a2", [128, 512], BF16); g2b = sb("g2b", [128, 512], BF16)
    ph.dma("pool", w2a2[0:64, :], I["w2"], W="w2a2"); ph.dma("pool", w2a2[64:128, :], I["a2"], W="w2a2")
    ph.dma("pool", g2b[:], I["g2"], W="g2b")
    pc = {}
    for nm, n in (("mu_shift", 14), ("w0", 4), ("a0", 4), ("k_k", 4), ("k_a", 4), ("r_k", 4), ("lnx_g", 4),
                  ("lnx_b", 4), ("D_skip", 4)):
        pc[nm] = sb("c_" + nm, [128, n], F32)
        load_col(ph, pc[nm][:], I[nm], n, "c_" + nm)
    PK = ["c_" + k for k in pc]
    scm = sb("scm", [128, 4, 128], F32)
    ph.memset(V, scm[:].rearrange("p a b -> p (a b)"), 1.0, W="scm"); ph.memset(V, scm[:, :, 0:1], 0.0, W="scm")
    eps_gn = sb("eps_gn", [128, 1], F32); ph.memset(V, eps_gn[:], 64e-5, W="eps_gn")
    if prompt:
        Sst = sb("Sst", [128, 4, 64], F32); Sbd = sb("Sbd", [128, 4, 128], BF16)
        ph.memset(V, Sst[:].rearrange("p a b -> p (a b)"), 0.0, W="Sst")
        ph.memset(V, Sbd[:].rearrange("p a b -> p (a b)"), 0.0, W="Sbd")
        Xs = sb("Xs", [128, 2, 16, 65], F32)
        ph.memset(V, Xs[:].rearrange("p a b c -> p (a b c)"), 0.0, W="Xs")
        Pf = sb("Pf", [128, 14, 513], F32)
        ph.memset(V, Pf[:, :, 0:1], 0.0, W="Pf")
    WB = 512 if prompt else NS
    WC = 128 if prompt else NS
    uf = sb("uf", [128, 4, WB], F32); ub = sb("ub", [128, 4, WB], BF16)
    YFb = sb("YFb", [128, 4, WB], BF16); ZZb = sb("ZZb", [128, 4, WB], BF16)
    f4 = lambda n: sb(n, [128, 4, WC], F32)
    b4 = lambda n: sb(n, [128, 4, WC], BF16)
    XS = sb("XS", [128, 14, WC], F32); dd = sb("dd", [128, 14, WC], F32)
    lin = sb("lin", [128, WC], BF16); sgx = sb("sgx", [128, WC], BF16)
    sig = f4("sig"); aa = f4("aa"); gg = f4("gg"); kk0 = f4("kk0"); tq = f4("tq"); rn = f4("rn"); kkn = f4("kkn")
    bb = f4("bb"); kmod = f4("kmod"); bon = f4("bon"); cs = f4("cs"); ex1 = f4("ex1"); ex2 = f4("ex2"); ex3 = f4("ex3")
    nbias = sb("nbias", [128, 4], F32); PCt = sb("PCt", [128, 4], F32)
    gns = f4("gns")
    KX = {n_: n_ for n_ in ("rT", "kT", "bT", "aT", "khT", "bhT", "vT", "PCt", "bon", "gg")}
    if prompt:
        rT = b4("rT"); kT = b4("kT"); bT = b4("bT"); aT = b4("aT"); khT = b4("khT"); bhT = b4("bhT"); vT = b4("vT")
        alt = {"rT": b4("rT1"), "kT": b4("kT1"), "bT": b4("bT1"), "aT": b4("aT1"), "khT": b4("khT1"),
               "bhT": b4("bhT1"), "vT": b4("vT1"), "PCt": sb("PCt1", [128, 4], F32), "bon": f4("bon1"), "gg": f4("gg1")}
        Vtok = sb("Vtok", [128, 512], BF16); Khtok = sb("Khtok", [128, 512], BF16); Bhtok = sb("Bhtok", [128, 512], BF16)
        h8 = lambda n: sb(n, [128, 8, 128], BF16)
        Nb = [h8("Nb0"), h8("Nb1")]; Lb = [h8("Lb0"), h8("Lb1")]; Mt = [h8("Mt0"), h8("Mt1")]
        LKb = h8("LKb"); Arb = h8("Arb"); Ark = h8("Ark")
        Wbf = sb("Wbf", [128, 512], BF16); Ubf = sb("Ubf", [128, 512], BF16)
        tS = sb("tS", [128, 4, 64], F32)
    Ysb = sb("Ysb", [128, 8, 64], F32); Ysq = sb("Ysq", [128, 8, 64], F32); ynb = sb("ynb", [128, 8, 64], BF16)
    gn = sb("gn", [128, 6, 8], F32)
    pF = [ps("pF%d" % i, [128, 4, 128], F32) for i in range(6)]
    pT = [ps("pTb%d" % i, [128, 8, 128], BF16) for i in range(2)]
    cnt = {"f": 0, "t": 0}

    def getF():
        i = cnt["f"] % 6; cnt["f"] += 1
        return pF[i], "pF%d" % i

    def mkpool(base):
        st_ = {"n": 0}

        def get():
            i = base + st_["n"] % 2; st_["n"] += 1
            return pF[i], "pF%d" % i
        return get
    getF_prep, getF_core, getFs = mkpool(0), mkpool(2), mkpool(4)

    def getT():
        i = cnt["t"] % 2; cnt["t"] += 1
        return pT[i], "pTb%d" % i

    ib = G0["identb"]

    if not prompt:
        sample_mixer(ph, I, G0, locals())
        ph.finish()
        return
    Lbase = dict(locals())
    Lpar = [dict(Lbase), dict(Lbase)]
    Lpar[1].update(alt)
    Lpar[1]["KX"] = {n_: n_ + "1" for n_ in KX}
    for bi, (t0, nt) in enumerate(BLOCKS[:4]):
        if bi > 0:
            ph.cp(V, Pf[:, :, 0:1], Pf[:, :, 512:513], R="Pf", W="Pf")
        ph.dma("sp", Pf[:, :, 1:513], I["PRW"][:, t0:t0 + nt].rearrange("(m p) t -> p m t", p=128), W="Pf")
        ph.dma("act", uf[:], I["UU"][:, t0:t0 + nt].rearrange("(m p) t -> p m t", p=128), W="uf")
        ph.cp("act", ub[:].rearrange("p a b -> p (a b)"), uf[:].rearrange("p a b -> p (a b)"), R="uf", W="ub")
        if bi == 3:
            ph.dma("sp", I["p_shift"].rearrange("(m p) -> p m", p=128), Pf[:, :, 512], R="Pf", slow=True)
        ph.rec_begin()
        s5_block(ph, I, G0, pc, Xs, ub, ZZb, getFs, nchunk=64, which=0, ncol=512)
        ph.dma("act", I["ZZ"][:, t0:t0 + nt].rearrange("(m p) t -> p m t", p=128), ZZb[:], R="ZZb")
        s5s = ph.rec_end()
        preps, cores = [], []
        for c in range(4):
            c0 = c * 128
            Lc = dict(Lpar[c % 2]); Lc["getF"] = getF_prep
            Lk = dict(Lpar[c % 2]); Lk["getF"] = getF_core
            ph.rec_begin()
            ph.tt(V, dd[:], Pf[:, :, c0:c0 + 128], Pf[:, :, c0 + 1:c0 + 129], ALU.subtract, R="Pf", W="dd")
            ph.tt(V, dd[:], dd[:], bc(pc["mu_shift"][:, :].unsqueeze(2), [128, 14, 128]), ALU.mult,
                  R=["dd", "c_mu_shift"], W="dd")
            ph.tt(V, XS[:], dd[:], Pf[:, :, c0 + 1:c0 + 129], ALU.add, R=["dd", "Pf"], W="XS")
            rwkv_prep_and_core(ph, Lc, c, c0)
            preps.append(ph.rec_end())
            ph.rec_begin()
            wkv_core(ph, Lk, c, c0)
            cores.append(ph.rec_end())
        q = (len(s5s) + 3) // 4
        s5p = [s5s[i * q:(i + 1) * q] for i in range(4)]
        ph.play(preps[0])
        ph.play(cores[0], preps[1], s5p[0])
        ph.play(cores[1], preps[2], s5p[1])
        ph.play(cores[2], preps[3], s5p[2])
        ph.play(cores[3], s5p[3])
        ph.dma("sp", I["YF"][:, t0:t0 + nt].rearrange("(m p) t -> p m t", p=128), YFb[:], R="YFb")
    ph.dma("sp", I["p_wkv"].rearrange("(m p) v -> p m v", p=128), Sst[:], R="Sst")
    ph.dma("sp", I["p_re"].rearrange("(P p) -> p P", p=128), Xs[:, 0, :, 0], R="Xs", slow=True)
    ph.dma("sp", I["p_im"].rearrange("(P p) -> p P", p=128), Xs[:, 1, :, 0], R="Xs", slow=True)
    ph.finish()


def rwkv_prep_and_core(ph, L, c, c0):
    V = "dve"
    KX = L["KX"]
    pc = L["pc"]; XS = L["XS"]; getF = L["getF"]; getT = L["getT"]; ib = L["ib"]
    sig, aa, gg, kk0, tq, rn, kkn = L["sig"], L["aa"], L["gg"], L["kk0"], L["tq"], L["rn"], L["kkn"]
    bb, kmod, bon, cs, ex1, ex2, ex3 = L["bb"], L["kmod"], L["bon"], L["cs"], L["ex1"], L["ex2"], L["ex3"]
    rT, kT, bT, aT, khT, bhT, vT = L["rT"], L["kT"], L["bT"], L["aT"], L["khT"], L["bhT"], L["vT"]
    lin, sgx, w2a2, g2b, blk64 = L["lin"], L["sgx"], L["w2a2"], L["g2b"], L["blk64"]
    nbias, PCt, scm = L["nbias"], L["PCt"], L["scm"]
    r_ = XS[:, 0:4, :]; k_ = XS[:, 4:8, :]; v_ = XS[:, 8:12, :]
    B4 = lambda t: bc(t[:, :].unsqueeze(2), [128, 4, 128])
    fl = lambda t: t[:].rearrange("p a b -> p (a b)")
    ph.act(lin[0:64, :], XS[0:64, 12, :], AF.Tanh, R="XS", W="lin")
    ph.cp("act", lin[64:128, :], XS[64:128, 12, :], R="XS", W="lin")
    ph.act(sgx[:], XS[:, 13, :], AF.Sigmoid, R="XS", W="sgx")
    pw_, kw_ = getF()
    for m in range(4):
        ph.mm(pw_[:, m, :], w2a2[0:64, m * 128:(m + 1) * 128], lin[0:64, :], True, True, R=["w2a2", "lin"], W=kw_)
    for m in range(4):
        ph.act(sig[:, m, :], pw_[:, m, :], AF.Sigmoid, R=[kw_, "c_w0"], W="sig", bias=pc["w0"][:, m:m + 1])
    pa_, ka_ = getF()
    for m in range(4):
        ph.mm(pa_[:, m, :], w2a2[64:128, m * 128:(m + 1) * 128], lin[64:128, :], True, True, R=["w2a2", "lin"], W=ka_)
    for m in range(4):
        ph.act(aa[:, m, :], pa_[:, m, :], AF.Sigmoid, R=[ka_, "c_a0"], W="aa", bias=pc["a0"][:, m:m + 1])
    pg_, kg_ = getF()
    for m in range(4):
        ph.mm(pg_[:, m, :], g2b[:, m * 128:(m + 1) * 128], sgx[:], True, True, R=["g2b", "sgx"], W=kg_)
    ph.cp("act", gg[:], pg_[:], R=kg_, W=KX["gg"])
    ph.tt(V, kk0[:], k_, B4(pc["k_k"]), ALU.mult, R=["XS", "c_k_k"], W="kk0")
    ph.tt(V, tq[:], kk0[:], kk0[:], ALU.mult, R="kk0", W="tq")
    pq, kq = getF()
    for m in range(4):
        ph.mm(pq[:, m, :], blk64[:], tq[:, m, :], True, True, R=["blk64", "tq"], W=kq)
    ph.act(rn[:], pq[:], AF.Sqrt, R=kq, W="rn")
    ph.ts(V, rn[:], rn[:], 1e-12, ALU.max, R="rn", W="rn")
    ph.op(V, lambda e: e.reciprocal(out=fl(rn), in_=fl(rn)), R="rn", W="rn")
    ph.tt(V, kkn[:], kk0[:], rn[:], ALU.mult, R=["kk0", "rn"], W="kkn")
    ph.tt(V, bb[:], kkn[:], aa[:], ALU.mult, R=["kkn", "aa"], W="bb")
    ph.tt(V, tq[:], aa[:], B4(pc["k_a"]), ALU.mult, R=["aa", "c_k_a", kq], W="tq")
    ph.tt(V, tq[:], tq[:], B4(pc["k_a"]), ALU.subtract, R=["tq", "c_k_a"], W="tq")
    ph.stt(kmod[:], tq[:], 1.0, k_, ALU.add, ALU.mult, R=["tq", "XS"], W="kmod")
    ph.tt(V, tq[:], r_, kmod[:], ALU.mult, R=["XS", "kmod"], W="tq")
    ph.tt(V, tq[:], tq[:], B4(pc["r_k"]), ALU.mult, R=["tq", "c_r_k"], W="tq")
    pq2, kq2 = getF()
    for m in range(4):
        ph.mm(pq2[:, m, :], blk64[:], tq[:, m, :], True, True, R=["blk64", "tq"], W=kq2)
    ph.tt(V, bon[:], pq2[:], v_, ALU.mult, R=[kq2, "XS"], W=KX["bon"])
    ph.op(V, lambda e: e.tensor_tensor_scan(out=fl(cs), data0=fl(scm), data1=fl(sig), initial=0.0, op0=ALU.mult,
                                             op1=ALU.add), R=["scm", "sig"], W="cs")
    ph.ts(V, nbias[:], cs[:, :, 127], -C1, ALU.mult, R="cs", W="nbias")
    ph.act(PCt[:], nbias[:], AF.Exp, R="nbias", W=KX["PCt"])
    ph.act(ex1[:], cs[:], AF.Exp, R="cs", W="ex1", scale=-C1)
    ph.tt(V, rT[:], r_, ex1[:], ALU.mult, R=["XS", "ex1"], W=KX["rT"])
    ph.act(ex2[:], cs[:], AF.Exp, R="cs", W="ex2", scale=C1)
    ph.tt(V, kT[:], kmod[:], ex2[:], ALU.mult, R=["kmod", "ex2"], W=KX["kT"])
    ph.tt(V, bT[:], bb[:], ex2[:], ALU.mult, R=["bb", "ex2"], W=KX["bT"])
    ph.tt(V, ex3[:], cs[:], sig[:], ALU.subtract, R=["cs", "sig"], W="ex3")
    ph.act(ex3[:], ex3[:], AF.Exp, R="ex3", W="ex3", scale=-C1)
    ph.stt(aT[:], kkn[:], -1.0, ex3[:], ALU.mult, ALU.mult, R=["kkn", "ex3"], W=KX["aT"])
    for m in range(4):
        ph.act(ex1[:, m, :], cs[:, m, :], AF.Exp, R=["cs", "nbias", KX["rT"]], W="ex1", bias=nbias[:, m:m + 1], scale=C1)
    ph.tt(V, khT[:], kmod[:], ex1[:], ALU.mult, R=["kmod", "ex1"], W=KX["khT"])
    ph.tt(V, bhT[:], bb[:], ex1[:], ALU.mult, R=["bb", "ex1"], W=KX["bhT"])
    ph.cp("act", vT[:], v_, R="XS", W=KX["vT"])


def wkv_core(ph, L, c, c0):
    V = "dve"
    KX = L["KX"]
    getF = L["getF"]; getT = L["getT"]; ib = L["ib"]
    rT, kT, bT, aT, khT, bhT, vT = L["rT"], L["kT"], L["bT"], L["aT"], L["khT"], L["bhT"], L["vT"]
    Vtok, Khtok, Bhtok = L["Vtok"], L["Khtok"], L["Bhtok"]
    Nb, Lb, Mt, LKb, Arb, Ark = L["Nb"], L["Lb"], L["Mt"], L["LKb"], L["Arb"], L["Ark"]
    msl, msu, mui = L["msl"], L["msu"], L["mui"]
    Wbf, Ubf, Ysb, Ysq, ynb, gn = L["Wbf"], L["Ubf"], L["Ysb"], L["Ysq"], L["ynb"], L["gn"]
    Sst, Sbd, PCt, tS = L["Sst"], L["Sbd"], L["PCt"], L["tS"]
    pc = L["pc"]; bon, gg, YFb = L["bon"], L["gg"], L["YFb"]
    M4 = lambda m_: bc(m_[:, :].unsqueeze(1), [128, 4, 128])
    pt, kt = getT()
    for m in range(4):
        ph.tr(pt[:, m, :], vT[:, m, :], ib[:], R=KX["vT"], W=kt)
    for m in range(4):
        ph.tr(pt[:, 4 + m, :], khT[:, m, :], ib[:], R=KX["khT"], W=kt)
    ph.cp("act", Vtok[:], pt[:, 0:4, :].rearrange("p a b -> p (a b)"), R=kt, W="Vtok")
    ph.cp(V, Khtok[:], pt[:, 4:8, :].rearrange("p a b -> p (a b)"), R=kt, W="Khtok")
    pt2, kt2 = getT()
    for m in range(4):
        ph.tr(pt2[:, m, :], bhT[:, m, :], ib[:], R=KX["bhT"], W=kt2)
    ph.cp("act", Bhtok[:], pt2[:, 0:4, :].rearrange("p a b -> p (a b)"), R=kt2, W="Bhtok")

    def hsl(t, h):
        return t[64 * (h % 2):64 * (h % 2) + 64, h // 2, :]

    def amat(dst, dkey, lhs, lkey, rhs, rkey, mask, mkey):
        for par in range(2):
            pb, pk = getF()
            for q in range(4):
                h = 2 * q + par
                ph.mm(pb[:, q, :], hsl(lhs, h), hsl(rhs, h), True, True, R=[lkey, rkey], W=pk)
            ph.tt(V, dst[:, par:8:2, :], pb[:], M4(mask), ALU.mult, R=[pk, mkey], W=dkey)

    amat(Nb[0], "Nb0", aT, KX["aT"], bT, KX["bT"], msl, "msl")
    amat(Lb[0], "Lb0", bT, KX["bT"], aT, KX["aT"], msu, "msu")
    amat(LKb, "LKb", kT, KX["kT"], aT, KX["aT"], msu, "msu")
    amat(Arb, "Arb", bT, KX["bT"], rT, KX["rT"], mui, "mui")
    amat(Ark, "Ark", kT, KX["kT"], rT, KX["rT"], mui, "mui")
    for half in range(2):
        ph.tt(V, Mt[0][:, half * 4:half * 4 + 4, :], Lb[0][:, half * 4:half * 4 + 4, :], M4(ib), ALU.add,
              R=["Lb0", "identb"], W="Mt0")
    cur = 0
    for lvl in range(6):
        nxt = 1 - cur
        for half in range(2):
            pb, pk = getF()
            for q in range(4):
                h = half * 4 + q
                ph.mm(pb[:, q, :], Lb[cur][:, h, :], Nb[cur][:, h, :], True, True, R=["Lb%d" % cur, "Nb%d" % cur], W=pk)
            ph.cp("act", Nb[nxt][:, half * 4:half * 4 + 4, :], pb[:], R=pk, W="Nb%d" % nxt)
        if lvl < 5:
            for half in range(2):
                pb, pk = getF()
                for q in range(4):
                    h = half * 4 + q
                    ph.mm(pb[:, q, :], Nb[cur][:, h, :], Lb[cur][:, h, :], True, True,
                          R=["Lb%d" % cur, "Nb%d" % cur], W=pk)
                ph.cp("act", Lb[nxt][:, half * 4:half * 4 + 4, :], pb[:], R=pk, W="Lb%d" % nxt)
        for half in range(2):
            pb, pk = getF()
            for q in range(4):
                h = half * 4 + q
                ph.mm(pb[:, q, :], Nb[nxt][:, h, :], Mt[cur][:, h, :], True, True, R=["Nb%d" % nxt, "Mt%d" % cur], W=pk)
            ph.tt(V, Mt[nxt][:, half * 4:half * 4 + 4, :], pb[:], Mt[cur][:, half * 4:half * 4 + 4, :], ALU.add,
                  R=[pk, "Mt%d" % cur], W="Mt%d" % nxt)
        cur = nxt
    MtF = Mt[cur]; mk = "Mt%d" % cur
    def hcols(pb, h):
        return pb[:].rearrange("p a b -> p (a b)")[:, h * 64:h * 64 + 64]

    def pcols(pb, m):
        return pb[:].rearrange("p a b -> p (a b)")[:, m * 128:m * 128 + 128]

    pb, pk = getF()
    for m in range(4):
        ph.mm(pcols(pb, m), aT[:, m, :], Sbd[:, m, :], True, False, R=[KX["aT"], "Sbd"], W=pk)
        for hh in range(2):
            h = 2 * m + hh
            ph.mm(hcols(pb, h), LKb[:, h, :], Vtok[:, h * 64:h * 64 + 64], False, hh == 1, R=["LKb", "Vtok"], W=pk)
    ph.cp("act", Wbf[:], pb[:].rearrange("p a b -> p (a b)"), R=pk, W="Wbf")
    pb, pk = getF()
    for h in range(8):
        ph.mm(hcols(pb, h), MtF[:, h, :], Wbf[:, h * 64:h * 64 + 64], True, True, R=[mk, "Wbf"], W=pk)
    ph.cp("act", Ubf[:], pb[:].rearrange("p a b -> p (a b)"), R=pk, W="Ubf")
    pb, pk = getF()
    for m in range(4):
        ph.mm(pcols(pb, m), rT[:, m, :], Sbd[:, m, :], True, False, R=[KX["rT"], "Sbd"], W=pk)
        for hh in range(2):
            h = 2 * m + hh
            ph.mm(hcols(pb, h), Arb[:, h, :], Ubf[:, h * 64:h * 64 + 64], False, False, R=["Arb", "Ubf"], W=pk)
            ph.mm(hcols(pb, h), Ark[:, h, :], Vtok[:, h * 64:h * 64 + 64], False, hh == 1, R=["Ark", "Vtok"], W=pk)
    ph.cp("act", Ysb[:].rearrange("p a b -> p (a b)"), pb[:].rearrange("p a b -> p (a b)"), R=pk, W="Ysb")
    pS, kS = getF()
    for m in range(4):
        ph.mm(pS[:, m, :], Bhtok[:, m * 128:(m + 1) * 128], Ubf[:, m * 128:(m + 1) * 128], True, False,
              R=["Bhtok", "Ubf"], W=kS)
        ph.mm(pS[:, m, :], Khtok[:, m * 128:(m + 1) * 128], Vtok[:, m * 128:(m + 1) * 128], False, True,
              R=["Khtok", "Vtok"], W=kS)
    ph.tt(V, tS[:], Sst[:], bc(PCt[:, :].unsqueeze(2), [128, 4, 64]), ALU.mult, R=["Sst", KX["PCt"]], W="tS")
    for hh in range(2):
        rs = slice(64 * hh, 64 * hh + 64)
        ph.tt(V, Sst[rs, :, :], tS[rs, :, :], pS[rs, :, 64 * hh:64 * hh + 64], ALU.add, R=["tS", kS], W="Sst")
        ph.cp(V, Sbd[rs, :, 64 * hh:64 * hh + 64], Sst[rs, :, :], R="Sst", W="Sbd")
    groupnorm_out(ph, L, c0, 128)


def groupnorm_out(ph, L, c0, P):
    V = "dve"
    KX = L["KX"]
    Ysb, Ysq, ynb, gn = L["Ysb"], L["Ysq"], L["ynb"], L["gn"]
    pc = L["pc"]; bon, gg, YFb = L["bon"], L["gg"], L["YFb"]; getT = L["getT"]; ib = L["ib"]
    eps_gn = L["eps_gn"]; ex2 = L["gns"]
    ph.op(V, lambda e: e.tensor_reduce(out=gn[:P, 0, :], in_=Ysb[:P], axis=AX.X, op=ALU.add), R="Ysb", W="gn")
    ph.act(Ysq[:P].rearrange("p a b -> p (a b)"), Ysb[:P].rearrange("p a b -> p (a b)"), AF.Square, R="Ysb", W="Ysq")
    ph.op(V, lambda e: e.tensor_reduce(out=gn[:P, 1, :], in_=Ysq[:P], axis=AX.X, op=ALU.add), R="Ysq", W="gn")
    ph.ts(V, gn[:P, 2, :], gn[:P, 0, :], 1.0 / 64, ALU.mult, R="gn", W="gn")
    ph.tt(V, gn[:P, 3, :], gn[:P, 2, :], gn[:P, 2, :], ALU.mult, R="gn", W="gn")
    ph.stt(gn[:P, 4, :], gn[:P, 1, :], 1.0 / 64, gn[:P, 3, :], ALU.mult, ALU.subtract, R="gn", W="gn")
    ph.act(gn[:P, 4, :], gn[:P, 4, :], AF.Sqrt, R=["gn", "eps_gn"], W="gn", bias=eps_gn[:P, 0:1])
    ph.op(V, lambda e: e.reciprocal(out=gn[:P, 5, :], in_=gn[:P, 4, :]), R="gn", W="gn")
    ph.tt(V, Ysq[:P], Ysb[:P], bc(gn[:P, 2, :].unsqueeze(2), [P, 8, 64]), ALU.subtract, R=["Ysb", "gn"], W="Ysq")
    ph.tt(V, ynb[:P], Ysq[:P], bc(gn[:P, 5, :].unsqueeze(2), [P, 8, 64]), ALU.mult, R=["Ysq", "gn"], W="ynb")
    pt, kt = getT()
    for m in range(4):
        ph.tr(pt[:, m, :P], ynb[:P, 2 * m:2 * m + 2, :].rearrange("p a b -> p (a b)"), ib[:P, :P], R="ynb", W=kt)
    B4 = lambda t: bc(t[:, :].unsqueeze(2), [128, 4, P])
    t1 = ex2
    ph.tt(V, t1[:, :, :P], pt[:, 0:4, :P], B4(pc["lnx_g"]), ALU.mult, R=[kt, "c_lnx_g"], W="gns")
    ph.tt(V, t1[:, :, :P], t1[:, :, :P], B4(pc["lnx_b"]), ALU.add, R=["gns", "c_lnx_b"], W="gns")
    ph.tt(V, t1[:, :, :P], t1[:, :, :P], bon[:, :, :P], ALU.add, R=["gns", KX["bon"]], W="gns")
    ph.tt(V, YFb[:, :, c0:c0 + P], t1[:, :, :P], gg[:, :, :P], ALU.mult, R=["gns", KX["gg"]], W="YFb")


def s5_block(ph, I, G0, pc, Xs, ub, ZZb, getF, nchunk, which, ncol, step=CS, npos=CS):
    V = "dve"
    BwT, Kmat, CwT, Ab = G0["BwT"], G0["Kmat"], G0["CwT"], G0["Abar"]
    nm = nchunk
    assert nm * 8 <= 512
    for Pl in range(4):
        pb, pk = getF()
        flat = pb[:].rearrange("p a b -> p (a b)")
        for ri in range(2):
            for k in range(4):
                q = ri * 4 + k
                dst = flat[:, q * nm:(q + 1) * nm]
                for j in range(npos):
                    jj = (CS - npos) + j
                    rhs = ub[32 * Pl:32 * Pl + 32, k, j:j + (nm - 1) * step + 1:step]
                    ph.mm(dst, BwT[32 * Pl:32 * Pl + 32, k, jj, ri, :], rhs, j == 0, j == npos - 1,
                          R=["BwT", "ub"], W=pk, tp=((96, 0) if Pl == 3 else None))
        for ri in range(2):
            ph.cp(V, Xs[:, ri, Pl:16:4, 1:1 + nm],
                  flat[:, ri * 4 * nm:(ri + 1) * 4 * nm].rearrange("p (q m) -> p q m", m=nm), R=[pk], W="Xs")
    A_r = bc(Ab[:, which, 0, :].unsqueeze(1), [128, 2, 16]); A_i = bc(Ab[:, which, 1, :].unsqueeze(1), [128, 2, 16])
    tmpa = ph._s5tmp[0]; tmpb = ph._s5tmp[1]
    for m in range(nm):
        ph.tt(V, tmpa[:], Xs[:, :, :, m], A_r, ALU.mult, R=["Xs", "Abar"], W="s5a")
        ph.tt(V, tmpb[:], Xs[:, :, :, m], A_i, ALU.mult, R=["Xs", "Abar"], W="s5b")
        ph.tt(V, Xs[:, :, :, m + 1], Xs[:, :, :, m + 1], tmpa[:], ALU.add, R=["Xs", "s5a"], W="Xs")
        ph.tt(V, Xs[:, 0, :, m + 1], Xs[:, 0, :, m + 1], tmpb[:, 1, :], ALU.subtract, R=["Xs", "s5b"], W="Xs")
        ph.tt(V, Xs[:, 1, :, m + 1], Xs[:, 1, :, m + 1], tmpb[:, 0, :], ALU.add, R=["Xs", "s5b"], W="Xs")
    Xb = ph._s5xb
    ph.cp("act", Xb[:, :, :, 0:nm], Xs[:, :, :, 0:nm], R="Xs", W="Xb")
    for k in range(4):
        pb, pk = getF()
        flat = pb[:].rearrange("p a b -> p (a b)")
        for i in range(npos):
            dst = flat[:, i * nm:(i + 1) * nm]
            for tau in range(i + 1):
                rhs = ub[:, k, (i - tau):(i - tau) + (nm - 1) * step + 1:step]
                ph.mm(dst, Kmat[:, k, tau, :], rhs, tau == 0, False, R=["Kmat", "ub"], W=pk)
            for Pl in range(4):
                P_ = 4 * k + Pl
                for ri in range(2):
                    ph.mm(flat[32 * Pl:32 * Pl + 32, i * nm:(i + 1) * nm], CwT[:, i, ri, P_, :], Xb[:, ri, P_, 0:nm],
                          False, ri == 1, R=["CwT", "Xb"], W=pk, tp=(0, 32 * Pl))
        du = ph._s5du
        ph.ts(V, du[:, 0:ncol], ub[:, k, 0:ncol], pc["D_skip"][:, k:k + 1], ALU.mult, R=["ub", "c_D_skip", "s5z"], W="s5du")
        if npos == 1:
            ph.tt(V, du[:, 0:ncol], du[:, 0:ncol], flat[:, 0:nm], ALU.add, R=["s5du", pk], W="s5du")
        else:
            ph.tt(V, du[:, 0:ncol].rearrange("p (m i) -> p m i", i=npos), du[:, 0:ncol].rearrange("p (m i) -> p m i", i=npos),
                  flat[:, 0:npos * nm].rearrange("p (i m) -> p m i", m=nm), ALU.add, R=["s5du", pk], W="s5du")
        ph.act(ZZb[:, k, 0:ncol], du[:, 0:ncol], AF.Gelu_apprx_tanh, R="s5du", W=["ZZb", "s5z"])
    ph.cp(V, Xs[:, :, :, 0], Xs[:, :, :, nm], R="Xs", W="Xs")


def sample_mixer(ph, I, G0, L):
    V = "dve"
    sb = ph.sb
    pc = L["pc"]; getF, getT, ib = L["getF"], L["getT"], L["ib"]
    identf = G0["identf"]
    XS = L["XS"]; dd = L["dd"]
    t0 = T
    n = NS
    cur = sb("s_cur", [128, 14, NS], F32); prv = sb("s_prv", [128, 14, NS], F32)
    ph.dma("sp", cur[:], I["PRW"][:, t0:t0 + n].rearrange("(m p) t -> p m t", p=128), W="s_cur")
    sst = sb("s_sst", [NS, 1792], F32)
    ph.dma("sp", sst[:], I["st_shift"], W="s_sst")
    for half in range(4):
        pb, pk = getF()
        flat = pb[:].rearrange("p a b -> p (a b)")
        ms = list(range(half * 4, min(14, half * 4 + 4)))
        for q, m in enumerate(ms):
            ph.tr(flat[:, q * NS:(q + 1) * NS], sst[:, m * 128:(m + 1) * 128], identf[:NS, :NS], R=["s_sst"], W=pk)
        ph.cp(V, prv[:, ms[0]:ms[-1] + 1, :], flat[:, 0:len(ms) * NS].rearrange("p (a b) -> p a b", b=NS), R=pk, W="s_prv")
    ph.dbg("cur", cur[:], [128, 14, NS], "s_cur")
    ph.dbg("prv", prv[:], [128, 14, NS], "s_prv")
    so = sst
    for half in range(4):
        pb, pk = getF()
        flat = pb[:].rearrange("p a b -> p (a b)")
        ms = list(range(half * 4, min(14, half * 4 + 4)))
        for q, m in enumerate(ms):
            ph.tr(flat[:NS, q * 128:(q + 1) * 128], cur[:, m, :], identf[:], R=["s_cur"], W=pk)
        ph.cp(V, so[:, ms[0] * 128:(ms[-1] + 1) * 128], flat[:NS, 0:len(ms) * 128], R=pk, W="s_sst")
    ph.dma("sp", I["s_shift"], so[:], R="s_sst")
    xs = XS[:, :, 0:NS]
    ph.tt(V, dd[:, :, 0:NS], prv[:], cur[:], ALU.subtract, R=["s_prv", "s_cur"], W="dd")
    ph.tt(V, dd[:, :, 0:NS], dd[:, :, 0:NS], bc(pc["mu_shift"][:, :].unsqueeze(2), [128, 14, NS]), ALU.mult,
          R=["dd", "c_mu_shift"], W="dd")
    ph.tt(V, xs, dd[:, :, 0:NS], cur[:], ALU.add, R=["dd", "s_cur"], W="XS")
    uf = L["uf"]; ub = L["ub"]; ZZb = L["ZZb"]
    ph.dma("act", uf[:, :, 0:NS], I["UU"][:, t0:t0 + n].rearrange("(m p) t -> p m t", p=128), W="uf")
    ph.cp("act", ub[:, :, 0:NS], uf[:, :, 0:NS], R="uf", W="ub")
    stx = [sb("s_stre", [NS, 2048], F32), sb("s_stim", [NS, 2048], F32)]
    ph.dma("sp", stx[0][:], I["st_re"], W="s_stx0"); ph.dma("sp", stx[1][:], I["st_im"], W="s_stx1")
    Xsm = sb("s_Xsm", [128, 2, 16, NS], F32)
    for ri in range(2):
        for q4 in range(4):
            pb, pk = getF()
            flat = pb[:].rearrange("p a b -> p (a b)")
            for q in range(4):
                P_ = q4 * 4 + q
                ph.tr(flat[:, q * NS:(q + 1) * NS], stx[ri][:, P_ * 128:(P_ + 1) * 128], identf[:NS, :NS],
                      R="s_stx%d" % ri, W=pk)
            ph.cp(V, Xsm[:, ri, q4 * 4:q4 * 4 + 4, :], flat[:, 0:4 * NS].rearrange("p (a b) -> p a b", b=NS), R=pk, W="s_Xsm")
    s5_sample(ph, I, G0, pc, Xsm, ub, ZZb, getF, stx)
    ph.dma("act", I["ZZ"][:, t0:t0 + n].rearrange("(m p) t -> p m t", p=128), ZZb[:, :, 0:NS], R="ZZb")
    rwkv_sample(ph, I, G0, L)
    ph.dma("sp", I["YF"][:, t0:t0 + n].rearrange("(m p) t -> p m t", p=128), L["YFb"][:, :, 0:NS], R="YFb")


def s5_sample(ph, I, G0, pc, Xsm, ub, ZZb, getF, stx):
    V = "dve"
    BwT, Kmat, CwT, Ab = G0["BwT"], G0["Kmat"], G0["CwT"], G0["Abar"]
    identf = G0["identf"]
    Xb = ph._s5xb
    ph.cp("act", Xb[:, :, :, 0:NS], Xsm[:], R="s_Xsm", W="Xb")
    du = ph._s5du
    for k in range(4):
        pb, pk = getF()
        flat = pb[:].rearrange("p a b -> p (a b)")
        ph.mm(flat[:, 0:NS], Kmat[:, k, 0, :], ub[:, k, 0:NS], True, False, R=["Kmat", "ub"], W=pk)
        for Pl in range(4):
            P_ = 4 * k + Pl
            for ri in range(2):
                ph.mm(flat[32 * Pl:32 * Pl + 32, 0:NS], CwT[:, 0, ri, P_, :], Xb[:, ri, P_, 0:NS], False,
                      ri == 1, R=["CwT", "Xb"], W=pk, tp=(0, 32 * Pl))
        ph.ts(V, du[:, 0:NS], ub[:, k, 0:NS], pc["D_skip"][:, k:k + 1], ALU.mult, R=["ub", "c_D_skip", "s5z"], W="s5du")
        ph.tt(V, du[:, 0:NS], du[:, 0:NS], flat[:, 0:NS], ALU.add, R=["s5du", pk], W="s5du")
        ph.act(ZZb[:, k, 0:NS], du[:, 0:NS], AF.Gelu_apprx_tanh, R="s5du", W=["ZZb", "s5z"])
    Gs = ph.sb("s_Gs", [128, 2, 16, NS], F32)
    for Pl in range(4):
        pb, pk = getF()
        flat = pb[:].rearrange("p a b -> p (a b)")
        for ri in range(2):
            for k in range(4):
                q = ri * 4 + k
                ph.mm(flat[:, q * NS:(q + 1) * NS], BwT[32 * Pl:32 * Pl + 32, k, CS - 1, ri, :],
                      ub[32 * Pl:32 * Pl + 32, k, 0:NS], True, True, R=["BwT", "ub"], W=pk,
                      tp=((96, 0) if Pl == 3 else None))
        for ri in range(2):
            ph.cp(V, Gs[:, ri, Pl:16:4, :], flat[:, ri * 4 * NS:(ri + 1) * 4 * NS].rearrange("p (q m) -> p q m", m=NS),
                  R=pk, W="s_Gs")
    A_r = bc(Ab[:, 1, 0, :].unsqueeze(2), [128, 16, NS]); A_i = bc(Ab[:, 1, 1, :].unsqueeze(2), [128, 16, NS])
    ta = ph.sb("s_ta", [128, 16, NS], F32)
    ph.tt(V, ta[:], Xsm[:, 0], A_r, ALU.mult, R=["s_Xsm", "Abar"], W="s_ta")
    ph.tt(V, Gs[:, 0], Gs[:, 0], ta[:], ALU.add, R=["s_Gs", "s_ta"], W="s_Gs")
    ph.tt(V, ta[:], Xsm[:, 1], A_i, ALU.mult, R=["s_Xsm", "Abar", "s_Gs"], W="s_ta")
    ph.tt(V, Gs[:, 0], Gs[:, 0], ta[:], ALU.subtract, R=["s_Gs", "s_ta"], W="s_Gs")
    ph.tt(V, ta[:], Xsm[:, 1], A_r, ALU.mult, R=["s_Xsm", "Abar", "s_Gs"], W="s_ta")
    ph.tt(V, Gs[:, 1], Gs[:, 1], ta[:], ALU.add, R=["s_Gs", "s_ta"], W="s_Gs")
    ph.tt(V, ta[:], Xsm[:, 0], A_i, ALU.mult, R=["s_Xsm", "Abar", "s_Gs"], W="s_ta")
    ph.tt(V, Gs[:, 1], Gs[:, 1], ta[:], ALU.add, R=["s_Gs", "s_ta"], W="s_Gs")
    for ri, nm in enumerate(("s_re", "s_im")):
        xo = stx[ri]
        for q4 in range(4):
            pb, pk = getF()
            flat = pb[:].rearrange("p a b -> p (a b)")
            for q in range(4):
                P_ = q4 * 4 + q
                ph.tr(flat[:NS, q * 128:(q + 1) * 128], Gs[:, ri, P_, :], identf[:], R="s_Gs", W=pk)
            ph.cp(V, xo[:, q4 * 512:(q4 + 1) * 512], flat[:NS, 0:512], R=pk, W="s_stx%d" % ri)
        ph.dma("sp", I[nm], xo[:], R="s_stx%d" % ri)


def rwkv_sample(ph, I, G0, L):
    V = "dve"
    sb = ph.sb
    pc = L["pc"]; getF, getT, ib = L["getF"], L["getT"], L["ib"]
    identf = G0["identf"]
    XS = L["XS"]
    sig, aa, gg, kk0, tq, rn, kkn = L["sig"], L["aa"], L["gg"], L["kk0"], L["tq"], L["rn"], L["kkn"]
    bb, kmod, bon = L["bb"], L["kmod"], L["bon"]
    lin, sgx, w2a2, g2b, blk64 = L["lin"], L["sgx"], L["w2a2"], L["g2b"], L["blk64"]
    n = NS
    r_ = XS[:, 0:4, 0:n]; k_ = XS[:, 4:8, 0:n]; v_ = XS[:, 8:12, 0:n]
    B4 = lambda t: bc(t[:, :].unsqueeze(2), [128, 4, n])
    S4 = lambda t: t[:, :, 0:n]
    ph.act(lin[0:64, 0:n], XS[0:64, 12, 0:n], AF.Tanh, R="XS", W="lin")
    ph.cp("act", lin[64:128, 0:n], XS[64:128, 12, 0:n], R="XS", W="lin")
    ph.act(sgx[:, 0:n], XS[:, 13, 0:n], AF.Sigmoid, R="XS", W="sgx")
    pw_, kw_ = getF(); pa_, ka_ = getF(); pg_, kg_ = getF()
    for m in range(4):
        ph.mm(pw_[:, m, 0:n], w2a2[0:64, m * 128:(m + 1) * 128], lin[0:64, 0:n], True, True, R=["w2a2", "lin"], W=kw_)
        ph.mm(pa_[:, m, 0:n], w2a2[64:128, m * 128:(m + 1) * 128], lin[64:128, 0:n], True, True, R=["w2a2", "lin"], W=ka_)
        ph.mm(pg_[:, m, 0:n], g2b[:, m * 128:(m + 1) * 128], sgx[:, 0:n], True, True, R=["g2b", "sgx"], W=kg_)
    for m in range(4):
        ph.act(sig[:, m, 0:n], pw_[:, m, 0:n], AF.Sigmoid, R=[kw_, "c_w0"], W="sig", bias=pc["w0"][:, m:m + 1])
        ph.act(aa[:, m, 0:n], pa_[:, m, 0:n], AF.Sigmoid, R=[ka_, "c_a0"], W="aa", bias=pc["a0"][:, m:m + 1])
    ph.cp("act", S4(gg), pg_[:, :, 0:n], R=kg_, W="gg")
    ph.tt(V, S4(kk0), k_, B4(pc["k_k"]), ALU.mult, R=["XS", "c_k_k"], W="kk0")
    ph.tt(V, S4(tq), S4(kk0), S4(kk0), ALU.mult, R="kk0", W="tq")
    pq, kq = getF()
    for m in range(4):
        ph.mm(pq[:, m, 0:n], blk64[:], tq[:, m, 0:n], True, True, R=["blk64", "tq"], W=kq)
    ph.act(S4(rn), pq[:, :, 0:n], AF.Sqrt, R=kq, W="rn")
    ph.ts(V, S4(rn), S4(rn), 1e-12, ALU.max, R="rn", W="rn")
    ph.op(V, lambda e: e.reciprocal(out=S4(rn), in_=S4(rn)), R="rn", W="rn")
    ph.tt(V, S4(kkn), S4(kk0), S4(rn), ALU.mult, R=["kk0", "rn"], W="kkn")
    ph.tt(V, S4(bb), S4(kkn), S4(aa), ALU.mult, R=["kkn", "aa"], W="bb")
    ph.tt(V, S4(tq), S4(aa), B4(pc["k_a"]), ALU.mult, R=["aa", "c_k_a", kq], W="tq")
    ph.tt(V, S4(tq), S4(tq), B4(pc["k_a"]), ALU.subtract, R=["tq", "c_k_a"], W="tq")
    ph.stt(S4(kmod), S4(tq), 1.0, k_, ALU.add, ALU.mult, R=["tq", "XS"], W="kmod")
    ph.tt(V, S4(tq), r_, S4(kmod), ALU.mult, R=["XS", "kmod"], W="tq")
    ph.tt(V, S4(tq), S4(tq), B4(pc["r_k"]), ALU.mult, R=["tq", "c_r_k"], W="tq")
    pq2, kq2 = getF()
    for m in range(4):
        ph.mm(pq2[:, m, 0:n], blk64[:], tq[:, m, 0:n], True, True, R=["blk64", "tq"], W=kq2)
    ph.tt(V, S4(bon), pq2[:, :, 0:n], v_, ALU.mult, R=[kq2, "XS"], W="bon")
    wdec = L["ex1"]
    ph.act(S4(wdec), S4(sig), AF.Exp, R="sig", W="ex1", scale=-C1)
    srcs = [r_, S4(wdec), S4(kmod), v_, S4(kkn), S4(bb)]
    keys = ["XS", "ex1", "kmod", "XS", "kkn", "bb"]
    tok = sb("s_tok", [NS, 6, 512], F32)
    for i, (src, kkey) in enumerate(zip(srcs, keys)):
        pb, pk = getF()
        flat = pb[:].rearrange("p a b -> p (a b)")
        for m in range(4):
            ph.tr(flat[:NS, m * 128:(m + 1) * 128], src[:, m, :], identf[:], R=kkey, W=pk)
        ph.cp(V if i % 2 else "act", tok[:, i, :], flat[:NS, 0:512], R=pk, W="s_tok")
    ph.dma("sp", I["SW"].rearrange("i b f -> b i f"), tok[:], R="s_tok", W="SWd")
    vec = sb("s_vec", [128, 6, 64], F32)
    ph.dma("sp", vec[:], I["SW"].rearrange("i b (h k) -> (b h) i k", h=8), R="SWd", W="s_vec")
    S0 = sb("s_S0", [128, 64, 64], F32)
    ph.dma("act", S0[:].rearrange("p a b -> p (a b)"), I["st_wkv"], W="s_S0")
    tmp = sb("s_tmp", [128, 64, 64], F32)
    sa = sb("s_sa", [128, 64], F32); yv = sb("s_yv", [128, 64], F32); kka = sb("s_kka", [128, 64], F32)
    kB = lambda i: bc(vec[:, i, :].unsqueeze(1), [128, 64, 64])
    ph.tt(V, tmp[:], S0[:], kB(4), ALU.mult, R=["s_S0", "s_vec"], W="s_tmp")
    ph.op(V, lambda e: e.tensor_reduce(out=sa[:], in_=tmp[:], axis=AX.X, op=ALU.add), R="s_tmp", W="s_sa")
    ph.tt(V, S0[:], S0[:], kB(1), ALU.mult, R=["s_S0", "s_vec", "s_tmp"], W="s_S0")
    ph.tt(V, tmp[:], bc(sa[:, :].unsqueeze(2), [128, 64, 64]), kB(5), ALU.mult, R=["s_sa", "s_vec"], W="s_tmp")
    ph.tt(V, S0[:], S0[:], tmp[:], ALU.subtract, R=["s_S0", "s_tmp"], W="s_S0")
    ph.tt(V, tmp[:], bc(vec[:, 3, :].unsqueeze(2), [128, 64, 64]), kB(2), ALU.mult, R=["s_vec", "s_S0"], W="s_tmp")
    ph.tt(V, S0[:], S0[:], tmp[:], ALU.add, R=["s_S0", "s_tmp"], W="s_S0")
    ph.dma("act", I["s_wkv"], S0[:].rearrange("p a b -> p (a b)"), R="s_S0")
    ph.tt(V, tmp[:], S0[:], kB(0), ALU.mult, R=["s_S0", "s_vec"], W="s_tmp")
    ph.op(V, lambda e: e.tensor_reduce(out=yv[:], in_=tmp[:], axis=AX.X, op=ALU.add), R="s_tmp", W="s_yv")
    ph.dma("sp", I["SY"], yv[:], R="s_yv", W="SYd")
    Ysb = L["Ysb"]
    ph.dma("sp", Ysb[:NS].rearrange("p a b -> p (a b)"), I["SY"].rearrange("(b h) v -> b (h v)", h=8), R="SYd", W="Ysb")
    groupnorm_out(ph, L, 0, NS)


def phase3(nc, I, G0, W3, WFI):
    ph = Ph(nc, "p3")
    V = "dve"
    W3 = alloc_w3(nc, ph.st)
    load_w3(ph, I, W3)
    rwo, glu, wo = W3["rwo"], W3["glu"], W3["wo"]
    for k in range(8):
        ph.dma("pool", WFI[:, k, :], I["w_ffn_in"][k * 128:(k + 1) * 128, :], W="wfi_pre")
    yf = ph.sb("yf", [128, 4, 512], BF16); zz = ph.sb("zz", [128, 4, 512], BF16); gt = ph.sb("gt", [128, 16, 512], BF16)
    trw = ph.sb("trw", [128, 8, 512], F32); mg = ph.sb("mg", [128, 8, 512], BF16)
    sgb = [ph.sb("sgb%d" % i, [128, 512], F32) for i in range(2)]
    s5t = [ph.sb("s5t%d" % i, [128, 512], F32) for i in range(2)]
    xts = [ph.sb("xt%d" % i, [128, D], F32) for i in range(2)]
    pm = [ph.ps("pm%d" % i, [128, 512], F32) for i in range(6)]
    npm = nx = ns = 0
    for (t0, nt) in BLOCKS:
        P = min(128, nt)
        r3 = lambda name: I[name][:, t0:t0 + nt].rearrange("(m p) t -> p m t", p=128)
        ph.dma("sp", yf[:, :, :nt], r3("YF"), W="yf"); ph.dma("sp", zz[:, :, :nt], r3("ZZ"), W="zz")
        ph.dma("act", gt[:, :, :nt], r3("GT"), W="gt")
        for m in range(8):
            pb = pm[npm % 6]; pk = "pm%d" % (npm % 6); npm += 1
            for k in range(4):
                ph.mm(pb[:, :nt], rwo[:, k, m * 128:(m + 1) * 128], yf[:, k, :nt], k == 0, k == 3, R=["rwo", "yf"], W=pk)
            ph.tt(V, trw[:, m, :nt], pb[:, :nt], gt[:, m, :nt], ALU.mult, R=[pk, "gt"], W="trw%d" % m)
        for m in range(8):
            pa = pm[npm % 6]; pka = "pm%d" % (npm % 6); npm += 1
            pb = pm[npm % 6]; pkb = "pm%d" % (npm % 6); npm += 1
            for k in range(4):
                ph.mm(pa[:, :nt], glu[:, k, m * 128:(m + 1) * 128], zz[:, k, :nt], k == 0, k == 3, R=["glu", "zz"], W=pka)
            for k in range(4):
                ph.mm(pb[:, :nt], glu[:, k, D + m * 128:D + (m + 1) * 128], zz[:, k, :nt], k == 0, k == 3,
                      R=["glu", "zz"], W=pkb)
            sg = sgb[ns % 2]; sk = "sgb%d" % (ns % 2); s5 = s5t[ns % 2]; s5k = "s5t%d" % (ns % 2); ns += 1
            ph.act(sg[:, :nt], pb[:, :nt], AF.Sigmoid, R=pkb, W=sk)
            ph.tt(V, s5[:, :nt], pa[:, :nt], sg[:, :nt], ALU.mult, R=[pka, sk], W=s5k)
            ph.tt(V, s5[:, :nt], s5[:, :nt], gt[:, 8 + m, :nt], ALU.mult, R=[s5k, "gt"], W=s5k)
            ph.tt(V, mg[:, m, :nt], s5[:, :nt], trw[:, m, :nt], ALU.add, R=[s5k, "trw%d" % m], W="mg")
        for s in range((nt + 127) // 128):
            xt = xts[nx % 2]; xk = "xt%d" % (nx % 2); nx += 1
            rows = slice(t0 + s * 128, t0 + s * 128 + P)
            ph.dma("sp", xt[:P, :], I["xall"][rows, :], W=xk)
            for half in range(2):
                pb = pm[npm % 6]; pk = "pm%d" % (npm % 6); npm += 1
                for k in range(8):
                    ph.mm(pb[:P, :], mg[:, k, s * 128:s * 128 + P], wo[:, k, half * 512:(half + 1) * 512], k == 0, k == 7,
                          R=["mg", "wo"], W=pk)
                ph.tt(V, xt[:P, half * 512:(half + 1) * 512], xt[:P, half * 512:(half + 1) * 512], pb[:P, :], ALU.add,
                      R=[pk, xk], W=xk)
            ph.dma("sp", I["X1"][rows, :], xt[:P, :], R=xk)
    ph.finish()


def phase4(nc, I, G0, WFI):
    ph = Ph(nc, "p4")
    V = "dve"
    G = norm_scratch(ph, G0)
    identf = G0["identf"]
    wfi = WFI; wfo = ph.sb("wfo", [128, 22, D], BF16)
    for k in range(22):
        ph.dma("pool", wfo[:, k, :], I["w_ffn_out"][k * 128:(k + 1) * 128, :], W="wfo")
    g2c = ph.sb("g2c", [128, 8], F32); load_col(ph, g2c[:], I["ln2_g"], 8, "g2c")
    cw = ph.sb("cw", [128, 3, 22], F32); cb = ph.sb("cb", [128, 22], F32)
    ph.dma("sp", cw[:], I["conv_w"].rearrange("t (f p) -> p t f", p=128), W="cw", slow=True)
    load_col(ph, cb[:], I["conv_b"], 22, "cb")
    hT = ph.sb("hT", [128, 8, 512], BF16)
    hid = ph.sb("hid", [128, 22, 512], BF16)
    xts = [ph.sb("xt%d" % i, [128, D], F32) for i in range(2)]
    At = [ph.sb("At%d" % i, [128, 514], F32) for i in range(2)]
    acc = [ph.sb("acc%d" % i, [128, 512], F32) for i in range(2)]
    cc = ph.sb("cc", [128, 22, 2], F32)
    ph.memset(V, cc[:].rearrange("p a b -> p (a b)"), 0.0, W="cc")
    pm = [ph.ps("pm%d" % i, [128, 512], F32) for i in range(6)]
    scs = ph.sb("scs", [NS, 2816], F32)
    scT = ph.sb("scT", [128, 22, 2, NS], F32)
    aout = scs
    npm = na = 0
    for (t0, nt) in BLOCKS:
        P = min(128, nt)
        sample = nt < 128
        nsub = (nt + 127) // 128
        for s in range(nsub):
            rows = slice(t0 + s * 128, t0 + s * 128 + P)
            ph.dma("sp", xts[s % 2][:P, :], I["X1"][rows, :], W="xt%d" % (s % 2))
            rms_to_hT(ph, G, xts[s % 2], P, g2c, hT, s * 128, str(s % 2), "g2c")
        if sample:
            for tt_ in range(2):
                ph.dma("sp", scs[:], I["st_conv"][:, tt_, :], W="scs")
                for q in range(6):
                    pb = pm[npm % 6]; pk = "pm%d" % (npm % 6); npm += 1
                    fs = list(range(q * 4, min(22, q * 4 + 4)))
                    for j, f_ in enumerate(fs):
                        ph.tr(pb[:, j * NS:(j + 1) * NS], scs[:, f_ * 128:(f_ + 1) * 128], identf[:NS, :NS], R="scs", W=pk)
                    ph.cp(V, scT[:, fs[0]:fs[-1] + 1, tt_, :], pb[:, 0:len(fs) * NS].rearrange("p (a b) -> p a b", b=NS),
                          R=pk, W="scT")
        for f in range(22):
            pa = pm[npm % 6]; pka = "pm%d" % (npm % 6); npm += 1
            pb = pm[npm % 6]; pkb = "pm%d" % (npm % 6); npm += 1
            for k in range(8):
                ph.mm(pa[:, :nt], wfi[:, k, f * 128:(f + 1) * 128], hT[:, k, :nt], k == 0, k == 7, R=["wfi", "hT"], W=pka)
            for k in range(8):
                ph.mm(pb[:, :nt], wfi[:, k, 2816 + f * 128:2816 + (f + 1) * 128], hT[:, k, :nt], k == 0, k == 7,
                      R=["wfi", "hT"], W=pkb)
            A = At[na % 2]; ak = "At%d" % (na % 2); ac = acc[na % 2]; ck = "acc%d" % (na % 2); na += 1
            ph.cp("act", A[:, 2:2 + nt], pa[:, :nt], R=pka, W=ak)
            if not sample:
                ph.cp(V, A[:, 0:2], cc[:, f, :], R="cc", W=ak)
                a0, a1, a2 = A[:, 0:nt], A[:, 1:1 + nt], A[:, 2:2 + nt]
            else:
                a0, a1, a2 = scT[:, f, 0, :], scT[:, f, 1, :], A[:, 2:2 + nt]
            ph.ts(V, ac[:, :nt], a0, cw[:, 0, f:f + 1], ALU.mult, cb[:, f:f + 1], ALU.add, R=[ak, "scT", "cw", "cb"], W=ck)
            ph.stt(ac[:, :nt], a1, cw[:, 1, f:f + 1], ac[:, :nt], ALU.mult, ALU.add, R=[ak, "scT", "cw", ck], W=ck)
            ph.stt(ac[:, :nt], a2, cw[:, 2, f:f + 1], ac[:, :nt], ALU.mult, ALU.add, R=[ak, "cw", ck], W=ck)
            ph.act(ac[:, :nt], ac[:, :nt], AF.Gelu_apprx_tanh, R=ck, W=ck)
            ph.tt(V, hid[:, f, :nt], ac[:, :nt], pb[:, :nt], ALU.mult, R=[ck, pkb], W="hid")
            if not sample:
                ph.cp(V, cc[:, f, :], A[:, nt:nt + 2], R=ak, W="cc")
            else:
                po = pm[npm % 6]; pko = "pm%d" % (npm % 6); npm += 1
                ph.tr(po[:NS, 0:128], A[:, 2:2 + NS], identf[:], R=ak, W=pko)
                ph.cp(V, aout[:, f * 128:(f + 1) * 128], po[:NS, 0:128], R=pko, W="scs")
        if t0 + nt == T:
            for tt_ in range(2):
                ph.dma("sp", I["p_conv"][tt_].rearrange("(f p) -> p f", p=128), cc[:, :, tt_], R="cc", slow=True)
        if sample:
            ph.dma("sp", I["s_conv"][:, 1, :], aout[:], R="scs")
            ph.dma("act", I["s_conv"][:, 0, :], I["st_conv"][:, 1, :])
        for s in range(nsub):
            rows = slice(t0 + s * 128, t0 + s * 128 + P)
            xt = xts[s % 2]; xk = "xt%d" % (s % 2)
            ph.dma("sp", xt[:P, :], I["X1"][rows, :], W=xk)
            for half in range(2):
                pb = pm[npm % 6]; pk = "pm%d" % (npm % 6); npm += 1
                for f in range(22):
                    ph.mm(pb[:P, :], hid[:, f, s * 128:s * 128 + P], wfo[:, f, half * 512:(half + 1) * 512], f == 0, f == 21,
                          R=["hid", "wfo"], W=pk)
                ph.tt(V, xt[:P, half * 512:(half + 1) * 512], xt[:P, half * 512:(half + 1) * 512], pb[:P, :],
                      ALU.add, R=[pk, xk], W=xk)
            ph.dma("sp", I["X2"][rows, :], xt[:P, :], R=xk)
    ph.finish()


def phase5(nc, I, G0):
    ph = Ph(nc, "p5")
    V = "dve"
    G = norm_scratch(ph, G0)
    wpg = ph.sb("wpg", [128, 8, D], BF16); wpl = ph.sb("wpl", [128, 2, D], BF16)
    for k in range(8):
        ph.dma("pool", wpg[:, k, :], I["w_ple_gate"][k * 128:(k + 1) * 128, :], W="wpg")
    for k in range(2):
        ph.dma("pool", wpl[:, k, :], I["w_ple"][k * 128:(k + 1) * 128, :], W="wpl")
    g3c = ph.sb("g3c", [128, 8], F32); load_col(ph, g3c[:], I["ln3_g"], 8, "g3c")
    fg = ph.sb("fg", [128, D], F32)
    ph.dma("sp", fg[:], I["final_g"].partition_broadcast(128), W="fg")
    hT = ph.sb("hT", [128, 8, 128], BF16)
    xts = [ph.sb("xt%d" % i, [128, D], F32) for i in range(2)]
    pbs = [ph.sb("pb%d" % i, [128, 256], BF16) for i in range(2)]
    pTs = ph.sb("pTs", [128, 2, 128], BF16)
    sg = [ph.sb("sg%d" % i, [128, 512], F32) for i in range(2)]
    yo = [ph.sb("yo%d" % i, [128, D], F32) for i in range(2)]
    pm = [ph.ps("pm%d" % i, [128, 512], F32) for i in range(4)]
    pq = ph.ps("pq", [128, 8, 128], BF16)
    npm = nx = nsg = 0
    for (t0, nt) in BLOCKS:
        P = min(128, nt)
        for s in range((nt + 127) // 128):
            rows = slice(t0 + s * 128, t0 + s * 128 + P)
            i2 = nx % 2; nx += 1
            xt = xts[i2]; xk = "xt%d" % i2; pbt = pbs[i2]; pbk = "pb%d" % i2
            ph.dma("sp", xt[:P, :], I["X2"][rows, :], W=xk)
            ph.dma("pool", pbt[:P, :], I["pall"][rows, :], W=pbk)
            rms_to_hT(ph, G, xt, P, g3c, hT, 0, str(i2), "g3c")
            for k in range(2):
                ph.tr(pq[:, k, :P], pbt[:P, k * 128:(k + 1) * 128], G0["identb"][:P, :P], R=pbk, W="pq")
            ph.cp("act", pTs[:, :, :P], pq[:, 0:2, :P], R="pq", W="pTs")
            for half in range(2):
                cs_ = slice(half * 512, (half + 1) * 512)
                pg = pm[npm % 4]; pgk = "pm%d" % (npm % 4); npm += 1
                pe = pm[npm % 4]; pek = "pm%d" % (npm % 4); npm += 1
                for k in range(8):
                    ph.mm(pg[:P, :], hT[:, k, :P], wpg[:, k, cs_], k == 0, k == 7, R=["hT", "wpg"], W=pgk)
                for k in range(2):
                    ph.mm(pe[:P, :], pTs[:, k, :P], wpl[:, k, cs_], k == 0, k == 1, R=["pTs", "wpl"], W=pek)
                sgt = sg[nsg % 2]; sgk = "sg%d" % (nsg % 2); nsg += 1
                ph.act(sgt[:P, :], pg[:P, :], AF.Sigmoid, R=pgk, W=sgk)
                ph.tt(V, sgt[:P, :], sgt[:P, :], pe[:P, :], ALU.mult, R=[sgk, pek], W=sgk)
                ph.tt(V, xt[:P, cs_], xt[:P, cs_], sgt[:P, :], ALU.add, R=[sgk, xk, "xn"], W=xk)
            ss = G["ss"]; sq = G["sq"]
            ph.act(sq[:P, :], xt[:P, :], AF.Square, R=xk, W=["sq", "ss"], accum=ss[:P, 0:1])
            ph.act(ss[:P, 1:2], ss[:P, 0:1], AF.Sqrt, R=["ss", "eps"], W="ss", bias=G["eps"][:P, 0:1], scale=1.0 / D)
            ph.op(V, lambda e, ss=ss, P=P: e.reciprocal(out=ss[:P, 3:4], in_=ss[:P, 1:2]), R="ss", W="ss3")
            y = yo[i2]; yk = "yo%d" % i2
            ph.stt(y[:P, :], xt[:P, :], ss[:P, 3:4], fg[:P, :], ALU.mult, ALU.mult, R=[xk, "ss3", "fg"], W=yk)
            ph.dma("sp", I["y"][rows, :], y[:P, :], R=yk)
    ph.finish()


_CACHE = {}


def _consts():
    i = np.arange(128)
    c = {}
    c["c_ident"] = np.eye(128, dtype=np.float32)
    c["c_msl"] = (i[None, :] < i[:, None]).astype(np.float32)
    c["c_msu"] = (i[:, None] < i[None, :]).astype(np.float32)
    c["c_mui"] = (i[:, None] <= i[None, :]).astype(np.float32)
    c["c_blk64"] = ((i[:, None] // 64) == (i[None, :] // 64)).astype(np.float32)
    c["c_blk32"] = ((i[:, None] // 32) == (i[None, :] // 32)).astype(np.float32)
    c["c_rowgp"] = (((i[:, None] // 16) % 2) == (i[None, :] // 64)).astype(np.float32)
    return c


def make_in_maps(inp):
    f = lambda a: np.ascontiguousarray(np.asarray(a, dtype=np.float32))
    cst = _consts()
    shared = {}
    for k in ("ln1_g", "w_in", "mu_shift", "w0", "w2", "a0", "a2", "g2", "k_k", "k_a", "lnx_g", "lnx_b", "w_rw_out",
              "A_re", "A_im", "log_dt", "B_re", "B_im", "D_skip", "w_glu", "w_out", "ln2_g", "w_ffn_in", "conv_w",
              "conv_b", "w_ffn_out", "ln3_g", "w_ple_gate", "w_ple"):
        shared[k] = f(inp[k])[0]
    shared["r_k"] = f(inp["r_k"])[0].reshape(512)
    shared["C_re"] = f(inp["C_re"])[0].reshape(512, 64)
    shared["C_im"] = f(inp["C_im"])[0].reshape(512, 64)
    shared["final_g"] = f(inp["final_g"])
    shared.update(cst)
    xp, xs = f(inp["x_prompt"]), f(inp["x_sample"])
    pp, psm = f(inp["p_prompt"])[0], f(inp["p_sample"])[0]
    in_maps = []
    for c in range(8):
        sl = slice(NS * c, NS * c + NS)
        m = dict(shared)
        m["xall"] = np.concatenate([xp[c], xs[sl, 0]], 0)
        m["pall"] = np.concatenate([pp[c], psm[sl, 0]], 0)
        m["st_shift"] = f(inp["state_shift"])[0, sl]
        m["st_wkv"] = f(inp["state_wkv"])[0, sl].reshape(128, 4096)
        m["st_re"] = f(inp["state_ssm_re"])[0, sl].reshape(NS, 2048)
        m["st_im"] = f(inp["state_ssm_im"])[0, sl].reshape(NS, 2048)
        m["st_conv"] = f(inp["state_conv"])[0, sl]
        in_maps.append({k: np.ascontiguousarray(v) for k, v in m.items()})
    return in_maps


def kernel(**inp):
    f = lambda a: np.ascontiguousarray(np.asarray(a, dtype=np.float32))
    if "nc" not in _CACHE:
        _CACHE["nc"] = build_program()
    nc = _CACHE["nc"]
    in_maps = make_in_maps(inp)
    res = run_bass_kernel_spmd(nc, in_maps, core_ids=list(range(8)))
    R = res.results
    cat = lambda fn: np.stack([fn(r) for r in R], 0)
    y_prompt = cat(lambda r: r["y"][:T])
    y_sample = np.concatenate([r["y"][T:] for r in R], 0)[:, None, :]
    p_shift = cat(lambda r: r["p_shift"])[None]
    p_wkv = cat(lambda r: r["p_wkv"].reshape(8, 64, 64).transpose(0, 2, 1))[None]
    p_re = cat(lambda r: r["p_re"].reshape(32, 64))[None]
    p_im = cat(lambda r: r["p_im"].reshape(32, 64))[None]
    p_conv = cat(lambda r: r["p_conv"])[None]
    s_shift = np.concatenate([r["s_shift"] for r in R], 0)[None]
    s_wkv = np.concatenate([r["s_wkv"].reshape(NS, 8, 64, 64) for r in R], 0)[None]
    s_re = np.concatenate([r["s_re"].reshape(NS, 32, 64) for r in R], 0)[None]
    s_im = np.concatenate([r["s_im"].reshape(NS, 32, 64) for r in R], 0)[None]
    s_conv = np.concatenate([r["s_conv"] for r in R], 0)[None]
    outs = (y_prompt, y_sample, p_shift, p_wkv, p_re, p_im, p_conv, s_shift, s_wkv, s_re, s_im, s_conv)
    return tuple(np.ascontiguousarray(o.astype(np.float32)) for o in outs)
```

```python
import contextlib
import math
import numpy as np
import concourse.bass as bass
import concourse.mybir as mybir
from concourse.bass_utils import run_bass_kernel_spmd

F32 = mybir.dt.float32
BF16 = mybir.dt.bfloat16
AF = mybir.ActivationFunctionType
ALU = mybir.AluOpType
AX = mybir.AxisListType

T = 2048
NS = 16
NT = T + NS
D = 1024
CS = 8
C1 = math.exp(-0.5)
BLOCKS = [(0, 512), (512, 512), (1024, 512), (1536, 512), (2048, 16)]

ENGS = ("pe", "act", "dve", "pool", "sp")
NDSEM = 12


class _Op:
    __slots__ = ("eng", "fn", "deps", "dma", "observed", "tok", "idx", "dslot")

    def __init__(self, eng, fn, dma):
        self.eng, self.fn, self.dma = eng, fn, dma
        self.deps = set()
        self.observed = False
        self.tok = None
        self.dslot = None


class Sched:
    def __init__(self, nc):
        self.nc = nc
        self.ops = []
        self.last_w = {}
        self.readers = {}
        self.dma_rr = {e: 0 for e in ENGS}
        self.dma_prev = {}
        self.excl = set()

    def _add(self, eng, fn, reads, writes, dma):
        op = _Op(eng, fn, dma)
        op.idx = len(self.ops)
        if self.excl:
            ex = tuple(b for b in reads if b in self.excl)
            if ex:
                writes = tuple(writes) + ex
        for b in reads:
            w = self.last_w.get(b)
            if w is not None:
                op.deps.add(w)
        for b in writes:
            w = self.last_w.get(b)
            if w is not None:
                op.deps.add(w)
            for r in self.readers.get(b, ()):
                op.deps.add(r)
        if dma:
            slot = (eng, self.dma_rr[eng] % NDSEM)
            self.dma_rr[eng] += 1
            op.dslot = slot
            prev = self.dma_prev.get(slot)
            if prev is not None:
                op.deps.add(prev)
            self.dma_prev[slot] = op.idx
        op.deps.discard(op.idx)
        self.ops.append(op)
        for b in writes:
            self.last_w[b] = op.idx
            self.readers[b] = []
        for b in reads:
            if b not in writes:
                self.readers.setdefault(b, []).append(op.idx)
        return op.idx

    def emit(self):
        nc = self.nc
        ops = self.ops
        need = []
        for op in ops:
            nd = []
            for d in op.deps:
                p = ops[d]
                if (not p.dma) and (not op.dma) and p.eng == op.eng == "pe":
                    continue
                nd.append(d)
                p.observed = True
            need.append(nd)
        last = {}
        for op in ops:
            key = op.dslot if op.dma else op.eng
            last[key] = op.idx
        for i in last.values():
            ops[i].observed = True
        g = getattr(nc, "_gsem", None)
        if g is None:
            g = {"sems": {}, "cnt": {e: 0 for e in ENGS}, "dcnt": {}}
            nc._gsem = g
        cnt = g["cnt"]
        dcnt = g["dcnt"]
        for op in ops:
            if op.dma:
                dcnt[op.dslot] = dcnt.get(op.dslot, 0) + 16
                op.tok = (op.dslot, dcnt[op.dslot])
            elif op.observed:
                cnt[op.eng] += 1
                op.tok = (op.eng, cnt[op.eng])
        sems = g["sems"]
        for k in list(ENGS) + sorted(set(o.dslot for o in ops if o.dma)):
            if k not in sems:
                nm = k if isinstance(k, str) else "d_%s_%d" % k
                sems[k] = nc.alloc_semaphore(name="s_" + nm)
        with contextlib.ExitStack() as st:
            block = st.enter_context(nc.Block())
            per = {e: [o for o in ops if o.eng == e] for e in ENGS}
            hw = {"pe": block.tensor, "act": block.scalar, "dve": block.vector,
                  "pool": block.gpsimd, "sp": block.sync}

            def make(e):
                def body(eng):
                    seen = {}
                    for op in per[e]:
                        waits = {}
                        for d in need[op.idx]:
                            k, v = ops[d].tok
                            if v > waits.get(k, 0):
                                waits[k] = v
                        for k, v in waits.items():
                            if seen.get(k, 0) >= v:
                                continue
                            seen[k] = v
                            eng.wait_ge(sems[k], v)
                        ins = op.fn(eng)
                        if op.dma:
                            ins.then_inc(sems[op.tok[0]], 16)
                        elif op.observed:
                            ins.then_inc(sems[e], 1)
                    if e == "sp":
                        for key, i in last.items():
                            k, v = ops[i].tok
                            if seen.get(k, 0) < v:
                                eng.wait_ge(sems[k], v)
                return body

            for e in ENGS:
                hw[e](make(e))


def _L(x):
    if x is None:
        return ()
    if isinstance(x, str):
        return (x,)
    return tuple(x)


class Ph:
    _uid = [0]

    def __init__(self, nc, tag):
        self.nc = nc
        self.tag = tag
        self.st = contextlib.ExitStack()
        self.S = Sched(nc)

    def sb(self, name, shape, dt):
        return self.st.enter_context(self.nc.sbuf_tensor(self.tag + "_" + name, list(shape), dt))

    def ps(self, name, shape, dt):
        self.S.excl.add(name)
        return self.st.enter_context(self.nc.psum_tensor(self.tag + "_" + name, list(shape), dt))

    def finish(self):
        self.S.emit()
        self.st.close()

    def dbg(self, name, ap, shape, key, dt=F32):
        import os
        if os.environ.get("K_DBG_DUMP", "") == "":
            return
        t = self.nc.dram_tensor("dbg_" + name, list(shape), dt, kind="ExternalOutput").ap()
        self.dma("sp", t, ap, R=key)

    _rec = None

    def rec_begin(self):
        self._rec = []

    def rec_end(self):
        r, self._rec = self._rec, None
        return r

    def play(self, *streams):
        streams = [st_ for st_ in streams if st_]
        pos = [0] * len(streams)
        while True:
            best, bi = None, -1
            for i, st_ in enumerate(streams):
                if pos[i] < len(st_):
                    f = (pos[i] + 1.0) / len(st_)
                    if best is None or f < best:
                        best, bi = f, i
            if bi < 0:
                break
            eng, fn, R, W, dma = streams[bi][pos[bi]]
            pos[bi] += 1
            self.S._add(eng, fn, R, W, dma)

    def op(self, eng, fn, R=None, W=None):
        if self._rec is not None:
            self._rec.append((eng, fn, _L(R), _L(W), False))
        else:
            self.S._add(eng, fn, _L(R), _L(W), False)

    def dma(self, q, out, in_, R=None, W=None, slow=False):
        if slow:
            fn = lambda e: e.dma_start(out=out, in_=in_, allow_slow_non_contiguous=True)
        else:
            fn = lambda e: e.dma_start(out=out, in_=in_)
        if self._rec is not None:
            self._rec.append((q, fn, _L(R), _L(W), True))
        else:
            self.S._add(q, fn, _L(R), _L(W), True)

    def tt(self, eng, out, in0, in1, op, R=None, W=None):
        self.op(eng, lambda e: e.tensor_tensor(out=out, in0=in0, in1=in1, op=op), R, W)

    def ts(self, eng, out, in0, s1, op0, s2=None, op1=None, R=None, W=None):
        if op1 is None:
            self.op(eng, lambda e: e.tensor_scalar(out=out, in0=in0, scalar1=s1, scalar2=None, op0=op0), R, W)
        else:
            self.op(eng, lambda e: e.tensor_scalar(out=out, in0=in0, scalar1=s1, scalar2=s2, op0=op0, op1=op1), R, W)

    def stt(self, out, in0, scalar, in1, op0, op1, R=None, W=None):
        self.op("dve", lambda e: e.scalar_tensor_tensor(out=out, in0=in0, scalar=scalar, in1=in1, op0=op0, op1=op1), R, W)

    def act(self, out, in_, func, R=None, W=None, bias=None, scale=1.0, accum=None):
        kw = {}
        if bias is not None:
            kw["bias"] = bias
        if accum is not None:
            kw["accum_out"] = accum
        self.op("act", lambda e: e.activation(out=out, in_=in_, func=func, scale=scale, **kw), R, W)

    def cp(self, eng, out, in_, R=None, W=None):
        if eng == "act":
            self.op("act", lambda e: e.activation(out=out, in_=in_, func=AF.Copy), R, W)
        else:
            self.op(eng, lambda e: e.tensor_copy(out=out, in_=in_), R, W)

    def mm(self, out, lhsT, rhs, start, stop, R=None, W=None, tp=None):
        if tp is None:
            self.op("pe", lambda e: e.matmul(out, lhsT=lhsT, rhs=rhs, start=start, stop=stop), R, W)
        else:
            self.op("pe", lambda e: e.matmul(out, lhsT=lhsT, rhs=rhs, start=start, stop=stop, tile_position=tp), R, W)

    def tr(self, out, in_, ident, R=None, W=None):
        self.op("pe", lambda e: e.transpose(out, in_, ident), R, W)

    def memset(self, eng, ap, v, W=None):
        self.op(eng, lambda e: e.memset(ap, v), None, W)


def bc(ap, shape):
    return ap.to_broadcast(list(shape))


def rms_to_hT(ph, G, xt, P, gcol, hT, c0, tag, gkey, hkey="hT"):
    sq, ss, xn, pT = G["sq"], G["ss"], G["xn"], G["pT"]
    x_ = G.get("sx", "")
    ksq, kss, kxn, kpT = "sq" + x_, "ss" + x_, "xn" + x_, "pT" + x_
    ph.act(sq[:P, :], xt[:P, :], AF.Square, R="xt" + tag, W=[ksq, kss], accum=ss[:P, 0:1])
    ph.act(ss[:P, 1:2], ss[:P, 0:1], AF.Sqrt, R=[kss, "eps"], W=kss, bias=G["eps"][:P, 0:1], scale=1.0 / D)
    ph.op("dve", lambda e: e.reciprocal(out=ss[:P, 2:3], in_=ss[:P, 1:2]), R=kss, W=kss)
    ph.ts("dve", xn[:P, :], xt[:P, :], ss[:P, 2:3], ALU.mult, R=["xt" + tag, kss], W=kxn)
    for k in range(8):
        ph.tr(pT[:, k, :P], xn[:P, k * 128:(k + 1) * 128], G["identb"][:P, :P], R=[kxn, "identb"], W=kpT)
    ph.tt("dve", hT[:, :, c0:c0 + P], pT[:, :, :P], bc(gcol[:, :].unsqueeze(2), [128, 8, P]), ALU.mult,
          R=[kpT, gkey], W=hkey)


def load_col(ph, dst, src1d, n, key):
    ph.dma("sp", dst, src1d.rearrange("(k p) -> p k", p=128), W=key, slow=True)


def norm_scratch(ph, G0, sx="", eps=None):
    G = dict(G0)
    G["sx"] = sx
    G["sq"] = ph.sb("sq" + sx, [128, D], F32)
    G["ss"] = ph.sb("ss" + sx, [128, 4], F32)
    G["xn"] = ph.sb("xn" + sx, [128, D], BF16)
    G["pT"] = ph.ps("pT" + sx, [128, 8, 128], BF16)
    if eps is None:
        G["eps"] = ph.sb("eps", [128, 1], F32)
        ph.memset("dve", G["eps"][:], 1e-6, W="eps")
    else:
        G["eps"] = eps
    return G


def build_program(upto=9, debug=False):
    nc = bass.Bass("TRN2", target_bir_lowering=False)
    I = {}

    def inp(name, shape, dt=F32):
        I[name] = nc.dram_tensor(name, list(shape), dt, kind="ExternalInput").ap()

    def outp(name, shape):
        I[name] = nc.dram_tensor(name, list(shape), F32, kind="ExternalOutput").ap()

    def scratch(name, shape, dt):
        if debug:
            I[name] = nc.dram_tensor(name, list(shape), dt, kind="ExternalOutput").ap()
        else:
            I[name] = nc.dram_tensor(name, list(shape), dt).ap()
    if debug:
        scratch("d_BwT", [128, 4 * CS * 2 * 128], BF16); scratch("d_Kmat", [128, 4 * CS * 128], BF16)
        scratch("d_CwT", [128, CS * 2 * 16 * 32], BF16); scratch("d_Abar", [128, 64], F32)

    inp("xall", [NT, D]); inp("pall", [NT, 256])
    inp("st_shift", [NS, 1792]); inp("st_wkv", [128, 4096]); inp("st_re", [NS, 2048]); inp("st_im", [NS, 2048])
    inp("st_conv", [NS, 2, 2816])
    inp("ln1_g", [D]); inp("w_in", [D, 4352]); inp("mu_shift", [1792]); inp("w0", [512]); inp("w2", [64, 512])
    inp("a0", [512]); inp("a2", [64, 512]); inp("g2", [128, 512]); inp("k_k", [512]); inp("k_a", [512])
    inp("r_k", [512]); inp("lnx_g", [512]); inp("lnx_b", [512]); inp("w_rw_out", [512, D])
    inp("A_re", [32, 64]); inp("A_im", [32, 64]); inp("log_dt", [32]); inp("B_re", [32, 64, 16]); inp("B_im", [32, 64, 16])
    inp("C_re", [512, 64]); inp("C_im", [512, 64]); inp("D_skip", [512]); inp("w_glu", [512, 2048]); inp("w_out", [D, D])
    inp("ln2_g", [D]); inp("w_ffn_in", [D, 5632]); inp("conv_w", [3, 2816]); inp("conv_b", [2816]); inp("w_ffn_out", [2816, D])
    inp("ln3_g", [D]); inp("w_ple_gate", [D, D]); inp("w_ple", [256, D]); inp("final_g", [D])
    inp("c_ident", [128, 128]); inp("c_msl", [128, 128]); inp("c_msu", [128, 128]); inp("c_mui", [128, 128])
    inp("c_blk64", [128, 128]); inp("c_blk32", [128, 128]); inp("c_rowgp", [128, 128])
    outp("y", [NT, D]); outp("p_shift", [1792]); outp("p_wkv", [512, 64]); outp("p_re", [2048]); outp("p_im", [2048])
    outp("p_conv", [2, 2816]); outp("s_shift", [NS, 1792]); outp("s_wkv", [128, 4096]); outp("s_re", [NS, 2048])
    outp("s_im", [NS, 2048]); outp("s_conv", [NS, 2, 2816])
    scratch("PRW", [1792, NT], F32); scratch("UU", [512, NT], F32); scratch("GT", [2048, NT], BF16)
    scratch("YF", [512, NT], BF16); scratch("ZZ", [512, NT], BF16); scratch("X1", [NT, D], F32); scratch("X2", [NT, D], F32)
    scratch("SW", [6, NS, 512], F32); scratch("SY", [128, 64], F32)

    with contextlib.ExitStack() as gst:
        def gsb(name, shape, dt):
            return gst.enter_context(nc.sbuf_tensor("g_" + name, list(shape), dt))
        G0 = {}
        G0["identb"] = gsb("identb", [128, 128], BF16)
        G0["identf"] = gsb("identf", [128, 128], F32)
        with contextlib.ExitStack() as g2:
            def g2sb(name, shape, dt):
                return g2.enter_context(nc.sbuf_tensor("g_" + name, list(shape), dt))
            G0["BwT"] = g2sb("BwT", [128, 4, CS, 2, 128], BF16)
            G0["Kmat"] = g2sb("Kmat", [128, 4, CS, 128], BF16)
            G0["CwT"] = g2sb("CwT", [128, CS, 2, 16, 32], BF16)
            G0["Abar"] = g2sb("Abar", [128, 2, 2, 16], F32)
            if upto >= 1:
                phase1(nc, I, G0, debug)
            else:
                phase0(nc, I, G0, debug)
            if upto >= 2:
                phase2(nc, I, G0, True)
            if upto >= 2.5:
                phase2(nc, I, G0, False)
        g4 = contextlib.ExitStack()
        WFI = g4.enter_context(nc.sbuf_tensor("g_wfi", [128, 8, 5632], BF16))
        if upto >= 3:
            phase3(nc, I, G0, None, WFI)
        if upto >= 4:
            phase4(nc, I, G0, WFI)
        g4.close()
        if upto >= 5:
            phase5(nc, I, G0)
    return nc


def phase0(nc, I, G0, debug=False, ph=None):
    own = ph is None
    if own:
        ph = Ph(nc, "p0")
        ph.dma("pool", G0["identb"][:], I["c_ident"], W="identb")
        ph.dma("sp", G0["identf"][:], I["c_ident"], W="identf")
    sb = ph.sb
    lr = sb("lr", [128, 16], F32); li = sb("li", [128, 16], F32); dtl = sb("dtl", [128, 16], F32)
    Bre = sb("Bre", [128, 16, 16], F32); Bim = sb("Bim", [128, 16, 16], F32)
    ph.dma("sp", lr[:], I["A_re"].rearrange("(P gp) n -> (gp n) P", gp=2), W="lr", slow=True)
    ph.dma("sp", li[:], I["A_im"].rearrange("(P gp) n -> (gp n) P", gp=2), W="li", slow=True)
    ldt2 = I["log_dt"].rearrange("(P gp) -> gp P", gp=2)
    for gp in range(2):
        ph.dma("sp", dtl[64 * gp:64 * gp + 64, :], ldt2[gp].partition_broadcast(64), W="dtl", slow=True)
    ph.dma("sp", Bre[:], I["B_re"].rearrange("(P gp) n c -> (gp n) P c", gp=2), W="Bre")
    ph.dma("sp", Bim[:], I["B_im"].rearrange("(P gp) n c -> (gp n) P c", gp=2), W="Bim")
    rowgp = sb("rowgp", [128, 128], F32); blk32 = sb("blk32", [128, 128], F32)
    ph.dma("sp", rowgp[:], I["c_rowgp"], W="rowgp"); ph.dma("sp", blk32[:], I["c_blk32"], W="blk32")
    CT = [sb("CTr", [128, 4, 128], F32), sb("CTi", [128, 4, 128], F32)]
    c2 = sb("c2", [128, 128], F32)
    pA = ph.ps("pA", [128, 4, 128], F32)
    for ri, nm in enumerate(("C_re", "C_im")):
        for k in range(4):
            src = I[nm][k * 128:(k + 1) * 128, :]
            ph.dma("sp", c2[:, 0:64], src, W="c2"); ph.dma("sp", c2[:, 64:128], src, W="c2")
            ph.tt("dve", c2[:], c2[:], rowgp[:], ALU.mult, R=["c2", "rowgp"], W="c2")
            ph.tr(pA[:, k, :], c2[:], G0["identf"][:], R=["c2", "identf"], W="pA")
        ph.cp("dve", CT[ri][:], pA[:], R="pA", W="CT%d" % ri)
    t = {n: sb(n, [128, 16], F32) for n in ("dt", "e1", "mag", "ang", "sa", "ca", "sinv", "cosv", "ar", "ai", "den",
                                             "rden", "am1", "fr", "fi", "t1", "t2")}
    V = "dve"
    K = lambda *n: list(n)
    hpi = sb("hpi", [128, 1], F32)
    ph.memset(V, hpi[:], math.pi / 2, W="hpi")
    ph.act(t["dt"][:], dtl[:], AF.Exp, R="dtl", W="dt")
    ph.tt(V, t["e1"][:], lr[:], t["dt"][:], ALU.mult, R=K("lr", "dt"), W="e1")
    ph.act(t["mag"][:], t["e1"][:], AF.Exp, R="e1", W="mag")
    ph.tt(V, t["ang"][:], li[:], t["dt"][:], ALU.mult, R=K("li", "dt"), W="ang")
    ph.ts(V, t["sa"][:], t["ang"][:], 1.0 / 64, ALU.mult, R="ang", W="sa")
    ph.act(t["sinv"][:], t["sa"][:], AF.Sin, R="sa", W="sinv")
    ph.act(t["cosv"][:], t["sa"][:], AF.Sin, R=["sa", "hpi"], W="cosv", bias=hpi[:, 0:1])
    for _ in range(6):
        ph.tt(V, t["t1"][:], t["cosv"][:], t["cosv"][:], ALU.mult, R="cosv", W="t1")
        ph.tt(V, t["t2"][:], t["sinv"][:], t["sinv"][:], ALU.mult, R="sinv", W="t2")
        ph.stt(t["sinv"][:], t["cosv"][:], 2.0, t["sinv"][:], ALU.mult, ALU.mult, R=["cosv", "sinv", "t2"], W="sinv")
        ph.tt(V, t["cosv"][:], t["t1"][:], t["t2"][:], ALU.subtract, R=["t1", "t2", "sinv"], W="cosv")
    ph.tt(V, t["ar"][:], t["mag"][:], t["cosv"][:], ALU.mult, R=K("mag", "cosv"), W="ar")
    ph.tt(V, t["ai"][:], t["mag"][:], t["sinv"][:], ALU.mult, R=K("mag", "sinv"), W="ai")
    ph.tt(V, t["den"][:], lr[:], lr[:], ALU.mult, R="lr", W="den")
    ph.tt(V, t["t1"][:], li[:], li[:], ALU.mult, R="li", W="t1")
    ph.tt(V, t["den"][:], t["den"][:], t["t1"][:], ALU.add, R=K("den", "t1"), W="den")
    ph.op(V, lambda e: e.reciprocal(out=t["rden"][:], in_=t["den"][:]), R="den", W="rden")
    ph.ts(V, t["am1"][:], t["ar"][:], -1.0, ALU.add, R="ar", W="am1")
    ph.tt(V, t["t1"][:], t["am1"][:], lr[:], ALU.mult, R=K("am1", "lr", "den"), W="t1")
    ph.tt(V, t["t2"][:], t["ai"][:], li[:], ALU.mult, R=K("ai", "li"), W="t2")
    ph.tt(V, t["t1"][:], t["t1"][:], t["t2"][:], ALU.add, R=K("t1", "t2"), W="t1")
    ph.tt(V, t["fr"][:], t["t1"][:], t["rden"][:], ALU.mult, R=K("t1", "rden"), W="fr")
    ph.tt(V, t["t1"][:], t["ai"][:], lr[:], ALU.mult, R=K("ai", "lr", "fr"), W="t1")
    ph.tt(V, t["t2"][:], t["am1"][:], li[:], ALU.mult, R=K("am1", "li"), W="t2")
    ph.tt(V, t["t1"][:], t["t1"][:], t["t2"][:], ALU.subtract, R=K("t1", "t2"), W="t1")
    ph.tt(V, t["fi"][:], t["t1"][:], t["rden"][:], ALU.mult, R=K("t1", "rden"), W="fi")
    pwr = sb("pwr", [128, CS + 1, 16], F32); pwi = sb("pwi", [128, CS + 1, 16], F32)
    ph.memset(V, pwr[:, 0, :], 1.0, W="pw"); ph.memset(V, pwi[:, 0, :], 0.0, W="pw")
    for e in range(CS):
        ph.tt(V, t["t1"][:], pwr[:, e, :], t["ar"][:], ALU.mult, R=K("pw", "ar", "fi"), W="t1")
        ph.tt(V, t["t2"][:], pwi[:, e, :], t["ai"][:], ALU.mult, R=K("pw", "ai"), W="t2")
        ph.tt(V, pwr[:, e + 1, :], t["t1"][:], t["t2"][:], ALU.subtract, R=K("t1", "t2"), W="pw")
        ph.tt(V, t["t1"][:], pwr[:, e, :], t["ai"][:], ALU.mult, R=K("pw", "ai"), W="t1")
        ph.tt(V, t["t2"][:], pwi[:, e, :], t["ar"][:], ALU.mult, R=K("pw", "ar"), W="t2")
        ph.tt(V, pwi[:, e + 1, :], t["t1"][:], t["t2"][:], ALU.add, R=K("t1", "t2"), W="pw")
    Ab = G0["Abar"]
    ph.cp(V, Ab[:, 0, 0, :], pwr[:, CS, :], R="pw", W="Abar"); ph.cp(V, Ab[:, 0, 1, :], pwi[:, CS, :], R="pw", W="Abar")
    ph.cp(V, Ab[:, 1, 0, :], pwr[:, 1, :], R="pw", W="Abar"); ph.cp(V, Ab[:, 1, 1, :], pwi[:, 1, :], R="pw", W="Abar")
    bbr = sb("bbr", [128, 16, 16], F32); bbi = sb("bbi", [128, 16, 16], F32)
    u1 = sb("u1", [128, 16, 16], F32); u2 = sb("u2", [128, 16, 16], F32)
    frb = bc(t["fr"][:, :].unsqueeze(2), [128, 16, 16]); fib = bc(t["fi"][:, :].unsqueeze(2), [128, 16, 16])
    ph.tt(V, u1[:], Bre[:], frb, ALU.mult, R=K("Bre", "fr"), W="u1")
    ph.tt(V, u2[:], Bim[:], fib, ALU.mult, R=K("Bim", "fi"), W="u2")
    ph.tt(V, bbr[:], u1[:], u2[:], ALU.subtract, R=K("u1", "u2"), W="bbr")
    ph.tt(V, u1[:], Bim[:], frb, ALU.mult, R=K("Bim", "fr", "bbr"), W="u1")
    ph.tt(V, u2[:], Bre[:], fib, ALU.mult, R=K("Bre", "fi", "bbr"), W="u2")
    ph.tt(V, bbi[:], u1[:], u2[:], ALU.add, R=K("u1", "u2"), W="bbi")
    Ew = sb("Ew", [128, CS, 2, 16, 2, 16], F32)
    ph.memset(V, Ew[:].rearrange("p a b c d e -> p (a b c d e)"), 0.0, W="Ew")
    for e in range(CS):
        pr = bc(pwr[:, e, :].unsqueeze(2), [128, 16, 16]); pi = bc(pwi[:, e, :].unsqueeze(2), [128, 16, 16])
        ph.tt(V, u1[:], bbr[:], pr, ALU.mult, R=K("bbr", "pw", "Ew"), W="u1")
        ph.tt(V, u2[:], bbi[:], pi, ALU.mult, R=K("bbi", "pw", "Ew"), W="u2")
        ph.tt(V, u1[:], u1[:], u2[:], ALU.subtract, R=K("u1", "u2"), W="u1")
        for gp in range(2):
            ph.cp(V, Ew[64 * gp:64 * gp + 64, e, 0, :, gp, :], u1[64 * gp:64 * gp + 64, :, :], R="u1", W="Ew")
        ph.tt(V, u1[:], bbr[:], pi, ALU.mult, R=K("bbr", "pw", "Ew"), W="u1")
        ph.tt(V, u2[:], bbi[:], pr, ALU.mult, R=K("bbi", "pw", "Ew"), W="u2")
        ph.tt(V, u1[:], u1[:], u2[:], ALU.add, R=K("u1", "u2"), W="u1")
        for gp in range(2):
            ph.cp(V, Ew[64 * gp:64 * gp + 64, e, 1, :, gp, :], u1[64 * gp:64 * gp + 64, :, :], R="u1", W="Ew")
    CTin = sb("CTin", [128, 4, 128], F32)
    ph.ts(V, CTin[:], CT[1][:], -1.0, ALU.mult, R="CT1", W="CTin")
    pB = [ph.ps("pB%d" % i, [128, 4, 128], F32) for i in range(2)]
    n = 0
    for j in range(CS):
        e = CS - 1 - j
        for ri in range(2):
            pb = pB[n % 2]; n += 1
            for k in range(4):
                src = Ew[:, e, ri, 4 * k:4 * k + 4, :, :].rearrange("p a b c -> p (a b c)")
                ph.tr(pb[:, k, :], src, G0["identf"][:], R=["Ew", "identf"], W="pB%d" % ((n - 1) % 2))
            ph.cp("act" if n % 2 else "dve", G0["BwT"][:, :, j, ri, :], pb[:], R="pB%d" % ((n - 1) % 2), W="BwT")
    for tau in range(CS):
        pb = pB[n % 2]; key = "pB%d" % (n % 2); n += 1
        for k in range(4):
            lr_ = Ew[:, tau, 0, 4 * k:4 * k + 4, :, :].rearrange("p a b c -> p (a b c)")
            li_ = Ew[:, tau, 1, 4 * k:4 * k + 4, :, :].rearrange("p a b c -> p (a b c)")
            ph.mm(pb[:, k, :], lr_, CT[0][:, k, :], True, False, R=["Ew", "CT0"], W=key)
            ph.mm(pb[:, k, :], li_, CTin[:, k, :], False, True, R=["Ew", "CTin"], W=key)
        ph.tt(V, G0["Kmat"][:, :, tau, :], pb[:], bc(blk32[:, :].unsqueeze(1), [128, 4, 128]), ALU.mult,
              R=[key, "blk32"], W="Kmat")
    w1 = sb("w1", [128, 16, 32], F32); w2_ = sb("w2", [128, 16, 32], F32)
    CTr3 = CT[0][:].rearrange("p k (a b) -> p (k a) b", a=4); CTi3 = CT[1][:].rearrange("p k (a b) -> p (k a) b", a=4)
    for i in range(CS):
        pr = bc(pwr[:, i + 1, :].unsqueeze(2), [128, 16, 32]); pi = bc(pwi[:, i + 1, :].unsqueeze(2), [128, 16, 32])
        ph.tt(V, w1[:], CTr3, pr, ALU.mult, R=K("CT0", "pw", "CwT"), W="w1")
        ph.tt(V, w2_[:], CTi3, pi, ALU.mult, R=K("CT1", "pw", "CwT"), W="w2")
        ph.tt(V, G0["CwT"][:, i, 0, :, :], w1[:], w2_[:], ALU.subtract, R=K("w1", "w2"), W="CwT")
        ph.tt(V, w1[:], CTr3, pi, ALU.mult, R=K("CT0", "pw", "CwT"), W="w1")
        ph.tt(V, w2_[:], CTi3, pr, ALU.mult, R=K("CT1", "pw", "CwT"), W="w2")
        ph.tt(V, w1[:], w1[:], w2_[:], ALU.add, R=K("w1", "w2"), W="w1")
        ph.ts(V, G0["CwT"][:, i, 1, :, :], w1[:], -1.0, ALU.mult, R="w1", W="CwT")
    if debug:
        ph.dma("sp", I["d_BwT"], G0["BwT"][:].rearrange("p a b c d -> p (a b c d)"), R="BwT")
        ph.dma("sp", I["d_Kmat"], G0["Kmat"][:].rearrange("p a b c -> p (a b c)"), R="Kmat")
        ph.dma("sp", I["d_CwT"], G0["CwT"][:].rearrange("p a b c d -> p (a b c d)"), R="CwT")
        ph.dma("sp", I["d_Abar"], G0["Abar"][:].rearrange("p a b c -> p (a b c)"), R="Abar")
    if own:
        ph.finish()


def phase1(nc, I, G0, debug=False):
    ph = Ph(nc, "p1")
    win = ph.sb("win", [128, 8, 4352], BF16)
    for k in range(8):
        ph.dma("pool", win[:, k, :], I["w_in"][k * 128:(k + 1) * 128, :], W="win%d" % k)
    ph.dma("pool", G0["identb"][:], I["c_ident"], W="identb")
    ph.dma("sp", G0["identf"][:], I["c_ident"], W="identf")
    ph.rec_begin()
    phase0(nc, I, G0, debug, ph=ph)
    s0 = ph.rec_end()
    ph.rec_begin()
    G = norm_scratch(ph, G0)
    g1c = ph.sb("g1c", [128, 8], F32)
    load_col(ph, g1c[:], I["ln1_g"], 8, "g1c")
    hTs = [ph.sb("hT%d" % i, [128, 8, 512], BF16) for i in range(2)]
    xts = [ph.sb("xt%d" % i, [128, D], F32) for i in range(2)]
    pm = [ph.ps("pm%d" % i, [128, 512], F32) for i in range(4)]
    stf = [ph.sb("stf%d" % i, [128, 512], F32) for i in range(4)]
    stb = [ph.sb("stb%d" % i, [128, 512], BF16) for i in range(3)]
    WK = ["win%d" % k for k in range(8)]
    nx = nf = nb = npm = 0
    for bi_, (t0, nt) in enumerate(BLOCKS):
        P = min(128, nt)
        hT = hTs[bi_ % 2]; hk = "hT%d" % (bi_ % 2)
        for s in range((nt + 127) // 128):
            xt = xts[nx % 2]; tg = str(nx % 2); nx += 1
            ph.dma("sp", xt[:P, :], I["xall"][t0 + s * 128:t0 + s * 128 + P, :], W="xt" + tg)
            rms_to_hT(ph, G, xt, P, g1c, hT, s * 128, tg, "g1c", hk)
        for m in range(34):
            pb = pm[npm % 4]; pk = "pm%d" % (npm % 4); npm += 1
            for k in range(8):
                ph.mm(pb[:, :nt], win[:, k, m * 128:(m + 1) * 128], hT[:, k, :nt], k == 0, k == 7,
                      R=["win%d" % k, hk], W=pk)
            if m < 18:
                sf = stf[nf % 4]; sk = "stf%d" % (nf % 4); nf += 1
                ph.cp("dve" if m % 2 else "act", sf[:, :nt], pb[:, :nt], R=pk, W=sk)
                if m < 14:
                    ph.dma("sp", I["PRW"][m * 128:(m + 1) * 128, t0:t0 + nt], sf[:, :nt], R=sk)
                else:
                    ph.dma("sp", I["UU"][(m - 14) * 128:(m - 13) * 128, t0:t0 + nt], sf[:, :nt], R=sk)
            else:
                sbf = stb[nb % 3]; sk = "stb%d" % (nb % 3); nb += 1
                ph.act(sbf[:, :nt], pb[:, :nt], AF.Sigmoid, R=pk, W=sk)
                ph.dma("act", I["GT"][(m - 18) * 128:(m - 17) * 128, t0:t0 + nt], sbf[:, :nt], R=sk)
    s1 = ph.rec_end()
    ph.play(s1, s0)
    ph.finish()


def alloc_w3(nc, st):
    t = lambda n, shp: st.enter_context(nc.sbuf_tensor("w3_" + n, shp, BF16))
    return {"rwo": t("rwo", [128, 4, D]), "glu": t("glu", [128, 4, 2048]), "wo": t("wo", [128, 8, D])}


def load_w3(ph, I, W3):
    for k in range(4):
        ph.dma("pool", W3["rwo"][:, k, :], I["w_rw_out"][k * 128:(k + 1) * 128, :], W="rwo")
        ph.dma("pool", W3["glu"][:, k, :], I["w_glu"][k * 128:(k + 1) * 128, :], W="glu")
    for k in range(8):
        ph.dma("pool", W3["wo"][:, k, :], I["w_out"][k * 128:(k + 1) * 128, :], W="wo")


def phase2(nc, I, G0, prompt, W3=None):
    ph = Ph(nc, "p2a" if prompt else "p2b")
    sb, ps = ph.sb, ph.ps
    V = "dve"
    if W3 is not None:
        load_w3(ph, I, W3)
    ph._s5tmp = [sb("s5a", [128, 2, 16], F32), sb("s5b", [128, 2, 16], F32)]
    ph._s5xb = sb("Xb", [128, 2, 16, 64], BF16)
    ph._s5du = sb("s5du", [128, 512], F32)
    if prompt:
        msl = sb("msl", [128, 128], BF16); msu = sb("msu", [128, 128], BF16); mui = sb("mui", [128, 128], BF16)
        ph.dma("pool", msl[:], I["c_msl"], W="msl"); ph.dma("pool", msu[:], I["c_msu"], W="msu")
        ph.dma("pool", mui[:], I["c_mui"], W="mui")
    blk64 = sb("blk64", [128, 128], F32); ph.dma("sp", blk64[:], I["c_blk64"], W="blk64")
    w2a2 = sb("w2a2", [128, 512], BF16); g2b = sb("g2b", [128, 512], BF16)
    ph.dma("pool", w2a2[0:64, :], I["w2"], W="w2a2"); ph.dma("pool", w2a2[64:128, :], I["a2"], W="w2a2")
    ph.dma("pool", g2b[:], I["g2"], W="g2b")
    pc = {}
    for nm, n in (("mu_shift", 14), ("w0", 4), ("a0", 4), ("k_k", 4), ("k_a", 4), ("r_k", 4), ("lnx_g", 4),
                  ("lnx_b", 4), ("D_skip", 4)):
        pc[nm] = sb("c_" + nm, [128, n], F32)
        load_col(ph, pc[nm][:], I[nm], n, "c_" + nm)
    PK = ["c_" + k for k in pc]
    scm = sb("scm", [128, 4, 128], F32)
    ph.memset(V, scm[:].rearrange("p a b -> p (a b)"), 1.0, W="scm"); ph.memset(V, scm[:, :, 0:1], 0.0, W="scm")
    eps_gn = sb("eps_gn", [128, 1], F32); ph.memset(V, eps_gn[:], 64e-5, W="eps_gn")
    if prompt:
        Sst = sb("Sst", [128, 4, 64], F32); Sbd = sb("Sbd", [128, 4, 128], BF16)
        ph.memset(V, Sst[:].rearrange("p a b -> p (a b)"), 0.0, W="Sst")
        ph.memset(V, Sbd[:].rearrange("p a b -> p (a b)"), 0.0, W="Sbd")
        Xs = sb("Xs", [128, 2, 16, 65], F32)
        ph.memset(V, Xs[:].rearrange("p a b c -> p (a b c)"), 0.0, W="Xs")
        Pf = sb("Pf", [128, 14, 513], F32)
        ph.memset(V, Pf[:, :, 0:1], 0.0, W="Pf")
    WB = 512 if prompt else NS
    WC = 128 if prompt else NS
    uf = sb("uf", [128, 4, WB], F32); ub = sb("ub", [128, 4, WB], BF16)
    YFb = sb("YFb", [128, 4, WB], BF16); ZZb = sb("ZZb", [128, 4, WB], BF16)
    f4 = lambda n: sb(n, [128, 4, WC], F32)
    b4 = lambda n: sb(n, [128, 4, WC], BF16)
    XS = sb("XS", [128, 14, WC], F32); dd = sb("dd", [128, 14, WC], F32)
    lin = sb("lin", [128, WC], BF16); sgx = sb("sgx", [128, WC], BF16)
    sig = f4("sig"); aa = f4("aa"); gg = f4("gg"); kk0 = f4("kk0"); tq = f4("tq"); rn = f4("rn"); kkn = f4("kkn")
    bb = f4("bb"); kmod = f4("kmod"); bon = f4("bon"); cs = f4("cs"); ex1 = f4("ex1"); ex2 = f4("ex2"); ex3 = f4("ex3")
    nbias = sb("nbias", [128, 4], F32); PCt = sb("PCt", [128, 4], F32)
    gns = f4("gns")
    KX = {n_: n_ for n_ in ("rT", "kT", "bT", "aT", "khT", "bhT", "vT", "PCt", "bon", "gg")}
    if prompt:
        rT = b4("rT"); kT = b4("kT"); bT = b4("bT"); aT = b4("aT"); khT = b4("khT"); bhT = b4("bhT"); vT = b4("vT")
        alt = {"rT": b4("rT1"), "kT": b4("kT1"), "bT": b4("bT1"), "aT": b4("aT1"), "khT": b4("khT1"),
               "bhT": b4("bhT1"), "vT": b4("vT1"), "PCt": sb("PCt1", [128, 4], F32), "bon": f4("bon1"), "gg": f4("gg1")}
        Vtok = sb("Vtok", [128, 512], BF16); Khtok = sb("Khtok", [128, 512], BF16); Bhtok = sb("Bhtok", [128, 512], BF16)
        h8 = lambda n: sb(n, [128, 8, 128], BF16)
        Nb = [h8("Nb0"), h8("Nb1")]; Lb = [h8("Lb0"), h8("Lb1")]; Mt = [h8("Mt0"), h8("Mt1")]
        LKb = h8("LKb"); Arb = h8("Arb"); Ark = h8("Ark")
        Wbf = sb("Wbf", [128, 512], BF16); Ubf = sb("Ubf", [128, 512], BF16)
        tS = sb("tS", [128, 4, 64], F32)
    Ysb = sb("Ysb", [128, 8, 64], F32); Ysq = sb("Ysq", [128, 8, 64], F32); ynb = sb("ynb", [128, 8, 64], BF16)
    gn = sb("gn", [128, 6, 8], F32)
    pF = [ps("pF%d" % i, [128, 4, 128], F32) for i in range(6)]
    pT = [ps("pTb%d" % i, [128, 8, 128], BF16) for i in range(2)]
    cnt = {"f": 0, "t": 0}

    def getF():
        i = cnt["f"] % 6; cnt["f"] += 1
        return pF[i], "pF%d" % i

    def mkpool(base):
        st_ = {"n": 0}

        def get():
            i = base + st_["n"] % 2; st_["n"] += 1
            return pF[i], "pF%d" % i
        return get
    getF_prep, getF_core, getFs = mkpool(0), mkpool(2), mkpool(4)

    def getT():
        i = cnt["t"] % 2; cnt["t"] += 1
        return pT[i], "pTb%d" % i

    ib = G0["identb"]

    if not prompt:
        sample_mixer(ph, I, G0, locals())
        ph.finish()
        return
    Lbase = dict(locals())
    Lpar = [dict(Lbase), dict(Lbase)]
    Lpar[1].update(alt)
    Lpar[1]["KX"] = {n_: n_ + "1" for n_ in KX}
    for bi, (t0, nt) in enumerate(BLOCKS[:4]):
        if bi > 0:
            ph.cp(V, Pf[:, :, 0:1], Pf[:, :, 512:513], R="Pf", W="Pf")
        ph.dma("sp", Pf[:, :, 1:513], I["PRW"][:, t0:t0 + nt].rearrange("(m p) t -> p m t", p=128), W="Pf")
        ph.dma("act", uf[:], I["UU"][:, t0:t0 + nt].rearrange("(m p) t -> p m t", p=128), W="uf")
        ph.cp("act", ub[:].rearrange("p a b -> p (a b)"), uf[:].rearrange("p a b -> p (a b)"), R="uf", W="ub")
        if bi == 3:
            ph.dma("sp", I["p_shift"].rearrange("(m p) -> p m", p=128), Pf[:, :, 512], R="Pf", slow=True)
        ph.rec_begin()
        s5_block(ph, I, G0, pc, Xs, ub, ZZb, getFs, nchunk=64, which=0, ncol=512)
        ph.dma("act", I["ZZ"][:, t0:t0 + nt].rearrange("(m p) t -> p m t", p=128), ZZb[:], R="ZZb")
        s5s = ph.rec_end()
        preps, cores = [], []
        for c in range(4):
            c0 = c * 128
            Lc = dict(Lpar[c % 2]); Lc["getF"] = getF_prep
            Lk = dict(Lpar[c % 2]); Lk["getF"] = getF_core
            ph.rec_begin()
            ph.tt(V, dd[:], Pf[:, :, c0:c0 + 128], Pf[:, :, c0 + 1:c0 + 129], ALU.subtract, R="Pf", W="dd")
            ph.tt(V, dd[:], dd[:], bc(pc["mu_shift"][:, :].unsqueeze(2), [128, 14, 128]), ALU.mult,
                  R=["dd", "c_mu_shift"], W="dd")
            ph.tt(V, XS[:], dd[:], Pf[:, :, c0 + 1:c0 + 129], ALU.add, R=["dd", "Pf"], W="XS")
            rwkv_prep_and_core(ph, Lc, c, c0)
            preps.append(ph.rec_end())
            ph.rec_begin()
            wkv_core(ph, Lk, c, c0)
            cores.append(ph.rec_end())
        q = (len(s5s) + 3) // 4
        s5p = [s5s[i * q:(i + 1) * q] for i in range(4)]
        ph.play(preps[0])
        ph.play(cores[0], preps[1], s5p[0])
        ph.play(cores[1], preps[2], s5p[1])
        ph.play(cores[2], preps[3], s5p[2])
        ph.play(cores[3], s5p[3])
        ph.dma("sp", I["YF"][:, t0:t0 + nt].rearrange("(m p) t -> p m t", p=128), YFb[:], R="YFb")
    ph.dma("sp", I["p_wkv"].rearrange("(m p) v -> p m v", p=128), Sst[:], R="Sst")
    ph.dma("sp", I["p_re"].rearrange("(P p) -> p P", p=128), Xs[:, 0, :, 0], R="Xs", slow=True)
    ph.dma("sp", I["p_im"].rearrange("(P p) -> p P", p=128), Xs[:, 1, :, 0], R="Xs", slow=True)
    ph.finish()


def rwkv_prep_and_core(ph, L, c, c0):
    V = "dve"
    KX = L["KX"]
    pc = L["pc"]; XS = L["XS"]; getF = L["getF"]; getT = L["getT"]; ib = L["ib"]
    sig, aa, gg, kk0, tq, rn, kkn = L["sig"], L["aa"], L["gg"], L["kk0"], L["tq"], L["rn"], L["kkn"]
    bb, kmod, bon, cs, ex1, ex2, ex3 = L["bb"], L["kmod"], L["bon"], L["cs"], L["ex1"], L["ex2"], L["ex3"]
    rT, kT, bT, aT, khT, bhT, vT = L["rT"], L["kT"], L["bT"], L["aT"], L["khT"], L["bhT"], L["vT"]
    lin, sgx, w2a2, g2b, blk64 = L["lin"], L["sgx"], L["w2a2"], L["g2b"], L["blk64"]
    nbias, PCt, scm = L["nbias"], L["PCt"], L["scm"]
    r_ = XS[:, 0:4, :]; k_ = XS[:, 4:8, :]; v_ = XS[:, 8:12, :]
    B4 = lambda t: bc(t[:, :].unsqueeze(2), [128, 4, 128])
    fl = lambda t: t[:].rearrange("p a b -> p (a b)")
    ph.act(lin[0:64, :], XS[0:64, 12, :], AF.Tanh, R="XS", W="lin")
    ph.cp("act", lin[64:128, :], XS[64:128, 12, :], R="XS", W="lin")
    ph.act(sgx[:], XS[:, 13, :], AF.Sigmoid, R="XS", W="sgx")
    pw_, kw_ = getF()
    for m in range(4):
        ph.mm(pw_[:, m, :], w2a2[0:64, m * 128:(m + 1) * 128], lin[0:64, :], True, True, R=["w2a2", "lin"], W=kw_)
    for m in range(4):
        ph.act(sig[:, m, :], pw_[:, m, :], AF.Sigmoid, R=[kw_, "c_w0"], W="sig", bias=pc["w0"][:, m:m + 1])
    pa_, ka_ = getF()
    for m in range(4):
        ph.mm(pa_[:, m, :], w2a2[64:128, m * 128:(m + 1) * 128], lin[64:128, :], True, True, R=["w2a2", "lin"], W=ka_)
    for m in range(4):
        ph.act(aa[:, m, :], pa_[:, m, :], AF.Sigmoid, R=[ka_, "c_a0"], W="aa", bias=pc["a0"][:, m:m + 1])
    pg_, kg_ = getF()
    for m in range(4):
        ph.mm(pg_[:, m, :], g2b[:, m * 128:(m + 1) * 128], sgx[:], True, True, R=["g2b", "sgx"], W=kg_)
    ph.cp("act", gg[:], pg_[:], R=kg_, W=KX["gg"])
    ph.tt(V, kk0[:], k_, B4(pc["k_k"]), ALU.mult, R=["XS", "c_k_k"], W="kk0")
    ph.tt(V, tq[:], kk0[:], kk0[:], ALU.mult, R="kk0", W="tq")
    pq, kq = getF()
    for m in range(4):
        ph.mm(pq[:, m, :], blk64[:], tq[:, m, :], True, True, R=["blk64", "tq"], W=kq)
    ph.act(rn[:], pq[:], AF.Sqrt, R=kq, W="rn")
    ph.ts(V, rn[:], rn[:], 1e-12, ALU.max, R="rn", W="rn")
    ph.op(V, lambda e: e.reciprocal(out=fl(rn), in_=fl(rn)), R="rn", W="rn")
    ph.tt(V, kkn[:], kk0[:], rn[:], ALU.mult, R=["kk0", "rn"], W="kkn")
    ph.tt(V, bb[:], kkn[:], aa[:], ALU.mult, R=["kkn", "aa"], W="bb")
    ph.tt(V, tq[:], aa[:], B4(pc["k_a"]), ALU.mult, R=["aa", "c_k_a", kq], W="tq")
    ph.tt(V, tq[:], tq[:], B4(pc["k_a"]), ALU.subtract, R=["tq", "c_k_a"], W="tq")
    ph.stt(kmod[:], tq[:], 1.0, k_, ALU.add, ALU.mult, R=["tq", "XS"], W="kmod")
    ph.tt(V, tq[:], r_, kmod[:], ALU.mult, R=["XS", "kmod"], W="tq")
    ph.tt(V, tq[:], tq[:], B4(pc["r_k"]), ALU.mult, R=["tq", "c_r_k"], W="tq")
    pq2, kq2 = getF()
    for m in range(4):
        ph.mm(pq2[:, m, :], blk64[:], tq[:, m, :], True, True, R=["blk64", "tq"], W=kq2)
    ph.tt(V, bon[:], pq2[:], v_, ALU.mult, R=[kq2, "XS"], W=KX["bon"])
    ph.op(V, lambda e: e.tensor_tensor_scan(out=fl(cs), data0=fl(scm), data1=fl(sig), initial=0.0, op0=ALU.mult,
                                             op1=ALU.add), R=["scm", "sig"], W="cs")
    ph.ts(V, nbias[:], cs[:, :, 127], -C1, ALU.mult, R="cs", W="nbias")
    ph.act(PCt[:], nbias[:], AF.Exp, R="nbias", W=KX["PCt"])
    ph.act(ex1[:], cs[:], AF.Exp, R="cs", W="ex1", scale=-C1)
    ph.tt(V, rT[:], r_, ex1[:], ALU.mult, R=["XS", "ex1"], W=KX["rT"])
    ph.act(ex2[:], cs[:], AF.Exp, R="cs", W="ex2", scale=C1)
    ph.tt(V, kT[:], kmod[:], ex2[:], ALU.mult, R=["kmod", "ex2"], W=KX["kT"])
    ph.tt(V, bT[:], bb[:], ex2[:], ALU.mult, R=["bb", "ex2"], W=KX["bT"])
    ph.tt(V, ex3[:], cs[:], sig[:], ALU.subtract, R=["cs", "sig"], W="ex3")
    ph.act(ex3[:], ex3[:], AF.Exp, R="ex3", W="ex3", scale=-C1)
    ph.stt(aT[:], kkn[:], -1.0, ex3[:], ALU.mult, ALU.mult, R=["kkn", "ex3"], W=KX["aT"])
    for m in range(4):
        ph.act(ex1[:, m, :], cs[:, m, :], AF.Exp, R=["cs", "nbias", KX["rT"]], W="ex1", bias=nbias[:, m:m + 1], scale=C1)
    ph.tt(V, khT[:], kmod[:], ex1[:], ALU.mult, R=["kmod", "ex1"], W=KX["khT"])
    ph.tt(V, bhT[:], bb[:], ex1[:], ALU.mult, R=["bb", "ex1"], W=KX["bhT"])
    ph.cp("act", vT[:], v_, R="XS", W=KX["vT"])


def wkv_core(ph, L, c, c0):
    V = "dve"
    KX = L["KX"]
    getF = L["getF"]; getT = L["getT"]; ib = L["ib"]
    rT, kT, bT, aT, khT, bhT, vT = L["rT"], L["kT"], L["bT"], L["aT"], L["khT"], L["bhT"], L["vT"]
    Vtok, Khtok, Bhtok = L["Vtok"], L["Khtok"], L["Bhtok"]
    Nb, Lb, Mt, LKb, Arb, Ark = L["Nb"], L["Lb"], L["Mt"], L["LKb"], L["Arb"], L["Ark"]
    msl, msu, mui = L["msl"], L["msu"], L["mui"]
    Wbf, Ubf, Ysb, Ysq, ynb, gn = L["Wbf"], L["Ubf"], L["Ysb"], L["Ysq"], L["ynb"], L["gn"]
    Sst, Sbd, PCt, tS = L["Sst"], L["Sbd"], L["PCt"], L["tS"]
    pc = L["pc"]; bon, gg, YFb = L["bon"], L["gg"], L["YFb"]
    M4 = lambda m_: bc(m_[:, :].unsqueeze(1), [128, 4, 128])
    pt, kt = getT()
    for m in range(4):
        ph.tr(pt[:, m, :], vT[:, m, :], ib[:], R=KX["vT"], W=kt)
    for m in range(4):
        ph.tr(pt[:, 4 + m, :], khT[:, m, :], ib[:], R=KX["khT"], W=kt)
    ph.cp("act", Vtok[:], pt[:, 0:4, :].rearrange("p a b -> p (a b)"), R=kt, W="Vtok")
    ph.cp(V, Khtok[:], pt[:, 4:8, :].rearrange("p a b -> p (a b)"), R=kt, W="Khtok")
    pt2, kt2 = getT()
    for m in range(4):
        ph.tr(pt2[:, m, :], bhT[:, m, :], ib[:], R=KX["bhT"], W=kt2)
    ph.cp("act", Bhtok[:], pt2[:, 0:4, :].rearrange("p a b -> p (a b)"), R=kt2, W="Bhtok")

    def hsl(t, h):
        return t[64 * (h % 2):64 * (h % 2) + 64, h // 2, :]

    def amat(dst, dkey, lhs, lkey, rhs, rkey, mask, mkey):
        for par in range(2):
            pb, pk = getF()
            for q in range(4):
                h = 2 * q + par
                ph.mm(pb[:, q, :], hsl(lhs, h), hsl(rhs, h), True, True, R=[lkey, rkey], W=pk)
            ph.tt(V, dst[:, par:8:2, :], pb[:], M4(mask), ALU.mult, R=[pk, mkey], W=dkey)

    amat(Nb[0], "Nb0", aT, KX["aT"], bT, KX["bT"], msl, "msl")
    amat(Lb[0], "Lb0", bT, KX["bT"], aT, KX["aT"], msu, "msu")
    amat(LKb, "LKb", kT, KX["kT"], aT, KX["aT"], msu, "msu")
    amat(Arb, "Arb", bT, KX["bT"], rT, KX["rT"], mui, "mui")
    amat(Ark, "Ark", kT, KX["kT"], rT, KX["rT"], mui, "mui")
    for half in range(2):
        ph.tt(V, Mt[0][:, half * 4:half * 4 + 4, :], Lb[0][:, half * 4:half * 4 + 4, :], M4(ib), ALU.add,
              R=["Lb0", "identb"], W="Mt0")
    cur = 0
    for lvl in range(6):
        nxt = 1 - cur
        for half in range(2):
            pb, pk = getF()
            for q in range(4):
                h = half * 4 + q
                ph.mm(pb[:, q, :], Lb[cur][:, h, :], Nb[cur][:, h, :], True, True, R=["Lb%d" % cur, "Nb%d" % cur], W=pk)
            ph.cp("act", Nb[nxt][:, half * 4:half * 4 + 4, :], pb[:], R=pk, W="Nb%d" % nxt)
        if lvl < 5:
            for half in range(2):
                pb, pk = getF()
                for q in range(4):
                    h = half * 4 + q
                    ph.mm(pb[:, q, :], Nb[cur][:, h, :], Lb[cur][:, h, :], True, True,
                          R=["Lb%d" % cur, "Nb%d" % cur], W=pk)
                ph.cp("act", Lb[nxt][:, half * 4:half * 4 + 4, :], pb[:], R=pk, W="Lb%d" % nxt)
        for half in range(2):
            pb, pk = getF()
            for q in range(4):
                h = half * 4 + q
                ph.mm(pb[:, q, :], Nb[nxt][:, h, :], Mt[cur][:, h, :], True, True, R=["Nb%d" % nxt, "Mt%d" % cur], W=pk)
            ph.tt(V, Mt[nxt][:, half * 4:half * 4 + 4, :], pb[:], Mt[cur][:, half * 4:half * 4 + 4, :], ALU.add,
                  R=[pk, "Mt%d" % cur], W="Mt%d" % nxt)
        cur = nxt
    MtF = Mt[cur]; mk = "Mt%d" % cur
    def hcols(pb, h):
        return pb[:].rearrange("p a b -> p (a b)")[:, h * 64:h * 64 + 64]

    def pcols(pb, m):
        return pb[:].rearrange("p a b -> p (a b)")[:, m * 128:m * 128 + 128]

    pb, pk = getF()
    for m in range(4):
        ph.mm(pcols(pb, m), aT[:, m, :], Sbd[:, m, :], True, False, R=[KX["aT"], "Sbd"], W=pk)
        for hh in range(2):
            h = 2 * m + hh
            ph.mm(hcols(pb, h), LKb[:, h, :], Vtok[:, h * 64:h * 64 + 64], False, hh == 1, R=["LKb", "Vtok"], W=pk)
    ph.cp("act", Wbf[:], pb[:].rearrange("p a b -> p (a b)"), R=pk, W="Wbf")
    pb, pk = getF()
    for h in range(8):
        ph.mm(hcols(pb, h), MtF[:, h, :], Wbf[:, h * 64:h * 64 + 64], True, True, R=[mk, "Wbf"], W=pk)
    ph.cp("act", Ubf[:], pb[:].rearrange("p a b -> p (a b)"), R=pk, W="Ubf")
    pb, pk = getF()
    for m in range(4):
        ph.mm(pcols(pb, m), rT[:, m, :], Sbd[:, m, :], True, False, R=[KX["rT"], "Sbd"], W=pk)
        for hh in range(2):
            h = 2 * m + hh
            ph.mm(hcols(pb, h), Arb[:, h, :], Ubf[:, h * 64:h * 64 + 64], False, False, R=["Arb", "Ubf"], W=pk)
            ph.mm(hcols(pb, h), Ark[:, h, :], Vtok[:, h * 64:h * 64 + 64], False, hh == 1, R=["Ark", "Vtok"], W=pk)
    ph.cp("act", Ysb[:].rearrange("p a b -> p (a b)"), pb[:].rearrange("p a b -> p (a b)"), R=pk, W="Ysb")
    pS, kS = getF()
    for m in range(4):
        ph.mm(pS[:, m, :], Bhtok[:, m * 128:(m + 1) * 128], Ubf[:, m * 128:(m + 1) * 128], True, False,
              R=["Bhtok", "Ubf"], W=kS)
        ph.mm(pS[:, m, :], Khtok[:, m * 128:(m + 1) * 128], Vtok[:, m * 128:(m + 1) * 128], False, True,
              R=["Khtok", "Vtok"], W=kS)
    ph.tt(V, tS[:], Sst[:], bc(PCt[:, :].unsqueeze(2), [128, 4, 64]), ALU.mult, R=["Sst", KX["PCt"]], W="tS")
    for hh in range(2):
        rs = slice(64 * hh, 64 * hh + 64)
        ph.tt(V, Sst[rs, :, :], tS[rs, :, :], pS[rs, :, 64 * hh:64 * hh + 64], ALU.add, R=["tS", kS], W="Sst")
        ph.cp(V, Sbd[rs, :, 64 * hh:64 * hh + 64], Sst[rs, :, :], R="Sst", W="Sbd")
    groupnorm_out(ph, L, c0, 128)


def groupnorm_out(ph, L, c0, P):
    V = "dve"
    KX = L["KX"]
    Ysb, Ysq, ynb, gn = L["Ysb"], L["Ysq"], L["ynb"], L["gn"]
    pc = L["pc"]; bon, gg, YFb = L["bon"], L["gg"], L["YFb"]; getT = L["getT"]; ib = L["ib"]
    eps_gn = L["eps_gn"]; ex2 = L["gns"]
    ph.op(V, lambda e: e.tensor_reduce(out=gn[:P, 0, :], in_=Ysb[:P], axis=AX.X, op=ALU.add), R="Ysb", W="gn")
    ph.act(Ysq[:P].rearrange("p a b -> p (a b)"), Ysb[:P].rearrange("p a b -> p (a b)"), AF.Square, R="Ysb", W="Ysq")
    ph.op(V, lambda e: e.tensor_reduce(out=gn[:P, 1, :], in_=Ysq[:P], axis=AX.X, op=ALU.add), R="Ysq", W="gn")
    ph.ts(V, gn[:P, 2, :], gn[:P, 0, :], 1.0 / 64, ALU.mult, R="gn", W="gn")
    ph.tt(V, gn[:P, 3, :], gn[:P, 2, :], gn[:P, 2, :], ALU.mult, R="gn", W="gn")
    ph.stt(gn[:P, 4, :], gn[:P, 1, :], 1.0 / 64, gn[:P, 3, :], ALU.mult, ALU.subtract, R="gn", W="gn")
    ph.act(gn[:P, 4, :], gn[:P, 4, :], AF.Sqrt, R=["gn", "eps_gn"], W="gn", bias=eps_gn[:P, 0:1])
    ph.op(V, lambda e: e.reciprocal(out=gn[:P, 5, :], in_=gn[:P, 4, :]), R="gn", W="gn")
    ph.tt(V, Ysq[:P], Ysb[:P], bc(gn[:P, 2, :].unsqueeze(2), [P, 8, 64]), ALU.subtract, R=["Ysb", "gn"], W="Ysq")
    ph.tt(V, ynb[:P], Ysq[:P], bc(gn[:P, 5, :].unsqueeze(2), [P, 8, 64]), ALU.mult, R=["Ysq", "gn"], W="ynb")
    pt, kt = getT()
    for m in range(4):
        ph.tr(pt[:, m, :P], ynb[:P, 2 * m:2 * m + 2, :].rearrange("p a b -> p (a b)"), ib[:P, :P], R="ynb", W=kt)
    B4 = lambda t: bc(t[:, :].unsqueeze(2), [128, 4, P])
    t1 = ex2
    ph.tt(V, t1[:, :, :P], pt[:, 0:4, :P], B4(pc["lnx_g"]), ALU.mult, R=[kt, "c_lnx_g"], W="gns")
    ph.tt(V, t1[:, :, :P], t1[:, :, :P], B4(pc["lnx_b"]), ALU.add, R=["gns", "c_lnx_b"], W="gns")
    ph.tt(V, t1[:, :, :P], t1[:, :, :P], bon[:, :, :P], ALU.add, R=["gns", KX["bon"]], W="gns")
    ph.tt(V, YFb[:, :, c0:c0 + P], t1[:, :, :P], gg[:, :, :P], ALU.mult, R=["gns", KX["gg"]], W="YFb")


def s5_block(ph, I, G0, pc, Xs, ub, ZZb, getF, nchunk, which, ncol, step=CS, npos=CS):
    V = "dve"
    BwT, Kmat, CwT, Ab = G0["BwT"], G0["Kmat"], G0["CwT"], G0["Abar"]
    nm = nchunk
    assert nm * 8 <= 512
    for Pl in range(4):
        pb, pk = getF()
        flat = pb[:].rearrange("p a b -> p (a b)")
        for ri in range(2):
            for k in range(4):
                q = ri * 4 + k
                dst = flat[:, q * nm:(q + 1) * nm]
                for j in range(npos):
                    jj = (CS - npos) + j
                    rhs = ub[32 * Pl:32 * Pl + 32, k, j:j + (nm - 1) * step + 1:step]
                    ph.mm(dst, BwT[32 * Pl:32 * Pl + 32, k, jj, ri, :], rhs, j == 0, j == npos - 1,
                          R=["BwT", "ub"], W=pk, tp=((96, 0) if Pl == 3 else None))
        for ri in range(2):
            ph.cp(V, Xs[:, ri, Pl:16:4, 1:1 + nm],
                  flat[:, ri * 4 * nm:(ri + 1) * 4 * nm].rearrange("p (q m) -> p q m", m=nm), R=[pk], W="Xs")
    A_r = bc(Ab[:, which, 0, :].unsqueeze(1), [128, 2, 16]); A_i = bc(Ab[:, which, 1, :].unsqueeze(1), [128, 2, 16])
    tmpa = ph._s5tmp[0]; tmpb = ph._s5tmp[1]
    for m in range(nm):
        ph.tt(V, tmpa[:], Xs[:, :, :, m], A_r, ALU.mult, R=["Xs", "Abar"], W="s5a")
        ph.tt(V, tmpb[:], Xs[:, :, :, m], A_i, ALU.mult, R=["Xs", "Abar"], W="s5b")
        ph.tt(V, Xs[:, :, :, m + 1], Xs[:, :, :, m + 1], tmpa[:], ALU.add, R=["Xs", "s5a"], W="Xs")
        ph.tt(V, Xs[:, 0, :, m + 1], Xs[:, 0, :, m + 1], tmpb[:, 1, :], ALU.subtract, R=["Xs", "s5b"], W="Xs")
        ph.tt(V, Xs[:, 1, :, m + 1], Xs[:, 1, :, m + 1], tmpb[:, 0, :], ALU.add, R=["Xs", "s5b"], W="Xs")
    Xb = ph._s5xb
    ph.cp("act", Xb[:, :, :, 0:nm], Xs[:, :, :, 0:nm], R="Xs", W="Xb")
    for k in range(4):
        pb, pk = getF()
        flat = pb[:].rearrange("p a b -> p (a b)")
        for i in range(npos):
            dst = flat[:, i * nm:(i + 1) * nm]
            for tau in range(i + 1):
                rhs = ub[:, k, (i - tau):(i - tau) + (nm - 1) * step + 1:step]
                ph.mm(dst, Kmat[:, k, tau, :], rhs, tau == 0, False, R=["Kmat", "ub"], W=pk)
            for Pl in range(4):
                P_ = 4 * k + Pl
                for ri in range(2):
                    ph.mm(flat[32 * Pl:32 * Pl + 32, i * nm:(i + 1) * nm], CwT[:, i, ri, P_, :], Xb[:, ri, P_, 0:nm],
                          False, ri == 1, R=["CwT", "Xb"], W=pk, tp=(0, 32 * Pl))
        du = ph._s5du
        ph.ts(V, du[:, 0:ncol], ub[:, k, 0:ncol], pc["D_skip"][:, k:k + 1], ALU.mult, R=["ub", "c_D_skip", "s5z"], W="s5du")
        if npos == 1:
            ph.tt(V, du[:, 0:ncol], du[:, 0:ncol], flat[:, 0:nm], ALU.add, R=["s5du", pk], W="s5du")
        else:
            ph.tt(V, du[:, 0:ncol].rearrange("p (m i) -> p m i", i=npos), du[:, 0:ncol].rearrange("p (m i) -> p m i", i=npos),
                  flat[:, 0:npos * nm].rearrange("p (i m) -> p m i", m=nm), ALU.add, R=["s5du", pk], W="s5du")
        ph.act(ZZb[:, k, 0:ncol], du[:, 0:ncol], AF.Gelu_apprx_tanh, R="s5du", W=["ZZb", "s5z"])
    ph.cp(V, Xs[:, :, :, 0], Xs[:, :, :, nm], R="Xs", W="Xs")


def sample_mixer(ph, I, G0, L):
    V = "dve"
    sb = ph.sb
    pc = L["pc"]; getF, getT, ib = L["getF"], L["getT"], L["ib"]
    identf = G0["identf"]
    XS = L["XS"]; dd = L["dd"]
    t0 = T
    n = NS
    cur = sb("s_cur", [128, 14, NS], F32); prv = sb("s_prv", [128, 14, NS], F32)
    ph.dma("sp", cur[:], I["PRW"][:, t0:t0 + n].rearrange("(m p) t -> p m t", p=128), W="s_cur")
    sst = sb("s_sst", [NS, 1792], F32)
    ph.dma("sp", sst[:], I["st_shift"], W="s_sst")
    for half in range(4):
        pb, pk = getF()
        flat = pb[:].rearrange("p a b -> p (a b)")
        ms = list(range(half * 4, min(14, half * 4 + 4)))
        for q, m in enumerate(ms):
            ph.tr(flat[:, q * NS:(q + 1) * NS], sst[:, m * 128:(m + 1) * 128], identf[:NS, :NS], R=["s_sst"], W=pk)
        ph.cp(V, prv[:, ms[0]:ms[-1] + 1, :], flat[:, 0:len(ms) * NS].rearrange("p (a b) -> p a b", b=NS), R=pk, W="s_prv")
    ph.dbg("cur", cur[:], [128, 14, NS], "s_cur")
    ph.dbg("prv", prv[:], [128, 14, NS], "s_prv")
    so = sst
    for half in range(4):
        pb, pk = getF()
        flat = pb[:].rearrange("p a b -> p (a b)")
        ms = list(range(half * 4, min(14, half * 4 + 4)))
        for q, m in enumerate(ms):
            ph.tr(flat[:NS, q * 128:(q + 1) * 128], cur[:, m, :], identf[:], R=["s_cur"], W=pk)
        ph.cp(V, so[:, ms[0] * 128:(ms[-1] + 1) * 128], flat[:NS, 0:len(ms) * 128], R=pk, W="s_sst")
    ph.dma("sp", I["s_shift"], so[:], R="s_sst")
    xs = XS[:, :, 0:NS]
    ph.tt(V, dd[:, :, 0:NS], prv[:], cur[:], ALU.subtract, R=["s_prv", "s_cur"], W="dd")
    ph.tt(V, dd[:, :, 0:NS], dd[:, :, 0:NS], bc(pc["mu_shift"][:, :].unsqueeze(2), [128, 14, NS]), ALU.mult,
          R=["dd", "c_mu_shift"], W="dd")
    ph.tt(V, xs, dd[:, :, 0:NS], cur[:], ALU.add, R=["dd", "s_cur"], W="XS")
    uf = L["uf"]; ub = L["ub"]; ZZb = L["ZZb"]
    ph.dma("act", uf[:, :, 0:NS], I["UU"][:, t0:t0 + n].rearrange("(m p) t -> p m t", p=128), W="uf")
    ph.cp("act", ub[:, :, 0:NS], uf[:, :, 0:NS], R="uf", W="ub")
    stx = [sb("s_stre", [NS, 2048], F32), sb("s_stim", [NS, 2048], F32)]
    ph.dma("sp", stx[0][:], I["st_re"], W="s_stx0"); ph.dma("sp", stx[1][:], I["st_im"], W="s_stx1")
    Xsm = sb("s_Xsm", [128, 2, 16, NS], F32)
    for ri in range(2):
        for q4 in range(4):
            pb, pk = getF()
            flat = pb[:].rearrange("p a b -> p (a b)")
            for q in range(4):
                P_ = q4 * 4 + q
                ph.tr(flat[:, q * NS:(q + 1) * NS], stx[ri][:, P_ * 128:(P_ + 1) * 128], identf[:NS, :NS],
                      R="s_stx%d" % ri, W=pk)
            ph.cp(V, Xsm[:, ri, q4 * 4:q4 * 4 + 4, :], flat[:, 0:4 * NS].rearrange("p (a b) -> p a b", b=NS), R=pk, W="s_Xsm")
    s5_sample(ph, I, G0, pc, Xsm, ub, ZZb, getF, stx)
    ph.dma("act", I["ZZ"][:, t0:t0 + n].rearrange("(m p) t -> p m t", p=128), ZZb[:, :, 0:NS], R="ZZb")
    rwkv_sample(ph, I, G0, L)
    ph.dma("sp", I["YF"][:, t0:t0 + n].rearrange("(m p) t -> p m t", p=128), L["YFb"][:, :, 0:NS], R="YFb")


def s5_sample(ph, I, G0, pc, Xsm, ub, ZZb, getF, stx):
    V = "dve"
    BwT, Kmat, CwT, Ab = G0["BwT"], G0["Kmat"], G0["CwT"], G0["Abar"]
    identf = G0["identf"]
    Xb = ph._s5xb
    ph.cp("act", Xb[:, :, :, 0:NS], Xsm[:], R="s_Xsm", W="Xb")
    du = ph._s5du
    for k in range(4):
        pb, pk = getF()
        flat = pb[:].rearrange("p a b -> p (a b)")
        ph.mm(flat[:, 0:NS], Kmat[:, k, 0, :], ub[:, k, 0:NS], True, False, R=["Kmat", "ub"], W=pk)
        for Pl in range(4):
            P_ = 4 * k + Pl
            for ri in range(2):
                ph.mm(flat[32 * Pl:32 * Pl + 32, 0:NS], CwT[:, 0, ri, P_, :], Xb[:, ri, P_, 0:NS], False,
                      ri == 1, R=["CwT", "Xb"], W=pk, tp=(0, 32 * Pl))
        ph.ts(V, du[:, 0:NS], ub[:, k, 0:NS], pc["D_skip"][:, k:k + 1], ALU.mult, R=["ub", "c_D_skip", "s5z"], W="s5du")
        ph.tt(V, du[:, 0:NS], du[:, 0:NS], flat[:, 0:NS], ALU.add, R=["s5du", pk], W="s5du")
        ph.act(ZZb[:, k, 0:NS], du[:, 0:NS], AF.Gelu_apprx_tanh, R="s5du", W=["ZZb", "s5z"])
    Gs = ph.sb("s_Gs", [128, 2, 16, NS], F32)
    for Pl in range(4):
        pb, pk = getF()
        flat = pb[:].rearrange("p a b -> p (a b)")
        for ri in range(2):
            for k in range(4):
                q = ri * 4 + k
                ph.mm(flat[:, q * NS:(q + 1) * NS], BwT[32 * Pl:32 * Pl + 32, k, CS - 1, ri, :],
                      ub[32 * Pl:32 * Pl + 32, k, 0:NS], True, True, R=["BwT", "ub"], W=pk,
                      tp=((96, 0) if Pl == 3 else None))
        for ri in range(2):
            ph.cp(V, Gs[:, ri, Pl:16:4, :], flat[:, ri * 4 * NS:(ri + 1) * 4 * NS].rearrange("p (q m) -> p q m", m=NS),
                  R=pk, W="s_Gs")
    A_r = bc(Ab[:, 1, 0, :].unsqueeze(2), [128, 16, NS]); A_i = bc(Ab[:, 1, 1, :].unsqueeze(2), [128, 16, NS])
    ta = ph.sb("s_ta", [128, 16, NS], F32)
    ph.tt(V, ta[:], Xsm[:, 0], A_r, ALU.mult, R=["s_Xsm", "Abar"], W="s_ta")
    ph.tt(V, Gs[:, 0], Gs[:, 0], ta[:], ALU.add, R=["s_Gs", "s_ta"], W="s_Gs")
    ph.tt(V, ta[:], Xsm[:, 1], A_i, ALU.mult, R=["s_Xsm", "Abar", "s_Gs"], W="s_ta")
    ph.tt(V, Gs[:, 0], Gs[:, 0], ta[:], ALU.subtract, R=["s_Gs", "s_ta"], W="s_Gs")
    ph.tt(V, ta[:], Xsm[:, 1], A_r, ALU.mult, R=["s_Xsm", "Abar", "s_Gs"], W="s_ta")
    ph.tt(V, Gs[:, 1], Gs[:, 1], ta[:], ALU.add, R=["s_Gs", "s_ta"], W="s_Gs")
    ph.tt(V, ta[:], Xsm[:, 0], A_i, ALU.mult, R=["s_Xsm", "Abar", "s_Gs"], W="s_ta")
    ph.tt(V, Gs[:, 1], Gs[:, 1], ta[:], ALU.add, R=["s_Gs", "s_ta"], W="s_Gs")
    for ri, nm in enumerate(("s_re", "s_im")):
        xo = stx[ri]
        for q4 in range(4):
            pb, pk = getF()
            flat = pb[:].rearrange("p a b -> p (a b)")
            for q in range(4):
                P_ = q4 * 4 + q
                ph.tr(flat[:NS, q * 128:(q + 1) * 128], Gs[:, ri, P_, :], identf[:], R="s_Gs", W=pk)
            ph.cp(V, xo[:, q4 * 512:(q4 + 1) * 512], flat[:NS, 0:512], R=pk, W="s_stx%d" % ri)
        ph.dma("sp", I[nm], xo[:], R="s_stx%d" % ri)


def rwkv_sample(ph, I, G0, L):
    V = "dve"
    sb = ph.sb
    pc = L["pc"]; getF, getT, ib = L["getF"], L["getT"], L["ib"]
    identf = G0["identf"]
    XS = L["XS"]
    sig, aa, gg, kk0, tq, rn, kkn = L["sig"], L["aa"], L["gg"], L["kk0"], L["tq"], L["rn"], L["kkn"]
    bb, kmod, bon = L["bb"], L["kmod"], L["bon"]
    lin, sgx, w2a2, g2b, blk64 = L["lin"], L["sgx"], L["w2a2"], L["g2b"], L["blk64"]
    n = NS
    r_ = XS[:, 0:4, 0:n]; k_ = XS[:, 4:8, 0:n]; v_ = XS[:, 8:12, 0:n]
    B4 = lambda t: bc(t[:, :].unsqueeze(2), [128, 4, n])
    S4 = lambda t: t[:, :, 0:n]
    ph.act(lin[0:64, 0:n], XS[0:64, 12, 0:n], AF.Tanh, R="XS", W="lin")
    ph.cp("act", lin[64:128, 0:n], XS[64:128, 12, 0:n], R="XS", W="lin")
    ph.act(sgx[:, 0:n], XS[:, 13, 0:n], AF.Sigmoid, R="XS", W="sgx")
    pw_, kw_ = getF(); pa_, ka_ = getF(); pg_, kg_ = getF()
    for m in range(4):
        ph.mm(pw_[:, m, 0:n], w2a2[0:64, m * 128:(m + 1) * 128], lin[0:64, 0:n], True, True, R=["w2a2", "lin"], W=kw_)
        ph.mm(pa_[:, m, 0:n], w2a2[64:128, m * 128:(m + 1) * 128], lin[64:128, 0:n], True, True, R=["w2a2", "lin"], W=ka_)
        ph.mm(pg_[:, m, 0:n], g2b[:, m * 128:(m + 1) * 128], sgx[:, 0:n], True, True, R=["g2b", "sgx"], W=kg_)
    for m in range(4):
        ph.act(sig[:, m, 0:n], pw_[:, m, 0:n], AF.Sigmoid, R=[kw_, "c_w0"], W="sig", bias=pc["w0"][:, m:m + 1])
        ph.act(aa[:, m, 0:n], pa_[:, m, 0:n], AF.Sigmoid, R=[ka_, "c_a0"], W="aa", bias=pc["a0"][:, m:m + 1])
    ph.cp("act", S4(gg), pg_[:, :, 0:n], R=kg_, W="gg")
    ph.tt(V, S4(kk0), k_, B4(pc["k_k"]), ALU.mult, R=["XS", "c_k_k"], W="kk0")
    ph.tt(V, S4(tq), S4(kk0), S4(kk0), ALU.mult, R="kk0", W="tq")
    pq, kq = getF()
    for m in range(4):
        ph.mm(pq[:, m, 0:n], blk64[:], tq[:, m, 0:n], True, True, R=["blk64", "tq"], W=kq)
    ph.act(S4(rn), pq[:, :, 0:n], AF.Sqrt, R=kq, W="rn")
    ph.ts(V, S4(rn), S4(rn), 1e-12, ALU.max, R="rn", W="rn")
    ph.op(V, lambda e: e.reciprocal(out=S4(rn), in_=S4(rn)), R="rn", W="rn")
    ph.tt(V, S4(kkn), S4(kk0), S4(rn), ALU.mult, R=["kk0", "rn"], W="kkn")
    ph.tt(V, S4(bb), S4(kkn), S4(aa), ALU.mult, R=["kkn", "aa"], W="bb")
    ph.tt(V, S4(tq), S4(aa), B4(pc["k_a"]), ALU.mult, R=["aa", "c_k_a", kq], W="tq")
    ph.tt(V, S4(tq), S4(tq), B4(pc["k_a"]), ALU.subtract, R=["tq", "c_k_a"], W="tq")
    ph.stt(S4(kmod), S4(tq), 1.0, k_, ALU.add, ALU.mult, R=["tq", "XS"], W="kmod")
    ph.tt(V, S4(tq), r_, S4(kmod), ALU.mult, R=["XS", "kmod"], W="tq")
    ph.tt(V, S4(tq), S4(tq), B4(pc["r_k"]), ALU.mult, R=["tq", "c_r_k"], W="tq")
    pq2, kq2 = getF()
    for m in range(4):
        ph.mm(pq2[:, m, 0:n], blk64[:], tq[:, m, 0:n], True, True, R=["blk64", "tq"], W=kq2)
    ph.tt(V, S4(bon), pq2[:, :, 0:n], v_, ALU.mult, R=[kq2, "XS"], W="bon")
    wdec = L["ex1"]
    ph.act(S4(wdec), S4(sig), AF.Exp, R="sig", W="ex1", scale=-C1)
    srcs = [r_, S4(wdec), S4(kmod), v_, S4(kkn), S4(bb)]
    keys = ["XS", "ex1", "kmod", "XS", "kkn", "bb"]
    tok = sb("s_tok", [NS, 6, 512], F32)
    for i, (src, kkey) in enumerate(zip(srcs, keys)):
        pb, pk = getF()
        flat = pb[:].rearrange("p a b -> p (a b)")
        for m in range(4):
            ph.tr(flat[:NS, m * 128:(m + 1) * 128], src[:, m, :], identf[:], R=kkey, W=pk)
        ph.cp(V if i % 2 else "act", tok[:, i, :], flat[:NS, 0:512], R=pk, W="s_tok")
    ph.dma("sp", I["SW"].rearrange("i b f -> b i f"), tok[:], R="s_tok", W="SWd")
    vec = sb("s_vec", [128, 6, 64], F32)
    ph.dma("sp", vec[:], I["SW"].rearrange("i b (h k) -> (b h) i k", h=8), R="SWd", W="s_vec")
    S0 = sb("s_S0", [128, 64, 64], F32)
    ph.dma("act", S0[:].rearrange("p a b -> p (a b)"), I["st_wkv"], W="s_S0")
    tmp = sb("s_tmp", [128, 64, 64], F32)
    sa = sb("s_sa", [128, 64], F32); yv = sb("s_yv", [128, 64], F32); kka = sb("s_kka", [128, 64], F32)
    kB = lambda i: bc(vec[:, i, :].unsqueeze(1), [128, 64, 64])
    ph.tt(V, tmp[:], S0[:], kB(4), ALU.mult, R=["s_S0", "s_vec"], W="s_tmp")
    ph.op(V, lambda e: e.tensor_reduce(out=sa[:], in_=tmp[:], axis=AX.X, op=ALU.add), R="s_tmp", W="s_sa")
    ph.tt(V, S0[:], S0[:], kB(1), ALU.mult, R=["s_S0", "s_vec", "s_tmp"], W="s_S0")
    ph.tt(V, tmp[:], bc(sa[:, :].unsqueeze(2), [128, 64, 64]), kB(5), ALU.mult, R=["s_sa", "s_vec"], W="s_tmp")
    ph.tt(V, S0[:], S0[:], tmp[:], ALU.subtract, R=["s_S0", "s_tmp"], W="s_S0")
    ph.tt(V, tmp[:], bc(vec[:, 3, :].unsqueeze(2), [128, 64, 64]), kB(2), ALU.mult, R=["s_vec", "s_S0"], W="s_tmp")
    ph.tt(V, S0[:], S0[:], tmp[:], ALU.add, R=["s_S0", "s_tmp"], W="s_S0")
    ph.dma("act", I["s_wkv"], S0[:].rearrange("p a b -> p (a b)"), R="s_S0")
    ph.tt(V, tmp[:], S0[:], kB(0), ALU.mult, R=["s_S0", "s_vec"], W="s_tmp")
    ph.op(V, lambda e: e.tensor_reduce(out=yv[:], in_=tmp[:], axis=AX.X, op=ALU.add), R="s_tmp", W="s_yv")
    ph.dma("sp", I["SY"], yv[:], R="s_yv", W="SYd")
    Ysb = L["Ysb"]
    ph.dma("sp", Ysb[:NS].rearrange("p a b -> p (a b)"), I["SY"].rearrange("(b h) v -> b (h v)", h=8), R="SYd", W="Ysb")
    groupnorm_out(ph, L, 0, NS)


def phase3(nc, I, G0, W3, WFI):
    ph = Ph(nc, "p3")
    V = "dve"
    W3 = alloc_w3(nc, ph.st)
    load_w3(ph, I, W3)
    rwo, glu, wo = W3["rwo"], W3["glu"], W3["wo"]
    for k in range(8):
        ph.dma("pool", WFI[:, k, :], I["w_ffn_in"][k * 128:(k + 1) * 128, :], W="wfi_pre")
    yf = ph.sb("yf", [128, 4, 512], BF16); zz = ph.sb("zz", [128, 4, 512], BF16); gt = ph.sb("gt", [128, 16, 512], BF16)
    trw = ph.sb("trw", [128, 8, 512], F32); mg = ph.sb("mg", [128, 8, 512], BF16)
    sgb = [ph.sb("sgb%d" % i, [128, 512], F32) for i in range(2)]
    s5t = [ph.sb("s5t%d" % i, [128, 512], F32) for i in range(2)]
    xts = [ph.sb("xt%d" % i, [128, D], F32) for i in range(2)]
    pm = [ph.ps("pm%d" % i, [128, 512], F32) for i in range(6)]
    npm = nx = ns = 0
    for (t0, nt) in BLOCKS:
        P = min(128, nt)
        r3 = lambda name: I[name][:, t0:t0 + nt].rearrange("(m p) t -> p m t", p=128)
        ph.dma("sp", yf[:, :, :nt], r3("YF"), W="yf"); ph.dma("sp", zz[:, :, :nt], r3("ZZ"), W="zz")
        ph.dma("act", gt[:, :, :nt], r3("GT"), W="gt")
        for m in range(8):
            pb = pm[npm % 6]; pk = "pm%d" % (npm % 6); npm += 1
            for k in range(4):
                ph.mm(pb[:, :nt], rwo[:, k, m * 128:(m + 1) * 128], yf[:, k, :nt], k == 0, k == 3, R=["rwo", "yf"], W=pk)
            ph.tt(V, trw[:, m, :nt], pb[:, :nt], gt[:, m, :nt], ALU.mult, R=[pk, "gt"], W="trw%d" % m)
        for m in range(8):
            pa = pm[npm % 6]; pka = "pm%d" % (npm % 6); npm += 1
            pb = pm[npm % 6]; pkb = "pm%d" % (npm % 6); npm += 1
            for k in range(4):
                ph.mm(pa[:, :nt], glu[:, k, m * 128:(m + 1) * 128], zz[:, k, :nt], k == 0, k == 3, R=["glu", "zz"], W=pka)
            for k in range(4):
                ph.mm(pb[:, :nt], glu[:, k, D + m * 128:D + (m + 1) * 128], zz[:, k, :nt], k == 0, k == 3,
                      R=["glu", "zz"], W=pkb)
            sg = sgb[ns % 2]; sk = "sgb%d" % (ns % 2); s5 = s5t[ns % 2]; s5k = "s5t%d" % (ns % 2); ns += 1
            ph.act(sg[:, :nt], pb[:, :nt], AF.Sigmoid, R=pkb, W=sk)
            ph.tt(V, s5[:, :nt], pa[:, :nt], sg[:, :nt], ALU.mult, R=[pka, sk], W=s5k)
            ph.tt(V, s5[:, :nt], s5[:, :nt], gt[:, 8 + m, :nt], ALU.mult, R=[s5k, "gt"], W=s5k)
            ph.tt(V, mg[:, m, :nt], s5[:, :nt], trw[:, m, :nt], ALU.add, R=[s5k, "trw%d" % m], W="mg")
        for s in range((nt + 127) // 128):
            xt = xts[nx % 2]; xk = "xt%d" % (nx % 2); nx += 1
            rows = slice(t0 + s * 128, t0 + s * 128 + P)
            ph.dma("sp", xt[:P, :], I["xall"][rows, :], W=xk)
            for half in range(2):
                pb = pm[npm % 6]; pk = "pm%d" % (npm % 6); npm += 1
                for k in range(8):
                    ph.mm(pb[:P, :], mg[:, k, s * 128:s * 128 + P], wo[:, k, half * 512:(half + 1) * 512], k == 0, k == 7,
                          R=["mg", "wo"], W=pk)
                ph.tt(V, xt[:P, half * 512:(half + 1) * 512], xt[:P, half * 512:(half + 1) * 512], pb[:P, :], ALU.add,
                      R=[pk, xk], W=xk)
            ph.dma("sp", I["X1"][rows, :], xt[:P, :], R=xk)
    ph.finish()


def phase4(nc, I, G0, WFI):
    ph = Ph(nc, "p4")
    V = "dve"
    G = norm_scratch(ph, G0)
    identf = G0["identf"]
    wfi = WFI; wfo = ph.sb("wfo", [128, 22, D], BF16)
    for k in range(22):
        ph.dma("pool", wfo[:, k, :], I["w_ffn_out"][k * 128:(k + 1) * 128, :], W="wfo")
    g2c = ph.sb("g2c", [128, 8], F32); load_col(ph, g2c[:], I["ln2_g"], 8, "g2c")
    cw = ph.sb("cw", [128, 3, 22], F32); cb = ph.sb("cb", [128, 22], F32)
    ph.dma("sp", cw[:], I["conv_w"].rearrange("t (f p) -> p t f", p=128), W="cw", slow=True)
    load_col(ph, cb[:], I["conv_b"], 22, "cb")
    hT = ph.sb("hT", [128, 8, 512], BF16)
    hid = ph.sb("hid", [128, 22, 512], BF16)
    xts = [ph.sb("xt%d" % i, [128, D], F32) for i in range(2)]
    At = [ph.sb("At%d" % i, [128, 514], F32) for i in range(2)]
    acc = [ph.sb("acc%d" % i, [128, 512], F32) for i in range(2)]
    cc = ph.sb("cc", [128, 22, 2], F32)
    ph.memset(V, cc[:].rearrange("p a b -> p (a b)"), 0.0, W="cc")
    pm = [ph.ps("pm%d" % i, [128, 512], F32) for i in range(6)]
    scs = ph.sb("scs", [NS, 2816], F32)
    scT = ph.sb("scT", [128, 22, 2, NS], F32)
    aout = scs
    npm = na = 0
    for (t0, nt) in BLOCKS:
        P = min(128, nt)
        sample = nt < 128
        nsub = (nt + 127) // 128
        for s in range(nsub):
            rows = slice(t0 + s * 128, t0 + s * 128 + P)
            ph.dma("sp", xts[s % 2][:P, :], I["X1"][rows, :], W="xt%d" % (s % 2))
            rms_to_hT(ph, G, xts[s % 2], P, g2c, hT, s * 128, str(s % 2), "g2c")
        if sample:
            for tt_ in range(2):
                ph.dma("sp", scs[:], I["st_conv"][:, tt_, :], W="scs")
                for q in range(6):
                    pb = pm[npm % 6]; pk = "pm%d" % (npm % 6); npm += 1
                    fs = list(range(q * 4, min(22, q * 4 + 4)))
                    for j, f_ in enumerate(fs):
                        ph.tr(pb[:, j * NS:(j + 1) * NS], scs[:, f_ * 128:(f_ + 1) * 128], identf[:NS, :NS], R="scs", W=pk)
                    ph.cp(V, scT[:, fs[0]:fs[-1] + 1, tt_, :], pb[:, 0:len(fs) * NS].rearrange("p (a b) -> p a b", b=NS),
                          R=pk, W="scT")
        for f in range(22):
            pa = pm[npm % 6]; pka = "pm%d" % (npm % 6); npm += 1
            pb = pm[npm % 6]; pkb = "pm%d" % (npm % 6); npm += 1
            for k in range(8):
                ph.mm(pa[:, :nt], wfi[:, k, f * 128:(f + 1) * 128], hT[:, k, :nt], k == 0, k == 7, R=["wfi", "hT"], W=pka)
            for k in range(8):
                ph.mm(pb[:, :nt], wfi[:, k, 2816 + f * 128:2816 + (f + 1) * 128], hT[:, k, :nt], k == 0, k == 7,
                      R=["wfi", "hT"], W=pkb)
            A = At[na % 2]; ak = "At%d" % (na % 2); ac = acc[na % 2]; ck = "acc%d" % (na % 2); na += 1
            ph.cp("act", A[:, 2:2 + nt], pa[:, :nt], R=pka, W=ak)
            if not sample:
                ph.cp(V, A[:, 0:2], cc[:, f, :], R="cc", W=ak)
                a0, a1, a2 = A[:, 0:nt], A[:, 1:1 + nt], A[:, 2:2 + nt]
            else:
                a0, a1, a2 = scT[:, f, 0, :], scT[:, f, 1, :], A[:, 2:2 + nt]
            ph.ts(V, ac[:, :nt], a0, cw[:, 0, f:f + 1], ALU.mult, cb[:, f:f + 1], ALU.add, R=[ak, "scT", "cw", "cb"], W=ck)
            ph.stt(ac[:, :nt], a1, cw[:, 1, f:f + 1], ac[:, :nt], ALU.mult, ALU.add, R=[ak, "scT", "cw", ck], W=ck)
            ph.stt(ac[:, :nt], a2, cw[:, 2, f:f + 1], ac[:, :nt], ALU.mult, ALU.add, R=[ak, "cw", ck], W=ck)
            ph.act(ac[:, :nt], ac[:, :nt], AF.Gelu_apprx_tanh, R=ck, W=ck)
            ph.tt(V, hid[:, f, :nt], ac[:, :nt], pb[:, :nt], ALU.mult, R=[ck, pkb], W="hid")
            if not sample:
                ph.cp(V, cc[:, f, :], A[:, nt:nt + 2], R=ak, W="cc")
            else:
                po = pm[npm % 6]; pko = "pm%d" % (npm % 6); npm += 1
                ph.tr(po[:NS, 0:128], A[:, 2:2 + NS], identf[:], R=ak, W=pko)
                ph.cp(V, aout[:, f * 128:(f + 1) * 128], po[:NS, 0:128], R=pko, W="scs")
        if t0 + nt == T:
            for tt_ in range(2):
                ph.dma("sp", I["p_conv"][tt_].rearrange("(f p) -> p f", p=128), cc[:, :, tt_], R="cc", slow=True)
        if sample:
            ph.dma("sp", I["s_conv"][:, 1, :], aout[:], R="scs")
            ph.dma("act", I["s_conv"][:, 0, :], I["st_conv"][:, 1, :])
        for s in range(nsub):
            rows = slice(t0 + s * 128, t0 + s * 128 + P)
            xt = xts[s % 2]; xk = "xt%d" % (s % 2)
            ph.dma("sp", xt[:P, :], I["X1"][rows, :], W=xk)
            for half in range(2):
                pb = pm[npm % 6]; pk = "pm%d" % (npm % 6); npm += 1
                for f in range(22):
                    ph.mm(pb[:P, :], hid[:, f, s * 128:s * 128 + P], wfo[:, f, half * 512:(half + 1) * 512], f == 0, f == 21,
                          R=["hid", "wfo"], W=pk)
                ph.tt(V, xt[:P, half * 512:(half + 1) * 512], xt[:P, half * 512:(half + 1) * 512], pb[:P, :],
                      ALU.add, R=[pk, xk], W=xk)
            ph.dma("sp", I["X2"][rows, :], xt[:P, :], R=xk)
    ph.finish()


def phase5(nc, I, G0):
    ph = Ph(nc, "p5")
    V = "dve"
    Ga = norm_scratch(ph, G0, "a")
    Gb = norm_scratch(ph, G0, "b", eps=Ga["eps"])
    wpg = ph.sb("wpg", [128, 8, D], BF16); wpl = ph.sb("wpl", [128, 2, D], BF16)
    for k in range(8):
        ph.dma("pool", wpg[:, k, :], I["w_ple_gate"][k * 128:(k + 1) * 128, :], W="wpg")
    for k in range(2):
        ph.dma("pool", wpl[:, k, :], I["w_ple"][k * 128:(k + 1) * 128, :], W="wpl")
    g3c = ph.sb("g3c", [128, 8], F32); load_col(ph, g3c[:], I["ln3_g"], 8, "g3c")
    fg = ph.sb("fg", [128, D], F32)
    ph.dma("sp", fg[:], I["final_g"].partition_broadcast(128), W="fg")
    hTs = [ph.sb("hT%d" % i, [128, 8, 128], BF16) for i in range(2)]
    xts = [ph.sb("xt%d" % i, [128, D], F32) for i in range(2)]
    pbs = [ph.sb("pb%d" % i, [128, 256], BF16) for i in range(2)]
    pTss = [ph.sb("pTs%d" % i, [128, 2, 128], BF16) for i in range(2)]
    sg = [ph.sb("sg%d" % i, [128, 512], F32) for i in range(2)]
    yo = [ph.sb("yo%d" % i, [128, D], F32) for i in range(2)]
    pm = [ph.ps("pm%d" % i, [128, 512], F32) for i in range(4)]
    pqs = [ph.ps("pq%d" % i, [128, 8, 128], BF16) for i in range(2)]
    npm = nx = nsg = 0
    for (t0, nt) in BLOCKS:
        P = min(128, nt)
        for s in range((nt + 127) // 128):
            rows = slice(t0 + s * 128, t0 + s * 128 + P)
            i2 = nx % 2; nx += 1
            xt = xts[i2]; xk = "xt%d" % i2; pbt = pbs[i2]; pbk = "pb%d" % i2
            G = (Ga, Gb)[i2]; hT = hTs[i2]; hk = "hT%d" % i2; pTs = pTss[i2]; ptk = "pTs%d" % i2
            pq = pqs[i2]; pqk = "pq%d" % i2
            ph.dma("sp", xt[:P, :], I["X2"][rows, :], W=xk)
            ph.dma("pool", pbt[:P, :], I["pall"][rows, :], W=pbk)
            rms_to_hT(ph, G, xt, P, g3c, hT, 0, str(i2), "g3c", hk)
            for k in range(2):
                ph.tr(pq[:, k, :P], pbt[:P, k * 128:(k + 1) * 128], G0["identb"][:P, :P], R=[pbk, "identb"], W=pqk)
            ph.cp("act", pTs[:, :, :P], pq[:, 0:2, :P], R=pqk, W=ptk)
            for half in range(2):
                cs_ = slice(half * 512, (half + 1) * 512)
                pg = pm[npm % 4]; pgk = "pm%d" % (npm % 4); npm += 1
                pe = pm[npm % 4]; pek = "pm%d" % (npm % 4); npm += 1
                for k in range(8):
                    ph.mm(pg[:P, :], hT[:, k, :P], wpg[:, k, cs_], k == 0, k == 7, R=[hk, "wpg"], W=pgk)
                for k in range(2):
                    ph.mm(pe[:P, :], pTs[:, k, :P], wpl[:, k, cs_], k == 0, k == 1, R=[ptk, "wpl"], W=pek)
                sgt = sg[nsg % 2]; sgk = "sg%d" % (nsg % 2); nsg += 1
                ph.act(sgt[:P, :], pg[:P, :], AF.Sigmoid, R=pgk, W=sgk)
                ph.tt(V, sgt[:P, :], sgt[:P, :], pe[:P, :], ALU.mult, R=[sgk, pek], W=sgk)
                ph.tt(V, xt[:P, cs_], xt[:P, cs_], sgt[:P, :], ALU.add, R=[sgk, xk, "xn" + G["sx"]], W=xk)
            ss = G["ss"]; sq = G["sq"]; kss = "ss" + G["sx"]; ksq = "sq" + G["sx"]
            ph.act(sq[:P, :], xt[:P, :], AF.Square, R=xk, W=[ksq, kss], accum=ss[:P, 0:1])
            ph.act(ss[:P, 1:2], ss[:P, 0:1], AF.Sqrt, R=[kss, "eps"], W=kss, bias=G["eps"][:P, 0:1], scale=1.0 / D)
            ph.op(V, lambda e, ss=ss, P=P: e.reciprocal(out=ss[:P, 3:4], in_=ss[:P, 1:2]), R=kss, W=kss + "3")
            y = yo[i2]; yk = "yo%d" % i2
            ph.stt(y[:P, :], xt[:P, :], ss[:P, 3:4], fg[:P, :], ALU.mult, ALU.mult, R=[xk, kss + "3", "fg"], W=yk)
            ph.dma("sp", I["y"][rows, :], y[:P, :], R=yk)
    ph.finish()


_CACHE = {}


def _consts():
    i = np.arange(128)
    c = {}
    c["c_ident"] = np.eye(128, dtype=np.float32)
    c["c_msl"] = (i[None, :] < i[:, None]).astype(np.float32)
    c["c_msu"] = (i[:, None] < i[None, :]).astype(np.float32)
    c["c_mui"] = (i[:, None] <= i[None, :]).astype(np.float32)
    c["c_blk64"] = ((i[:, None] // 64) == (i[None, :] // 64)).astype(np.float32)
    c["c_blk32"] = ((i[:, None] // 32) == (i[None, :] // 32)).astype(np.float32)
    c["c_rowgp"] = (((i[:, None] // 16) % 2) == (i[None, :] // 64)).astype(np.float32)
    return c


def make_in_maps(inp):
    f = lambda a: np.ascontiguousarray(np.asarray(a, dtype=np.float32))
    cst = _consts()
    shared = {}
    for k in ("ln1_g", "w_in", "mu_shift", "w0", "w2", "a0", "a2", "g2", "k_k", "k_a", "lnx_g", "lnx_b", "w_rw_out",
              "A_re", "A_im", "log_dt", "B_re", "B_im", "D_skip", "w_glu", "w_out", "ln2_g", "w_ffn_in", "conv_w",
              "conv_b", "w_ffn_out", "ln3_g", "w_ple_gate", "w_ple"):
        shared[k] = f(inp[k])[0]
    shared["r_k"] = f(inp["r_k"])[0].reshape(512)
    shared["C_re"] = f(inp["C_re"])[0].reshape(512, 64)
    shared["C_im"] = f(inp["C_im"])[0].reshape(512, 64)
    shared["final_g"] = f(inp["final_g"])
    shared.update(cst)
    xp, xs = f(inp["x_prompt"]), f(inp["x_sample"])
    pp, psm = f(inp["p_prompt"])[0], f(inp["p_sample"])[0]
    in_maps = []
    for c in range(8):
        sl = slice(NS * c, NS * c + NS)
        m = dict(shared)
        m["xall"] = np.concatenate([xp[c], xs[sl, 0]], 0)
        m["pall"] = np.concatenate([pp[c], psm[sl, 0]], 0)
        m["st_shift"] = f(inp["state_shift"])[0, sl]
        m["st_wkv"] = f(inp["state_wkv"])[0, sl].reshape(128, 4096)
        m["st_re"] = f(inp["state_ssm_re"])[0, sl].reshape(NS, 2048)
        m["st_im"] = f(inp["state_ssm_im"])[0, sl].reshape(NS, 2048)
        m["st_conv"] = f(inp["state_conv"])[0, sl]
        in_maps.append({k: np.ascontiguousarray(v) for k, v in m.items()})
    return in_maps


def kernel(**inp):
    f = lambda a: np.ascontiguousarray(np.asarray(a, dtype=np.float32))
    if "nc" not in _CACHE:
        _CACHE["nc"] = build_program()
    nc = _CACHE["nc"]
    in_maps = make_in_maps(inp)
    res = run_bass_kernel_spmd(nc, in_maps, core_ids=list(range(8)))
    R = res.results
    cat = lambda fn: np.stack([fn(r) for r in R], 0)
    y_prompt = cat(lambda r: r["y"][:T])
    y_sample = np.concatenate([r["y"][T:] for r in R], 0)[:, None, :]
    p_shift = cat(lambda r: r["p_shift"])[None]
    p_wkv = cat(lambda r: r["p_wkv"].reshape(8, 64, 64).transpose(0, 2, 1))[None]
    p_re = cat(lambda r: r["p_re"].reshape(32, 64))[None]
    p_im = cat(lambda r: r["p_im"].reshape(32, 64))[None]
    p_conv = cat(lambda r: r["p_conv"])[None]
    s_shift = np.concatenate([r["s_shift"] for r in R], 0)[None]
    s_wkv = np.concatenate([r["s_wkv"].reshape(NS, 8, 64, 64) for r in R], 0)[None]
    s_re = np.concatenate([r["s_re"].reshape(NS, 32, 64) for r in R], 0)[None]
    s_im = np.concatenate([r["s_im"].reshape(NS, 32, 64) for r in R], 0)[None]
    s_conv = np.concatenate([r["s_conv"] for r in R], 0)[None]
    outs = (y_prompt, y_sample, p_shift, p_wkv, p_re, p_im, p_conv, s_shift, s_wkv, s_re, s_im, s_conv)
    return tuple(np.ascontiguousarray(o.astype(np.float32)) for o in outs)
```

```python
import contextlib
import math
import numpy as np
import concourse.bass as bass
import concourse.mybir as mybir
from concourse.bass_utils import run_bass_kernel_spmd

F32 = mybir.dt.float32
BF16 = mybir.dt.bfloat16
AF = mybir.ActivationFunctionType
ALU = mybir.AluOpType
AX = mybir.AxisListType

T = 2048
NS = 16
NT = T + NS
D = 1024
CS = 8
SCAN_ENG = "pool"
C1 = math.exp(-0.5)
BLOCKS = [(0, 512), (512, 512), (1024, 512), (1536, 512), (2048, 16)]

ENGS = ("pe", "act", "dve", "pool", "sp")
NDSEM = 12


class _Op:
    __slots__ = ("eng", "fn", "deps", "dma", "observed", "tok", "idx", "dslot")

    def __init__(self, eng, fn, dma):
        self.eng, self.fn, self.dma = eng, fn, dma
        self.deps = set()
        self.observed = False
        self.tok = None
        self.dslot = None


class Sched:
    def __init__(self, nc):
        self.nc = nc
        self.ops = []
        self.last_w = {}
        self.readers = {}
        self.dma_rr = {e: 0 for e in ENGS}
        self.dma_prev = {}
        self.excl = set()

    def _add(self, eng, fn, reads, writes, dma):
        op = _Op(eng, fn, dma)
        op.idx = len(self.ops)
        if self.excl:
            ex = tuple(b for b in reads if b in self.excl)
            if ex:
                writes = tuple(writes) + ex
        for b in reads:
            w = self.last_w.get(b)
            if w is not None:
                op.deps.add(w)
        for b in writes:
            w = self.last_w.get(b)
            if w is not None:
                op.deps.add(w)
            for r in self.readers.get(b, ()):
                op.deps.add(r)
        if dma:
            slot = (eng, self.dma_rr[eng] % NDSEM)
            self.dma_rr[eng] += 1
            op.dslot = slot
            prev = self.dma_prev.get(slot)
            if prev is not None:
                op.deps.add(prev)
            self.dma_prev[slot] = op.idx
        op.deps.discard(op.idx)
        self.ops.append(op)
        for b in writes:
            self.last_w[b] = op.idx
            self.readers[b] = []
        for b in reads:
            if b not in writes:
                self.readers.setdefault(b, []).append(op.idx)
        return op.idx

    def emit(self):
        nc = self.nc
        ops = self.ops
        need = []
        for op in ops:
            nd = []
            for d in op.deps:
                p = ops[d]
                if (not p.dma) and (not op.dma) and p.eng == op.eng == "pe":
                    continue
                nd.append(d)
                p.observed = True
            need.append(nd)
        last = {}
        for op in ops:
            key = op.dslot if op.dma else op.eng
            last[key] = op.idx
        for i in last.values():
            ops[i].observed = True
        g = getattr(nc, "_gsem", None)
        if g is None:
            g = {"sems": {}, "cnt": {e: 0 for e in ENGS}, "dcnt": {}}
            nc._gsem = g
        cnt = g["cnt"]
        dcnt = g["dcnt"]
        for op in ops:
            if op.dma:
                dcnt[op.dslot] = dcnt.get(op.dslot, 0) + 16
                op.tok = (op.dslot, dcnt[op.dslot])
            elif op.observed:
                cnt[op.eng] += 1
                op.tok = (op.eng, cnt[op.eng])
        sems = g["sems"]
        for k in list(ENGS) + sorted(set(o.dslot for o in ops if o.dma)):
            if k not in sems:
                nm = k if isinstance(k, str) else "d_%s_%d" % k
                sems[k] = nc.alloc_semaphore(name="s_" + nm)
        with contextlib.ExitStack() as st:
            block = st.enter_context(nc.Block())
            per = {e: [o for o in ops if o.eng == e] for e in ENGS}
            hw = {"pe": block.tensor, "act": block.scalar, "dve": block.vector,
                  "pool": block.gpsimd, "sp": block.sync}

            def make(e):
                def body(eng):
                    seen = {}
                    for op in per[e]:
                        waits = {}
                        for d in need[op.idx]:
                            k, v = ops[d].tok
                            if v > waits.get(k, 0):
                                waits[k] = v
                        for k, v in waits.items():
                            if seen.get(k, 0) >= v:
                                continue
                            seen[k] = v
                            eng.wait_ge(sems[k], v)
                        ins = op.fn(eng)
                        if op.dma:
                            ins.then_inc(sems[op.tok[0]], 16)
                        elif op.observed:
                            ins.then_inc(sems[e], 1)
                    if e == "sp":
                        for key, i in last.items():
                            k, v = ops[i].tok
                            if seen.get(k, 0) < v:
                                eng.wait_ge(sems[k], v)
                return body

            for e in ENGS:
                hw[e](make(e))


def _L(x):
    if x is None:
        return ()
    if isinstance(x, str):
        return (x,)
    return tuple(x)


class Ph:
    _uid = [0]

    def __init__(self, nc, tag):
        self.nc = nc
        self.tag = tag
        self.st = contextlib.ExitStack()
        self.S = Sched(nc)

    def sb(self, name, shape, dt):
        return self.st.enter_context(self.nc.sbuf_tensor(self.tag + "_" + name, list(shape), dt))

    def ps(self, name, shape, dt):
        self.S.excl.add(name)
        return self.st.enter_context(self.nc.psum_tensor(self.tag + "_" + name, list(shape), dt))

    def finish(self):
        self.S.emit()
        self.st.close()

    def dbg(self, name, ap, shape, key, dt=F32):
        import os
        if os.environ.get("K_DBG_DUMP", "") == "":
            return
        t = self.nc.dram_tensor("dbg_" + name, list(shape), dt, kind="ExternalOutput").ap()
        self.dma("sp", t, ap, R=key)

    _rec = None

    def rec_begin(self):
        self._rec = []

    def rec_end(self):
        r, self._rec = self._rec, None
        return r

    def play(self, *streams, spans=None):
        if spans is None:
            spans = [(0.0, 1.0)] * len(streams)
        keep = [i for i, st_ in enumerate(streams) if st_]
        spans = [spans[i] for i in keep]
        streams = [streams[i] for i in keep]
        pos = [0] * len(streams)
        while True:
            best, bi = None, -1
            for i, st_ in enumerate(streams):
                if pos[i] < len(st_):
                    f = spans[i][0] + spans[i][1] * (pos[i] + 1.0) / len(st_)
                    if best is None or f < best:
                        best, bi = f, i
            if bi < 0:
                break
            eng, fn, R, W, dma = streams[bi][pos[bi]]
            pos[bi] += 1
            self.S._add(eng, fn, R, W, dma)

    def op(self, eng, fn, R=None, W=None):
        if self._rec is not None:
            self._rec.append((eng, fn, _L(R), _L(W), False))
        else:
            self.S._add(eng, fn, _L(R), _L(W), False)

    def dma(self, q, out, in_, R=None, W=None, slow=False):
        if slow:
            fn = lambda e: e.dma_start(out=out, in_=in_, allow_slow_non_contiguous=True)
        else:
            fn = lambda e: e.dma_start(out=out, in_=in_)
        if self._rec is not None:
            self._rec.append((q, fn, _L(R), _L(W), True))
        else:
            self.S._add(q, fn, _L(R), _L(W), True)

    def tt(self, eng, out, in0, in1, op, R=None, W=None):
        self.op(eng, lambda e: e.tensor_tensor(out=out, in0=in0, in1=in1, op=op), R, W)

    def ts(self, eng, out, in0, s1, op0, s2=None, op1=None, R=None, W=None):
        if op1 is None:
            self.op(eng, lambda e: e.tensor_scalar(out=out, in0=in0, scalar1=s1, scalar2=None, op0=op0), R, W)
        else:
            self.op(eng, lambda e: e.tensor_scalar(out=out, in0=in0, scalar1=s1, scalar2=s2, op0=op0, op1=op1), R, W)

    def stt(self, out, in0, scalar, in1, op0, op1, R=None, W=None):
        self.op("dve", lambda e: e.scalar_tensor_tensor(out=out, in0=in0, scalar=scalar, in1=in1, op0=op0, op1=op1), R, W)

    def act(self, out, in_, func, R=None, W=None, bias=None, scale=1.0, accum=None):
        kw = {}
        if bias is not None:
            kw["bias"] = bias
        if accum is not None:
            kw["accum_out"] = accum
        self.op("act", lambda e: e.activation(out=out, in_=in_, func=func, scale=scale, **kw), R, W)

    def cp(self, eng, out, in_, R=None, W=None):
        if eng == "act":
            self.op("act", lambda e: e.activation(out=out, in_=in_, func=AF.Copy), R, W)
        else:
            self.op(eng, lambda e: e.tensor_copy(out=out, in_=in_), R, W)

    def mm(self, out, lhsT, rhs, start, stop, R=None, W=None, tp=None):
        if tp is None:
            self.op("pe", lambda e: e.matmul(out, lhsT=lhsT, rhs=rhs, start=start, stop=stop), R, W)
        else:
            self.op("pe", lambda e: e.matmul(out, lhsT=lhsT, rhs=rhs, start=start, stop=stop, tile_position=tp), R, W)

    def tr(self, out, in_, ident, R=None, W=None):
        self.op("pe", lambda e: e.transpose(out, in_, ident), R, W)

    def memset(self, eng, ap, v, W=None):
        self.op(eng, lambda e: e.memset(ap, v), None, W)


def bc(ap, shape):
    return ap.to_broadcast(list(shape))


def rms_to_hT(ph, G, xt, P, gcol, hT, c0, tag, gkey, hkey="hT"):
    sq, ss, xn, pT = G["sq"], G["ss"], G["xn"], G["pT"]
    x_ = G.get("sx", "")
    ksq, kss, kxn, kpT = "sq" + x_, "ss" + x_, "xn" + x_, "pT" + x_
    ph.act(sq[:P, :], xt[:P, :], AF.Square, R="xt" + tag, W=[ksq, kss], accum=ss[:P, 0:1])
    ph.act(ss[:P, 1:2], ss[:P, 0:1], AF.Sqrt, R=[kss, "eps"], W=kss, bias=G["eps"][:P, 0:1], scale=1.0 / D)
    ph.op("dve", lambda e: e.reciprocal(out=ss[:P, 2:3], in_=ss[:P, 1:2]), R=kss, W=kss)
    ph.ts("dve", xn[:P, :], xt[:P, :], ss[:P, 2:3], ALU.mult, R=["xt" + tag, kss], W=kxn)
    for k in range(8):
        ph.tr(pT[:, k, :P], xn[:P, k * 128:(k + 1) * 128], G["identb"][:P, :P], R=[kxn, "identb"], W=kpT)
    ph.tt("dve", hT[:, :, c0:c0 + P], pT[:, :, :P], bc(gcol[:, :].unsqueeze(2), [128, 8, P]), ALU.mult,
          R=[kpT, gkey], W=hkey)


def load_col(ph, dst, src1d, n, key):
    ph.dma("sp", dst, src1d.rearrange("(k p) -> p k", p=128), W=key, slow=True)


def norm_scratch(ph, G0, sx="", eps=None):
    G = dict(G0)
    G["sx"] = sx
    G["sq"] = ph.sb("sq" + sx, [128, D], F32)
    G["ss"] = ph.sb("ss" + sx, [128, 4], F32)
    G["xn"] = ph.sb("xn" + sx, [128, D], BF16)
    G["pT"] = ph.ps("pT" + sx, [128, 8, 128], BF16)
    if eps is None:
        G["eps"] = ph.sb("eps", [128, 1], F32)
        ph.memset("dve", G["eps"][:], 1e-6, W="eps")
    else:
        G["eps"] = eps
    return G


def build_program(upto=9, debug=False):
    nc = bass.Bass("TRN2", target_bir_lowering=False)
    I = {}

    def inp(name, shape, dt=F32):
        I[name] = nc.dram_tensor(name, list(shape), dt, kind="ExternalInput").ap()

    def outp(name, shape):
        I[name] = nc.dram_tensor(name, list(shape), F32, kind="ExternalOutput").ap()

    def scratch(name, shape, dt):
        if debug:
            I[name] = nc.dram_tensor(name, list(shape), dt, kind="ExternalOutput").ap()
        else:
            I[name] = nc.dram_tensor(name, list(shape), dt).ap()
    if debug:
        scratch("d_BwT", [128, 4 * CS * 2 * 128], BF16); scratch("d_Kmat", [128, 4 * CS * 128], BF16)
        scratch("d_CwT", [128, CS * 2 * 16 * 32], BF16); scratch("d_Abar", [128, 64], F32)

    inp("xall", [NT, D]); inp("pall", [NT, 256])
    inp("st_shift", [NS, 1792]); inp("st_wkv", [128, 4096]); inp("st_re", [NS, 2048]); inp("st_im", [NS, 2048])
    inp("st_conv", [NS, 2, 2816])
    inp("ln1_g", [D]); inp("w_in", [D, 4352]); inp("mu_shift", [1792]); inp("w0", [512]); inp("w2", [64, 512])
    inp("a0", [512]); inp("a2", [64, 512]); inp("g2", [128, 512]); inp("k_k", [512]); inp("k_a", [512])
    inp("r_k", [512]); inp("lnx_g", [512]); inp("lnx_b", [512]); inp("w_rw_out", [512, D])
    inp("A_re", [32, 64]); inp("A_im", [32, 64]); inp("log_dt", [32]); inp("B_re", [32, 64, 16]); inp("B_im", [32, 64, 16])
    inp("C_re", [512, 64]); inp("C_im", [512, 64]); inp("D_skip", [512]); inp("w_glu", [512, 2048]); inp("w_out", [D, D])
    inp("ln2_g", [D]); inp("w_ffn_in", [D, 5632]); inp("conv_w", [3, 2816]); inp("conv_b", [2816]); inp("w_ffn_out", [2816, D])
    inp("ln3_g", [D]); inp("w_ple_gate", [D, D]); inp("w_ple", [256, D]); inp("final_g", [D])
    inp("c_ident", [128, 128]); inp("c_msl", [128, 128]); inp("c_msu", [128, 128]); inp("c_mui", [128, 128])
    inp("c_blk64", [128, 128]); inp("c_blk32", [128, 128]); inp("c_rowgp", [128, 128])
    outp("y", [NT, D]); outp("p_shift", [1792]); outp("p_wkv", [512, 64]); outp("p_re", [2048]); outp("p_im", [2048])
    outp("p_conv", [2, 2816]); outp("s_shift", [NS, 1792]); outp("s_wkv", [128, 4096]); outp("s_re", [NS, 2048])
    outp("s_im", [NS, 2048]); outp("s_conv", [NS, 2, 2816])
    scratch("PRW", [1792, NT], F32); scratch("UU", [512, NT], F32); scratch("GT", [2048, NT], BF16)
    scratch("YF", [512, NT], BF16); scratch("ZZ", [512, NT], BF16); scratch("X1", [NT, D], F32); scratch("X2", [NT, D], F32)
    scratch("SW", [6, NS, 512], F32); scratch("SY", [128, 64], F32)

    with contextlib.ExitStack() as gst:
        def gsb(name, shape, dt):
            return gst.enter_context(nc.sbuf_tensor("g_" + name, list(shape), dt))
        G0 = {}
        G0["identb"] = gsb("identb", [128, 128], BF16)
        G0["identf"] = gsb("identf", [128, 128], F32)
        with contextlib.ExitStack() as g2:
            def g2sb(name, shape, dt):
                return g2.enter_context(nc.sbuf_tensor("g_" + name, list(shape), dt))
            G0["BwT"] = g2sb("BwT", [128, 4, CS, 2, 128], BF16)
            G0["Kmat"] = g2sb("Kmat", [128, 4, CS, 128], BF16)
            G0["CwT"] = g2sb("CwT", [128, CS, 2, 16, 32], BF16)
            G0["Abar"] = g2sb("Abar", [128, 2, 2, 16], F32)
            if upto >= 1:
                phase1(nc, I, G0, debug)
            else:
                phase0(nc, I, G0, debug)
            if upto >= 2:
                phase2(nc, I, G0, True)
            if upto >= 2.5:
                phase2(nc, I, G0, False)
        g4 = contextlib.ExitStack()
        WFI = g4.enter_context(nc.sbuf_tensor("g_wfi", [128, 8, 5632], BF16))
        if upto >= 3:
            phase3(nc, I, G0, None, WFI)
        if upto >= 4:
            phase4(nc, I, G0, WFI)
        g4.close()
        if upto >= 5:
            phase5(nc, I, G0)
    return nc


def phase0(nc, I, G0, debug=False, ph=None):
    own = ph is None
    if own:
        ph = Ph(nc, "p0")
        ph.dma("pool", G0["identb"][:], I["c_ident"], W="identb")
        ph.dma("sp", G0["identf"][:], I["c_ident"], W="identf")
    sb = ph.sb
    lr = sb("lr", [128, 16], F32); li = sb("li", [128, 16], F32); dtl = sb("dtl", [128, 16], F32)
    Bre = sb("Bre", [128, 16, 16], F32); Bim = sb("Bim", [128, 16, 16], F32)
    ph.dma("sp", lr[:], I["A_re"].rearrange("(P gp) n -> (gp n) P", gp=2), W="lr", slow=True)
    ph.dma("sp", li[:], I["A_im"].rearrange("(P gp) n -> (gp n) P", gp=2), W="li", slow=True)
    ldt2 = I["log_dt"].rearrange("(P gp) -> gp P", gp=2)
    for gp in range(2):
        ph.dma("sp", dtl[64 * gp:64 * gp + 64, :], ldt2[gp].partition_broadcast(64), W="dtl", slow=True)
    ph.dma("sp", Bre[:], I["B_re"].rearrange("(P gp) n c -> (gp n) P c", gp=2), W="Bre")
    ph.dma("sp", Bim[:], I["B_im"].rearrange("(P gp) n c -> (gp n) P c", gp=2), W="Bim")
    rowgp = sb("rowgp", [128, 128], F32); blk32 = sb("blk32", [128, 128], F32)
    ph.dma("sp", rowgp[:], I["c_rowgp"], W="rowgp"); ph.dma("sp", blk32[:], I["c_blk32"], W="blk32")
    CT = [sb("CTr", [128, 4, 128], F32), sb("CTi", [128, 4, 128], F32)]
    c2 = sb("c2", [128, 128], F32)
    pA = ph.ps("pA", [128, 4, 128], F32)
    for ri, nm in enumerate(("C_re", "C_im")):
        for k in range(4):
            src = I[nm][k * 128:(k + 1) * 128, :]
            ph.dma("sp", c2[:, 0:64], src, W="c2"); ph.dma("sp", c2[:, 64:128], src, W="c2")
            ph.tt("dve", c2[:], c2[:], rowgp[:], ALU.mult, R=["c2", "rowgp"], W="c2")
            ph.tr(pA[:, k, :], c2[:], G0["identf"][:], R=["c2", "identf"], W="pA")
        ph.cp("dve", CT[ri][:], pA[:], R="pA", W="CT%d" % ri)
    t = {n: sb(n, [128, 16], F32) for n in ("dt", "e1", "mag", "ang", "sa", "ca", "sinv", "cosv", "ar", "ai", "den",
                                             "rden", "am1", "fr", "fi", "t1", "t2")}
    V = "dve"
    K = lambda *n: list(n)
    hpi = sb("hpi", [128, 1], F32)
    ph.memset(V, hpi[:], math.pi / 2, W="hpi")
    ph.act(t["dt"][:], dtl[:], AF.Exp, R="dtl", W="dt")
    ph.tt(V, t["e1"][:], lr[:], t["dt"][:], ALU.mult, R=K("lr", "dt"), W="e1")
    ph.act(t["mag"][:], t["e1"][:], AF.Exp, R="e1", W="mag")
    ph.tt(V, t["ang"][:], li[:], t["dt"][:], ALU.mult, R=K("li", "dt"), W="ang")
    ph.ts(V, t["sa"][:], t["ang"][:], 1.0 / 64, ALU.mult, R="ang", W="sa")
    ph.act(t["sinv"][:], t["sa"][:], AF.Sin, R="sa", W="sinv")
    ph.act(t["cosv"][:], t["sa"][:], AF.Sin, R=["sa", "hpi"], W="cosv", bias=hpi[:, 0:1])
    for _ in range(6):
        ph.tt(V, t["t1"][:], t["cosv"][:], t["cosv"][:], ALU.mult, R="cosv", W="t1")
        ph.tt(V, t["t2"][:], t["sinv"][:], t["sinv"][:], ALU.mult, R="sinv", W="t2")
        ph.stt(t["sinv"][:], t["cosv"][:], 2.0, t["sinv"][:], ALU.mult, ALU.mult, R=["cosv", "sinv", "t2"], W="sinv")
        ph.tt(V, t["cosv"][:], t["t1"][:], t["t2"][:], ALU.subtract, R=["t1", "t2", "sinv"], W="cosv")
    ph.tt(V, t["ar"][:], t["mag"][:], t["cosv"][:], ALU.mult, R=K("mag", "cosv"), W="ar")
    ph.tt(V, t["ai"][:], t["mag"][:], t["sinv"][:], ALU.mult, R=K("mag", "sinv"), W="ai")
    ph.tt(V, t["den"][:], lr[:], lr[:], ALU.mult, R="lr", W="den")
    ph.tt(V, t["t1"][:], li[:], li[:], ALU.mult, R="li", W="t1")
    ph.tt(V, t["den"][:], t["den"][:], t["t1"][:], ALU.add, R=K("den", "t1"), W="den")
    ph.op(V, lambda e: e.reciprocal(out=t["rden"][:], in_=t["den"][:]), R="den", W="rden")
    ph.ts(V, t["am1"][:], t["ar"][:], -1.0, ALU.add, R="ar", W="am1")
    ph.tt(V, t["t1"][:], t["am1"][:], lr[:], ALU.mult, R=K("am1", "lr", "den"), W="t1")
    ph.tt(V, t["t2"][:], t["ai"][:], li[:], ALU.mult, R=K("ai", "li"), W="t2")
    ph.tt(V, t["t1"][:], t["t1"][:], t["t2"][:], ALU.add, R=K("t1", "t2"), W="t1")
    ph.tt(V, t["fr"][:], t["t1"][:], t["rden"][:], ALU.mult, R=K("t1", "rden"), W="fr")
    ph.tt(V, t["t1"][:], t["ai"][:], lr[:], ALU.mult, R=K("ai", "lr", "fr"), W="t1")
    ph.tt(V, t["t2"][:], t["am1"][:], li[:], ALU.mult, R=K("am1", "li"), W="t2")
    ph.tt(V, t["t1"][:], t["t1"][:], t["t2"][:], ALU.subtract, R=K("t1", "t2"), W="t1")
    ph.tt(V, t["fi"][:], t["t1"][:], t["rden"][:], ALU.mult, R=K("t1", "rden"), W="fi")
    pwr = sb("pwr", [128, CS + 1, 16], F32); pwi = sb("pwi", [128, CS + 1, 16], F32)
    ph.memset(V, pwr[:, 0, :], 1.0, W="pw"); ph.memset(V, pwi[:, 0, :], 0.0, W="pw")
    for e in range(CS):
        ph.tt(V, t["t1"][:], pwr[:, e, :], t["ar"][:], ALU.mult, R=K("pw", "ar", "fi"), W="t1")
        ph.tt(V, t["t2"][:], pwi[:, e, :], t["ai"][:], ALU.mult, R=K("pw", "ai"), W="t2")
        ph.tt(V, pwr[:, e + 1, :], t["t1"][:], t["t2"][:], ALU.subtract, R=K("t1", "t2"), W="pw")
        ph.tt(V, t["t1"][:], pwr[:, e, :], t["ai"][:], ALU.mult, R=K("pw", "ai"), W="t1")
        ph.tt(V, t["t2"][:], pwi[:, e, :], t["ar"][:], ALU.mult, R=K("pw", "ar"), W="t2")
        ph.tt(V, pwi[:, e + 1, :], t["t1"][:], t["t2"][:], ALU.add, R=K("t1", "t2"), W="pw")
    Ab = G0["Abar"]
    ph.cp(V, Ab[:, 0, 0, :], pwr[:, CS, :], R="pw", W="Abar"); ph.cp(V, Ab[:, 0, 1, :], pwi[:, CS, :], R="pw", W="Abar")
    ph.cp(V, Ab[:, 1, 0, :], pwr[:, 1, :], R="pw", W="Abar"); ph.cp(V, Ab[:, 1, 1, :], pwi[:, 1, :], R="pw", W="Abar")
    bbr = sb("bbr", [128, 16, 16], F32); bbi = sb("bbi", [128, 16, 16], F32)
    u1 = sb("u1", [128, 16, 16], F32); u2 = sb("u2", [128, 16, 16], F32)
    frb = bc(t["fr"][:, :].unsqueeze(2), [128, 16, 16]); fib = bc(t["fi"][:, :].unsqueeze(2), [128, 16, 16])
    ph.tt(V, u1[:], Bre[:], frb, ALU.mult, R=K("Bre", "fr"), W="u1")
    ph.tt(V, u2[:], Bim[:], fib, ALU.mult, R=K("Bim", "fi"), W="u2")
    ph.tt(V, bbr[:], u1[:], u2[:], ALU.subtract, R=K("u1", "u2"), W="bbr")
    ph.tt(V, u1[:], Bim[:], frb, ALU.mult, R=K("Bim", "fr", "bbr"), W="u1")
    ph.tt(V, u2[:], Bre[:], fib, ALU.mult, R=K("Bre", "fi", "bbr"), W="u2")
    ph.tt(V, bbi[:], u1[:], u2[:], ALU.add, R=K("u1", "u2"), W="bbi")
    Ew = sb("Ew", [128, CS, 2, 16, 2, 16], F32)
    ph.memset(V, Ew[:].rearrange("p a b c d e -> p (a b c d e)"), 0.0, W="Ew")
    for e in range(CS):
        pr = bc(pwr[:, e, :].unsqueeze(2), [128, 16, 16]); pi = bc(pwi[:, e, :].unsqueeze(2), [128, 16, 16])
        ph.tt(V, u1[:], bbr[:], pr, ALU.mult, R=K("bbr", "pw", "Ew"), W="u1")
        ph.tt(V, u2[:], bbi[:], pi, ALU.mult, R=K("bbi", "pw", "Ew"), W="u2")
        ph.tt(V, u1[:], u1[:], u2[:], ALU.subtract, R=K("u1", "u2"), W="u1")
        for gp in range(2):
            ph.cp(V, Ew[64 * gp:64 * gp + 64, e, 0, :, gp, :], u1[64 * gp:64 * gp + 64, :, :], R="u1", W="Ew")
        ph.tt(V, u1[:], bbr[:], pi, ALU.mult, R=K("bbr", "pw", "Ew"), W="u1")
        ph.tt(V, u2[:], bbi[:], pr, ALU.mult, R=K("bbi", "pw", "Ew"), W="u2")
        ph.tt(V, u1[:], u1[:], u2[:], ALU.add, R=K("u1", "u2"), W="u1")
        for gp in range(2):
            ph.cp(V, Ew[64 * gp:64 * gp + 64, e, 1, :, gp, :], u1[64 * gp:64 * gp + 64, :, :], R="u1", W="Ew")
    CTin = sb("CTin", [128, 4, 128], F32)
    ph.ts(V, CTin[:], CT[1][:], -1.0, ALU.mult, R="CT1", W="CTin")
    pB = [ph.ps("pB%d" % i, [128, 4, 128], F32) for i in range(2)]
    n = 0
    for j in range(CS):
        e = CS - 1 - j
        for ri in range(2):
            pb = pB[n % 2]; n += 1
            for k in range(4):
                src = Ew[:, e, ri, 4 * k:4 * k + 4, :, :].rearrange("p a b c -> p (a b c)")
                ph.tr(pb[:, k, :], src, G0["identf"][:], R=["Ew", "identf"], W="pB%d" % ((n - 1) % 2))
            ph.cp("act" if n % 2 else "dve", G0["BwT"][:, :, j, ri, :], pb[:], R="pB%d" % ((n - 1) % 2), W="BwT")
    for tau in range(CS):
        pb = pB[n % 2]; key = "pB%d" % (n % 2); n += 1
        for k in range(4):
            lr_ = Ew[:, tau, 0, 4 * k:4 * k + 4, :, :].rearrange("p a b c -> p (a b c)")
            li_ = Ew[:, tau, 1, 4 * k:4 * k + 4, :, :].rearrange("p a b c -> p (a b c)")
            ph.mm(pb[:, k, :], lr_, CT[0][:, k, :], True, False, R=["Ew", "CT0"], W=key)
            ph.mm(pb[:, k, :], li_, CTin[:, k, :], False, True, R=["Ew", "CTin"], W=key)
        ph.tt(V, G0["Kmat"][:, :, tau, :], pb[:], bc(blk32[:, :].unsqueeze(1), [128, 4, 128]), ALU.mult,
              R=[key, "blk32"], W="Kmat")
    w1 = sb("w1", [128, 16, 32], F32); w2_ = sb("w2", [128, 16, 32], F32)
    CTr3 = CT[0][:].rearrange("p k (a b) -> p (k a) b", a=4); CTi3 = CT[1][:].rearrange("p k (a b) -> p (k a) b", a=4)
    for i in range(CS):
        pr = bc(pwr[:, i + 1, :].unsqueeze(2), [128, 16, 32]); pi = bc(pwi[:, i + 1, :].unsqueeze(2), [128, 16, 32])
        ph.tt(V, w1[:], CTr3, pr, ALU.mult, R=K("CT0", "pw", "CwT"), W="w1")
        ph.tt(V, w2_[:], CTi3, pi, ALU.mult, R=K("CT1", "pw", "CwT"), W="w2")
        ph.tt(V, G0["CwT"][:, i, 0, :, :], w1[:], w2_[:], ALU.subtract, R=K("w1", "w2"), W="CwT")
        ph.tt(V, w1[:], CTr3, pi, ALU.mult, R=K("CT0", "pw", "CwT"), W="w1")
        ph.tt(V, w2_[:], CTi3, pr, ALU.mult, R=K("CT1", "pw", "CwT"), W="w2")
        ph.tt(V, w1[:], w1[:], w2_[:], ALU.add, R=K("w1", "w2"), W="w1")
        ph.ts(V, G0["CwT"][:, i, 1, :, :], w1[:], -1.0, ALU.mult, R="w1", W="CwT")
    if debug:
        ph.dma("sp", I["d_BwT"], G0["BwT"][:].rearrange("p a b c d -> p (a b c d)"), R="BwT")
        ph.dma("sp", I["d_Kmat"], G0["Kmat"][:].rearrange("p a b c -> p (a b c)"), R="Kmat")
        ph.dma("sp", I["d_CwT"], G0["CwT"][:].rearrange("p a b c d -> p (a b c d)"), R="CwT")
        ph.dma("sp", I["d_Abar"], G0["Abar"][:].rearrange("p a b c -> p (a b c)"), R="Abar")
    if own:
        ph.finish()


def phase1(nc, I, G0, debug=False):
    ph = Ph(nc, "p1")
    win = ph.sb("win", [128, 8, 4352], BF16)
    for k in range(8):
        ph.dma("pool", win[:, k, :], I["w_in"][k * 128:(k + 1) * 128, :], W="win%d" % k)
    ph.dma("pool", G0["identb"][:], I["c_ident"], W="identb")
    ph.dma("sp", G0["identf"][:], I["c_ident"], W="identf")
    ph.rec_begin()
    phase0(nc, I, G0, debug, ph=ph)
    s0 = ph.rec_end()
    ph.rec_begin()
    G = norm_scratch(ph, G0)
    g1c = ph.sb("g1c", [128, 8], F32)
    load_col(ph, g1c[:], I["ln1_g"], 8, "g1c")
    hTs = [ph.sb("hT%d" % i, [128, 8, 512], BF16) for i in range(2)]
    xts = [ph.sb("xt%d" % i, [128, D], F32) for i in range(2)]
    pm = [ph.ps("pm%d" % i, [128, 512], F32) for i in range(4)]
    stf = [ph.sb("stf%d" % i, [128, 512], F32) for i in range(4)]
    stb = [ph.sb("stb%d" % i, [128, 512], BF16) for i in range(3)]
    WK = ["win%d" % k for k in range(8)]
    nx = nf = nb = npm = 0
    for bi_, (t0, nt) in enumerate(BLOCKS):
        P = min(128, nt)
        hT = hTs[bi_ % 2]; hk = "hT%d" % (bi_ % 2)
        for s in range((nt + 127) // 128):
            xt = xts[nx % 2]; tg = str(nx % 2); nx += 1
            ph.dma("sp", xt[:P, :], I["xall"][t0 + s * 128:t0 + s * 128 + P, :], W="xt" + tg)
            rms_to_hT(ph, G, xt, P, g1c, hT, s * 128, tg, "g1c", hk)
        for m in range(34):
            pb = pm[npm % 4]; pk = "pm%d" % (npm % 4); npm += 1
            for k in range(8):
                ph.mm(pb[:, :nt], win[:, k, m * 128:(m + 1) * 128], hT[:, k, :nt], k == 0, k == 7,
                      R=["win%d" % k, hk], W=pk)
            if m < 18:
                sf = stf[nf % 4]; sk = "stf%d" % (nf % 4); nf += 1
                ph.cp("dve" if m % 2 else "act", sf[:, :nt], pb[:, :nt], R=pk, W=sk)
                if m < 14:
                    ph.dma("pool", I["PRW"][m * 128:(m + 1) * 128, t0:t0 + nt], sf[:, :nt], R=sk)
                else:
                    ph.dma("pool", I["UU"][(m - 14) * 128:(m - 13) * 128, t0:t0 + nt], sf[:, :nt], R=sk)
            else:
                sbf = stb[nb % 3]; sk = "stb%d" % (nb % 3); nb += 1
                ph.act(sbf[:, :nt], pb[:, :nt], AF.Sigmoid, R=pk, W=sk)
                ph.dma("act", I["GT"][(m - 18) * 128:(m - 17) * 128, t0:t0 + nt], sbf[:, :nt], R=sk)
    s1 = ph.rec_end()
    ph.play(s1, s0)
    ph.finish()


def alloc_w3(nc, st):
    t = lambda n, shp: st.enter_context(nc.sbuf_tensor("w3_" + n, shp, BF16))
    return {"rwo": t("rwo", [128, 4, D]), "glu": t("glu", [128, 4, 2048]), "wo": t("wo", [128, 8, D])}


def load_w3(ph, I, W3):
    for k in range(4):
        ph.dma("pool", W3["rwo"][:, k, :], I["w_rw_out"][k * 128:(k + 1) * 128, :], W="rwo")
        ph.dma("pool", W3["glu"][:, k, :], I["w_glu"][k * 128:(k + 1) * 128, :], W="glu")
    for k in range(8):
        ph.dma("pool", W3["wo"][:, k, :], I["w_out"][k * 128:(k + 1) * 128, :], W="wo")


def phase2(nc, I, G0, prompt, W3=None):
    ph = Ph(nc, "p2a" if prompt else "p2b")
    sb, ps = ph.sb, ph.ps
    V = "dve"
    if W3 is not None:
        load_w3(ph, I, W3)
    ph._s5tmp = [sb("s5a", [128, 2, 16], F32), sb("s5b", [128, 2, 16], F32)]
    ph._s5xb = sb("Xb", [128, 2, 16, 64], BF16)
    ph._s5du = sb("s5du", [128, 512], F32)
    if prompt:
        msl = sb("msl", [128, 128], BF16); msu = sb("msu", [128, 128], BF16); mui = sb("mui", [128, 128], BF16)
        ph.dma("pool", msl[:], I["c_msl"], W="msl"); ph.dma("pool", msu[:], I["c_msu"], W="msu")
        ph.dma("pool", mui[:], I["c_mui"], W="mui")
    blk64 = sb("blk64", [128, 128], F32); ph.dma("sp", blk64[:], I["c_blk64"], W="blk64")
    w2a2 = sb("w2a2", [128, 512], BF16); g2b = sb("g2b", [128, 512], BF16)
    ph.dma("pool", w2a2[0:64, :], I["w2"], W="w2a2"); ph.dma("pool", w2a2[64:128, :], I["a2"], W="w2a2")
    ph.dma("pool", g2b[:], I["g2"], W="g2b")
    pc = {}
    for nm, n in (("mu_shift", 14), ("w0", 4), ("a0", 4), ("k_k", 4), ("k_a", 4), ("r_k", 4), ("lnx_g", 4),
                  ("lnx_b", 4), ("D_skip", 4)):
        pc[nm] = sb("c_" + nm, [128, n], F32)
        load_col(ph, pc[nm][:], I[nm], n, "c_" + nm)
    PK = ["c_" + k for k in pc]
    scm = sb("scm", [128, 4, 128], F32)
    ph.memset(V, scm[:].rearrange("p a b -> p (a b)"), 1.0, W="scm"); ph.memset(V, scm[:, :, 0:1], 0.0, W="scm")
    eps_gn = sb("eps_gn", [128, 1], F32); ph.memset(V, eps_gn[:], 64e-5, W="eps_gn")
    if prompt:
        Sst = sb("Sst", [128, 4, 64], F32); Sbd = sb("Sbd", [128, 4, 128], BF16)
        ph.memset(V, Sst[:].rearrange("p a b -> p (a b)"), 0.0, W="Sst")
        ph.memset(V, Sbd[:].rearrange("p a b -> p (a b)"), 0.0, W="Sbd")
        Xs = sb("Xs", [128, 2, 16, 65], F32)
        ph.memset(V, Xs[:].rearrange("p a b c -> p (a b c)"), 0.0, W="Xs")
        Pf = sb("Pf", [128, 14, 513], F32)
        ph.memset(V, Pf[:, :, 0:1], 0.0, W="Pf")
    WB = 512 if prompt else NS
    WC = 128 if prompt else NS
    uf = sb("uf", [128, 4, WB], F32); ub = sb("ub", [128, 4, WB], BF16)
    YFb = sb("YFb", [128, 4, WB], BF16); ZZb = sb("ZZb", [128, 4, WB], BF16)
    f4 = lambda n: sb(n, [128, 4, WC], F32)
    b4 = lambda n: sb(n, [128, 4, WC], BF16)
    XS = sb("XS", [128, 14, WC], F32); dd = sb("dd", [128, 14, WC], F32)
    lin = sb("lin", [128, WC], BF16); sgx = sb("sgx", [128, WC], BF16)
    sig = f4("sig"); aa = f4("aa"); gg = f4("gg"); kk0 = f4("kk0"); tq = f4("tq"); rn = f4("rn"); kkn = f4("kkn")
    bb = f4("bb"); kmod = f4("kmod"); bon = f4("bon"); cs = f4("cs"); ex1 = f4("ex1"); ex2 = f4("ex2"); ex3 = f4("ex3")
    nbias = sb("nbias", [128, 4], F32); PCt = sb("PCt", [128, 4], F32)
    gns = f4("gns")
    KX = {n_: n_ for n_ in ("rT", "kT", "bT", "aT", "khT", "bhT", "vT", "PCt", "bon", "gg")}
    if prompt:
        rT = b4("rT"); kT = b4("kT"); bT = b4("bT"); aT = b4("aT"); khT = b4("khT"); bhT = b4("bhT"); vT = b4("vT")
        alt = {"rT": b4("rT1"), "kT": b4("kT1"), "bT": b4("bT1"), "aT": b4("aT1"), "khT": b4("khT1"),
               "bhT": b4("bhT1"), "vT": b4("vT1"), "PCt": sb("PCt1", [128, 4], F32), "bon": f4("bon1"), "gg": f4("gg1")}
        Vtok = sb("Vtok", [128, 512], BF16); Khtok = sb("Khtok", [128, 512], BF16); Bhtok = sb("Bhtok", [128, 512], BF16)
        h8 = lambda n: sb(n, [128, 8, 128], BF16)
        Nb = [h8("Nb0"), h8("Nb1")]; Lb = [h8("Lb0"), h8("Lb1")]; Mt = [h8("Mt0"), h8("Mt1")]
        LKb = h8("LKb"); Arb = h8("Arb"); Ark = h8("Ark")
        Wbf = sb("Wbf", [128, 512], BF16); Ubf = sb("Ubf", [128, 512], BF16)
        tS = sb("tS", [128, 4, 64], F32)
    Ysb = sb("Ysb", [128, 8, 64], F32); Ysq = sb("Ysq", [128, 8, 64], F32); ynb = sb("ynb", [128, 8, 64], BF16)
    gn = sb("gn", [128, 6, 8], F32)
    pF = [ps("pF%d" % i, [128, 4, 128], F32) for i in range(6)]
    pT = [ps("pTb%d" % i, [128, 8, 128], BF16) for i in range(2)]
    cnt = {"f": 0, "t": 0}

    def getF():
        i = cnt["f"] % 6; cnt["f"] += 1
        return pF[i], "pF%d" % i

    def mkpool(base):
        st_ = {"n": 0}

        def get():
            i = base + st_["n"] % 2; st_["n"] += 1
            return pF[i], "pF%d" % i
        return get
    getF_prep, getF_core, getFs = mkpool(0), mkpool(2), mkpool(4)

    def getT():
        i = cnt["t"] % 2; cnt["t"] += 1
        return pT[i], "pTb%d" % i

    ib = G0["identb"]

    if not prompt:
        sample_mixer(ph, I, G0, locals())
        ph.finish()
        return
    Lbase = dict(locals())
    Lpar = [dict(Lbase), dict(Lbase)]
    Lpar[1].update(alt)
    Lpar[1]["KX"] = {n_: n_ + "1" for n_ in KX}
    for bi, (t0, nt) in enumerate(BLOCKS[:4]):
        if bi > 0:
            ph.cp(V, Pf[:, :, 0:1], Pf[:, :, 512:513], R="Pf", W="Pf")
        ph.dma("sp", Pf[:, :, 1:513], I["PRW"][:, t0:t0 + nt].rearrange("(m p) t -> p m t", p=128), W="Pf")
        ph.dma("act", uf[:], I["UU"][:, t0:t0 + nt].rearrange("(m p) t -> p m t", p=128), W="uf")
        ph.cp("act", ub[:].rearrange("p a b -> p (a b)"), uf[:].rearrange("p a b -> p (a b)"), R="uf", W="ub")
        if bi == 3:
            ph.dma("sp", I["p_shift"].rearrange("(m p) -> p m", p=128), Pf[:, :, 512], R="Pf", slow=True)
        ph.rec_begin()
        s5_block(ph, I, G0, pc, Xs, ub, ZZb, getFs, nchunk=64, which=0, ncol=512)
        ph.dma("act", I["ZZ"][:, t0:t0 + nt].rearrange("(m p) t -> p m t", p=128), ZZb[:], R="ZZb")
        s5s = ph.rec_end()
        preps, cores = [], []
        for c in range(4):
            c0 = c * 128
            Lc = dict(Lpar[c % 2]); Lc["getF"] = getF_prep
            Lk = dict(Lpar[c % 2]); Lk["getF"] = getF_core
            ph.rec_begin()
            ph.tt(V, dd[:], Pf[:, :, c0:c0 + 128], Pf[:, :, c0 + 1:c0 + 129], ALU.subtract, R="Pf", W="dd")
            ph.tt(V, dd[:], dd[:], bc(pc["mu_shift"][:, :].unsqueeze(2), [128, 14, 128]), ALU.mult,
                  R=["dd", "c_mu_shift"], W="dd")
            ph.tt(V, XS[:], dd[:], Pf[:, :, c0 + 1:c0 + 129], ALU.add, R=["dd", "Pf"], W="XS")
            rwkv_prep_and_core(ph, Lc, c, c0)
            preps.append(ph.rec_end())
            ph.rec_begin()
            wkv_core(ph, Lk, c, c0)
            cores.append(ph.rec_end())
        m0, m1 = ph._s5marks
        SG, SS, SY = s5s[:m0], s5s[m0:m1], s5s[m1:]
        hs = len(SS) // 2
        ph.play(preps[0])
        import os
        po, pw = float(os.environ.get("K_PO", "0")), float(os.environ.get("K_PW", "1"))
        sp3 = [(0.0, 1.0), (po, pw), (0.0, 1.0)]
        ph.play(cores[0], preps[1], SG, spans=sp3)
        ph.play(cores[1], preps[2], SS[:hs], spans=sp3)
        ph.play(cores[2], preps[3], SS[hs:], spans=sp3)
        ph.play(cores[3], SY)
        ph.dma("pool", I["YF"][:, t0:t0 + nt].rearrange("(m p) t -> p m t", p=128), YFb[:], R="YFb")
    ph.dma("sp", I["p_wkv"].rearrange("(m p) v -> p m v", p=128), Sst[:], R="Sst")
    ph.dma("sp", I["p_re"].rearrange("(P p) -> p P", p=128), Xs[:, 0, :, 0], R="Xs", slow=True)
    ph.dma("sp", I["p_im"].rearrange("(P p) -> p P", p=128), Xs[:, 1, :, 0], R="Xs", slow=True)
    ph.finish()


def rwkv_prep_and_core(ph, L, c, c0):
    V = "dve"
    PV = L.get("PV", "dve")
    KX = L["KX"]
    pc = L["pc"]; XS = L["XS"]; getF = L["getF"]; getT = L["getT"]; ib = L["ib"]
    sig, aa, gg, kk0, tq, rn, kkn = L["sig"], L["aa"], L["gg"], L["kk0"], L["tq"], L["rn"], L["kkn"]
    bb, kmod, bon, cs, ex1, ex2, ex3 = L["bb"], L["kmod"], L["bon"], L["cs"], L["ex1"], L["ex2"], L["ex3"]
    rT, kT, bT, aT, khT, bhT, vT = L["rT"], L["kT"], L["bT"], L["aT"], L["khT"], L["bhT"], L["vT"]
    lin, sgx, w2a2, g2b, blk64 = L["lin"], L["sgx"], L["w2a2"], L["g2b"], L["blk64"]
    nbias, PCt, scm = L["nbias"], L["PCt"], L["scm"]
    r_ = XS[:, 0:4, :]; k_ = XS[:, 4:8, :]; v_ = XS[:, 8:12, :]
    B4 = lambda t: bc(t[:, :].unsqueeze(2), [128, 4, 128])
    fl = lambda t: t[:].rearrange("p a b -> p (a b)")
    ph.act(lin[0:64, :], XS[0:64, 12, :], AF.Tanh, R="XS", W="lin")
    ph.cp("act", lin[64:128, :], XS[64:128, 12, :], R="XS", W="lin")
    ph.act(sgx[:], XS[:, 13, :], AF.Sigmoid, R="XS", W="sgx")
    pw_, kw_ = getF()
    for m in range(4):
        ph.mm(pw_[:, m, :], w2a2[0:64, m * 128:(m + 1) * 128], lin[0:64, :], True, True, R=["w2a2", "lin"], W=kw_)
    for m in range(4):
        ph.act(sig[:, m, :], pw_[:, m, :], AF.Sigmoid, R=[kw_, "c_w0"], W="sig", bias=pc["w0"][:, m:m + 1])
    pa_, ka_ = getF()
    for m in range(4):
        ph.mm(pa_[:, m, :], w2a2[64:128, m * 128:(m + 1) * 128], lin[64:128, :], True, True, R=["w2a2", "lin"], W=ka_)
    for m in range(4):
        ph.act(aa[:, m, :], pa_[:, m, :], AF.Sigmoid, R=[ka_, "c_a0"], W="aa", bias=pc["a0"][:, m:m + 1])
    pg_, kg_ = getF()
    for m in range(4):
        ph.mm(pg_[:, m, :], g2b[:, m * 128:(m + 1) * 128], sgx[:], True, True, R=["g2b", "sgx"], W=kg_)
    ph.cp("act", gg[:], pg_[:], R=kg_, W=KX["gg"])
    ph.tt(PV, kk0[:], k_, B4(pc["k_k"]), ALU.mult, R=["XS", "c_k_k"], W="kk0")
    ph.tt(PV, tq[:], kk0[:], kk0[:], ALU.mult, R="kk0", W="tq")
    pq, kq = getF()
    for m in range(4):
        ph.mm(pq[:, m, :], blk64[:], tq[:, m, :], True, True, R=["blk64", "tq"], W=kq)
    ph.act(rn[:], pq[:], AF.Sqrt, R=kq, W="rn")
    ph.ts(V, rn[:], rn[:], 1e-12, ALU.max, R="rn", W="rn")
    ph.op(V, lambda e: e.reciprocal(out=fl(rn), in_=fl(rn)), R="rn", W="rn")
    ph.tt(PV, kkn[:], kk0[:], rn[:], ALU.mult, R=["kk0", "rn"], W="kkn")
    ph.tt(PV, bb[:], kkn[:], aa[:], ALU.mult, R=["kkn", "aa"], W="bb")
    ph.tt(PV, tq[:], aa[:], B4(pc["k_a"]), ALU.mult, R=["aa", "c_k_a", kq], W="tq")
    ph.tt(PV, tq[:], tq[:], B4(pc["k_a"]), ALU.subtract, R=["tq", "c_k_a"], W="tq")
    ph.stt(kmod[:], tq[:], 1.0, k_, ALU.add, ALU.mult, R=["tq", "XS"], W="kmod")
    ph.tt(PV, tq[:], r_, kmod[:], ALU.mult, R=["XS", "kmod"], W="tq")
    ph.tt(PV, tq[:], tq[:], B4(pc["r_k"]), ALU.mult, R=["tq", "c_r_k"], W="tq")
    pq2, kq2 = getF()
    for m in range(4):
        ph.mm(pq2[:, m, :], blk64[:], tq[:, m, :], True, True, R=["blk64", "tq"], W=kq2)
    ph.tt(V, bon[:], pq2[:], v_, ALU.mult, R=[kq2, "XS"], W=KX["bon"])
    ph.op(V, lambda e: e.tensor_tensor_scan(out=fl(cs), data0=fl(scm), data1=fl(sig), initial=0.0, op0=ALU.mult,
                                             op1=ALU.add), R=["scm", "sig"], W="cs")
    ph.ts(V, nbias[:], cs[:, :, 127], -C1, ALU.mult, R="cs", W="nbias")
    ph.act(PCt[:], nbias[:], AF.Exp, R="nbias", W=KX["PCt"])
    ph.act(ex1[:], cs[:], AF.Exp, R="cs", W="ex1", scale=-C1)
    ph.tt(PV, rT[:], r_, ex1[:], ALU.mult, R=["XS", "ex1"], W=KX["rT"])
    ph.act(ex2[:], cs[:], AF.Exp, R="cs", W="ex2", scale=C1)
    ph.tt(PV, kT[:], kmod[:], ex2[:], ALU.mult, R=["kmod", "ex2"], W=KX["kT"])
    ph.tt(PV, bT[:], bb[:], ex2[:], ALU.mult, R=["bb", "ex2"], W=KX["bT"])
    ph.tt(PV, ex3[:], cs[:], sig[:], ALU.subtract, R=["cs", "sig"], W="ex3")
    ph.act(ex3[:], ex3[:], AF.Exp, R="ex3", W="ex3", scale=-C1)
    ph.stt(aT[:], kkn[:], -1.0, ex3[:], ALU.mult, ALU.mult, R=["kkn", "ex3"], W=KX["aT"])
    for m in range(4):
        ph.act(ex1[:, m, :], cs[:, m, :], AF.Exp, R=["cs", "nbias", KX["rT"]], W="ex1", bias=nbias[:, m:m + 1], scale=C1)
    ph.tt(PV, khT[:], kmod[:], ex1[:], ALU.mult, R=["kmod", "ex1"], W=KX["khT"])
    ph.tt(PV, bhT[:], bb[:], ex1[:], ALU.mult, R=["bb", "ex1"], W=KX["bhT"])
    ph.cp("act", vT[:], v_, R="XS", W=KX["vT"])


def wkv_core(ph, L, c, c0):
    V = "dve"
    KX = L["KX"]
    getF = L["getF"]; getT = L["getT"]; ib = L["ib"]
    rT, kT, bT, aT, khT, bhT, vT = L["rT"], L["kT"], L["bT"], L["aT"], L["khT"], L["bhT"], L["vT"]
    Vtok, Khtok, Bhtok = L["Vtok"], L["Khtok"], L["Bhtok"]
    Nb, Lb, Mt, LKb, Arb, Ark = L["Nb"], L["Lb"], L["Mt"], L["LKb"], L["Arb"], L["Ark"]
    msl, msu, mui = L["msl"], L["msu"], L["mui"]
    Wbf, Ubf, Ysb, Ysq, ynb, gn = L["Wbf"], L["Ubf"], L["Ysb"], L["Ysq"], L["ynb"], L["gn"]
    Sst, Sbd, PCt, tS = L["Sst"], L["Sbd"], L["PCt"], L["tS"]
    pc = L["pc"]; bon, gg, YFb = L["bon"], L["gg"], L["YFb"]
    M4 = lambda m_: bc(m_[:, :].unsqueeze(1), [128, 4, 128])
    pt, kt = getT()
    for m in range(4):
        ph.tr(pt[:, m, :], vT[:, m, :], ib[:], R=KX["vT"], W=kt)
    for m in range(4):
        ph.tr(pt[:, 4 + m, :], khT[:, m, :], ib[:], R=KX["khT"], W=kt)
    ph.cp("act", Vtok[:], pt[:, 0:4, :].rearrange("p a b -> p (a b)"), R=kt, W="Vtok")
    ph.cp(V, Khtok[:], pt[:, 4:8, :].rearrange("p a b -> p (a b)"), R=kt, W="Khtok")
    pt2, kt2 = getT()
    for m in range(4):
        ph.tr(pt2[:, m, :], bhT[:, m, :], ib[:], R=KX["bhT"], W=kt2)
    ph.cp("act", Bhtok[:], pt2[:, 0:4, :].rearrange("p a b -> p (a b)"), R=kt2, W="Bhtok")

    def hsl(t, h):
        return t[64 * (h % 2):64 * (h % 2) + 64, h // 2, :]

    def amat(dst, dkey, lhs, lkey, rhs, rkey, mask, mkey):
        for par in range(2):
            pb, pk = getF()
            for q in range(4):
                h = 2 * q + par
                ph.mm(pb[:, q, :], hsl(lhs, h), hsl(rhs, h), True, True, R=[lkey, rkey], W=pk)
            ph.tt(V, dst[:, par:8:2, :], pb[:], M4(mask), ALU.mult, R=[pk, mkey], W=dkey)

    amat(Nb[0], "Nb0", aT, KX["aT"], bT, KX["bT"], msl, "msl")
    amat(Lb[0], "Lb0", bT, KX["bT"], aT, KX["aT"], msu, "msu")
    amat(LKb, "LKb", kT, KX["kT"], aT, KX["aT"], msu, "msu")
    amat(Arb, "Arb", bT, KX["bT"], rT, KX["rT"], mui, "mui")
    amat(Ark, "Ark", kT, KX["kT"], rT, KX["rT"], mui, "mui")
    for half in range(2):
        ph.tt(V, Mt[0][:, half * 4:half * 4 + 4, :], Lb[0][:, half * 4:half * 4 + 4, :], M4(ib), ALU.add,
              R=["Lb0", "identb"], W="Mt0")
    cur = 0
    for lvl in range(6):
        nxt = 1 - cur
        for half in range(2):
            pb, pk = getF()
            for q in range(4):
                h = half * 4 + q
                ph.mm(pb[:, q, :], Lb[cur][:, h, :], Nb[cur][:, h, :], True, True, R=["Lb%d" % cur, "Nb%d" % cur], W=pk)
            ph.cp("act", Nb[nxt][:, half * 4:half * 4 + 4, :], pb[:], R=pk, W="Nb%d" % nxt)
        if lvl < 5:
            for half in range(2):
                pb, pk = getF()
                for q in range(4):
                    h = half * 4 + q
                    ph.mm(pb[:, q, :], Nb[cur][:, h, :], Lb[cur][:, h, :], True, True,
                          R=["Lb%d" % cur, "Nb%d" % cur], W=pk)
                ph.cp("act", Lb[nxt][:, half * 4:half * 4 + 4, :], pb[:], R=pk, W="Lb%d" % nxt)
        for half in range(2):
            pb, pk = getF()
            for q in range(4):
                h = half * 4 + q
                ph.mm(pb[:, q, :], Nb[nxt][:, h, :], Mt[cur][:, h, :], True, True, R=["Nb%d" % nxt, "Mt%d" % cur], W=pk)
            ph.tt(V, Mt[nxt][:, half * 4:half * 4 + 4, :], pb[:], Mt[cur][:, half * 4:half * 4 + 4, :], ALU.add,
                  R=[pk, "Mt%d" % cur], W="Mt%d" % nxt)
        cur = nxt
    MtF = Mt[cur]; mk = "Mt%d" % cur
    def hcols(pb, h):
        return pb[:].rearrange("p a b -> p (a b)")[:, h * 64:h * 64 + 64]

    def pcols(pb, m):
        return pb[:].rearrange("p a b -> p (a b)")[:, m * 128:m * 128 + 128]

    pb, pk = getF()
    for m in range(4):
        ph.mm(pcols(pb, m), aT[:, m, :], Sbd[:, m, :], True, False, R=[KX["aT"], "Sbd"], W=pk)
        for hh in range(2):
            h = 2 * m + hh
            ph.mm(hcols(pb, h), LKb[:, h, :], Vtok[:, h * 64:h * 64 + 64], False, hh == 1, R=["LKb", "Vtok"], W=pk)
    ph.cp("act", Wbf[:], pb[:].rearrange("p a b -> p (a b)"), R=pk, W="Wbf")
    pb, pk = getF()
    for h in range(8):
        ph.mm(hcols(pb, h), MtF[:, h, :], Wbf[:, h * 64:h * 64 + 64], True, True, R=[mk, "Wbf"], W=pk)
    ph.cp("act", Ubf[:], pb[:].rearrange("p a b -> p (a b)"), R=pk, W="Ubf")
    pb, pk = getF()
    for m in range(4):
        ph.mm(pcols(pb, m), rT[:, m, :], Sbd[:, m, :], True, False, R=[KX["rT"], "Sbd"], W=pk)
        for hh in range(2):
            h = 2 * m + hh
            ph.mm(hcols(pb, h), Arb[:, h, :], Ubf[:, h * 64:h * 64 + 64], False, False, R=["Arb", "Ubf"], W=pk)
            ph.mm(hcols(pb, h), Ark[:, h, :], Vtok[:, h * 64:h * 64 + 64], False, hh == 1, R=["Ark", "Vtok"], W=pk)
    ph.cp("act", Ysb[:].rearrange("p a b -> p (a b)"), pb[:].rearrange("p a b -> p (a b)"), R=pk, W="Ysb")
    pS, kS = getF()
    for m in range(4):
        ph.mm(pS[:, m, :], Bhtok[:, m * 128:(m + 1) * 128], Ubf[:, m * 128:(m + 1) * 128], True, False,
              R=["Bhtok", "Ubf"], W=kS)
        ph.mm(pS[:, m, :], Khtok[:, m * 128:(m + 1) * 128], Vtok[:, m * 128:(m + 1) * 128], False, True,
              R=["Khtok", "Vtok"], W=kS)
    ph.tt(V, tS[:], Sst[:], bc(PCt[:, :].unsqueeze(2), [128, 4, 64]), ALU.mult, R=["Sst", KX["PCt"]], W="tS")
    for hh in range(2):
        rs = slice(64 * hh, 64 * hh + 64)
        ph.tt(V, Sst[rs, :, :], tS[rs, :, :], pS[rs, :, 64 * hh:64 * hh + 64], ALU.add, R=["tS", kS], W="Sst")
        ph.cp(V, Sbd[rs, :, 64 * hh:64 * hh + 64], Sst[rs, :, :], R="Sst", W="Sbd")
    groupnorm_out(ph, L, c0, 128)


def groupnorm_out(ph, L, c0, P):
    V = "dve"
    KX = L["KX"]
    Ysb, Ysq, ynb, gn = L["Ysb"], L["Ysq"], L["ynb"], L["gn"]
    pc = L["pc"]; bon, gg, YFb = L["bon"], L["gg"], L["YFb"]; getT = L["getT"]; ib = L["ib"]
    eps_gn = L["eps_gn"]; ex2 = L["gns"]
    ph.op(V, lambda e: e.tensor_reduce(out=gn[:P, 0, :], in_=Ysb[:P], axis=AX.X, op=ALU.add), R="Ysb", W="gn")
    ph.act(Ysq[:P].rearrange("p a b -> p (a b)"), Ysb[:P].rearrange("p a b -> p (a b)"), AF.Square, R="Ysb", W="Ysq")
    ph.op(V, lambda e: e.tensor_reduce(out=gn[:P, 1, :], in_=Ysq[:P], axis=AX.X, op=ALU.add), R="Ysq", W="gn")
    ph.ts(V, gn[:P, 2, :], gn[:P, 0, :], 1.0 / 64, ALU.mult, R="gn", W="gn")
    ph.tt(V, gn[:P, 3, :], gn[:P, 2, :], gn[:P, 2, :], ALU.mult, R="gn", W="gn")
    ph.stt(gn[:P, 4, :], gn[:P, 1, :], 1.0 / 64, gn[:P, 3, :], ALU.mult, ALU.subtract, R="gn", W="gn")
    ph.act(gn[:P, 4, :], gn[:P, 4, :], AF.Sqrt, R=["gn", "eps_gn"], W="gn", bias=eps_gn[:P, 0:1])
    ph.op(V, lambda e: e.reciprocal(out=gn[:P, 5, :], in_=gn[:P, 4, :]), R="gn", W="gn")
    ph.tt(V, Ysq[:P], Ysb[:P], bc(gn[:P, 2, :].unsqueeze(2), [P, 8, 64]), ALU.subtract, R=["Ysb", "gn"], W="Ysq")
    ph.tt(V, ynb[:P], Ysq[:P], bc(gn[:P, 5, :].unsqueeze(2), [P, 8, 64]), ALU.mult, R=["Ysq", "gn"], W="ynb")
    pt, kt = getT()
    for m in range(4):
        ph.tr(pt[:, m, :P], ynb[:P, 2 * m:2 * m + 2, :].rearrange("p a b -> p (a b)"), ib[:P, :P], R="ynb", W=kt)
    B4 = lambda t: bc(t[:, :].unsqueeze(2), [128, 4, P])
    t1 = ex2
    ph.tt(V, t1[:, :, :P], pt[:, 0:4, :P], B4(pc["lnx_g"]), ALU.mult, R=[kt, "c_lnx_g"], W="gns")
    ph.tt(V, t1[:, :, :P], t1[:, :, :P], B4(pc["lnx_b"]), ALU.add, R=["gns", "c_lnx_b"], W="gns")
    ph.tt(V, t1[:, :, :P], t1[:, :, :P], bon[:, :, :P], ALU.add, R=["gns", KX["bon"]], W="gns")
    ph.tt(V, YFb[:, :, c0:c0 + P], t1[:, :, :P], gg[:, :, :P], ALU.mult, R=["gns", KX["gg"]], W="YFb")


def s5_block(ph, I, G0, pc, Xs, ub, ZZb, getF, nchunk, which, ncol, step=CS, npos=CS):
    V = "dve"
    BwT, Kmat, CwT, Ab = G0["BwT"], G0["Kmat"], G0["CwT"], G0["Abar"]
    nm = nchunk
    assert nm * 8 <= 512
    for Pl in range(4):
        pb, pk = getF()
        flat = pb[:].rearrange("p a b -> p (a b)")
        for ri in range(2):
            for k in range(4):
                q = ri * 4 + k
                dst = flat[:, q * nm:(q + 1) * nm]
                for j in range(npos):
                    jj = (CS - npos) + j
                    rhs = ub[32 * Pl:32 * Pl + 32, k, j:j + (nm - 1) * step + 1:step]
                    ph.mm(dst, BwT[32 * Pl:32 * Pl + 32, k, jj, ri, :], rhs, j == 0, j == npos - 1,
                          R=["BwT", "ub"], W=pk, tp=((96, 0) if Pl == 3 else None))
        for ri in range(2):
            ph.cp(V, Xs[:, ri, Pl:16:4, 1:1 + nm],
                  flat[:, ri * 4 * nm:(ri + 1) * 4 * nm].rearrange("p (q m) -> p q m", m=nm), R=[pk], W="Xs")
    A_r = bc(Ab[:, which, 0, :].unsqueeze(1), [128, 2, 16]); A_i = bc(Ab[:, which, 1, :].unsqueeze(1), [128, 2, 16])
    ph._s5marks = [len(ph._rec) if ph._rec is not None else 0]
    tmpa = ph._s5tmp[0]; tmpb = ph._s5tmp[1]
    for m in range(nm):
        ph.tt(SCAN_ENG, tmpa[:], Xs[:, :, :, m], A_r, ALU.mult, R=["Xs", "Abar"], W="s5a")
        ph.tt(SCAN_ENG, tmpb[:], Xs[:, :, :, m], A_i, ALU.mult, R=["Xs", "Abar"], W="s5b")
        ph.tt(SCAN_ENG, Xs[:, :, :, m + 1], Xs[:, :, :, m + 1], tmpa[:], ALU.add, R=["Xs", "s5a"], W="Xs")
        ph.tt(SCAN_ENG, Xs[:, 0, :, m + 1], Xs[:, 0, :, m + 1], tmpb[:, 1, :], ALU.subtract, R=["Xs", "s5b"], W="Xs")
        ph.tt(SCAN_ENG, Xs[:, 1, :, m + 1], Xs[:, 1, :, m + 1], tmpb[:, 0, :], ALU.add, R=["Xs", "s5b"], W="Xs")
    ph._s5marks.append(len(ph._rec) if ph._rec is not None else 0)
    Xb = ph._s5xb
    ph.cp("act", Xb[:, :, :, 0:nm], Xs[:, :, :, 0:nm], R="Xs", W="Xb")
    for k in range(4):
        pb, pk = getF()
        flat = pb[:].rearrange("p a b -> p (a b)")
        for i in range(npos):
            dst = flat[:, i * nm:(i + 1) * nm]
            for tau in range(i + 1):
                rhs = ub[:, k, (i - tau):(i - tau) + (nm - 1) * step + 1:step]
                ph.mm(dst, Kmat[:, k, tau, :], rhs, tau == 0, False, R=["Kmat", "ub"], W=pk)
            for Pl in range(4):
                P_ = 4 * k + Pl
                for ri in range(2):
                    ph.mm(flat[32 * Pl:32 * Pl + 32, i * nm:(i + 1) * nm], CwT[:, i, ri, P_, :], Xb[:, ri, P_, 0:nm],
                          False, ri == 1, R=["CwT", "Xb"], W=pk, tp=(0, 32 * Pl))
        du = ph._s5du
        ph.ts(V, du[:, 0:ncol], ub[:, k, 0:ncol], pc["D_skip"][:, k:k + 1], ALU.mult, R=["ub", "c_D_skip", "s5z"], W="s5du")
        if npos == 1:
            ph.tt(V, du[:, 0:ncol], du[:, 0:ncol], flat[:, 0:nm], ALU.add, R=["s5du", pk], W="s5du")
        else:
            ph.tt(V, du[:, 0:ncol].rearrange("p (m i) -> p m i", i=npos), du[:, 0:ncol].rearrange("p (m i) -> p m i", i=npos),
                  flat[:, 0:npos * nm].rearrange("p (i m) -> p m i", m=nm), ALU.add, R=["s5du", pk], W="s5du")
        ph.act(ZZb[:, k, 0:ncol], du[:, 0:ncol], AF.Gelu_apprx_tanh, R="s5du", W=["ZZb", "s5z"])
    ph.cp(V, Xs[:, :, :, 0], Xs[:, :, :, nm], R="Xs", W="Xs")


def sample_mixer(ph, I, G0, L):
    V = "dve"
    sb = ph.sb
    pc = L["pc"]; getF, getT, ib = L["getF"], L["getT"], L["ib"]
    identf = G0["identf"]
    XS = L["XS"]; dd = L["dd"]
    t0 = T
    n = NS
    cur = sb("s_cur", [128, 14, NS], F32); prv = sb("s_prv", [128, 14, NS], F32)
    ph.dma("sp", cur[:], I["PRW"][:, t0:t0 + n].rearrange("(m p) t -> p m t", p=128), W="s_cur")
    sst = sb("s_sst", [NS, 1792], F32)
    ph.dma("sp", sst[:], I["st_shift"], W="s_sst")
    for half in range(4):
        pb, pk = getF()
        flat = pb[:].rearrange("p a b -> p (a b)")
        ms = list(range(half * 4, min(14, half * 4 + 4)))
        for q, m in enumerate(ms):
            ph.tr(flat[:, q * NS:(q + 1) * NS], sst[:, m * 128:(m + 1) * 128], identf[:NS, :NS], R=["s_sst"], W=pk)
        ph.cp(V, prv[:, ms[0]:ms[-1] + 1, :], flat[:, 0:len(ms) * NS].rearrange("p (a b) -> p a b", b=NS), R=pk, W="s_prv")
    ph.dbg("cur", cur[:], [128, 14, NS], "s_cur")
    ph.dbg("prv", prv[:], [128, 14, NS], "s_prv")
    so = sst
    for half in range(4):
        pb, pk = getF()
        flat = pb[:].rearrange("p a b -> p (a b)")
        ms = list(range(half * 4, min(14, half * 4 + 4)))
        for q, m in enumerate(ms):
            ph.tr(flat[:NS, q * 128:(q + 1) * 128], cur[:, m, :], identf[:], R=["s_cur"], W=pk)
        ph.cp(V, so[:, ms[0] * 128:(ms[-1] + 1) * 128], flat[:NS, 0:len(ms) * 128], R=pk, W="s_sst")
    ph.dma("sp", I["s_shift"], so[:], R="s_sst")
    xs = XS[:, :, 0:NS]
    ph.tt(V, dd[:, :, 0:NS], prv[:], cur[:], ALU.subtract, R=["s_prv", "s_cur"], W="dd")
    ph.tt(V, dd[:, :, 0:NS], dd[:, :, 0:NS], bc(pc["mu_shift"][:, :].unsqueeze(2), [128, 14, NS]), ALU.mult,
          R=["dd", "c_mu_shift"], W="dd")
    ph.tt(V, xs, dd[:, :, 0:NS], cur[:], ALU.add, R=["dd", "s_cur"], W="XS")
    uf = L["uf"]; ub = L["ub"]; ZZb = L["ZZb"]
    ph.dma("act", uf[:, :, 0:NS], I["UU"][:, t0:t0 + n].rearrange("(m p) t -> p m t", p=128), W="uf")
    ph.cp("act", ub[:, :, 0:NS], uf[:, :, 0:NS], R="uf", W="ub")
    stx = [sb("s_stre", [NS, 2048], F32), sb("s_stim", [NS, 2048], F32)]
    ph.dma("sp", stx[0][:], I["st_re"], W="s_stx0"); ph.dma("sp", stx[1][:], I["st_im"], W="s_stx1")
    Xsm = sb("s_Xsm", [128, 2, 16, NS], F32)
    for ri in range(2):
        for q4 in range(4):
            pb, pk = getF()
            flat = pb[:].rearrange("p a b -> p (a b)")
            for q in range(4):
                P_ = q4 * 4 + q
                ph.tr(flat[:, q * NS:(q + 1) * NS], stx[ri][:, P_ * 128:(P_ + 1) * 128], identf[:NS, :NS],
                      R="s_stx%d" % ri, W=pk)
            ph.cp(V, Xsm[:, ri, q4 * 4:q4 * 4 + 4, :], flat[:, 0:4 * NS].rearrange("p (a b) -> p a b", b=NS), R=pk, W="s_Xsm")
    s5_sample(ph, I, G0, pc, Xsm, ub, ZZb, getF, stx)
    ph.dma("act", I["ZZ"][:, t0:t0 + n].rearrange("(m p) t -> p m t", p=128), ZZb[:, :, 0:NS], R="ZZb")
    rwkv_sample(ph, I, G0, L)
    ph.dma("sp", I["YF"][:, t0:t0 + n].rearrange("(m p) t -> p m t", p=128), L["YFb"][:, :, 0:NS], R="YFb")


def s5_sample(ph, I, G0, pc, Xsm, ub, ZZb, getF, stx):
    V = "dve"
    BwT, Kmat, CwT, Ab = G0["BwT"], G0["Kmat"], G0["CwT"], G0["Abar"]
    identf = G0["identf"]
    Xb = ph._s5xb
    ph.cp("act", Xb[:, :, :, 0:NS], Xsm[:], R="s_Xsm", W="Xb")
    du = ph._s5du
    for k in range(4):
        pb, pk = getF()
        flat = pb[:].rearrange("p a b -> p (a b)")
        ph.mm(flat[:, 0:NS], Kmat[:, k, 0, :], ub[:, k, 0:NS], True, False, R=["Kmat", "ub"], W=pk)
        for Pl in range(4):
            P_ = 4 * k + Pl
            for ri in range(2):
                ph.mm(flat[32 * Pl:32 * Pl + 32, 0:NS], CwT[:, 0, ri, P_, :], Xb[:, ri, P_, 0:NS], False,
                      ri == 1, R=["CwT", "Xb"], W=pk, tp=(0, 32 * Pl))
        ph.ts(V, du[:, 0:NS], ub[:, k, 0:NS], pc["D_skip"][:, k:k + 1], ALU.mult, R=["ub", "c_D_skip", "s5z"], W="s5du")
        ph.tt(V, du[:, 0:NS], du[:, 0:NS], flat[:, 0:NS], ALU.add, R=["s5du", pk], W="s5du")
        ph.act(ZZb[:, k, 0:NS], du[:, 0:NS], AF.Gelu_apprx_tanh, R="s5du", W=["ZZb", "s5z"])
    Gs = ph.sb("s_Gs", [128, 2, 16, NS], F32)
    for Pl in range(4):
        pb, pk = getF()
        flat = pb[:].rearrange("p a b -> p (a b)")
        for ri in range(2):
            for k in range(4):
                q = ri * 4 + k
                ph.mm(flat[:, q * NS:(q + 1) * NS], BwT[32 * Pl:32 * Pl + 32, k, CS - 1, ri, :],
                      ub[32 * Pl:32 * Pl + 32, k, 0:NS], True, True, R=["BwT", "ub"], W=pk,
                      tp=((96, 0) if Pl == 3 else None))
        for ri in range(2):
            ph.cp(V, Gs[:, ri, Pl:16:4, :], flat[:, ri * 4 * NS:(ri + 1) * 4 * NS].rearrange("p (q m) -> p q m", m=NS),
                  R=pk, W="s_Gs")
    A_r = bc(Ab[:, 1, 0, :].unsqueeze(2), [128, 16, NS]); A_i = bc(Ab[:, 1, 1, :].unsqueeze(2), [128, 16, NS])
    ta = ph.sb("s_ta", [128, 16, NS], F32)
    ph.tt(V, ta[:], Xsm[:, 0], A_r, ALU.mult, R=["s_Xsm", "Abar"], W="s_ta")
    ph.tt(V, Gs[:, 0], Gs[:, 0], ta[:], ALU.add, R=["s_Gs", "s_ta"], W="s_Gs")
    ph.tt(V, ta[:], Xsm[:, 1], A_i, ALU.mult, R=["s_Xsm", "Abar", "s_Gs"], W="s_ta")
    ph.tt(V, Gs[:, 0], Gs[:, 0], ta[:], ALU.subtract, R=["s_Gs", "s_ta"], W="s_Gs")
    ph.tt(V, ta[:], Xsm[:, 1], A_r, ALU.mult, R=["s_Xsm", "Abar", "s_Gs"], W="s_ta")
    ph.tt(V, Gs[:, 1], Gs[:, 1], ta[:], ALU.add, R=["s_Gs", "s_ta"], W="s_Gs")
    ph.tt(V, ta[:], Xsm[:, 0], A_i, ALU.mult, R=["s_Xsm", "Abar", "s_Gs"], W="s_ta")
    ph.tt(V, Gs[:, 1], Gs[:, 1], ta[:], ALU.add, R=["s_Gs", "s_ta"], W="s_Gs")
    for ri, nm in enumerate(("s_re", "s_im")):
        xo = stx[ri]
        for q4 in range(4):
            pb, pk = getF()
            flat = pb[:].rearrange("p a b -> p (a b)")
            for q in range(4):
                P_ = q4 * 4 + q
                ph.tr(flat[:NS, q * 128:(q + 1) * 128], Gs[:, ri, P_, :], identf[:], R="s_Gs", W=pk)
            ph.cp(V, xo[:, q4 * 512:(q4 + 1) * 512], flat[:NS, 0:512], R=pk, W="s_stx%d" % ri)
        ph.dma("sp", I[nm], xo[:], R="s_stx%d" % ri)


def rwkv_sample(ph, I, G0, L):
    V = "dve"
    sb = ph.sb
    pc = L["pc"]; getF, getT, ib = L["getF"], L["getT"], L["ib"]
    identf = G0["identf"]
    XS = L["XS"]
    sig, aa, gg, kk0, tq, rn, kkn = L["sig"], L["aa"], L["gg"], L["kk0"], L["tq"], L["rn"], L["kkn"]
    bb, kmod, bon = L["bb"], L["kmod"], L["bon"]
    lin, sgx, w2a2, g2b, blk64 = L["lin"], L["sgx"], L["w2a2"], L["g2b"], L["blk64"]
    n = NS
    r_ = XS[:, 0:4, 0:n]; k_ = XS[:, 4:8, 0:n]; v_ = XS[:, 8:12, 0:n]
    B4 = lambda t: bc(t[:, :].unsqueeze(2), [128, 4, n])
    S4 = lambda t: t[:, :, 0:n]
    ph.act(lin[0:64, 0:n], XS[0:64, 12, 0:n], AF.Tanh, R="XS", W="lin")
    ph.cp("act", lin[64:128, 0:n], XS[64:128, 12, 0:n], R="XS", W="lin")
    ph.act(sgx[:, 0:n], XS[:, 13, 0:n], AF.Sigmoid, R="XS", W="sgx")
    pw_, kw_ = getF(); pa_, ka_ = getF(); pg_, kg_ = getF()
    for m in range(4):
        ph.mm(pw_[:, m, 0:n], w2a2[0:64, m * 128:(m + 1) * 128], lin[0:64, 0:n], True, True, R=["w2a2", "lin"], W=kw_)
        ph.mm(pa_[:, m, 0:n], w2a2[64:128, m * 128:(m + 1) * 128], lin[64:128, 0:n], True, True, R=["w2a2", "lin"], W=ka_)
        ph.mm(pg_[:, m, 0:n], g2b[:, m * 128:(m + 1) * 128], sgx[:, 0:n], True, True, R=["g2b", "sgx"], W=kg_)
    for m in range(4):
        ph.act(sig[:, m, 0:n], pw_[:, m, 0:n], AF.Sigmoid, R=[kw_, "c_w0"], W="sig", bias=pc["w0"][:, m:m + 1])
        ph.act(aa[:, m, 0:n], pa_[:, m, 0:n], AF.Sigmoid, R=[ka_, "c_a0"], W="aa", bias=pc["a0"][:, m:m + 1])
    ph.cp("act", S4(gg), pg_[:, :, 0:n], R=kg_, W="gg")
    ph.tt(V, S4(kk0), k_, B4(pc["k_k"]), ALU.mult, R=["XS", "c_k_k"], W="kk0")
    ph.tt(V, S4(tq), S4(kk0), S4(kk0), ALU.mult, R="kk0", W="tq")
    pq, kq = getF()
    for m in range(4):
        ph.mm(pq[:, m, 0:n], blk64[:], tq[:, m, 0:n], True, True, R=["blk64", "tq"], W=kq)
    ph.act(S4(rn), pq[:, :, 0:n], AF.Sqrt, R=kq, W="rn")
    ph.ts(V, S4(rn), S4(rn), 1e-12, ALU.max, R="rn", W="rn")
    ph.op(V, lambda e: e.reciprocal(out=S4(rn), in_=S4(rn)), R="rn", W="rn")
    ph.tt(V, S4(kkn), S4(kk0), S4(rn), ALU.mult, R=["kk0", "rn"], W="kkn")
    ph.tt(V, S4(bb), S4(kkn), S4(aa), ALU.mult, R=["kkn", "aa"], W="bb")
    ph.tt(V, S4(tq), S4(aa), B4(pc["k_a"]), ALU.mult, R=["aa", "c_k_a", kq], W="tq")
    ph.tt(V, S4(tq), S4(tq), B4(pc["k_a"]), ALU.subtract, R=["tq", "c_k_a"], W="tq")
    ph.stt(S4(kmod), S4(tq), 1.0, k_, ALU.add, ALU.mult, R=["tq", "XS"], W="kmod")
    ph.tt(V, S4(tq), r_, S4(kmod), ALU.mult, R=["XS", "kmod"], W="tq")
    ph.tt(V, S4(tq), S4(tq), B4(pc["r_k"]), ALU.mult, R=["tq", "c_r_k"], W="tq")
    pq2, kq2 = getF()
    for m in range(4):
        ph.mm(pq2[:, m, 0:n], blk64[:], tq[:, m, 0:n], True, True, R=["blk64", "tq"], W=kq2)
    ph.tt(V, S4(bon), pq2[:, :, 0:n], v_, ALU.mult, R=[kq2, "XS"], W="bon")
    wdec = L["ex1"]
    ph.act(S4(wdec), S4(sig), AF.Exp, R="sig", W="ex1", scale=-C1)
    srcs = [r_, S4(wdec), S4(kmod), v_, S4(kkn), S4(bb)]
    keys = ["XS", "ex1", "kmod", "XS", "kkn", "bb"]
    tok = sb("s_tok", [NS, 6, 512], F32)
    for i, (src, kkey) in enumerate(zip(srcs, keys)):
        pb, pk = getF()
        flat = pb[:].rearrange("p a b -> p (a b)")
        for m in range(4):
            ph.tr(flat[:NS, m * 128:(m + 1) * 128], src[:, m, :], identf[:], R=kkey, W=pk)
        ph.cp(V if i % 2 else "act", tok[:, i, :], flat[:NS, 0:512], R=pk, W="s_tok")
    ph.dma("sp", I["SW"].rearrange("i b f -> b i f"), tok[:], R="s_tok", W="SWd")
    vec = sb("s_vec", [128, 6, 64], F32)
    ph.dma("sp", vec[:], I["SW"].rearrange("i b (h k) -> (b h) i k", h=8), R="SWd", W="s_vec")
    S0 = sb("s_S0", [128, 64, 64], F32)
    ph.dma("act", S0[:].rearrange("p a b -> p (a b)"), I["st_wkv"], W="s_S0")
    tmp = sb("s_tmp", [128, 64, 64], F32)
    sa = sb("s_sa", [128, 64], F32); yv = sb("s_yv", [128, 64], F32); kka = sb("s_kka", [128, 64], F32)
    kB = lambda i: bc(vec[:, i, :].unsqueeze(1), [128, 64, 64])
    ph.tt(V, tmp[:], S0[:], kB(4), ALU.mult, R=["s_S0", "s_vec"], W="s_tmp")
    ph.op(V, lambda e: e.tensor_reduce(out=sa[:], in_=tmp[:], axis=AX.X, op=ALU.add), R="s_tmp", W="s_sa")
    ph.tt(V, S0[:], S0[:], kB(1), ALU.mult, R=["s_S0", "s_vec", "s_tmp"], W="s_S0")
    ph.tt(V, tmp[:], bc(sa[:, :].unsqueeze(2), [128, 64, 64]), kB(5), ALU.mult, R=["s_sa", "s_vec"], W="s_tmp")
    ph.tt(V, S0[:], S0[:], tmp[:], ALU.subtract, R=["s_S0", "s_tmp"], W="s_S0")
    ph.tt(V, tmp[:], bc(vec[:, 3, :].unsqueeze(2), [128, 64, 64]), kB(2), ALU.mult, R=["s_vec", "s_S0"], W="s_tmp")
    ph.tt(V, S0[:], S0[:], tmp[:], ALU.add, R=["s_S0", "s_tmp"], W="s_S0")
    ph.dma("act", I["s_wkv"], S0[:].rearrange("p a b -> p (a b)"), R="s_S0")
    ph.tt(V, tmp[:], S0[:], kB(0), ALU.mult, R=["s_S0", "s_vec"], W="s_tmp")
    ph.op(V, lambda e: e.tensor_reduce(out=yv[:], in_=tmp[:], axis=AX.X, op=ALU.add), R="s_tmp", W="s_yv")
    ph.dma("sp", I["SY"], yv[:], R="s_yv", W="SYd")
    Ysb = L["Ysb"]
    ph.dma("sp", Ysb[:NS].rearrange("p a b -> p (a b)"), I["SY"].rearrange("(b h) v -> b (h v)", h=8), R="SYd", W="Ysb")
    groupnorm_out(ph, L, 0, NS)


def phase3(nc, I, G0, W3, WFI):
    ph = Ph(nc, "p3")
    V = "dve"
    W3 = alloc_w3(nc, ph.st)
    load_w3(ph, I, W3)
    rwo, glu, wo = W3["rwo"], W3["glu"], W3["wo"]
    for k in range(8):
        ph.dma("pool", WFI[:, k, :], I["w_ffn_in"][k * 128:(k + 1) * 128, :], W="wfi_pre")
    yf = ph.sb("yf", [128, 4, 512], BF16); zz = ph.sb("zz", [128, 4, 512], BF16); gt = ph.sb("gt", [128, 16, 512], BF16)
    trw = ph.sb("trw", [128, 8, 512], F32); mg = ph.sb("mg", [128, 8, 512], BF16)
    sgb = [ph.sb("sgb%d" % i, [128, 512], F32) for i in range(2)]
    s5t = [ph.sb("s5t%d" % i, [128, 512], F32) for i in range(2)]
    xts = [ph.sb("xt%d" % i, [128, D], F32) for i in range(2)]
    pm = [ph.ps("pm%d" % i, [128, 512], F32) for i in range(6)]
    npm = nx = ns = 0
    for (t0, nt) in BLOCKS:
        P = min(128, nt)
        r3 = lambda name: I[name][:, t0:t0 + nt].rearrange("(m p) t -> p m t", p=128)
        ph.dma("sp", yf[:, :, :nt], r3("YF"), W="yf"); ph.dma("sp", zz[:, :, :nt], r3("ZZ"), W="zz")
        ph.dma("act", gt[:, :, :nt], r3("GT"), W="gt")
        for m in range(8):
            pb = pm[npm % 6]; pk = "pm%d" % (npm % 6); npm += 1
            for k in range(4):
                ph.mm(pb[:, :nt], rwo[:, k, m * 128:(m + 1) * 128], yf[:, k, :nt], k == 0, k == 3, R=["rwo", "yf"], W=pk)
            ph.tt(V, trw[:, m, :nt], pb[:, :nt], gt[:, m, :nt], ALU.mult, R=[pk, "gt"], W="trw%d" % m)
        for m in range(8):
            pa = pm[npm % 6]; pka = "pm%d" % (npm % 6); npm += 1
            pb = pm[npm % 6]; pkb = "pm%d" % (npm % 6); npm += 1
            for k in range(4):
                ph.mm(pa[:, :nt], glu[:, k, m * 128:(m + 1) * 128], zz[:, k, :nt], k == 0, k == 3, R=["glu", "zz"], W=pka)
            for k in range(4):
                ph.mm(pb[:, :nt], glu[:, k, D + m * 128:D + (m + 1) * 128], zz[:, k, :nt], k == 0, k == 3,
                      R=["glu", "zz"], W=pkb)
            sg = sgb[ns % 2]; sk = "sgb%d" % (ns % 2); s5 = s5t[ns % 2]; s5k = "s5t%d" % (ns % 2); ns += 1
            ph.act(sg[:, :nt], pb[:, :nt], AF.Sigmoid, R=pkb, W=sk)
            ph.tt(V, s5[:, :nt], pa[:, :nt], sg[:, :nt], ALU.mult, R=[pka, sk], W=s5k)
            ph.tt(V, s5[:, :nt], s5[:, :nt], gt[:, 8 + m, :nt], ALU.mult, R=[s5k, "gt"], W=s5k)
            ph.tt(V, mg[:, m, :nt], s5[:, :nt], trw[:, m, :nt], ALU.add, R=[s5k, "trw%d" % m], W="mg")
        for s in range((nt + 127) // 128):
            xt = xts[nx % 2]; xk = "xt%d" % (nx % 2); nx += 1
            rows = slice(t0 + s * 128, t0 + s * 128 + P)
            ph.dma("sp", xt[:P, :], I["xall"][rows, :], W=xk)
            for half in range(2):
                pb = pm[npm % 6]; pk = "pm%d" % (npm % 6); npm += 1
                for k in range(8):
                    ph.mm(pb[:P, :], mg[:, k, s * 128:s * 128 + P], wo[:, k, half * 512:(half + 1) * 512], k == 0, k == 7,
                          R=["mg", "wo"], W=pk)
                ph.tt(V, xt[:P, half * 512:(half + 1) * 512], xt[:P, half * 512:(half + 1) * 512], pb[:P, :], ALU.add,
                      R=[pk, xk], W=xk)
            ph.dma("pool", I["X1"][rows, :], xt[:P, :], R=xk)
    ph.finish()


def phase4(nc, I, G0, WFI):
    ph = Ph(nc, "p4")
    V = "dve"
    G = norm_scratch(ph, G0)
    identf = G0["identf"]
    wfi = WFI; wfo = ph.sb("wfo", [128, 22, D], BF16)
    for k in range(22):
        ph.dma("pool", wfo[:, k, :], I["w_ffn_out"][k * 128:(k + 1) * 128, :], W="wfo")
    g2c = ph.sb("g2c", [128, 8], F32); load_col(ph, g2c[:], I["ln2_g"], 8, "g2c")
    cw = ph.sb("cw", [128, 3, 22], F32); cb = ph.sb("cb", [128, 22], F32)
    ph.dma("sp", cw[:], I["conv_w"].rearrange("t (f p) -> p t f", p=128), W="cw", slow=True)
    load_col(ph, cb[:], I["conv_b"], 22, "cb")
    hT = ph.sb("hT", [128, 8, 512], BF16)
    hid = ph.sb("hid", [128, 22, 512], BF16)
    xts = [ph.sb("xt%d" % i, [128, D], F32) for i in range(2)]
    At = [ph.sb("At%d" % i, [128, 514], F32) for i in range(2)]
    acc = [ph.sb("acc%d" % i, [128, 512], F32) for i in range(2)]
    cc = ph.sb("cc", [128, 22, 2], F32)
    ph.memset(V, cc[:].rearrange("p a b -> p (a b)"), 0.0, W="cc")
    pm = [ph.ps("pm%d" % i, [128, 512], F32) for i in range(6)]
    scs = ph.sb("scs", [NS, 2816], F32)
    scT = ph.sb("scT", [128, 22, 2, NS], F32)
    aout = scs
    npm = na = 0
    for (t0, nt) in BLOCKS:
        P = min(128, nt)
        sample = nt < 128
        nsub = (nt + 127) // 128
        for s in range(nsub):
            rows = slice(t0 + s * 128, t0 + s * 128 + P)
            ph.dma("sp", xts[s % 2][:P, :], I["X1"][rows, :], W="xt%d" % (s % 2))
            rms_to_hT(ph, G, xts[s % 2], P, g2c, hT, s * 128, str(s % 2), "g2c")
        if sample:
            for tt_ in range(2):
                ph.dma("sp", scs[:], I["st_conv"][:, tt_, :], W="scs")
                for q in range(6):
                    pb = pm[npm % 6]; pk = "pm%d" % (npm % 6); npm += 1
                    fs = list(range(q * 4, min(22, q * 4 + 4)))
                    for j, f_ in enumerate(fs):
                        ph.tr(pb[:, j * NS:(j + 1) * NS], scs[:, f_ * 128:(f_ + 1) * 128], identf[:NS, :NS], R="scs", W=pk)
                    ph.cp(V, scT[:, fs[0]:fs[-1] + 1, tt_, :], pb[:, 0:len(fs) * NS].rearrange("p (a b) -> p a b", b=NS),
                          R=pk, W="scT")
        for f in range(22):
            pa = pm[npm % 6]; pka = "pm%d" % (npm % 6); npm += 1
            pb = pm[npm % 6]; pkb = "pm%d" % (npm % 6); npm += 1
            for k in range(8):
                ph.mm(pa[:, :nt], wfi[:, k, f * 128:(f + 1) * 128], hT[:, k, :nt], k == 0, k == 7, R=["wfi", "hT"], W=pka)
            for k in range(8):
                ph.mm(pb[:, :nt], wfi[:, k, 2816 + f * 128:2816 + (f + 1) * 128], hT[:, k, :nt], k == 0, k == 7,
                      R=["wfi", "hT"], W=pkb)
            A = At[na % 2]; ak = "At%d" % (na % 2); ac = acc[na % 2]; ck = "acc%d" % (na % 2); na += 1
            ph.cp("act", A[:, 2:2 + nt], pa[:, :nt], R=pka, W=ak)
            if not sample:
                ph.cp(V, A[:, 0:2], cc[:, f, :], R="cc", W=ak)
                a0, a1, a2 = A[:, 0:nt], A[:, 1:1 + nt], A[:, 2:2 + nt]
            else:
                a0, a1, a2 = scT[:, f, 0, :], scT[:, f, 1, :], A[:, 2:2 + nt]
            ph.ts(V, ac[:, :nt], a0, cw[:, 0, f:f + 1], ALU.mult, cb[:, f:f + 1], ALU.add, R=[ak, "scT", "cw", "cb"], W=ck)
            ph.stt(ac[:, :nt], a1, cw[:, 1, f:f + 1], ac[:, :nt], ALU.mult, ALU.add, R=[ak, "scT", "cw", ck], W=ck)
            ph.stt(ac[:, :nt], a2, cw[:, 2, f:f + 1], ac[:, :nt], ALU.mult, ALU.add, R=[ak, "cw", ck], W=ck)
            ph.act(ac[:, :nt], ac[:, :nt], AF.Gelu_apprx_tanh, R=ck, W=ck)
            ph.tt(V, hid[:, f, :nt], ac[:, :nt], pb[:, :nt], ALU.mult, R=[ck, pkb], W="hid")
            if not sample:
                ph.cp(V, cc[:, f, :], A[:, nt:nt + 2], R=ak, W="cc")
            else:
                po = pm[npm % 6]; pko = "pm%d" % (npm % 6); npm += 1
                ph.tr(po[:NS, 0:128], A[:, 2:2 + NS], identf[:], R=ak, W=pko)
                ph.cp(V, aout[:, f * 128:(f + 1) * 128], po[:NS, 0:128], R=pko, W="scs")
        if t0 + nt == T:
            for tt_ in range(2):
                ph.dma("sp", I["p_conv"][tt_].rearrange("(f p) -> p f", p=128), cc[:, :, tt_], R="cc", slow=True)
        if sample:
            ph.dma("sp", I["s_conv"][:, 1, :], aout[:], R="scs")
            ph.dma("act", I["s_conv"][:, 0, :], I["st_conv"][:, 1, :])
        for s in range(nsub):
            rows = slice(t0 + s * 128, t0 + s * 128 + P)
            xt = xts[s % 2]; xk = "xt%d" % (s % 2)
            ph.dma("sp", xt[:P, :], I["X1"][rows, :], W=xk)
            for half in range(2):
                pb = pm[npm % 6]; pk = "pm%d" % (npm % 6); npm += 1
                for f in range(22):
                    ph.mm(pb[:P, :], hid[:, f, s * 128:s * 128 + P], wfo[:, f, half * 512:(half + 1) * 512], f == 0, f == 21,
                          R=["hid", "wfo"], W=pk)
                ph.tt(V, xt[:P, half * 512:(half + 1) * 512], xt[:P, half * 512:(half + 1) * 512], pb[:P, :],
                      ALU.add, R=[pk, xk], W=xk)
            ph.dma("pool", I["X2"][rows, :], xt[:P, :], R=xk)
    ph.finish()


def phase5(nc, I, G0):
    ph = Ph(nc, "p5")
    V = "dve"
    Ga = norm_scratch(ph, G0, "a")
    Gb = norm_scratch(ph, G0, "b", eps=Ga["eps"])
    wpg = ph.sb("wpg", [128, 8, D], BF16); wpl = ph.sb("wpl", [128, 2, D], BF16)
    for k in range(8):
        ph.dma("pool", wpg[:, k, :], I["w_ple_gate"][k * 128:(k + 1) * 128, :], W="wpg")
    for k in range(2):
        ph.dma("pool", wpl[:, k, :], I["w_ple"][k * 128:(k + 1) * 128, :], W="wpl")
    g3c = ph.sb("g3c", [128, 8], F32); load_col(ph, g3c[:], I["ln3_g"], 8, "g3c")
    fg = ph.sb("fg", [128, D], F32)
    ph.dma("sp", fg[:], I["final_g"].partition_broadcast(128), W="fg")
    hTs = [ph.sb("hT%d" % i, [128, 8, 128], BF16) for i in range(2)]
    xts = [ph.sb("xt%d" % i, [128, D], F32) for i in range(2)]
    pball = ph.sb("pball", [128, 17, 256], BF16)
    _i = 0
    for (t0_, nt_) in BLOCKS:
        P_ = min(128, nt_)
        for s_ in range((nt_ + 127) // 128):
            ph.dma("pool", pball[:P_, _i, :], I["pall"][t0_ + s_ * 128:t0_ + s_ * 128 + P_, :], W="pb%d" % _i)
            _i += 1
    pTss = [ph.sb("pTs%d" % i, [128, 2, 128], BF16) for i in range(2)]
    sg = [ph.sb("sg%d" % i, [128, 512], F32) for i in range(2)]
    yo = [ph.sb("yo%d" % i, [128, D], F32) for i in range(2)]
    pm = [ph.ps("pm%d" % i, [128, 512], F32) for i in range(4)]
    pqs = [ph.ps("pq%d" % i, [128, 8, 128], BF16) for i in range(2)]
    npm = nx = nsg = 0
    for (t0, nt) in BLOCKS:
        P = min(128, nt)
        for s in range((nt + 127) // 128):
            rows = slice(t0 + s * 128, t0 + s * 128 + P)
            i2 = nx % 2; nx += 1
            xt = xts[i2]; xk = "xt%d" % i2; pbt = pball[:, nx - 1, :]; pbk = "pb%d" % (nx - 1)
            G = (Ga, Gb)[i2]; hT = hTs[i2]; hk = "hT%d" % i2; pTs = pTss[i2]; ptk = "pTs%d" % i2
            pq = pqs[i2]; pqk = "pq%d" % i2
            ph.dma("sp", xt[:P, :], I["X2"][rows, :], W=xk)
            rms_to_hT(ph, G, xt, P, g3c, hT, 0, str(i2), "g3c", hk)
            for k in range(2):
                ph.tr(pq[:, k, :P], pbt[:P, k * 128:(k + 1) * 128], G0["identb"][:P, :P], R=[pbk, "identb"], W=pqk)
            ph.cp("act", pTs[:, :, :P], pq[:, 0:2, :P], R=pqk, W=ptk)
            for half in range(2):
                cs_ = slice(half * 512, (half + 1) * 512)
                pg = pm[npm % 4]; pgk = "pm%d" % (npm % 4); npm += 1
                pe = pm[npm % 4]; pek = "pm%d" % (npm % 4); npm += 1
                for k in range(8):
                    ph.mm(pg[:P, :], hT[:, k, :P], wpg[:, k, cs_], k == 0, k == 7, R=[hk, "wpg"], W=pgk)
                for k in range(2):
                    ph.mm(pe[:P, :], pTs[:, k, :P], wpl[:, k, cs_], k == 0, k == 1, R=[ptk, "wpl"], W=pek)
                sgt = sg[nsg % 2]; sgk = "sg%d" % (nsg % 2); nsg += 1
                ph.act(sgt[:P, :], pg[:P, :], AF.Sigmoid, R=pgk, W=sgk)
                ph.tt(V, sgt[:P, :], sgt[:P, :], pe[:P, :], ALU.mult, R=[sgk, pek], W=sgk)
                ph.tt(V, xt[:P, cs_], xt[:P, cs_], sgt[:P, :], ALU.add, R=[sgk, xk, "xn" + G["sx"]], W=xk)
            ss = G["ss"]; sq = G["sq"]; kss = "ss" + G["sx"]; ksq = "sq" + G["sx"]
            ph.act(sq[:P, :], xt[:P, :], AF.Square, R=xk, W=[ksq, kss], accum=ss[:P, 0:1])
            ph.act(ss[:P, 1:2], ss[:P, 0:1], AF.Sqrt, R=[kss, "eps"], W=kss, bias=G["eps"][:P, 0:1], scale=1.0 / D)
            ph.op(V, lambda e, ss=ss, P=P: e.reciprocal(out=ss[:P, 3:4], in_=ss[:P, 1:2]), R=kss, W=kss + "3")
            y = yo[i2]; yk = "yo%d" % i2
            ph.stt(y[:P, :], xt[:P, :], ss[:P, 3:4], fg[:P, :], ALU.mult, ALU.mult, R=[xk, kss + "3", "fg"], W=yk)
            ph.dma("pool", I["y"][rows, :], y[:P, :], R=yk)
    ph.finish()


_CACHE = {}


def _consts():
    i = np.arange(128)
    c = {}
    c["c_ident"] = np.eye(128, dtype=np.float32)
    c["c_msl"] = (i[None, :] < i[:, None]).astype(np.float32)
    c["c_msu"] = (i[:, None] < i[None, :]).astype(np.float32)
    c["c_mui"] = (i[:, None] <= i[None, :]).astype(np.float32)
    c["c_blk64"] = ((i[:, None] // 64) == (i[None, :] // 64)).astype(np.float32)
    c["c_blk32"] = ((i[:, None] // 32) == (i[None, :] // 32)).astype(np.float32)
    c["c_rowgp"] = (((i[:, None] // 16) % 2) == (i[None, :] // 64)).astype(np.float32)
    return c


def make_in_maps(inp):
    f = lambda a: np.ascontiguousarray(np.asarray(a, dtype=np.float32))
    cst = _consts()
    shared = {}
    for k in ("ln1_g", "w_in", "mu_shift", "w0", "w2", "a0", "a2", "g2", "k_k", "k_a", "lnx_g", "lnx_b", "w_rw_out",
              "A_re", "A_im", "log_dt", "B_re", "B_im", "D_skip", "w_glu", "w_out", "ln2_g", "w_ffn_in", "conv_w",
              "conv_b", "w_ffn_out", "ln3_g", "w_ple_gate", "w_ple"):
        shared[k] = f(inp[k])[0]
    shared["r_k"] = f(inp["r_k"])[0].reshape(512)
    shared["C_re"] = f(inp["C_re"])[0].reshape(512, 64)
    shared["C_im"] = f(inp["C_im"])[0].reshape(512, 64)
    shared["final_g"] = f(inp["final_g"])
    shared.update(cst)
    xp, xs = f(inp["x_prompt"]), f(inp["x_sample"])
    pp, psm = f(inp["p_prompt"])[0], f(inp["p_sample"])[0]
    in_maps = []
    for c in range(8):
        sl = slice(NS * c, NS * c + NS)
        m = dict(shared)
        m["xall"] = np.concatenate([xp[c], xs[sl, 0]], 0)
        m["pall"] = np.concatenate([pp[c], psm[sl, 0]], 0)
        m["st_shift"] = f(inp["state_shift"])[0, sl]
        m["st_wkv"] = f(inp["state_wkv"])[0, sl].reshape(128, 4096)
        m["st_re"] = f(inp["state_ssm_re"])[0, sl].reshape(NS, 2048)
        m["st_im"] = f(inp["state_ssm_im"])[0, sl].reshape(NS, 2048)
        m["st_conv"] = f(inp["state_conv"])[0, sl]
        in_maps.append({k: np.ascontiguousarray(v) for k, v in m.items()})
    return in_maps


def kernel(**inp):
    f = lambda a: np.ascontiguousarray(np.asarray(a, dtype=np.float32))
    if "nc" not in _CACHE:
        _CACHE["nc"] = build_program()
    nc = _CACHE["nc"]
    in_maps = make_in_maps(inp)
    res = run_bass_kernel_spmd(nc, in_maps, core_ids=list(range(8)))
    R = res.results
    cat = lambda fn: np.stack([fn(r) for r in R], 0)
    y_prompt = cat(lambda r: r["y"][:T])
    y_sample = np.concatenate([r["y"][T:] for r in R], 0)[:, None, :]
    p_shift = cat(lambda r: r["p_shift"])[None]
    p_wkv = cat(lambda r: r["p_wkv"].reshape(8, 64, 64).transpose(0, 2, 1))[None]
    p_re = cat(lambda r: r["p_re"].reshape(32, 64))[None]
    p_im = cat(lambda r: r["p_im"].reshape(32, 64))[None]
    p_conv = cat(lambda r: r["p_conv"])[None]
    s_shift = np.concatenate([r["s_shift"] for r in R], 0)[None]
    s_wkv = np.concatenate([r["s_wkv"].reshape(NS, 8, 64, 64) for r in R], 0)[None]
    s_re = np.concatenate([r["s_re"].reshape(NS, 32, 64) for r in R], 0)[None]
    s_im = np.concatenate([r["s_im"].reshape(NS, 32, 64) for r in R], 0)[None]
    s_conv = np.concatenate([r["s_conv"] for r in R], 0)[None]
    outs = (y_prompt, y_sample, p_shift, p_wkv, p_re, p_im, p_conv, s_shift, s_wkv, s_re, s_im, s_conv)
    return tuple(np.ascontiguousarray(o.astype(np.float32)) for o in outs)
```

```python
import contextlib
import math
import numpy as np
import concourse.bass as bass
import concourse.mybir as mybir
from concourse.bass_utils import run_bass_kernel_spmd

F32 = mybir.dt.float32
BF16 = mybir.dt.bfloat16
AF = mybir.ActivationFunctionType
ALU = mybir.AluOpType
AX = mybir.AxisListType

T = 2048
NS = 16
NT = T + NS
D = 1024
CS = 8
SCAN_ENG = "pool"
C1 = math.exp(-0.5)
BLOCKS = [(0, 512), (512, 512), (1024, 512), (1536, 512), (2048, 16)]

ENGS = ("pe", "act", "dve", "pool", "sp")
NDSEM = 12


class _Op:
    __slots__ = ("eng", "fn", "deps", "dma", "observed", "tok", "idx", "dslot")

    def __init__(self, eng, fn, dma):
        self.eng, self.fn, self.dma = eng, fn, dma
        self.deps = set()
        self.observed = False
        self.tok = None
        self.dslot = None


class Sched:
    def __init__(self, nc):
        self.nc = nc
        self.ops = []
        self.last_w = {}
        self.readers = {}
        self.dma_rr = {e: 0 for e in ENGS}
        self.dma_prev = {}
        self.excl = set()

    def _add(self, eng, fn, reads, writes, dma):
        op = _Op(eng, fn, dma)
        op.idx = len(self.ops)
        if self.excl:
            ex = tuple(b for b in reads if b in self.excl)
            if ex:
                writes = tuple(writes) + ex
        for b in reads:
            w = self.last_w.get(b)
            if w is not None:
                op.deps.add(w)
        for b in writes:
            w = self.last_w.get(b)
            if w is not None:
                op.deps.add(w)
            for r in self.readers.get(b, ()):
                op.deps.add(r)
        if dma:
            slot = (eng, self.dma_rr[eng] % NDSEM)
            self.dma_rr[eng] += 1
            op.dslot = slot
            prev = self.dma_prev.get(slot)
            if prev is not None:
                op.deps.add(prev)
            self.dma_prev[slot] = op.idx
        op.deps.discard(op.idx)
        self.ops.append(op)
        for b in writes:
            self.last_w[b] = op.idx
            self.readers[b] = []
        for b in reads:
            if b not in writes:
                self.readers.setdefault(b, []).append(op.idx)
        return op.idx

    def emit(self):
        nc = self.nc
        ops = self.ops
        need = []
        for op in ops:
            nd = []
            for d in op.deps:
                p = ops[d]
                if (not p.dma) and (not op.dma) and p.eng == op.eng == "pe":
                    continue
                nd.append(d)
                p.observed = True
            need.append(nd)
        last = {}
        for op in ops:
            key = op.dslot if op.dma else op.eng
            last[key] = op.idx
        for i in last.values():
            ops[i].observed = True
        g = getattr(nc, "_gsem", None)
        if g is None:
            g = {"sems": {}, "cnt": {e: 0 for e in ENGS}, "dcnt": {}}
            nc._gsem = g
        cnt = g["cnt"]
        dcnt = g["dcnt"]
        for op in ops:
            if op.dma:
                dcnt[op.dslot] = dcnt.get(op.dslot, 0) + 16
                op.tok = (op.dslot, dcnt[op.dslot])
            elif op.observed:
                cnt[op.eng] += 1
                op.tok = (op.eng, cnt[op.eng])
        sems = g["sems"]
        for k in list(ENGS) + sorted(set(o.dslot for o in ops if o.dma)):
            if k not in sems:
                nm = k if isinstance(k, str) else "d_%s_%d" % k
                sems[k] = nc.alloc_semaphore(name="s_" + nm)
        with contextlib.ExitStack() as st:
            block = st.enter_context(nc.Block())
            per = {e: [o for o in ops if o.eng == e] for e in ENGS}
            hw = {"pe": block.tensor, "act": block.scalar, "dve": block.vector,
                  "pool": block.gpsimd, "sp": block.sync}

            def make(e):
                def body(eng):
                    seen = {}
                    for op in per[e]:
                        waits = {}
                        for d in need[op.idx]:
                            k, v = ops[d].tok
                            if v > waits.get(k, 0):
                                waits[k] = v
                        for k, v in waits.items():
                            if seen.get(k, 0) >= v:
                                continue
                            seen[k] = v
                            eng.wait_ge(sems[k], v)
                        ins = op.fn(eng)
                        if op.dma:
                            ins.then_inc(sems[op.tok[0]], 16)
                        elif op.observed:
                            ins.then_inc(sems[e], 1)
                    if e == "sp":
                        for key, i in last.items():
                            k, v = ops[i].tok
                            if seen.get(k, 0) < v:
                                eng.wait_ge(sems[k], v)
                return body

            for e in ENGS:
                hw[e](make(e))


def _L(x):
    if x is None:
        return ()
    if isinstance(x, str):
        return (x,)
    return tuple(x)


class Ph:
    _uid = [0]

    def __init__(self, nc, tag):
        self.nc = nc
        self.tag = tag
        self.st = contextlib.ExitStack()
        self.S = Sched(nc)

    def sb(self, name, shape, dt):
        return self.st.enter_context(self.nc.sbuf_tensor(self.tag + "_" + name, list(shape), dt))

    def ps(self, name, shape, dt):
        self.S.excl.add(name)
        return self.st.enter_context(self.nc.psum_tensor(self.tag + "_" + name, list(shape), dt))

    def finish(self):
        self.S.emit()
        self.st.close()

    def dbg(self, name, ap, shape, key, dt=F32):
        import os
        if os.environ.get("K_DBG_DUMP", "") == "":
            return
        t = self.nc.dram_tensor("dbg_" + name, list(shape), dt, kind="ExternalOutput").ap()
        self.dma("sp", t, ap, R=key)

    _rec = None

    def rec_begin(self):
        self._rec = []

    def rec_end(self):
        r, self._rec = self._rec, None
        return r

    def play(self, *streams, spans=None):
        if spans is None:
            spans = [(0.0, 1.0)] * len(streams)
        keep = [i for i, st_ in enumerate(streams) if st_]
        spans = [spans[i] for i in keep]
        streams = [streams[i] for i in keep]
        pos = [0] * len(streams)
        while True:
            best, bi = None, -1
            for i, st_ in enumerate(streams):
                if pos[i] < len(st_):
                    f = spans[i][0] + spans[i][1] * (pos[i] + 1.0) / len(st_)
                    if best is None or f < best:
                        best, bi = f, i
            if bi < 0:
                break
            eng, fn, R, W, dma = streams[bi][pos[bi]]
            pos[bi] += 1
            self.S._add(eng, fn, R, W, dma)

    def op(self, eng, fn, R=None, W=None):
        if self._rec is not None:
            self._rec.append((eng, fn, _L(R), _L(W), False))
        else:
            self.S._add(eng, fn, _L(R), _L(W), False)

    def dma(self, q, out, in_, R=None, W=None, slow=False):
        if slow:
            fn = lambda e: e.dma_start(out=out, in_=in_, allow_slow_non_contiguous=True)
        else:
            fn = lambda e: e.dma_start(out=out, in_=in_)
        if self._rec is not None:
            self._rec.append((q, fn, _L(R), _L(W), True))
        else:
            self.S._add(q, fn, _L(R), _L(W), True)

    def tt(self, eng, out, in0, in1, op, R=None, W=None):
        self.op(eng, lambda e: e.tensor_tensor(out=out, in0=in0, in1=in1, op=op), R, W)

    def ts(self, eng, out, in0, s1, op0, s2=None, op1=None, R=None, W=None):
        if op1 is None:
            self.op(eng, lambda e: e.tensor_scalar(out=out, in0=in0, scalar1=s1, scalar2=None, op0=op0), R, W)
        else:
            self.op(eng, lambda e: e.tensor_scalar(out=out, in0=in0, scalar1=s1, scalar2=s2, op0=op0, op1=op1), R, W)

    def stt(self, out, in0, scalar, in1, op0, op1, R=None, W=None):
        self.op("dve", lambda e: e.scalar_tensor_tensor(out=out, in0=in0, scalar=scalar, in1=in1, op0=op0, op1=op1), R, W)

    def act(self, out, in_, func, R=None, W=None, bias=None, scale=1.0, accum=None):
        kw = {}
        if bias is not None:
            kw["bias"] = bias
        if accum is not None:
            kw["accum_out"] = accum
        self.op("act", lambda e: e.activation(out=out, in_=in_, func=func, scale=scale, **kw), R, W)

    def cp(self, eng, out, in_, R=None, W=None):
        if eng == "act":
            self.op("act", lambda e: e.activation(out=out, in_=in_, func=AF.Copy), R, W)
        else:
            self.op(eng, lambda e: e.tensor_copy(out=out, in_=in_), R, W)

    def mm(self, out, lhsT, rhs, start, stop, R=None, W=None, tp=None):
        if tp is None:
            self.op("pe", lambda e: e.matmul(out, lhsT=lhsT, rhs=rhs, start=start, stop=stop), R, W)
        else:
            self.op("pe", lambda e: e.matmul(out, lhsT=lhsT, rhs=rhs, start=start, stop=stop, tile_position=tp), R, W)

    def tr(self, out, in_, ident, R=None, W=None):
        self.op("pe", lambda e: e.transpose(out, in_, ident), R, W)

    def memset(self, eng, ap, v, W=None):
        self.op(eng, lambda e: e.memset(ap, v), None, W)


def bc(ap, shape):
    return ap.to_broadcast(list(shape))


def rms_to_hT(ph, G, xt, P, gcol, hT, c0, tag, gkey, hkey="hT"):
    sq, ss, xn, pT = G["sq"], G["ss"], G["xn"], G["pT"]
    x_ = G.get("sx", "")
    ksq, kss, kxn, kpT = "sq" + x_, "ss" + x_, "xn" + x_, "pT" + x_
    ph.act(sq[:P, :], xt[:P, :], AF.Square, R="xt" + tag, W=[ksq, kss], accum=ss[:P, 0:1])
    ph.act(ss[:P, 1:2], ss[:P, 0:1], AF.Sqrt, R=[kss, "eps"], W=kss, bias=G["eps"][:P, 0:1], scale=1.0 / D)
    ph.op("dve", lambda e: e.reciprocal(out=ss[:P, 2:3], in_=ss[:P, 1:2]), R=kss, W=kss)
    ph.ts("dve", xn[:P, :], xt[:P, :], ss[:P, 2:3], ALU.mult, R=["xt" + tag, kss], W=kxn)
    for k in range(8):
        ph.tr(pT[:, k, :P], xn[:P, k * 128:(k + 1) * 128], G["identb"][:P, :P], R=[kxn, "identb"], W=kpT)
    ph.tt("dve", hT[:, :, c0:c0 + P], pT[:, :, :P], bc(gcol[:, :].unsqueeze(2), [128, 8, P]), ALU.mult,
          R=[kpT, gkey], W=hkey)


def load_col(ph, dst, src1d, n, key):
    ph.dma("sp", dst, src1d.rearrange("(k p) -> p k", p=128), W=key, slow=True)


def norm_scratch(ph, G0, sx="", eps=None):
    G = dict(G0)
    G["sx"] = sx
    G["sq"] = ph.sb("sq" + sx, [128, D], F32)
    G["ss"] = ph.sb("ss" + sx, [128, 4], F32)
    G["xn"] = ph.sb("xn" + sx, [128, D], BF16)
    G["pT"] = ph.ps("pT" + sx, [128, 8, 128], BF16)
    if eps is None:
        G["eps"] = ph.sb("eps", [128, 1], F32)
        ph.memset("dve", G["eps"][:], 1e-6, W="eps")
    else:
        G["eps"] = eps
    return G


def build_program(upto=9, debug=False):
    nc = bass.Bass("TRN2", target_bir_lowering=False)
    I = {}

    def inp(name, shape, dt=F32):
        I[name] = nc.dram_tensor(name, list(shape), dt, kind="ExternalInput").ap()

    def outp(name, shape):
        I[name] = nc.dram_tensor(name, list(shape), F32, kind="ExternalOutput").ap()

    def scratch(name, shape, dt):
        if debug:
            I[name] = nc.dram_tensor(name, list(shape), dt, kind="ExternalOutput").ap()
        else:
            I[name] = nc.dram_tensor(name, list(shape), dt).ap()
    if debug:
        scratch("d_BwT", [128, 4 * CS * 2 * 128], BF16); scratch("d_Kmat", [128, 4 * CS * 128], BF16)
        scratch("d_CwT", [128, CS * 2 * 16 * 32], BF16); scratch("d_Abar", [128, 64], F32)

    inp("xall", [NT, D]); inp("pall", [NT, 256])
    inp("st_shift", [NS, 1792]); inp("st_wkv", [128, 4096]); inp("st_re", [NS, 2048]); inp("st_im", [NS, 2048])
    inp("st_conv", [NS, 2, 2816])
    inp("ln1_g", [D]); inp("w_in", [D, 4352]); inp("mu_shift", [1792]); inp("w0", [512]); inp("w2", [64, 512])
    inp("a0", [512]); inp("a2", [64, 512]); inp("g2", [128, 512]); inp("k_k", [512]); inp("k_a", [512])
    inp("r_k", [512]); inp("lnx_g", [512]); inp("lnx_b", [512]); inp("w_rw_out", [512, D])
    inp("A_re", [32, 64]); inp("A_im", [32, 64]); inp("log_dt", [32]); inp("B_re", [32, 64, 16]); inp("B_im", [32, 64, 16])
    inp("C_re", [512, 64]); inp("C_im", [512, 64]); inp("D_skip", [512]); inp("w_glu", [512, 2048]); inp("w_out", [D, D])
    inp("ln2_g", [D]); inp("w_ffn_in", [D, 5632]); inp("conv_w", [3, 2816]); inp("conv_b", [2816]); inp("w_ffn_out", [2816, D])
    inp("ln3_g", [D]); inp("w_ple_gate", [D, D]); inp("w_ple", [256, D]); inp("final_g", [D])
    inp("c_ident", [128, 128]); inp("c_msl", [128, 128]); inp("c_msu", [128, 128]); inp("c_mui", [128, 128])
    inp("c_blk64", [128, 128]); inp("c_blk32", [128, 128]); inp("c_rowgp", [128, 128])
    outp("y", [NT, D]); outp("p_shift", [1792]); outp("p_wkv", [512, 64]); outp("p_re", [2048]); outp("p_im", [2048])
    outp("p_conv", [2, 2816]); outp("s_shift", [NS, 1792]); outp("s_wkv", [128, 4096]); outp("s_re", [NS, 2048])
    outp("s_im", [NS, 2048]); outp("s_conv", [NS, 2, 2816])
    scratch("PRW", [1792, NT], F32); scratch("UU", [512, NT], F32); scratch("GT", [2048, NT], BF16)
    scratch("YF", [512, NT], BF16); scratch("ZZ", [512, NT], BF16); scratch("X1", [NT, D], F32); scratch("X2", [NT, D], F32)
    scratch("SW", [6, NS, 512], F32); scratch("SY", [128, 64], F32)

    with contextlib.ExitStack() as gst:
        def gsb(name, shape, dt):
            return gst.enter_context(nc.sbuf_tensor("g_" + name, list(shape), dt))
        G0 = {}
        G0["identb"] = gsb("identb", [128, 128], BF16)
        G0["identf"] = gsb("identf", [128, 128], F32)
        with contextlib.ExitStack() as g2:
            def g2sb(name, shape, dt):
                return g2.enter_context(nc.sbuf_tensor("g_" + name, list(shape), dt))
            G0["BwT"] = g2sb("BwT", [128, 4, CS, 2, 128], BF16)
            G0["Kmat"] = g2sb("Kmat", [128, 4, CS, 128], BF16)
            G0["CwT"] = g2sb("CwT", [128, CS, 2, 16, 32], BF16)
            G0["Abar"] = g2sb("Abar", [128, 2, 2, 16], F32)
            if upto >= 1:
                phase1(nc, I, G0, debug)
            else:
                phase0(nc, I, G0, debug)
            if upto >= 2:
                phase2(nc, I, G0, True)
            if upto >= 2.5:
                phase2(nc, I, G0, False)
        g4 = contextlib.ExitStack()
        WFI = g4.enter_context(nc.sbuf_tensor("g_wfi", [128, 8, 5632], BF16))
        if upto >= 3:
            phase3(nc, I, G0, None, WFI)
        if upto >= 4:
            phase4(nc, I, G0, WFI)
        g4.close()
        if upto >= 5:
            phase5(nc, I, G0)
    return nc


def phase0(nc, I, G0, debug=False, ph=None):
    own = ph is None
    if own:
        ph = Ph(nc, "p0")
        ph.dma("pool", G0["identb"][:], I["c_ident"], W="identb")
        ph.dma("sp", G0["identf"][:], I["c_ident"], W="identf")
    sb = ph.sb
    lr = sb("lr", [128, 16], F32); li = sb("li", [128, 16], F32); dtl = sb("dtl", [128, 16], F32)
    Bre = sb("Bre", [128, 16, 16], F32); Bim = sb("Bim", [128, 16, 16], F32)
    ph.dma("sp", lr[:], I["A_re"].rearrange("(P gp) n -> (gp n) P", gp=2), W="lr", slow=True)
    ph.dma("sp", li[:], I["A_im"].rearrange("(P gp) n -> (gp n) P", gp=2), W="li", slow=True)
    ldt2 = I["log_dt"].rearrange("(P gp) -> gp P", gp=2)
    for gp in range(2):
        ph.dma("sp", dtl[64 * gp:64 * gp + 64, :], ldt2[gp].partition_broadcast(64), W="dtl", slow=True)
    ph.dma("sp", Bre[:], I["B_re"].rearrange("(P gp) n c -> (gp n) P c", gp=2), W="Bre")
    ph.dma("sp", Bim[:], I["B_im"].rearrange("(P gp) n c -> (gp n) P c", gp=2), W="Bim")
    rowgp = sb("rowgp", [128, 128], F32); blk32 = sb("blk32", [128, 128], F32)
    ph.dma("sp", rowgp[:], I["c_rowgp"], W="rowgp"); ph.dma("sp", blk32[:], I["c_blk32"], W="blk32")
    CT = [sb("CTr", [128, 4, 128], F32), sb("CTi", [128, 4, 128], F32)]
    c2 = sb("c2", [128, 128], F32)
    pA = ph.ps("pA", [128, 4, 128], F32)
    for ri, nm in enumerate(("C_re", "C_im")):
        for k in range(4):
            src = I[nm][k * 128:(k + 1) * 128, :]
            ph.dma("sp", c2[:, 0:64], src, W="c2"); ph.dma("sp", c2[:, 64:128], src, W="c2")
            ph.tt("dve", c2[:], c2[:], rowgp[:], ALU.mult, R=["c2", "rowgp"], W="c2")
            ph.tr(pA[:, k, :], c2[:], G0["identf"][:], R=["c2", "identf"], W="pA")
        ph.cp("dve", CT[ri][:], pA[:], R="pA", W="CT%d" % ri)
    t = {n: sb(n, [128, 16], F32) for n in ("dt", "e1", "mag", "ang", "sa", "ca", "sinv", "cosv", "ar", "ai", "den",
                                             "rden", "am1", "fr", "fi", "t1", "t2")}
    V = "dve"
    K = lambda *n: list(n)
    hpi = sb("hpi", [128, 1], F32)
    ph.memset(V, hpi[:], math.pi / 2, W="hpi")
    ph.act(t["dt"][:], dtl[:], AF.Exp, R="dtl", W="dt")
    ph.tt(V, t["e1"][:], lr[:], t["dt"][:], ALU.mult, R=K("lr", "dt"), W="e1")
    ph.act(t["mag"][:], t["e1"][:], AF.Exp, R="e1", W="mag")
    ph.tt(V, t["ang"][:], li[:], t["dt"][:], ALU.mult, R=K("li", "dt"), W="ang")
    ph.ts(V, t["sa"][:], t["ang"][:], 1.0 / 64, ALU.mult, R="ang", W="sa")
    ph.act(t["sinv"][:], t["sa"][:], AF.Sin, R="sa", W="sinv")
    ph.act(t["cosv"][:], t["sa"][:], AF.Sin, R=["sa", "hpi"], W="cosv", bias=hpi[:, 0:1])
    for _ in range(6):
        ph.tt(V, t["t1"][:], t["cosv"][:], t["cosv"][:], ALU.mult, R="cosv", W="t1")
        ph.tt(V, t["t2"][:], t["sinv"][:], t["sinv"][:], ALU.mult, R="sinv", W="t2")
        ph.stt(t["sinv"][:], t["cosv"][:], 2.0, t["sinv"][:], ALU.mult, ALU.mult, R=["cosv", "sinv", "t2"], W="sinv")
        ph.tt(V, t["cosv"][:], t["t1"][:], t["t2"][:], ALU.subtract, R=["t1", "t2", "sinv"], W="cosv")
    ph.tt(V, t["ar"][:], t["mag"][:], t["cosv"][:], ALU.mult, R=K("mag", "cosv"), W="ar")
    ph.tt(V, t["ai"][:], t["mag"][:], t["sinv"][:], ALU.mult, R=K("mag", "sinv"), W="ai")
    ph.tt(V, t["den"][:], lr[:], lr[:], ALU.mult, R="lr", W="den")
    ph.tt(V, t["t1"][:], li[:], li[:], ALU.mult, R="li", W="t1")
    ph.tt(V, t["den"][:], t["den"][:], t["t1"][:], ALU.add, R=K("den", "t1"), W="den")
    ph.op(V, lambda e: e.reciprocal(out=t["rden"][:], in_=t["den"][:]), R="den", W="rden")
    ph.ts(V, t["am1"][:], t["ar"][:], -1.0, ALU.add, R="ar", W="am1")
    ph.tt(V, t["t1"][:], t["am1"][:], lr[:], ALU.mult, R=K("am1", "lr", "den"), W="t1")
    ph.tt(V, t["t2"][:], t["ai"][:], li[:], ALU.mult, R=K("ai", "li"), W="t2")
    ph.tt(V, t["t1"][:], t["t1"][:], t["t2"][:], ALU.add, R=K("t1", "t2"), W="t1")
    ph.tt(V, t["fr"][:], t["t1"][:], t["rden"][:], ALU.mult, R=K("t1", "rden"), W="fr")
    ph.tt(V, t["t1"][:], t["ai"][:], lr[:], ALU.mult, R=K("ai", "lr", "fr"), W="t1")
    ph.tt(V, t["t2"][:], t["am1"][:], li[:], ALU.mult, R=K("am1", "li"), W="t2")
    ph.tt(V, t["t1"][:], t["t1"][:], t["t2"][:], ALU.subtract, R=K("t1", "t2"), W="t1")
    ph.tt(V, t["fi"][:], t["t1"][:], t["rden"][:], ALU.mult, R=K("t1", "rden"), W="fi")
    pwr = sb("pwr", [128, CS + 1, 16], F32); pwi = sb("pwi", [128, CS + 1, 16], F32)
    ph.memset(V, pwr[:, 0, :], 1.0, W="pw"); ph.memset(V, pwi[:, 0, :], 0.0, W="pw")
    for e in range(CS):
        ph.tt(V, t["t1"][:], pwr[:, e, :], t["ar"][:], ALU.mult, R=K("pw", "ar", "fi"), W="t1")
        ph.tt(V, t["t2"][:], pwi[:, e, :], t["ai"][:], ALU.mult, R=K("pw", "ai"), W="t2")
        ph.tt(V, pwr[:, e + 1, :], t["t1"][:], t["t2"][:], ALU.subtract, R=K("t1", "t2"), W="pw")
        ph.tt(V, t["t1"][:], pwr[:, e, :], t["ai"][:], ALU.mult, R=K("pw", "ai"), W="t1")
        ph.tt(V, t["t2"][:], pwi[:, e, :], t["ar"][:], ALU.mult, R=K("pw", "ar"), W="t2")
        ph.tt(V, pwi[:, e + 1, :], t["t1"][:], t["t2"][:], ALU.add, R=K("t1", "t2"), W="pw")
    Ab = G0["Abar"]
    ph.cp(V, Ab[:, 0, 0, :], pwr[:, CS, :], R="pw", W="Abar"); ph.cp(V, Ab[:, 0, 1, :], pwi[:, CS, :], R="pw", W="Abar")
    ph.cp(V, Ab[:, 1, 0, :], pwr[:, 1, :], R="pw", W="Abar"); ph.cp(V, Ab[:, 1, 1, :], pwi[:, 1, :], R="pw", W="Abar")
    bbr = sb("bbr", [128, 16, 16], F32); bbi = sb("bbi", [128, 16, 16], F32)
    u1 = sb("u1", [128, 16, 16], F32); u2 = sb("u2", [128, 16, 16], F32)
    frb = bc(t["fr"][:, :].unsqueeze(2), [128, 16, 16]); fib = bc(t["fi"][:, :].unsqueeze(2), [128, 16, 16])
    ph.tt(V, u1[:], Bre[:], frb, ALU.mult, R=K("Bre", "fr"), W="u1")
    ph.tt(V, u2[:], Bim[:], fib, ALU.mult, R=K("Bim", "fi"), W="u2")
    ph.tt(V, bbr[:], u1[:], u2[:], ALU.subtract, R=K("u1", "u2"), W="bbr")
    ph.tt(V, u1[:], Bim[:], frb, ALU.mult, R=K("Bim", "fr", "bbr"), W="u1")
    ph.tt(V, u2[:], Bre[:], fib, ALU.mult, R=K("Bre", "fi", "bbr"), W="u2")
    ph.tt(V, bbi[:], u1[:], u2[:], ALU.add, R=K("u1", "u2"), W="bbi")
    Ew = sb("Ew", [128, CS, 2, 16, 2, 16], F32)
    ph.memset(V, Ew[:].rearrange("p a b c d e -> p (a b c d e)"), 0.0, W="Ew")
    for e in range(CS):
        pr = bc(pwr[:, e, :].unsqueeze(2), [128, 16, 16]); pi = bc(pwi[:, e, :].unsqueeze(2), [128, 16, 16])
        ph.tt(V, u1[:], bbr[:], pr, ALU.mult, R=K("bbr", "pw", "Ew"), W="u1")
        ph.tt(V, u2[:], bbi[:], pi, ALU.mult, R=K("bbi", "pw", "Ew"), W="u2")
        ph.tt(V, u1[:], u1[:], u2[:], ALU.subtract, R=K("u1", "u2"), W="u1")
        for gp in range(2):
            ph.cp(V, Ew[64 * gp:64 * gp + 64, e, 0, :, gp, :], u1[64 * gp:64 * gp + 64, :, :], R="u1", W="Ew")
        ph.tt(V, u1[:], bbr[:], pi, ALU.mult, R=K("bbr", "pw", "Ew"), W="u1")
        ph.tt(V, u2[:], bbi[:], pr, ALU.mult, R=K("bbi", "pw", "Ew"), W="u2")
        ph.tt(V, u1[:], u1[:], u2[:], ALU.add, R=K("u1", "u2"), W="u1")
        for gp in range(2):
            ph.cp(V, Ew[64 * gp:64 * gp + 64, e, 1, :, gp, :], u1[64 * gp:64 * gp + 64, :, :], R="u1", W="Ew")
    CTin = sb("CTin", [128, 4, 128], F32)
    ph.ts(V, CTin[:], CT[1][:], -1.0, ALU.mult, R="CT1", W="CTin")
    pB = [ph.ps("pB%d" % i, [128, 4, 128], F32) for i in range(2)]
    n = 0
    for j in range(CS):
        e = CS - 1 - j
        for ri in range(2):
            pb = pB[n % 2]; n += 1
            for k in range(4):
                src = Ew[:, e, ri, 4 * k:4 * k + 4, :, :].rearrange("p a b c -> p (a b c)")
                ph.tr(pb[:, k, :], src, G0["identf"][:], R=["Ew", "identf"], W="pB%d" % ((n - 1) % 2))
            ph.cp("act" if n % 2 else "dve", G0["BwT"][:, :, j, ri, :], pb[:], R="pB%d" % ((n - 1) % 2), W="BwT")
    for tau in range(CS):
        pb = pB[n % 2]; key = "pB%d" % (n % 2); n += 1
        for k in range(4):
            lr_ = Ew[:, tau, 0, 4 * k:4 * k + 4, :, :].rearrange("p a b c -> p (a b c)")
            li_ = Ew[:, tau, 1, 4 * k:4 * k + 4, :, :].rearrange("p a b c -> p (a b c)")
            ph.mm(pb[:, k, :], lr_, CT[0][:, k, :], True, False, R=["Ew", "CT0"], W=key)
            ph.mm(pb[:, k, :], li_, CTin[:, k, :], False, True, R=["Ew", "CTin"], W=key)
        ph.tt(V, G0["Kmat"][:, :, tau, :], pb[:], bc(blk32[:, :].unsqueeze(1), [128, 4, 128]), ALU.mult,
              R=[key, "blk32"], W="Kmat")
    w1 = sb("w1", [128, 16, 32], F32); w2_ = sb("w2", [128, 16, 32], F32)
    CTr3 = CT[0][:].rearrange("p k (a b) -> p (k a) b", a=4); CTi3 = CT[1][:].rearrange("p k (a b) -> p (k a) b", a=4)
    for i in range(CS):
        pr = bc(pwr[:, i + 1, :].unsqueeze(2), [128, 16, 32]); pi = bc(pwi[:, i + 1, :].unsqueeze(2), [128, 16, 32])
        ph.tt(V, w1[:], CTr3, pr, ALU.mult, R=K("CT0", "pw", "CwT"), W="w1")
        ph.tt(V, w2_[:], CTi3, pi, ALU.mult, R=K("CT1", "pw", "CwT"), W="w2")
        ph.tt(V, G0["CwT"][:, i, 0, :, :], w1[:], w2_[:], ALU.subtract, R=K("w1", "w2"), W="CwT")
        ph.tt(V, w1[:], CTr3, pi, ALU.mult, R=K("CT0", "pw", "CwT"), W="w1")
        ph.tt(V, w2_[:], CTi3, pr, ALU.mult, R=K("CT1", "pw", "CwT"), W="w2")
        ph.tt(V, w1[:], w1[:], w2_[:], ALU.add, R=K("w1", "w2"), W="w1")
        ph.ts(V, G0["CwT"][:, i, 1, :, :], w1[:], -1.0, ALU.mult, R="w1", W="CwT")
    if debug:
        ph.dma("sp", I["d_BwT"], G0["BwT"][:].rearrange("p a b c d -> p (a b c d)"), R="BwT")
        ph.dma("sp", I["d_Kmat"], G0["Kmat"][:].rearrange("p a b c -> p (a b c)"), R="Kmat")
        ph.dma("sp", I["d_CwT"], G0["CwT"][:].rearrange("p a b c d -> p (a b c d)"), R="CwT")
        ph.dma("sp", I["d_Abar"], G0["Abar"][:].rearrange("p a b c -> p (a b c)"), R="Abar")
    if own:
        ph.finish()


def phase1(nc, I, G0, debug=False):
    ph = Ph(nc, "p1")
    win = ph.sb("win", [128, 8, 4352], BF16)
    for k in range(8):
        ph.dma("pool", win[:, k, :], I["w_in"][k * 128:(k + 1) * 128, :], W="win%d" % k)
    ph.dma("pool", G0["identb"][:], I["c_ident"], W="identb")
    ph.dma("sp", G0["identf"][:], I["c_ident"], W="identf")
    ph.rec_begin()
    phase0(nc, I, G0, debug, ph=ph)
    s0 = ph.rec_end()
    ph.rec_begin()
    G = norm_scratch(ph, G0)
    g1c = ph.sb("g1c", [128, 8], F32)
    load_col(ph, g1c[:], I["ln1_g"], 8, "g1c")
    hTs = [ph.sb("hT%d" % i, [128, 8, 512], BF16) for i in range(2)]
    xts = [ph.sb("xt%d" % i, [128, D], F32) for i in range(2)]
    pm = [ph.ps("pm%d" % i, [128, 512], F32) for i in range(4)]
    stf = [ph.sb("stf%d" % i, [128, 512], F32) for i in range(4)]
    stb = [ph.sb("stb%d" % i, [128, 512], BF16) for i in range(3)]
    WK = ["win%d" % k for k in range(8)]
    nx = nf = nb = npm = 0
    for bi_, (t0, nt) in enumerate(BLOCKS):
        P = min(128, nt)
        hT = hTs[bi_ % 2]; hk = "hT%d" % (bi_ % 2)
        for s in range((nt + 127) // 128):
            xt = xts[nx % 2]; tg = str(nx % 2); nx += 1
            ph.dma("sp", xt[:P, :], I["xall"][t0 + s * 128:t0 + s * 128 + P, :], W="xt" + tg)
            rms_to_hT(ph, G, xt, P, g1c, hT, s * 128, tg, "g1c", hk)
        for m in range(34):
            pb = pm[npm % 4]; pk = "pm%d" % (npm % 4); npm += 1
            for k in range(8):
                ph.mm(pb[:, :nt], win[:, k, m * 128:(m + 1) * 128], hT[:, k, :nt], k == 0, k == 7,
                      R=["win%d" % k, hk], W=pk)
            if m < 18:
                sf = stf[nf % 4]; sk = "stf%d" % (nf % 4); nf += 1
                ph.cp("dve" if m % 2 else "act", sf[:, :nt], pb[:, :nt], R=pk, W=sk)
                if m < 14:
                    ph.dma("pool", I["PRW"][m * 128:(m + 1) * 128, t0:t0 + nt], sf[:, :nt], R=sk)
                else:
                    ph.dma("pool", I["UU"][(m - 14) * 128:(m - 13) * 128, t0:t0 + nt], sf[:, :nt], R=sk)
            else:
                sbf = stb[nb % 3]; sk = "stb%d" % (nb % 3); nb += 1
                ph.act(sbf[:, :nt], pb[:, :nt], AF.Sigmoid, R=pk, W=sk)
                ph.dma("act", I["GT"][(m - 18) * 128:(m - 17) * 128, t0:t0 + nt], sbf[:, :nt], R=sk)
    s1 = ph.rec_end()
    ph.play(s1, s0)
    ph.finish()


def alloc_w3(nc, st):
    t = lambda n, shp: st.enter_context(nc.sbuf_tensor("w3_" + n, shp, BF16))
    return {"rwo": t("rwo", [128, 4, D]), "glu": t("glu", [128, 4, 2048]), "wo": t("wo", [128, 8, D])}


def load_w3(ph, I, W3):
    for k in range(4):
        ph.dma("pool", W3["rwo"][:, k, :], I["w_rw_out"][k * 128:(k + 1) * 128, :], W="rwo")
        ph.dma("pool", W3["glu"][:, k, :], I["w_glu"][k * 128:(k + 1) * 128, :], W="glu")
    for k in range(8):
        ph.dma("pool", W3["wo"][:, k, :], I["w_out"][k * 128:(k + 1) * 128, :], W="wo")


def phase2(nc, I, G0, prompt, W3=None):
    ph = Ph(nc, "p2a" if prompt else "p2b")
    sb, ps = ph.sb, ph.ps
    V = "dve"
    if W3 is not None:
        load_w3(ph, I, W3)
    ph._s5tmp = [sb("s5a", [128, 2, 16], F32), sb("s5b", [128, 2, 16], F32)]
    ph._s5xb = sb("Xb", [128, 2, 16, 64], BF16)
    ph._s5du = sb("s5du", [128, 512], F32)
    if prompt:
        msl = sb("msl", [128, 128], BF16); msu = sb("msu", [128, 128], BF16); mui = sb("mui", [128, 128], BF16)
        ph.dma("pool", msl[:], I["c_msl"], W="msl"); ph.dma("pool", msu[:], I["c_msu"], W="msu")
        ph.dma("pool", mui[:], I["c_mui"], W="mui")
    blk64 = sb("blk64", [128, 128], F32); ph.dma("sp", blk64[:], I["c_blk64"], W="blk64")
    w2a2 = sb("w2a2", [128, 512], BF16); g2b = sb("g2b", [128, 512], BF16)
    ph.dma("pool", w2a2[0:64, :], I["w2"], W="w2a2"); ph.dma("pool", w2a2[64:128, :], I["a2"], W="w2a2")
    ph.dma("pool", g2b[:], I["g2"], W="g2b")
    pc = {}
    for nm, n in (("mu_shift", 14), ("w0", 4), ("a0", 4), ("k_k", 4), ("k_a", 4), ("r_k", 4), ("lnx_g", 4),
                  ("lnx_b", 4), ("D_skip", 4)):
        pc[nm] = sb("c_" + nm, [128, n], F32)
        load_col(ph, pc[nm][:], I[nm], n, "c_" + nm)
    PK = ["c_" + k for k in pc]
    scm = sb("scm", [128, 4, 128], F32)
    ph.memset(V, scm[:].rearrange("p a b -> p (a b)"), 1.0, W="scm"); ph.memset(V, scm[:, :, 0:1], 0.0, W="scm")
    eps_gn = sb("eps_gn", [128, 1], F32); ph.memset(V, eps_gn[:], 64e-5, W="eps_gn")
    if prompt:
        Sst = sb("Sst", [128, 4, 64], F32); Sbd = sb("Sbd", [128, 4, 128], BF16)
        ph.memset(V, Sst[:].rearrange("p a b -> p (a b)"), 0.0, W="Sst")
        ph.memset(V, Sbd[:].rearrange("p a b -> p (a b)"), 0.0, W="Sbd")
        Xs = sb("Xs", [128, 2, 16, 65], F32)
        ph.memset(V, Xs[:].rearrange("p a b c -> p (a b c)"), 0.0, W="Xs")
        Pf = sb("Pf", [128, 14, 513], F32)
        ph.memset(V, Pf[:, :, 0:1], 0.0, W="Pf")
    WB = 512 if prompt else NS
    WC = 128 if prompt else NS
    uf = sb("uf", [128, 4, WB], F32); ub = sb("ub", [128, 4, WB], BF16)
    YFb = sb("YFb", [128, 4, WB], BF16); ZZb = sb("ZZb", [128, 4, WB], BF16)
    f4 = lambda n: sb(n, [128, 4, WC], F32)
    b4 = lambda n: sb(n, [128, 4, WC], BF16)
    XS = sb("XS", [128, 14, WC], F32); dd = sb("dd", [128, 14, WC], F32)
    lin = sb("lin", [128, WC], BF16); sgx = sb("sgx", [128, WC], BF16)
    sig = f4("sig"); aa = f4("aa"); gg = f4("gg"); kk0 = f4("kk0"); tq = f4("tq"); rn = f4("rn"); kkn = f4("kkn")
    bb = f4("bb"); kmod = f4("kmod"); bon = f4("bon"); cs = f4("cs"); ex1 = f4("ex1"); ex2 = f4("ex2"); ex3 = f4("ex3")
    nbias = sb("nbias", [128, 4], F32); PCt = sb("PCt", [128, 4], F32)
    gns = f4("gns")
    KX = {n_: n_ for n_ in ("rT", "kT", "bT", "aT", "khT", "bhT", "vT", "PCt", "bon", "gg")}
    if prompt:
        rT = b4("rT"); kT = b4("kT"); bT = b4("bT"); aT = b4("aT"); khT = b4("khT"); bhT = b4("bhT"); vT = b4("vT")
        alt = {"rT": b4("rT1"), "kT": b4("kT1"), "bT": b4("bT1"), "aT": b4("aT1"), "khT": b4("khT1"),
               "bhT": b4("bhT1"), "vT": b4("vT1"), "PCt": sb("PCt1", [128, 4], F32), "bon": f4("bon1"), "gg": f4("gg1")}
        Vtok = sb("Vtok", [128, 512], BF16); Khtok = sb("Khtok", [128, 512], BF16); Bhtok = sb("Bhtok", [128, 512], BF16)
        h8 = lambda n: sb(n, [128, 8, 128], BF16)
        Nb = [h8("Nb0"), h8("Nb1")]; Lb = [h8("Lb0"), h8("Lb1")]; Mt = [h8("Mt0"), h8("Mt1")]
        LKb = h8("LKb"); Arb = h8("Arb"); Ark = h8("Ark")
        Wbf = sb("Wbf", [128, 512], BF16); Ubf = sb("Ubf", [128, 512], BF16)
        tS = sb("tS", [128, 4, 64], F32)
    Ysb = sb("Ysb", [128, 8, 64], F32); Ysq = sb("Ysq", [128, 8, 64], F32); ynb = sb("ynb", [128, 8, 64], BF16)
    gn = sb("gn", [128, 6, 8], F32)
    pF = [ps("pF%d" % i, [128, 4, 128], F32) for i in range(6)]
    pT = [ps("pTb%d" % i, [128, 8, 128], BF16) for i in range(2)]
    cnt = {"f": 0, "t": 0}

    def getF():
        i = cnt["f"] % 6; cnt["f"] += 1
        return pF[i], "pF%d" % i

    def mkpool(base):
        st_ = {"n": 0}

        def get():
            i = base + st_["n"] % 2; st_["n"] += 1
            return pF[i], "pF%d" % i
        return get
    getF_prep, getF_core, getFs = mkpool(0), mkpool(2), mkpool(4)

    def getT():
        i = cnt["t"] % 2; cnt["t"] += 1
        return pT[i], "pTb%d" % i

    ib = G0["identb"]

    if not prompt:
        sample_mixer(ph, I, G0, locals())
        ph.finish()
        return
    Lbase = dict(locals())
    Lpar = [dict(Lbase), dict(Lbase)]
    Lpar[1].update(alt)
    Lpar[1]["KX"] = {n_: n_ + "1" for n_ in KX}
    REC = []
    for bi, (t0, nt) in enumerate(BLOCKS[:4]):
        ph.rec_begin()
        if bi > 0:
            ph.cp(V, Pf[:, :, 0:1], Pf[:, :, 512:513], R="Pf", W="Pf")
        ph.dma("sp", Pf[:, :, 1:513], I["PRW"][:, t0:t0 + nt].rearrange("(m p) t -> p m t", p=128), W="Pf")
        if bi == 3:
            ph.dma("sp", I["p_shift"].rearrange("(m p) -> p m", p=128), Pf[:, :, 512], R="Pf", slow=True)
        hdr_pf = ph.rec_end()
        ph.rec_begin()
        ph.dma("act", uf[:], I["UU"][:, t0:t0 + nt].rearrange("(m p) t -> p m t", p=128), W="uf")
        ph.cp("act", ub[:].rearrange("p a b -> p (a b)"), uf[:].rearrange("p a b -> p (a b)"), R="uf", W="ub")
        hdr_ub = ph.rec_end()
        ph.rec_begin()
        s5_block(ph, I, G0, pc, Xs, ub, ZZb, getFs, nchunk=64, which=0, ncol=512)
        ph.dma("act", I["ZZ"][:, t0:t0 + nt].rearrange("(m p) t -> p m t", p=128), ZZb[:], R="ZZb")
        s5s = ph.rec_end()
        m0, m1 = ph._s5marks
        preps, cores = [], []
        for c in range(4):
            c0 = c * 128
            Lc = dict(Lpar[c % 2]); Lc["getF"] = getF_prep
            Lk = dict(Lpar[c % 2]); Lk["getF"] = getF_core
            ph.rec_begin()
            ph.tt(V, dd[:], Pf[:, :, c0:c0 + 128], Pf[:, :, c0 + 1:c0 + 129], ALU.subtract, R="Pf", W="dd")
            ph.tt(V, dd[:], dd[:], bc(pc["mu_shift"][:, :].unsqueeze(2), [128, 14, 128]), ALU.mult,
                  R=["dd", "c_mu_shift"], W="dd")
            ph.tt(V, XS[:], dd[:], Pf[:, :, c0 + 1:c0 + 129], ALU.add, R=["dd", "Pf"], W="XS")
            rwkv_prep_and_core(ph, Lc, c, c0)
            preps.append(ph.rec_end())
            ph.rec_begin()
            wkv_core(ph, Lk, c, c0)
            cores.append(ph.rec_end())
        ph.rec_begin()
        ph.dma("pool", I["YF"][:, t0:t0 + nt].rearrange("(m p) t -> p m t", p=128), YFb[:], R="YFb")
        yfst = ph.rec_end()
        hs = (m1 - m0) // 2
        REC.append(dict(hdr_pf=hdr_pf, hdr_ub=hdr_ub, SG=s5s[:m0], SS1=s5s[m0:m0 + hs], SS2=s5s[m0 + hs:m1],
                        SY=s5s[m1:], preps=preps, cores=cores, yfst=yfst))
    ph.play(REC[0]["hdr_pf"])
    ph.play(REC[0]["preps"][0])
    for bi in range(4):
        Rb = REC[bi]
        ph.play(Rb["hdr_ub"])
        ph.play(Rb["cores"][0], Rb["preps"][1], Rb["SG"])
        ph.play(Rb["cores"][1], Rb["preps"][2], Rb["SS1"])
        ph.play(Rb["cores"][2], Rb["preps"][3], Rb["SS2"])
        if bi < 3:
            ph.play(REC[bi + 1]["hdr_pf"])
            ph.play(Rb["cores"][3], Rb["SY"], REC[bi + 1]["preps"][0])
        else:
            ph.play(Rb["cores"][3], Rb["SY"])
        ph.play(Rb["yfst"])
    ph.dma("sp", I["p_wkv"].rearrange("(m p) v -> p m v", p=128), Sst[:], R="Sst")
    ph.dma("sp", I["p_re"].rearrange("(P p) -> p P", p=128), Xs[:, 0, :, 0], R="Xs", slow=True)
    ph.dma("sp", I["p_im"].rearrange("(P p) -> p P", p=128), Xs[:, 1, :, 0], R="Xs", slow=True)
    ph.finish()


def rwkv_prep_and_core(ph, L, c, c0):
    V = "dve"
    PV = L.get("PV", "dve")
    KX = L["KX"]
    pc = L["pc"]; XS = L["XS"]; getF = L["getF"]; getT = L["getT"]; ib = L["ib"]
    sig, aa, gg, kk0, tq, rn, kkn = L["sig"], L["aa"], L["gg"], L["kk0"], L["tq"], L["rn"], L["kkn"]
    bb, kmod, bon, cs, ex1, ex2, ex3 = L["bb"], L["kmod"], L["bon"], L["cs"], L["ex1"], L["ex2"], L["ex3"]
    rT, kT, bT, aT, khT, bhT, vT = L["rT"], L["kT"], L["bT"], L["aT"], L["khT"], L["bhT"], L["vT"]
    lin, sgx, w2a2, g2b, blk64 = L["lin"], L["sgx"], L["w2a2"], L["g2b"], L["blk64"]
    nbias, PCt, scm = L["nbias"], L["PCt"], L["scm"]
    r_ = XS[:, 0:4, :]; k_ = XS[:, 4:8, :]; v_ = XS[:, 8:12, :]
    B4 = lambda t: bc(t[:, :].unsqueeze(2), [128, 4, 128])
    fl = lambda t: t[:].rearrange("p a b -> p (a b)")
    ph.act(lin[0:64, :], XS[0:64, 12, :], AF.Tanh, R="XS", W="lin")
    ph.cp("act", lin[64:128, :], XS[64:128, 12, :], R="XS", W="lin")
    ph.act(sgx[:], XS[:, 13, :], AF.Sigmoid, R="XS", W="sgx")
    pw_, kw_ = getF()
    for m in range(4):
        ph.mm(pw_[:, m, :], w2a2[0:64, m * 128:(m + 1) * 128], lin[0:64, :], True, True, R=["w2a2", "lin"], W=kw_)
    for m in range(4):
        ph.act(sig[:, m, :], pw_[:, m, :], AF.Sigmoid, R=[kw_, "c_w0"], W="sig", bias=pc["w0"][:, m:m + 1])
    pa_, ka_ = getF()
    for m in range(4):
        ph.mm(pa_[:, m, :], w2a2[64:128, m * 128:(m + 1) * 128], lin[64:128, :], True, True, R=["w2a2", "lin"], W=ka_)
    for m in range(4):
        ph.act(aa[:, m, :], pa_[:, m, :], AF.Sigmoid, R=[ka_, "c_a0"], W="aa", bias=pc["a0"][:, m:m + 1])
    pg_, kg_ = getF()
    for m in range(4):
        ph.mm(pg_[:, m, :], g2b[:, m * 128:(m + 1) * 128], sgx[:], True, True, R=["g2b", "sgx"], W=kg_)
    ph.cp("act", gg[:], pg_[:], R=kg_, W=KX["gg"])
    ph.tt(PV, kk0[:], k_, B4(pc["k_k"]), ALU.mult, R=["XS", "c_k_k"], W="kk0")
    ph.tt(PV, tq[:], kk0[:], kk0[:], ALU.mult, R="kk0", W="tq")
    pq, kq = getF()
    for m in range(4):
        ph.mm(pq[:, m, :], blk64[:], tq[:, m, :], True, True, R=["blk64", "tq"], W=kq)
    ph.act(rn[:], pq[:], AF.Sqrt, R=kq, W="rn")
    ph.ts(V, rn[:], rn[:], 1e-12, ALU.max, R="rn", W="rn")
    ph.op(V, lambda e: e.reciprocal(out=fl(rn), in_=fl(rn)), R="rn", W="rn")
    ph.tt(PV, kkn[:], kk0[:], rn[:], ALU.mult, R=["kk0", "rn"], W="kkn")
    ph.tt(PV, bb[:], kkn[:], aa[:], ALU.mult, R=["kkn", "aa"], W="bb")
    ph.tt(PV, tq[:], aa[:], B4(pc["k_a"]), ALU.mult, R=["aa", "c_k_a", kq], W="tq")
    ph.tt(PV, tq[:], tq[:], B4(pc["k_a"]), ALU.subtract, R=["tq", "c_k_a"], W="tq")
    ph.stt(kmod[:], tq[:], 1.0, k_, ALU.add, ALU.mult, R=["tq", "XS"], W="kmod")
    ph.tt(PV, tq[:], r_, kmod[:], ALU.mult, R=["XS", "kmod"], W="tq")
    ph.tt(PV, tq[:], tq[:], B4(pc["r_k"]), ALU.mult, R=["tq", "c_r_k"], W="tq")
    pq2, kq2 = getF()
    for m in range(4):
        ph.mm(pq2[:, m, :], blk64[:], tq[:, m, :], True, True, R=["blk64", "tq"], W=kq2)
    ph.tt(V, bon[:], pq2[:], v_, ALU.mult, R=[kq2, "XS"], W=KX["bon"])
    ph.op(V, lambda e: e.tensor_tensor_scan(out=fl(cs), data0=fl(scm), data1=fl(sig), initial=0.0, op0=ALU.mult,
                                             op1=ALU.add), R=["scm", "sig"], W="cs")
    ph.ts(V, nbias[:], cs[:, :, 127], -C1, ALU.mult, R="cs", W="nbias")
    ph.act(PCt[:], nbias[:], AF.Exp, R="nbias", W=KX["PCt"])
    ph.act(ex1[:], cs[:], AF.Exp, R="cs", W="ex1", scale=-C1)
    ph.tt(PV, rT[:], r_, ex1[:], ALU.mult, R=["XS", "ex1"], W=KX["rT"])
    ph.act(ex2[:], cs[:], AF.Exp, R="cs", W="ex2", scale=C1)
    ph.tt(PV, kT[:], kmod[:], ex2[:], ALU.mult, R=["kmod", "ex2"], W=KX["kT"])
    ph.tt(PV, bT[:], bb[:], ex2[:], ALU.mult, R=["bb", "ex2"], W=KX["bT"])
    ph.tt(PV, ex3[:], cs[:], sig[:], ALU.subtract, R=["cs", "sig"], W="ex3")
    ph.act(ex3[:], ex3[:], AF.Exp, R="ex3", W="ex3", scale=-C1)
    ph.stt(aT[:], kkn[:], -1.0, ex3[:], ALU.mult, ALU.mult, R=["kkn", "ex3"], W=KX["aT"])
    for m in range(4):
        ph.act(ex1[:, m, :], cs[:, m, :], AF.Exp, R=["cs", "nbias", KX["rT"]], W="ex1", bias=nbias[:, m:m + 1], scale=C1)
    ph.tt(PV, khT[:], kmod[:], ex1[:], ALU.mult, R=["kmod", "ex1"], W=KX["khT"])
    ph.tt(PV, bhT[:], bb[:], ex1[:], ALU.mult, R=["bb", "ex1"], W=KX["bhT"])
    ph.cp("act", vT[:], v_, R="XS", W=KX["vT"])


def wkv_core(ph, L, c, c0):
    V = "dve"
    KX = L["KX"]
    getF = L["getF"]; getT = L["getT"]; ib = L["ib"]
    rT, kT, bT, aT, khT, bhT, vT = L["rT"], L["kT"], L["bT"], L["aT"], L["khT"], L["bhT"], L["vT"]
    Vtok, Khtok, Bhtok = L["Vtok"], L["Khtok"], L["Bhtok"]
    Nb, Lb, Mt, LKb, Arb, Ark = L["Nb"], L["Lb"], L["Mt"], L["LKb"], L["Arb"], L["Ark"]
    msl, msu, mui = L["msl"], L["msu"], L["mui"]
    Wbf, Ubf, Ysb, Ysq, ynb, gn = L["Wbf"], L["Ubf"], L["Ysb"], L["Ysq"], L["ynb"], L["gn"]
    Sst, Sbd, PCt, tS = L["Sst"], L["Sbd"], L["PCt"], L["tS"]
    pc = L["pc"]; bon, gg, YFb = L["bon"], L["gg"], L["YFb"]
    M4 = lambda m_: bc(m_[:, :].unsqueeze(1), [128, 4, 128])
    pt, kt = getT()
    for m in range(4):
        ph.tr(pt[:, m, :], vT[:, m, :], ib[:], R=KX["vT"], W=kt)
    for m in range(4):
        ph.tr(pt[:, 4 + m, :], khT[:, m, :], ib[:], R=KX["khT"], W=kt)
    ph.cp("act", Vtok[:], pt[:, 0:4, :].rearrange("p a b -> p (a b)"), R=kt, W="Vtok")
    ph.cp(V, Khtok[:], pt[:, 4:8, :].rearrange("p a b -> p (a b)"), R=kt, W="Khtok")
    pt2, kt2 = getT()
    for m in range(4):
        ph.tr(pt2[:, m, :], bhT[:, m, :], ib[:], R=KX["bhT"], W=kt2)
    ph.cp("act", Bhtok[:], pt2[:, 0:4, :].rearrange("p a b -> p (a b)"), R=kt2, W="Bhtok")

    def hsl(t, h):
        return t[64 * (h % 2):64 * (h % 2) + 64, h // 2, :]

    def amat(dst, dkey, lhs, lkey, rhs, rkey, mask, mkey):
        for par in range(2):
            pb, pk = getF()
            for q in range(4):
                h = 2 * q + par
                ph.mm(pb[:, q, :], hsl(lhs, h), hsl(rhs, h), True, True, R=[lkey, rkey], W=pk)
            ph.tt(V, dst[:, par:8:2, :], pb[:], M4(mask), ALU.mult, R=[pk, mkey], W=dkey)

    amat(Nb[0], "Nb0", aT, KX["aT"], bT, KX["bT"], msl, "msl")
    amat(Lb[0], "Lb0", bT, KX["bT"], aT, KX["aT"], msu, "msu")
    amat(LKb, "LKb", kT, KX["kT"], aT, KX["aT"], msu, "msu")
    amat(Arb, "Arb", bT, KX["bT"], rT, KX["rT"], mui, "mui")
    amat(Ark, "Ark", kT, KX["kT"], rT, KX["rT"], mui, "mui")
    for half in range(2):
        ph.tt(V, Mt[0][:, half * 4:half * 4 + 4, :], Lb[0][:, half * 4:half * 4 + 4, :], M4(ib), ALU.add,
              R=["Lb0", "identb"], W="Mt0")
    cur = 0
    for lvl in range(6):
        nxt = 1 - cur
        for half in range(2):
            pb, pk = getF()
            for q in range(4):
                h = half * 4 + q
                ph.mm(pb[:, q, :], Lb[cur][:, h, :], Nb[cur][:, h, :], True, True, R=["Lb%d" % cur, "Nb%d" % cur], W=pk)
            ph.cp("act", Nb[nxt][:, half * 4:half * 4 + 4, :], pb[:], R=pk, W="Nb%d" % nxt)
        if lvl < 5:
            for half in range(2):
                pb, pk = getF()
                for q in range(4):
                    h = half * 4 + q
                    ph.mm(pb[:, q, :], Nb[cur][:, h, :], Lb[cur][:, h, :], True, True,
                          R=["Lb%d" % cur, "Nb%d" % cur], W=pk)
                ph.cp("act", Lb[nxt][:, half * 4:half * 4 + 4, :], pb[:], R=pk, W="Lb%d" % nxt)
        for half in range(2):
            pb, pk = getF()
            for q in range(4):
                h = half * 4 + q
                ph.mm(pb[:, q, :], Nb[nxt][:, h, :], Mt[cur][:, h, :], True, True, R=["Nb%d" % nxt, "Mt%d" % cur], W=pk)
            ph.tt(V, Mt[nxt][:, half * 4:half * 4 + 4, :], pb[:], Mt[cur][:, half * 4:half * 4 + 4, :], ALU.add,
                  R=[pk, "Mt%d" % cur], W="Mt%d" % nxt)
        cur = nxt
    MtF = Mt[cur]; mk = "Mt%d" % cur
    def hcols(pb, h):
        return pb[:].rearrange("p a b -> p (a b)")[:, h * 64:h * 64 + 64]

    def pcols(pb, m):
        return pb[:].rearrange("p a b -> p (a b)")[:, m * 128:m * 128 + 128]

    pb, pk = getF()
    for m in range(4):
        ph.mm(pcols(pb, m), aT[:, m, :], Sbd[:, m, :], True, False, R=[KX["aT"], "Sbd"], W=pk)
        for hh in range(2):
            h = 2 * m + hh
            ph.mm(hcols(pb, h), LKb[:, h, :], Vtok[:, h * 64:h * 64 + 64], False, hh == 1, R=["LKb", "Vtok"], W=pk)
    ph.cp("act", Wbf[:], pb[:].rearrange("p a b -> p (a b)"), R=pk, W="Wbf")
    pb, pk = getF()
    for h in range(8):
        ph.mm(hcols(pb, h), MtF[:, h, :], Wbf[:, h * 64:h * 64 + 64], True, True, R=[mk, "Wbf"], W=pk)
    ph.cp("act", Ubf[:], pb[:].rearrange("p a b -> p (a b)"), R=pk, W="Ubf")
    pb, pk = getF()
    for m in range(4):
        ph.mm(pcols(pb, m), rT[:, m, :], Sbd[:, m, :], True, False, R=[KX["rT"], "Sbd"], W=pk)
        for hh in range(2):
            h = 2 * m + hh
            ph.mm(hcols(pb, h), Arb[:, h, :], Ubf[:, h * 64:h * 64 + 64], False, False, R=["Arb", "Ubf"], W=pk)
            ph.mm(hcols(pb, h), Ark[:, h, :], Vtok[:, h * 64:h * 64 + 64], False, hh == 1, R=["Ark", "Vtok"], W=pk)
    ph.cp("act", Ysb[:].rearrange("p a b -> p (a b)"), pb[:].rearrange("p a b -> p (a b)"), R=pk, W="Ysb")
    pS, kS = getF()
    for m in range(4):
        ph.mm(pS[:, m, :], Bhtok[:, m * 128:(m + 1) * 128], Ubf[:, m * 128:(m + 1) * 128], True, False,
              R=["Bhtok", "Ubf"], W=kS)
        ph.mm(pS[:, m, :], Khtok[:, m * 128:(m + 1) * 128], Vtok[:, m * 128:(m + 1) * 128], False, True,
              R=["Khtok", "Vtok"], W=kS)
    ph.tt(V, tS[:], Sst[:], bc(PCt[:, :].unsqueeze(2), [128, 4, 64]), ALU.mult, R=["Sst", KX["PCt"]], W="tS")
    for hh in range(2):
        rs = slice(64 * hh, 64 * hh + 64)
        ph.tt(V, Sst[rs, :, :], tS[rs, :, :], pS[rs, :, 64 * hh:64 * hh + 64], ALU.add, R=["tS", kS], W="Sst")
        ph.cp(V, Sbd[rs, :, 64 * hh:64 * hh + 64], Sst[rs, :, :], R="Sst", W="Sbd")
    groupnorm_out(ph, L, c0, 128)


def groupnorm_out(ph, L, c0, P):
    V = "dve"
    KX = L["KX"]
    Ysb, Ysq, ynb, gn = L["Ysb"], L["Ysq"], L["ynb"], L["gn"]
    pc = L["pc"]; bon, gg, YFb = L["bon"], L["gg"], L["YFb"]; getT = L["getT"]; ib = L["ib"]
    eps_gn = L["eps_gn"]; ex2 = L["gns"]
    ph.op(V, lambda e: e.tensor_reduce(out=gn[:P, 0, :], in_=Ysb[:P], axis=AX.X, op=ALU.add), R="Ysb", W="gn")
    ph.act(Ysq[:P].rearrange("p a b -> p (a b)"), Ysb[:P].rearrange("p a b -> p (a b)"), AF.Square, R="Ysb", W="Ysq")
    ph.op(V, lambda e: e.tensor_reduce(out=gn[:P, 1, :], in_=Ysq[:P], axis=AX.X, op=ALU.add), R="Ysq", W="gn")
    ph.ts(V, gn[:P, 2, :], gn[:P, 0, :], 1.0 / 64, ALU.mult, R="gn", W="gn")
    ph.tt(V, gn[:P, 3, :], gn[:P, 2, :], gn[:P, 2, :], ALU.mult, R="gn", W="gn")
    ph.stt(gn[:P, 4, :], gn[:P, 1, :], 1.0 / 64, gn[:P, 3, :], ALU.mult, ALU.subtract, R="gn", W="gn")
    ph.act(gn[:P, 4, :], gn[:P, 4, :], AF.Sqrt, R=["gn", "eps_gn"], W="gn", bias=eps_gn[:P, 0:1])
    ph.op(V, lambda e: e.reciprocal(out=gn[:P, 5, :], in_=gn[:P, 4, :]), R="gn", W="gn")
    ph.tt(V, Ysq[:P], Ysb[:P], bc(gn[:P, 2, :].unsqueeze(2), [P, 8, 64]), ALU.subtract, R=["Ysb", "gn"], W="Ysq")
    ph.tt(V, ynb[:P], Ysq[:P], bc(gn[:P, 5, :].unsqueeze(2), [P, 8, 64]), ALU.mult, R=["Ysq", "gn"], W="ynb")
    pt, kt = getT()
    for m in range(4):
        ph.tr(pt[:, m, :P], ynb[:P, 2 * m:2 * m + 2, :].rearrange("p a b -> p (a b)"), ib[:P, :P], R="ynb", W=kt)
    B4 = lambda t: bc(t[:, :].unsqueeze(2), [128, 4, P])
    t1 = ex2
    ph.tt(V, t1[:, :, :P], pt[:, 0:4, :P], B4(pc["lnx_g"]), ALU.mult, R=[kt, "c_lnx_g"], W="gns")
    ph.tt(V, t1[:, :, :P], t1[:, :, :P], B4(pc["lnx_b"]), ALU.add, R=["gns", "c_lnx_b"], W="gns")
    ph.tt(V, t1[:, :, :P], t1[:, :, :P], bon[:, :, :P], ALU.add, R=["gns", KX["bon"]], W="gns")
    ph.tt(V, YFb[:, :, c0:c0 + P], t1[:, :, :P], gg[:, :, :P], ALU.mult, R=["gns", KX["gg"]], W="YFb")


def s5_block(ph, I, G0, pc, Xs, ub, ZZb, getF, nchunk, which, ncol, step=CS, npos=CS):
    V = "dve"
    BwT, Kmat, CwT, Ab = G0["BwT"], G0["Kmat"], G0["CwT"], G0["Abar"]
    nm = nchunk
    assert nm * 8 <= 512
    for Pl in range(4):
        pb, pk = getF()
        flat = pb[:].rearrange("p a b -> p (a b)")
        for ri in range(2):
            for k in range(4):
                q = ri * 4 + k
                dst = flat[:, q * nm:(q + 1) * nm]
                for j in range(npos):
                    jj = (CS - npos) + j
                    rhs = ub[32 * Pl:32 * Pl + 32, k, j:j + (nm - 1) * step + 1:step]
                    ph.mm(dst, BwT[32 * Pl:32 * Pl + 32, k, jj, ri, :], rhs, j == 0, j == npos - 1,
                          R=["BwT", "ub"], W=pk, tp=((96, 0) if Pl == 3 else None))
        for ri in range(2):
            ph.cp(V, Xs[:, ri, Pl:16:4, 1:1 + nm],
                  flat[:, ri * 4 * nm:(ri + 1) * 4 * nm].rearrange("p (q m) -> p q m", m=nm), R=[pk], W="Xs")
    A_r = bc(Ab[:, which, 0, :].unsqueeze(1), [128, 2, 16]); A_i = bc(Ab[:, which, 1, :].unsqueeze(1), [128, 2, 16])
    ph._s5marks = [len(ph._rec) if ph._rec is not None else 0]
    tmpa = ph._s5tmp[0]; tmpb = ph._s5tmp[1]
    for m in range(nm):
        ph.tt(SCAN_ENG, tmpa[:], Xs[:, :, :, m], A_r, ALU.mult, R=["Xs", "Abar"], W="s5a")
        ph.tt(SCAN_ENG, tmpb[:], Xs[:, :, :, m], A_i, ALU.mult, R=["Xs", "Abar"], W="s5b")
        ph.tt(SCAN_ENG, Xs[:, :, :, m + 1], Xs[:, :, :, m + 1], tmpa[:], ALU.add, R=["Xs", "s5a"], W="Xs")
        ph.tt(SCAN_ENG, Xs[:, 0, :, m + 1], Xs[:, 0, :, m + 1], tmpb[:, 1, :], ALU.subtract, R=["Xs", "s5b"], W="Xs")
        ph.tt(SCAN_ENG, Xs[:, 1, :, m + 1], Xs[:, 1, :, m + 1], tmpb[:, 0, :], ALU.add, R=["Xs", "s5b"], W="Xs")
    ph._s5marks.append(len(ph._rec) if ph._rec is not None else 0)
    Xb = ph._s5xb
    ph.cp("act", Xb[:, :, :, 0:nm], Xs[:, :, :, 0:nm], R="Xs", W="Xb")
    for k in range(4):
        pb, pk = getF()
        flat = pb[:].rearrange("p a b -> p (a b)")
        for i in range(npos):
            dst = flat[:, i * nm:(i + 1) * nm]
            for tau in range(i + 1):
                rhs = ub[:, k, (i - tau):(i - tau) + (nm - 1) * step + 1:step]
                ph.mm(dst, Kmat[:, k, tau, :], rhs, tau == 0, False, R=["Kmat", "ub"], W=pk)
            for Pl in range(4):
                P_ = 4 * k + Pl
                for ri in range(2):
                    ph.mm(flat[32 * Pl:32 * Pl + 32, i * nm:(i + 1) * nm], CwT[:, i, ri, P_, :], Xb[:, ri, P_, 0:nm],
                          False, ri == 1, R=["CwT", "Xb"], W=pk, tp=(0, 32 * Pl))
        du = ph._s5du
        ph.ts(V, du[:, 0:ncol], ub[:, k, 0:ncol], pc["D_skip"][:, k:k + 1], ALU.mult, R=["ub", "c_D_skip", "s5z"], W="s5du")
        if npos == 1:
            ph.tt(V, du[:, 0:ncol], du[:, 0:ncol], flat[:, 0:nm], ALU.add, R=["s5du", pk], W="s5du")
        else:
            ph.tt(V, du[:, 0:ncol].rearrange("p (m i) -> p m i", i=npos), du[:, 0:ncol].rearrange("p (m i) -> p m i", i=npos),
                  flat[:, 0:npos * nm].rearrange("p (i m) -> p m i", m=nm), ALU.add, R=["s5du", pk], W="s5du")
        ph.act(ZZb[:, k, 0:ncol], du[:, 0:ncol], AF.Gelu_apprx_tanh, R="s5du", W=["ZZb", "s5z"])
    ph.cp(V, Xs[:, :, :, 0], Xs[:, :, :, nm], R="Xs", W="Xs")


def sample_mixer(ph, I, G0, L):
    V = "dve"
    sb = ph.sb
    pc = L["pc"]; getF, getT, ib = L["getF"], L["getT"], L["ib"]
    identf = G0["identf"]
    XS = L["XS"]; dd = L["dd"]
    t0 = T
    n = NS
    cur = sb("s_cur", [128, 14, NS], F32); prv = sb("s_prv", [128, 14, NS], F32)
    ph.dma("sp", cur[:], I["PRW"][:, t0:t0 + n].rearrange("(m p) t -> p m t", p=128), W="s_cur")
    sst = sb("s_sst", [NS, 1792], F32)
    ph.dma("sp", sst[:], I["st_shift"], W="s_sst")
    for half in range(4):
        pb, pk = getF()
        flat = pb[:].rearrange("p a b -> p (a b)")
        ms = list(range(half * 4, min(14, half * 4 + 4)))
        for q, m in enumerate(ms):
            ph.tr(flat[:, q * NS:(q + 1) * NS], sst[:, m * 128:(m + 1) * 128], identf[:NS, :NS], R=["s_sst"], W=pk)
        ph.cp(V, prv[:, ms[0]:ms[-1] + 1, :], flat[:, 0:len(ms) * NS].rearrange("p (a b) -> p a b", b=NS), R=pk, W="s_prv")
    ph.dbg("cur", cur[:], [128, 14, NS], "s_cur")
    ph.dbg("prv", prv[:], [128, 14, NS], "s_prv")
    so = sst
    for half in range(4):
        pb, pk = getF()
        flat = pb[:].rearrange("p a b -> p (a b)")
        ms = list(range(half * 4, min(14, half * 4 + 4)))
        for q, m in enumerate(ms):
            ph.tr(flat[:NS, q * 128:(q + 1) * 128], cur[:, m, :], identf[:], R=["s_cur"], W=pk)
        ph.cp(V, so[:, ms[0] * 128:(ms[-1] + 1) * 128], flat[:NS, 0:len(ms) * 128], R=pk, W="s_sst")
    ph.dma("sp", I["s_shift"], so[:], R="s_sst")
    xs = XS[:, :, 0:NS]
    ph.tt(V, dd[:, :, 0:NS], prv[:], cur[:], ALU.subtract, R=["s_prv", "s_cur"], W="dd")
    ph.tt(V, dd[:, :, 0:NS], dd[:, :, 0:NS], bc(pc["mu_shift"][:, :].unsqueeze(2), [128, 14, NS]), ALU.mult,
          R=["dd", "c_mu_shift"], W="dd")
    ph.tt(V, xs, dd[:, :, 0:NS], cur[:], ALU.add, R=["dd", "s_cur"], W="XS")
    uf = L["uf"]; ub = L["ub"]; ZZb = L["ZZb"]
    ph.dma("act", uf[:, :, 0:NS], I["UU"][:, t0:t0 + n].rearrange("(m p) t -> p m t", p=128), W="uf")
    ph.cp("act", ub[:, :, 0:NS], uf[:, :, 0:NS], R="uf", W="ub")
    stx = [sb("s_stre", [NS, 2048], F32), sb("s_stim", [NS, 2048], F32)]
    ph.dma("sp", stx[0][:], I["st_re"], W="s_stx0"); ph.dma("sp", stx[1][:], I["st_im"], W="s_stx1")
    Xsm = sb("s_Xsm", [128, 2, 16, NS], F32)
    for ri in range(2):
        for q4 in range(4):
            pb, pk = getF()
            flat = pb[:].rearrange("p a b -> p (a b)")
            for q in range(4):
                P_ = q4 * 4 + q
                ph.tr(flat[:, q * NS:(q + 1) * NS], stx[ri][:, P_ * 128:(P_ + 1) * 128], identf[:NS, :NS],
                      R="s_stx%d" % ri, W=pk)
            ph.cp(V, Xsm[:, ri, q4 * 4:q4 * 4 + 4, :], flat[:, 0:4 * NS].rearrange("p (a b) -> p a b", b=NS), R=pk, W="s_Xsm")
    s5_sample(ph, I, G0, pc, Xsm, ub, ZZb, getF, stx)
    ph.dma("act", I["ZZ"][:, t0:t0 + n].rearrange("(m p) t -> p m t", p=128), ZZb[:, :, 0:NS], R="ZZb")
    rwkv_sample(ph, I, G0, L)
    ph.dma("sp", I["YF"][:, t0:t0 + n].rearrange("(m p) t -> p m t", p=128), L["YFb"][:, :, 0:NS], R="YFb")


def s5_sample(ph, I, G0, pc, Xsm, ub, ZZb, getF, stx):
    V = "dve"
    BwT, Kmat, CwT, Ab = G0["BwT"], G0["Kmat"], G0["CwT"], G0["Abar"]
    identf = G0["identf"]
    Xb = ph._s5xb
    ph.cp("act", Xb[:, :, :, 0:NS], Xsm[:], R="s_Xsm", W="Xb")
    du = ph._s5du
    for k in range(4):
        pb, pk = getF()
        flat = pb[:].rearrange("p a b -> p (a b)")
        ph.mm(flat[:, 0:NS], Kmat[:, k, 0, :], ub[:, k, 0:NS], True, False, R=["Kmat", "ub"], W=pk)
        for Pl in range(4):
            P_ = 4 * k + Pl
            for ri in range(2):
                ph.mm(flat[32 * Pl:32 * Pl + 32, 0:NS], CwT[:, 0, ri, P_, :], Xb[:, ri, P_, 0:NS], False,
                      ri == 1, R=["CwT", "Xb"], W=pk, tp=(0, 32 * Pl))
        ph.ts(V, du[:, 0:NS], ub[:, k, 0:NS], pc["D_skip"][:, k:k + 1], ALU.mult, R=["ub", "c_D_skip", "s5z"], W="s5du")
        ph.tt(V, du[:, 0:NS], du[:, 0:NS], flat[:, 0:NS], ALU.add, R=["s5du", pk], W="s5du")
        ph.act(ZZb[:, k, 0:NS], du[:, 0:NS], AF.Gelu_apprx_tanh, R="s5du", W=["ZZb", "s5z"])
    Gs = ph.sb("s_Gs", [128, 2, 16, NS], F32)
    for Pl in range(4):
        pb, pk = getF()
        flat = pb[:].rearrange("p a b -> p (a b)")
        for ri in range(2):
            for k in range(4):
                q = ri * 4 + k
                ph.mm(flat[:, q * NS:(q + 1) * NS], BwT[32 * Pl:32 * Pl + 32, k, CS - 1, ri, :],
                      ub[32 * Pl:32 * Pl + 32, k, 0:NS], True, True, R=["BwT", "ub"], W=pk,
                      tp=((96, 0) if Pl == 3 else None))
        for ri in range(2):
            ph.cp(V, Gs[:, ri, Pl:16:4, :], flat[:, ri * 4 * NS:(ri + 1) * 4 * NS].rearrange("p (q m) -> p q m", m=NS),
                  R=pk, W="s_Gs")
    A_r = bc(Ab[:, 1, 0, :].unsqueeze(2), [128, 16, NS]); A_i = bc(Ab[:, 1, 1, :].unsqueeze(2), [128, 16, NS])
    ta = ph.sb("s_ta", [128, 16, NS], F32)
    ph.tt(V, ta[:], Xsm[:, 0], A_r, ALU.mult, R=["s_Xsm", "Abar"], W="s_ta")
    ph.tt(V, Gs[:, 0], Gs[:, 0], ta[:], ALU.add, R=["s_Gs", "s_ta"], W="s_Gs")
    ph.tt(V, ta[:], Xsm[:, 1], A_i, ALU.mult, R=["s_Xsm", "Abar", "s_Gs"], W="s_ta")
    ph.tt(V, Gs[:, 0], Gs[:, 0], ta[:], ALU.subtract, R=["s_Gs", "s_ta"], W="s_Gs")
    ph.tt(V, ta[:], Xsm[:, 1], A_r, ALU.mult, R=["s_Xsm", "Abar", "s_Gs"], W="s_ta")
    ph.tt(V, Gs[:, 1], Gs[:, 1], ta[:], ALU.add, R=["s_Gs", "s_ta"], W="s_Gs")
    ph.tt(V, ta[:], Xsm[:, 0], A_i, ALU.mult, R=["s_Xsm", "Abar", "s_Gs"], W="s_ta")
    ph.tt(V, Gs[:, 1], Gs[:, 1], ta[:], ALU.add, R=["s_Gs", "s_ta"], W="s_Gs")
    for ri, nm in enumerate(("s_re", "s_im")):
        xo = stx[ri]
        for q4 in range(4):
            pb, pk = getF()
            flat = pb[:].rearrange("p a b -> p (a b)")
            for q in range(4):
                P_ = q4 * 4 + q
                ph.tr(flat[:NS, q * 128:(q + 1) * 128], Gs[:, ri, P_, :], identf[:], R="s_Gs", W=pk)
            ph.cp(V, xo[:, q4 * 512:(q4 + 1) * 512], flat[:NS, 0:512], R=pk, W="s_stx%d" % ri)
        ph.dma("sp", I[nm], xo[:], R="s_stx%d" % ri)


def rwkv_sample(ph, I, G0, L):
    V = "dve"
    sb = ph.sb
    pc = L["pc"]; getF, getT, ib = L["getF"], L["getT"], L["ib"]
    identf = G0["identf"]
    XS = L["XS"]
    sig, aa, gg, kk0, tq, rn, kkn = L["sig"], L["aa"], L["gg"], L["kk0"], L["tq"], L["rn"], L["kkn"]
    bb, kmod, bon = L["bb"], L["kmod"], L["bon"]
    lin, sgx, w2a2, g2b, blk64 = L["lin"], L["sgx"], L["w2a2"], L["g2b"], L["blk64"]
    n = NS
    r_ = XS[:, 0:4, 0:n]; k_ = XS[:, 4:8, 0:n]; v_ = XS[:, 8:12, 0:n]
    B4 = lambda t: bc(t[:, :].unsqueeze(2), [128, 4, n])
    S4 = lambda t: t[:, :, 0:n]
    ph.act(lin[0:64, 0:n], XS[0:64, 12, 0:n], AF.Tanh, R="XS", W="lin")
    ph.cp("act", lin[64:128, 0:n], XS[64:128, 12, 0:n], R="XS", W="lin")
    ph.act(sgx[:, 0:n], XS[:, 13, 0:n], AF.Sigmoid, R="XS", W="sgx")
    pw_, kw_ = getF(); pa_, ka_ = getF(); pg_, kg_ = getF()
    for m in range(4):
        ph.mm(pw_[:, m, 0:n], w2a2[0:64, m * 128:(m + 1) * 128], lin[0:64, 0:n], True, True, R=["w2a2", "lin"], W=kw_)
        ph.mm(pa_[:, m, 0:n], w2a2[64:128, m * 128:(m + 1) * 128], lin[64:128, 0:n], True, True, R=["w2a2", "lin"], W=ka_)
        ph.mm(pg_[:, m, 0:n], g2b[:, m * 128:(m + 1) * 128], sgx[:, 0:n], True, True, R=["g2b", "sgx"], W=kg_)
    for m in range(4):
        ph.act(sig[:, m, 0:n], pw_[:, m, 0:n], AF.Sigmoid, R=[kw_, "c_w0"], W="sig", bias=pc["w0"][:, m:m + 1])
        ph.act(aa[:, m, 0:n], pa_[:, m, 0:n], AF.Sigmoid, R=[ka_, "c_a0"], W="aa", bias=pc["a0"][:, m:m + 1])
    ph.cp("act", S4(gg), pg_[:, :, 0:n], R=kg_, W="gg")
    ph.tt(V, S4(kk0), k_, B4(pc["k_k"]), ALU.mult, R=["XS", "c_k_k"], W="kk0")
    ph.tt(V, S4(tq), S4(kk0), S4(kk0), ALU.mult, R="kk0", W="tq")
    pq, kq = getF()
    for m in range(4):
        ph.mm(pq[:, m, 0:n], blk64[:], tq[:, m, 0:n], True, True, R=["blk64", "tq"], W=kq)
    ph.act(S4(rn), pq[:, :, 0:n], AF.Sqrt, R=kq, W="rn")
    ph.ts(V, S4(rn), S4(rn), 1e-12, ALU.max, R="rn", W="rn")
    ph.op(V, lambda e: e.reciprocal(out=S4(rn), in_=S4(rn)), R="rn", W="rn")
    ph.tt(V, S4(kkn), S4(kk0), S4(rn), ALU.mult, R=["kk0", "rn"], W="kkn")
    ph.tt(V, S4(bb), S4(kkn), S4(aa), ALU.mult, R=["kkn", "aa"], W="bb")
    ph.tt(V, S4(tq), S4(aa), B4(pc["k_a"]), ALU.mult, R=["aa", "c_k_a", kq], W="tq")
    ph.tt(V, S4(tq), S4(tq), B4(pc["k_a"]), ALU.subtract, R=["tq", "c_k_a"], W="tq")
    ph.stt(S4(kmod), S4(tq), 1.0, k_, ALU.add, ALU.mult, R=["tq", "XS"], W="kmod")
    ph.tt(V, S4(tq), r_, S4(kmod), ALU.mult, R=["XS", "kmod"], W="tq")
    ph.tt(V, S4(tq), S4(tq), B4(pc["r_k"]), ALU.mult, R=["tq", "c_r_k"], W="tq")
    pq2, kq2 = getF()
    for m in range(4):
        ph.mm(pq2[:, m, 0:n], blk64[:], tq[:, m, 0:n], True, True, R=["blk64", "tq"], W=kq2)
    ph.tt(V, S4(bon), pq2[:, :, 0:n], v_, ALU.mult, R=[kq2, "XS"], W="bon")
    wdec = L["ex1"]
    ph.act(S4(wdec), S4(sig), AF.Exp, R="sig", W="ex1", scale=-C1)
    srcs = [r_, S4(wdec), S4(kmod), v_, S4(kkn), S4(bb)]
    keys = ["XS", "ex1", "kmod", "XS", "kkn", "bb"]
    tok = sb("s_tok", [NS, 6, 512], F32)
    for i, (src, kkey) in enumerate(zip(srcs, keys)):
        pb, pk = getF()
        flat = pb[:].rearrange("p a b -> p (a b)")
        for m in range(4):
            ph.tr(flat[:NS, m * 128:(m + 1) * 128], src[:, m, :], identf[:], R=kkey, W=pk)
        ph.cp(V if i % 2 else "act", tok[:, i, :], flat[:NS, 0:512], R=pk, W="s_tok")
    ph.dma("sp", I["SW"].rearrange("i b f -> b i f"), tok[:], R="s_tok", W="SWd")
    vec = sb("s_vec", [128, 6, 64], F32)
    ph.dma("sp", vec[:], I["SW"].rearrange("i b (h k) -> (b h) i k", h=8), R="SWd", W="s_vec")
    S0 = sb("s_S0", [128, 64, 64], F32)
    ph.dma("act", S0[:].rearrange("p a b -> p (a b)"), I["st_wkv"], W="s_S0")
    tmp = sb("s_tmp", [128, 64, 64], F32)
    sa = sb("s_sa", [128, 64], F32); yv = sb("s_yv", [128, 64], F32); kka = sb("s_kka", [128, 64], F32)
    kB = lambda i: bc(vec[:, i, :].unsqueeze(1), [128, 64, 64])
    ph.tt(V, tmp[:], S0[:], kB(4), ALU.mult, R=["s_S0", "s_vec"], W="s_tmp")
    ph.op(V, lambda e: e.tensor_reduce(out=sa[:], in_=tmp[:], axis=AX.X, op=ALU.add), R="s_tmp", W="s_sa")
    ph.tt(V, S0[:], S0[:], kB(1), ALU.mult, R=["s_S0", "s_vec", "s_tmp"], W="s_S0")
    ph.tt(V, tmp[:], bc(sa[:, :].unsqueeze(2), [128, 64, 64]), kB(5), ALU.mult, R=["s_sa", "s_vec"], W="s_tmp")
    ph.tt(V, S0[:], S0[:], tmp[:], ALU.subtract, R=["s_S0", "s_tmp"], W="s_S0")
    ph.tt(V, tmp[:], bc(vec[:, 3, :].unsqueeze(2), [128, 64, 64]), kB(2), ALU.mult, R=["s_vec", "s_S0"], W="s_tmp")
    ph.tt(V, S0[:], S0[:], tmp[:], ALU.add, R=["s_S0", "s_tmp"], W="s_S0")
    ph.dma("act", I["s_wkv"], S0[:].rearrange("p a b -> p (a b)"), R="s_S0")
    ph.tt(V, tmp[:], S0[:], kB(0), ALU.mult, R=["s_S0", "s_vec"], W="s_tmp")
    ph.op(V, lambda e: e.tensor_reduce(out=yv[:], in_=tmp[:], axis=AX.X, op=ALU.add), R="s_tmp", W="s_yv")
    ph.dma("sp", I["SY"], yv[:], R="s_yv", W="SYd")
    Ysb = L["Ysb"]
    ph.dma("sp", Ysb[:NS].rearrange("p a b -> p (a b)"), I["SY"].rearrange("(b h) v -> b (h v)", h=8), R="SYd", W="Ysb")
    groupnorm_out(ph, L, 0, NS)


def phase3(nc, I, G0, W3, WFI):
    ph = Ph(nc, "p3")
    V = "dve"
    W3 = alloc_w3(nc, ph.st)
    load_w3(ph, I, W3)
    rwo, glu, wo = W3["rwo"], W3["glu"], W3["wo"]
    for k in range(8):
        ph.dma("pool", WFI[:, k, :], I["w_ffn_in"][k * 128:(k + 1) * 128, :], W="wfi_pre")
    yf = ph.sb("yf", [128, 4, 512], BF16); zz = ph.sb("zz", [128, 4, 512], BF16); gt = ph.sb("gt", [128, 16, 512], BF16)
    trw = ph.sb("trw", [128, 8, 512], F32); mg = ph.sb("mg", [128, 8, 512], BF16)
    sgb = [ph.sb("sgb%d" % i, [128, 512], F32) for i in range(2)]
    s5t = [ph.sb("s5t%d" % i, [128, 512], F32) for i in range(2)]
    xts = [ph.sb("xt%d" % i, [128, D], F32) for i in range(2)]
    pm = [ph.ps("pm%d" % i, [128, 512], F32) for i in range(6)]
    npm = nx = ns = 0
    for (t0, nt) in BLOCKS:
        P = min(128, nt)
        r3 = lambda name: I[name][:, t0:t0 + nt].rearrange("(m p) t -> p m t", p=128)
        ph.dma("sp", yf[:, :, :nt], r3("YF"), W="yf"); ph.dma("sp", zz[:, :, :nt], r3("ZZ"), W="zz")
        ph.dma("act", gt[:, :, :nt], r3("GT"), W="gt")
        for m in range(8):
            pb = pm[npm % 6]; pk = "pm%d" % (npm % 6); npm += 1
            for k in range(4):
                ph.mm(pb[:, :nt], rwo[:, k, m * 128:(m + 1) * 128], yf[:, k, :nt], k == 0, k == 3, R=["rwo", "yf"], W=pk)
            ph.tt(V, trw[:, m, :nt], pb[:, :nt], gt[:, m, :nt], ALU.mult, R=[pk, "gt"], W="trw%d" % m)
        for m in range(8):
            pa = pm[npm % 6]; pka = "pm%d" % (npm % 6); npm += 1
            pb = pm[npm % 6]; pkb = "pm%d" % (npm % 6); npm += 1
            for k in range(4):
                ph.mm(pa[:, :nt], glu[:, k, m * 128:(m + 1) * 128], zz[:, k, :nt], k == 0, k == 3, R=["glu", "zz"], W=pka)
            for k in range(4):
                ph.mm(pb[:, :nt], glu[:, k, D + m * 128:D + (m + 1) * 128], zz[:, k, :nt], k == 0, k == 3,
                      R=["glu", "zz"], W=pkb)
            sg = sgb[ns % 2]; sk = "sgb%d" % (ns % 2); s5 = s5t[ns % 2]; s5k = "s5t%d" % (ns % 2); ns += 1
            ph.act(sg[:, :nt], pb[:, :nt], AF.Sigmoid, R=pkb, W=sk)
            ph.tt(V, s5[:, :nt], pa[:, :nt], sg[:, :nt], ALU.mult, R=[pka, sk], W=s5k)
            ph.tt(V, s5[:, :nt], s5[:, :nt], gt[:, 8 + m, :nt], ALU.mult, R=[s5k, "gt"], W=s5k)
            ph.tt(V, mg[:, m, :nt], s5[:, :nt], trw[:, m, :nt], ALU.add, R=[s5k, "trw%d" % m], W="mg")
        for s in range((nt + 127) // 128):
            xt = xts[nx % 2]; xk = "xt%d" % (nx % 2); nx += 1
            rows = slice(t0 + s * 128, t0 + s * 128 + P)
            ph.dma("sp", xt[:P, :], I["xall"][rows, :], W=xk)
            for half in range(2):
                pb = pm[npm % 6]; pk = "pm%d" % (npm % 6); npm += 1
                for k in range(8):
                    ph.mm(pb[:P, :], mg[:, k, s * 128:s * 128 + P], wo[:, k, half * 512:(half + 1) * 512], k == 0, k == 7,
                          R=["mg", "wo"], W=pk)
                ph.tt(V, xt[:P, half * 512:(half + 1) * 512], xt[:P, half * 512:(half + 1) * 512], pb[:P, :], ALU.add,
                      R=[pk, xk], W=xk)
            ph.dma("pool", I["X1"][rows, :], xt[:P, :], R=xk)
    ph.finish()


def phase4(nc, I, G0, WFI):
    ph = Ph(nc, "p4")
    V = "dve"
    G = norm_scratch(ph, G0)
    identf = G0["identf"]
    wfi = WFI; wfo = ph.sb("wfo", [128, 22, D], BF16)
    for k in range(22):
        ph.dma("pool", wfo[:, k, :], I["w_ffn_out"][k * 128:(k + 1) * 128, :], W="wfo")
    g2c = ph.sb("g2c", [128, 8], F32); load_col(ph, g2c[:], I["ln2_g"], 8, "g2c")
    cw = ph.sb("cw", [128, 3, 22], F32); cb = ph.sb("cb", [128, 22], F32)
    ph.dma("sp", cw[:], I["conv_w"].rearrange("t (f p) -> p t f", p=128), W="cw", slow=True)
    load_col(ph, cb[:], I["conv_b"], 22, "cb")
    hT = ph.sb("hT", [128, 8, 512], BF16)
    hid = ph.sb("hid", [128, 22, 512], BF16)
    xts = [ph.sb("xt%d" % i, [128, D], F32) for i in range(2)]
    At = [ph.sb("At%d" % i, [128, 514], F32) for i in range(2)]
    acc = [ph.sb("acc%d" % i, [128, 512], F32) for i in range(2)]
    cc = ph.sb("cc", [128, 22, 2], F32)
    ph.memset(V, cc[:].rearrange("p a b -> p (a b)"), 0.0, W="cc")
    pm = [ph.ps("pm%d" % i, [128, 512], F32) for i in range(6)]
    scs = ph.sb("scs", [NS, 2816], F32)
    scT = ph.sb("scT", [128, 22, 2, NS], F32)
    aout = scs
    npm = na = 0
    for (t0, nt) in BLOCKS:
        P = min(128, nt)
        sample = nt < 128
        nsub = (nt + 127) // 128
        for s in range(nsub):
            rows = slice(t0 + s * 128, t0 + s * 128 + P)
            ph.dma("sp", xts[s % 2][:P, :], I["X1"][rows, :], W="xt%d" % (s % 2))
            rms_to_hT(ph, G, xts[s % 2], P, g2c, hT, s * 128, str(s % 2), "g2c")
        if sample:
            for tt_ in range(2):
                ph.dma("sp", scs[:], I["st_conv"][:, tt_, :], W="scs")
                for q in range(6):
                    pb = pm[npm % 6]; pk = "pm%d" % (npm % 6); npm += 1
                    fs = list(range(q * 4, min(22, q * 4 + 4)))
                    for j, f_ in enumerate(fs):
                        ph.tr(pb[:, j * NS:(j + 1) * NS], scs[:, f_ * 128:(f_ + 1) * 128], identf[:NS, :NS], R="scs", W=pk)
                    ph.cp(V, scT[:, fs[0]:fs[-1] + 1, tt_, :], pb[:, 0:len(fs) * NS].rearrange("p (a b) -> p a b", b=NS),
                          R=pk, W="scT")
        for f in range(22):
            pa = pm[npm % 6]; pka = "pm%d" % (npm % 6); npm += 1
            pb = pm[npm % 6]; pkb = "pm%d" % (npm % 6); npm += 1
            for k in range(8):
                ph.mm(pa[:, :nt], wfi[:, k, f * 128:(f + 1) * 128], hT[:, k, :nt], k == 0, k == 7, R=["wfi", "hT"], W=pka)
            for k in range(8):
                ph.mm(pb[:, :nt], wfi[:, k, 2816 + f * 128:2816 + (f + 1) * 128], hT[:, k, :nt], k == 0, k == 7,
                      R=["wfi", "hT"], W=pkb)
            A = At[na % 2]; ak = "At%d" % (na % 2); ac = acc[na % 2]; ck = "acc%d" % (na % 2); na += 1
            ph.cp("act", A[:, 2:2 + nt], pa[:, :nt], R=pka, W=ak)
            if not sample:
                ph.cp(V, A[:, 0:2], cc[:, f, :], R="cc", W=ak)
                a0, a1, a2 = A[:, 0:nt], A[:, 1:1 + nt], A[:, 2:2 + nt]
            else:
                a0, a1, a2 = scT[:, f, 0, :], scT[:, f, 1, :], A[:, 2:2 + nt]
            ph.ts(V, ac[:, :nt], a0, cw[:, 0, f:f + 1], ALU.mult, cb[:, f:f + 1], ALU.add, R=[ak, "scT", "cw", "cb"], W=ck)
            ph.stt(ac[:, :nt], a1, cw[:, 1, f:f + 1], ac[:, :nt], ALU.mult, ALU.add, R=[ak, "scT", "cw", ck], W=ck)
            ph.stt(ac[:, :nt], a2, cw[:, 2, f:f + 1], ac[:, :nt], ALU.mult, ALU.add, R=[ak, "cw", ck], W=ck)
            ph.act(ac[:, :nt], ac[:, :nt], AF.Gelu_apprx_tanh, R=ck, W=ck)
            ph.tt(V, hid[:, f, :nt], ac[:, :nt], pb[:, :nt], ALU.mult, R=[ck, pkb], W="hid")
            if not sample:
                ph.cp(V, cc[:, f, :], A[:, nt:nt + 2], R=ak, W="cc")
            else:
                po = pm[npm % 6]; pko = "pm%d" % (npm % 6); npm += 1
                ph.tr(po[:NS, 0:128], A[:, 2:2 + NS], identf[:], R=ak, W=pko)
                ph.cp(V, aout[:, f * 128:(f + 1) * 128], po[:NS, 0:128], R=pko, W="scs")
        if t0 + nt == T:
            for tt_ in range(2):
                ph.dma("sp", I["p_conv"][tt_].rearrange("(f p) -> p f", p=128), cc[:, :, tt_], R="cc", slow=True)
        if sample:
            ph.dma("sp", I["s_conv"][:, 1, :], aout[:], R="scs")
            ph.dma("act", I["s_conv"][:, 0, :], I["st_conv"][:, 1, :])
        for s in range(nsub):
            rows = slice(t0 + s * 128, t0 + s * 128 + P)
            xt = xts[s % 2]; xk = "xt%d" % (s % 2)
            ph.dma("sp", xt[:P, :], I["X1"][rows, :], W=xk)
            for half in range(2):
                pb = pm[npm % 6]; pk = "pm%d" % (npm % 6); npm += 1
                for f in range(22):
                    ph.mm(pb[:P, :], hid[:, f, s * 128:s * 128 + P], wfo[:, f, half * 512:(half + 1) * 512], f == 0, f == 21,
                          R=["hid", "wfo"], W=pk)
                ph.tt(V, xt[:P, half * 512:(half + 1) * 512], xt[:P, half * 512:(half + 1) * 512], pb[:P, :],
                      ALU.add, R=[pk, xk], W=xk)
            ph.dma("pool", I["X2"][rows, :], xt[:P, :], R=xk)
    ph.finish()


def phase5(nc, I, G0):
    ph = Ph(nc, "p5")
    V = "dve"
    Ga = norm_scratch(ph, G0, "a")
    Gb = norm_scratch(ph, G0, "b", eps=Ga["eps"])
    wpg = ph.sb("wpg", [128, 8, D], BF16); wpl = ph.sb("wpl", [128, 2, D], BF16)
    for k in range(8):
        ph.dma("pool", wpg[:, k, :], I["w_ple_gate"][k * 128:(k + 1) * 128, :], W="wpg")
    for k in range(2):
        ph.dma("pool", wpl[:, k, :], I["w_ple"][k * 128:(k + 1) * 128, :], W="wpl")
    g3c = ph.sb("g3c", [128, 8], F32); load_col(ph, g3c[:], I["ln3_g"], 8, "g3c")
    fg = ph.sb("fg", [128, D], F32)
    ph.dma("sp", fg[:], I["final_g"].partition_broadcast(128), W="fg")
    hTs = [ph.sb("hT%d" % i, [128, 8, 128], BF16) for i in range(2)]
    xts = [ph.sb("xt%d" % i, [128, D], F32) for i in range(2)]
    pball = ph.sb("pball", [128, 17, 256], BF16)
    _i = 0
    for (t0_, nt_) in BLOCKS:
        P_ = min(128, nt_)
        for s_ in range((nt_ + 127) // 128):
            ph.dma("pool", pball[:P_, _i, :], I["pall"][t0_ + s_ * 128:t0_ + s_ * 128 + P_, :], W="pb%d" % _i)
            _i += 1
    pTss = [ph.sb("pTs%d" % i, [128, 2, 128], BF16) for i in range(2)]
    sg = [ph.sb("sg%d" % i, [128, 512], F32) for i in range(2)]
    yo = [ph.sb("yo%d" % i, [128, D], F32) for i in range(2)]
    pm = [ph.ps("pm%d" % i, [128, 512], F32) for i in range(4)]
    pqs = [ph.ps("pq%d" % i, [128, 8, 128], BF16) for i in range(2)]
    npm = nx = nsg = 0
    for (t0, nt) in BLOCKS:
        P = min(128, nt)
        for s in range((nt + 127) // 128):
            rows = slice(t0 + s * 128, t0 + s * 128 + P)
            i2 = nx % 2; nx += 1
            xt = xts[i2]; xk = "xt%d" % i2; pbt = pball[:, nx - 1, :]; pbk = "pb%d" % (nx - 1)
            G = (Ga, Gb)[i2]; hT = hTs[i2]; hk = "hT%d" % i2; pTs = pTss[i2]; ptk = "pTs%d" % i2
            pq = pqs[i2]; pqk = "pq%d" % i2
            ph.dma("sp", xt[:P, :], I["X2"][rows, :], W=xk)
            rms_to_hT(ph, G, xt, P, g3c, hT, 0, str(i2), "g3c", hk)
            for k in range(2):
                ph.tr(pq[:, k, :P], pbt[:P, k * 128:(k + 1) * 128], G0["identb"][:P, :P], R=[pbk, "identb"], W=pqk)
            ph.cp("act", pTs[:, :, :P], pq[:, 0:2, :P], R=pqk, W=ptk)
            for half in range(2):
                cs_ = slice(half * 512, (half + 1) * 512)
                pg = pm[npm % 4]; pgk = "pm%d" % (npm % 4); npm += 1
                pe = pm[npm % 4]; pek = "pm%d" % (npm % 4); npm += 1
                for k in range(8):
                    ph.mm(pg[:P, :], hT[:, k, :P], wpg[:, k, cs_], k == 0, k == 7, R=[hk, "wpg"], W=pgk)
                for k in range(2):
                    ph.mm(pe[:P, :], pTs[:, k, :P], wpl[:, k, cs_], k == 0, k == 1, R=[ptk, "wpl"], W=pek)
                sgt = sg[nsg % 2]; sgk = "sg%d" % (nsg % 2); nsg += 1
                ph.act(sgt[:P, :], pg[:P, :], AF.Sigmoid, R=pgk, W=sgk)
                ph.tt(V, sgt[:P, :], sgt[:P, :], pe[:P, :], ALU.mult, R=[sgk, pek], W=sgk)
                ph.tt(V, xt[:P, cs_], xt[:P, cs_], sgt[:P, :], ALU.add, R=[sgk, xk, "xn" + G["sx"]], W=xk)
            ss = G["ss"]; sq = G["sq"]; kss = "ss" + G["sx"]; ksq = "sq" + G["sx"]
            ph.act(sq[:P, :], xt[:P, :], AF.Square, R=xk, W=[ksq, kss], accum=ss[:P, 0:1])
            ph.act(ss[:P, 1:2], ss[:P, 0:1], AF.Sqrt, R=[kss, "eps"], W=kss, bias=G["eps"][:P, 0:1], scale=1.0 / D)
            ph.op(V, lambda e, ss=ss, P=P: e.reciprocal(out=ss[:P, 3:4], in_=ss[:P, 1:2]), R=kss, W=kss + "3")
            y = yo[i2]; yk = "yo%d" % i2
            ph.stt(y[:P, :], xt[:P, :], ss[:P, 3:4], fg[:P, :], ALU.mult, ALU.mult, R=[xk, kss + "3", "fg"], W=yk)
            ph.dma("pool", I["y"][rows, :], y[:P, :], R=yk)
    ph.finish()


_CACHE = {}


def _consts():
    i = np.arange(128)
    c = {}
    c["c_ident"] = np.eye(128, dtype=np.float32)
    c["c_msl"] = (i[None, :] < i[:, None]).astype(np.float32)
    c["c_msu"] = (i[:, None] < i[None, :]).astype(np.float32)
    c["c_mui"] = (i[:, None] <= i[None, :]).astype(np.float32)
    c["c_blk64"] = ((i[:, None] // 64) == (i[None, :] // 64)).astype(np.float32)
    c["c_blk32"] = ((i[:, None] // 32) == (i[None, :] // 32)).astype(np.float32)
    c["c_rowgp"] = (((i[:, None] // 16) % 2) == (i[None, :] // 64)).astype(np.float32)
    return c


def make_in_maps(inp):
    f = lambda a: np.ascontiguousarray(np.asarray(a, dtype=np.float32))
    cst = _consts()
    shared = {}
    for k in ("ln1_g", "w_in", "mu_shift", "w0", "w2", "a0", "a2", "g2", "k_k", "k_a", "lnx_g", "lnx_b", "w_rw_out",
              "A_re", "A_im", "log_dt", "B_re", "B_im", "D_skip", "w_glu", "w_out", "ln2_g", "w_ffn_in", "conv_w",
              "conv_b", "w_ffn_out", "ln3_g", "w_ple_gate", "w_ple"):
        shared[k] = f(inp[k])[0]
    shared["r_k"] = f(inp["r_k"])[0].reshape(512)
    shared["C_re"] = f(inp["C_re"])[0].reshape(512, 64)
    shared["C_im"] = f(inp["C_im"])[0].reshape(512, 64)
    shared["final_g"] = f(inp["final_g"])
    shared.update(cst)
    xp, xs = f(inp["x_prompt"]), f(inp["x_sample"])
    pp, psm = f(inp["p_prompt"])[0], f(inp["p_sample"])[0]
    in_maps = []
    for c in range(8):
        sl = slice(NS * c, NS * c + NS)
        m = dict(shared)
        m["xall"] = np.concatenate([xp[c], xs[sl, 0]], 0)
        m["pall"] = np.concatenate([pp[c], psm[sl, 0]], 0)
        m["st_shift"] = f(inp["state_shift"])[0, sl]
        m["st_wkv"] = f(inp["state_wkv"])[0, sl].reshape(128, 4096)
        m["st_re"] = f(inp["state_ssm_re"])[0, sl].reshape(NS, 2048)
        m["st_im"] = f(inp["state_ssm_im"])[0, sl].reshape(NS, 2048)
        m["st_conv"] = f(inp["state_conv"])[0, sl]
        in_maps.append({k: np.ascontiguousarray(v) for k, v in m.items()})
    return in_maps


def kernel(**inp):
    f = lambda a: np.ascontiguousarray(np.asarray(a, dtype=np.float32))
    if "nc" not in _CACHE:
        _CACHE["nc"] = build_program()
    nc = _CACHE["nc"]
    in_maps = make_in_maps(inp)
    res = run_bass_kernel_spmd(nc, in_maps, core_ids=list(range(8)))
    R = res.results
    cat = lambda fn: np.stack([fn(r) for r in R], 0)
    y_prompt = cat(lambda r: r["y"][:T])
    y_sample = np.concatenate([r["y"][T:] for r in R], 0)[:, None, :]
    p_shift = cat(lambda r: r["p_shift"])[None]
    p_wkv = cat(lambda r: r["p_wkv"].reshape(8, 64, 64).transpose(0, 2, 1))[None]
    p_re = cat(lambda r: r["p_re"].reshape(32, 64))[None]
    p_im = cat(lambda r: r["p_im"].reshape(32, 64))[None]
    p_conv = cat(lambda r: r["p_conv"])[None]
    s_shift = np.concatenate([r["s_shift"] for r in R], 0)[None]
    s_wkv = np.concatenate([r["s_wkv"].reshape(NS, 8, 64, 64) for r in R], 0)[None]
    s_re = np.concatenate([r["s_re"].reshape(NS, 32, 64) for r in R], 0)[None]
    s_im = np.concatenate([r["s_im"].reshape(NS, 32, 64) for r in R], 0)[None]
    s_conv = np.concatenate([r["s_conv"] for r in R], 0)[None]
    outs = (y_prompt, y_sample, p_shift, p_wkv, p_re, p_im, p_conv, s_shift, s_wkv, s_re, s_im, s_conv)
    return tuple(np.ascontiguousarray(o.astype(np.float32)) for o in outs)
```

```python
import contextlib
import math
import numpy as np
import concourse.bass as bass
import concourse.mybir as mybir
from concourse.bass_utils import run_bass_kernel_spmd

F32 = mybir.dt.float32
BF16 = mybir.dt.bfloat16
AF = mybir.ActivationFunctionType
ALU = mybir.AluOpType
AX = mybir.AxisListType

T = 2048
NS = 16
NT = T + NS
D = 1024
CS = 8
SCAN_ENG = "pool"
C1 = math.exp(-0.5)
BLOCKS = [(0, 512), (512, 512), (1024, 512), (1536, 512), (2048, 16)]

ENGS = ("pe", "act", "dve", "pool", "sp")
NDSEM = 12


class _Op:
    __slots__ = ("eng", "fn", "deps", "dma", "observed", "tok", "idx", "dslot")

    def __init__(self, eng, fn, dma):
        self.eng, self.fn, self.dma = eng, fn, dma
        self.deps = set()
        self.observed = False
        self.tok = None
        self.dslot = None


class Sched:
    def __init__(self, nc):
        self.nc = nc
        self.ops = []
        self.last_w = {}
        self.readers = {}
        self.dma_rr = {e: 0 for e in ENGS}
        self.dma_prev = {}
        self.excl = set()

    def _add(self, eng, fn, reads, writes, dma):
        op = _Op(eng, fn, dma)
        op.idx = len(self.ops)
        if self.excl:
            ex = tuple(b for b in reads if b in self.excl)
            if ex:
                writes = tuple(writes) + ex
        for b in reads:
            w = self.last_w.get(b)
            if w is not None:
                op.deps.add(w)
        for b in writes:
            w = self.last_w.get(b)
            if w is not None:
                op.deps.add(w)
            for r in self.readers.get(b, ()):
                op.deps.add(r)
        if dma:
            slot = (eng, self.dma_rr[eng] % NDSEM)
            self.dma_rr[eng] += 1
            op.dslot = slot
            prev = self.dma_prev.get(slot)
            if prev is not None:
                op.deps.add(prev)
            self.dma_prev[slot] = op.idx
        op.deps.discard(op.idx)
        self.ops.append(op)
        for b in writes:
            self.last_w[b] = op.idx
            self.readers[b] = []
        for b in reads:
            if b not in writes:
                self.readers.setdefault(b, []).append(op.idx)
        return op.idx

    def emit(self):
        nc = self.nc
        ops = self.ops
        need = []
        for op in ops:
            nd = []
            for d in op.deps:
                p = ops[d]
                if (not p.dma) and (not op.dma) and p.eng == op.eng == "pe":
                    continue
                nd.append(d)
                p.observed = True
            need.append(nd)
        last = {}
        for op in ops:
            key = op.dslot if op.dma else op.eng
            last[key] = op.idx
        for i in last.values():
            ops[i].observed = True
        g = getattr(nc, "_gsem", None)
        if g is None:
            g = {"sems": {}, "cnt": {e: 0 for e in ENGS}, "dcnt": {}}
            nc._gsem = g
        cnt = g["cnt"]
        dcnt = g["dcnt"]
        for op in ops:
            if op.dma:
                dcnt[op.dslot] = dcnt.get(op.dslot, 0) + 16
                op.tok = (op.dslot, dcnt[op.dslot])
            elif op.observed:
                cnt[op.eng] += 1
                op.tok = (op.eng, cnt[op.eng])
        sems = g["sems"]
        for k in list(ENGS) + sorted(set(o.dslot for o in ops if o.dma)):
            if k not in sems:
                nm = k if isinstance(k, str) else "d_%s_%d" % k
                sems[k] = nc.alloc_semaphore(name="s_" + nm)
        with contextlib.ExitStack() as st:
            block = st.enter_context(nc.Block())
            per = {e: [o for o in ops if o.eng == e] for e in ENGS}
            hw = {"pe": block.tensor, "act": block.scalar, "dve": block.vector,
                  "pool": block.gpsimd, "sp": block.sync}

            def make(e):
                def body(eng):
                    seen = {}
                    for op in per[e]:
                        waits = {}
                        for d in need[op.idx]:
                            k, v = ops[d].tok
                            if v > waits.get(k, 0):
                                waits[k] = v
                        for k, v in waits.items():
                            if seen.get(k, 0) >= v:
                                continue
                            seen[k] = v
                            eng.wait_ge(sems[k], v)
                        ins = op.fn(eng)
                        if op.dma:
                            ins.then_inc(sems[op.tok[0]], 16)
                        elif op.observed:
                            ins.then_inc(sems[e], 1)
                    if e == "sp":
                        for key, i in last.items():
                            k, v = ops[i].tok
                            if seen.get(k, 0) < v:
                                eng.wait_ge(sems[k], v)
                return body

            for e in ENGS:
                hw[e](make(e))


def _L(x):
    if x is None:
        return ()
    if isinstance(x, str):
        return (x,)
    return tuple(x)


class Ph:
    _uid = [0]

    def __init__(self, nc, tag):
        self.nc = nc
        self.tag = tag
        self.st = contextlib.ExitStack()
        self.S = Sched(nc)

    def sb(self, name, shape, dt):
        return self.st.enter_context(self.nc.sbuf_tensor(self.tag + "_" + name, list(shape), dt))

    def ps(self, name, shape, dt):
        self.S.excl.add(name)
        return self.st.enter_context(self.nc.psum_tensor(self.tag + "_" + name, list(shape), dt))

    def finish(self):
        self.S.emit()
        self.st.close()

    def dbg(self, name, ap, shape, key, dt=F32):
        import os
        if os.environ.get("K_DBG_DUMP", "") == "":
            return
        t = self.nc.dram_tensor("dbg_" + name, list(shape), dt, kind="ExternalOutput").ap()
        self.dma("sp", t, ap, R=key)

    _rec = None

    def rec_begin(self):
        self._rec = []

    def rec_end(self):
        r, self._rec = self._rec, None
        return r

    def merge(self, *streams, spans=None):
        if spans is None:
            spans = [(0.0, 1.0)] * len(streams)
        keep = [i for i, st_ in enumerate(streams) if st_]
        spans = [spans[i] for i in keep]
        streams = [streams[i] for i in keep]
        pos = [0] * len(streams)
        out = []
        while True:
            best, bi = None, -1
            for i, st_ in enumerate(streams):
                if pos[i] < len(st_):
                    f = spans[i][0] + spans[i][1] * (pos[i] + 1.0) / len(st_)
                    if best is None or f < best:
                        best, bi = f, i
            if bi < 0:
                break
            out.append(streams[bi][pos[bi]])
            pos[bi] += 1
        return out

    def play(self, *streams, spans=None):
        for eng, fn, R, W, dma in self.merge(*streams, spans=spans):
            self.S._add(eng, fn, R, W, dma)

    def op(self, eng, fn, R=None, W=None):
        if self._rec is not None:
            self._rec.append((eng, fn, _L(R), _L(W), False))
        else:
            self.S._add(eng, fn, _L(R), _L(W), False)

    def dma(self, q, out, in_, R=None, W=None, slow=False):
        if slow:
            fn = lambda e: e.dma_start(out=out, in_=in_, allow_slow_non_contiguous=True)
        else:
            fn = lambda e: e.dma_start(out=out, in_=in_)
        if self._rec is not None:
            self._rec.append((q, fn, _L(R), _L(W), True))
        else:
            self.S._add(q, fn, _L(R), _L(W), True)

    def tt(self, eng, out, in0, in1, op, R=None, W=None):
        self.op(eng, lambda e: e.tensor_tensor(out=out, in0=in0, in1=in1, op=op), R, W)

    def ts(self, eng, out, in0, s1, op0, s2=None, op1=None, R=None, W=None):
        if op1 is None:
            self.op(eng, lambda e: e.tensor_scalar(out=out, in0=in0, scalar1=s1, scalar2=None, op0=op0), R, W)
        else:
            self.op(eng, lambda e: e.tensor_scalar(out=out, in0=in0, scalar1=s1, scalar2=s2, op0=op0, op1=op1), R, W)

    def stt(self, out, in0, scalar, in1, op0, op1, R=None, W=None):
        self.op("dve", lambda e: e.scalar_tensor_tensor(out=out, in0=in0, scalar=scalar, in1=in1, op0=op0, op1=op1), R, W)

    def act(self, out, in_, func, R=None, W=None, bias=None, scale=1.0, accum=None):
        kw = {}
        if bias is not None:
            kw["bias"] = bias
        if accum is not None:
            kw["accum_out"] = accum
        self.op("act", lambda e: e.activation(out=out, in_=in_, func=func, scale=scale, **kw), R, W)

    def cp(self, eng, out, in_, R=None, W=None):
        if eng == "act":
            self.op("act", lambda e: e.activation(out=out, in_=in_, func=AF.Copy), R, W)
        else:
            self.op(eng, lambda e: e.tensor_copy(out=out, in_=in_), R, W)

    def mm(self, out, lhsT, rhs, start, stop, R=None, W=None, tp=None):
        if tp is None:
            self.op("pe", lambda e: e.matmul(out, lhsT=lhsT, rhs=rhs, start=start, stop=stop), R, W)
        else:
            self.op("pe", lambda e: e.matmul(out, lhsT=lhsT, rhs=rhs, start=start, stop=stop, tile_position=tp), R, W)

    def tr(self, out, in_, ident, R=None, W=None):
        self.op("pe", lambda e: e.transpose(out, in_, ident), R, W)

    def memset(self, eng, ap, v, W=None):
        self.op(eng, lambda e: e.memset(ap, v), None, W)


def bc(ap, shape):
    return ap.to_broadcast(list(shape))


def rms_to_hT(ph, G, xt, P, gcol, hT, c0, tag, gkey, hkey="hT"):
    sq, ss, xn, pT = G["sq"], G["ss"], G["xn"], G["pT"]
    x_ = G.get("sx", "")
    ksq, kss, kxn, kpT = "sq" + x_, "ss" + x_, "xn" + x_, "pT" + x_
    ph.act(sq[:P, :], xt[:P, :], AF.Square, R="xt" + tag, W=[ksq, kss], accum=ss[:P, 0:1])
    ph.act(ss[:P, 1:2], ss[:P, 0:1], AF.Sqrt, R=[kss, "eps"], W=kss, bias=G["eps"][:P, 0:1], scale=1.0 / D)
    ph.op("dve", lambda e: e.reciprocal(out=ss[:P, 2:3], in_=ss[:P, 1:2]), R=kss, W=kss)
    ph.ts("dve", xn[:P, :], xt[:P, :], ss[:P, 2:3], ALU.mult, R=["xt" + tag, kss], W=kxn)
    for k in range(8):
        ph.tr(pT[:, k, :P], xn[:P, k * 128:(k + 1) * 128], G["identb"][:P, :P], R=[kxn, "identb"], W=kpT)
    ph.tt("dve", hT[:, :, c0:c0 + P], pT[:, :, :P], bc(gcol[:, :].unsqueeze(2), [128, 8, P]), ALU.mult,
          R=[kpT, gkey], W=hkey)


def load_col(ph, dst, src1d, n, key):
    ph.dma("sp", dst, src1d.rearrange("(k p) -> p k", p=128), W=key, slow=True)


def norm_scratch(ph, G0, sx="", eps=None):
    G = dict(G0)
    G["sx"] = sx
    G["sq"] = ph.sb("sq" + sx, [128, D], F32)
    G["ss"] = ph.sb("ss" + sx, [128, 4], F32)
    G["xn"] = ph.sb("xn" + sx, [128, D], BF16)
    G["pT"] = ph.ps("pT" + sx, [128, 8, 128], BF16)
    if eps is None:
        G["eps"] = ph.sb("eps", [128, 1], F32)
        ph.memset("dve", G["eps"][:], 1e-6, W="eps")
    else:
        G["eps"] = eps
    return G


def build_program(upto=9, debug=False):
    nc = bass.Bass("TRN2", target_bir_lowering=False)
    I = {}

    def inp(name, shape, dt=F32):
        I[name] = nc.dram_tensor(name, list(shape), dt, kind="ExternalInput").ap()

    def outp(name, shape):
        I[name] = nc.dram_tensor(name, list(shape), F32, kind="ExternalOutput").ap()

    def scratch(name, shape, dt):
        if debug:
            I[name] = nc.dram_tensor(name, list(shape), dt, kind="ExternalOutput").ap()
        else:
            I[name] = nc.dram_tensor(name, list(shape), dt).ap()
    if debug:
        scratch("d_BwT", [128, 4 * CS * 2 * 128], BF16); scratch("d_Kmat", [128, 4 * CS * 128], BF16)
        scratch("d_CwT", [128, CS * 2 * 16 * 32], BF16); scratch("d_Abar", [128, 64], F32)

    inp("xall", [NT, D]); inp("pall", [NT, 256])
    inp("st_shift", [NS, 1792]); inp("st_wkv", [128, 4096]); inp("st_re", [NS, 2048]); inp("st_im", [NS, 2048])
    inp("st_conv", [NS, 2, 2816])
    inp("ln1_g", [D]); inp("w_in", [D, 4352]); inp("mu_shift", [1792]); inp("w0", [512]); inp("w2", [64, 512])
    inp("a0", [512]); inp("a2", [64, 512]); inp("g2", [128, 512]); inp("k_k", [512]); inp("k_a", [512])
    inp("r_k", [512]); inp("lnx_g", [512]); inp("lnx_b", [512]); inp("w_rw_out", [512, D])
    inp("A_re", [32, 64]); inp("A_im", [32, 64]); inp("log_dt", [32]); inp("B_re", [32, 64, 16]); inp("B_im", [32, 64, 16])
    inp("C_re", [512, 64]); inp("C_im", [512, 64]); inp("D_skip", [512]); inp("w_glu", [512, 2048]); inp("w_out", [D, D])
    inp("ln2_g", [D]); inp("w_ffn_in", [D, 5632]); inp("conv_w", [3, 2816]); inp("conv_b", [2816]); inp("w_ffn_out", [2816, D])
    inp("ln3_g", [D]); inp("w_ple_gate", [D, D]); inp("w_ple", [256, D]); inp("final_g", [D])
    inp("c_ident", [128, 128]); inp("c_msl", [128, 128]); inp("c_msu", [128, 128]); inp("c_mui", [128, 128])
    inp("c_blk64", [128, 128]); inp("c_blk32", [128, 128]); inp("c_rowgp", [128, 128])
    outp("y", [NT, D]); outp("p_shift", [1792]); outp("p_wkv", [512, 64]); outp("p_re", [2048]); outp("p_im", [2048])
    outp("p_conv", [2, 2816]); outp("s_shift", [NS, 1792]); outp("s_wkv", [128, 4096]); outp("s_re", [NS, 2048])
    outp("s_im", [NS, 2048]); outp("s_conv", [NS, 2, 2816])
    scratch("PRW", [1792, NT], F32); scratch("UU", [512, NT], F32); scratch("GT", [2048, NT], BF16)
    scratch("YF", [512, NT], BF16); scratch("ZZ", [512, NT], BF16); scratch("X1", [NT, D], F32); scratch("X2", [NT, D], F32)
    scratch("SW", [6, NS, 512], F32); scratch("SY", [128, 64], F32)

    with contextlib.ExitStack() as gst:
        def gsb(name, shape, dt):
            return gst.enter_context(nc.sbuf_tensor("g_" + name, list(shape), dt))
        G0 = {}
        G0["identb"] = gsb("identb", [128, 128], BF16)
        G0["identf"] = gsb("identf", [128, 128], F32)
        with contextlib.ExitStack() as g2:
            def g2sb(name, shape, dt):
                return g2.enter_context(nc.sbuf_tensor("g_" + name, list(shape), dt))
            G0["BwT"] = g2sb("BwT", [128, 4, CS, 2, 128], BF16)
            G0["Kmat"] = g2sb("Kmat", [128, 4, CS, 128], BF16)
            G0["CwT"] = g2sb("CwT", [128, CS, 2, 16, 32], BF16)
            G0["Abar"] = g2sb("Abar", [128, 2, 2, 16], F32)
            if upto >= 1:
                phase1(nc, I, G0, debug)
            else:
                phase0(nc, I, G0, debug)
            if upto >= 2:
                phase2(nc, I, G0, True)
            if upto >= 2.5:
                phase2(nc, I, G0, False)
        g4 = contextlib.ExitStack()
        WFI = g4.enter_context(nc.sbuf_tensor("g_wfi", [128, 8, 5632], BF16))
        if upto >= 3:
            phase3(nc, I, G0, None, WFI)
        if upto >= 4:
            phase4(nc, I, G0, WFI)
        g4.close()
        if upto >= 5:
            phase5(nc, I, G0)
    return nc


def phase0(nc, I, G0, debug=False, ph=None):
    own = ph is None
    if own:
        ph = Ph(nc, "p0")
        ph.dma("pool", G0["identb"][:], I["c_ident"], W="identb")
        ph.dma("sp", G0["identf"][:], I["c_ident"], W="identf")
    sb = ph.sb
    lr = sb("lr", [128, 16], F32); li = sb("li", [128, 16], F32); dtl = sb("dtl", [128, 16], F32)
    Bre = sb("Bre", [128, 16, 16], F32); Bim = sb("Bim", [128, 16, 16], F32)
    ph.dma("sp", lr[:], I["A_re"].rearrange("(P gp) n -> (gp n) P", gp=2), W="lr", slow=True)
    ph.dma("sp", li[:], I["A_im"].rearrange("(P gp) n -> (gp n) P", gp=2), W="li", slow=True)
    ldt2 = I["log_dt"].rearrange("(P gp) -> gp P", gp=2)
    for gp in range(2):
        ph.dma("sp", dtl[64 * gp:64 * gp + 64, :], ldt2[gp].partition_broadcast(64), W="dtl", slow=True)
    ph.dma("sp", Bre[:], I["B_re"].rearrange("(P gp) n c -> (gp n) P c", gp=2), W="Bre")
    ph.dma("sp", Bim[:], I["B_im"].rearrange("(P gp) n c -> (gp n) P c", gp=2), W="Bim")
    rowgp = sb("rowgp", [128, 128], F32); blk32 = sb("blk32", [128, 128], F32)
    ph.dma("sp", rowgp[:], I["c_rowgp"], W="rowgp"); ph.dma("sp", blk32[:], I["c_blk32"], W="blk32")
    CT = [sb("CTr", [128, 4, 128], F32), sb("CTi", [128, 4, 128], F32)]
    c2 = sb("c2", [128, 128], F32)
    pA = ph.ps("pA", [128, 4, 128], F32)
    for ri, nm in enumerate(("C_re", "C_im")):
        for k in range(4):
            src = I[nm][k * 128:(k + 1) * 128, :]
            ph.dma("sp", c2[:, 0:64], src, W="c2"); ph.dma("sp", c2[:, 64:128], src, W="c2")
            ph.tt("dve", c2[:], c2[:], rowgp[:], ALU.mult, R=["c2", "rowgp"], W="c2")
            ph.tr(pA[:, k, :], c2[:], G0["identf"][:], R=["c2", "identf"], W="pA")
        ph.cp("dve", CT[ri][:], pA[:], R="pA", W="CT%d" % ri)
    t = {n: sb(n, [128, 16], F32) for n in ("dt", "e1", "mag", "ang", "sa", "ca", "sinv", "cosv", "ar", "ai", "den",
                                             "rden", "am1", "fr", "fi", "t1", "t2")}
    V = "dve"
    K = lambda *n: list(n)
    hpi = sb("hpi", [128, 1], F32)
    ph.memset(V, hpi[:], math.pi / 2, W="hpi")
    ph.act(t["dt"][:], dtl[:], AF.Exp, R="dtl", W="dt")
    ph.tt(V, t["e1"][:], lr[:], t["dt"][:], ALU.mult, R=K("lr", "dt"), W="e1")
    ph.act(t["mag"][:], t["e1"][:], AF.Exp, R="e1", W="mag")
    ph.tt(V, t["ang"][:], li[:], t["dt"][:], ALU.mult, R=K("li", "dt"), W="ang")
    ph.ts(V, t["sa"][:], t["ang"][:], 1.0 / 64, ALU.mult, R="ang", W="sa")
    ph.act(t["sinv"][:], t["sa"][:], AF.Sin, R="sa", W="sinv")
    ph.act(t["cosv"][:], t["sa"][:], AF.Sin, R=["sa", "hpi"], W="cosv", bias=hpi[:, 0:1])
    for _ in range(6):
        ph.tt(V, t["t1"][:], t["cosv"][:], t["cosv"][:], ALU.mult, R="cosv", W="t1")
        ph.tt(V, t["t2"][:], t["sinv"][:], t["sinv"][:], ALU.mult, R="sinv", W="t2")
        ph.stt(t["sinv"][:], t["cosv"][:], 2.0, t["sinv"][:], ALU.mult, ALU.mult, R=["cosv", "sinv", "t2"], W="sinv")
        ph.tt(V, t["cosv"][:], t["t1"][:], t["t2"][:], ALU.subtract, R=["t1", "t2", "sinv"], W="cosv")
    ph.tt(V, t["ar"][:], t["mag"][:], t["cosv"][:], ALU.mult, R=K("mag", "cosv"), W="ar")
    ph.tt(V, t["ai"][:], t["mag"][:], t["sinv"][:], ALU.mult, R=K("mag", "sinv"), W="ai")
    ph.tt(V, t["den"][:], lr[:], lr[:], ALU.mult, R="lr", W="den")
    ph.tt(V, t["t1"][:], li[:], li[:], ALU.mult, R="li", W="t1")
    ph.tt(V, t["den"][:], t["den"][:], t["t1"][:], ALU.add, R=K("den", "t1"), W="den")
    ph.op(V, lambda e: e.reciprocal(out=t["rden"][:], in_=t["den"][:]), R="den", W="rden")
    ph.ts(V, t["am1"][:], t["ar"][:], -1.0, ALU.add, R="ar", W="am1")
    ph.tt(V, t["t1"][:], t["am1"][:], lr[:], ALU.mult, R=K("am1", "lr", "den"), W="t1")
    ph.tt(V, t["t2"][:], t["ai"][:], li[:], ALU.mult, R=K("ai", "li"), W="t2")
    ph.tt(V, t["t1"][:], t["t1"][:], t["t2"][:], ALU.add, R=K("t1", "t2"), W="t1")
    ph.tt(V, t["fr"][:], t["t1"][:], t["rden"][:], ALU.mult, R=K("t1", "rden"), W="fr")
    ph.tt(V, t["t1"][:], t["ai"][:], lr[:], ALU.mult, R=K("ai", "lr", "fr"), W="t1")
    ph.tt(V, t["t2"][:], t["am1"][:], li[:], ALU.mult, R=K("am1", "li"), W="t2")
    ph.tt(V, t["t1"][:], t["t1"][:], t["t2"][:], ALU.subtract, R=K("t1", "t2"), W="t1")
    ph.tt(V, t["fi"][:], t["t1"][:], t["rden"][:], ALU.mult, R=K("t1", "rden"), W="fi")
    pwr = sb("pwr", [128, CS + 1, 16], F32); pwi = sb("pwi", [128, CS + 1, 16], F32)
    ph.memset(V, pwr[:, 0, :], 1.0, W="pw"); ph.memset(V, pwi[:, 0, :], 0.0, W="pw")
    for e in range(CS):
        ph.tt(V, t["t1"][:], pwr[:, e, :], t["ar"][:], ALU.mult, R=K("pw", "ar", "fi"), W="t1")
        ph.tt(V, t["t2"][:], pwi[:, e, :], t["ai"][:], ALU.mult, R=K("pw", "ai"), W="t2")
        ph.tt(V, pwr[:, e + 1, :], t["t1"][:], t["t2"][:], ALU.subtract, R=K("t1", "t2"), W="pw")
        ph.tt(V, t["t1"][:], pwr[:, e, :], t["ai"][:], ALU.mult, R=K("pw", "ai"), W="t1")
        ph.tt(V, t["t2"][:], pwi[:, e, :], t["ar"][:], ALU.mult, R=K("pw", "ar"), W="t2")
        ph.tt(V, pwi[:, e + 1, :], t["t1"][:], t["t2"][:], ALU.add, R=K("t1", "t2"), W="pw")
    Ab = G0["Abar"]
    ph.cp(V, Ab[:, 0, 0, :], pwr[:, CS, :], R="pw", W="Abar"); ph.cp(V, Ab[:, 0, 1, :], pwi[:, CS, :], R="pw", W="Abar")
    ph.cp(V, Ab[:, 1, 0, :], pwr[:, 1, :], R="pw", W="Abar"); ph.cp(V, Ab[:, 1, 1, :], pwi[:, 1, :], R="pw", W="Abar")
    bbr = sb("bbr", [128, 16, 16], F32); bbi = sb("bbi", [128, 16, 16], F32)
    u1 = sb("u1", [128, 16, 16], F32); u2 = sb("u2", [128, 16, 16], F32)
    frb = bc(t["fr"][:, :].unsqueeze(2), [128, 16, 16]); fib = bc(t["fi"][:, :].unsqueeze(2), [128, 16, 16])
    ph.tt(V, u1[:], Bre[:], frb, ALU.mult, R=K("Bre", "fr"), W="u1")
    ph.tt(V, u2[:], Bim[:], fib, ALU.mult, R=K("Bim", "fi"), W="u2")
    ph.tt(V, bbr[:], u1[:], u2[:], ALU.subtract, R=K("u1", "u2"), W="bbr")
    ph.tt(V, u1[:], Bim[:], frb, ALU.mult, R=K("Bim", "fr", "bbr"), W="u1")
    ph.tt(V, u2[:], Bre[:], fib, ALU.mult, R=K("Bre", "fi", "bbr"), W="u2")
    ph.tt(V, bbi[:], u1[:], u2[:], ALU.add, R=K("u1", "u2"), W="bbi")
    Ew = sb("Ew", [128, CS, 2, 16, 2, 16], F32)
    ph.memset(V, Ew[:].rearrange("p a b c d e -> p (a b c d e)"), 0.0, W="Ew")
    for e in range(CS):
        pr = bc(pwr[:, e, :].unsqueeze(2), [128, 16, 16]); pi = bc(pwi[:, e, :].unsqueeze(2), [128, 16, 16])
        ph.tt(V, u1[:], bbr[:], pr, ALU.mult, R=K("bbr", "pw", "Ew"), W="u1")
        ph.tt(V, u2[:], bbi[:], pi, ALU.mult, R=K("bbi", "pw", "Ew"), W="u2")
        ph.tt(V, u1[:], u1[:], u2[:], ALU.subtract, R=K("u1", "u2"), W="u1")
        for gp in range(2):
            ph.cp(V, Ew[64 * gp:64 * gp + 64, e, 0, :, gp, :], u1[64 * gp:64 * gp + 64, :, :], R="u1", W="Ew")
        ph.tt(V, u1[:], bbr[:], pi, ALU.mult, R=K("bbr", "pw", "Ew"), W="u1")
        ph.tt(V, u2[:], bbi[:], pr, ALU.mult, R=K("bbi", "pw", "Ew"), W="u2")
        ph.tt(V, u1[:], u1[:], u2[:], ALU.add, R=K("u1", "u2"), W="u1")
        for gp in range(2):
            ph.cp(V, Ew[64 * gp:64 * gp + 64, e, 1, :, gp, :], u1[64 * gp:64 * gp + 64, :, :], R="u1", W="Ew")
    CTin = sb("CTin", [128, 4, 128], F32)
    ph.ts(V, CTin[:], CT[1][:], -1.0, ALU.mult, R="CT1", W="CTin")
    pB = [ph.ps("pB%d" % i, [128, 4, 128], F32) for i in range(2)]
    n = 0
    for j in range(CS):
        e = CS - 1 - j
        for ri in range(2):
            pb = pB[n % 2]; n += 1
            for k in range(4):
                src = Ew[:, e, ri, 4 * k:4 * k + 4, :, :].rearrange("p a b c -> p (a b c)")
                ph.tr(pb[:, k, :], src, G0["identf"][:], R=["Ew", "identf"], W="pB%d" % ((n - 1) % 2))
            ph.cp("act" if n % 2 else "dve", G0["BwT"][:, :, j, ri, :], pb[:], R="pB%d" % ((n - 1) % 2), W="BwT")
    for tau in range(CS):
        pb = pB[n % 2]; key = "pB%d" % (n % 2); n += 1
        for k in range(4):
            lr_ = Ew[:, tau, 0, 4 * k:4 * k + 4, :, :].rearrange("p a b c -> p (a b c)")
            li_ = Ew[:, tau, 1, 4 * k:4 * k + 4, :, :].rearrange("p a b c -> p (a b c)")
            ph.mm(pb[:, k, :], lr_, CT[0][:, k, :], True, False, R=["Ew", "CT0"], W=key)
            ph.mm(pb[:, k, :], li_, CTin[:, k, :], False, True, R=["Ew", "CTin"], W=key)
        ph.tt(V, G0["Kmat"][:, :, tau, :], pb[:], bc(blk32[:, :].unsqueeze(1), [128, 4, 128]), ALU.mult,
              R=[key, "blk32"], W="Kmat")
    w1 = sb("w1", [128, 16, 32], F32); w2_ = sb("w2", [128, 16, 32], F32)
    CTr3 = CT[0][:].rearrange("p k (a b) -> p (k a) b", a=4); CTi3 = CT[1][:].rearrange("p k (a b) -> p (k a) b", a=4)
    for i in range(CS):
        pr = bc(pwr[:, i + 1, :].unsqueeze(2), [128, 16, 32]); pi = bc(pwi[:, i + 1, :].unsqueeze(2), [128, 16, 32])
        ph.tt(V, w1[:], CTr3, pr, ALU.mult, R=K("CT0", "pw", "CwT"), W="w1")
        ph.tt(V, w2_[:], CTi3, pi, ALU.mult, R=K("CT1", "pw", "CwT"), W="w2")
        ph.tt(V, G0["CwT"][:, i, 0, :, :], w1[:], w2_[:], ALU.subtract, R=K("w1", "w2"), W="CwT")
        ph.tt(V, w1[:], CTr3, pi, ALU.mult, R=K("CT0", "pw", "CwT"), W="w1")
        ph.tt(V, w2_[:], CTi3, pr, ALU.mult, R=K("CT1", "pw", "CwT"), W="w2")
        ph.tt(V, w1[:], w1[:], w2_[:], ALU.add, R=K("w1", "w2"), W="w1")
        ph.ts(V, G0["CwT"][:, i, 1, :, :], w1[:], -1.0, ALU.mult, R="w1", W="CwT")
    if debug:
        ph.dma("sp", I["d_BwT"], G0["BwT"][:].rearrange("p a b c d -> p (a b c d)"), R="BwT")
        ph.dma("sp", I["d_Kmat"], G0["Kmat"][:].rearrange("p a b c -> p (a b c)"), R="Kmat")
        ph.dma("sp", I["d_CwT"], G0["CwT"][:].rearrange("p a b c d -> p (a b c d)"), R="CwT")
        ph.dma("sp", I["d_Abar"], G0["Abar"][:].rearrange("p a b c -> p (a b c)"), R="Abar")
    if own:
        ph.finish()


def phase1(nc, I, G0, debug=False):
    ph = Ph(nc, "p1")
    win = ph.sb("win", [128, 8, 4352], BF16)
    for k in range(8):
        ph.dma("pool", win[:, k, :], I["w_in"][k * 128:(k + 1) * 128, :], W="win%d" % k)
    ph.dma("pool", G0["identb"][:], I["c_ident"], W="identb")
    ph.dma("sp", G0["identf"][:], I["c_ident"], W="identf")
    ph.rec_begin()
    phase0(nc, I, G0, debug, ph=ph)
    s0 = ph.rec_end()
    ph.rec_begin()
    G = norm_scratch(ph, G0)
    g1c = ph.sb("g1c", [128, 8], F32)
    load_col(ph, g1c[:], I["ln1_g"], 8, "g1c")
    hTs = [ph.sb("hT%d" % i, [128, 8, 512], BF16) for i in range(2)]
    xts = [ph.sb("xt%d" % i, [128, D], F32) for i in range(2)]
    pm = [ph.ps("pm%d" % i, [128, 512], F32) for i in range(4)]
    stf = [ph.sb("stf%d" % i, [128, 512], F32) for i in range(4)]
    stb = [ph.sb("stb%d" % i, [128, 512], BF16) for i in range(3)]
    WK = ["win%d" % k for k in range(8)]
    nx = nf = nb = npm = 0
    pre1 = ph.rec_end()
    NR1, MM1 = [], []
    for bi_, (t0, nt) in enumerate(BLOCKS):
        P = min(128, nt)
        hT = hTs[bi_ % 2]; hk = "hT%d" % (bi_ % 2)
        ph.rec_begin()
        for s in range((nt + 127) // 128):
            xt = xts[nx % 2]; tg = str(nx % 2); nx += 1
            ph.dma("sp", xt[:P, :], I["xall"][t0 + s * 128:t0 + s * 128 + P, :], W="xt" + tg)
            rms_to_hT(ph, G, xt, P, g1c, hT, s * 128, tg, "g1c", hk)
        NR1.append(ph.rec_end())
        ph.rec_begin()
        for m in range(34):
            pb = pm[npm % 4]; pk = "pm%d" % (npm % 4); npm += 1
            for k in range(8):
                ph.mm(pb[:, :nt], win[:, k, m * 128:(m + 1) * 128], hT[:, k, :nt], k == 0, k == 7,
                      R=["win%d" % k, hk], W=pk)
            if m < 18:
                sf = stf[nf % 4]; sk = "stf%d" % (nf % 4); nf += 1
                ph.cp("dve" if m % 2 else "act", sf[:, :nt], pb[:, :nt], R=pk, W=sk)
                if m < 14:
                    ph.dma("pool", I["PRW"][m * 128:(m + 1) * 128, t0:t0 + nt], sf[:, :nt], R=sk)
                else:
                    ph.dma("pool", I["UU"][(m - 14) * 128:(m - 13) * 128, t0:t0 + nt], sf[:, :nt], R=sk)
            else:
                sbf = stb[nb % 3]; sk = "stb%d" % (nb % 3); nb += 1
                ph.act(sbf[:, :nt], pb[:, :nt], AF.Sigmoid, R=pk, W=sk)
                ph.dma("act", I["GT"][(m - 18) * 128:(m - 17) * 128, t0:t0 + nt], sbf[:, :nt], R=sk)
        MM1.append(ph.rec_end())
    s1 = pre1 + NR1[0]
    for b_ in range(len(BLOCKS)):
        s1 = s1 + ph.merge(MM1[b_], NR1[b_ + 1] if b_ + 1 < len(BLOCKS) else [])
    ph.play(s1, s0)
    ph.finish()


def alloc_w3(nc, st):
    t = lambda n, shp: st.enter_context(nc.sbuf_tensor("w3_" + n, shp, BF16))
    return {"rwo": t("rwo", [128, 4, D]), "glu": t("glu", [128, 4, 2048]), "wo": t("wo", [128, 8, D])}


def load_w3(ph, I, W3):
    for k in range(4):
        ph.dma("pool", W3["rwo"][:, k, :], I["w_rw_out"][k * 128:(k + 1) * 128, :], W="rwo")
        ph.dma("pool", W3["glu"][:, k, :], I["w_glu"][k * 128:(k + 1) * 128, :], W="glu")
    for k in range(8):
        ph.dma("pool", W3["wo"][:, k, :], I["w_out"][k * 128:(k + 1) * 128, :], W="wo")


def phase2(nc, I, G0, prompt, W3=None):
    ph = Ph(nc, "p2a" if prompt else "p2b")
    sb, ps = ph.sb, ph.ps
    V = "dve"
    if W3 is not None:
        load_w3(ph, I, W3)
    ph._s5tmp = [sb("s5a", [128, 2, 16], F32), sb("s5b", [128, 2, 16], F32)]
    ph._s5xb = sb("Xb", [128, 2, 16, 64], BF16)
    ph._s5du = sb("s5du", [128, 512], F32)
    if prompt:
        msl = sb("msl", [128, 128], BF16); msu = sb("msu", [128, 128], BF16); mui = sb("mui", [128, 128], BF16)
        ph.dma("pool", msl[:], I["c_msl"], W="msl"); ph.dma("pool", msu[:], I["c_msu"], W="msu")
        ph.dma("pool", mui[:], I["c_mui"], W="mui")
    blk64 = sb("blk64", [128, 128], F32); ph.dma("sp", blk64[:], I["c_blk64"], W="blk64")
    w2a2 = sb("w2a2", [128, 512], BF16); g2b = sb("g2b", [128, 512], BF16)
    ph.dma("pool", w2a2[0:64, :], I["w2"], W="w2a2"); ph.dma("pool", w2a2[64:128, :], I["a2"], W="w2a2")
    ph.dma("pool", g2b[:], I["g2"], W="g2b")
    pc = {}
    for nm, n in (("mu_shift", 14), ("w0", 4), ("a0", 4), ("k_k", 4), ("k_a", 4), ("r_k", 4), ("lnx_g", 4),
                  ("lnx_b", 4), ("D_skip", 4)):
        pc[nm] = sb("c_" + nm, [128, n], F32)
        load_col(ph, pc[nm][:], I[nm], n, "c_" + nm)
    PK = ["c_" + k for k in pc]
    scm = sb("scm", [128, 4, 128], F32)
    ph.memset(V, scm[:].rearrange("p a b -> p (a b)"), 1.0, W="scm"); ph.memset(V, scm[:, :, 0:1], 0.0, W="scm")
    eps_gn = sb("eps_gn", [128, 1], F32); ph.memset(V, eps_gn[:], 64e-5, W="eps_gn")
    if prompt:
        Sst = sb("Sst", [128, 4, 64], F32); Sbd = sb("Sbd", [128, 4, 128], BF16)
        ph.memset(V, Sst[:].rearrange("p a b -> p (a b)"), 0.0, W="Sst")
        ph.memset(V, Sbd[:].rearrange("p a b -> p (a b)"), 0.0, W="Sbd")
        Xs = sb("Xs", [128, 2, 16, 65], F32)
        ph.memset(V, Xs[:].rearrange("p a b c -> p (a b c)"), 0.0, W="Xs")
        Pf = sb("Pf", [128, 14, 513], F32)
        ph.memset(V, Pf[:, :, 0:1], 0.0, W="Pf")
    WB = 512 if prompt else NS
    WC = 128 if prompt else NS
    uf = sb("uf", [128, 4, WB], F32); ub = sb("ub", [128, 4, WB], BF16)
    YFb = sb("YFb", [128, 4, WB], BF16); ZZb = sb("ZZb", [128, 4, WB], BF16)
    f4 = lambda n: sb(n, [128, 4, WC], F32)
    b4 = lambda n: sb(n, [128, 4, WC], BF16)
    XS = sb("XS", [128, 14, WC], F32); dd = sb("dd", [128, 14, WC], F32)
    lin = sb("lin", [128, WC], BF16); sgx = sb("sgx", [128, WC], BF16)
    sig = f4("sig"); aa = f4("aa"); gg = f4("gg"); kk0 = f4("kk0"); tq = f4("tq"); rn = f4("rn"); kkn = f4("kkn")
    bb = f4("bb"); kmod = f4("kmod"); bon = f4("bon"); cs = f4("cs"); ex1 = f4("ex1"); ex2 = f4("ex2"); ex3 = f4("ex3")
    nbias = sb("nbias", [128, 4], F32); PCt = sb("PCt", [128, 4], F32)
    gns = f4("gns")
    KX = {n_: n_ for n_ in ("rT", "kT", "bT", "aT", "khT", "bhT", "vT", "PCt", "bon", "gg")}
    if prompt:
        rT = b4("rT"); kT = b4("kT"); bT = b4("bT"); aT = b4("aT"); khT = b4("khT"); bhT = b4("bhT"); vT = b4("vT")
        alt = {"rT": b4("rT1"), "kT": b4("kT1"), "bT": b4("bT1"), "aT": b4("aT1"), "khT": b4("khT1"),
               "bhT": b4("bhT1"), "vT": b4("vT1"), "PCt": sb("PCt1", [128, 4], F32), "bon": f4("bon1"), "gg": f4("gg1")}
        Vtok = sb("Vtok", [128, 512], BF16); Khtok = sb("Khtok", [128, 512], BF16); Bhtok = sb("Bhtok", [128, 512], BF16)
        h8 = lambda n: sb(n, [128, 8, 128], BF16)
        Nb = [h8("Nb0"), h8("Nb1")]; Lb = [h8("Lb0"), h8("Lb1")]; Mt = [h8("Mt0"), h8("Mt1")]
        LKb = h8("LKb"); Arb = h8("Arb"); Ark = h8("Ark")
        Wbf = sb("Wbf", [128, 512], BF16); Ubf = sb("Ubf", [128, 512], BF16)
        tS = sb("tS", [128, 4, 64], F32)
    Ysb = sb("Ysb", [128, 8, 64], F32); Ysq = sb("Ysq", [128, 8, 64], F32); ynb = sb("ynb", [128, 8, 64], BF16)
    gn = sb("gn", [128, 6, 8], F32)
    pF = [ps("pF%d" % i, [128, 4, 128], F32) for i in range(6)]
    pT = [ps("pTb%d" % i, [128, 8, 128], BF16) for i in range(2)]
    cnt = {"f": 0, "t": 0}

    def getF():
        i = cnt["f"] % 6; cnt["f"] += 1
        return pF[i], "pF%d" % i

    def mkpool(base):
        st_ = {"n": 0}

        def get():
            i = base + st_["n"] % 2; st_["n"] += 1
            return pF[i], "pF%d" % i
        return get
    getF_prep, getF_core, getFs = mkpool(0), mkpool(2), mkpool(4)

    def getT():
        i = cnt["t"] % 2; cnt["t"] += 1
        return pT[i], "pTb%d" % i

    ib = G0["identb"]

    if not prompt:
        sample_mixer(ph, I, G0, locals())
        ph.finish()
        return
    Lbase = dict(locals())
    Lpar = [dict(Lbase), dict(Lbase)]
    Lpar[1].update(alt)
    Lpar[1]["KX"] = {n_: n_ + "1" for n_ in KX}
    REC = []
    for bi, (t0, nt) in enumerate(BLOCKS[:4]):
        ph.rec_begin()
        if bi > 0:
            ph.cp(V, Pf[:, :, 0:1], Pf[:, :, 512:513], R="Pf", W="Pf")
        ph.dma("sp", Pf[:, :, 1:513], I["PRW"][:, t0:t0 + nt].rearrange("(m p) t -> p m t", p=128), W="Pf")
        if bi == 3:
            ph.dma("sp", I["p_shift"].rearrange("(m p) -> p m", p=128), Pf[:, :, 512], R="Pf", slow=True)
        hdr_pf = ph.rec_end()
        ph.rec_begin()
        ph.dma("act", uf[:], I["UU"][:, t0:t0 + nt].rearrange("(m p) t -> p m t", p=128), W="uf")
        ph.cp("act", ub[:].rearrange("p a b -> p (a b)"), uf[:].rearrange("p a b -> p (a b)"), R="uf", W="ub")
        hdr_ub = ph.rec_end()
        ph.rec_begin()
        s5_block(ph, I, G0, pc, Xs, ub, ZZb, getFs, nchunk=64, which=0, ncol=512)
        ph.dma("act", I["ZZ"][:, t0:t0 + nt].rearrange("(m p) t -> p m t", p=128), ZZb[:], R="ZZb")
        s5s = ph.rec_end()
        m0, m1 = ph._s5marks
        preps, cores = [], []
        for c in range(4):
            c0 = c * 128
            Lc = dict(Lpar[c % 2]); Lc["getF"] = getF_prep
            Lk = dict(Lpar[c % 2]); Lk["getF"] = getF_core
            ph.rec_begin()
            ph.tt(V, dd[:], Pf[:, :, c0:c0 + 128], Pf[:, :, c0 + 1:c0 + 129], ALU.subtract, R="Pf", W="dd")
            ph.tt(V, dd[:], dd[:], bc(pc["mu_shift"][:, :].unsqueeze(2), [128, 14, 128]), ALU.mult,
                  R=["dd", "c_mu_shift"], W="dd")
            ph.tt(V, XS[:], dd[:], Pf[:, :, c0 + 1:c0 + 129], ALU.add, R=["dd", "Pf"], W="XS")
            rwkv_prep_and_core(ph, Lc, c, c0)
            preps.append(ph.rec_end())
            ph.rec_begin()
            wkv_core(ph, Lk, c, c0)
            cores.append(ph.rec_end())
        ph.rec_begin()
        ph.dma("pool", I["YF"][:, t0:t0 + nt].rearrange("(m p) t -> p m t", p=128), YFb[:], R="YFb")
        yfst = ph.rec_end()
        hs = (m1 - m0) // 2
        REC.append(dict(hdr_pf=hdr_pf, hdr_ub=hdr_ub, SG=s5s[:m0], SS1=s5s[m0:m0 + hs], SS2=s5s[m0 + hs:m1],
                        SY=s5s[m1:], preps=preps, cores=cores, yfst=yfst))
    ph.play(REC[0]["hdr_pf"])
    ph.play(REC[0]["preps"][0])
    for bi in range(4):
        Rb = REC[bi]
        ph.play(Rb["hdr_ub"])
        ph.play(Rb["cores"][0], Rb["preps"][1], Rb["SG"])
        ph.play(Rb["cores"][1], Rb["preps"][2], Rb["SS1"])
        ph.play(Rb["cores"][2], Rb["preps"][3], Rb["SS2"])
        if bi < 3:
            ph.play(REC[bi + 1]["hdr_pf"])
            ph.play(Rb["cores"][3], Rb["SY"], REC[bi + 1]["preps"][0])
        else:
            ph.play(Rb["cores"][3], Rb["SY"])
        ph.play(Rb["yfst"])
    ph.dma("sp", I["p_wkv"].rearrange("(m p) v -> p m v", p=128), Sst[:], R="Sst")
    ph.dma("sp", I["p_re"].rearrange("(P p) -> p P", p=128), Xs[:, 0, :, 0], R="Xs", slow=True)
    ph.dma("sp", I["p_im"].rearrange("(P p) -> p P", p=128), Xs[:, 1, :, 0], R="Xs", slow=True)
    ph.finish()


def rwkv_prep_and_core(ph, L, c, c0):
    V = "dve"
    PV = L.get("PV", "dve")
    KX = L["KX"]
    pc = L["pc"]; XS = L["XS"]; getF = L["getF"]; getT = L["getT"]; ib = L["ib"]
    sig, aa, gg, kk0, tq, rn, kkn = L["sig"], L["aa"], L["gg"], L["kk0"], L["tq"], L["rn"], L["kkn"]
    bb, kmod, bon, cs, ex1, ex2, ex3 = L["bb"], L["kmod"], L["bon"], L["cs"], L["ex1"], L["ex2"], L["ex3"]
    rT, kT, bT, aT, khT, bhT, vT = L["rT"], L["kT"], L["bT"], L["aT"], L["khT"], L["bhT"], L["vT"]
    lin, sgx, w2a2, g2b, blk64 = L["lin"], L["sgx"], L["w2a2"], L["g2b"], L["blk64"]
    nbias, PCt, scm = L["nbias"], L["PCt"], L["scm"]
    r_ = XS[:, 0:4, :]; k_ = XS[:, 4:8, :]; v_ = XS[:, 8:12, :]
    B4 = lambda t: bc(t[:, :].unsqueeze(2), [128, 4, 128])
    fl = lambda t: t[:].rearrange("p a b -> p (a b)")
    ph.act(lin[0:64, :], XS[0:64, 12, :], AF.Tanh, R="XS", W="lin")
    ph.cp("act", lin[64:128, :], XS[64:128, 12, :], R="XS", W="lin")
    ph.act(sgx[:], XS[:, 13, :], AF.Sigmoid, R="XS", W="sgx")
    pw_, kw_ = getF()
    for m in range(4):
        ph.mm(pw_[:, m, :], w2a2[0:64, m * 128:(m + 1) * 128], lin[0:64, :], True, True, R=["w2a2", "lin"], W=kw_)
    for m in range(4):
        ph.act(sig[:, m, :], pw_[:, m, :], AF.Sigmoid, R=[kw_, "c_w0"], W="sig", bias=pc["w0"][:, m:m + 1])
    pa_, ka_ = getF()
    for m in range(4):
        ph.mm(pa_[:, m, :], w2a2[64:128, m * 128:(m + 1) * 128], lin[64:128, :], True, True, R=["w2a2", "lin"], W=ka_)
    for m in range(4):
        ph.act(aa[:, m, :], pa_[:, m, :], AF.Sigmoid, R=[ka_, "c_a0"], W="aa", bias=pc["a0"][:, m:m + 1])
    pg_, kg_ = getF()
    for m in range(4):
        ph.mm(pg_[:, m, :], g2b[:, m * 128:(m + 1) * 128], sgx[:], True, True, R=["g2b", "sgx"], W=kg_)
    ph.cp("act", gg[:], pg_[:], R=kg_, W=KX["gg"])
    ph.tt(PV, kk0[:], k_, B4(pc["k_k"]), ALU.mult, R=["XS", "c_k_k"], W="kk0")
    ph.tt(PV, tq[:], kk0[:], kk0[:], ALU.mult, R="kk0", W="tq")
    pq, kq = getF()
    for m in range(4):
        ph.mm(pq[:, m, :], blk64[:], tq[:, m, :], True, True, R=["blk64", "tq"], W=kq)
    ph.act(rn[:], pq[:], AF.Sqrt, R=kq, W="rn")
    ph.ts(V, rn[:], rn[:], 1e-12, ALU.max, R="rn", W="rn")
    ph.op(V, lambda e: e.reciprocal(out=fl(rn), in_=fl(rn)), R="rn", W="rn")
    ph.tt(PV, kkn[:], kk0[:], rn[:], ALU.mult, R=["kk0", "rn"], W="kkn")
    ph.tt(PV, bb[:], kkn[:], aa[:], ALU.mult, R=["kkn", "aa"], W="bb")
    ph.tt(PV, tq[:], aa[:], B4(pc["k_a"]), ALU.mult, R=["aa", "c_k_a", kq], W="tq")
    ph.tt(PV, tq[:], tq[:], B4(pc["k_a"]), ALU.subtract, R=["tq", "c_k_a"], W="tq")
    ph.stt(kmod[:], tq[:], 1.0, k_, ALU.add, ALU.mult, R=["tq", "XS"], W="kmod")
    ph.tt(PV, tq[:], r_, kmod[:], ALU.mult, R=["XS", "kmod"], W="tq")
    ph.tt(PV, tq[:], tq[:], B4(pc["r_k"]), ALU.mult, R=["tq", "c_r_k"], W="tq")
    pq2, kq2 = getF()
    for m in range(4):
        ph.mm(pq2[:, m, :], blk64[:], tq[:, m, :], True, True, R=["blk64", "tq"], W=kq2)
    ph.tt(V, bon[:], pq2[:], v_, ALU.mult, R=[kq2, "XS"], W=KX["bon"])
    ph.op(V, lambda e: e.tensor_tensor_scan(out=fl(cs), data0=fl(scm), data1=fl(sig), initial=0.0, op0=ALU.mult,
                                             op1=ALU.add), R=["scm", "sig"], W="cs")
    ph.ts(V, nbias[:], cs[:, :, 127], -C1, ALU.mult, R="cs", W="nbias")
    ph.act(PCt[:], nbias[:], AF.Exp, R="nbias", W=KX["PCt"])
    ph.act(ex1[:], cs[:], AF.Exp, R="cs", W="ex1", scale=-C1)
    ph.tt(PV, rT[:], r_, ex1[:], ALU.mult, R=["XS", "ex1"], W=KX["rT"])
    ph.act(ex2[:], cs[:], AF.Exp, R="cs", W="ex2", scale=C1)
    ph.tt(PV, kT[:], kmod[:], ex2[:], ALU.mult, R=["kmod", "ex2"], W=KX["kT"])
    ph.tt(PV, bT[:], bb[:], ex2[:], ALU.mult, R=["bb", "ex2"], W=KX["bT"])
    ph.tt(PV, ex3[:], cs[:], sig[:], ALU.subtract, R=["cs", "sig"], W="ex3")
    ph.act(ex3[:], ex3[:], AF.Exp, R="ex3", W="ex3", scale=-C1)
    ph.stt(aT[:], kkn[:], -1.0, ex3[:], ALU.mult, ALU.mult, R=["kkn", "ex3"], W=KX["aT"])
    for m in range(4):
        ph.act(ex1[:, m, :], cs[:, m, :], AF.Exp, R=["cs", "nbias", KX["rT"]], W="ex1", bias=nbias[:, m:m + 1], scale=C1)
    ph.tt(PV, khT[:], kmod[:], ex1[:], ALU.mult, R=["kmod", "ex1"], W=KX["khT"])
    ph.tt(PV, bhT[:], bb[:], ex1[:], ALU.mult, R=["bb", "ex1"], W=KX["bhT"])
    ph.cp("act", vT[:], v_, R="XS", W=KX["vT"])


def wkv_core(ph, L, c, c0):
    V = "dve"
    KX = L["KX"]
    getF = L["getF"]; getT = L["getT"]; ib = L["ib"]
    rT, kT, bT, aT, khT, bhT, vT = L["rT"], L["kT"], L["bT"], L["aT"], L["khT"], L["bhT"], L["vT"]
    Vtok, Khtok, Bhtok = L["Vtok"], L["Khtok"], L["Bhtok"]
    Nb, Lb, Mt, LKb, Arb, Ark = L["Nb"], L["Lb"], L["Mt"], L["LKb"], L["Arb"], L["Ark"]
    msl, msu, mui = L["msl"], L["msu"], L["mui"]
    Wbf, Ubf, Ysb, Ysq, ynb, gn = L["Wbf"], L["Ubf"], L["Ysb"], L["Ysq"], L["ynb"], L["gn"]
    Sst, Sbd, PCt, tS = L["Sst"], L["Sbd"], L["PCt"], L["tS"]
    pc = L["pc"]; bon, gg, YFb = L["bon"], L["gg"], L["YFb"]
    M4 = lambda m_: bc(m_[:, :].unsqueeze(1), [128, 4, 128])
    pt, kt = getT()
    for m in range(4):
        ph.tr(pt[:, m, :], vT[:, m, :], ib[:], R=KX["vT"], W=kt)
    for m in range(4):
        ph.tr(pt[:, 4 + m, :], khT[:, m, :], ib[:], R=KX["khT"], W=kt)
    ph.cp("act", Vtok[:], pt[:, 0:4, :].rearrange("p a b -> p (a b)"), R=kt, W="Vtok")
    ph.cp(V, Khtok[:], pt[:, 4:8, :].rearrange("p a b -> p (a b)"), R=kt, W="Khtok")
    pt2, kt2 = getT()
    for m in range(4):
        ph.tr(pt2[:, m, :], bhT[:, m, :], ib[:], R=KX["bhT"], W=kt2)
    ph.cp("act", Bhtok[:], pt2[:, 0:4, :].rearrange("p a b -> p (a b)"), R=kt2, W="Bhtok")

    def hsl(t, h):
        return t[64 * (h % 2):64 * (h % 2) + 64, h // 2, :]

    def amat(dst, dkey, lhs, lkey, rhs, rkey, mask, mkey):
        for par in range(2):
            pb, pk = getF()
            for q in range(4):
                h = 2 * q + par
                ph.mm(pb[:, q, :], hsl(lhs, h), hsl(rhs, h), True, True, R=[lkey, rkey], W=pk)
            ph.tt(V, dst[:, par:8:2, :], pb[:], M4(mask), ALU.mult, R=[pk, mkey], W=dkey)

    amat(Nb[0], "Nb0", aT, KX["aT"], bT, KX["bT"], msl, "msl")
    amat(Lb[0], "Lb0", bT, KX["bT"], aT, KX["aT"], msu, "msu")
    amat(LKb, "LKb", kT, KX["kT"], aT, KX["aT"], msu, "msu")
    amat(Arb, "Arb", bT, KX["bT"], rT, KX["rT"], mui, "mui")
    amat(Ark, "Ark", kT, KX["kT"], rT, KX["rT"], mui, "mui")
    for half in range(2):
        ph.tt(V, Mt[0][:, half * 4:half * 4 + 4, :], Lb[0][:, half * 4:half * 4 + 4, :], M4(ib), ALU.add,
              R=["Lb0", "identb"], W="Mt0")
    cur = 0
    for lvl in range(6):
        nxt = 1 - cur
        for half in range(2):
            pb, pk = getF()
            for q in range(4):
                h = half * 4 + q
                ph.mm(pb[:, q, :], Lb[cur][:, h, :], Nb[cur][:, h, :], True, True, R=["Lb%d" % cur, "Nb%d" % cur], W=pk)
            ph.cp("act", Nb[nxt][:, half * 4:half * 4 + 4, :], pb[:], R=pk, W="Nb%d" % nxt)
        if lvl < 5:
            for half in range(2):
                pb, pk = getF()
                for q in range(4):
                    h = half * 4 + q
                    ph.mm(pb[:, q, :], Nb[cur][:, h, :], Lb[cur][:, h, :], True, True,
                          R=["Lb%d" % cur, "Nb%d" % cur], W=pk)
                ph.cp("act", Lb[nxt][:, half * 4:half * 4 + 4, :], pb[:], R=pk, W="Lb%d" % nxt)
        for half in range(2):
            pb, pk = getF()
            for q in range(4):
                h = half * 4 + q
                ph.mm(pb[:, q, :], Nb[nxt][:, h, :], Mt[cur][:, h, :], True, True, R=["Nb%d" % nxt, "Mt%d" % cur], W=pk)
            ph.tt(V, Mt[nxt][:, half * 4:half * 4 + 4, :], pb[:], Mt[cur][:, half * 4:half * 4 + 4, :], ALU.add,
                  R=[pk, "Mt%d" % cur], W="Mt%d" % nxt)
        cur = nxt
    MtF = Mt[cur]; mk = "Mt%d" % cur
    def hcols(pb, h):
        return pb[:].rearrange("p a b -> p (a b)")[:, h * 64:h * 64 + 64]

    def pcols(pb, m):
        return pb[:].rearrange("p a b -> p (a b)")[:, m * 128:m * 128 + 128]

    pb, pk = getF()
    for m in range(4):
        ph.mm(pcols(pb, m), aT[:, m, :], Sbd[:, m, :], True, False, R=[KX["aT"], "Sbd"], W=pk)
        for hh in range(2):
            h = 2 * m + hh
            ph.mm(hcols(pb, h), LKb[:, h, :], Vtok[:, h * 64:h * 64 + 64], False, hh == 1, R=["LKb", "Vtok"], W=pk)
    ph.cp("act", Wbf[:], pb[:].rearrange("p a b -> p (a b)"), R=pk, W="Wbf")
    pb, pk = getF()
    for h in range(8):
        ph.mm(hcols(pb, h), MtF[:, h, :], Wbf[:, h * 64:h * 64 + 64], True, True, R=[mk, "Wbf"], W=pk)
    ph.cp("act", Ubf[:], pb[:].rearrange("p a b -> p (a b)"), R=pk, W="Ubf")
    pb, pk = getF()
    for m in range(4):
        ph.mm(pcols(pb, m), rT[:, m, :], Sbd[:, m, :], True, False, R=[KX["rT"], "Sbd"], W=pk)
        for hh in range(2):
            h = 2 * m + hh
            ph.mm(hcols(pb, h), Arb[:, h, :], Ubf[:, h * 64:h * 64 + 64], False, False, R=["Arb", "Ubf"], W=pk)
            ph.mm(hcols(pb, h), Ark[:, h, :], Vtok[:, h * 64:h * 64 + 64], False, hh == 1, R=["Ark", "Vtok"], W=pk)
    ph.cp("act", Ysb[:].rearrange("p a b -> p (a b)"), pb[:].rearrange("p a b -> p (a b)"), R=pk, W="Ysb")
    pS, kS = getF()
    for m in range(4):
        ph.mm(pS[:, m, :], Bhtok[:, m * 128:(m + 1) * 128], Ubf[:, m * 128:(m + 1) * 128], True, False,
              R=["Bhtok", "Ubf"], W=kS)
        ph.mm(pS[:, m, :], Khtok[:, m * 128:(m + 1) * 128], Vtok[:, m * 128:(m + 1) * 128], False, True,
              R=["Khtok", "Vtok"], W=kS)
    ph.tt(V, tS[:], Sst[:], bc(PCt[:, :].unsqueeze(2), [128, 4, 64]), ALU.mult, R=["Sst", KX["PCt"]], W="tS")
    for hh in range(2):
        rs = slice(64 * hh, 64 * hh + 64)
        ph.tt(V, Sst[rs, :, :], tS[rs, :, :], pS[rs, :, 64 * hh:64 * hh + 64], ALU.add, R=["tS", kS], W="Sst")
        ph.cp(V, Sbd[rs, :, 64 * hh:64 * hh + 64], Sst[rs, :, :], R="Sst", W="Sbd")
    groupnorm_out(ph, L, c0, 128)


def groupnorm_out(ph, L, c0, P):
    V = "dve"
    KX = L["KX"]
    Ysb, Ysq, ynb, gn = L["Ysb"], L["Ysq"], L["ynb"], L["gn"]
    pc = L["pc"]; bon, gg, YFb = L["bon"], L["gg"], L["YFb"]; getT = L["getT"]; ib = L["ib"]
    eps_gn = L["eps_gn"]; ex2 = L["gns"]
    ph.op(V, lambda e: e.tensor_reduce(out=gn[:P, 0, :], in_=Ysb[:P], axis=AX.X, op=ALU.add), R="Ysb", W="gn")
    ph.act(Ysq[:P].rearrange("p a b -> p (a b)"), Ysb[:P].rearrange("p a b -> p (a b)"), AF.Square, R="Ysb", W="Ysq")
    ph.op(V, lambda e: e.tensor_reduce(out=gn[:P, 1, :], in_=Ysq[:P], axis=AX.X, op=ALU.add), R="Ysq", W="gn")
    ph.ts(V, gn[:P, 2, :], gn[:P, 0, :], 1.0 / 64, ALU.mult, R="gn", W="gn")
    ph.tt(V, gn[:P, 3, :], gn[:P, 2, :], gn[:P, 2, :], ALU.mult, R="gn", W="gn")
    ph.stt(gn[:P, 4, :], gn[:P, 1, :], 1.0 / 64, gn[:P, 3, :], ALU.mult, ALU.subtract, R="gn", W="gn")
    ph.act(gn[:P, 4, :], gn[:P, 4, :], AF.Sqrt, R=["gn", "eps_gn"], W="gn", bias=eps_gn[:P, 0:1])
    ph.op(V, lambda e: e.reciprocal(out=gn[:P, 5, :], in_=gn[:P, 4, :]), R="gn", W="gn")
    ph.tt(V, Ysq[:P], Ysb[:P], bc(gn[:P, 2, :].unsqueeze(2), [P, 8, 64]), ALU.subtract, R=["Ysb", "gn"], W="Ysq")
    ph.tt(V, ynb[:P], Ysq[:P], bc(gn[:P, 5, :].unsqueeze(2), [P, 8, 64]), ALU.mult, R=["Ysq", "gn"], W="ynb")
    pt, kt = getT()
    for m in range(4):
        ph.tr(pt[:, m, :P], ynb[:P, 2 * m:2 * m + 2, :].rearrange("p a b -> p (a b)"), ib[:P, :P], R="ynb", W=kt)
    B4 = lambda t: bc(t[:, :].unsqueeze(2), [128, 4, P])
    t1 = ex2
    ph.tt(V, t1[:, :, :P], pt[:, 0:4, :P], B4(pc["lnx_g"]), ALU.mult, R=[kt, "c_lnx_g"], W="gns")
    ph.tt(V, t1[:, :, :P], t1[:, :, :P], B4(pc["lnx_b"]), ALU.add, R=["gns", "c_lnx_b"], W="gns")
    ph.tt(V, t1[:, :, :P], t1[:, :, :P], bon[:, :, :P], ALU.add, R=["gns", KX["bon"]], W="gns")
    ph.tt(V, YFb[:, :, c0:c0 + P], t1[:, :, :P], gg[:, :, :P], ALU.mult, R=["gns", KX["gg"]], W="YFb")


def s5_block(ph, I, G0, pc, Xs, ub, ZZb, getF, nchunk, which, ncol, step=CS, npos=CS):
    V = "dve"
    BwT, Kmat, CwT, Ab = G0["BwT"], G0["Kmat"], G0["CwT"], G0["Abar"]
    nm = nchunk
    assert nm * 8 <= 512
    for Pl in range(4):
        pb, pk = getF()
        flat = pb[:].rearrange("p a b -> p (a b)")
        for ri in range(2):
            for k in range(4):
                q = ri * 4 + k
                dst = flat[:, q * nm:(q + 1) * nm]
                for j in range(npos):
                    jj = (CS - npos) + j
                    rhs = ub[32 * Pl:32 * Pl + 32, k, j:j + (nm - 1) * step + 1:step]
                    ph.mm(dst, BwT[32 * Pl:32 * Pl + 32, k, jj, ri, :], rhs, j == 0, j == npos - 1,
                          R=["BwT", "ub"], W=pk, tp=((96, 0) if Pl == 3 else None))
        for ri in range(2):
            ph.cp(V, Xs[:, ri, Pl:16:4, 1:1 + nm],
                  flat[:, ri * 4 * nm:(ri + 1) * 4 * nm].rearrange("p (q m) -> p q m", m=nm), R=[pk], W="Xs")
    A_r = bc(Ab[:, which, 0, :].unsqueeze(1), [128, 2, 16]); A_i = bc(Ab[:, which, 1, :].unsqueeze(1), [128, 2, 16])
    ph._s5marks = [len(ph._rec) if ph._rec is not None else 0]
    tmpa = ph._s5tmp[0]; tmpb = ph._s5tmp[1]
    for m in range(nm):
        ph.tt(SCAN_ENG, tmpa[:], Xs[:, :, :, m], A_r, ALU.mult, R=["Xs", "Abar"], W="s5a")
        ph.tt(SCAN_ENG, tmpb[:], Xs[:, :, :, m], A_i, ALU.mult, R=["Xs", "Abar"], W="s5b")
        ph.tt(SCAN_ENG, Xs[:, :, :, m + 1], Xs[:, :, :, m + 1], tmpa[:], ALU.add, R=["Xs", "s5a"], W="Xs")
        ph.tt(SCAN_ENG, Xs[:, 0, :, m + 1], Xs[:, 0, :, m + 1], tmpb[:, 1, :], ALU.subtract, R=["Xs", "s5b"], W="Xs")
        ph.tt(SCAN_ENG, Xs[:, 1, :, m + 1], Xs[:, 1, :, m + 1], tmpb[:, 0, :], ALU.add, R=["Xs", "s5b"], W="Xs")
    ph._s5marks.append(len(ph._rec) if ph._rec is not None else 0)
    Xb = ph._s5xb
    ph.cp("act", Xb[:, :, :, 0:nm], Xs[:, :, :, 0:nm], R="Xs", W="Xb")
    for k in range(4):
        pb, pk = getF()
        flat = pb[:].rearrange("p a b -> p (a b)")
        for i in range(npos):
            dst = flat[:, i * nm:(i + 1) * nm]
            for tau in range(i + 1):
                rhs = ub[:, k, (i - tau):(i - tau) + (nm - 1) * step + 1:step]
                ph.mm(dst, Kmat[:, k, tau, :], rhs, tau == 0, False, R=["Kmat", "ub"], W=pk)
            for Pl in range(4):
                P_ = 4 * k + Pl
                for ri in range(2):
                    ph.mm(flat[32 * Pl:32 * Pl + 32, i * nm:(i + 1) * nm], CwT[:, i, ri, P_, :], Xb[:, ri, P_, 0:nm],
                          False, ri == 1, R=["CwT", "Xb"], W=pk, tp=(0, 32 * Pl))
        du = ph._s5du
        ph.ts(V, du[:, 0:ncol], ub[:, k, 0:ncol], pc["D_skip"][:, k:k + 1], ALU.mult, R=["ub", "c_D_skip", "s5z"], W="s5du")
        if npos == 1:
            ph.tt(V, du[:, 0:ncol], du[:, 0:ncol], flat[:, 0:nm], ALU.add, R=["s5du", pk], W="s5du")
        else:
            ph.tt(V, du[:, 0:ncol].rearrange("p (m i) -> p m i", i=npos), du[:, 0:ncol].rearrange("p (m i) -> p m i", i=npos),
                  flat[:, 0:npos * nm].rearrange("p (i m) -> p m i", m=nm), ALU.add, R=["s5du", pk], W="s5du")
        ph.act(ZZb[:, k, 0:ncol], du[:, 0:ncol], AF.Gelu_apprx_tanh, R="s5du", W=["ZZb", "s5z"])
    ph.cp(V, Xs[:, :, :, 0], Xs[:, :, :, nm], R="Xs", W="Xs")


def sample_mixer(ph, I, G0, L):
    V = "dve"
    sb = ph.sb
    pc = L["pc"]; getF, getT, ib = L["getF"], L["getT"], L["ib"]
    identf = G0["identf"]
    XS = L["XS"]; dd = L["dd"]
    t0 = T
    n = NS
    cur = sb("s_cur", [128, 14, NS], F32); prv = sb("s_prv", [128, 14, NS], F32)
    ph.dma("sp", cur[:], I["PRW"][:, t0:t0 + n].rearrange("(m p) t -> p m t", p=128), W="s_cur")
    sst = sb("s_sst", [NS, 1792], F32)
    ph.dma("sp", sst[:], I["st_shift"], W="s_sst")
    for half in range(4):
        pb, pk = getF()
        flat = pb[:].rearrange("p a b -> p (a b)")
        ms = list(range(half * 4, min(14, half * 4 + 4)))
        for q, m in enumerate(ms):
            ph.tr(flat[:, q * NS:(q + 1) * NS], sst[:, m * 128:(m + 1) * 128], identf[:NS, :NS], R=["s_sst"], W=pk)
        ph.cp(V, prv[:, ms[0]:ms[-1] + 1, :], flat[:, 0:len(ms) * NS].rearrange("p (a b) -> p a b", b=NS), R=pk, W="s_prv")
    ph.dbg("cur", cur[:], [128, 14, NS], "s_cur")
    ph.dbg("prv", prv[:], [128, 14, NS], "s_prv")
    so = sst
    for half in range(4):
        pb, pk = getF()
        flat = pb[:].rearrange("p a b -> p (a b)")
        ms = list(range(half * 4, min(14, half * 4 + 4)))
        for q, m in enumerate(ms):
            ph.tr(flat[:NS, q * 128:(q + 1) * 128], cur[:, m, :], identf[:], R=["s_cur"], W=pk)
        ph.cp(V, so[:, ms[0] * 128:(ms[-1] + 1) * 128], flat[:NS, 0:len(ms) * 128], R=pk, W="s_sst")
    ph.dma("sp", I["s_shift"], so[:], R="s_sst")
    xs = XS[:, :, 0:NS]
    ph.tt(V, dd[:, :, 0:NS], prv[:], cur[:], ALU.subtract, R=["s_prv", "s_cur"], W="dd")
    ph.tt(V, dd[:, :, 0:NS], dd[:, :, 0:NS], bc(pc["mu_shift"][:, :].unsqueeze(2), [128, 14, NS]), ALU.mult,
          R=["dd", "c_mu_shift"], W="dd")
    ph.tt(V, xs, dd[:, :, 0:NS], cur[:], ALU.add, R=["dd", "s_cur"], W="XS")
    uf = L["uf"]; ub = L["ub"]; ZZb = L["ZZb"]
    ph.dma("act", uf[:, :, 0:NS], I["UU"][:, t0:t0 + n].rearrange("(m p) t -> p m t", p=128), W="uf")
    ph.cp("act", ub[:, :, 0:NS], uf[:, :, 0:NS], R="uf", W="ub")
    stx = [sb("s_stre", [NS, 2048], F32), sb("s_stim", [NS, 2048], F32)]
    ph.dma("sp", stx[0][:], I["st_re"], W="s_stx0"); ph.dma("sp", stx[1][:], I["st_im"], W="s_stx1")
    Xsm = sb("s_Xsm", [128, 2, 16, NS], F32)
    for ri in range(2):
        for q4 in range(4):
            pb, pk = getF()
            flat = pb[:].rearrange("p a b -> p (a b)")
            for q in range(4):
                P_ = q4 * 4 + q
                ph.tr(flat[:, q * NS:(q + 1) * NS], stx[ri][:, P_ * 128:(P_ + 1) * 128], identf[:NS, :NS],
                      R="s_stx%d" % ri, W=pk)
            ph.cp(V, Xsm[:, ri, q4 * 4:q4 * 4 + 4, :], flat[:, 0:4 * NS].rearrange("p (a b) -> p a b", b=NS), R=pk, W="s_Xsm")
    s5_sample(ph, I, G0, pc, Xsm, ub, ZZb, getF, stx)
    ph.dma("act", I["ZZ"][:, t0:t0 + n].rearrange("(m p) t -> p m t", p=128), ZZb[:, :, 0:NS], R="ZZb")
    rwkv_sample(ph, I, G0, L)
    ph.dma("sp", I["YF"][:, t0:t0 + n].rearrange("(m p) t -> p m t", p=128), L["YFb"][:, :, 0:NS], R="YFb")


def s5_sample(ph, I, G0, pc, Xsm, ub, ZZb, getF, stx):
    V = "dve"
    BwT, Kmat, CwT, Ab = G0["BwT"], G0["Kmat"], G0["CwT"], G0["Abar"]
    identf = G0["identf"]
    Xb = ph._s5xb
    ph.cp("act", Xb[:, :, :, 0:NS], Xsm[:], R="s_Xsm", W="Xb")
    du = ph._s5du
    for k in range(4):
        pb, pk = getF()
        flat = pb[:].rearrange("p a b -> p (a b)")
        ph.mm(flat[:, 0:NS], Kmat[:, k, 0, :], ub[:, k, 0:NS], True, False, R=["Kmat", "ub"], W=pk)
        for Pl in range(4):
            P_ = 4 * k + Pl
            for ri in range(2):
                ph.mm(flat[32 * Pl:32 * Pl + 32, 0:NS], CwT[:, 0, ri, P_, :], Xb[:, ri, P_, 0:NS], False,
                      ri == 1, R=["CwT", "Xb"], W=pk, tp=(0, 32 * Pl))
        ph.ts(V, du[:, 0:NS], ub[:, k, 0:NS], pc["D_skip"][:, k:k + 1], ALU.mult, R=["ub", "c_D_skip", "s5z"], W="s5du")
        ph.tt(V, du[:, 0:NS], du[:, 0:NS], flat[:, 0:NS], ALU.add, R=["s5du", pk], W="s5du")
        ph.act(ZZb[:, k, 0:NS], du[:, 0:NS], AF.Gelu_apprx_tanh, R="s5du", W=["ZZb", "s5z"])
    Gs = ph.sb("s_Gs", [128, 2, 16, NS], F32)
    for Pl in range(4):
        pb, pk = getF()
        flat = pb[:].rearrange("p a b -> p (a b)")
        for ri in range(2):
            for k in range(4):
                q = ri * 4 + k
                ph.mm(flat[:, q * NS:(q + 1) * NS], BwT[32 * Pl:32 * Pl + 32, k, CS - 1, ri, :],
                      ub[32 * Pl:32 * Pl + 32, k, 0:NS], True, True, R=["BwT", "ub"], W=pk,
                      tp=((96, 0) if Pl == 3 else None))
        for ri in range(2):
            ph.cp(V, Gs[:, ri, Pl:16:4, :], flat[:, ri * 4 * NS:(ri + 1) * 4 * NS].rearrange("p (q m) -> p q m", m=NS),
                  R=pk, W="s_Gs")
    A_r = bc(Ab[:, 1, 0, :].unsqueeze(2), [128, 16, NS]); A_i = bc(Ab[:, 1, 1, :].unsqueeze(2), [128, 16, NS])
    ta = ph.sb("s_ta", [128, 16, NS], F32)
    ph.tt(V, ta[:], Xsm[:, 0], A_r, ALU.mult, R=["s_Xsm", "Abar"], W="s_ta")
    ph.tt(V, Gs[:, 0], Gs[:, 0], ta[:], ALU.add, R=["s_Gs", "s_ta"], W="s_Gs")
    ph.tt(V, ta[:], Xsm[:, 1], A_i, ALU.mult, R=["s_Xsm", "Abar", "s_Gs"], W="s_ta")
    ph.tt(V, Gs[:, 0], Gs[:, 0], ta[:], ALU.subtract, R=["s_Gs", "s_ta"], W="s_Gs")
    ph.tt(V, ta[:], Xsm[:, 1], A_r, ALU.mult, R=["s_Xsm", "Abar", "s_Gs"], W="s_ta")
    ph.tt(V, Gs[:, 1], Gs[:, 1], ta[:], ALU.add, R=["s_Gs", "s_ta"], W="s_Gs")
    ph.tt(V, ta[:], Xsm[:, 0], A_i, ALU.mult, R=["s_Xsm", "Abar", "s_Gs"], W="s_ta")
    ph.tt(V, Gs[:, 1], Gs[:, 1], ta[:], ALU.add, R=["s_Gs", "s_ta"], W="s_Gs")
    for ri, nm in enumerate(("s_re", "s_im")):
        xo = stx[ri]
        for q4 in range(4):
            pb, pk = getF()
            flat = pb[:].rearrange("p a b -> p (a b)")
            for q in range(4):
                P_ = q4 * 4 + q
                ph.tr(flat[:NS, q * 128:(q + 1) * 128], Gs[:, ri, P_, :], identf[:], R="s_Gs", W=pk)
            ph.cp(V, xo[:, q4 * 512:(q4 + 1) * 512], flat[:NS, 0:512], R=pk, W="s_stx%d" % ri)
        ph.dma("sp", I[nm], xo[:], R="s_stx%d" % ri)


def rwkv_sample(ph, I, G0, L):
    V = "dve"
    sb = ph.sb
    pc = L["pc"]; getF, getT, ib = L["getF"], L["getT"], L["ib"]
    identf = G0["identf"]
    XS = L["XS"]
    sig, aa, gg, kk0, tq, rn, kkn = L["sig"], L["aa"], L["gg"], L["kk0"], L["tq"], L["rn"], L["kkn"]
    bb, kmod, bon = L["bb"], L["kmod"], L["bon"]
    lin, sgx, w2a2, g2b, blk64 = L["lin"], L["sgx"], L["w2a2"], L["g2b"], L["blk64"]
    n = NS
    r_ = XS[:, 0:4, 0:n]; k_ = XS[:, 4:8, 0:n]; v_ = XS[:, 8:12, 0:n]
    B4 = lambda t: bc(t[:, :].unsqueeze(2), [128, 4, n])
    S4 = lambda t: t[:, :, 0:n]
    ph.act(lin[0:64, 0:n], XS[0:64, 12, 0:n], AF.Tanh, R="XS", W="lin")
    ph.cp("act", lin[64:128, 0:n], XS[64:128, 12, 0:n], R="XS", W="lin")
    ph.act(sgx[:, 0:n], XS[:, 13, 0:n], AF.Sigmoid, R="XS", W="sgx")
    pw_, kw_ = getF(); pa_, ka_ = getF(); pg_, kg_ = getF()
    for m in range(4):
        ph.mm(pw_[:, m, 0:n], w2a2[0:64, m * 128:(m + 1) * 128], lin[0:64, 0:n], True, True, R=["w2a2", "lin"], W=kw_)
        ph.mm(pa_[:, m, 0:n], w2a2[64:128, m * 128:(m + 1) * 128], lin[64:128, 0:n], True, True, R=["w2a2", "lin"], W=ka_)
        ph.mm(pg_[:, m, 0:n], g2b[:, m * 128:(m + 1) * 128], sgx[:, 0:n], True, True, R=["g2b", "sgx"], W=kg_)
    for m in range(4):
        ph.act(sig[:, m, 0:n], pw_[:, m, 0:n], AF.Sigmoid, R=[kw_, "c_w0"], W="sig", bias=pc["w0"][:, m:m + 1])
        ph.act(aa[:, m, 0:n], pa_[:, m, 0:n], AF.Sigmoid, R=[ka_, "c_a0"], W="aa", bias=pc["a0"][:, m:m + 1])
    ph.cp("act", S4(gg), pg_[:, :, 0:n], R=kg_, W="gg")
    ph.tt(V, S4(kk0), k_, B4(pc["k_k"]), ALU.mult, R=["XS", "c_k_k"], W="kk0")
    ph.tt(V, S4(tq), S4(kk0), S4(kk0), ALU.mult, R="kk0", W="tq")
    pq, kq = getF()
    for m in range(4):
        ph.mm(pq[:, m, 0:n], blk64[:], tq[:, m, 0:n], True, True, R=["blk64", "tq"], W=kq)
    ph.act(S4(rn), pq[:, :, 0:n], AF.Sqrt, R=kq, W="rn")
    ph.ts(V, S4(rn), S4(rn), 1e-12, ALU.max, R="rn", W="rn")
    ph.op(V, lambda e: e.reciprocal(out=S4(rn), in_=S4(rn)), R="rn", W="rn")
    ph.tt(V, S4(kkn), S4(kk0), S4(rn), ALU.mult, R=["kk0", "rn"], W="kkn")
    ph.tt(V, S4(bb), S4(kkn), S4(aa), ALU.mult, R=["kkn", "aa"], W="bb")
    ph.tt(V, S4(tq), S4(aa), B4(pc["k_a"]), ALU.mult, R=["aa", "c_k_a", kq], W="tq")
    ph.tt(V, S4(tq), S4(tq), B4(pc["k_a"]), ALU.subtract, R=["tq", "c_k_a"], W="tq")
    ph.stt(S4(kmod), S4(tq), 1.0, k_, ALU.add, ALU.mult, R=["tq", "XS"], W="kmod")
    ph.tt(V, S4(tq), r_, S4(kmod), ALU.mult, R=["XS", "kmod"], W="tq")
    ph.tt(V, S4(tq), S4(tq), B4(pc["r_k"]), ALU.mult, R=["tq", "c_r_k"], W="tq")
    pq2, kq2 = getF()
    for m in range(4):
        ph.mm(pq2[:, m, 0:n], blk64[:], tq[:, m, 0:n], True, True, R=["blk64", "tq"], W=kq2)
    ph.tt(V, S4(bon), pq2[:, :, 0:n], v_, ALU.mult, R=[kq2, "XS"], W="bon")
    wdec = L["ex1"]
    ph.act(S4(wdec), S4(sig), AF.Exp, R="sig", W="ex1", scale=-C1)
    srcs = [r_, S4(wdec), S4(kmod), v_, S4(kkn), S4(bb)]
    keys = ["XS", "ex1", "kmod", "XS", "kkn", "bb"]
    tok = sb("s_tok", [NS, 6, 512], F32)
    for i, (src, kkey) in enumerate(zip(srcs, keys)):
        pb, pk = getF()
        flat = pb[:].rearrange("p a b -> p (a b)")
        for m in range(4):
            ph.tr(flat[:NS, m * 128:(m + 1) * 128], src[:, m, :], identf[:], R=kkey, W=pk)
        ph.cp(V if i % 2 else "act", tok[:, i, :], flat[:NS, 0:512], R=pk, W="s_tok")
    ph.dma("sp", I["SW"].rearrange("i b f -> b i f"), tok[:], R="s_tok", W="SWd")
    vec = sb("s_vec", [128, 6, 64], F32)
    ph.dma("sp", vec[:], I["SW"].rearrange("i b (h k) -> (b h) i k", h=8), R="SWd", W="s_vec")
    S0 = sb("s_S0", [128, 64, 64], F32)
    ph.dma("act", S0[:].rearrange("p a b -> p (a b)"), I["st_wkv"], W="s_S0")
    tmp = sb("s_tmp", [128, 64, 64], F32)
    sa = sb("s_sa", [128, 64], F32); yv = sb("s_yv", [128, 64], F32); kka = sb("s_kka", [128, 64], F32)
    kB = lambda i: bc(vec[:, i, :].unsqueeze(1), [128, 64, 64])
    ph.tt(V, tmp[:], S0[:], kB(4), ALU.mult, R=["s_S0", "s_vec"], W="s_tmp")
    ph.op(V, lambda e: e.tensor_reduce(out=sa[:], in_=tmp[:], axis=AX.X, op=ALU.add), R="s_tmp", W="s_sa")
    ph.tt(V, S0[:], S0[:], kB(1), ALU.mult, R=["s_S0", "s_vec", "s_tmp"], W="s_S0")
    ph.tt(V, tmp[:], bc(sa[:, :].unsqueeze(2), [128, 64, 64]), kB(5), ALU.mult, R=["s_sa", "s_vec"], W="s_tmp")
    ph.tt(V, S0[:], S0[:], tmp[:], ALU.subtract, R=["s_S0", "s_tmp"], W="s_S0")
    ph.tt(V, tmp[:], bc(vec[:, 3, :].unsqueeze(2), [128, 64, 64]), kB(2), ALU.mult, R=["s_vec", "s_S0"], W="s_tmp")
    ph.tt(V, S0[:], S0[:], tmp[:], ALU.add, R=["s_S0", "s_tmp"], W="s_S0")
    ph.dma("act", I["s_wkv"], S0[:].rearrange("p a b -> p (a b)"), R="s_S0")
    ph.tt(V, tmp[:], S0[:], kB(0), ALU.mult, R=["s_S0", "s_vec"], W="s_tmp")
    ph.op(V, lambda e: e.tensor_reduce(out=yv[:], in_=tmp[:], axis=AX.X, op=ALU.add), R="s_tmp", W="s_yv")
    ph.dma("sp", I["SY"], yv[:], R="s_yv", W="SYd")
    Ysb = L["Ysb"]
    ph.dma("sp", Ysb[:NS].rearrange("p a b -> p (a b)"), I["SY"].rearrange("(b h) v -> b (h v)", h=8), R="SYd", W="Ysb")
    groupnorm_out(ph, L, 0, NS)


def phase3(nc, I, G0, W3, WFI):
    ph = Ph(nc, "p3")
    V = "dve"
    W3 = alloc_w3(nc, ph.st)
    load_w3(ph, I, W3)
    rwo, glu, wo = W3["rwo"], W3["glu"], W3["wo"]
    for k in range(8):
        ph.dma("pool", WFI[:, k, :], I["w_ffn_in"][k * 128:(k + 1) * 128, :], W="wfi_pre")
    yf = ph.sb("yf", [128, 4, 512], BF16); zz = ph.sb("zz", [128, 4, 512], BF16); gt = ph.sb("gt", [128, 16, 512], BF16)
    trw = ph.sb("trw", [128, 8, 512], F32); mg = ph.sb("mg", [128, 8, 512], BF16)
    sgb = [ph.sb("sgb%d" % i, [128, 512], F32) for i in range(2)]
    s5t = [ph.sb("s5t%d" % i, [128, 512], F32) for i in range(2)]
    xts = [ph.sb("xt%d" % i, [128, D], F32) for i in range(2)]
    pm = [ph.ps("pm%d" % i, [128, 512], F32) for i in range(6)]
    npm = nx = ns = 0
    for (t0, nt) in BLOCKS:
        P = min(128, nt)
        r3 = lambda name: I[name][:, t0:t0 + nt].rearrange("(m p) t -> p m t", p=128)
        ph.dma("sp", yf[:, :, :nt], r3("YF"), W="yf"); ph.dma("sp", zz[:, :, :nt], r3("ZZ"), W="zz")
        ph.dma("act", gt[:, :, :nt], r3("GT"), W="gt")
        for m in range(8):
            pb = pm[npm % 6]; pk = "pm%d" % (npm % 6); npm += 1
            for k in range(4):
                ph.mm(pb[:, :nt], rwo[:, k, m * 128:(m + 1) * 128], yf[:, k, :nt], k == 0, k == 3, R=["rwo", "yf"], W=pk)
            ph.tt(V, trw[:, m, :nt], pb[:, :nt], gt[:, m, :nt], ALU.mult, R=[pk, "gt"], W="trw%d" % m)
        for m in range(8):
            pa = pm[npm % 6]; pka = "pm%d" % (npm % 6); npm += 1
            pb = pm[npm % 6]; pkb = "pm%d" % (npm % 6); npm += 1
            for k in range(4):
                ph.mm(pa[:, :nt], glu[:, k, m * 128:(m + 1) * 128], zz[:, k, :nt], k == 0, k == 3, R=["glu", "zz"], W=pka)
            for k in range(4):
                ph.mm(pb[:, :nt], glu[:, k, D + m * 128:D + (m + 1) * 128], zz[:, k, :nt], k == 0, k == 3,
                      R=["glu", "zz"], W=pkb)
            sg = sgb[ns % 2]; sk = "sgb%d" % (ns % 2); s5 = s5t[ns % 2]; s5k = "s5t%d" % (ns % 2); ns += 1
            ph.act(sg[:, :nt], pb[:, :nt], AF.Sigmoid, R=pkb, W=sk)
            ph.tt(V, s5[:, :nt], pa[:, :nt], sg[:, :nt], ALU.mult, R=[pka, sk], W=s5k)
            ph.tt(V, s5[:, :nt], s5[:, :nt], gt[:, 8 + m, :nt], ALU.mult, R=[s5k, "gt"], W=s5k)
            ph.tt(V, mg[:, m, :nt], s5[:, :nt], trw[:, m, :nt], ALU.add, R=[s5k, "trw%d" % m], W="mg")
        for s in range((nt + 127) // 128):
            xt = xts[nx % 2]; xk = "xt%d" % (nx % 2); nx += 1
            rows = slice(t0 + s * 128, t0 + s * 128 + P)
            ph.dma("sp", xt[:P, :], I["xall"][rows, :], W=xk)
            for half in range(2):
                pb = pm[npm % 6]; pk = "pm%d" % (npm % 6); npm += 1
                for k in range(8):
                    ph.mm(pb[:P, :], mg[:, k, s * 128:s * 128 + P], wo[:, k, half * 512:(half + 1) * 512], k == 0, k == 7,
                          R=["mg", "wo"], W=pk)
                ph.tt(V, xt[:P, half * 512:(half + 1) * 512], xt[:P, half * 512:(half + 1) * 512], pb[:P, :], ALU.add,
                      R=[pk, xk], W=xk)
            ph.dma("pool", I["X1"][rows, :], xt[:P, :], R=xk)
    ph.finish()


def phase4(nc, I, G0, WFI):
    ph = Ph(nc, "p4")
    V = "dve"
    G = norm_scratch(ph, G0)
    identf = G0["identf"]
    wfi = WFI; wfo = ph.sb("wfo", [128, 22, D], BF16)
    for k in range(22):
        ph.dma("pool", wfo[:, k, :], I["w_ffn_out"][k * 128:(k + 1) * 128, :], W="wfo")
    g2c = ph.sb("g2c", [128, 8], F32); load_col(ph, g2c[:], I["ln2_g"], 8, "g2c")
    cw = ph.sb("cw", [128, 3, 22], F32); cb = ph.sb("cb", [128, 22], F32)
    ph.dma("sp", cw[:], I["conv_w"].rearrange("t (f p) -> p t f", p=128), W="cw", slow=True)
    load_col(ph, cb[:], I["conv_b"], 22, "cb")
    hTs = [ph.sb("hT%d" % i, [128, 8, 512], BF16) for i in range(2)]
    hid = ph.sb("hid", [128, 22, 512], BF16)
    xts = [ph.sb("xt%d" % i, [128, D], F32) for i in range(2)]
    At = [ph.sb("At%d" % i, [128, 514], F32) for i in range(2)]
    acc = [ph.sb("acc%d" % i, [128, 512], F32) for i in range(2)]
    cc = ph.sb("cc", [128, 22, 2], F32)
    ph.memset(V, cc[:].rearrange("p a b -> p (a b)"), 0.0, W="cc")
    pm = [ph.ps("pm%d" % i, [128, 512], F32) for i in range(6)]
    scs = ph.sb("scs", [NS, 2816], F32)
    scT = ph.sb("scT", [128, 22, 2, NS], F32)
    aout = scs
    npm = na = 0
    NR, FI, FO = [], [], []
    for bi_, (t0, nt) in enumerate(BLOCKS):
        P = min(128, nt)
        hT = hTs[bi_ % 2]; hk = "hT%d" % (bi_ % 2)
        sample = nt < 128
        nsub = (nt + 127) // 128
        ph.rec_begin()
        for s in range(nsub):
            rows = slice(t0 + s * 128, t0 + s * 128 + P)
            ph.dma("sp", xts[s % 2][:P, :], I["X1"][rows, :], W="xt%d" % (s % 2))
            rms_to_hT(ph, G, xts[s % 2], P, g2c, hT, s * 128, str(s % 2), "g2c", hk)
        NR.append(ph.rec_end())
        ph.rec_begin()
        if sample:
            for tt_ in range(2):
                ph.dma("sp", scs[:], I["st_conv"][:, tt_, :], W="scs")
                for q in range(6):
                    pb = pm[npm % 6]; pk = "pm%d" % (npm % 6); npm += 1
                    fs = list(range(q * 4, min(22, q * 4 + 4)))
                    for j, f_ in enumerate(fs):
                        ph.tr(pb[:, j * NS:(j + 1) * NS], scs[:, f_ * 128:(f_ + 1) * 128], identf[:NS, :NS], R="scs", W=pk)
                    ph.cp(V, scT[:, fs[0]:fs[-1] + 1, tt_, :], pb[:, 0:len(fs) * NS].rearrange("p (a b) -> p a b", b=NS),
                          R=pk, W="scT")
        for f in range(22):
            pa = pm[npm % 6]; pka = "pm%d" % (npm % 6); npm += 1
            pb = pm[npm % 6]; pkb = "pm%d" % (npm % 6); npm += 1
            for k in range(8):
                ph.mm(pa[:, :nt], wfi[:, k, f * 128:(f + 1) * 128], hT[:, k, :nt], k == 0, k == 7, R=["wfi", hk], W=pka)
            for k in range(8):
                ph.mm(pb[:, :nt], wfi[:, k, 2816 + f * 128:2816 + (f + 1) * 128], hT[:, k, :nt], k == 0, k == 7,
                      R=["wfi", hk], W=pkb)
            A = At[na % 2]; ak = "At%d" % (na % 2); ac = acc[na % 2]; ck = "acc%d" % (na % 2); na += 1
            ph.cp("act", A[:, 2:2 + nt], pa[:, :nt], R=pka, W=ak)
            if not sample:
                ph.cp(V, A[:, 0:2], cc[:, f, :], R="cc", W=ak)
                a0, a1, a2 = A[:, 0:nt], A[:, 1:1 + nt], A[:, 2:2 + nt]
            else:
                a0, a1, a2 = scT[:, f, 0, :], scT[:, f, 1, :], A[:, 2:2 + nt]
            ph.ts(V, ac[:, :nt], a0, cw[:, 0, f:f + 1], ALU.mult, cb[:, f:f + 1], ALU.add, R=[ak, "scT", "cw", "cb"], W=ck)
            ph.stt(ac[:, :nt], a1, cw[:, 1, f:f + 1], ac[:, :nt], ALU.mult, ALU.add, R=[ak, "scT", "cw", ck], W=ck)
            ph.stt(ac[:, :nt], a2, cw[:, 2, f:f + 1], ac[:, :nt], ALU.mult, ALU.add, R=[ak, "cw", ck], W=ck)
            ph.act(ac[:, :nt], ac[:, :nt], AF.Gelu_apprx_tanh, R=ck, W=ck)
            ph.tt(V, hid[:, f, :nt], ac[:, :nt], pb[:, :nt], ALU.mult, R=[ck, pkb], W="hid")
            if not sample:
                ph.cp(V, cc[:, f, :], A[:, nt:nt + 2], R=ak, W="cc")
            else:
                po = pm[npm % 6]; pko = "pm%d" % (npm % 6); npm += 1
                ph.tr(po[:NS, 0:128], A[:, 2:2 + NS], identf[:], R=ak, W=pko)
                ph.cp(V, aout[:, f * 128:(f + 1) * 128], po[:NS, 0:128], R=pko, W="scs")
        if t0 + nt == T:
            for tt_ in range(2):
                ph.dma("sp", I["p_conv"][tt_].rearrange("(f p) -> p f", p=128), cc[:, :, tt_], R="cc", slow=True)
        if sample:
            ph.dma("sp", I["s_conv"][:, 1, :], aout[:], R="scs")
            ph.dma("act", I["s_conv"][:, 0, :], I["st_conv"][:, 1, :])
        FI.append(ph.rec_end())
        ph.rec_begin()
        for s in range(nsub):
            rows = slice(t0 + s * 128, t0 + s * 128 + P)
            xt = xts[s % 2]; xk = "xt%d" % (s % 2)
            ph.dma("sp", xt[:P, :], I["X1"][rows, :], W=xk)
            for half in range(2):
                pb = pm[npm % 6]; pk = "pm%d" % (npm % 6); npm += 1
                for f in range(22):
                    ph.mm(pb[:P, :], hid[:, f, s * 128:s * 128 + P], wfo[:, f, half * 512:(half + 1) * 512], f == 0, f == 21,
                          R=["hid", "wfo"], W=pk)
                ph.tt(V, xt[:P, half * 512:(half + 1) * 512], xt[:P, half * 512:(half + 1) * 512], pb[:P, :],
                      ALU.add, R=[pk, xk], W=xk)
            ph.dma("pool", I["X2"][rows, :], xt[:P, :], R=xk)
        FO.append(ph.rec_end())
    ph.play(NR[0])
    for b_ in range(len(BLOCKS)):
        ph.play(FI[b_], NR[b_ + 1] if b_ + 1 < len(BLOCKS) else [])
        ph.play(FO[b_])
    ph.finish()


def phase5(nc, I, G0):
    ph = Ph(nc, "p5")
    V = "dve"
    Ga = norm_scratch(ph, G0, "a")
    Gb = norm_scratch(ph, G0, "b", eps=Ga["eps"])
    wpg = ph.sb("wpg", [128, 8, D], BF16); wpl = ph.sb("wpl", [128, 2, D], BF16)
    for k in range(8):
        ph.dma("pool", wpg[:, k, :], I["w_ple_gate"][k * 128:(k + 1) * 128, :], W="wpg")
    for k in range(2):
        ph.dma("pool", wpl[:, k, :], I["w_ple"][k * 128:(k + 1) * 128, :], W="wpl")
    g3c = ph.sb("g3c", [128, 8], F32); load_col(ph, g3c[:], I["ln3_g"], 8, "g3c")
    fg = ph.sb("fg", [128, D], F32)
    ph.dma("sp", fg[:], I["final_g"].partition_broadcast(128), W="fg")
    hTs = [ph.sb("hT%d" % i, [128, 8, 128], BF16) for i in range(2)]
    xts = [ph.sb("xt%d" % i, [128, D], F32) for i in range(2)]
    pball = ph.sb("pball", [128, 17, 256], BF16)
    _i = 0
    for (t0_, nt_) in BLOCKS:
        P_ = min(128, nt_)
        for s_ in range((nt_ + 127) // 128):
            ph.dma("pool", pball[:P_, _i, :], I["pall"][t0_ + s_ * 128:t0_ + s_ * 128 + P_, :], W="pb%d" % _i)
            _i += 1
    pTss = [ph.sb("pTs%d" % i, [128, 2, 128], BF16) for i in range(2)]
    sg = [ph.sb("sg%d" % i, [128, 512], F32) for i in range(2)]
    yo = [ph.sb("yo%d" % i, [128, D], F32) for i in range(2)]
    pm = [ph.ps("pm%d" % i, [128, 512], F32) for i in range(4)]
    pqs = [ph.ps("pq%d" % i, [128, 8, 128], BF16) for i in range(2)]
    npm = nx = nsg = 0
    for (t0, nt) in BLOCKS:
        P = min(128, nt)
        for s in range((nt + 127) // 128):
            rows = slice(t0 + s * 128, t0 + s * 128 + P)
            i2 = nx % 2; nx += 1
            xt = xts[i2]; xk = "xt%d" % i2; pbt = pball[:, nx - 1, :]; pbk = "pb%d" % (nx - 1)
            G = (Ga, Gb)[i2]; hT = hTs[i2]; hk = "hT%d" % i2; pTs = pTss[i2]; ptk = "pTs%d" % i2
            pq = pqs[i2]; pqk = "pq%d" % i2
            ph.dma("sp", xt[:P, :], I["X2"][rows, :], W=xk)
            rms_to_hT(ph, G, xt, P, g3c, hT, 0, str(i2), "g3c", hk)
            for k in range(2):
                ph.tr(pq[:, k, :P], pbt[:P, k * 128:(k + 1) * 128], G0["identb"][:P, :P], R=[pbk, "identb"], W=pqk)
            ph.cp("act", pTs[:, :, :P], pq[:, 0:2, :P], R=pqk, W=ptk)
            for half in range(2):
                cs_ = slice(half * 512, (half + 1) * 512)
                pg = pm[npm % 4]; pgk = "pm%d" % (npm % 4); npm += 1
                pe = pm[npm % 4]; pek = "pm%d" % (npm % 4); npm += 1
                for k in range(8):
                    ph.mm(pg[:P, :], hT[:, k, :P], wpg[:, k, cs_], k == 0, k == 7, R=[hk, "wpg"], W=pgk)
                for k in range(2):
                    ph.mm(pe[:P, :], pTs[:, k, :P], wpl[:, k, cs_], k == 0, k == 1, R=[ptk, "wpl"], W=pek)
                sgt = sg[nsg % 2]; sgk = "sg%d" % (nsg % 2); nsg += 1
                ph.act(sgt[:P, :], pg[:P, :], AF.Sigmoid, R=pgk, W=sgk)
                ph.tt(V, sgt[:P, :], sgt[:P, :], pe[:P, :], ALU.mult, R=[sgk, pek], W=sgk)
                ph.tt(V, xt[:P, cs_], xt[:P, cs_], sgt[:P, :], ALU.add, R=[sgk, xk, "xn" + G["sx"]], W=xk)
            ss = G["ss"]; sq = G["sq"]; kss = "ss" + G["sx"]; ksq = "sq" + G["sx"]
            ph.act(sq[:P, :], xt[:P, :], AF.Square, R=xk, W=[ksq, kss], accum=ss[:P, 0:1])
            ph.act(ss[:P, 1:2], ss[:P, 0:1], AF.Sqrt, R=[kss, "eps"], W=kss, bias=G["eps"][:P, 0:1], scale=1.0 / D)
            ph.op(V, lambda e, ss=ss, P=P: e.reciprocal(out=ss[:P, 3:4], in_=ss[:P, 1:2]), R=kss, W=kss + "3")
            y = yo[i2]; yk = "yo%d" % i2
            ph.stt(y[:P, :], xt[:P, :], ss[:P, 3:4], fg[:P, :], ALU.mult, ALU.mult, R=[xk, kss + "3", "fg"], W=yk)
            ph.dma("pool", I["y"][rows, :], y[:P, :], R=yk)
    ph.finish()


_CACHE = {}


def _consts():
    i = np.arange(128)
    c = {}
    c["c_ident"] = np.eye(128, dtype=np.float32)
    c["c_msl"] = (i[None, :] < i[:, None]).astype(np.float32)
    c["c_msu"] = (i[:, None] < i[None, :]).astype(np.float32)
    c["c_mui"] = (i[:, None] <= i[None, :]).astype(np.float32)
    c["c_blk64"] = ((i[:, None] // 64) == (i[None, :] // 64)).astype(np.float32)
    c["c_blk32"] = ((i[:, None] // 32) == (i[None, :] // 32)).astype(np.float32)
    c["c_rowgp"] = (((i[:, None] // 16) % 2) == (i[None, :] // 64)).astype(np.float32)
    return c


def make_in_maps(inp):
    f = lambda a: np.ascontiguousarray(np.asarray(a, dtype=np.float32))
    cst = _consts()
    shared = {}
    for k in ("ln1_g", "w_in", "mu_shift", "w0", "w2", "a0", "a2", "g2", "k_k", "k_a", "lnx_g", "lnx_b", "w_rw_out",
              "A_re", "A_im", "log_dt", "B_re", "B_im", "D_skip", "w_glu", "w_out", "ln2_g", "w_ffn_in", "conv_w",
              "conv_b", "w_ffn_out", "ln3_g", "w_ple_gate", "w_ple"):
        shared[k] = f(inp[k])[0]
    shared["r_k"] = f(inp["r_k"])[0].reshape(512)
    shared["C_re"] = f(inp["C_re"])[0].reshape(512, 64)
    shared["C_im"] = f(inp["C_im"])[0].reshape(512, 64)
    shared["final_g"] = f(inp["final_g"])
    shared.update(cst)
    xp, xs = f(inp["x_prompt"]), f(inp["x_sample"])
    pp, psm = f(inp["p_prompt"])[0], f(inp["p_sample"])[0]
    in_maps = []
    for c in range(8):
        sl = slice(NS * c, NS * c + NS)
        m = dict(shared)
        m["xall"] = np.concatenate([xp[c], xs[sl, 0]], 0)
        m["pall"] = np.concatenate([pp[c], psm[sl, 0]], 0)
        m["st_shift"] = f(inp["state_shift"])[0, sl]
        m["st_wkv"] = f(inp["state_wkv"])[0, sl].reshape(128, 4096)
        m["st_re"] = f(inp["state_ssm_re"])[0, sl].reshape(NS, 2048)
        m["st_im"] = f(inp["state_ssm_im"])[0, sl].reshape(NS, 2048)
        m["st_conv"] = f(inp["state_conv"])[0, sl]
        in_maps.append({k: np.ascontiguousarray(v) for k, v in m.items()})
    return in_maps


def kernel(**inp):
    f = lambda a: np.ascontiguousarray(np.asarray(a, dtype=np.float32))
    if "nc" not in _CACHE:
        _CACHE["nc"] = build_program()
    nc = _CACHE["nc"]
    in_maps = make_in_maps(inp)
    res = run_bass_kernel_spmd(nc, in_maps, core_ids=list(range(8)))
    R = res.results
    cat = lambda fn: np.stack([fn(r) for r in R], 0)
    y_prompt = cat(lambda r: r["y"][:T])
    y_sample = np.concatenate([r["y"][T:] for r in R], 0)[:, None, :]
    p_shift = cat(lambda r: r["p_shift"])[None]
    p_wkv = cat(lambda r: r["p_wkv"].reshape(8, 64, 64).transpose(0, 2, 1))[None]
    p_re = cat(lambda r: r["p_re"].reshape(32, 64))[None]
    p_im = cat(lambda r: r["p_im"].reshape(32, 64))[None]
    p_conv = cat(lambda r: r["p_conv"])[None]
    s_shift = np.concatenate([r["s_shift"] for r in R], 0)[None]
    s_wkv = np.concatenate([r["s_wkv"].reshape(NS, 8, 64, 64) for r in R], 0)[None]
    s_re = np.concatenate([r["s_re"].reshape(NS, 32, 64) for r in R], 0)[None]
    s_im = np.concatenate([r["s_im"].reshape(NS, 32, 64) for r in R], 0)[None]
    s_conv = np.concatenate([r["s_conv"] for r in R], 0)[None]
    outs = (y_prompt, y_sample, p_shift, p_wkv, p_re, p_im, p_conv, s_shift, s_wkv, s_re, s_im, s_conv)
    return tuple(np.ascontiguousarray(o.astype(np.float32)) for o in outs)
```

```python
import contextlib
import math
import numpy as np
import concourse.bass as bass
import concourse.mybir as mybir
from concourse.bass_utils import run_bass_kernel_spmd

F32 = mybir.dt.float32
BF16 = mybir.dt.bfloat16
AF = mybir.ActivationFunctionType
ALU = mybir.AluOpType
AX = mybir.AxisListType

T = 2048
NS = 16
NT = T + NS
D = 1024
CS = 8
SCAN_ENG = "pool"
import os as _os
BUB = int(_os.environ.get("K_BUB", "48"))
BUB2 = int(_os.environ.get("K_BUB2", "0"))
SYO = float(_os.environ.get("K_SYO", "0.5"))
C1 = math.exp(-0.5)
BLOCKS = [(0, 512), (512, 512), (1024, 512), (1536, 512), (2048, 16)]

ENGS = ("pe", "act", "dve", "pool", "sp")
NDSEM = 12


class _Op:
    __slots__ = ("eng", "fn", "deps", "dma", "observed", "tok", "idx", "dslot")

    def __init__(self, eng, fn, dma):
        self.eng, self.fn, self.dma = eng, fn, dma
        self.deps = set()
        self.observed = False
        self.tok = None
        self.dslot = None


class Sched:
    def __init__(self, nc):
        self.nc = nc
        self.ops = []
        self.last_w = {}
        self.readers = {}
        self.dma_rr = {e: 0 for e in ENGS}
        self.dma_prev = {}
        self.excl = set()

    def _add(self, eng, fn, reads, writes, dma):
        op = _Op(eng, fn, dma)
        op.idx = len(self.ops)
        if self.excl:
            ex = tuple(b for b in reads if b in self.excl)
            if ex:
                writes = tuple(writes) + ex
        for b in reads:
            w = self.last_w.get(b)
            if w is not None:
                op.deps.add(w)
        for b in writes:
            w = self.last_w.get(b)
            if w is not None:
                op.deps.add(w)
            for r in self.readers.get(b, ()):
                op.deps.add(r)
        if dma:
            slot = (eng, self.dma_rr[eng] % NDSEM)
            self.dma_rr[eng] += 1
            op.dslot = slot
            prev = self.dma_prev.get(slot)
            if prev is not None:
                op.deps.add(prev)
            self.dma_prev[slot] = op.idx
        op.deps.discard(op.idx)
        self.ops.append(op)
        for b in writes:
            self.last_w[b] = op.idx
            self.readers[b] = []
        for b in reads:
            if b not in writes:
                self.readers.setdefault(b, []).append(op.idx)
        return op.idx

    def emit(self):
        nc = self.nc
        ops = self.ops
        need = []
        for op in ops:
            nd = []
            for d in op.deps:
                p = ops[d]
                if (not p.dma) and (not op.dma) and p.eng == op.eng == "pe":
                    continue
                nd.append(d)
                p.observed = True
            need.append(nd)
        last = {}
        for op in ops:
            key = op.dslot if op.dma else op.eng
            last[key] = op.idx
        for i in last.values():
            ops[i].observed = True
        g = getattr(nc, "_gsem", None)
        if g is None:
            g = {"sems": {}, "cnt": {e: 0 for e in ENGS}, "dcnt": {}}
            nc._gsem = g
        cnt = g["cnt"]
        dcnt = g["dcnt"]
        for op in ops:
            if op.dma:
                dcnt[op.dslot] = dcnt.get(op.dslot, 0) + 16
                op.tok = (op.dslot, dcnt[op.dslot])
            elif op.observed:
                cnt[op.eng] += 1
                op.tok = (op.eng, cnt[op.eng])
        sems = g["sems"]
        for k in list(ENGS) + sorted(set(o.dslot for o in ops if o.dma)):
            if k not in sems:
                nm = k if isinstance(k, str) else "d_%s_%d" % k
                sems[k] = nc.alloc_semaphore(name="s_" + nm)
        with contextlib.ExitStack() as st:
            block = st.enter_context(nc.Block())
            per = {e: [o for o in ops if o.eng == e] for e in ENGS}
            hw = {"pe": block.tensor, "act": block.scalar, "dve": block.vector,
                  "pool": block.gpsimd, "sp": block.sync}

            def make(e):
                def body(eng):
                    seen = {}
                    for op in per[e]:
                        waits = {}
                        for d in need[op.idx]:
                            k, v = ops[d].tok
                            if v > waits.get(k, 0):
                                waits[k] = v
                        for k, v in waits.items():
                            if seen.get(k, 0) >= v:
                                continue
                            seen[k] = v
                            eng.wait_ge(sems[k], v)
                        ins = op.fn(eng)
                        if op.dma:
                            ins.then_inc(sems[op.tok[0]], 16)
                        elif op.observed:
                            ins.then_inc(sems[e], 1)
                    if e == "sp":
                        for key, i in last.items():
                            k, v = ops[i].tok
                            if seen.get(k, 0) < v:
                                eng.wait_ge(sems[k], v)
                return body

            for e in ENGS:
                hw[e](make(e))


def _L(x):
    if x is None:
        return ()
    if isinstance(x, str):
        return (x,)
    return tuple(x)


class Ph:
    _uid = [0]

    def __init__(self, nc, tag):
        self.nc = nc
        self.tag = tag
        self.st = contextlib.ExitStack()
        self.S = Sched(nc)

    def sb(self, name, shape, dt):
        return self.st.enter_context(self.nc.sbuf_tensor(self.tag + "_" + name, list(shape), dt))

    def ps(self, name, shape, dt):
        self.S.excl.add(name)
        return self.st.enter_context(self.nc.psum_tensor(self.tag + "_" + name, list(shape), dt))

    def finish(self):
        self.S.emit()
        self.st.close()

    def dbg(self, name, ap, shape, key, dt=F32):
        import os
        if os.environ.get("K_DBG_DUMP", "") == "":
            return
        t = self.nc.dram_tensor("dbg_" + name, list(shape), dt, kind="ExternalOutput").ap()
        self.dma("sp", t, ap, R=key)

    _rec = None

    def rec_begin(self):
        self._rec = []

    def rec_end(self):
        r, self._rec = self._rec, None
        return r

    def bubble(self, k):
        if self._rec is not None:
            self._rec.append(("bubble", k))

    def merge(self, *streams, spans=None):
        if spans is None:
            spans = [(0.0, 1.0)] * len(streams)
        keep = [i for i, st_ in enumerate(streams) if st_]
        spans = [spans[i] for i in keep]
        streams = [streams[i] for i in keep]
        pos = [0] * len(streams)
        out = []
        while True:
            best, bi = None, -1
            for i, st_ in enumerate(streams):
                if pos[i] < len(st_):
                    f = spans[i][0] + spans[i][1] * (pos[i] + 1.0) / len(st_)
                    if best is None or f < best:
                        best, bi = f, i
            if bi < 0:
                break
            item = streams[bi][pos[bi]]
            pos[bi] += 1
            if item[0] == "bubble":
                left = item[1]
                prog = True
                while left > 0 and prog:
                    prog = False
                    for j in range(len(streams)):
                        if j != bi and pos[j] < len(streams[j]) and left > 0:
                            it2 = streams[j][pos[j]]
                            pos[j] += 1
                            prog = True
                            if it2[0] != "bubble":
                                out.append(it2)
                                left -= 1
                continue
            out.append(item)
        return out

    def play(self, *streams, spans=None):
        for item in self.merge(*streams, spans=spans):
            if item[0] == "bubble":
                continue
            eng, fn, R, W, dma = item
            self.S._add(eng, fn, R, W, dma)

    def op(self, eng, fn, R=None, W=None):
        if self._rec is not None:
            self._rec.append((eng, fn, _L(R), _L(W), False))
        else:
            self.S._add(eng, fn, _L(R), _L(W), False)

    def dma(self, q, out, in_, R=None, W=None, slow=False):
        if slow:
            fn = lambda e: e.dma_start(out=out, in_=in_, allow_slow_non_contiguous=True)
        else:
            fn = lambda e: e.dma_start(out=out, in_=in_)
        if self._rec is not None:
            self._rec.append((q, fn, _L(R), _L(W), True))
        else:
            self.S._add(q, fn, _L(R), _L(W), True)

    def tt(self, eng, out, in0, in1, op, R=None, W=None):
        self.op(eng, lambda e: e.tensor_tensor(out=out, in0=in0, in1=in1, op=op), R, W)

    def ts(self, eng, out, in0, s1, op0, s2=None, op1=None, R=None, W=None):
        if op1 is None:
            self.op(eng, lambda e: e.tensor_scalar(out=out, in0=in0, scalar1=s1, scalar2=None, op0=op0), R, W)
        else:
            self.op(eng, lambda e: e.tensor_scalar(out=out, in0=in0, scalar1=s1, scalar2=s2, op0=op0, op1=op1), R, W)

    def stt(self, out, in0, scalar, in1, op0, op1, R=None, W=None):
        self.op("dve", lambda e: e.scalar_tensor_tensor(out=out, in0=in0, scalar=scalar, in1=in1, op0=op0, op1=op1), R, W)

    def act(self, out, in_, func, R=None, W=None, bias=None, scale=1.0, accum=None):
        kw = {}
        if bias is not None:
            kw["bias"] = bias
        if accum is not None:
            kw["accum_out"] = accum
        self.op("act", lambda e: e.activation(out=out, in_=in_, func=func, scale=scale, **kw), R, W)

    def cp(self, eng, out, in_, R=None, W=None):
        if eng == "act":
            self.op("act", lambda e: e.activation(out=out, in_=in_, func=AF.Copy), R, W)
        else:
            self.op(eng, lambda e: e.tensor_copy(out=out, in_=in_), R, W)

    def mm(self, out, lhsT, rhs, start, stop, R=None, W=None, tp=None):
        if tp is None:
            self.op("pe", lambda e: e.matmul(out, lhsT=lhsT, rhs=rhs, start=start, stop=stop), R, W)
        else:
            self.op("pe", lambda e: e.matmul(out, lhsT=lhsT, rhs=rhs, start=start, stop=stop, tile_position=tp), R, W)

    def tr(self, out, in_, ident, R=None, W=None):
        self.op("pe", lambda e: e.transpose(out, in_, ident), R, W)

    def memset(self, eng, ap, v, W=None):
        self.op(eng, lambda e: e.memset(ap, v), None, W)


def bc(ap, shape):
    return ap.to_broadcast(list(shape))


def rms_to_hT(ph, G, xt, P, gcol, hT, c0, tag, gkey, hkey="hT"):
    sq, ss, xn, pT = G["sq"], G["ss"], G["xn"], G["pT"]
    x_ = G.get("sx", "")
    ksq, kss, kxn, kpT = "sq" + x_, "ss" + x_, "xn" + x_, "pT" + x_
    ph.act(sq[:P, :], xt[:P, :], AF.Square, R="xt" + tag, W=[ksq, kss], accum=ss[:P, 0:1])
    ph.act(ss[:P, 1:2], ss[:P, 0:1], AF.Sqrt, R=[kss, "eps"], W=kss, bias=G["eps"][:P, 0:1], scale=1.0 / D)
    ph.op("dve", lambda e: e.reciprocal(out=ss[:P, 2:3], in_=ss[:P, 1:2]), R=kss, W=kss)
    ph.ts("dve", xn[:P, :], xt[:P, :], ss[:P, 2:3], ALU.mult, R=["xt" + tag, kss], W=kxn)
    for k in range(8):
        ph.tr(pT[:, k, :P], xn[:P, k * 128:(k + 1) * 128], G["identb"][:P, :P], R=[kxn, "identb"], W=kpT)
    ph.tt("dve", hT[:, :, c0:c0 + P], pT[:, :, :P], bc(gcol[:, :].unsqueeze(2), [128, 8, P]), ALU.mult,
          R=[kpT, gkey], W=hkey)


def load_col(ph, dst, src1d, n, key):
    ph.dma("sp", dst, src1d.rearrange("(k p) -> p k", p=128), W=key, slow=True)


def norm_scratch(ph, G0, sx="", eps=None):
    G = dict(G0)
    G["sx"] = sx
    G["sq"] = ph.sb("sq" + sx, [128, D], F32)
    G["ss"] = ph.sb("ss" + sx, [128, 4], F32)
    G["xn"] = ph.sb("xn" + sx, [128, D], BF16)
    G["pT"] = ph.ps("pT" + sx, [128, 8, 128], BF16)
    if eps is None:
        G["eps"] = ph.sb("eps", [128, 1], F32)
        ph.memset("dve", G["eps"][:], 1e-6, W="eps")
    else:
        G["eps"] = eps
    return G


def build_program(upto=9, debug=False):
    nc = bass.Bass("TRN2", target_bir_lowering=False)
    I = {}

    def inp(name, shape, dt=F32):
        I[name] = nc.dram_tensor(name, list(shape), dt, kind="ExternalInput").ap()

    def outp(name, shape):
        I[name] = nc.dram_tensor(name, list(shape), F32, kind="ExternalOutput").ap()

    def scratch(name, shape, dt):
        if debug:
            I[name] = nc.dram_tensor(name, list(shape), dt, kind="ExternalOutput").ap()
        else:
            I[name] = nc.dram_tensor(name, list(shape), dt).ap()
    if debug:
        scratch("d_BwT", [128, 4 * CS * 2 * 128], BF16); scratch("d_Kmat", [128, 4 * CS * 128], BF16)
        scratch("d_CwT", [128, CS * 2 * 16 * 32], BF16); scratch("d_Abar", [128, 64], F32)

    inp("xall", [NT, D]); inp("pall", [NT, 256])
    inp("st_shift", [NS, 1792]); inp("st_wkv", [128, 4096]); inp("st_re", [NS, 2048]); inp("st_im", [NS, 2048])
    inp("st_conv", [NS, 2, 2816])
    inp("ln1_g", [D]); inp("w_in", [D, 4352]); inp("mu_shift", [1792]); inp("w0", [512]); inp("w2", [64, 512])
    inp("a0", [512]); inp("a2", [64, 512]); inp("g2", [128, 512]); inp("k_k", [512]); inp("k_a", [512])
    inp("r_k", [512]); inp("lnx_g", [512]); inp("lnx_b", [512]); inp("w_rw_out", [512, D])
    inp("A_re", [32, 64]); inp("A_im", [32, 64]); inp("log_dt", [32]); inp("B_re", [32, 64, 16]); inp("B_im", [32, 64, 16])
    inp("C_re", [512, 64]); inp("C_im", [512, 64]); inp("D_skip", [512]); inp("w_glu", [512, 2048]); inp("w_out", [D, D])
    inp("ln2_g", [D]); inp("w_ffn_in", [D, 5632]); inp("conv_w", [3, 2816]); inp("conv_b", [2816]); inp("w_ffn_out", [2816, D])
    inp("ln3_g", [D]); inp("w_ple_gate", [D, D]); inp("w_ple", [256, D]); inp("final_g", [D])
    inp("c_ident", [128, 128]); inp("c_msl", [128, 128]); inp("c_msu", [128, 128]); inp("c_mui", [128, 128])
    inp("c_blk64", [128, 128]); inp("c_blk32", [128, 128]); inp("c_rowgp", [128, 128])
    outp("y", [NT, D]); outp("p_shift", [1792]); outp("p_wkv", [512, 64]); outp("p_re", [2048]); outp("p_im", [2048])
    outp("p_conv", [2, 2816]); outp("s_shift", [NS, 1792]); outp("s_wkv", [128, 4096]); outp("s_re", [NS, 2048])
    outp("s_im", [NS, 2048]); outp("s_conv", [NS, 2, 2816])
    scratch("PRW", [1792, NT], F32); scratch("UU", [512, NT], F32); scratch("GT", [2048, NT], BF16)
    scratch("YF", [512, NT], BF16); scratch("ZZ", [512, NT], BF16); scratch("X1", [NT, D], F32); scratch("X2", [NT, D], F32)
    scratch("SW", [6, NS, 512], F32); scratch("SY", [128, 64], F32)

    with contextlib.ExitStack() as gst:
        def gsb(name, shape, dt):
            return gst.enter_context(nc.sbuf_tensor("g_" + name, list(shape), dt))
        G0 = {}
        G0["identb"] = gsb("identb", [128, 128], BF16)
        G0["identf"] = gsb("identf", [128, 128], F32)
        with contextlib.ExitStack() as g2:
            def g2sb(name, shape, dt):
                return g2.enter_context(nc.sbuf_tensor("g_" + name, list(shape), dt))
            G0["BwT"] = g2sb("BwT", [128, 4, CS, 2, 128], BF16)
            G0["Kmat"] = g2sb("Kmat", [128, 4, CS, 128], BF16)
            G0["CwT"] = g2sb("CwT", [128, CS, 2, 16, 32], BF16)
            G0["Abar"] = g2sb("Abar", [128, 2, 2, 16], F32)
            if upto >= 1:
                phase1(nc, I, G0, debug)
            else:
                phase0(nc, I, G0, debug)
            if upto >= 2:
                phase2(nc, I, G0, True)
            if upto >= 2.5:
                phase2(nc, I, G0, False)
        g4 = contextlib.ExitStack()
        WFI = g4.enter_context(nc.sbuf_tensor("g_wfi", [128, 8, 5632], BF16))
        if upto >= 3:
            phase3(nc, I, G0, None, WFI)
        if upto >= 4:
            phase4(nc, I, G0, WFI)
        g4.close()
        if upto >= 5:
            phase5(nc, I, G0)
    return nc


def phase0(nc, I, G0, debug=False, ph=None):
    own = ph is None
    if own:
        ph = Ph(nc, "p0")
        ph.dma("pool", G0["identb"][:], I["c_ident"], W="identb")
        ph.dma("sp", G0["identf"][:], I["c_ident"], W="identf")
    sb = ph.sb
    lr = sb("lr", [128, 16], F32); li = sb("li", [128, 16], F32); dtl = sb("dtl", [128, 16], F32)
    Bre = sb("Bre", [128, 16, 16], F32); Bim = sb("Bim", [128, 16, 16], F32)
    ph.dma("sp", lr[:], I["A_re"].rearrange("(P gp) n -> (gp n) P", gp=2), W="lr", slow=True)
    ph.dma("sp", li[:], I["A_im"].rearrange("(P gp) n -> (gp n) P", gp=2), W="li", slow=True)
    ldt2 = I["log_dt"].rearrange("(P gp) -> gp P", gp=2)
    for gp in range(2):
        ph.dma("sp", dtl[64 * gp:64 * gp + 64, :], ldt2[gp].partition_broadcast(64), W="dtl", slow=True)
    ph.dma("sp", Bre[:], I["B_re"].rearrange("(P gp) n c -> (gp n) P c", gp=2), W="Bre")
    ph.dma("sp", Bim[:], I["B_im"].rearrange("(P gp) n c -> (gp n) P c", gp=2), W="Bim")
    rowgp = sb("rowgp", [128, 128], F32); blk32 = sb("blk32", [128, 128], F32)
    ph.dma("sp", rowgp[:], I["c_rowgp"], W="rowgp"); ph.dma("sp", blk32[:], I["c_blk32"], W="blk32")
    CT = [sb("CTr", [128, 4, 128], F32), sb("CTi", [128, 4, 128], F32)]
    c2 = sb("c2", [128, 128], F32)
    pA = ph.ps("pA", [128, 4, 128], F32)
    for ri, nm in enumerate(("C_re", "C_im")):
        for k in range(4):
            src = I[nm][k * 128:(k + 1) * 128, :]
            ph.dma("sp", c2[:, 0:64], src, W="c2"); ph.dma("sp", c2[:, 64:128], src, W="c2")
            ph.tt("dve", c2[:], c2[:], rowgp[:], ALU.mult, R=["c2", "rowgp"], W="c2")
            ph.tr(pA[:, k, :], c2[:], G0["identf"][:], R=["c2", "identf"], W="pA")
        ph.cp("dve", CT[ri][:], pA[:], R="pA", W="CT%d" % ri)
    t = {n: sb(n, [128, 16], F32) for n in ("dt", "e1", "mag", "ang", "sa", "ca", "sinv", "cosv", "ar", "ai", "den",
                                             "rden", "am1", "fr", "fi", "t1", "t2")}
    V = "dve"
    K = lambda *n: list(n)
    hpi = sb("hpi", [128, 1], F32)
    ph.memset(V, hpi[:], math.pi / 2, W="hpi")
    ph.act(t["dt"][:], dtl[:], AF.Exp, R="dtl", W="dt")
    ph.tt(V, t["e1"][:], lr[:], t["dt"][:], ALU.mult, R=K("lr", "dt"), W="e1")
    ph.act(t["mag"][:], t["e1"][:], AF.Exp, R="e1", W="mag")
    ph.tt(V, t["ang"][:], li[:], t["dt"][:], ALU.mult, R=K("li", "dt"), W="ang")
    ph.ts(V, t["sa"][:], t["ang"][:], 1.0 / 64, ALU.mult, R="ang", W="sa")
    ph.act(t["sinv"][:], t["sa"][:], AF.Sin, R="sa", W="sinv")
    ph.act(t["cosv"][:], t["sa"][:], AF.Sin, R=["sa", "hpi"], W="cosv", bias=hpi[:, 0:1])
    for _ in range(6):
        ph.tt(V, t["t1"][:], t["cosv"][:], t["cosv"][:], ALU.mult, R="cosv", W="t1")
        ph.tt(V, t["t2"][:], t["sinv"][:], t["sinv"][:], ALU.mult, R="sinv", W="t2")
        ph.stt(t["sinv"][:], t["cosv"][:], 2.0, t["sinv"][:], ALU.mult, ALU.mult, R=["cosv", "sinv", "t2"], W="sinv")
        ph.tt(V, t["cosv"][:], t["t1"][:], t["t2"][:], ALU.subtract, R=["t1", "t2", "sinv"], W="cosv")
    ph.tt(V, t["ar"][:], t["mag"][:], t["cosv"][:], ALU.mult, R=K("mag", "cosv"), W="ar")
    ph.tt(V, t["ai"][:], t["mag"][:], t["sinv"][:], ALU.mult, R=K("mag", "sinv"), W="ai")
    ph.tt(V, t["den"][:], lr[:], lr[:], ALU.mult, R="lr", W="den")
    ph.tt(V, t["t1"][:], li[:], li[:], ALU.mult, R="li", W="t1")
    ph.tt(V, t["den"][:], t["den"][:], t["t1"][:], ALU.add, R=K("den", "t1"), W="den")
    ph.op(V, lambda e: e.reciprocal(out=t["rden"][:], in_=t["den"][:]), R="den", W="rden")
    ph.ts(V, t["am1"][:], t["ar"][:], -1.0, ALU.add, R="ar", W="am1")
    ph.tt(V, t["t1"][:], t["am1"][:], lr[:], ALU.mult, R=K("am1", "lr", "den"), W="t1")
    ph.tt(V, t["t2"][:], t["ai"][:], li[:], ALU.mult, R=K("ai", "li"), W="t2")
    ph.tt(V, t["t1"][:], t["t1"][:], t["t2"][:], ALU.add, R=K("t1", "t2"), W="t1")
    ph.tt(V, t["fr"][:], t["t1"][:], t["rden"][:], ALU.mult, R=K("t1", "rden"), W="fr")
    ph.tt(V, t["t1"][:], t["ai"][:], lr[:], ALU.mult, R=K("ai", "lr", "fr"), W="t1")
    ph.tt(V, t["t2"][:], t["am1"][:], li[:], ALU.mult, R=K("am1", "li"), W="t2")
    ph.tt(V, t["t1"][:], t["t1"][:], t["t2"][:], ALU.subtract, R=K("t1", "t2"), W="t1")
    ph.tt(V, t["fi"][:], t["t1"][:], t["rden"][:], ALU.mult, R=K("t1", "rden"), W="fi")
    pwr = sb("pwr", [128, CS + 1, 16], F32); pwi = sb("pwi", [128, CS + 1, 16], F32)
    ph.memset(V, pwr[:, 0, :], 1.0, W="pw"); ph.memset(V, pwi[:, 0, :], 0.0, W="pw")
    for e in range(CS):
        ph.tt(V, t["t1"][:], pwr[:, e, :], t["ar"][:], ALU.mult, R=K("pw", "ar", "fi"), W="t1")
        ph.tt(V, t["t2"][:], pwi[:, e, :], t["ai"][:], ALU.mult, R=K("pw", "ai"), W="t2")
        ph.tt(V, pwr[:, e + 1, :], t["t1"][:], t["t2"][:], ALU.subtract, R=K("t1", "t2"), W="pw")
        ph.tt(V, t["t1"][:], pwr[:, e, :], t["ai"][:], ALU.mult, R=K("pw", "ai"), W="t1")
        ph.tt(V, t["t2"][:], pwi[:, e, :], t["ar"][:], ALU.mult, R=K("pw", "ar"), W="t2")
        ph.tt(V, pwi[:, e + 1, :], t["t1"][:], t["t2"][:], ALU.add, R=K("t1", "t2"), W="pw")
    Ab = G0["Abar"]
    ph.cp(V, Ab[:, 0, 0, :], pwr[:, CS, :], R="pw", W="Abar"); ph.cp(V, Ab[:, 0, 1, :], pwi[:, CS, :], R="pw", W="Abar")
    ph.cp(V, Ab[:, 1, 0, :], pwr[:, 1, :], R="pw", W="Abar"); ph.cp(V, Ab[:, 1, 1, :], pwi[:, 1, :], R="pw", W="Abar")
    bbr = sb("bbr", [128, 16, 16], F32); bbi = sb("bbi", [128, 16, 16], F32)
    u1 = sb("u1", [128, 16, 16], F32); u2 = sb("u2", [128, 16, 16], F32)
    frb = bc(t["fr"][:, :].unsqueeze(2), [128, 16, 16]); fib = bc(t["fi"][:, :].unsqueeze(2), [128, 16, 16])
    ph.tt(V, u1[:], Bre[:], frb, ALU.mult, R=K("Bre", "fr"), W="u1")
    ph.tt(V, u2[:], Bim[:], fib, ALU.mult, R=K("Bim", "fi"), W="u2")
    ph.tt(V, bbr[:], u1[:], u2[:], ALU.subtract, R=K("u1", "u2"), W="bbr")
    ph.tt(V, u1[:], Bim[:], frb, ALU.mult, R=K("Bim", "fr", "bbr"), W="u1")
    ph.tt(V, u2[:], Bre[:], fib, ALU.mult, R=K("Bre", "fi", "bbr"), W="u2")
    ph.tt(V, bbi[:], u1[:], u2[:], ALU.add, R=K("u1", "u2"), W="bbi")
    Ew = sb("Ew", [128, CS, 2, 16, 2, 16], F32)
    ph.memset(V, Ew[:].rearrange("p a b c d e -> p (a b c d e)"), 0.0, W="Ew")
    for e in range(CS):
        pr = bc(pwr[:, e, :].unsqueeze(2), [128, 16, 16]); pi = bc(pwi[:, e, :].unsqueeze(2), [128, 16, 16])
        ph.tt(V, u1[:], bbr[:], pr, ALU.mult, R=K("bbr", "pw", "Ew"), W="u1")
        ph.tt(V, u2[:], bbi[:], pi, ALU.mult, R=K("bbi", "pw", "Ew"), W="u2")
        ph.tt(V, u1[:], u1[:], u2[:], ALU.subtract, R=K("u1", "u2"), W="u1")
        for gp in range(2):
            ph.cp(V, Ew[64 * gp:64 * gp + 64, e, 0, :, gp, :], u1[64 * gp:64 * gp + 64, :, :], R="u1", W="Ew")
        ph.tt(V, u1[:], bbr[:], pi, ALU.mult, R=K("bbr", "pw", "Ew"), W="u1")
        ph.tt(V, u2[:], bbi[:], pr, ALU.mult, R=K("bbi", "pw", "Ew"), W="u2")
        ph.tt(V, u1[:], u1[:], u2[:], ALU.add, R=K("u1", "u2"), W="u1")
        for gp in range(2):
            ph.cp(V, Ew[64 * gp:64 * gp + 64, e, 1, :, gp, :], u1[64 * gp:64 * gp + 64, :, :], R="u1", W="Ew")
    CTin = sb("CTin", [128, 4, 128], F32)
    ph.ts(V, CTin[:], CT[1][:], -1.0, ALU.mult, R="CT1", W="CTin")
    pB = [ph.ps("pB%d" % i, [128, 4, 128], F32) for i in range(2)]
    n = 0
    for j in range(CS):
        e = CS - 1 - j
        for ri in range(2):
            pb = pB[n % 2]; n += 1
            for k in range(4):
                src = Ew[:, e, ri, 4 * k:4 * k + 4, :, :].rearrange("p a b c -> p (a b c)")
                ph.tr(pb[:, k, :], src, G0["identf"][:], R=["Ew", "identf"], W="pB%d" % ((n - 1) % 2))
            ph.cp("act" if n % 2 else "dve", G0["BwT"][:, :, j, ri, :], pb[:], R="pB%d" % ((n - 1) % 2), W="BwT")
    for tau in range(CS):
        pb = pB[n % 2]; key = "pB%d" % (n % 2); n += 1
        for k in range(4):
            lr_ = Ew[:, tau, 0, 4 * k:4 * k + 4, :, :].rearrange("p a b c -> p (a b c)")
            li_ = Ew[:, tau, 1, 4 * k:4 * k + 4, :, :].rearrange("p a b c -> p (a b c)")
            ph.mm(pb[:, k, :], lr_, CT[0][:, k, :], True, False, R=["Ew", "CT0"], W=key)
            ph.mm(pb[:, k, :], li_, CTin[:, k, :], False, True, R=["Ew", "CTin"], W=key)
        ph.tt(V, G0["Kmat"][:, :, tau, :], pb[:], bc(blk32[:, :].unsqueeze(1), [128, 4, 128]), ALU.mult,
              R=[key, "blk32"], W="Kmat")
    w1 = sb("w1", [128, 16, 32], F32); w2_ = sb("w2", [128, 16, 32], F32)
    CTr3 = CT[0][:].rearrange("p k (a b) -> p (k a) b", a=4); CTi3 = CT[1][:].rearrange("p k (a b) -> p (k a) b", a=4)
    for i in range(CS):
        pr = bc(pwr[:, i + 1, :].unsqueeze(2), [128, 16, 32]); pi = bc(pwi[:, i + 1, :].unsqueeze(2), [128, 16, 32])
        ph.tt(V, w1[:], CTr3, pr, ALU.mult, R=K("CT0", "pw", "CwT"), W="w1")
        ph.tt(V, w2_[:], CTi3, pi, ALU.mult, R=K("CT1", "pw", "CwT"), W="w2")
        ph.tt(V, G0["CwT"][:, i, 0, :, :], w1[:], w2_[:], ALU.subtract, R=K("w1", "w2"), W="CwT")
        ph.tt(V, w1[:], CTr3, pi, ALU.mult, R=K("CT0", "pw", "CwT"), W="w1")
        ph.tt(V, w2_[:], CTi3, pr, ALU.mult, R=K("CT1", "pw", "CwT"), W="w2")
        ph.tt(V, w1[:], w1[:], w2_[:], ALU.add, R=K("w1", "w2"), W="w1")
        ph.ts(V, G0["CwT"][:, i, 1, :, :], w1[:], -1.0, ALU.mult, R="w1", W="CwT")
    if debug:
        ph.dma("sp", I["d_BwT"], G0["BwT"][:].rearrange("p a b c d -> p (a b c d)"), R="BwT")
        ph.dma("sp", I["d_Kmat"], G0["Kmat"][:].rearrange("p a b c -> p (a b c)"), R="Kmat")
        ph.dma("sp", I["d_CwT"], G0["CwT"][:].rearrange("p a b c d -> p (a b c d)"), R="CwT")
        ph.dma("sp", I["d_Abar"], G0["Abar"][:].rearrange("p a b c -> p (a b c)"), R="Abar")
    if own:
        ph.finish()


def phase1(nc, I, G0, debug=False):
    ph = Ph(nc, "p1")
    win = ph.sb("win", [128, 8, 4352], BF16)
    for k in range(8):
        ph.dma("pool", win[:, k, :], I["w_in"][k * 128:(k + 1) * 128, :], W="win%d" % k)
    ph.dma("pool", G0["identb"][:], I["c_ident"], W="identb")
    ph.dma("sp", G0["identf"][:], I["c_ident"], W="identf")
    ph.rec_begin()
    phase0(nc, I, G0, debug, ph=ph)
    s0 = ph.rec_end()
    ph.rec_begin()
    G = norm_scratch(ph, G0)
    g1c = ph.sb("g1c", [128, 8], F32)
    load_col(ph, g1c[:], I["ln1_g"], 8, "g1c")
    hTs = [ph.sb("hT%d" % i, [128, 8, 512], BF16) for i in range(2)]
    xts = [ph.sb("xt%d" % i, [128, D], F32) for i in range(2)]
    pm = [ph.ps("pm%d" % i, [128, 512], F32) for i in range(4)]
    stf = [ph.sb("stf%d" % i, [128, 512], F32) for i in range(4)]
    stb = [ph.sb("stb%d" % i, [128, 512], BF16) for i in range(3)]
    WK = ["win%d" % k for k in range(8)]
    nx = nf = nb = npm = 0
    pre1 = ph.rec_end()
    NR1, MM1 = [], []
    for bi_, (t0, nt) in enumerate(BLOCKS):
        P = min(128, nt)
        hT = hTs[bi_ % 2]; hk = "hT%d" % (bi_ % 2)
        ph.rec_begin()
        for s in range((nt + 127) // 128):
            xt = xts[nx % 2]; tg = str(nx % 2); nx += 1
            ph.dma("sp", xt[:P, :], I["xall"][t0 + s * 128:t0 + s * 128 + P, :], W="xt" + tg)
            rms_to_hT(ph, G, xt, P, g1c, hT, s * 128, tg, "g1c", hk)
        NR1.append(ph.rec_end())
        ph.rec_begin()
        for m in range(34):
            pb = pm[npm % 4]; pk = "pm%d" % (npm % 4); npm += 1
            for k in range(8):
                ph.mm(pb[:, :nt], win[:, k, m * 128:(m + 1) * 128], hT[:, k, :nt], k == 0, k == 7,
                      R=["win%d" % k, hk], W=pk)
            if m < 18:
                sf = stf[nf % 4]; sk = "stf%d" % (nf % 4); nf += 1
                ph.cp("dve" if m % 2 else "act", sf[:, :nt], pb[:, :nt], R=pk, W=sk)
                if m < 14:
                    ph.dma("pool", I["PRW"][m * 128:(m + 1) * 128, t0:t0 + nt], sf[:, :nt], R=sk)
                else:
                    ph.dma("pool", I["UU"][(m - 14) * 128:(m - 13) * 128, t0:t0 + nt], sf[:, :nt], R=sk)
            else:
                sbf = stb[nb % 3]; sk = "stb%d" % (nb % 3); nb += 1
                ph.act(sbf[:, :nt], pb[:, :nt], AF.Sigmoid, R=pk, W=sk)
                ph.dma("act", I["GT"][(m - 18) * 128:(m - 17) * 128, t0:t0 + nt], sbf[:, :nt], R=sk)
        MM1.append(ph.rec_end())
    s1 = pre1 + NR1[0]
    for b_ in range(len(BLOCKS)):
        s1 = s1 + ph.merge(MM1[b_], NR1[b_ + 1] if b_ + 1 < len(BLOCKS) else [])
    ph.play(s1, s0)
    ph.finish()


def alloc_w3(nc, st):
    t = lambda n, shp: st.enter_context(nc.sbuf_tensor("w3_" + n, shp, BF16))
    return {"rwo": t("rwo", [128, 4, D]), "glu": t("glu", [128, 4, 2048]), "wo": t("wo", [128, 8, D])}


def load_w3(ph, I, W3):
    for k in range(4):
        ph.dma("pool", W3["rwo"][:, k, :], I["w_rw_out"][k * 128:(k + 1) * 128, :], W="rwo")
        ph.dma("pool", W3["glu"][:, k, :], I["w_glu"][k * 128:(k + 1) * 128, :], W="glu")
    for k in range(8):
        ph.dma("pool", W3["wo"][:, k, :], I["w_out"][k * 128:(k + 1) * 128, :], W="wo")


def phase2(nc, I, G0, prompt, W3=None):
    ph = Ph(nc, "p2a" if prompt else "p2b")
    sb, ps = ph.sb, ph.ps
    V = "dve"
    if W3 is not None:
        load_w3(ph, I, W3)
    ph._s5tmp = [sb("s5a", [128, 2, 16], F32), sb("s5b", [128, 2, 16], F32)]
    ph._s5xb = sb("Xb", [128, 2, 16, 64], BF16)
    ph._s5du = sb("s5du", [128, 512], F32)
    if prompt:
        msl = sb("msl", [128, 128], BF16); msu = sb("msu", [128, 128], BF16); mui = sb("mui", [128, 128], BF16)
        ph.dma("pool", msl[:], I["c_msl"], W="msl"); ph.dma("pool", msu[:], I["c_msu"], W="msu")
        ph.dma("pool", mui[:], I["c_mui"], W="mui")
    blk64 = sb("blk64", [128, 128], F32); ph.dma("sp", blk64[:], I["c_blk64"], W="blk64")
    w2a2 = sb("w2a2", [128, 512], BF16); g2b = sb("g2b", [128, 512], BF16)
    ph.dma("pool", w2a2[0:64, :], I["w2"], W="w2a2"); ph.dma("pool", w2a2[64:128, :], I["a2"], W="w2a2")
    ph.dma("pool", g2b[:], I["g2"], W="g2b")
    pc = {}
    for nm, n in (("mu_shift", 14), ("w0", 4), ("a0", 4), ("k_k", 4), ("k_a", 4), ("r_k", 4), ("lnx_g", 4),
                  ("lnx_b", 4), ("D_skip", 4)):
        pc[nm] = sb("c_" + nm, [128, n], F32)
        load_col(ph, pc[nm][:], I[nm], n, "c_" + nm)
    PK = ["c_" + k for k in pc]
    scm = sb("scm", [128, 4, 128], F32)
    ph.memset(V, scm[:].rearrange("p a b -> p (a b)"), 1.0, W="scm"); ph.memset(V, scm[:, :, 0:1], 0.0, W="scm")
    eps_gn = sb("eps_gn", [128, 1], F32); ph.memset(V, eps_gn[:], 64e-5, W="eps_gn")
    if prompt:
        Sst = sb("Sst", [128, 4, 64], F32); Sbd = sb("Sbd", [128, 4, 128], BF16)
        ph.memset(V, Sst[:].rearrange("p a b -> p (a b)"), 0.0, W="Sst")
        ph.memset(V, Sbd[:].rearrange("p a b -> p (a b)"), 0.0, W="Sbd")
        Xs = sb("Xs", [128, 2, 16, 65], F32)
        ph.memset(V, Xs[:].rearrange("p a b c -> p (a b c)"), 0.0, W="Xs")
        Pf = sb("Pf", [128, 14, 513], F32)
        ph.memset(V, Pf[:, :, 0:1], 0.0, W="Pf")
    WB = 512 if prompt else NS
    WC = 128 if prompt else NS
    uf = sb("uf", [128, 4, WB], F32); ub = sb("ub", [128, 4, WB], BF16)
    YFb = sb("YFb", [128, 4, WB], BF16); ZZb = sb("ZZb", [128, 4, WB], BF16)
    f4 = lambda n: sb(n, [128, 4, WC], F32)
    b4 = lambda n: sb(n, [128, 4, WC], BF16)
    XS = sb("XS", [128, 14, WC], F32); dd = sb("dd", [128, 14, WC], F32)
    lin = sb("lin", [128, WC], BF16); sgx = sb("sgx", [128, WC], BF16)
    sig = f4("sig"); aa = f4("aa"); gg = f4("gg"); kk0 = f4("kk0"); tq = f4("tq"); rn = f4("rn"); kkn = f4("kkn")
    bb = f4("bb"); kmod = f4("kmod"); bon = f4("bon"); cs = f4("cs"); ex1 = f4("ex1"); ex2 = f4("ex2"); ex3 = f4("ex3")
    nbias = sb("nbias", [128, 4], F32); PCt = sb("PCt", [128, 4], F32)
    gns = f4("gns")
    KX = {n_: n_ for n_ in ("rT", "kT", "bT", "aT", "khT", "bhT", "vT", "PCt", "bon", "gg")}
    if prompt:
        rT = b4("rT"); kT = b4("kT"); bT = b4("bT"); aT = b4("aT"); khT = b4("khT"); bhT = b4("bhT"); vT = b4("vT")
        alt = {"rT": b4("rT1"), "kT": b4("kT1"), "bT": b4("bT1"), "aT": b4("aT1"), "khT": b4("khT1"),
               "bhT": b4("bhT1"), "vT": b4("vT1"), "PCt": sb("PCt1", [128, 4], F32), "bon": f4("bon1"), "gg": f4("gg1")}
        Vtok = sb("Vtok", [128, 512], BF16); Khtok = sb("Khtok", [128, 512], BF16); Bhtok = sb("Bhtok", [128, 512], BF16)
        h8 = lambda n: sb(n, [128, 8, 128], BF16)
        Nb = [h8("Nb0"), h8("Nb1")]; Lb = [h8("Lb0"), h8("Lb1")]; Mt = [h8("Mt0"), h8("Mt1")]
        LKb = h8("LKb"); Arb = h8("Arb"); Ark = h8("Ark")
        Wbf = sb("Wbf", [128, 512], BF16); Ubf = sb("Ubf", [128, 512], BF16)
        tS = sb("tS", [128, 4, 64], F32)
    Ysb = sb("Ysb", [128, 8, 64], F32); Ysq = sb("Ysq", [128, 8, 64], F32); ynb = sb("ynb", [128, 8, 64], BF16)
    gn = sb("gn", [128, 6, 8], F32)
    pF = [ps("pF%d" % i, [128, 4, 128], F32) for i in range(6)]
    pT = [ps("pTb%d" % i, [128, 8, 128], BF16) for i in range(2)]
    cnt = {"f": 0, "t": 0}

    def getF():
        i = cnt["f"] % 6; cnt["f"] += 1
        return pF[i], "pF%d" % i

    def mkpool(base):
        st_ = {"n": 0}

        def get():
            i = base + st_["n"] % 2; st_["n"] += 1
            return pF[i], "pF%d" % i
        return get
    getF_prep, getF_core, getFs = mkpool(0), mkpool(2), mkpool(4)

    def getT():
        i = cnt["t"] % 2; cnt["t"] += 1
        return pT[i], "pTb%d" % i

    ib = G0["identb"]

    if not prompt:
        sample_mixer(ph, I, G0, locals())
        ph.finish()
        return
    Lbase = dict(locals())
    Lpar = [dict(Lbase), dict(Lbase)]
    Lpar[1].update(alt)
    Lpar[1]["KX"] = {n_: n_ + "1" for n_ in KX}
    REC = []
    for bi, (t0, nt) in enumerate(BLOCKS[:4]):
        ph.rec_begin()
        if bi > 0:
            ph.cp(V, Pf[:, :, 0:1], Pf[:, :, 512:513], R="Pf", W="Pf")
        ph.dma("sp", Pf[:, :, 1:513], I["PRW"][:, t0:t0 + nt].rearrange("(m p) t -> p m t", p=128), W="Pf")
        if bi == 3:
            ph.dma("sp", I["p_shift"].rearrange("(m p) -> p m", p=128), Pf[:, :, 512], R="Pf", slow=True)
        hdr_pf = ph.rec_end()
        ph.rec_begin()
        ph.dma("act", uf[:], I["UU"][:, t0:t0 + nt].rearrange("(m p) t -> p m t", p=128), W="uf")
        ph.cp("act", ub[:].rearrange("p a b -> p (a b)"), uf[:].rearrange("p a b -> p (a b)"), R="uf", W="ub")
        hdr_ub = ph.rec_end()
        ph.rec_begin()
        s5_block(ph, I, G0, pc, Xs, ub, ZZb, getFs, nchunk=64, which=0, ncol=512)
        ph.dma("act", I["ZZ"][:, t0:t0 + nt].rearrange("(m p) t -> p m t", p=128), ZZb[:], R="ZZb")
        s5s = ph.rec_end()
        m0, m1 = ph._s5marks
        preps, cores = [], []
        for c in range(4):
            c0 = c * 128
            Lc = dict(Lpar[c % 2]); Lc["getF"] = getF_prep
            Lk = dict(Lpar[c % 2]); Lk["getF"] = getF_core
            ph.rec_begin()
            ph.tt(V, dd[:], Pf[:, :, c0:c0 + 128], Pf[:, :, c0 + 1:c0 + 129], ALU.subtract, R="Pf", W="dd")
            ph.tt(V, dd[:], dd[:], bc(pc["mu_shift"][:, :].unsqueeze(2), [128, 14, 128]), ALU.mult,
                  R=["dd", "c_mu_shift"], W="dd")
            ph.tt(V, XS[:], dd[:], Pf[:, :, c0 + 1:c0 + 129], ALU.add, R=["dd", "Pf"], W="XS")
            rwkv_prep_and_core(ph, Lc, c, c0)
            preps.append(ph.rec_end())
            ph.rec_begin()
            wkv_core(ph, Lk, c, c0)
            cores.append(ph.rec_end())
        ph.rec_begin()
        ph.dma("pool", I["YF"][:, t0:t0 + nt].rearrange("(m p) t -> p m t", p=128), YFb[:], R="YFb")
        yfst = ph.rec_end()
        hs = (m1 - m0) // 2
        REC.append(dict(hdr_pf=hdr_pf, hdr_ub=hdr_ub, SG=s5s[:m0], SS1=s5s[m0:m0 + hs], SS2=s5s[m0 + hs:m1],
                        SY=s5s[m1:], preps=preps, cores=cores, yfst=yfst))
    ph.play(REC[0]["hdr_pf"])
    ph.play(REC[0]["preps"][0])
    for bi in range(4):
        Rb = REC[bi]
        ph.play(Rb["hdr_ub"])
        ph.play(Rb["cores"][0], Rb["preps"][1], Rb["SG"])
        ph.play(Rb["cores"][1], Rb["preps"][2], Rb["SS1"])
        ph.play(Rb["cores"][2], Rb["preps"][3], Rb["SS2"])
        if bi < 3:
            ph.play(REC[bi + 1]["hdr_pf"])
            ph.play(Rb["cores"][3], Rb["SY"], REC[bi + 1]["preps"][0], spans=[(0.0, 1.0), (SYO, 1.0 - SYO), (0.0, 1.0)])
        else:
            ph.play(Rb["cores"][3], Rb["SY"], spans=[(0.0, 1.0), (SYO, 1.0 - SYO)])
        ph.play(Rb["yfst"])
    ph.dma("sp", I["p_wkv"].rearrange("(m p) v -> p m v", p=128), Sst[:], R="Sst")
    ph.dma("sp", I["p_re"].rearrange("(P p) -> p P", p=128), Xs[:, 0, :, 0], R="Xs", slow=True)
    ph.dma("sp", I["p_im"].rearrange("(P p) -> p P", p=128), Xs[:, 1, :, 0], R="Xs", slow=True)
    ph.finish()


def rwkv_prep_and_core(ph, L, c, c0):
    V = "dve"
    PV = L.get("PV", "dve")
    KX = L["KX"]
    pc = L["pc"]; XS = L["XS"]; getF = L["getF"]; getT = L["getT"]; ib = L["ib"]
    sig, aa, gg, kk0, tq, rn, kkn = L["sig"], L["aa"], L["gg"], L["kk0"], L["tq"], L["rn"], L["kkn"]
    bb, kmod, bon, cs, ex1, ex2, ex3 = L["bb"], L["kmod"], L["bon"], L["cs"], L["ex1"], L["ex2"], L["ex3"]
    rT, kT, bT, aT, khT, bhT, vT = L["rT"], L["kT"], L["bT"], L["aT"], L["khT"], L["bhT"], L["vT"]
    lin, sgx, w2a2, g2b, blk64 = L["lin"], L["sgx"], L["w2a2"], L["g2b"], L["blk64"]
    nbias, PCt, scm = L["nbias"], L["PCt"], L["scm"]
    r_ = XS[:, 0:4, :]; k_ = XS[:, 4:8, :]; v_ = XS[:, 8:12, :]
    B4 = lambda t: bc(t[:, :].unsqueeze(2), [128, 4, 128])
    fl = lambda t: t[:].rearrange("p a b -> p (a b)")
    ph.act(lin[0:64, :], XS[0:64, 12, :], AF.Tanh, R="XS", W="lin")
    ph.cp("act", lin[64:128, :], XS[64:128, 12, :], R="XS", W="lin")
    ph.act(sgx[:], XS[:, 13, :], AF.Sigmoid, R="XS", W="sgx")
    pw_, kw_ = getF()
    for m in range(4):
        ph.mm(pw_[:, m, :], w2a2[0:64, m * 128:(m + 1) * 128], lin[0:64, :], True, True, R=["w2a2", "lin"], W=kw_)
    for m in range(4):
        ph.act(sig[:, m, :], pw_[:, m, :], AF.Sigmoid, R=[kw_, "c_w0"], W="sig", bias=pc["w0"][:, m:m + 1])
    pa_, ka_ = getF()
    for m in range(4):
        ph.mm(pa_[:, m, :], w2a2[64:128, m * 128:(m + 1) * 128], lin[64:128, :], True, True, R=["w2a2", "lin"], W=ka_)
    for m in range(4):
        ph.act(aa[:, m, :], pa_[:, m, :], AF.Sigmoid, R=[ka_, "c_a0"], W="aa", bias=pc["a0"][:, m:m + 1])
    pg_, kg_ = getF()
    for m in range(4):
        ph.mm(pg_[:, m, :], g2b[:, m * 128:(m + 1) * 128], sgx[:], True, True, R=["g2b", "sgx"], W=kg_)
    ph.cp("act", gg[:], pg_[:], R=kg_, W=KX["gg"])
    ph.tt(PV, kk0[:], k_, B4(pc["k_k"]), ALU.mult, R=["XS", "c_k_k"], W="kk0")
    ph.tt(PV, tq[:], kk0[:], kk0[:], ALU.mult, R="kk0", W="tq")
    pq, kq = getF()
    for m in range(4):
        ph.mm(pq[:, m, :], blk64[:], tq[:, m, :], True, True, R=["blk64", "tq"], W=kq)
    ph.act(rn[:], pq[:], AF.Sqrt, R=kq, W="rn")
    ph.ts(V, rn[:], rn[:], 1e-12, ALU.max, R="rn", W="rn")
    ph.op(V, lambda e: e.reciprocal(out=fl(rn), in_=fl(rn)), R="rn", W="rn")
    ph.tt(PV, kkn[:], kk0[:], rn[:], ALU.mult, R=["kk0", "rn"], W="kkn")
    ph.tt(PV, bb[:], kkn[:], aa[:], ALU.mult, R=["kkn", "aa"], W="bb")
    ph.tt(PV, tq[:], aa[:], B4(pc["k_a"]), ALU.mult, R=["aa", "c_k_a", kq], W="tq")
    ph.tt(PV, tq[:], tq[:], B4(pc["k_a"]), ALU.subtract, R=["tq", "c_k_a"], W="tq")
    ph.stt(kmod[:], tq[:], 1.0, k_, ALU.add, ALU.mult, R=["tq", "XS"], W="kmod")
    ph.tt(PV, tq[:], r_, kmod[:], ALU.mult, R=["XS", "kmod"], W="tq")
    ph.tt(PV, tq[:], tq[:], B4(pc["r_k"]), ALU.mult, R=["tq", "c_r_k"], W="tq")
    pq2, kq2 = getF()
    for m in range(4):
        ph.mm(pq2[:, m, :], blk64[:], tq[:, m, :], True, True, R=["blk64", "tq"], W=kq2)
    ph.tt(V, bon[:], pq2[:], v_, ALU.mult, R=[kq2, "XS"], W=KX["bon"])
    ph.op(V, lambda e: e.tensor_tensor_scan(out=fl(cs), data0=fl(scm), data1=fl(sig), initial=0.0, op0=ALU.mult,
                                             op1=ALU.add), R=["scm", "sig"], W="cs")
    ph.ts(V, nbias[:], cs[:, :, 127], -C1, ALU.mult, R="cs", W="nbias")
    ph.act(PCt[:], nbias[:], AF.Exp, R="nbias", W=KX["PCt"])
    ph.act(ex1[:], cs[:], AF.Exp, R="cs", W="ex1", scale=-C1)
    ph.tt(PV, rT[:], r_, ex1[:], ALU.mult, R=["XS", "ex1"], W=KX["rT"])
    ph.act(ex2[:], cs[:], AF.Exp, R="cs", W="ex2", scale=C1)
    ph.tt(PV, kT[:], kmod[:], ex2[:], ALU.mult, R=["kmod", "ex2"], W=KX["kT"])
    ph.tt(PV, bT[:], bb[:], ex2[:], ALU.mult, R=["bb", "ex2"], W=KX["bT"])
    ph.tt(PV, ex3[:], cs[:], sig[:], ALU.subtract, R=["cs", "sig"], W="ex3")
    ph.act(ex3[:], ex3[:], AF.Exp, R="ex3", W="ex3", scale=-C1)
    ph.stt(aT[:], kkn[:], -1.0, ex3[:], ALU.mult, ALU.mult, R=["kkn", "ex3"], W=KX["aT"])
    for m in range(4):
        ph.act(ex1[:, m, :], cs[:, m, :], AF.Exp, R=["cs", "nbias", KX["rT"]], W="ex1", bias=nbias[:, m:m + 1], scale=C1)
    ph.tt(PV, khT[:], kmod[:], ex1[:], ALU.mult, R=["kmod", "ex1"], W=KX["khT"])
    ph.tt(PV, bhT[:], bb[:], ex1[:], ALU.mult, R=["bb", "ex1"], W=KX["bhT"])
    ph.cp("act", vT[:], v_, R="XS", W=KX["vT"])


def wkv_core(ph, L, c, c0):
    V = "dve"
    KX = L["KX"]
    getF = L["getF"]; getT = L["getT"]; ib = L["ib"]
    rT, kT, bT, aT, khT, bhT, vT = L["rT"], L["kT"], L["bT"], L["aT"], L["khT"], L["bhT"], L["vT"]
    Vtok, Khtok, Bhtok = L["Vtok"], L["Khtok"], L["Bhtok"]
    Nb, Lb, Mt, LKb, Arb, Ark = L["Nb"], L["Lb"], L["Mt"], L["LKb"], L["Arb"], L["Ark"]
    msl, msu, mui = L["msl"], L["msu"], L["mui"]
    Wbf, Ubf, Ysb, Ysq, ynb, gn = L["Wbf"], L["Ubf"], L["Ysb"], L["Ysq"], L["ynb"], L["gn"]
    Sst, Sbd, PCt, tS = L["Sst"], L["Sbd"], L["PCt"], L["tS"]
    pc = L["pc"]; bon, gg, YFb = L["bon"], L["gg"], L["YFb"]
    M4 = lambda m_: bc(m_[:, :].unsqueeze(1), [128, 4, 128])
    pt, kt = getT()
    for m in range(4):
        ph.tr(pt[:, m, :], vT[:, m, :], ib[:], R=KX["vT"], W=kt)
    for m in range(4):
        ph.tr(pt[:, 4 + m, :], khT[:, m, :], ib[:], R=KX["khT"], W=kt)
    ph.cp("act", Vtok[:], pt[:, 0:4, :].rearrange("p a b -> p (a b)"), R=kt, W="Vtok")
    ph.cp(V, Khtok[:], pt[:, 4:8, :].rearrange("p a b -> p (a b)"), R=kt, W="Khtok")
    pt2, kt2 = getT()
    for m in range(4):
        ph.tr(pt2[:, m, :], bhT[:, m, :], ib[:], R=KX["bhT"], W=kt2)
    ph.cp("act", Bhtok[:], pt2[:, 0:4, :].rearrange("p a b -> p (a b)"), R=kt2, W="Bhtok")

    def hsl(t, h):
        return t[64 * (h % 2):64 * (h % 2) + 64, h // 2, :]

    def amat(dst, dkey, lhs, lkey, rhs, rkey, mask, mkey):
        for par in range(2):
            pb, pk = getF()
            for q in range(4):
                h = 2 * q + par
                ph.mm(pb[:, q, :], hsl(lhs, h), hsl(rhs, h), True, True, R=[lkey, rkey], W=pk)
            ph.tt(V, dst[:, par:8:2, :], pb[:], M4(mask), ALU.mult, R=[pk, mkey], W=dkey)

    amat(Nb[0], "Nb0", aT, KX["aT"], bT, KX["bT"], msl, "msl")
    amat(Lb[0], "Lb0", bT, KX["bT"], aT, KX["aT"], msu, "msu")
    amat(LKb, "LKb", kT, KX["kT"], aT, KX["aT"], msu, "msu")
    amat(Arb, "Arb", bT, KX["bT"], rT, KX["rT"], mui, "mui")
    amat(Ark, "Ark", kT, KX["kT"], rT, KX["rT"], mui, "mui")
    for half in range(2):
        ph.tt(V, Mt[0][:, half * 4:half * 4 + 4, :], Lb[0][:, half * 4:half * 4 + 4, :], M4(ib), ALU.add,
              R=["Lb0", "identb"], W="Mt0")
    cur = 0
    for lvl in range(6):
        nxt = 1 - cur
        for half in range(2):
            pb, pk = getF()
            for q in range(4):
                h = half * 4 + q
                ph.mm(pb[:, q, :], Lb[cur][:, h, :], Nb[cur][:, h, :], True, True, R=["Lb%d" % cur, "Nb%d" % cur], W=pk)
            ph.cp("act", Nb[nxt][:, half * 4:half * 4 + 4, :], pb[:], R=pk, W="Nb%d" % nxt)
        if BUB2:
            ph.bubble(BUB2)
        if lvl < 5:
            for half in range(2):
                pb, pk = getF()
                for q in range(4):
                    h = half * 4 + q
                    ph.mm(pb[:, q, :], Nb[cur][:, h, :], Lb[cur][:, h, :], True, True,
                          R=["Lb%d" % cur, "Nb%d" % cur], W=pk)
                ph.cp("act", Lb[nxt][:, half * 4:half * 4 + 4, :], pb[:], R=pk, W="Lb%d" % nxt)
        for half in range(2):
            pb, pk = getF()
            for q in range(4):
                h = half * 4 + q
                ph.mm(pb[:, q, :], Nb[nxt][:, h, :], Mt[cur][:, h, :], True, True, R=["Nb%d" % nxt, "Mt%d" % cur], W=pk)
            ph.tt(V, Mt[nxt][:, half * 4:half * 4 + 4, :], pb[:], Mt[cur][:, half * 4:half * 4 + 4, :], ALU.add,
                  R=[pk, "Mt%d" % cur], W="Mt%d" % nxt)
        cur = nxt
    MtF = Mt[cur]; mk = "Mt%d" % cur
    def hcols(pb, h):
        return pb[:].rearrange("p a b -> p (a b)")[:, h * 64:h * 64 + 64]

    def pcols(pb, m):
        return pb[:].rearrange("p a b -> p (a b)")[:, m * 128:m * 128 + 128]

    pb, pk = getF()
    for m in range(4):
        ph.mm(pcols(pb, m), aT[:, m, :], Sbd[:, m, :], True, False, R=[KX["aT"], "Sbd"], W=pk)
        for hh in range(2):
            h = 2 * m + hh
            ph.mm(hcols(pb, h), LKb[:, h, :], Vtok[:, h * 64:h * 64 + 64], False, hh == 1, R=["LKb", "Vtok"], W=pk)
    ph.cp("act", Wbf[:], pb[:].rearrange("p a b -> p (a b)"), R=pk, W="Wbf")
    ph.bubble(BUB)
    pb, pk = getF()
    for h in range(8):
        ph.mm(hcols(pb, h), MtF[:, h, :], Wbf[:, h * 64:h * 64 + 64], True, True, R=[mk, "Wbf"], W=pk)
    ph.cp("act", Ubf[:], pb[:].rearrange("p a b -> p (a b)"), R=pk, W="Ubf")
    ph.bubble(BUB)
    pb, pk = getF()
    for m in range(4):
        ph.mm(pcols(pb, m), rT[:, m, :], Sbd[:, m, :], True, False, R=[KX["rT"], "Sbd"], W=pk)
        for hh in range(2):
            h = 2 * m + hh
            ph.mm(hcols(pb, h), Arb[:, h, :], Ubf[:, h * 64:h * 64 + 64], False, False, R=["Arb", "Ubf"], W=pk)
            ph.mm(hcols(pb, h), Ark[:, h, :], Vtok[:, h * 64:h * 64 + 64], False, hh == 1, R=["Ark", "Vtok"], W=pk)
    ph.cp("act", Ysb[:].rearrange("p a b -> p (a b)"), pb[:].rearrange("p a b -> p (a b)"), R=pk, W="Ysb")
    pS, kS = getF()
    for m in range(4):
        ph.mm(pS[:, m, :], Bhtok[:, m * 128:(m + 1) * 128], Ubf[:, m * 128:(m + 1) * 128], True, False,
              R=["Bhtok", "Ubf"], W=kS)
        ph.mm(pS[:, m, :], Khtok[:, m * 128:(m + 1) * 128], Vtok[:, m * 128:(m + 1) * 128], False, True,
              R=["Khtok", "Vtok"], W=kS)
    ph.tt(V, tS[:], Sst[:], bc(PCt[:, :].unsqueeze(2), [128, 4, 64]), ALU.mult, R=["Sst", KX["PCt"]], W="tS")
    for hh in range(2):
        rs = slice(64 * hh, 64 * hh + 64)
        ph.tt(V, Sst[rs, :, :], tS[rs, :, :], pS[rs, :, 64 * hh:64 * hh + 64], ALU.add, R=["tS", kS], W="Sst")
        ph.cp(V, Sbd[rs, :, 64 * hh:64 * hh + 64], Sst[rs, :, :], R="Sst", W="Sbd")
    groupnorm_out(ph, L, c0, 128)


def groupnorm_out(ph, L, c0, P):
    V = "dve"
    KX = L["KX"]
    Ysb, Ysq, ynb, gn = L["Ysb"], L["Ysq"], L["ynb"], L["gn"]
    pc = L["pc"]; bon, gg, YFb = L["bon"], L["gg"], L["YFb"]; getT = L["getT"]; ib = L["ib"]
    eps_gn = L["eps_gn"]; ex2 = L["gns"]
    ph.op(V, lambda e: e.tensor_reduce(out=gn[:P, 0, :], in_=Ysb[:P], axis=AX.X, op=ALU.add), R="Ysb", W="gn")
    ph.act(Ysq[:P].rearrange("p a b -> p (a b)"), Ysb[:P].rearrange("p a b -> p (a b)"), AF.Square, R="Ysb", W="Ysq")
    ph.op(V, lambda e: e.tensor_reduce(out=gn[:P, 1, :], in_=Ysq[:P], axis=AX.X, op=ALU.add), R="Ysq", W="gn")
    ph.ts(V, gn[:P, 2, :], gn[:P, 0, :], 1.0 / 64, ALU.mult, R="gn", W="gn")
    ph.tt(V, gn[:P, 3, :], gn[:P, 2, :], gn[:P, 2, :], ALU.mult, R="gn", W="gn")
    ph.stt(gn[:P, 4, :], gn[:P, 1, :], 1.0 / 64, gn[:P, 3, :], ALU.mult, ALU.subtract, R="gn", W="gn")
    ph.act(gn[:P, 4, :], gn[:P, 4, :], AF.Sqrt, R=["gn", "eps_gn"], W="gn", bias=eps_gn[:P, 0:1])
    ph.op(V, lambda e: e.reciprocal(out=gn[:P, 5, :], in_=gn[:P, 4, :]), R="gn", W="gn")
    ph.tt(V, Ysq[:P], Ysb[:P], bc(gn[:P, 2, :].unsqueeze(2), [P, 8, 64]), ALU.subtract, R=["Ysb", "gn"], W="Ysq")
    ph.tt(V, ynb[:P], Ysq[:P], bc(gn[:P, 5, :].unsqueeze(2), [P, 8, 64]), ALU.mult, R=["Ysq", "gn"], W="ynb")
    pt, kt = getT()
    for m in range(4):
        ph.tr(pt[:, m, :P], ynb[:P, 2 * m:2 * m + 2, :].rearrange("p a b -> p (a b)"), ib[:P, :P], R="ynb", W=kt)
    B4 = lambda t: bc(t[:, :].unsqueeze(2), [128, 4, P])
    t1 = ex2
    ph.tt(V, t1[:, :, :P], pt[:, 0:4, :P], B4(pc["lnx_g"]), ALU.mult, R=[kt, "c_lnx_g"], W="gns")
    ph.tt(V, t1[:, :, :P], t1[:, :, :P], B4(pc["lnx_b"]), ALU.add, R=["gns", "c_lnx_b"], W="gns")
    ph.tt(V, t1[:, :, :P], t1[:, :, :P], bon[:, :, :P], ALU.add, R=["gns", KX["bon"]], W="gns")
    ph.tt(V, YFb[:, :, c0:c0 + P], t1[:, :, :P], gg[:, :, :P], ALU.mult, R=["gns", KX["gg"]], W="YFb")


def s5_block(ph, I, G0, pc, Xs, ub, ZZb, getF, nchunk, which, ncol, step=CS, npos=CS):
    V = "dve"
    BwT, Kmat, CwT, Ab = G0["BwT"], G0["Kmat"], G0["CwT"], G0["Abar"]
    nm = nchunk
    assert nm * 8 <= 512
    for Pl in range(4):
        pb, pk = getF()
        flat = pb[:].rearrange("p a b -> p (a b)")
        for ri in range(2):
            for k in range(4):
                q = ri * 4 + k
                dst = flat[:, q * nm:(q + 1) * nm]
                for j in range(npos):
                    jj = (CS - npos) + j
                    rhs = ub[32 * Pl:32 * Pl + 32, k, j:j + (nm - 1) * step + 1:step]
                    ph.mm(dst, BwT[32 * Pl:32 * Pl + 32, k, jj, ri, :], rhs, j == 0, j == npos - 1,
                          R=["BwT", "ub"], W=pk, tp=((96, 0) if Pl == 3 else None))
        for ri in range(2):
            ph.cp(V, Xs[:, ri, Pl:16:4, 1:1 + nm],
                  flat[:, ri * 4 * nm:(ri + 1) * 4 * nm].rearrange("p (q m) -> p q m", m=nm), R=[pk], W="Xs")
    A_r = bc(Ab[:, which, 0, :].unsqueeze(1), [128, 2, 16]); A_i = bc(Ab[:, which, 1, :].unsqueeze(1), [128, 2, 16])
    ph._s5marks = [len(ph._rec) if ph._rec is not None else 0]
    tmpa = ph._s5tmp[0]; tmpb = ph._s5tmp[1]
    for m in range(nm):
        ph.tt(SCAN_ENG, tmpa[:], Xs[:, :, :, m], A_r, ALU.mult, R=["Xs", "Abar"], W="s5a")
        ph.tt(SCAN_ENG, tmpb[:], Xs[:, :, :, m], A_i, ALU.mult, R=["Xs", "Abar"], W="s5b")
        ph.tt(SCAN_ENG, Xs[:, :, :, m + 1], Xs[:, :, :, m + 1], tmpa[:], ALU.add, R=["Xs", "s5a"], W="Xs")
        ph.tt(SCAN_ENG, Xs[:, 0, :, m + 1], Xs[:, 0, :, m + 1], tmpb[:, 1, :], ALU.subtract, R=["Xs", "s5b"], W="Xs")
        ph.tt(SCAN_ENG, Xs[:, 1, :, m + 1], Xs[:, 1, :, m + 1], tmpb[:, 0, :], ALU.add, R=["Xs", "s5b"], W="Xs")
    ph._s5marks.append(len(ph._rec) if ph._rec is not None else 0)
    Xb = ph._s5xb
    ph.cp("act", Xb[:, :, :, 0:nm], Xs[:, :, :, 0:nm], R="Xs", W="Xb")
    for k in range(4):
        pb, pk = getF()
        flat = pb[:].rearrange("p a b -> p (a b)")
        for i in range(npos):
            dst = flat[:, i * nm:(i + 1) * nm]
            for tau in range(i + 1):
                rhs = ub[:, k, (i - tau):(i - tau) + (nm - 1) * step + 1:step]
                ph.mm(dst, Kmat[:, k, tau, :], rhs, tau == 0, False, R=["Kmat", "ub"], W=pk)
            for Pl in range(4):
                P_ = 4 * k + Pl
                for ri in range(2):
                    ph.mm(flat[32 * Pl:32 * Pl + 32, i * nm:(i + 1) * nm], CwT[:, i, ri, P_, :], Xb[:, ri, P_, 0:nm],
                          False, ri == 1, R=["CwT", "Xb"], W=pk, tp=(0, 32 * Pl))
        du = ph._s5du
        ph.ts(V, du[:, 0:ncol], ub[:, k, 0:ncol], pc["D_skip"][:, k:k + 1], ALU.mult, R=["ub", "c_D_skip", "s5z"], W="s5du")
        if npos == 1:
            ph.tt(V, du[:, 0:ncol], du[:, 0:ncol], flat[:, 0:nm], ALU.add, R=["s5du", pk], W="s5du")
        else:
            ph.tt(V, du[:, 0:ncol].rearrange("p (m i) -> p m i", i=npos), du[:, 0:ncol].rearrange("p (m i) -> p m i", i=npos),
                  flat[:, 0:npos * nm].rearrange("p (i m) -> p m i", m=nm), ALU.add, R=["s5du", pk], W="s5du")
        ph.act(ZZb[:, k, 0:ncol], du[:, 0:ncol], AF.Gelu_apprx_tanh, R="s5du", W=["ZZb", "s5z"])
    ph.cp(V, Xs[:, :, :, 0], Xs[:, :, :, nm], R="Xs", W="Xs")


def sample_mixer(ph, I, G0, L):
    V = "dve"
    sb = ph.sb
    pc = L["pc"]; getF, getT, ib = L["getF"], L["getT"], L["ib"]
    identf = G0["identf"]
    XS = L["XS"]; dd = L["dd"]
    t0 = T
    n = NS
    cur = sb("s_cur", [128, 14, NS], F32); prv = sb("s_prv", [128, 14, NS], F32)
    ph.dma("sp", cur[:], I["PRW"][:, t0:t0 + n].rearrange("(m p) t -> p m t", p=128), W="s_cur")
    sst = sb("s_sst", [NS, 1792], F32)
    ph.dma("sp", sst[:], I["st_shift"], W="s_sst")
    for half in range(4):
        pb, pk = getF()
        flat = pb[:].rearrange("p a b -> p (a b)")
        ms = list(range(half * 4, min(14, half * 4 + 4)))
        for q, m in enumerate(ms):
            ph.tr(flat[:, q * NS:(q + 1) * NS], sst[:, m * 128:(m + 1) * 128], identf[:NS, :NS], R=["s_sst"], W=pk)
        ph.cp(V, prv[:, ms[0]:ms[-1] + 1, :], flat[:, 0:len(ms) * NS].rearrange("p (a b) -> p a b", b=NS), R=pk, W="s_prv")
    ph.dbg("cur", cur[:], [128, 14, NS], "s_cur")
    ph.dbg("prv", prv[:], [128, 14, NS], "s_prv")
    so = sst
    for half in range(4):
        pb, pk = getF()
        flat = pb[:].rearrange("p a b -> p (a b)")
        ms = list(range(half * 4, min(14, half * 4 + 4)))
        for q, m in enumerate(ms):
            ph.tr(flat[:NS, q * 128:(q + 1) * 128], cur[:, m, :], identf[:], R=["s_cur"], W=pk)
        ph.cp(V, so[:, ms[0] * 128:(ms[-1] + 1) * 128], flat[:NS, 0:len(ms) * 128], R=pk, W="s_sst")
    ph.dma("sp", I["s_shift"], so[:], R="s_sst")
    xs = XS[:, :, 0:NS]
    ph.tt(V, dd[:, :, 0:NS], prv[:], cur[:], ALU.subtract, R=["s_prv", "s_cur"], W="dd")
    ph.tt(V, dd[:, :, 0:NS], dd[:, :, 0:NS], bc(pc["mu_shift"][:, :].unsqueeze(2), [128, 14, NS]), ALU.mult,
          R=["dd", "c_mu_shift"], W="dd")
    ph.tt(V, xs, dd[:, :, 0:NS], cur[:], ALU.add, R=["dd", "s_cur"], W="XS")
    uf = L["uf"]; ub = L["ub"]; ZZb = L["ZZb"]
    ph.dma("act", uf[:, :, 0:NS], I["UU"][:, t0:t0 + n].rearrange("(m p) t -> p m t", p=128), W="uf")
    ph.cp("act", ub[:, :, 0:NS], uf[:, :, 0:NS], R="uf", W="ub")
    stx = [sb("s_stre", [NS, 2048], F32), sb("s_stim", [NS, 2048], F32)]
    ph.dma("sp", stx[0][:], I["st_re"], W="s_stx0"); ph.dma("sp", stx[1][:], I["st_im"], W="s_stx1")
    Xsm = sb("s_Xsm", [128, 2, 16, NS], F32)
    for ri in range(2):
        for q4 in range(4):
            pb, pk = getF()
            flat = pb[:].rearrange("p a b -> p (a b)")
            for q in range(4):
                P_ = q4 * 4 + q
                ph.tr(flat[:, q * NS:(q + 1) * NS], stx[ri][:, P_ * 128:(P_ + 1) * 128], identf[:NS, :NS],
                      R="s_stx%d" % ri, W=pk)
            ph.cp(V, Xsm[:, ri, q4 * 4:q4 * 4 + 4, :], flat[:, 0:4 * NS].rearrange("p (a b) -> p a b", b=NS), R=pk, W="s_Xsm")
    s5_sample(ph, I, G0, pc, Xsm, ub, ZZb, getF, stx)
    ph.dma("act", I["ZZ"][:, t0:t0 + n].rearrange("(m p) t -> p m t", p=128), ZZb[:, :, 0:NS], R="ZZb")
    rwkv_sample(ph, I, G0, L)
    ph.dma("sp", I["YF"][:, t0:t0 + n].rearrange("(m p) t -> p m t", p=128), L["YFb"][:, :, 0:NS], R="YFb")


def s5_sample(ph, I, G0, pc, Xsm, ub, ZZb, getF, stx):
    V = "dve"
    BwT, Kmat, CwT, Ab = G0["BwT"], G0["Kmat"], G0["CwT"], G0["Abar"]
    identf = G0["identf"]
    Xb = ph._s5xb
    ph.cp("act", Xb[:, :, :, 0:NS], Xsm[:], R="s_Xsm", W="Xb")
    du = ph._s5du
    for k in range(4):
        pb, pk = getF()
        flat = pb[:].rearrange("p a b -> p (a b)")
        ph.mm(flat[:, 0:NS], Kmat[:, k, 0, :], ub[:, k, 0:NS], True, False, R=["Kmat", "ub"], W=pk)
        for Pl in range(4):
            P_ = 4 * k + Pl
            for ri in range(2):
                ph.mm(flat[32 * Pl:32 * Pl + 32, 0:NS], CwT[:, 0, ri, P_, :], Xb[:, ri, P_, 0:NS], False,
                      ri == 1, R=["CwT", "Xb"], W=pk, tp=(0, 32 * Pl))
        ph.ts(V, du[:, 0:NS], ub[:, k, 0:NS], pc["D_skip"][:, k:k + 1], ALU.mult, R=["ub", "c_D_skip", "s5z"], W="s5du")
        ph.tt(V, du[:, 0:NS], du[:, 0:NS], flat[:, 0:NS], ALU.add, R=["s5du", pk], W="s5du")
        ph.act(ZZb[:, k, 0:NS], du[:, 0:NS], AF.Gelu_apprx_tanh, R="s5du", W=["ZZb", "s5z"])
    Gs = ph.sb("s_Gs", [128, 2, 16, NS], F32)
    for Pl in range(4):
        pb, pk = getF()
        flat = pb[:].rearrange("p a b -> p (a b)")
        for ri in range(2):
            for k in range(4):
                q = ri * 4 + k
                ph.mm(flat[:, q * NS:(q + 1) * NS], BwT[32 * Pl:32 * Pl + 32, k, CS - 1, ri, :],
                      ub[32 * Pl:32 * Pl + 32, k, 0:NS], True, True, R=["BwT", "ub"], W=pk,
                      tp=((96, 0) if Pl == 3 else None))
        for ri in range(2):
            ph.cp(V, Gs[:, ri, Pl:16:4, :], flat[:, ri * 4 * NS:(ri + 1) * 4 * NS].rearrange("p (q m) -> p q m", m=NS),
                  R=pk, W="s_Gs")
    A_r = bc(Ab[:, 1, 0, :].unsqueeze(2), [128, 16, NS]); A_i = bc(Ab[:, 1, 1, :].unsqueeze(2), [128, 16, NS])
    ta = ph.sb("s_ta", [128, 16, NS], F32)
    ph.tt(V, ta[:], Xsm[:, 0], A_r, ALU.mult, R=["s_Xsm", "Abar"], W="s_ta")
    ph.tt(V, Gs[:, 0], Gs[:, 0], ta[:], ALU.add, R=["s_Gs", "s_ta"], W="s_Gs")
    ph.tt(V, ta[:], Xsm[:, 1], A_i, ALU.mult, R=["s_Xsm", "Abar", "s_Gs"], W="s_ta")
    ph.tt(V, Gs[:, 0], Gs[:, 0], ta[:], ALU.subtract, R=["s_Gs", "s_ta"], W="s_Gs")
    ph.tt(V, ta[:], Xsm[:, 1], A_r, ALU.mult, R=["s_Xsm", "Abar", "s_Gs"], W="s_ta")
    ph.tt(V, Gs[:, 1], Gs[:, 1], ta[:], ALU.add, R=["s_Gs", "s_ta"], W="s_Gs")
    ph.tt(V, ta[:], Xsm[:, 0], A_i, ALU.mult, R=["s_Xsm", "Abar", "s_Gs"], W="s_ta")
    ph.tt(V, Gs[:, 1], Gs[:, 1], ta[:], ALU.add, R=["s_Gs", "s_ta"], W="s_Gs")
    for ri, nm in enumerate(("s_re", "s_im")):
        xo = stx[ri]
        for q4 in range(4):
            pb, pk = getF()
            flat = pb[:].rearrange("p a b -> p (a b)")
            for q in range(4):
                P_ = q4 * 4 + q
                ph.tr(flat[:NS, q * 128:(q + 1) * 128], Gs[:, ri, P_, :], identf[:], R="s_Gs", W=pk)
            ph.cp(V, xo[:, q4 * 512:(q4 + 1) * 512], flat[:NS, 0:512], R=pk, W="s_stx%d" % ri)
        ph.dma("sp", I[nm], xo[:], R="s_stx%d" % ri)


def rwkv_sample(ph, I, G0, L):
    V = "dve"
    sb = ph.sb
    pc = L["pc"]; getF, getT, ib = L["getF"], L["getT"], L["ib"]
    identf = G0["identf"]
    XS = L["XS"]
    sig, aa, gg, kk0, tq, rn, kkn = L["sig"], L["aa"], L["gg"], L["kk0"], L["tq"], L["rn"], L["kkn"]
    bb, kmod, bon = L["bb"], L["kmod"], L["bon"]
    lin, sgx, w2a2, g2b, blk64 = L["lin"], L["sgx"], L["w2a2"], L["g2b"], L["blk64"]
    n = NS
    r_ = XS[:, 0:4, 0:n]; k_ = XS[:, 4:8, 0:n]; v_ = XS[:, 8:12, 0:n]
    B4 = lambda t: bc(t[:, :].unsqueeze(2), [128, 4, n])
    S4 = lambda t: t[:, :, 0:n]
    ph.act(lin[0:64, 0:n], XS[0:64, 12, 0:n], AF.Tanh, R="XS", W="lin")
    ph.cp("act", lin[64:128, 0:n], XS[64:128, 12, 0:n], R="XS", W="lin")
    ph.act(sgx[:, 0:n], XS[:, 13, 0:n], AF.Sigmoid, R="XS", W="sgx")
    pw_, kw_ = getF(); pa_, ka_ = getF(); pg_, kg_ = getF()
    for m in range(4):
        ph.mm(pw_[:, m, 0:n], w2a2[0:64, m * 128:(m + 1) * 128], lin[0:64, 0:n], True, True, R=["w2a2", "lin"], W=kw_)
        ph.mm(pa_[:, m, 0:n], w2a2[64:128, m * 128:(m + 1) * 128], lin[64:128, 0:n], True, True, R=["w2a2", "lin"], W=ka_)
        ph.mm(pg_[:, m, 0:n], g2b[:, m * 128:(m + 1) * 128], sgx[:, 0:n], True, True, R=["g2b", "sgx"], W=kg_)
    for m in range(4):
        ph.act(sig[:, m, 0:n], pw_[:, m, 0:n], AF.Sigmoid, R=[kw_, "c_w0"], W="sig", bias=pc["w0"][:, m:m + 1])
        ph.act(aa[:, m, 0:n], pa_[:, m, 0:n], AF.Sigmoid, R=[ka_, "c_a0"], W="aa", bias=pc["a0"][:, m:m + 1])
    ph.cp("act", S4(gg), pg_[:, :, 0:n], R=kg_, W="gg")
    ph.tt(V, S4(kk0), k_, B4(pc["k_k"]), ALU.mult, R=["XS", "c_k_k"], W="kk0")
    ph.tt(V, S4(tq), S4(kk0), S4(kk0), ALU.mult, R="kk0", W="tq")
    pq, kq = getF()
    for m in range(4):
        ph.mm(pq[:, m, 0:n], blk64[:], tq[:, m, 0:n], True, True, R=["blk64", "tq"], W=kq)
    ph.act(S4(rn), pq[:, :, 0:n], AF.Sqrt, R=kq, W="rn")
    ph.ts(V, S4(rn), S4(rn), 1e-12, ALU.max, R="rn", W="rn")
    ph.op(V, lambda e: e.reciprocal(out=S4(rn), in_=S4(rn)), R="rn", W="rn")
    ph.tt(V, S4(kkn), S4(kk0), S4(rn), ALU.mult, R=["kk0", "rn"], W="kkn")
    ph.tt(V, S4(bb), S4(kkn), S4(aa), ALU.mult, R=["kkn", "aa"], W="bb")
    ph.tt(V, S4(tq), S4(aa), B4(pc["k_a"]), ALU.mult, R=["aa", "c_k_a", kq], W="tq")
    ph.tt(V, S4(tq), S4(tq), B4(pc["k_a"]), ALU.subtract, R=["tq", "c_k_a"], W="tq")
    ph.stt(S4(kmod), S4(tq), 1.0, k_, ALU.add, ALU.mult, R=["tq", "XS"], W="kmod")
    ph.tt(V, S4(tq), r_, S4(kmod), ALU.mult, R=["XS", "kmod"], W="tq")
    ph.tt(V, S4(tq), S4(tq), B4(pc["r_k"]), ALU.mult, R=["tq", "c_r_k"], W="tq")
    pq2, kq2 = getF()
    for m in range(4):
        ph.mm(pq2[:, m, 0:n], blk64[:], tq[:, m, 0:n], True, True, R=["blk64", "tq"], W=kq2)
    ph.tt(V, S4(bon), pq2[:, :, 0:n], v_, ALU.mult, R=[kq2, "XS"], W="bon")
    wdec = L["ex1"]
    ph.act(S4(wdec), S4(sig), AF.Exp, R="sig", W="ex1", scale=-C1)
    srcs = [r_, S4(wdec), S4(kmod), v_, S4(kkn), S4(bb)]
    keys = ["XS", "ex1", "kmod", "XS", "kkn", "bb"]
    tok = sb("s_tok", [NS, 6, 512], F32)
    for i, (src, kkey) in enumerate(zip(srcs, keys)):
        pb, pk = getF()
        flat = pb[:].rearrange("p a b -> p (a b)")
        for m in range(4):
            ph.tr(flat[:NS, m * 128:(m + 1) * 128], src[:, m, :], identf[:], R=kkey, W=pk)
        ph.cp(V if i % 2 else "act", tok[:, i, :], flat[:NS, 0:512], R=pk, W="s_tok")
    ph.dma("sp", I["SW"].rearrange("i b f -> b i f"), tok[:], R="s_tok", W="SWd")
    vec = sb("s_vec", [128, 6, 64], F32)
    ph.dma("sp", vec[:], I["SW"].rearrange("i b (h k) -> (b h) i k", h=8), R="SWd", W="s_vec")
    S0 = sb("s_S0", [128, 64, 64], F32)
    ph.dma("act", S0[:].rearrange("p a b -> p (a b)"), I["st_wkv"], W="s_S0")
    tmp = sb("s_tmp", [128, 64, 64], F32)
    sa = sb("s_sa", [128, 64], F32); yv = sb("s_yv", [128, 64], F32); kka = sb("s_kka", [128, 64], F32)
    kB = lambda i: bc(vec[:, i, :].unsqueeze(1), [128, 64, 64])
    ph.tt(V, tmp[:], S0[:], kB(4), ALU.mult, R=["s_S0", "s_vec"], W="s_tmp")
    ph.op(V, lambda e: e.tensor_reduce(out=sa[:], in_=tmp[:], axis=AX.X, op=ALU.add), R="s_tmp", W="s_sa")
    ph.tt(V, S0[:], S0[:], kB(1), ALU.mult, R=["s_S0", "s_vec", "s_tmp"], W="s_S0")
    ph.tt(V, tmp[:], bc(sa[:, :].unsqueeze(2), [128, 64, 64]), kB(5), ALU.mult, R=["s_sa", "s_vec"], W="s_tmp")
    ph.tt(V, S0[:], S0[:], tmp[:], ALU.subtract, R=["s_S0", "s_tmp"], W="s_S0")
    ph.tt(V, tmp[:], bc(vec[:, 3, :].unsqueeze(2), [128, 64, 64]), kB(2), ALU.mult, R=["s_vec", "s_S0"], W="s_tmp")
    ph.tt(V, S0[:], S0[:], tmp[:], ALU.add, R=["s_S0", "s_tmp"], W="s_S0")
    ph.dma("act", I["s_wkv"], S0[:].rearrange("p a b -> p (a b)"), R="s_S0")
    ph.tt(V, tmp[:], S0[:], kB(0), ALU.mult, R=["s_S0", "s_vec"], W="s_tmp")
    ph.op(V, lambda e: e.tensor_reduce(out=yv[:], in_=tmp[:], axis=AX.X, op=ALU.add), R="s_tmp", W="s_yv")
    ph.dma("sp", I["SY"], yv[:], R="s_yv", W="SYd")
    Ysb = L["Ysb"]
    ph.dma("sp", Ysb[:NS].rearrange("p a b -> p (a b)"), I["SY"].rearrange("(b h) v -> b (h v)", h=8), R="SYd", W="Ysb")
    groupnorm_out(ph, L, 0, NS)


def phase3(nc, I, G0, W3, WFI):
    ph = Ph(nc, "p3")
    V = "dve"
    W3 = alloc_w3(nc, ph.st)
    load_w3(ph, I, W3)
    rwo, glu, wo = W3["rwo"], W3["glu"], W3["wo"]
    for k in range(8):
        ph.dma("pool", WFI[:, k, :], I["w_ffn_in"][k * 128:(k + 1) * 128, :], W="wfi_pre")
    yf = ph.sb("yf", [128, 4, 512], BF16); zz = ph.sb("zz", [128, 4, 512], BF16); gt = ph.sb("gt", [128, 16, 512], BF16)
    trw = ph.sb("trw", [128, 8, 512], F32); mg = ph.sb("mg", [128, 8, 512], BF16)
    sgb = [ph.sb("sgb%d" % i, [128, 512], F32) for i in range(2)]
    s5t = [ph.sb("s5t%d" % i, [128, 512], F32) for i in range(2)]
    xts = [ph.sb("xt%d" % i, [128, D], F32) for i in range(2)]
    pm = [ph.ps("pm%d" % i, [128, 512], F32) for i in range(6)]
    npm = nx = ns = 0
    for (t0, nt) in BLOCKS:
        P = min(128, nt)
        r3 = lambda name: I[name][:, t0:t0 + nt].rearrange("(m p) t -> p m t", p=128)
        ph.dma("sp", yf[:, :, :nt], r3("YF"), W="yf"); ph.dma("sp", zz[:, :, :nt], r3("ZZ"), W="zz")
        ph.dma("act", gt[:, :, :nt], r3("GT"), W="gt")
        for m in range(8):
            pb = pm[npm % 6]; pk = "pm%d" % (npm % 6); npm += 1
            for k in range(4):
                ph.mm(pb[:, :nt], rwo[:, k, m * 128:(m + 1) * 128], yf[:, k, :nt], k == 0, k == 3, R=["rwo", "yf"], W=pk)
            ph.tt(V, trw[:, m, :nt], pb[:, :nt], gt[:, m, :nt], ALU.mult, R=[pk, "gt"], W="trw%d" % m)
        for m in range(8):
            pa = pm[npm % 6]; pka = "pm%d" % (npm % 6); npm += 1
            pb = pm[npm % 6]; pkb = "pm%d" % (npm % 6); npm += 1
            for k in range(4):
                ph.mm(pa[:, :nt], glu[:, k, m * 128:(m + 1) * 128], zz[:, k, :nt], k == 0, k == 3, R=["glu", "zz"], W=pka)
            for k in range(4):
                ph.mm(pb[:, :nt], glu[:, k, D + m * 128:D + (m + 1) * 128], zz[:, k, :nt], k == 0, k == 3,
                      R=["glu", "zz"], W=pkb)
            sg = sgb[ns % 2]; sk = "sgb%d" % (ns % 2); s5 = s5t[ns % 2]; s5k = "s5t%d" % (ns % 2); ns += 1
            ph.act(sg[:, :nt], pb[:, :nt], AF.Sigmoid, R=pkb, W=sk)
            ph.tt(V, s5[:, :nt], pa[:, :nt], sg[:, :nt], ALU.mult, R=[pka, sk], W=s5k)
            ph.tt(V, s5[:, :nt], s5[:, :nt], gt[:, 8 + m, :nt], ALU.mult, R=[s5k, "gt"], W=s5k)
            ph.tt(V, mg[:, m, :nt], s5[:, :nt], trw[:, m, :nt], ALU.add, R=[s5k, "trw%d" % m], W="mg")
        for s in range((nt + 127) // 128):
            xt = xts[nx % 2]; xk = "xt%d" % (nx % 2); nx += 1
            rows = slice(t0 + s * 128, t0 + s * 128 + P)
            ph.dma("sp", xt[:P, :], I["xall"][rows, :], W=xk)
            for half in range(2):
                pb = pm[npm % 6]; pk = "pm%d" % (npm % 6); npm += 1
                for k in range(8):
                    ph.mm(pb[:P, :], mg[:, k, s * 128:s * 128 + P], wo[:, k, half * 512:(half + 1) * 512], k == 0, k == 7,
                          R=["mg", "wo"], W=pk)
                ph.tt(V, xt[:P, half * 512:(half + 1) * 512], xt[:P, half * 512:(half + 1) * 512], pb[:P, :], ALU.add,
                      R=[pk, xk], W=xk)
            ph.dma("pool", I["X1"][rows, :], xt[:P, :], R=xk)
    ph.finish()


def phase4(nc, I, G0, WFI):
    ph = Ph(nc, "p4")
    V = "dve"
    G = norm_scratch(ph, G0)
    identf = G0["identf"]
    wfi = WFI; wfo = ph.sb("wfo", [128, 22, D], BF16)
    for k in range(22):
        ph.dma("pool", wfo[:, k, :], I["w_ffn_out"][k * 128:(k + 1) * 128, :], W="wfo")
    g2c = ph.sb("g2c", [128, 8], F32); load_col(ph, g2c[:], I["ln2_g"], 8, "g2c")
    cw = ph.sb("cw", [128, 3, 22], F32); cb = ph.sb("cb", [128, 22], F32)
    ph.dma("sp", cw[:], I["conv_w"].rearrange("t (f p) -> p t f", p=128), W="cw", slow=True)
    load_col(ph, cb[:], I["conv_b"], 22, "cb")
    hTs = [ph.sb("hT%d" % i, [128, 8, 512], BF16) for i in range(2)]
    hid = ph.sb("hid", [128, 22, 512], BF16)
    xts = [ph.sb("xt%d" % i, [128, D], F32) for i in range(2)]
    At = [ph.sb("At%d" % i, [128, 514], F32) for i in range(2)]
    acc = [ph.sb("acc%d" % i, [128, 512], F32) for i in range(2)]
    cc = ph.sb("cc", [128, 22, 2], F32)
    ph.memset(V, cc[:].rearrange("p a b -> p (a b)"), 0.0, W="cc")
    pm = [ph.ps("pm%d" % i, [128, 512], F32) for i in range(6)]
    scs = ph.sb("scs", [NS, 2816], F32)
    scT = ph.sb("scT", [128, 22, 2, NS], F32)
    aout = scs
    npm = na = 0
    NR, FI, FO = [], [], []
    for bi_, (t0, nt) in enumerate(BLOCKS):
        P = min(128, nt)
        hT = hTs[bi_ % 2]; hk = "hT%d" % (bi_ % 2)
        sample = nt < 128
        nsub = (nt + 127) // 128
        ph.rec_begin()
        for s in range(nsub):
            rows = slice(t0 + s * 128, t0 + s * 128 + P)
            ph.dma("sp", xts[s % 2][:P, :], I["X1"][rows, :], W="xt%d" % (s % 2))
            rms_to_hT(ph, G, xts[s % 2], P, g2c, hT, s * 128, str(s % 2), "g2c", hk)
        NR.append(ph.rec_end())
        ph.rec_begin()
        if sample:
            for tt_ in range(2):
                ph.dma("sp", scs[:], I["st_conv"][:, tt_, :], W="scs")
                for q in range(6):
                    pb = pm[npm % 6]; pk = "pm%d" % (npm % 6); npm += 1
                    fs = list(range(q * 4, min(22, q * 4 + 4)))
                    for j, f_ in enumerate(fs):
                        ph.tr(pb[:, j * NS:(j + 1) * NS], scs[:, f_ * 128:(f_ + 1) * 128], identf[:NS, :NS], R="scs", W=pk)
                    ph.cp(V, scT[:, fs[0]:fs[-1] + 1, tt_, :], pb[:, 0:len(fs) * NS].rearrange("p (a b) -> p a b", b=NS),
                          R=pk, W="scT")
        for f in range(22):
            pa = pm[npm % 6]; pka = "pm%d" % (npm % 6); npm += 1
            pb = pm[npm % 6]; pkb = "pm%d" % (npm % 6); npm += 1
            for k in range(8):
                ph.mm(pa[:, :nt], wfi[:, k, f * 128:(f + 1) * 128], hT[:, k, :nt], k == 0, k == 7, R=["wfi", hk], W=pka)
            for k in range(8):
                ph.mm(pb[:, :nt], wfi[:, k, 2816 + f * 128:2816 + (f + 1) * 128], hT[:, k, :nt], k == 0, k == 7,
                      R=["wfi", hk], W=pkb)
            A = At[na % 2]; ak = "At%d" % (na % 2); ac = acc[na % 2]; ck = "acc%d" % (na % 2); na += 1
            ph.cp("act", A[:, 2:2 + nt], pa[:, :nt], R=pka, W=ak)
            if not sample:
                ph.cp(V, A[:, 0:2], cc[:, f, :], R="cc", W=ak)
                a0, a1, a2 = A[:, 0:nt], A[:, 1:1 + nt], A[:, 2:2 + nt]
            else:
                a0, a1, a2 = scT[:, f, 0, :], scT[:, f, 1, :], A[:, 2:2 + nt]
            ph.ts(V, ac[:, :nt], a0, cw[:, 0, f:f + 1], ALU.mult, cb[:, f:f + 1], ALU.add, R=[ak, "scT", "cw", "cb"], W=ck)
            ph.stt(ac[:, :nt], a1, cw[:, 1, f:f + 1], ac[:, :nt], ALU.mult, ALU.add, R=[ak, "scT", "cw", ck], W=ck)
            ph.stt(ac[:, :nt], a2, cw[:, 2, f:f + 1], ac[:, :nt], ALU.mult, ALU.add, R=[ak, "cw", ck], W=ck)
            ph.act(ac[:, :nt], ac[:, :nt], AF.Gelu_apprx_tanh, R=ck, W=ck)
            ph.tt(V, hid[:, f, :nt], ac[:, :nt], pb[:, :nt], ALU.mult, R=[ck, pkb], W="hid")
            if not sample:
                ph.cp(V, cc[:, f, :], A[:, nt:nt + 2], R=ak, W="cc")
            else:
                po = pm[npm % 6]; pko = "pm%d" % (npm % 6); npm += 1
                ph.tr(po[:NS, 0:128], A[:, 2:2 + NS], identf[:], R=ak, W=pko)
                ph.cp(V, aout[:, f * 128:(f + 1) * 128], po[:NS, 0:128], R=pko, W="scs")
        if t0 + nt == T:
            for tt_ in range(2):
                ph.dma("sp", I["p_conv"][tt_].rearrange("(f p) -> p f", p=128), cc[:, :, tt_], R="cc", slow=True)
        if sample:
            ph.dma("sp", I["s_conv"][:, 1, :], aout[:], R="scs")
            ph.dma("act", I["s_conv"][:, 0, :], I["st_conv"][:, 1, :])
        FI.append(ph.rec_end())
        ph.rec_begin()
        for s in range(nsub):
            rows = slice(t0 + s * 128, t0 + s * 128 + P)
            xt = xts[s % 2]; xk = "xt%d" % (s % 2)
            ph.dma("sp", xt[:P, :], I["X1"][rows, :], W=xk)
            for half in range(2):
                pb = pm[npm % 6]; pk = "pm%d" % (npm % 6); npm += 1
                for f in range(22):
                    ph.mm(pb[:P, :], hid[:, f, s * 128:s * 128 + P], wfo[:, f, half * 512:(half + 1) * 512], f == 0, f == 21,
                          R=["hid", "wfo"], W=pk)
                ph.tt(V, xt[:P, half * 512:(half + 1) * 512], xt[:P, half * 512:(half + 1) * 512], pb[:P, :],
                      ALU.add, R=[pk, xk], W=xk)
            ph.dma("pool", I["X2"][rows, :], xt[:P, :], R=xk)
        FO.append(ph.rec_end())
    ph.play(NR[0])
    for b_ in range(len(BLOCKS)):
        ph.play(FI[b_], NR[b_ + 1] if b_ + 1 < len(BLOCKS) else [])
        ph.play(FO[b_])
    ph.finish()


def phase5(nc, I, G0):
    ph = Ph(nc, "p5")
    V = "dve"
    Ga = norm_scratch(ph, G0, "a")
    Gb = norm_scratch(ph, G0, "b", eps=Ga["eps"])
    wpg = ph.sb("wpg", [128, 8, D], BF16); wpl = ph.sb("wpl", [128, 2, D], BF16)
    for k in range(8):
        ph.dma("pool", wpg[:, k, :], I["w_ple_gate"][k * 128:(k + 1) * 128, :], W="wpg")
    for k in range(2):
        ph.dma("pool", wpl[:, k, :], I["w_ple"][k * 128:(k + 1) * 128, :], W="wpl")
    g3c = ph.sb("g3c", [128, 8], F32); load_col(ph, g3c[:], I["ln3_g"], 8, "g3c")
    fg = ph.sb("fg", [128, D], F32)
    ph.dma("sp", fg[:], I["final_g"].partition_broadcast(128), W="fg")
    hTs = [ph.sb("hT%d" % i, [128, 8, 128], BF16) for i in range(2)]
    xts = [ph.sb("xt%d" % i, [128, D], F32) for i in range(2)]
    pball = ph.sb("pball", [128, 17, 256], BF16)
    _i = 0
    for (t0_, nt_) in BLOCKS:
        P_ = min(128, nt_)
        for s_ in range((nt_ + 127) // 128):
            ph.dma("pool", pball[:P_, _i, :], I["pall"][t0_ + s_ * 128:t0_ + s_ * 128 + P_, :], W="pb%d" % _i)
            _i += 1
    pTss = [ph.sb("pTs%d" % i, [128, 2, 128], BF16) for i in range(2)]
    sg = [ph.sb("sg%d" % i, [128, 512], F32) for i in range(2)]
    yo = [ph.sb("yo%d" % i, [128, D], F32) for i in range(2)]
    pm = [ph.ps("pm%d" % i, [128, 512], F32) for i in range(4)]
    pqs = [ph.ps("pq%d" % i, [128, 8, 128], BF16) for i in range(2)]
    npm = nx = nsg = 0
    for (t0, nt) in BLOCKS:
        P = min(128, nt)
        for s in range((nt + 127) // 128):
            rows = slice(t0 + s * 128, t0 + s * 128 + P)
            i2 = nx % 2; nx += 1
            xt = xts[i2]; xk = "xt%d" % i2; pbt = pball[:, nx - 1, :]; pbk = "pb%d" % (nx - 1)
            G = (Ga, Gb)[i2]; hT = hTs[i2]; hk = "hT%d" % i2; pTs = pTss[i2]; ptk = "pTs%d" % i2
            pq = pqs[i2]; pqk = "pq%d" % i2
            ph.dma("sp", xt[:P, :], I["X2"][rows, :], W=xk)
            rms_to_hT(ph, G, xt, P, g3c, hT, 0, str(i2), "g3c", hk)
            for k in range(2):
                ph.tr(pq[:, k, :P], pbt[:P, k * 128:(k + 1) * 128], G0["identb"][:P, :P], R=[pbk, "identb"], W=pqk)
            ph.cp("act", pTs[:, :, :P], pq[:, 0:2, :P], R=pqk, W=ptk)
            for half in range(2):
                cs_ = slice(half * 512, (half + 1) * 512)
                pg = pm[npm % 4]; pgk = "pm%d" % (npm % 4); npm += 1
                pe = pm[npm % 4]; pek = "pm%d" % (npm % 4); npm += 1
                for k in range(8):
                    ph.mm(pg[:P, :], hT[:, k, :P], wpg[:, k, cs_], k == 0, k == 7, R=[hk, "wpg"], W=pgk)
                for k in range(2):
                    ph.mm(pe[:P, :], pTs[:, k, :P], wpl[:, k, cs_], k == 0, k == 1, R=[ptk, "wpl"], W=pek)
                sgt = sg[nsg % 2]; sgk = "sg%d" % (nsg % 2); nsg += 1
                ph.act(sgt[:P, :], pg[:P, :], AF.Sigmoid, R=pgk, W=sgk)
                ph.tt(V, sgt[:P, :], sgt[:P, :], pe[:P, :], ALU.mult, R=[sgk, pek], W=sgk)
                ph.tt(V, xt[:P, cs_], xt[:P, cs_], sgt[:P, :], ALU.add, R=[sgk, xk, "xn" + G["sx"]], W=xk)
            ss = G["ss"]; sq = G["sq"]; kss = "ss" + G["sx"]; ksq = "sq" + G["sx"]
            ph.act(sq[:P, :], xt[:P, :], AF.Square, R=xk, W=[ksq, kss], accum=ss[:P, 0:1])
            ph.act(ss[:P, 1:2], ss[:P, 0:1], AF.Sqrt, R=[kss, "eps"], W=kss, bias=G["eps"][:P, 0:1], scale=1.0 / D)
            ph.op(V, lambda e, ss=ss, P=P: e.reciprocal(out=ss[:P, 3:4], in_=ss[:P, 1:2]), R=kss, W=kss + "3")
            y = yo[i2]; yk = "yo%d" % i2
            ph.stt(y[:P, :], xt[:P, :], ss[:P, 3:4], fg[:P, :], ALU.mult, ALU.mult, R=[xk, kss + "3", "fg"], W=yk)
            ph.dma("pool", I["y"][rows, :], y[:P, :], R=yk)
    ph.finish()


_CACHE = {}


def _consts():
    i = np.arange(128)
    c = {}
    c["c_ident"] = np.eye(128, dtype=np.float32)
    c["c_msl"] = (i[None, :] < i[:, None]).astype(np.float32)
    c["c_msu"] = (i[:, None] < i[None, :]).astype(np.float32)
    c["c_mui"] = (i[:, None] <= i[None, :]).astype(np.float32)
    c["c_blk64"] = ((i[:, None] // 64) == (i[None, :] // 64)).astype(np.float32)
    c["c_blk32"] = ((i[:, None] // 32) == (i[None, :] // 32)).astype(np.float32)
    c["c_rowgp"] = (((i[:, None] // 16) % 2) == (i[None, :] // 64)).astype(np.float32)
    return c


def make_in_maps(inp):
    f = lambda a: np.ascontiguousarray(np.asarray(a, dtype=np.float32))
    cst = _consts()
    shared = {}
    for k in ("ln1_g", "w_in", "mu_shift", "w0", "w2", "a0", "a2", "g2", "k_k", "k_a", "lnx_g", "lnx_b", "w_rw_out",
              "A_re", "A_im", "log_dt", "B_re", "B_im", "D_skip", "w_glu", "w_out", "ln2_g", "w_ffn_in", "conv_w",
              "conv_b", "w_ffn_out", "ln3_g", "w_ple_gate", "w_ple"):
        shared[k] = f(inp[k])[0]
    shared["r_k"] = f(inp["r_k"])[0].reshape(512)
    shared["C_re"] = f(inp["C_re"])[0].reshape(512, 64)
    shared["C_im"] = f(inp["C_im"])[0].reshape(512, 64)
    shared["final_g"] = f(inp["final_g"])
    shared.update(cst)
    xp, xs = f(inp["x_prompt"]), f(inp["x_sample"])
    pp, psm = f(inp["p_prompt"])[0], f(inp["p_sample"])[0]
    in_maps = []
    for c in range(8):
        sl = slice(NS * c, NS * c + NS)
        m = dict(shared)
        m["xall"] = np.concatenate([xp[c], xs[sl, 0]], 0)
        m["pall"] = np.concatenate([pp[c], psm[sl, 0]], 0)
        m["st_shift"] = f(inp["state_shift"])[0, sl]
        m["st_wkv"] = f(inp["state_wkv"])[0, sl].reshape(128, 4096)
        m["st_re"] = f(inp["state_ssm_re"])[0, sl].reshape(NS, 2048)
        m["st_im"] = f(inp["state_ssm_im"])[0, sl].reshape(NS, 2048)
        m["st_conv"] = f(inp["state_conv"])[0, sl]
        in_maps.append({k: np.ascontiguousarray(v) for k, v in m.items()})
    return in_maps


def kernel(**inp):
    f = lambda a: np.ascontiguousarray(np.asarray(a, dtype=np.float32))
    if "nc" not in _CACHE:
        _CACHE["nc"] = build_program()
    nc = _CACHE["nc"]
    in_maps = make_in_maps(inp)
    res = run_bass_kernel_spmd(nc, in_maps, core_ids=list(range(8)))
    R = res.results
    cat = lambda fn: np.stack([fn(r) for r in R], 0)
    y_prompt = cat(lambda r: r["y"][:T])
    y_sample = np.concatenate([r["y"][T:] for r in R], 0)[:, None, :]
    p_shift = cat(lambda r: r["p_shift"])[None]
    p_wkv = cat(lambda r: r["p_wkv"].reshape(8, 64, 64).transpose(0, 2, 1))[None]
    p_re = cat(lambda r: r["p_re"].reshape(32, 64))[None]
    p_im = cat(lambda r: r["p_im"].reshape(32, 64))[None]
    p_conv = cat(lambda r: r["p_conv"])[None]
    s_shift = np.concatenate([r["s_shift"] for r in R], 0)[None]
    s_wkv = np.concatenate([r["s_wkv"].reshape(NS, 8, 64, 64) for r in R], 0)[None]
    s_re = np.concatenate([r["s_re"].reshape(NS, 32, 64) for r in R], 0)[None]
    s_im = np.concatenate([r["s_im"].reshape(NS, 32, 64) for r in R], 0)[None]
    s_conv = np.concatenate([r["s_conv"] for r in R], 0)[None]
    outs = (y_prompt, y_sample, p_shift, p_wkv, p_re, p_im, p_conv, s_shift, s_wkv, s_re, s_im, s_conv)
    return tuple(np.ascontiguousarray(o.astype(np.float32)) for o in outs)
```

```python
import contextlib
import math
import numpy as np
import concourse.bass as bass
import concourse.mybir as mybir
from concourse.bass_utils import run_bass_kernel_spmd

F32 = mybir.dt.float32
BF16 = mybir.dt.bfloat16
AF = mybir.ActivationFunctionType
ALU = mybir.AluOpType
AX = mybir.AxisListType

T = 2048
NS = 16
NT = T + NS
D = 1024
CS = 8
SCAN_ENG = "pool"
import os as _os
BUB = int(_os.environ.get("K_BUB", "48"))
BUB2 = int(_os.environ.get("K_BUB2", "0"))
SYO = float(_os.environ.get("K_SYO", "0.5"))
C1 = math.exp(-0.5)
BLOCKS = [(0, 512), (512, 512), (1024, 512), (1536, 512), (2048, 16)]

ENGS = ("pe", "act", "dve", "pool", "sp")
NDSEM = 12


class _Op:
    __slots__ = ("eng", "fn", "deps", "dma", "observed", "tok", "idx", "dslot")

    def __init__(self, eng, fn, dma):
        self.eng, self.fn, self.dma = eng, fn, dma
        self.deps = set()
        self.observed = False
        self.tok = None
        self.dslot = None


class Sched:
    def __init__(self, nc):
        self.nc = nc
        self.ops = []
        self.last_w = {}
        self.readers = {}
        self.dma_rr = {e: 0 for e in ENGS}
        self.dma_prev = {}
        self.excl = set()

    def _add(self, eng, fn, reads, writes, dma):
        op = _Op(eng, fn, dma)
        op.idx = len(self.ops)
        if self.excl:
            ex = tuple(b for b in reads if b in self.excl)
            if ex:
                writes = tuple(writes) + ex
        for b in reads:
            w = self.last_w.get(b)
            if w is not None:
                op.deps.add(w)
        for b in writes:
            w = self.last_w.get(b)
            if w is not None:
                op.deps.add(w)
            for r in self.readers.get(b, ()):
                op.deps.add(r)
        if dma:
            slot = (eng, self.dma_rr[eng] % NDSEM)
            self.dma_rr[eng] += 1
            op.dslot = slot
            prev = self.dma_prev.get(slot)
            if prev is not None:
                op.deps.add(prev)
            self.dma_prev[slot] = op.idx
        op.deps.discard(op.idx)
        self.ops.append(op)
        for b in writes:
            self.last_w[b] = op.idx
            self.readers[b] = []
        for b in reads:
            if b not in writes:
                self.readers.setdefault(b, []).append(op.idx)
        return op.idx

    def emit(self):
        nc = self.nc
        ops = self.ops
        need = []
        for op in ops:
            nd = []
            for d in op.deps:
                p = ops[d]
                if (not p.dma) and (not op.dma) and p.eng == op.eng == "pe":
                    continue
                nd.append(d)
                p.observed = True
            need.append(nd)
        last = {}
        for op in ops:
            key = op.dslot if op.dma else op.eng
            last[key] = op.idx
        for i in last.values():
            ops[i].observed = True
        g = getattr(nc, "_gsem", None)
        if g is None:
            g = {"sems": {}, "cnt": {e: 0 for e in ENGS}, "dcnt": {}}
            nc._gsem = g
        cnt = g["cnt"]
        dcnt = g["dcnt"]
        for op in ops:
            if op.dma:
                dcnt[op.dslot] = dcnt.get(op.dslot, 0) + 16
                op.tok = (op.dslot, dcnt[op.dslot])
            elif op.observed:
                cnt[op.eng] += 1
                op.tok = (op.eng, cnt[op.eng])
        sems = g["sems"]
        for k in list(ENGS) + sorted(set(o.dslot for o in ops if o.dma)):
            if k not in sems:
                nm = k if isinstance(k, str) else "d_%s_%d" % k
                sems[k] = nc.alloc_semaphore(name="s_" + nm)
        with contextlib.ExitStack() as st:
            block = st.enter_context(nc.Block())
            per = {e: [o for o in ops if o.eng == e] for e in ENGS}
            hw = {"pe": block.tensor, "act": block.scalar, "dve": block.vector,
                  "pool": block.gpsimd, "sp": block.sync}

            def make(e):
                def body(eng):
                    seen = {}
                    for op in per[e]:
                        waits = {}
                        for d in need[op.idx]:
                            k, v = ops[d].tok
                            if v > waits.get(k, 0):
                                waits[k] = v
                        for k, v in waits.items():
                            if seen.get(k, 0) >= v:
                                continue
                            seen[k] = v
                            eng.wait_ge(sems[k], v)
                        ins = op.fn(eng)
                        if op.dma:
                            ins.then_inc(sems[op.tok[0]], 16)
                        elif op.observed:
                            ins.then_inc(sems[e], 1)
                    if e == "sp":
                        for key, i in last.items():
                            k, v = ops[i].tok
                            if seen.get(k, 0) < v:
                                eng.wait_ge(sems[k], v)
                return body

            for e in ENGS:
                hw[e](make(e))


def _L(x):
    if x is None:
        return ()
    if isinstance(x, str):
        return (x,)
    return tuple(x)


class Ph:
    _uid = [0]

    def __init__(self, nc, tag):
        self.nc = nc
        self.tag = tag
        self.st = contextlib.ExitStack()
        self.S = Sched(nc)

    def sb(self, name, shape, dt):
        return self.st.enter_context(self.nc.sbuf_tensor(self.tag + "_" + name, list(shape), dt))

    def ps(self, name, shape, dt):
        self.S.excl.add(name)
        return self.st.enter_context(self.nc.psum_tensor(self.tag + "_" + name, list(shape), dt))

    def finish(self):
        self.S.emit()
        self.st.close()

    def dbg(self, name, ap, shape, key, dt=F32):
        import os
        if os.environ.get("K_DBG_DUMP", "") == "":
            return
        t = self.nc.dram_tensor("dbg_" + name, list(shape), dt, kind="ExternalOutput").ap()
        self.dma("sp", t, ap, R=key)

    _rec = None

    def rec_begin(self):
        self._rec = []

    def rec_end(self):
        r, self._rec = self._rec, None
        return r

    def bubble(self, k):
        if self._rec is not None:
            self._rec.append(("bubble", k))

    def merge(self, *streams, spans=None):
        if spans is None:
            spans = [(0.0, 1.0)] * len(streams)
        keep = [i for i, st_ in enumerate(streams) if st_]
        spans = [spans[i] for i in keep]
        streams = [streams[i] for i in keep]
        pos = [0] * len(streams)
        out = []
        while True:
            best, bi = None, -1
            for i, st_ in enumerate(streams):
                if pos[i] < len(st_):
                    f = spans[i][0] + spans[i][1] * (pos[i] + 1.0) / len(st_)
                    if best is None or f < best:
                        best, bi = f, i
            if bi < 0:
                break
            item = streams[bi][pos[bi]]
            pos[bi] += 1
            if item[0] == "bubble":
                left = item[1]
                prog = True
                while left > 0 and prog:
                    prog = False
                    for j in range(len(streams)):
                        if j != bi and pos[j] < len(streams[j]) and left > 0:
                            it2 = streams[j][pos[j]]
                            pos[j] += 1
                            prog = True
                            if it2[0] != "bubble":
                                out.append(it2)
                                left -= 1
                continue
            out.append(item)
        return out

    def play(self, *streams, spans=None):
        for item in self.merge(*streams, spans=spans):
            if item[0] == "bubble":
                continue
            eng, fn, R, W, dma = item
            self.S._add(eng, fn, R, W, dma)

    def op(self, eng, fn, R=None, W=None):
        if self._rec is not None:
            self._rec.append((eng, fn, _L(R), _L(W), False))
        else:
            self.S._add(eng, fn, _L(R), _L(W), False)

    def dma(self, q, out, in_, R=None, W=None, slow=False):
        if slow:
            fn = lambda e: e.dma_start(out=out, in_=in_, allow_slow_non_contiguous=True)
        else:
            fn = lambda e: e.dma_start(out=out, in_=in_)
        if self._rec is not None:
            self._rec.append((q, fn, _L(R), _L(W), True))
        else:
            self.S._add(q, fn, _L(R), _L(W), True)

    def tt(self, eng, out, in0, in1, op, R=None, W=None):
        self.op(eng, lambda e: e.tensor_tensor(out=out, in0=in0, in1=in1, op=op), R, W)

    def ts(self, eng, out, in0, s1, op0, s2=None, op1=None, R=None, W=None):
        if op1 is None:
            self.op(eng, lambda e: e.tensor_scalar(out=out, in0=in0, scalar1=s1, scalar2=None, op0=op0), R, W)
        else:
            self.op(eng, lambda e: e.tensor_scalar(out=out, in0=in0, scalar1=s1, scalar2=s2, op0=op0, op1=op1), R, W)

    def stt(self, out, in0, scalar, in1, op0, op1, R=None, W=None):
        self.op("dve", lambda e: e.scalar_tensor_tensor(out=out, in0=in0, scalar=scalar, in1=in1, op0=op0, op1=op1), R, W)

    def act(self, out, in_, func, R=None, W=None, bias=None, scale=1.0, accum=None):
        kw = {}
        if bias is not None:
            kw["bias"] = bias
        if accum is not None:
            kw["accum_out"] = accum
        self.op("act", lambda e: e.activation(out=out, in_=in_, func=func, scale=scale, **kw), R, W)

    def cp(self, eng, out, in_, R=None, W=None):
        if eng == "act":
            self.op("act", lambda e: e.activation(out=out, in_=in_, func=AF.Copy), R, W)
        else:
            self.op(eng, lambda e: e.tensor_copy(out=out, in_=in_), R, W)

    def mm(self, out, lhsT, rhs, start, stop, R=None, W=None, tp=None):
        if tp is None:
            self.op("pe", lambda e: e.matmul(out, lhsT=lhsT, rhs=rhs, start=start, stop=stop), R, W)
        else:
            self.op("pe", lambda e: e.matmul(out, lhsT=lhsT, rhs=rhs, start=start, stop=stop, tile_position=tp), R, W)

    def tr(self, out, in_, ident, R=None, W=None):
        self.op("pe", lambda e: e.transpose(out, in_, ident), R, W)

    def memset(self, eng, ap, v, W=None):
        self.op(eng, lambda e: e.memset(ap, v), None, W)


def bc(ap, shape):
    return ap.to_broadcast(list(shape))


def rms_to_hT(ph, G, xt, P, gcol, hT, c0, tag, gkey, hkey="hT"):
    sq, ss, xn, pT = G["sq"], G["ss"], G["xn"], G["pT"]
    x_ = G.get("sx", "")
    ksq, kss, kxn, kpT = "sq" + x_, "ss" + x_, "xn" + x_, "pT" + x_
    ph.act(sq[:P, :], xt[:P, :], AF.Square, R="xt" + tag, W=[ksq, kss], accum=ss[:P, 0:1])
    ph.act(ss[:P, 1:2], ss[:P, 0:1], AF.Sqrt, R=[kss, "eps"], W=kss, bias=G["eps"][:P, 0:1], scale=1.0 / D)
    ph.op("dve", lambda e: e.reciprocal(out=ss[:P, 2:3], in_=ss[:P, 1:2]), R=kss, W=kss)
    ph.ts("dve", xn[:P, :], xt[:P, :], ss[:P, 2:3], ALU.mult, R=["xt" + tag, kss], W=kxn)
    for k in range(8):
        ph.tr(pT[:, k, :P], xn[:P, k * 128:(k + 1) * 128], G["identb"][:P, :P], R=[kxn, "identb"], W=kpT)
    ph.tt("dve", hT[:, :, c0:c0 + P], pT[:, :, :P], bc(gcol[:, :].unsqueeze(2), [128, 8, P]), ALU.mult,
          R=[kpT, gkey], W=hkey)


def load_col(ph, dst, src1d, n, key):
    ph.dma("sp", dst, src1d.rearrange("(k p) -> p k", p=128), W=key, slow=True)


def norm_scratch(ph, G0, sx="", eps=None):
    G = dict(G0)
    G["sx"] = sx
    G["sq"] = ph.sb("sq" + sx, [128, D], F32)
    G["ss"] = ph.sb("ss" + sx, [128, 4], F32)
    G["xn"] = ph.sb("xn" + sx, [128, D], BF16)
    G["pT"] = ph.ps("pT" + sx, [128, 8, 128], BF16)
    if eps is None:
        G["eps"] = ph.sb("eps", [128, 1], F32)
        ph.memset("dve", G["eps"][:], 1e-6, W="eps")
    else:
        G["eps"] = eps
    return G


def build_program(upto=9, debug=False):
    nc = bass.Bass("TRN2", target_bir_lowering=False)
    I = {}

    def inp(name, shape, dt=F32):
        I[name] = nc.dram_tensor(name, list(shape), dt, kind="ExternalInput").ap()

    def outp(name, shape):
        I[name] = nc.dram_tensor(name, list(shape), F32, kind="ExternalOutput").ap()

    def scratch(name, shape, dt):
        if debug:
            I[name] = nc.dram_tensor(name, list(shape), dt, kind="ExternalOutput").ap()
        else:
            I[name] = nc.dram_tensor(name, list(shape), dt).ap()
    if debug:
        scratch("d_BwT", [128, 4 * CS * 2 * 128], BF16); scratch("d_Kmat", [128, 4 * CS * 128], BF16)
        scratch("d_CwT", [128, CS * 2 * 16 * 32], BF16); scratch("d_Abar", [128, 64], F32)

    inp("xall", [NT, D]); inp("pall", [NT, 256])
    inp("st_shift", [NS, 1792]); inp("st_wkv", [128, 4096]); inp("st_re", [NS, 2048]); inp("st_im", [NS, 2048])
    inp("st_conv", [NS, 2, 2816])
    inp("ln1_g", [D]); inp("w_in", [D, 4352]); inp("mu_shift", [1792]); inp("w0", [512]); inp("w2", [64, 512])
    inp("a0", [512]); inp("a2", [64, 512]); inp("g2", [128, 512]); inp("k_k", [512]); inp("k_a", [512])
    inp("r_k", [512]); inp("lnx_g", [512]); inp("lnx_b", [512]); inp("w_rw_out", [512, D])
    inp("A_re", [32, 64]); inp("A_im", [32, 64]); inp("log_dt", [32]); inp("B_re", [32, 64, 16]); inp("B_im", [32, 64, 16])
    inp("C_re", [512, 64]); inp("C_im", [512, 64]); inp("D_skip", [512]); inp("w_glu", [512, 2048]); inp("w_out", [D, D])
    inp("ln2_g", [D]); inp("w_ffn_in", [D, 5632]); inp("conv_w", [3, 2816]); inp("conv_b", [2816]); inp("w_ffn_out", [2816, D])
    inp("ln3_g", [D]); inp("w_ple_gate", [D, D]); inp("w_ple", [256, D]); inp("final_g", [D])
    inp("c_ident", [128, 128]); inp("c_msl", [128, 128]); inp("c_msu", [128, 128]); inp("c_mui", [128, 128])
    inp("c_blk64", [128, 128]); inp("c_blk32", [128, 128]); inp("c_rowgp", [128, 128])
    outp("y", [NT, D]); outp("p_shift", [1792]); outp("p_wkv", [512, 64]); outp("p_re", [2048]); outp("p_im", [2048])
    outp("p_conv", [2, 2816]); outp("s_shift", [NS, 1792]); outp("s_wkv", [128, 4096]); outp("s_re", [NS, 2048])
    outp("s_im", [NS, 2048]); outp("s_conv", [NS, 2, 2816])
    scratch("PRW", [1792, NT], F32); scratch("UU", [512, NT], F32); scratch("GT", [2048, NT], BF16)
    scratch("YF", [512, NT], BF16); scratch("ZZ", [512, NT], BF16); scratch("X1", [NT, D], F32); scratch("X2", [NT, D], F32)
    scratch("SW", [6, NS, 512], F32); scratch("SY", [128, 64], F32)

    with contextlib.ExitStack() as gst:
        def gsb(name, shape, dt):
            return gst.enter_context(nc.sbuf_tensor("g_" + name, list(shape), dt))
        G0 = {}
        G0["identb"] = gsb("identb", [128, 128], BF16)
        G0["identf"] = gsb("identf", [128, 128], F32)
        with contextlib.ExitStack() as g2:
            def g2sb(name, shape, dt):
                return g2.enter_context(nc.sbuf_tensor("g_" + name, list(shape), dt))
            G0["BwT"] = g2sb("BwT", [128, 4, CS, 2, 128], BF16)
            G0["Kmat"] = g2sb("Kmat", [128, 4, CS, 128], BF16)
            G0["CwT"] = g2sb("CwT", [128, CS, 2, 16, 32], BF16)
            G0["Abar"] = g2sb("Abar", [128, 2, 2, 16], F32)
            if upto >= 1:
                phase1(nc, I, G0, debug)
            else:
                phase0(nc, I, G0, debug)
            if upto >= 2:
                phase2(nc, I, G0, True)
            if upto >= 2.5:
                phase2(nc, I, G0, False)
        g4 = contextlib.ExitStack()
        WFI = g4.enter_context(nc.sbuf_tensor("g_wfi", [128, 8, 5632], BF16))
        if upto >= 3:
            phase3(nc, I, G0, None, WFI)
        if upto >= 4:
            phase4(nc, I, G0, WFI)
        g4.close()
        if upto >= 5:
            phase5(nc, I, G0)
    return nc


def phase0(nc, I, G0, debug=False, ph=None):
    own = ph is None
    if own:
        ph = Ph(nc, "p0")
        ph.dma("pool", G0["identb"][:], I["c_ident"], W="identb")
        ph.dma("sp", G0["identf"][:], I["c_ident"], W="identf")
    sb = ph.sb
    lr = sb("lr", [128, 16], F32); li = sb("li", [128, 16], F32); dtl = sb("dtl", [128, 16], F32)
    Bre = sb("Bre", [128, 16, 16], F32); Bim = sb("Bim", [128, 16, 16], F32)
    ph.dma("sp", lr[:], I["A_re"].rearrange("(P gp) n -> (gp n) P", gp=2), W="lr", slow=True)
    ph.dma("sp", li[:], I["A_im"].rearrange("(P gp) n -> (gp n) P", gp=2), W="li", slow=True)
    ldt2 = I["log_dt"].rearrange("(P gp) -> gp P", gp=2)
    for gp in range(2):
        ph.dma("sp", dtl[64 * gp:64 * gp + 64, :], ldt2[gp].partition_broadcast(64), W="dtl", slow=True)
    ph.dma("sp", Bre[:], I["B_re"].rearrange("(P gp) n c -> (gp n) P c", gp=2), W="Bre")
    ph.dma("sp", Bim[:], I["B_im"].rearrange("(P gp) n c -> (gp n) P c", gp=2), W="Bim")
    rowgp = sb("rowgp", [128, 128], F32); blk32 = sb("blk32", [128, 128], F32)
    ph.dma("sp", rowgp[:], I["c_rowgp"], W="rowgp"); ph.dma("sp", blk32[:], I["c_blk32"], W="blk32")
    CT = [sb("CTr", [128, 4, 128], F32), sb("CTi", [128, 4, 128], F32)]
    c2 = sb("c2", [128, 128], F32)
    pA = ph.ps("pA", [128, 4, 128], F32)
    for ri, nm in enumerate(("C_re", "C_im")):
        for k in range(4):
            src = I[nm][k * 128:(k + 1) * 128, :]
            ph.dma("sp", c2[:, 0:64], src, W="c2"); ph.dma("sp", c2[:, 64:128], src, W="c2")
            ph.tt("dve", c2[:], c2[:], rowgp[:], ALU.mult, R=["c2", "rowgp"], W="c2")
            ph.tr(pA[:, k, :], c2[:], G0["identf"][:], R=["c2", "identf"], W="pA")
        ph.cp("dve", CT[ri][:], pA[:], R="pA", W="CT%d" % ri)
    t = {n: sb(n, [128, 16], F32) for n in ("dt", "e1", "mag", "ang", "sa", "ca", "sinv", "cosv", "ar", "ai", "den",
                                             "rden", "am1", "fr", "fi", "t1", "t2")}
    V = "dve"
    K = lambda *n: list(n)
    hpi = sb("hpi", [128, 1], F32)
    ph.memset(V, hpi[:], math.pi / 2, W="hpi")
    ph.act(t["dt"][:], dtl[:], AF.Exp, R="dtl", W="dt")
    ph.tt(V, t["e1"][:], lr[:], t["dt"][:], ALU.mult, R=K("lr", "dt"), W="e1")
    ph.act(t["mag"][:], t["e1"][:], AF.Exp, R="e1", W="mag")
    ph.tt(V, t["ang"][:], li[:], t["dt"][:], ALU.mult, R=K("li", "dt"), W="ang")
    ph.ts(V, t["sa"][:], t["ang"][:], 1.0 / 64, ALU.mult, R="ang", W="sa")
    ph.act(t["sinv"][:], t["sa"][:], AF.Sin, R="sa", W="sinv")
    ph.act(t["cosv"][:], t["sa"][:], AF.Sin, R=["sa", "hpi"], W="cosv", bias=hpi[:, 0:1])
    for _ in range(6):
        ph.tt(V, t["t1"][:], t["cosv"][:], t["cosv"][:], ALU.mult, R="cosv", W="t1")
        ph.tt(V, t["t2"][:], t["sinv"][:], t["sinv"][:], ALU.mult, R="sinv", W="t2")
        ph.stt(t["sinv"][:], t["cosv"][:], 2.0, t["sinv"][:], ALU.mult, ALU.mult, R=["cosv", "sinv", "t2"], W="sinv")
        ph.tt(V, t["cosv"][:], t["t1"][:], t["t2"][:], ALU.subtract, R=["t1", "t2", "sinv"], W="cosv")
    ph.tt(V, t["ar"][:], t["mag"][:], t["cosv"][:], ALU.mult, R=K("mag", "cosv"), W="ar")
    ph.tt(V, t["ai"][:], t["mag"][:], t["sinv"][:], ALU.mult, R=K("mag", "sinv"), W="ai")
    ph.tt(V, t["den"][:], lr[:], lr[:], ALU.mult, R="lr", W="den")
    ph.tt(V, t["t1"][:], li[:], li[:], ALU.mult, R="li", W="t1")
    ph.tt(V, t["den"][:], t["den"][:], t["t1"][:], ALU.add, R=K("den", "t1"), W="den")
    ph.op(V, lambda e: e.reciprocal(out=t["rden"][:], in_=t["den"][:]), R="den", W="rden")
    ph.ts(V, t["am1"][:], t["ar"][:], -1.0, ALU.add, R="ar", W="am1")
    ph.tt(V, t["t1"][:], t["am1"][:], lr[:], ALU.mult, R=K("am1", "lr", "den"), W="t1")
    ph.tt(V, t["t2"][:], t["ai"][:], li[:], ALU.mult, R=K("ai", "li"), W="t2")
    ph.tt(V, t["t1"][:], t["t1"][:], t["t2"][:], ALU.add, R=K("t1", "t2"), W="t1")
    ph.tt(V, t["fr"][:], t["t1"][:], t["rden"][:], ALU.mult, R=K("t1", "rden"), W="fr")
    ph.tt(V, t["t1"][:], t["ai"][:], lr[:], ALU.mult, R=K("ai", "lr", "fr"), W="t1")
    ph.tt(V, t["t2"][:], t["am1"][:], li[:], ALU.mult, R=K("am1", "li"), W="t2")
    ph.tt(V, t["t1"][:], t["t1"][:], t["t2"][:], ALU.subtract, R=K("t1", "t2"), W="t1")
    ph.tt(V, t["fi"][:], t["t1"][:], t["rden"][:], ALU.mult, R=K("t1", "rden"), W="fi")
    pwr = sb("pwr", [128, CS + 1, 16], F32); pwi = sb("pwi", [128, CS + 1, 16], F32)
    ph.memset(V, pwr[:, 0, :], 1.0, W="pw"); ph.memset(V, pwi[:, 0, :], 0.0, W="pw")
    for e in range(CS):
        ph.tt(V, t["t1"][:], pwr[:, e, :], t["ar"][:], ALU.mult, R=K("pw", "ar", "fi"), W="t1")
        ph.tt(V, t["t2"][:], pwi[:, e, :], t["ai"][:], ALU.mult, R=K("pw", "ai"), W="t2")
        ph.tt(V, pwr[:, e + 1, :], t["t1"][:], t["t2"][:], ALU.subtract, R=K("t1", "t2"), W="pw")
        ph.tt(V, t["t1"][:], pwr[:, e, :], t["ai"][:], ALU.mult, R=K("pw", "ai"), W="t1")
        ph.tt(V, t["t2"][:], pwi[:, e, :], t["ar"][:], ALU.mult, R=K("pw", "ar"), W="t2")
        ph.tt(V, pwi[:, e + 1, :], t["t1"][:], t["t2"][:], ALU.add, R=K("t1", "t2"), W="pw")
    Ab = G0["Abar"]
    ph.cp(V, Ab[:, 0, 0, :], pwr[:, CS, :], R="pw", W="Abar"); ph.cp(V, Ab[:, 0, 1, :], pwi[:, CS, :], R="pw", W="Abar")
    ph.cp(V, Ab[:, 1, 0, :], pwr[:, 1, :], R="pw", W="Abar"); ph.cp(V, Ab[:, 1, 1, :], pwi[:, 1, :], R="pw", W="Abar")
    bbr = sb("bbr", [128, 16, 16], F32); bbi = sb("bbi", [128, 16, 16], F32)
    u1 = sb("u1", [128, 16, 16], F32); u2 = sb("u2", [128, 16, 16], F32)
    frb = bc(t["fr"][:, :].unsqueeze(2), [128, 16, 16]); fib = bc(t["fi"][:, :].unsqueeze(2), [128, 16, 16])
    ph.tt(V, u1[:], Bre[:], frb, ALU.mult, R=K("Bre", "fr"), W="u1")
    ph.tt(V, u2[:], Bim[:], fib, ALU.mult, R=K("Bim", "fi"), W="u2")
    ph.tt(V, bbr[:], u1[:], u2[:], ALU.subtract, R=K("u1", "u2"), W="bbr")
    ph.tt(V, u1[:], Bim[:], frb, ALU.mult, R=K("Bim", "fr", "bbr"), W="u1")
    ph.tt(V, u2[:], Bre[:], fib, ALU.mult, R=K("Bre", "fi", "bbr"), W="u2")
    ph.tt(V, bbi[:], u1[:], u2[:], ALU.add, R=K("u1", "u2"), W="bbi")
    Ew = sb("Ew", [128, CS, 2, 16, 2, 16], F32)
    ph.memset(V, Ew[:].rearrange("p a b c d e -> p (a b c d e)"), 0.0, W="Ew")
    for e in range(CS):
        pr = bc(pwr[:, e, :].unsqueeze(2), [128, 16, 16]); pi = bc(pwi[:, e, :].unsqueeze(2), [128, 16, 16])
        ph.tt(V, u1[:], bbr[:], pr, ALU.mult, R=K("bbr", "pw", "Ew"), W="u1")
        ph.tt(V, u2[:], bbi[:], pi, ALU.mult, R=K("bbi", "pw", "Ew"), W="u2")
        ph.tt(V, u1[:], u1[:], u2[:], ALU.subtract, R=K("u1", "u2"), W="u1")
        for gp in range(2):
            ph.cp(V, Ew[64 * gp:64 * gp + 64, e, 0, :, gp, :], u1[64 * gp:64 * gp + 64, :, :], R="u1", W="Ew")
        ph.tt(V, u1[:], bbr[:], pi, ALU.mult, R=K("bbr", "pw", "Ew"), W="u1")
        ph.tt(V, u2[:], bbi[:], pr, ALU.mult, R=K("bbi", "pw", "Ew"), W="u2")
        ph.tt(V, u1[:], u1[:], u2[:], ALU.add, R=K("u1", "u2"), W="u1")
        for gp in range(2):
            ph.cp(V, Ew[64 * gp:64 * gp + 64, e, 1, :, gp, :], u1[64 * gp:64 * gp + 64, :, :], R="u1", W="Ew")
    CTin = sb("CTin", [128, 4, 128], F32)
    ph.ts(V, CTin[:], CT[1][:], -1.0, ALU.mult, R="CT1", W="CTin")
    pB = [ph.ps("pB%d" % i, [128, 4, 128], F32) for i in range(2)]
    n = 0
    for j in range(CS):
        e = CS - 1 - j
        for ri in range(2):
            pb = pB[n % 2]; n += 1
            for k in range(4):
                src = Ew[:, e, ri, 4 * k:4 * k + 4, :, :].rearrange("p a b c -> p (a b c)")
                ph.tr(pb[:, k, :], src, G0["identf"][:], R=["Ew", "identf"], W="pB%d" % ((n - 1) % 2))
            ph.cp("act" if n % 2 else "dve", G0["BwT"][:, :, j, ri, :], pb[:], R="pB%d" % ((n - 1) % 2), W="BwT")
    for tau in range(CS):
        pb = pB[n % 2]; key = "pB%d" % (n % 2); n += 1
        for k in range(4):
            lr_ = Ew[:, tau, 0, 4 * k:4 * k + 4, :, :].rearrange("p a b c -> p (a b c)")
            li_ = Ew[:, tau, 1, 4 * k:4 * k + 4, :, :].rearrange("p a b c -> p (a b c)")
            ph.mm(pb[:, k, :], lr_, CT[0][:, k, :], True, False, R=["Ew", "CT0"], W=key)
            ph.mm(pb[:, k, :], li_, CTin[:, k, :], False, True, R=["Ew", "CTin"], W=key)
        ph.tt(V, G0["Kmat"][:, :, tau, :], pb[:], bc(blk32[:, :].unsqueeze(1), [128, 4, 128]), ALU.mult,
              R=[key, "blk32"], W="Kmat")
    w1 = sb("w1", [128, 16, 32], F32); w2_ = sb("w2", [128, 16, 32], F32)
    CTr3 = CT[0][:].rearrange("p k (a b) -> p (k a) b", a=4); CTi3 = CT[1][:].rearrange("p k (a b) -> p (k a) b", a=4)
    for i in range(CS):
        pr = bc(pwr[:, i + 1, :].unsqueeze(2), [128, 16, 32]); pi = bc(pwi[:, i + 1, :].unsqueeze(2), [128, 16, 32])
        ph.tt(V, w1[:], CTr3, pr, ALU.mult, R=K("CT0", "pw", "CwT"), W="w1")
        ph.tt(V, w2_[:], CTi3, pi, ALU.mult, R=K("CT1", "pw", "CwT"), W="w2")
        ph.tt(V, G0["CwT"][:, i, 0, :, :], w1[:], w2_[:], ALU.subtract, R=K("w1", "w2"), W="CwT")
        ph.tt(V, w1[:], CTr3, pi, ALU.mult, R=K("CT0", "pw", "CwT"), W="w1")
        ph.tt(V, w2_[:], CTi3, pr, ALU.mult, R=K("CT1", "pw", "CwT"), W="w2")
        ph.tt(V, w1[:], w1[:], w2_[:], ALU.add, R=K("w1", "w2"), W="w1")
        ph.ts(V, G0["CwT"][:, i, 1, :, :], w1[:], -1.0, ALU.mult, R="w1", W="CwT")
    if debug:
        ph.dma("sp", I["d_BwT"], G0["BwT"][:].rearrange("p a b c d -> p (a b c d)"), R="BwT")
        ph.dma("sp", I["d_Kmat"], G0["Kmat"][:].rearrange("p a b c -> p (a b c)"), R="Kmat")
        ph.dma("sp", I["d_CwT"], G0["CwT"][:].rearrange("p a b c d -> p (a b c d)"), R="CwT")
        ph.dma("sp", I["d_Abar"], G0["Abar"][:].rearrange("p a b c -> p (a b c)"), R="Abar")
    if own:
        ph.finish()


def phase1(nc, I, G0, debug=False):
    ph = Ph(nc, "p1")
    win = ph.sb("win", [128, 8, 4352], BF16)
    for k in range(8):
        ph.dma("pool", win[:, k, :], I["w_in"][k * 128:(k + 1) * 128, :], W="win%d" % k)
    ph.dma("pool", G0["identb"][:], I["c_ident"], W="identb")
    ph.dma("sp", G0["identf"][:], I["c_ident"], W="identf")
    ph.rec_begin()
    phase0(nc, I, G0, debug, ph=ph)
    s0 = ph.rec_end()
    ph.rec_begin()
    G = norm_scratch(ph, G0)
    g1c = ph.sb("g1c", [128, 8], F32)
    load_col(ph, g1c[:], I["ln1_g"], 8, "g1c")
    hTs = [ph.sb("hT%d" % i, [128, 8, 512], BF16) for i in range(2)]
    xts = [ph.sb("xt%d" % i, [128, D], F32) for i in range(2)]
    pm = [ph.ps("pm%d" % i, [128, 512], F32) for i in range(4)]
    stf = [ph.sb("stf%d" % i, [128, 512], F32) for i in range(4)]
    stb = [ph.sb("stb%d" % i, [128, 512], BF16) for i in range(3)]
    WK = ["win%d" % k for k in range(8)]
    nx = nf = nb = npm = 0
    pre1 = ph.rec_end()
    NR1, MM1 = [], []
    for bi_, (t0, nt) in enumerate(BLOCKS):
        P = min(128, nt)
        hT = hTs[bi_ % 2]; hk = "hT%d" % (bi_ % 2)
        ph.rec_begin()
        for s in range((nt + 127) // 128):
            xt = xts[nx % 2]; tg = str(nx % 2); nx += 1
            ph.dma("sp", xt[:P, :], I["xall"][t0 + s * 128:t0 + s * 128 + P, :], W="xt" + tg)
            rms_to_hT(ph, G, xt, P, g1c, hT, s * 128, tg, "g1c", hk)
        NR1.append(ph.rec_end())
        ph.rec_begin()
        for m in range(34):
            pb = pm[npm % 4]; pk = "pm%d" % (npm % 4); npm += 1
            for k in range(8):
                ph.mm(pb[:, :nt], win[:, k, m * 128:(m + 1) * 128], hT[:, k, :nt], k == 0, k == 7,
                      R=["win%d" % k, hk], W=pk)
            if m < 18:
                sf = stf[nf % 4]; sk = "stf%d" % (nf % 4); nf += 1
                ph.cp("dve" if m % 2 else "act", sf[:, :nt], pb[:, :nt], R=pk, W=sk)
                if m < 14:
                    ph.dma("pool", I["PRW"][m * 128:(m + 1) * 128, t0:t0 + nt], sf[:, :nt], R=sk)
                else:
                    ph.dma("pool", I["UU"][(m - 14) * 128:(m - 13) * 128, t0:t0 + nt], sf[:, :nt], R=sk)
            else:
                sbf = stb[nb % 3]; sk = "stb%d" % (nb % 3); nb += 1
                ph.act(sbf[:, :nt], pb[:, :nt], AF.Sigmoid, R=pk, W=sk)
                ph.dma("act", I["GT"][(m - 18) * 128:(m - 17) * 128, t0:t0 + nt], sbf[:, :nt], R=sk)
        MM1.append(ph.rec_end())
    s1 = pre1 + NR1[0]
    for b_ in range(len(BLOCKS)):
        s1 = s1 + ph.merge(MM1[b_], NR1[b_ + 1] if b_ + 1 < len(BLOCKS) else [])
    ph.play(s1, s0)
    ph.finish()


def alloc_w3(nc, st):
    t = lambda n, shp: st.enter_context(nc.sbuf_tensor("w3_" + n, shp, BF16))
    return {"rwo": t("rwo", [128, 4, D]), "glu": t("glu", [128, 4, 2048]), "wo": t("wo", [128, 8, D])}


def load_w3(ph, I, W3):
    for k in range(4):
        ph.dma("pool", W3["rwo"][:, k, :], I["w_rw_out"][k * 128:(k + 1) * 128, :], W="rwo")
        ph.dma("pool", W3["glu"][:, k, :], I["w_glu"][k * 128:(k + 1) * 128, :], W="glu")
    for k in range(8):
        ph.dma("pool", W3["wo"][:, k, :], I["w_out"][k * 128:(k + 1) * 128, :], W="wo")


def phase2(nc, I, G0, prompt, W3=None):
    ph = Ph(nc, "p2a" if prompt else "p2b")
    sb, ps = ph.sb, ph.ps
    V = "dve"
    if W3 is not None:
        load_w3(ph, I, W3)
    ph._s5tmp = [sb("s5a", [128, 2, 16], F32), sb("s5b", [128, 2, 16], F32)]
    ph._s5xb = sb("Xb", [128, 2, 16, 64], BF16)
    ph._s5du = sb("s5du", [128, 512], F32)
    if prompt:
        msl = sb("msl", [128, 128], BF16); msu = sb("msu", [128, 128], BF16); mui = sb("mui", [128, 128], BF16)
        ph.dma("pool", msl[:], I["c_msl"], W="msl"); ph.dma("pool", msu[:], I["c_msu"], W="msu")
        ph.dma("pool", mui[:], I["c_mui"], W="mui")
    blk64 = sb("blk64", [128, 128], F32); ph.dma("sp", blk64[:], I["c_blk64"], W="blk64")
    w2a2 = sb("w2a2", [128, 512], BF16); g2b = sb("g2b", [128, 512], BF16)
    ph.dma("pool", w2a2[0:64, :], I["w2"], W="w2a2"); ph.dma("pool", w2a2[64:128, :], I["a2"], W="w2a2")
    ph.dma("pool", g2b[:], I["g2"], W="g2b")
    pc = {}
    for nm, n in (("mu_shift", 14), ("w0", 4), ("a0", 4), ("k_k", 4), ("k_a", 4), ("r_k", 4), ("lnx_g", 4),
                  ("lnx_b", 4), ("D_skip", 4)):
        pc[nm] = sb("c_" + nm, [128, n], F32)
        load_col(ph, pc[nm][:], I[nm], n, "c_" + nm)
    PK = ["c_" + k for k in pc]
    scm = sb("scm", [128, 4, 128], F32)
    ph.memset(V, scm[:].rearrange("p a b -> p (a b)"), 1.0, W="scm"); ph.memset(V, scm[:, :, 0:1], 0.0, W="scm")
    eps_gn = sb("eps_gn", [128, 1], F32); ph.memset(V, eps_gn[:], 64e-5, W="eps_gn")
    if prompt:
        Sst = sb("Sst", [128, 4, 64], F32); Sbd = sb("Sbd", [128, 4, 128], BF16)
        ph.memset(V, Sst[:].rearrange("p a b -> p (a b)"), 0.0, W="Sst")
        ph.memset(V, Sbd[:].rearrange("p a b -> p (a b)"), 0.0, W="Sbd")
        Xs = sb("Xs", [128, 2, 16, 65], F32)
        ph.memset(V, Xs[:].rearrange("p a b c -> p (a b c)"), 0.0, W="Xs")
        Pf = sb("Pf", [128, 14, 513], F32)
        ph.memset(V, Pf[:, :, 0:1], 0.0, W="Pf")
    WB = 512 if prompt else NS
    WC = 128 if prompt else NS
    uf = sb("uf", [128, 4, WB], F32); ub = sb("ub", [128, 4, WB], BF16)
    YFb = sb("YFb", [128, 4, WB], BF16); ZZb = sb("ZZb", [128, 4, WB], BF16)
    f4 = lambda n: sb(n, [128, 4, WC], F32)
    b4 = lambda n: sb(n, [128, 4, WC], BF16)
    XS = sb("XS", [128, 14, WC], F32); dd = sb("dd", [128, 14, WC], F32)
    lin = sb("lin", [128, WC], BF16); sgx = sb("sgx", [128, WC], BF16)
    sig = f4("sig"); aa = f4("aa"); gg = f4("gg"); kk0 = f4("kk0"); tq = f4("tq"); rn = f4("rn"); kkn = f4("kkn")
    bb = f4("bb"); kmod = f4("kmod"); bon = f4("bon"); cs = f4("cs"); ex1 = f4("ex1"); ex2 = f4("ex2"); ex3 = f4("ex3")
    nbias = sb("nbias", [128, 4], F32); PCt = sb("PCt", [128, 4], F32)
    gns = f4("gns")
    KX = {n_: n_ for n_ in ("rT", "kT", "bT", "aT", "khT", "bhT", "vT", "PCt", "bon", "gg")}
    if prompt:
        rT = b4("rT"); kT = b4("kT"); bT = b4("bT"); aT = b4("aT"); khT = b4("khT"); bhT = b4("bhT"); vT = b4("vT")
        alt = {"rT": b4("rT1"), "kT": b4("kT1"), "bT": b4("bT1"), "aT": b4("aT1"), "khT": b4("khT1"),
               "bhT": b4("bhT1"), "vT": b4("vT1"), "PCt": sb("PCt1", [128, 4], F32), "bon": f4("bon1"), "gg": f4("gg1")}
        Vtok = sb("Vtok", [128, 512], BF16); Khtok = sb("Khtok", [128, 512], BF16); Bhtok = sb("Bhtok", [128, 512], BF16)
        h8 = lambda n: sb(n, [128, 8, 128], BF16)
        Nb = [h8("Nb0"), h8("Nb1")]; Lb = [h8("Lb0"), h8("Lb1")]; Mt = [h8("Mt0"), h8("Mt1")]
        LKb = h8("LKb"); Arb = h8("Arb"); Ark = h8("Ark")
        Wbf = sb("Wbf", [128, 512], BF16); Ubf = sb("Ubf", [128, 512], BF16)
        tS = sb("tS", [128, 4, 64], F32)
    Ysb = sb("Ysb", [128, 8, 64], F32); Ysq = sb("Ysq", [128, 8, 64], F32); ynb = sb("ynb", [128, 8, 64], BF16)
    gn = sb("gn", [128, 6, 8], F32)
    pF = [ps("pF%d" % i, [128, 4, 128], F32) for i in range(6)]
    pT = [ps("pTb%d" % i, [128, 8, 128], BF16) for i in range(2)]
    cnt = {"f": 0, "t": 0}

    def getF():
        i = cnt["f"] % 6; cnt["f"] += 1
        return pF[i], "pF%d" % i

    def mkpool(base):
        st_ = {"n": 0}

        def get():
            i = base + st_["n"] % 2; st_["n"] += 1
            return pF[i], "pF%d" % i
        return get
    getF_prep, getF_core, getFs = mkpool(0), mkpool(2), mkpool(4)

    def getT():
        i = cnt["t"] % 2; cnt["t"] += 1
        return pT[i], "pTb%d" % i

    ib = G0["identb"]

    if not prompt:
        sample_mixer(ph, I, G0, locals())
        ph.finish()
        return
    Lbase = dict(locals())
    Lpar = [dict(Lbase), dict(Lbase)]
    Lpar[1].update(alt)
    Lpar[1]["KX"] = {n_: n_ + "1" for n_ in KX}
    REC = []
    for bi, (t0, nt) in enumerate(BLOCKS[:4]):
        ph.rec_begin()
        if bi > 0:
            ph.cp(V, Pf[:, :, 0:1], Pf[:, :, 512:513], R="Pf", W="Pf")
        ph.dma("sp", Pf[:, :, 1:513], I["PRW"][:, t0:t0 + nt].rearrange("(m p) t -> p m t", p=128), W="Pf")
        if bi == 3:
            ph.dma("sp", I["p_shift"].rearrange("(m p) -> p m", p=128), Pf[:, :, 512], R="Pf", slow=True)
        hdr_pf = ph.rec_end()
        ph.rec_begin()
        ph.dma("act", uf[:], I["UU"][:, t0:t0 + nt].rearrange("(m p) t -> p m t", p=128), W="uf")
        ph.cp("act", ub[:].rearrange("p a b -> p (a b)"), uf[:].rearrange("p a b -> p (a b)"), R="uf", W="ub")
        hdr_ub = ph.rec_end()
        ph.rec_begin()
        s5_block(ph, I, G0, pc, Xs, ub, ZZb, getFs, nchunk=64, which=0, ncol=512)
        ph.dma("act", I["ZZ"][:, t0:t0 + nt].rearrange("(m p) t -> p m t", p=128), ZZb[:], R="ZZb")
        s5s = ph.rec_end()
        m0, m1 = ph._s5marks
        preps, cores = [], []
        for c in range(4):
            c0 = c * 128
            Lc = dict(Lpar[c % 2]); Lc["getF"] = getF_prep
            Lk = dict(Lpar[c % 2]); Lk["getF"] = getF_core
            ph.rec_begin()
            ph.tt(V, dd[:], Pf[:, :, c0:c0 + 128], Pf[:, :, c0 + 1:c0 + 129], ALU.subtract, R="Pf", W="dd")
            ph.tt(V, dd[:], dd[:], bc(pc["mu_shift"][:, :].unsqueeze(2), [128, 14, 128]), ALU.mult,
                  R=["dd", "c_mu_shift"], W="dd")
            ph.tt(V, XS[:], dd[:], Pf[:, :, c0 + 1:c0 + 129], ALU.add, R=["dd", "Pf"], W="XS")
            rwkv_prep_and_core(ph, Lc, c, c0)
            preps.append(ph.rec_end())
            ph.rec_begin()
            wkv_core(ph, Lk, c, c0)
            cores.append(ph.rec_end())
        ph.rec_begin()
        ph.dma("pool", I["YF"][:, t0:t0 + nt].rearrange("(m p) t -> p m t", p=128), YFb[:], R="YFb")
        yfst = ph.rec_end()
        hs = (m1 - m0) // 2
        REC.append(dict(hdr_pf=hdr_pf, hdr_ub=hdr_ub, SG=s5s[:m0], SS1=s5s[m0:m0 + hs], SS2=s5s[m0 + hs:m1],
                        SY=s5s[m1:], preps=preps, cores=cores, yfst=yfst))
    ph.play(REC[0]["hdr_pf"])
    ph.play(REC[0]["preps"][0])
    for bi in range(4):
        Rb = REC[bi]
        ph.play(Rb["hdr_ub"])
        ph.play(Rb["cores"][0], Rb["preps"][1], Rb["SG"])
        ph.play(Rb["cores"][1], Rb["preps"][2], Rb["SS1"])
        ph.play(Rb["cores"][2], Rb["preps"][3], Rb["SS2"])
        if bi < 3:
            ph.play(REC[bi + 1]["hdr_pf"])
            ph.play(Rb["cores"][3], Rb["SY"], REC[bi + 1]["preps"][0], spans=[(0.0, 1.0), (SYO, 1.0 - SYO), (0.0, 1.0)])
        else:
            ph.play(Rb["cores"][3], Rb["SY"], spans=[(0.0, 1.0), (SYO, 1.0 - SYO)])
        ph.play(Rb["yfst"])
    ph.dma("sp", I["p_wkv"].rearrange("(m p) v -> p m v", p=128), Sst[:], R="Sst")
    ph.dma("sp", I["p_re"].rearrange("(P p) -> p P", p=128), Xs[:, 0, :, 0], R="Xs", slow=True)
    ph.dma("sp", I["p_im"].rearrange("(P p) -> p P", p=128), Xs[:, 1, :, 0], R="Xs", slow=True)
    ph.finish()


def rwkv_prep_and_core(ph, L, c, c0):
    V = "dve"
    PV = L.get("PV", "dve")
    KX = L["KX"]
    pc = L["pc"]; XS = L["XS"]; getF = L["getF"]; getT = L["getT"]; ib = L["ib"]
    sig, aa, gg, kk0, tq, rn, kkn = L["sig"], L["aa"], L["gg"], L["kk0"], L["tq"], L["rn"], L["kkn"]
    bb, kmod, bon, cs, ex1, ex2, ex3 = L["bb"], L["kmod"], L["bon"], L["cs"], L["ex1"], L["ex2"], L["ex3"]
    rT, kT, bT, aT, khT, bhT, vT = L["rT"], L["kT"], L["bT"], L["aT"], L["khT"], L["bhT"], L["vT"]
    lin, sgx, w2a2, g2b, blk64 = L["lin"], L["sgx"], L["w2a2"], L["g2b"], L["blk64"]
    nbias, PCt, scm = L["nbias"], L["PCt"], L["scm"]
    r_ = XS[:, 0:4, :]; k_ = XS[:, 4:8, :]; v_ = XS[:, 8:12, :]
    B4 = lambda t: bc(t[:, :].unsqueeze(2), [128, 4, 128])
    fl = lambda t: t[:].rearrange("p a b -> p (a b)")
    ph.act(lin[0:64, :], XS[0:64, 12, :], AF.Tanh, R="XS", W="lin")
    ph.cp("act", lin[64:128, :], XS[64:128, 12, :], R="XS", W="lin")
    ph.act(sgx[:], XS[:, 13, :], AF.Sigmoid, R="XS", W="sgx")
    pw_, kw_ = getF()
    for m in range(4):
        ph.mm(pw_[:, m, :], w2a2[0:64, m * 128:(m + 1) * 128], lin[0:64, :], True, True, R=["w2a2", "lin"], W=kw_)
    for m in range(4):
        ph.act(sig[:, m, :], pw_[:, m, :], AF.Sigmoid, R=[kw_, "c_w0"], W="sig", bias=pc["w0"][:, m:m + 1])
    pa_, ka_ = getF()
    for m in range(4):
        ph.mm(pa_[:, m, :], w2a2[64:128, m * 128:(m + 1) * 128], lin[64:128, :], True, True, R=["w2a2", "lin"], W=ka_)
    for m in range(4):
        ph.act(aa[:, m, :], pa_[:, m, :], AF.Sigmoid, R=[ka_, "c_a0"], W="aa", bias=pc["a0"][:, m:m + 1])
    pg_, kg_ = getF()
    for m in range(4):
        ph.mm(pg_[:, m, :], g2b[:, m * 128:(m + 1) * 128], sgx[:], True, True, R=["g2b", "sgx"], W=kg_)
    ph.cp("act", gg[:], pg_[:], R=kg_, W=KX["gg"])
    ph.tt(PV, kk0[:], k_, B4(pc["k_k"]), ALU.mult, R=["XS", "c_k_k"], W="kk0")
    ph.tt(PV, tq[:], kk0[:], kk0[:], ALU.mult, R="kk0", W="tq")
    pq, kq = getF()
    for m in range(4):
        ph.mm(pq[:, m, :], blk64[:], tq[:, m, :], True, True, R=["blk64", "tq"], W=kq)
    ph.act(rn[:], pq[:], AF.Sqrt, R=kq, W="rn")
    ph.ts(V, rn[:], rn[:], 1e-12, ALU.max, R="rn", W="rn")
    ph.op(V, lambda e: e.reciprocal(out=fl(rn), in_=fl(rn)), R="rn", W="rn")
    ph.tt(PV, kkn[:], kk0[:], rn[:], ALU.mult, R=["kk0", "rn"], W="kkn")
    ph.tt(PV, bb[:], kkn[:], aa[:], ALU.mult, R=["kkn", "aa"], W="bb")
    ph.tt(PV, tq[:], aa[:], B4(pc["k_a"]), ALU.mult, R=["aa", "c_k_a", kq], W="tq")
    ph.tt(PV, tq[:], tq[:], B4(pc["k_a"]), ALU.subtract, R=["tq", "c_k_a"], W="tq")
    ph.stt(kmod[:], tq[:], 1.0, k_, ALU.add, ALU.mult, R=["tq", "XS"], W="kmod")
    ph.tt(PV, tq[:], r_, kmod[:], ALU.mult, R=["XS", "kmod"], W="tq")
    ph.tt(PV, tq[:], tq[:], B4(pc["r_k"]), ALU.mult, R=["tq", "c_r_k"], W="tq")
    pq2, kq2 = getF()
    for m in range(4):
        ph.mm(pq2[:, m, :], blk64[:], tq[:, m, :], True, True, R=["blk64", "tq"], W=kq2)
    ph.tt(V, bon[:], pq2[:], v_, ALU.mult, R=[kq2, "XS"], W=KX["bon"])
    ph.op(V, lambda e: e.tensor_tensor_scan(out=fl(cs), data0=fl(scm), data1=fl(sig), initial=0.0, op0=ALU.mult,
                                             op1=ALU.add), R=["scm", "sig"], W="cs")
    ph.ts(V, nbias[:], cs[:, :, 127], -C1, ALU.mult, R="cs", W="nbias")
    ph.act(PCt[:], nbias[:], AF.Exp, R="nbias", W=KX["PCt"])
    ph.act(ex1[:], cs[:], AF.Exp, R="cs", W="ex1", scale=-C1)
    ph.tt(PV, rT[:], r_, ex1[:], ALU.mult, R=["XS", "ex1"], W=KX["rT"])
    ph.act(ex2[:], cs[:], AF.Exp, R="cs", W="ex2", scale=C1)
    ph.tt(PV, kT[:], kmod[:], ex2[:], ALU.mult, R=["kmod", "ex2"], W=KX["kT"])
    ph.tt(PV, bT[:], bb[:], ex2[:], ALU.mult, R=["bb", "ex2"], W=KX["bT"])
    ph.tt(PV, ex3[:], cs[:], sig[:], ALU.subtract, R=["cs", "sig"], W="ex3")
    ph.act(ex3[:], ex3[:], AF.Exp, R="ex3", W="ex3", scale=-C1)
    ph.stt(aT[:], kkn[:], -1.0, ex3[:], ALU.mult, ALU.mult, R=["kkn", "ex3"], W=KX["aT"])
    for m in range(4):
        ph.act(ex1[:, m, :], cs[:, m, :], AF.Exp, R=["cs", "nbias", KX["rT"]], W="ex1", bias=nbias[:, m:m + 1], scale=C1)
    ph.tt(PV, khT[:], kmod[:], ex1[:], ALU.mult, R=["kmod", "ex1"], W=KX["khT"])
    ph.tt(PV, bhT[:], bb[:], ex1[:], ALU.mult, R=["bb", "ex1"], W=KX["bhT"])
    ph.cp("act", vT[:], v_, R="XS", W=KX["vT"])


def wkv_core(ph, L, c, c0):
    V = "dve"
    KX = L["KX"]
    getF = L["getF"]; getT = L["getT"]; ib = L["ib"]
    rT, kT, bT, aT, khT, bhT, vT = L["rT"], L["kT"], L["bT"], L["aT"], L["khT"], L["bhT"], L["vT"]
    Vtok, Khtok, Bhtok = L["Vtok"], L["Khtok"], L["Bhtok"]
    Nb, Lb, Mt, LKb, Arb, Ark = L["Nb"], L["Lb"], L["Mt"], L["LKb"], L["Arb"], L["Ark"]
    msl, msu, mui = L["msl"], L["msu"], L["mui"]
    Wbf, Ubf, Ysb, Ysq, ynb, gn = L["Wbf"], L["Ubf"], L["Ysb"], L["Ysq"], L["ynb"], L["gn"]
    Sst, Sbd, PCt, tS = L["Sst"], L["Sbd"], L["PCt"], L["tS"]
    pc = L["pc"]; bon, gg, YFb = L["bon"], L["gg"], L["YFb"]
    M4 = lambda m_: bc(m_[:, :].unsqueeze(1), [128, 4, 128])
    pt, kt = getT()
    for m in range(4):
        ph.tr(pt[:, m, :], vT[:, m, :], ib[:], R=KX["vT"], W=kt)
    for m in range(4):
        ph.tr(pt[:, 4 + m, :], khT[:, m, :], ib[:], R=KX["khT"], W=kt)
    ph.cp("act", Vtok[:], pt[:, 0:4, :].rearrange("p a b -> p (a b)"), R=kt, W="Vtok")
    ph.cp(V, Khtok[:], pt[:, 4:8, :].rearrange("p a b -> p (a b)"), R=kt, W="Khtok")
    pt2, kt2 = getT()
    for m in range(4):
        ph.tr(pt2[:, m, :], bhT[:, m, :], ib[:], R=KX["bhT"], W=kt2)
    ph.cp("act", Bhtok[:], pt2[:, 0:4, :].rearrange("p a b -> p (a b)"), R=kt2, W="Bhtok")

    def hsl(t, h):
        return t[64 * (h % 2):64 * (h % 2) + 64, h // 2, :]

    def amat(dst, dkey, lhs, lkey, rhs, rkey, mask, mkey):
        for par in range(2):
            pb, pk = getF()
            for q in range(4):
                h = 2 * q + par
                ph.mm(pb[:, q, :], hsl(lhs, h), hsl(rhs, h), True, True, R=[lkey, rkey], W=pk)
            ph.tt(V, dst[:, par:8:2, :], pb[:], M4(mask), ALU.mult, R=[pk, mkey], W=dkey)

    amat(Nb[0], "Nb0", aT, KX["aT"], bT, KX["bT"], msl, "msl")
    amat(Lb[0], "Lb0", bT, KX["bT"], aT, KX["aT"], msu, "msu")
    amat(LKb, "LKb", kT, KX["kT"], aT, KX["aT"], msu, "msu")
    amat(Arb, "Arb", bT, KX["bT"], rT, KX["rT"], mui, "mui")
    amat(Ark, "Ark", kT, KX["kT"], rT, KX["rT"], mui, "mui")
    for half in range(2):
        ph.tt(V, Mt[0][:, half * 4:half * 4 + 4, :], Lb[0][:, half * 4:half * 4 + 4, :], M4(ib), ALU.add,
              R=["Lb0", "identb"], W="Mt0")
    cur = 0
    for lvl in range(6):
        nxt = 1 - cur
        for half in range(2):
            pb, pk = getF()
            for q in range(4):
                h = half * 4 + q
                ph.mm(pb[:, q, :], Lb[cur][:, h, :], Nb[cur][:, h, :], True, True, R=["Lb%d" % cur, "Nb%d" % cur], W=pk)
            ph.cp("act", Nb[nxt][:, half * 4:half * 4 + 4, :], pb[:], R=pk, W="Nb%d" % nxt)
        if BUB2:
            ph.bubble(BUB2)
        if lvl < 5:
            for half in range(2):
                pb, pk = getF()
                for q in range(4):
                    h = half * 4 + q
                    ph.mm(pb[:, q, :], Nb[cur][:, h, :], Lb[cur][:, h, :], True, True,
                          R=["Lb%d" % cur, "Nb%d" % cur], W=pk)
                ph.cp("act", Lb[nxt][:, half * 4:half * 4 + 4, :], pb[:], R=pk, W="Lb%d" % nxt)
        for half in range(2):
            pb, pk = getF()
            for q in range(4):
                h = half * 4 + q
                ph.mm(pb[:, q, :], Nb[nxt][:, h, :], Mt[cur][:, h, :], True, True, R=["Nb%d" % nxt, "Mt%d" % cur], W=pk)
            ph.tt(V, Mt[nxt][:, half * 4:half * 4 + 4, :], pb[:], Mt[cur][:, half * 4:half * 4 + 4, :], ALU.add,
                  R=[pk, "Mt%d" % cur], W="Mt%d" % nxt)
        cur = nxt
    MtF = Mt[cur]; mk = "Mt%d" % cur
    def hcols(pb, h):
        return pb[:].rearrange("p a b -> p (a b)")[:, h * 64:h * 64 + 64]

    def pcols(pb, m):
        return pb[:].rearrange("p a b -> p (a b)")[:, m * 128:m * 128 + 128]

    pb, pk = getF()
    for m in range(4):
        ph.mm(pcols(pb, m), aT[:, m, :], Sbd[:, m, :], True, False, R=[KX["aT"], "Sbd"], W=pk)
        for hh in range(2):
            h = 2 * m + hh
            ph.mm(hcols(pb, h), LKb[:, h, :], Vtok[:, h * 64:h * 64 + 64], False, hh == 1, R=["LKb", "Vtok"], W=pk)
    ph.cp("act", Wbf[:], pb[:].rearrange("p a b -> p (a b)"), R=pk, W="Wbf")
    ph.bubble(BUB)
    pb, pk = getF()
    for h in range(8):
        ph.mm(hcols(pb, h), MtF[:, h, :], Wbf[:, h * 64:h * 64 + 64], True, True, R=[mk, "Wbf"], W=pk)
    ph.cp("act", Ubf[:], pb[:].rearrange("p a b -> p (a b)"), R=pk, W="Ubf")
    ph.bubble(BUB)
    pb, pk = getF()
    for m in range(4):
        ph.mm(pcols(pb, m), rT[:, m, :], Sbd[:, m, :], True, False, R=[KX["rT"], "Sbd"], W=pk)
        for hh in range(2):
            h = 2 * m + hh
            ph.mm(hcols(pb, h), Arb[:, h, :], Ubf[:, h * 64:h * 64 + 64], False, False, R=["Arb", "Ubf"], W=pk)
            ph.mm(hcols(pb, h), Ark[:, h, :], Vtok[:, h * 64:h * 64 + 64], False, hh == 1, R=["Ark", "Vtok"], W=pk)
    ph.cp("act", Ysb[:].rearrange("p a b -> p (a b)"), pb[:].rearrange("p a b -> p (a b)"), R=pk, W="Ysb")
    pS, kS = getF()
    for m in range(4):
        ph.mm(pS[:, m, :], Bhtok[:, m * 128:(m + 1) * 128], Ubf[:, m * 128:(m + 1) * 128], True, False,
              R=["Bhtok", "Ubf"], W=kS)
        ph.mm(pS[:, m, :], Khtok[:, m * 128:(m + 1) * 128], Vtok[:, m * 128:(m + 1) * 128], False, True,
              R=["Khtok", "Vtok"], W=kS)
    ph.tt(V, tS[:], Sst[:], bc(PCt[:, :].unsqueeze(2), [128, 4, 64]), ALU.mult, R=["Sst", KX["PCt"]], W="tS")
    for hh in range(2):
        rs = slice(64 * hh, 64 * hh + 64)
        ph.tt(V, Sst[rs, :, :], tS[rs, :, :], pS[rs, :, 64 * hh:64 * hh + 64], ALU.add, R=["tS", kS], W="Sst")
        ph.cp(V, Sbd[rs, :, 64 * hh:64 * hh + 64], Sst[rs, :, :], R="Sst", W="Sbd")
    groupnorm_out(ph, L, c0, 128)


def groupnorm_out(ph, L, c0, P):
    V = "dve"
    KX = L["KX"]
    Ysb, Ysq, ynb, gn = L["Ysb"], L["Ysq"], L["ynb"], L["gn"]
    pc = L["pc"]; bon, gg, YFb = L["bon"], L["gg"], L["YFb"]; getT = L["getT"]; ib = L["ib"]
    eps_gn = L["eps_gn"]; ex2 = L["gns"]
    ph.op(V, lambda e: e.tensor_reduce(out=gn[:P, 0, :], in_=Ysb[:P], axis=AX.X, op=ALU.add), R="Ysb", W="gn")
    ph.act(Ysq[:P].rearrange("p a b -> p (a b)"), Ysb[:P].rearrange("p a b -> p (a b)"), AF.Square, R="Ysb", W="Ysq")
    ph.op(V, lambda e: e.tensor_reduce(out=gn[:P, 1, :], in_=Ysq[:P], axis=AX.X, op=ALU.add), R="Ysq", W="gn")
    ph.ts(V, gn[:P, 2, :], gn[:P, 0, :], 1.0 / 64, ALU.mult, R="gn", W="gn")
    ph.tt(V, gn[:P, 3, :], gn[:P, 2, :], gn[:P, 2, :], ALU.mult, R="gn", W="gn")
    ph.stt(gn[:P, 4, :], gn[:P, 1, :], 1.0 / 64, gn[:P, 3, :], ALU.mult, ALU.subtract, R="gn", W="gn")
    ph.act(gn[:P, 4, :], gn[:P, 4, :], AF.Sqrt, R=["gn", "eps_gn"], W="gn", bias=eps_gn[:P, 0:1])
    ph.op(V, lambda e: e.reciprocal(out=gn[:P, 5, :], in_=gn[:P, 4, :]), R="gn", W="gn")
    ph.tt(V, Ysq[:P], Ysb[:P], bc(gn[:P, 2, :].unsqueeze(2), [P, 8, 64]), ALU.subtract, R=["Ysb", "gn"], W="Ysq")
    ph.tt(V, ynb[:P], Ysq[:P], bc(gn[:P, 5, :].unsqueeze(2), [P, 8, 64]), ALU.mult, R=["Ysq", "gn"], W="ynb")
    pt, kt = getT()
    for m in range(4):
        ph.tr(pt[:, m, :P], ynb[:P, 2 * m:2 * m + 2, :].rearrange("p a b -> p (a b)"), ib[:P, :P], R="ynb", W=kt)
    B4 = lambda t: bc(t[:, :].unsqueeze(2), [128, 4, P])
    t1 = ex2
    ph.tt(V, t1[:, :, :P], pt[:, 0:4, :P], B4(pc["lnx_g"]), ALU.mult, R=[kt, "c_lnx_g"], W="gns")
    ph.tt(V, t1[:, :, :P], t1[:, :, :P], B4(pc["lnx_b"]), ALU.add, R=["gns", "c_lnx_b"], W="gns")
    ph.tt(V, t1[:, :, :P], t1[:, :, :P], bon[:, :, :P], ALU.add, R=["gns", KX["bon"]], W="gns")
    ph.tt(V, YFb[:, :, c0:c0 + P], t1[:, :, :P], gg[:, :, :P], ALU.mult, R=["gns", KX["gg"]], W="YFb")


def s5_block(ph, I, G0, pc, Xs, ub, ZZb, getF, nchunk, which, ncol, step=CS, npos=CS):
    V = "dve"
    BwT, Kmat, CwT, Ab = G0["BwT"], G0["Kmat"], G0["CwT"], G0["Abar"]
    nm = nchunk
    assert nm * 8 <= 512
    for Pl in range(4):
        pb, pk = getF()
        flat = pb[:].rearrange("p a b -> p (a b)")
        for ri in range(2):
            for k in range(4):
                q = ri * 4 + k
                dst = flat[:, q * nm:(q + 1) * nm]
                for j in range(npos):
                    jj = (CS - npos) + j
                    rhs = ub[32 * Pl:32 * Pl + 32, k, j:j + (nm - 1) * step + 1:step]
                    ph.mm(dst, BwT[32 * Pl:32 * Pl + 32, k, jj, ri, :], rhs, j == 0, j == npos - 1,
                          R=["BwT", "ub"], W=pk, tp=((96, 0) if Pl == 3 else None))
        for ri in range(2):
            ph.cp(V, Xs[:, ri, Pl:16:4, 1:1 + nm],
                  flat[:, ri * 4 * nm:(ri + 1) * 4 * nm].rearrange("p (q m) -> p q m", m=nm), R=[pk], W="Xs")
    A_r = bc(Ab[:, which, 0, :].unsqueeze(1), [128, 2, 16]); A_i = bc(Ab[:, which, 1, :].unsqueeze(1), [128, 2, 16])
    ph._s5marks = [len(ph._rec) if ph._rec is not None else 0]
    tmpa = ph._s5tmp[0]; tmpb = ph._s5tmp[1]
    for m in range(nm):
        ph.tt(SCAN_ENG, tmpa[:], Xs[:, :, :, m], A_r, ALU.mult, R=["Xs", "Abar"], W="s5a")
        ph.tt(SCAN_ENG, tmpb[:], Xs[:, :, :, m], A_i, ALU.mult, R=["Xs", "Abar"], W="s5b")
        ph.tt(SCAN_ENG, Xs[:, :, :, m + 1], Xs[:, :, :, m + 1], tmpa[:], ALU.add, R=["Xs", "s5a"], W="Xs")
        ph.tt(SCAN_ENG, Xs[:, 0, :, m + 1], Xs[:, 0, :, m + 1], tmpb[:, 1, :], ALU.subtract, R=["Xs", "s5b"], W="Xs")
        ph.tt(SCAN_ENG, Xs[:, 1, :, m + 1], Xs[:, 1, :, m + 1], tmpb[:, 0, :], ALU.add, R=["Xs", "s5b"], W="Xs")
    ph._s5marks.append(len(ph._rec) if ph._rec is not None else 0)
    Xb = ph._s5xb
    ph.cp("act", Xb[:, :, :, 0:nm], Xs[:, :, :, 0:nm], R="Xs", W="Xb")
    for k in range(4):
        pb, pk = getF()
        flat = pb[:].rearrange("p a b -> p (a b)")
        for i in range(npos):
            dst = flat[:, i * nm:(i + 1) * nm]
            for tau in range(i + 1):
                rhs = ub[:, k, (i - tau):(i - tau) + (nm - 1) * step + 1:step]
                ph.mm(dst, Kmat[:, k, tau, :], rhs, tau == 0, False, R=["Kmat", "ub"], W=pk)
            for Pl in range(4):
                P_ = 4 * k + Pl
                for ri in range(2):
                    ph.mm(flat[32 * Pl:32 * Pl + 32, i * nm:(i + 1) * nm], CwT[:, i, ri, P_, :], Xb[:, ri, P_, 0:nm],
                          False, ri == 1, R=["CwT", "Xb"], W=pk, tp=(0, 32 * Pl))
        du = ph._s5du
        ph.ts(V, du[:, 0:ncol], ub[:, k, 0:ncol], pc["D_skip"][:, k:k + 1], ALU.mult, R=["ub", "c_D_skip", "s5z"], W="s5du")
        if npos == 1:
            ph.tt(V, du[:, 0:ncol], du[:, 0:ncol], flat[:, 0:nm], ALU.add, R=["s5du", pk], W="s5du")
        else:
            ph.tt(V, du[:, 0:ncol].rearrange("p (m i) -> p m i", i=npos), du[:, 0:ncol].rearrange("p (m i) -> p m i", i=npos),
                  flat[:, 0:npos * nm].rearrange("p (i m) -> p m i", m=nm), ALU.add, R=["s5du", pk], W="s5du")
        ph.act(ZZb[:, k, 0:ncol], du[:, 0:ncol], AF.Gelu_apprx_tanh, R="s5du", W=["ZZb", "s5z"])
    ph.cp(V, Xs[:, :, :, 0], Xs[:, :, :, nm], R="Xs", W="Xs")


def sample_mixer(ph, I, G0, L):
    V = "dve"
    sb = ph.sb
    pc = L["pc"]; getF, getT, ib = L["getF"], L["getT"], L["ib"]
    identf = G0["identf"]
    XS = L["XS"]; dd = L["dd"]
    t0 = T
    n = NS
    cur = sb("s_cur", [128, 14, NS], F32); prv = sb("s_prv", [128, 14, NS], F32)
    ph.dma("sp", cur[:], I["PRW"][:, t0:t0 + n].rearrange("(m p) t -> p m t", p=128), W="s_cur")
    sst = sb("s_sst", [NS, 1792], F32)
    ph.dma("sp", sst[:], I["st_shift"], W="s_sst")
    for half in range(4):
        pb, pk = getF()
        flat = pb[:].rearrange("p a b -> p (a b)")
        ms = list(range(half * 4, min(14, half * 4 + 4)))
        for q, m in enumerate(ms):
            ph.tr(flat[:, q * NS:(q + 1) * NS], sst[:, m * 128:(m + 1) * 128], identf[:NS, :NS], R=["s_sst"], W=pk)
        ph.cp(V, prv[:, ms[0]:ms[-1] + 1, :], flat[:, 0:len(ms) * NS].rearrange("p (a b) -> p a b", b=NS), R=pk, W="s_prv")
    ph.dbg("cur", cur[:], [128, 14, NS], "s_cur")
    ph.dbg("prv", prv[:], [128, 14, NS], "s_prv")
    so = sst
    for half in range(4):
        pb, pk = getF()
        flat = pb[:].rearrange("p a b -> p (a b)")
        ms = list(range(half * 4, min(14, half * 4 + 4)))
        for q, m in enumerate(ms):
            ph.tr(flat[:NS, q * 128:(q + 1) * 128], cur[:, m, :], identf[:], R=["s_cur"], W=pk)
        ph.cp(V, so[:, ms[0] * 128:(ms[-1] + 1) * 128], flat[:NS, 0:len(ms) * 128], R=pk, W="s_sst")
    ph.dma("sp", I["s_shift"], so[:], R="s_sst")
    xs = XS[:, :, 0:NS]
    ph.tt(V, dd[:, :, 0:NS], prv[:], cur[:], ALU.subtract, R=["s_prv", "s_cur"], W="dd")
    ph.tt(V, dd[:, :, 0:NS], dd[:, :, 0:NS], bc(pc["mu_shift"][:, :].unsqueeze(2), [128, 14, NS]), ALU.mult,
          R=["dd", "c_mu_shift"], W="dd")
    ph.tt(V, xs, dd[:, :, 0:NS], cur[:], ALU.add, R=["dd", "s_cur"], W="XS")
    uf = L["uf"]; ub = L["ub"]; ZZb = L["ZZb"]
    ph.dma("act", uf[:, :, 0:NS], I["UU"][:, t0:t0 + n].rearrange("(m p) t -> p m t", p=128), W="uf")
    ph.cp("act", ub[:, :, 0:NS], uf[:, :, 0:NS], R="uf", W="ub")
    stx = [sb("s_stre", [NS, 2048], F32), sb("s_stim", [NS, 2048], F32)]
    ph.dma("sp", stx[0][:], I["st_re"], W="s_stx0"); ph.dma("sp", stx[1][:], I["st_im"], W="s_stx1")
    Xsm = sb("s_Xsm", [128, 2, 16, NS], F32)
    for ri in range(2):
        for q4 in range(4):
            pb, pk = getF()
            flat = pb[:].rearrange("p a b -> p (a b)")
            for q in range(4):
                P_ = q4 * 4 + q
                ph.tr(flat[:, q * NS:(q + 1) * NS], stx[ri][:, P_ * 128:(P_ + 1) * 128], identf[:NS, :NS],
                      R="s_stx%d" % ri, W=pk)
            ph.cp(V, Xsm[:, ri, q4 * 4:q4 * 4 + 4, :], flat[:, 0:4 * NS].rearrange("p (a b) -> p a b", b=NS), R=pk, W="s_Xsm")
    s5_sample(ph, I, G0, pc, Xsm, ub, ZZb, getF, stx)
    ph.dma("act", I["ZZ"][:, t0:t0 + n].rearrange("(m p) t -> p m t", p=128), ZZb[:, :, 0:NS], R="ZZb")
    rwkv_sample(ph, I, G0, L)
    ph.dma("sp", I["YF"][:, t0:t0 + n].rearrange("(m p) t -> p m t", p=128), L["YFb"][:, :, 0:NS], R="YFb")


def s5_sample(ph, I, G0, pc, Xsm, ub, ZZb, getF, stx):
    V = "dve"
    BwT, Kmat, CwT, Ab = G0["BwT"], G0["Kmat"], G0["CwT"], G0["Abar"]
    identf = G0["identf"]
    Xb = ph._s5xb
    ph.cp("act", Xb[:, :, :, 0:NS], Xsm[:], R="s_Xsm", W="Xb")
    du = ph._s5du
    for k in range(4):
        pb, pk = getF()
        flat = pb[:].rearrange("p a b -> p (a b)")
        ph.mm(flat[:, 0:NS], Kmat[:, k, 0, :], ub[:, k, 0:NS], True, False, R=["Kmat", "ub"], W=pk)
        for Pl in range(4):
            P_ = 4 * k + Pl
            for ri in range(2):
                ph.mm(flat[32 * Pl:32 * Pl + 32, 0:NS], CwT[:, 0, ri, P_, :], Xb[:, ri, P_, 0:NS], False,
                      ri == 1, R=["CwT", "Xb"], W=pk, tp=(0, 32 * Pl))
        ph.ts(V, du[:, 0:NS], ub[:, k, 0:NS], pc["D_skip"][:, k:k + 1], ALU.mult, R=["ub", "c_D_skip", "s5z"], W="s5du")
        ph.tt(V, du[:, 0:NS], du[:, 0:NS], flat[:, 0:NS], ALU.add, R=["s5du", pk], W="s5du")
        ph.act(ZZb[:, k, 0:NS], du[:, 0:NS], AF.Gelu_apprx_tanh, R="s5du", W=["ZZb", "s5z"])
    Gs = ph.sb("s_Gs", [128, 2, 16, NS], F32)
    for Pl in range(4):
        pb, pk = getF()
        flat = pb[:].rearrange("p a b -> p (a b)")
        for ri in range(2):
            for k in range(4):
                q = ri * 4 + k
                ph.mm(flat[:, q * NS:(q + 1) * NS], BwT[32 * Pl:32 * Pl + 32, k, CS - 1, ri, :],
                      ub[32 * Pl:32 * Pl + 32, k, 0:NS], True, True, R=["BwT", "ub"], W=pk,
                      tp=((96, 0) if Pl == 3 else None))
        for ri in range(2):
            ph.cp(V, Gs[:, ri, Pl:16:4, :], flat[:, ri * 4 * NS:(ri + 1) * 4 * NS].rearrange("p (q m) -> p q m", m=NS),
                  R=pk, W="s_Gs")
    A_r = bc(Ab[:, 1, 0, :].unsqueeze(2), [128, 16, NS]); A_i = bc(Ab[:, 1, 1, :].unsqueeze(2), [128, 16, NS])
    ta = ph.sb("s_ta", [128, 16, NS], F32)
    ph.tt(V, ta[:], Xsm[:, 0], A_r, ALU.mult, R=["s_Xsm", "Abar"], W="s_ta")
    ph.tt(V, Gs[:, 0], Gs[:, 0], ta[:], ALU.add, R=["s_Gs", "s_ta"], W="s_Gs")
    ph.tt(V, ta[:], Xsm[:, 1], A_i, ALU.mult, R=["s_Xsm", "Abar", "s_Gs"], W="s_ta")
    ph.tt(V, Gs[:, 0], Gs[:, 0], ta[:], ALU.subtract, R=["s_Gs", "s_ta"], W="s_Gs")
    ph.tt(V, ta[:], Xsm[:, 1], A_r, ALU.mult, R=["s_Xsm", "Abar", "s_Gs"], W="s_ta")
    ph.tt(V, Gs[:, 1], Gs[:, 1], ta[:], ALU.add, R=["s_Gs", "s_ta"], W="s_Gs")
    ph.tt(V, ta[:], Xsm[:, 0], A_i, ALU.mult, R=["s_Xsm", "Abar", "s_Gs"], W="s_ta")
    ph.tt(V, Gs[:, 1], Gs[:, 1], ta[:], ALU.add, R=["s_Gs", "s_ta"], W="s_Gs")
    for ri, nm in enumerate(("s_re", "s_im")):
        xo = stx[ri]
        for q4 in range(4):
            pb, pk = getF()
            flat = pb[:].rearrange("p a b -> p (a b)")
            for q in range(4):
                P_ = q4 * 4 + q
                ph.tr(flat[:NS, q * 128:(q + 1) * 128], Gs[:, ri, P_, :], identf[:], R="s_Gs", W=pk)
            ph.cp(V, xo[:, q4 * 512:(q4 + 1) * 512], flat[:NS, 0:512], R=pk, W="s_stx%d" % ri)
        ph.dma("sp", I[nm], xo[:], R="s_stx%d" % ri)


def rwkv_sample(ph, I, G0, L):
    V = "dve"
    sb = ph.sb
    pc = L["pc"]; getF, getT, ib = L["getF"], L["getT"], L["ib"]
    identf = G0["identf"]
    XS = L["XS"]
    sig, aa, gg, kk0, tq, rn, kkn = L["sig"], L["aa"], L["gg"], L["kk0"], L["tq"], L["rn"], L["kkn"]
    bb, kmod, bon = L["bb"], L["kmod"], L["bon"]
    lin, sgx, w2a2, g2b, blk64 = L["lin"], L["sgx"], L["w2a2"], L["g2b"], L["blk64"]
    n = NS
    r_ = XS[:, 0:4, 0:n]; k_ = XS[:, 4:8, 0:n]; v_ = XS[:, 8:12, 0:n]
    B4 = lambda t: bc(t[:, :].unsqueeze(2), [128, 4, n])
    S4 = lambda t: t[:, :, 0:n]
    ph.act(lin[0:64, 0:n], XS[0:64, 12, 0:n], AF.Tanh, R="XS", W="lin")
    ph.cp("act", lin[64:128, 0:n], XS[64:128, 12, 0:n], R="XS", W="lin")
    ph.act(sgx[:, 0:n], XS[:, 13, 0:n], AF.Sigmoid, R="XS", W="sgx")
    pw_, kw_ = getF(); pa_, ka_ = getF(); pg_, kg_ = getF()
    for m in range(4):
        ph.mm(pw_[:, m, 0:n], w2a2[0:64, m * 128:(m + 1) * 128], lin[0:64, 0:n], True, True, R=["w2a2", "lin"], W=kw_)
        ph.mm(pa_[:, m, 0:n], w2a2[64:128, m * 128:(m + 1) * 128], lin[64:128, 0:n], True, True, R=["w2a2", "lin"], W=ka_)
        ph.mm(pg_[:, m, 0:n], g2b[:, m * 128:(m + 1) * 128], sgx[:, 0:n], True, True, R=["g2b", "sgx"], W=kg_)
    for m in range(4):
        ph.act(sig[:, m, 0:n], pw_[:, m, 0:n], AF.Sigmoid, R=[kw_, "c_w0"], W="sig", bias=pc["w0"][:, m:m + 1])
        ph.act(aa[:, m, 0:n], pa_[:, m, 0:n], AF.Sigmoid, R=[ka_, "c_a0"], W="aa", bias=pc["a0"][:, m:m + 1])
    ph.cp("act", S4(gg), pg_[:, :, 0:n], R=kg_, W="gg")
    ph.tt(V, S4(kk0), k_, B4(pc["k_k"]), ALU.mult, R=["XS", "c_k_k"], W="kk0")
    ph.tt(V, S4(tq), S4(kk0), S4(kk0), ALU.mult, R="kk0", W="tq")
    pq, kq = getF()
    for m in range(4):
        ph.mm(pq[:, m, 0:n], blk64[:], tq[:, m, 0:n], True, True, R=["blk64", "tq"], W=kq)
    ph.act(S4(rn), pq[:, :, 0:n], AF.Sqrt, R=kq, W="rn")
    ph.ts(V, S4(rn), S4(rn), 1e-12, ALU.max, R="rn", W="rn")
    ph.op(V, lambda e: e.reciprocal(out=S4(rn), in_=S4(rn)), R="rn", W="rn")
    ph.tt(V, S4(kkn), S4(kk0), S4(rn), ALU.mult, R=["kk0", "rn"], W="kkn")
    ph.tt(V, S4(bb), S4(kkn), S4(aa), ALU.mult, R=["kkn", "aa"], W="bb")
    ph.tt(V, S4(tq), S4(aa), B4(pc["k_a"]), ALU.mult, R=["aa", "c_k_a", kq], W="tq")
    ph.tt(V, S4(tq), S4(tq), B4(pc["k_a"]), ALU.subtract, R=["tq", "c_k_a"], W="tq")
    ph.stt(S4(kmod), S4(tq), 1.0, k_, ALU.add, ALU.mult, R=["tq", "XS"], W="kmod")
    ph.tt(V, S4(tq), r_, S4(kmod), ALU.mult, R=["XS", "kmod"], W="tq")
    ph.tt(V, S4(tq), S4(tq), B4(pc["r_k"]), ALU.mult, R=["tq", "c_r_k"], W="tq")
    pq2, kq2 = getF()
    for m in range(4):
        ph.mm(pq2[:, m, 0:n], blk64[:], tq[:, m, 0:n], True, True, R=["blk64", "tq"], W=kq2)
    ph.tt(V, S4(bon), pq2[:, :, 0:n], v_, ALU.mult, R=[kq2, "XS"], W="bon")
    wdec = L["ex1"]
    ph.act(S4(wdec), S4(sig), AF.Exp, R="sig", W="ex1", scale=-C1)
    srcs = [r_, S4(wdec), S4(kmod), v_, S4(kkn), S4(bb)]
    keys = ["XS", "ex1", "kmod", "XS", "kkn", "bb"]
    tok = sb("s_tok", [NS, 6, 512], F32)
    for i, (src, kkey) in enumerate(zip(srcs, keys)):
        pb, pk = getF()
        flat = pb[:].rearrange("p a b -> p (a b)")
        for m in range(4):
            ph.tr(flat[:NS, m * 128:(m + 1) * 128], src[:, m, :], identf[:], R=kkey, W=pk)
        ph.cp(V if i % 2 else "act", tok[:, i, :], flat[:NS, 0:512], R=pk, W="s_tok")
    ph.dma("sp", I["SW"].rearrange("i b f -> b i f"), tok[:], R="s_tok", W="SWd")
    vec = sb("s_vec", [128, 6, 64], F32)
    ph.dma("sp", vec[:], I["SW"].rearrange("i b (h k) -> (b h) i k", h=8), R="SWd", W="s_vec")
    S0 = sb("s_S0", [128, 64, 64], F32)
    ph.dma("act", S0[:].rearrange("p a b -> p (a b)"), I["st_wkv"], W="s_S0")
    tmp = sb("s_tmp", [128, 64, 64], F32)
    sa = sb("s_sa", [128, 64], F32); yv = sb("s_yv", [128, 64], F32); kka = sb("s_kka", [128, 64], F32)
    kB = lambda i: bc(vec[:, i, :].unsqueeze(1), [128, 64, 64])
    ph.tt(V, tmp[:], S0[:], kB(4), ALU.mult, R=["s_S0", "s_vec"], W="s_tmp")
    ph.op(V, lambda e: e.tensor_reduce(out=sa[:], in_=tmp[:], axis=AX.X, op=ALU.add), R="s_tmp", W="s_sa")
    ph.tt(V, S0[:], S0[:], kB(1), ALU.mult, R=["s_S0", "s_vec", "s_tmp"], W="s_S0")
    ph.tt(V, tmp[:], bc(sa[:, :].unsqueeze(2), [128, 64, 64]), kB(5), ALU.mult, R=["s_sa", "s_vec"], W="s_tmp")
    ph.tt(V, S0[:], S0[:], tmp[:], ALU.subtract, R=["s_S0", "s_tmp"], W="s_S0")
    ph.tt(V, tmp[:], bc(vec[:, 3, :].unsqueeze(2), [128, 64, 64]), kB(2), ALU.mult, R=["s_vec", "s_S0"], W="s_tmp")
    ph.tt(V, S0[:], S0[:], tmp[:], ALU.add, R=["s_S0", "s_tmp"], W="s_S0")
    ph.dma("act", I["s_wkv"], S0[:].rearrange("p a b -> p (a b)"), R="s_S0")
    ph.tt(V, tmp[:], S0[:], kB(0), ALU.mult, R=["s_S0", "s_vec"], W="s_tmp")
    ph.op(V, lambda e: e.tensor_reduce(out=yv[:], in_=tmp[:], axis=AX.X, op=ALU.add), R="s_tmp", W="s_yv")
    ph.dma("sp", I["SY"], yv[:], R="s_yv", W="SYd")
    Ysb = L["Ysb"]
    ph.dma("sp", Ysb[:NS].rearrange("p a b -> p (a b)"), I["SY"].rearrange("(b h) v -> b (h v)", h=8), R="SYd", W="Ysb")
    groupnorm_out(ph, L, 0, NS)


def phase3(nc, I, G0, W3, WFI):
    ph = Ph(nc, "p3")
    V = "dve"
    W3 = alloc_w3(nc, ph.st)
    load_w3(ph, I, W3)
    rwo, glu, wo = W3["rwo"], W3["glu"], W3["wo"]
    for k in range(8):
        ph.dma("pool", WFI[:, k, :], I["w_ffn_in"][k * 128:(k + 1) * 128, :], W="wfi_pre")
    yf = ph.sb("yf", [128, 4, 512], BF16); zz = ph.sb("zz", [128, 4, 512], BF16); gt = ph.sb("gt", [128, 16, 512], BF16)
    trw = ph.sb("trw", [128, 8, 512], F32); mg = ph.sb("mg", [128, 8, 512], BF16)
    sgb = [ph.sb("sgb%d" % i, [128, 512], F32) for i in range(2)]
    s5t = [ph.sb("s5t%d" % i, [128, 512], F32) for i in range(2)]
    xts = [ph.sb("xt%d" % i, [128, D], F32) for i in range(2)]
    pm = [ph.ps("pm%d" % i, [128, 512], F32) for i in range(6)]
    npm = nx = ns = 0
    for (t0, nt) in BLOCKS:
        P = min(128, nt)
        r3 = lambda name: I[name][:, t0:t0 + nt].rearrange("(m p) t -> p m t", p=128)
        ph.dma("sp", yf[:, :, :nt], r3("YF"), W="yf"); ph.dma("sp", zz[:, :, :nt], r3("ZZ"), W="zz")
        ph.dma("act", gt[:, :, :nt], r3("GT"), W="gt")
        for m in range(8):
            pb = pm[npm % 6]; pk = "pm%d" % (npm % 6); npm += 1
            for k in range(4):
                ph.mm(pb[:, :nt], rwo[:, k, m * 128:(m + 1) * 128], yf[:, k, :nt], k == 0, k == 3, R=["rwo", "yf"], W=pk)
            ph.tt(V, trw[:, m, :nt], pb[:, :nt], gt[:, m, :nt], ALU.mult, R=[pk, "gt"], W="trw%d" % m)
        for m in range(8):
            pa = pm[npm % 6]; pka = "pm%d" % (npm % 6); npm += 1
            pb = pm[npm % 6]; pkb = "pm%d" % (npm % 6); npm += 1
            for k in range(4):
                ph.mm(pa[:, :nt], glu[:, k, m * 128:(m + 1) * 128], zz[:, k, :nt], k == 0, k == 3, R=["glu", "zz"], W=pka)
            for k in range(4):
                ph.mm(pb[:, :nt], glu[:, k, D + m * 128:D + (m + 1) * 128], zz[:, k, :nt], k == 0, k == 3,
                      R=["glu", "zz"], W=pkb)
            sg = sgb[ns % 2]; sk = "sgb%d" % (ns % 2); s5 = s5t[ns % 2]; s5k = "s5t%d" % (ns % 2); ns += 1
            ph.act(sg[:, :nt], pb[:, :nt], AF.Sigmoid, R=pkb, W=sk)
            ph.tt(V, s5[:, :nt], pa[:, :nt], sg[:, :nt], ALU.mult, R=[pka, sk], W=s5k)
            ph.tt(V, s5[:, :nt], s5[:, :nt], gt[:, 8 + m, :nt], ALU.mult, R=[s5k, "gt"], W=s5k)
            ph.tt(V, mg[:, m, :nt], s5[:, :nt], trw[:, m, :nt], ALU.add, R=[s5k, "trw%d" % m], W="mg")
        for s in range((nt + 127) // 128):
            xt = xts[nx % 2]; xk = "xt%d" % (nx % 2); nx += 1
            rows = slice(t0 + s * 128, t0 + s * 128 + P)
            ph.dma("sp", xt[:P, :], I["xall"][rows, :], W=xk)
            for half in range(2):
                pb = pm[npm % 6]; pk = "pm%d" % (npm % 6); npm += 1
                for k in range(8):
                    ph.mm(pb[:P, :], mg[:, k, s * 128:s * 128 + P], wo[:, k, half * 512:(half + 1) * 512], k == 0, k == 7,
                          R=["mg", "wo"], W=pk)
                ph.tt(V, xt[:P, half * 512:(half + 1) * 512], xt[:P, half * 512:(half + 1) * 512], pb[:P, :], ALU.add,
                      R=[pk, xk], W=xk)
            ph.dma("pool", I["X1"][rows, :], xt[:P, :], R=xk)
    ph.finish()


def phase4(nc, I, G0, WFI):
    ph = Ph(nc, "p4")
    V = "dve"
    G = norm_scratch(ph, G0)
    identf = G0["identf"]
    wfi = WFI; wfo = ph.sb("wfo", [128, 22, D], BF16)
    for k in range(22):
        ph.dma("pool", wfo[:, k, :], I["w_ffn_out"][k * 128:(k + 1) * 128, :], W="wfo")
    g2c = ph.sb("g2c", [128, 8], F32); load_col(ph, g2c[:], I["ln2_g"], 8, "g2c")
    cw = ph.sb("cw", [128, 3, 22], F32); cb = ph.sb("cb", [128, 22], F32)
    ph.dma("sp", cw[:], I["conv_w"].rearrange("t (f p) -> p t f", p=128), W="cw", slow=True)
    load_col(ph, cb[:], I["conv_b"], 22, "cb")
    hTs = [ph.sb("hT%d" % i, [128, 8, 512], BF16) for i in range(2)]
    hid = ph.sb("hid", [128, 22, 512], BF16)
    xts = [ph.sb("xt%d" % i, [128, D], F32) for i in range(2)]
    At = [ph.sb("At%d" % i, [128, 514], F32) for i in range(2)]
    acc = [ph.sb("acc%d" % i, [128, 512], F32) for i in range(2)]
    cc = ph.sb("cc", [128, 22, 2], F32)
    ph.memset(V, cc[:].rearrange("p a b -> p (a b)"), 0.0, W="cc")
    pm = [ph.ps("pm%d" % i, [128, 512], F32) for i in range(6)]
    scs = ph.sb("scs", [NS, 2816], F32)
    scT = ph.sb("scT", [128, 22, 2, NS], F32)
    aout = scs
    npm = na = 0
    NR, FI, FO = [], [], []
    for bi_, (t0, nt) in enumerate(BLOCKS):
        P = min(128, nt)
        hT = hTs[bi_ % 2]; hk = "hT%d" % (bi_ % 2)
        sample = nt < 128
        nsub = (nt + 127) // 128
        ph.rec_begin()
        for s in range(nsub):
            rows = slice(t0 + s * 128, t0 + s * 128 + P)
            ph.dma("sp", xts[s % 2][:P, :], I["X1"][rows, :], W="xt%d" % (s % 2))
            rms_to_hT(ph, G, xts[s % 2], P, g2c, hT, s * 128, str(s % 2), "g2c", hk)
        NR.append(ph.rec_end())
        ph.rec_begin()
        if sample:
            for tt_ in range(2):
                ph.dma("sp", scs[:], I["st_conv"][:, tt_, :], W="scs")
                for q in range(6):
                    pb = pm[npm % 6]; pk = "pm%d" % (npm % 6); npm += 1
                    fs = list(range(q * 4, min(22, q * 4 + 4)))
                    for j, f_ in enumerate(fs):
                        ph.tr(pb[:, j * NS:(j + 1) * NS], scs[:, f_ * 128:(f_ + 1) * 128], identf[:NS, :NS], R="scs", W=pk)
                    ph.cp(V, scT[:, fs[0]:fs[-1] + 1, tt_, :], pb[:, 0:len(fs) * NS].rearrange("p (a b) -> p a b", b=NS),
                          R=pk, W="scT")
        for f in range(22):
            pa = pm[npm % 6]; pka = "pm%d" % (npm % 6); npm += 1
            pb = pm[npm % 6]; pkb = "pm%d" % (npm % 6); npm += 1
            for k in range(8):
                ph.mm(pa[:, :nt], wfi[:, k, f * 128:(f + 1) * 128], hT[:, k, :nt], k == 0, k == 7, R=["wfi", hk], W=pka)
            for k in range(8):
                ph.mm(pb[:, :nt], wfi[:, k, 2816 + f * 128:2816 + (f + 1) * 128], hT[:, k, :nt], k == 0, k == 7,
                      R=["wfi", hk], W=pkb)
            A = At[na % 2]; ak = "At%d" % (na % 2); ac = acc[na % 2]; ck = "acc%d" % (na % 2); na += 1
            ph.cp("act", A[:, 2:2 + nt], pa[:, :nt], R=pka, W=ak)
            if not sample:
                ph.cp(V, A[:, 0:2], cc[:, f, :], R="cc", W=ak)
                a0, a1, a2 = A[:, 0:nt], A[:, 1:1 + nt], A[:, 2:2 + nt]
            else:
                a0, a1, a2 = scT[:, f, 0, :], scT[:, f, 1, :], A[:, 2:2 + nt]
            ph.ts(V, ac[:, :nt], a0, cw[:, 0, f:f + 1], ALU.mult, cb[:, f:f + 1], ALU.add, R=[ak, "scT", "cw", "cb"], W=ck)
            ph.stt(ac[:, :nt], a1, cw[:, 1, f:f + 1], ac[:, :nt], ALU.mult, ALU.add, R=[ak, "scT", "cw", ck], W=ck)
            ph.stt(ac[:, :nt], a2, cw[:, 2, f:f + 1], ac[:, :nt], ALU.mult, ALU.add, R=[ak, "cw", ck], W=ck)
            ph.act(ac[:, :nt], ac[:, :nt], AF.Gelu_apprx_tanh, R=ck, W=ck)
            ph.tt(V, hid[:, f, :nt], ac[:, :nt], pb[:, :nt], ALU.mult, R=[ck, pkb], W="hid")
            if not sample:
                ph.cp(V, cc[:, f, :], A[:, nt:nt + 2], R=ak, W="cc")
            else:
                po = pm[npm % 6]; pko = "pm%d" % (npm % 6); npm += 1
                ph.tr(po[:NS, 0:128], A[:, 2:2 + NS], identf[:], R=ak, W=pko)
                ph.cp(V, aout[:, f * 128:(f + 1) * 128], po[:NS, 0:128], R=pko, W="scs")
        if t0 + nt == T:
            for tt_ in range(2):
                ph.dma("sp", I["p_conv"][tt_].rearrange("(f p) -> p f", p=128), cc[:, :, tt_], R="cc", slow=True)
        if sample:
            ph.dma("sp", I["s_conv"][:, 1, :], aout[:], R="scs")
            ph.dma("act", I["s_conv"][:, 0, :], I["st_conv"][:, 1, :])
        FI.append(ph.rec_end())
        ph.rec_begin()
        for s in range(nsub):
            rows = slice(t0 + s * 128, t0 + s * 128 + P)
            xt = xts[s % 2]; xk = "xt%d" % (s % 2)
            ph.dma("sp", xt[:P, :], I["X1"][rows, :], W=xk)
            for half in range(2):
                pb = pm[npm % 6]; pk = "pm%d" % (npm % 6); npm += 1
                for f in range(22):
                    ph.mm(pb[:P, :], hid[:, f, s * 128:s * 128 + P], wfo[:, f, half * 512:(half + 1) * 512], f == 0, f == 21,
                          R=["hid", "wfo"], W=pk)
                ph.tt(V, xt[:P, half * 512:(half + 1) * 512], xt[:P, half * 512:(half + 1) * 512], pb[:P, :],
                      ALU.add, R=[pk, xk], W=xk)
            ph.dma("pool", I["X2"][rows, :], xt[:P, :], R=xk)
        FO.append(ph.rec_end())
    ph.play(NR[0])
    for b_ in range(len(BLOCKS)):
        ph.play(FI[b_], NR[b_ + 1] if b_ + 1 < len(BLOCKS) else [])
        ph.play(FO[b_])
    ph.finish()


def phase5(nc, I, G0):
    ph = Ph(nc, "p5")
    V = "dve"
    NB = 4
    Gs = [norm_scratch(ph, G0, "a")]
    for i in range(1, NB):
        Gs.append(norm_scratch(ph, G0, "abcd"[i], eps=Gs[0]["eps"]))
    wpg = ph.sb("wpg", [128, 8, D], BF16); wpl = ph.sb("wpl", [128, 2, D], BF16)
    for k in range(8):
        ph.dma("pool", wpg[:, k, :], I["w_ple_gate"][k * 128:(k + 1) * 128, :], W="wpg")
    for k in range(2):
        ph.dma("pool", wpl[:, k, :], I["w_ple"][k * 128:(k + 1) * 128, :], W="wpl")
    g3c = ph.sb("g3c", [128, 8], F32); load_col(ph, g3c[:], I["ln3_g"], 8, "g3c")
    fg = ph.sb("fg", [128, D], F32)
    ph.dma("sp", fg[:], I["final_g"].partition_broadcast(128), W="fg")
    hTs = [ph.sb("hT%d" % i, [128, 8, 128], BF16) for i in range(NB)]
    xts = [ph.sb("xt%d" % i, [128, D], F32) for i in range(NB)]
    sg = [ph.sb("sg%d" % i, [128, 512], F32) for i in range(NB)]
    yo = [ph.sb("yo%d" % i, [128, D], F32) for i in range(NB)]
    pm = [ph.ps("pm%d" % i, [128, 512], F32) for i in range(3)]
    pq = ph.ps("pq", [128, 8, 128], BF16)
    subs = []
    for (t0, nt) in BLOCKS:
        P = min(128, nt)
        for s in range((nt + 127) // 128):
            subs.append((t0 + s * 128, P))
    NSUB = len(subs)
    pball = ph.sb("pball", [128, NSUB, 256], BF16)
    pTall = ph.sb("pTall", [128, NSUB, 2, 128], BF16)
    for i, (r0, P) in enumerate(subs):
        ph.dma("pool", pball[:P, i, :], I["pall"][r0:r0 + P, :], W="pb%d" % i)
    for i0_ in range(0, NSUB, 4):
        grp = list(range(i0_, min(NSUB, i0_ + 4)))
        for j, i in enumerate(grp):
            P = subs[i][1]
            for k in range(2):
                ph.tr(pq[:, 2 * j + k, :P], pball[:P, i, k * 128:(k + 1) * 128], G0["identb"][:P, :P],
                      R=["pb%d" % i, "identb"], W="pq")
        for j, i in enumerate(grp):
            P = subs[i][1]
            ph.cp("act", pTall[:, i, :, :P], pq[:, 2 * j:2 * j + 2, :P], R="pq", W="pT%d" % i)
    npm = nsg = 0
    FR, BK = [], []
    for i, (r0, P) in enumerate(subs):
        rows = slice(r0, r0 + P)
        i2 = i % NB
        xt = xts[i2]; xk = "xt%d" % i2
        G = Gs[i2]; hT = hTs[i2]; hk = "hT%d" % i2
        ph.rec_begin()
        ph.dma("sp", xt[:P, :], I["X2"][rows, :], W=xk)
        rms_to_hT(ph, G, xt, P, g3c, hT, 0, str(i2), "g3c", hk)
        FR.append(ph.rec_end())
        ph.rec_begin()
        for half in range(2):
            cs_ = slice(half * 512, (half + 1) * 512)
            pg = pm[npm % 3]; pgk = "pm%d" % (npm % 3); npm += 1
            pe = pm[npm % 3]; pek = "pm%d" % (npm % 3); npm += 1
            for k in range(8):
                ph.mm(pg[:P, :], hT[:, k, :P], wpg[:, k, cs_], k == 0, k == 7, R=[hk, "wpg"], W=pgk)
            for k in range(2):
                ph.mm(pe[:P, :], pTall[:, i, k, :P], wpl[:, k, cs_], k == 0, k == 1, R=["pT%d" % i, "wpl"], W=pek)
            sgt = sg[nsg % NB]; sgk = "sg%d" % (nsg % NB); nsg += 1
            ph.act(sgt[:P, :], pg[:P, :], AF.Sigmoid, R=pgk, W=sgk)
            ph.tt(V, sgt[:P, :], sgt[:P, :], pe[:P, :], ALU.mult, R=[sgk, pek], W=sgk)
            ph.tt(V, xt[:P, cs_], xt[:P, cs_], sgt[:P, :], ALU.add, R=[sgk, xk, "xn" + G["sx"]], W=xk)
        ss = G["ss"]; sq = G["sq"]; kss = "ss" + G["sx"]; ksq = "sq" + G["sx"]
        ph.act(sq[:P, :], xt[:P, :], AF.Square, R=xk, W=[ksq, kss], accum=ss[:P, 0:1])
        ph.act(ss[:P, 1:2], ss[:P, 0:1], AF.Sqrt, R=[kss, "eps"], W=kss, bias=G["eps"][:P, 0:1], scale=1.0 / D)
        ph.op(V, lambda e, ss=ss, P=P: e.reciprocal(out=ss[:P, 3:4], in_=ss[:P, 1:2]), R=kss, W=kss + "3")
        y = yo[i2]; yk = "yo%d" % i2
        ph.stt(y[:P, :], xt[:P, :], ss[:P, 3:4], fg[:P, :], ALU.mult, ALU.mult, R=[xk, kss + "3", "fg"], W=yk)
        ph.dma("pool", I["y"][rows, :], y[:P, :], R=yk)
        BK.append(ph.rec_end())
    AHEAD = 2
    for i in range(min(AHEAD, NSUB)):
        ph.play(FR[i])
    for i in range(NSUB):
        if i + AHEAD < NSUB:
            ph.play(FR[i + AHEAD])
        ph.play(BK[i])
    ph.finish()


_CACHE = {}


def _consts():
    i = np.arange(128)
    c = {}
    c["c_ident"] = np.eye(128, dtype=np.float32)
    c["c_msl"] = (i[None, :] < i[:, None]).astype(np.float32)
    c["c_msu"] = (i[:, None] < i[None, :]).astype(np.float32)
    c["c_mui"] = (i[:, None] <= i[None, :]).astype(np.float32)
    c["c_blk64"] = ((i[:, None] // 64) == (i[None, :] // 64)).astype(np.float32)
    c["c_blk32"] = ((i[:, None] // 32) == (i[None, :] // 32)).astype(np.float32)
    c["c_rowgp"] = (((i[:, None] // 16) % 2) == (i[None, :] // 64)).astype(np.float32)
    return c


def make_in_maps(inp):
    f = lambda a: np.ascontiguousarray(np.asarray(a, dtype=np.float32))
    cst = _consts()
    shared = {}
    for k in ("ln1_g", "w_in", "mu_shift", "w0", "w2", "a0", "a2", "g2", "k_k", "k_a", "lnx_g", "lnx_b", "w_rw_out",
              "A_re", "A_im", "log_dt", "B_re", "B_im", "D_skip", "w_glu", "w_out", "ln2_g", "w_ffn_in", "conv_w",
              "conv_b", "w_ffn_out", "ln3_g", "w_ple_gate", "w_ple"):
        shared[k] = f(inp[k])[0]
    shared["r_k"] = f(inp["r_k"])[0].reshape(512)
    shared["C_re"] = f(inp["C_re"])[0].reshape(512, 64)
    shared["C_im"] = f(inp["C_im"])[0].reshape(512, 64)
    shared["final_g"] = f(inp["final_g"])
    shared.update(cst)
    xp, xs = f(inp["x_prompt"]), f(inp["x_sample"])
    pp, psm = f(inp["p_prompt"])[0], f(inp["p_sample"])[0]
    in_maps = []
    for c in range(8):
        sl = slice(NS * c, NS * c + NS)
        m = dict(shared)
        m["xall"] = np.concatenate([xp[c], xs[sl, 0]], 0)
        m["pall"] = np.concatenate([pp[c], psm[sl, 0]], 0)
        m["st_shift"] = f(inp["state_shift"])[0, sl]
        m["st_wkv"] = f(inp["state_wkv"])[0, sl].reshape(128, 4096)
        m["st_re"] = f(inp["state_ssm_re"])[0, sl].reshape(NS, 2048)
        m["st_im"] = f(inp["state_ssm_im"])[0, sl].reshape(NS, 2048)
        m["st_conv"] = f(inp["state_conv"])[0, sl]
        in_maps.append({k: np.ascontiguousarray(v) for k, v in m.items()})
    return in_maps


def kernel(**inp):
    f = lambda a: np.ascontiguousarray(np.asarray(a, dtype=np.float32))
    if "nc" not in _CACHE:
        _CACHE["nc"] = build_program()
    nc = _CACHE["nc"]
    in_maps = make_in_maps(inp)
    res = run_bass_kernel_spmd(nc, in_maps, core_ids=list(range(8)))
    R = res.results
    cat = lambda fn: np.stack([fn(r) for r in R], 0)
    y_prompt = cat(lambda r: r["y"][:T])
    y_sample = np.concatenate([r["y"][T:] for r in R], 0)[:, None, :]
    p_shift = cat(lambda r: r["p_shift"])[None]
    p_wkv = cat(lambda r: r["p_wkv"].reshape(8, 64, 64).transpose(0, 2, 1))[None]
    p_re = cat(lambda r: r["p_re"].reshape(32, 64))[None]
    p_im = cat(lambda r: r["p_im"].reshape(32, 64))[None]
    p_conv = cat(lambda r: r["p_conv"])[None]
    s_shift = np.concatenate([r["s_shift"] for r in R], 0)[None]
    s_wkv = np.concatenate([r["s_wkv"].reshape(NS, 8, 64, 64) for r in R], 0)[None]
    s_re = np.concatenate([r["s_re"].reshape(NS, 32, 64) for r in R], 0)[None]
    s_im = np.concatenate([r["s_im"].reshape(NS, 32, 64) for r in R], 0)[None]
    s_conv = np.concatenate([r["s_conv"] for r in R], 0)[None]
    outs = (y_prompt, y_sample, p_shift, p_wkv, p_re, p_im, p_conv, s_shift, s_wkv, s_re, s_im, s_conv)
    return tuple(np.ascontiguousarray(o.astype(np.float32)) for o in outs)
```

```python
import contextlib
import math
import numpy as np
import concourse.bass as bass
import concourse.mybir as mybir
from concourse.bass_utils import run_bass_kernel_spmd

F32 = mybir.dt.float32
BF16 = mybir.dt.bfloat16
AF = mybir.ActivationFunctionType
ALU = mybir.AluOpType
AX = mybir.AxisListType

T = 2048
NS = 16
NT = T + NS
D = 1024
CS = 8
SCAN_ENG = "pool"
import os as _os
BUB = int(_os.environ.get("K_BUB", "48"))
BUB2 = int(_os.environ.get("K_BUB2", "0"))
SYO = float(_os.environ.get("K_SYO", "0.5"))
STRICT1 = int(_os.environ.get("K_ST1", "0"))
STRICT4 = int(_os.environ.get("K_ST4", "1"))
C1 = math.exp(-0.5)
BLOCKS = [(0, 512), (512, 512), (1024, 512), (1536, 512), (2048, 16)]

ENGS = ("pe", "act", "dve", "pool", "sp")
NDSEM = 12


class _Op:
    __slots__ = ("eng", "fn", "deps", "dma", "observed", "tok", "idx", "dslot")

    def __init__(self, eng, fn, dma):
        self.eng, self.fn, self.dma = eng, fn, dma
        self.deps = set()
        self.observed = False
        self.tok = None
        self.dslot = None


class Sched:
    def __init__(self, nc):
        self.nc = nc
        self.ops = []
        self.last_w = {}
        self.readers = {}
        self.dma_rr = {e: 0 for e in ENGS}
        self.dma_prev = {}
        self.excl = set()

    def _add(self, eng, fn, reads, writes, dma):
        op = _Op(eng, fn, dma)
        op.idx = len(self.ops)
        if self.excl:
            ex = tuple(b for b in reads if b in self.excl)
            if ex:
                writes = tuple(writes) + ex
        for b in reads:
            w = self.last_w.get(b)
            if w is not None:
                op.deps.add(w)
        for b in writes:
            w = self.last_w.get(b)
            if w is not None:
                op.deps.add(w)
            for r in self.readers.get(b, ()):
                op.deps.add(r)
        if dma:
            slot = (eng, self.dma_rr[eng] % NDSEM)
            self.dma_rr[eng] += 1
            op.dslot = slot
            prev = self.dma_prev.get(slot)
            if prev is not None:
                op.deps.add(prev)
            self.dma_prev[slot] = op.idx
        op.deps.discard(op.idx)
        self.ops.append(op)
        for b in writes:
            self.last_w[b] = op.idx
            self.readers[b] = []
        for b in reads:
            if b not in writes:
                self.readers.setdefault(b, []).append(op.idx)
        return op.idx

    def emit(self):
        nc = self.nc
        ops = self.ops
        need = []
        for op in ops:
            nd = []
            for d in op.deps:
                p = ops[d]
                if (not p.dma) and (not op.dma) and p.eng == op.eng == "pe":
                    continue
                nd.append(d)
                p.observed = True
            need.append(nd)
        last = {}
        for op in ops:
            key = op.dslot if op.dma else op.eng
            last[key] = op.idx
        for i in last.values():
            ops[i].observed = True
        g = getattr(nc, "_gsem", None)
        if g is None:
            g = {"sems": {}, "cnt": {e: 0 for e in ENGS}, "dcnt": {}}
            nc._gsem = g
        cnt = g["cnt"]
        dcnt = g["dcnt"]
        for op in ops:
            if op.dma:
                dcnt[op.dslot] = dcnt.get(op.dslot, 0) + 16
                op.tok = (op.dslot, dcnt[op.dslot])
            elif op.observed:
                cnt[op.eng] += 1
                op.tok = (op.eng, cnt[op.eng])
        sems = g["sems"]
        for k in list(ENGS) + sorted(set(o.dslot for o in ops if o.dma)):
            if k not in sems:
                nm = k if isinstance(k, str) else "d_%s_%d" % k
                sems[k] = nc.alloc_semaphore(name="s_" + nm)
        with contextlib.ExitStack() as st:
            block = st.enter_context(nc.Block())
            per = {e: [o for o in ops if o.eng == e] for e in ENGS}
            hw = {"pe": block.tensor, "act": block.scalar, "dve": block.vector,
                  "pool": block.gpsimd, "sp": block.sync}

            def make(e):
                def body(eng):
                    seen = {}
                    for op in per[e]:
                        waits = {}
                        for d in need[op.idx]:
                            k, v = ops[d].tok
                            if v > waits.get(k, 0):
                                waits[k] = v
                        for k, v in waits.items():
                            if seen.get(k, 0) >= v:
                                continue
                            seen[k] = v
                            eng.wait_ge(sems[k], v)
                        ins = op.fn(eng)
                        if op.dma:
                            ins.then_inc(sems[op.tok[0]], 16)
                        elif op.observed:
                            ins.then_inc(sems[e], 1)
                    if e == "sp":
                        for key, i in last.items():
                            k, v = ops[i].tok
                            if seen.get(k, 0) < v:
                                eng.wait_ge(sems[k], v)
                return body

            for e in ENGS:
                hw[e](make(e))


def _L(x):
    if x is None:
        return ()
    if isinstance(x, str):
        return (x,)
    return tuple(x)


class Ph:
    _uid = [0]

    def __init__(self, nc, tag):
        self.nc = nc
        self.tag = tag
        self.st = contextlib.ExitStack()
        self.S = Sched(nc)

    def sb(self, name, shape, dt):
        return self.st.enter_context(self.nc.sbuf_tensor(self.tag + "_" + name, list(shape), dt))

    def ps(self, name, shape, dt):
        self.S.excl.add(name)
        return self.st.enter_context(self.nc.psum_tensor(self.tag + "_" + name, list(shape), dt))

    def finish(self):
        self.S.emit()
        self.st.close()

    def dbg(self, name, ap, shape, key, dt=F32):
        import os
        if os.environ.get("K_DBG_DUMP", "") == "":
            return
        t = self.nc.dram_tensor("dbg_" + name, list(shape), dt, kind="ExternalOutput").ap()
        self.dma("sp", t, ap, R=key)

    _rec = None

    def rec_begin(self):
        self._rec = []

    def rec_end(self):
        r, self._rec = self._rec, None
        return r

    def bubble(self, k):
        if self._rec is not None:
            self._rec.append(("bubble", k))

    def merge(self, *streams, spans=None):
        if spans is None:
            spans = [(0.0, 1.0)] * len(streams)
        keep = [i for i, st_ in enumerate(streams) if st_]
        spans = [spans[i] for i in keep]
        streams = [streams[i] for i in keep]
        pos = [0] * len(streams)
        out = []
        while True:
            best, bi = None, -1
            for i, st_ in enumerate(streams):
                if pos[i] < len(st_):
                    f = spans[i][0] + spans[i][1] * (pos[i] + 1.0) / len(st_)
                    if best is None or f < best:
                        best, bi = f, i
            if bi < 0:
                break
            item = streams[bi][pos[bi]]
            pos[bi] += 1
            if item[0] == "bubble":
                left = item[1]
                prog = True
                while left > 0 and prog:
                    prog = False
                    for j in range(len(streams)):
                        if j != bi and pos[j] < len(streams[j]) and left > 0:
                            it2 = streams[j][pos[j]]
                            pos[j] += 1
                            prog = True
                            if it2[0] != "bubble":
                                out.append(it2)
                                left -= 1
                continue
            out.append(item)
        return out

    def play(self, *streams, spans=None):
        for item in self.merge(*streams, spans=spans):
            if item[0] == "bubble":
                continue
            eng, fn, R, W, dma = item
            self.S._add(eng, fn, R, W, dma)

    def op(self, eng, fn, R=None, W=None):
        if self._rec is not None:
            self._rec.append((eng, fn, _L(R), _L(W), False))
        else:
            self.S._add(eng, fn, _L(R), _L(W), False)

    def dma(self, q, out, in_, R=None, W=None, slow=False):
        if slow:
            fn = lambda e: e.dma_start(out=out, in_=in_, allow_slow_non_contiguous=True)
        else:
            fn = lambda e: e.dma_start(out=out, in_=in_)
        if self._rec is not None:
            self._rec.append((q, fn, _L(R), _L(W), True))
        else:
            self.S._add(q, fn, _L(R), _L(W), True)

    def tt(self, eng, out, in0, in1, op, R=None, W=None):
        self.op(eng, lambda e: e.tensor_tensor(out=out, in0=in0, in1=in1, op=op), R, W)

    def ts(self, eng, out, in0, s1, op0, s2=None, op1=None, R=None, W=None):
        if op1 is None:
            self.op(eng, lambda e: e.tensor_scalar(out=out, in0=in0, scalar1=s1, scalar2=None, op0=op0), R, W)
        else:
            self.op(eng, lambda e: e.tensor_scalar(out=out, in0=in0, scalar1=s1, scalar2=s2, op0=op0, op1=op1), R, W)

    def stt(self, out, in0, scalar, in1, op0, op1, R=None, W=None):
        self.op("dve", lambda e: e.scalar_tensor_tensor(out=out, in0=in0, scalar=scalar, in1=in1, op0=op0, op1=op1), R, W)

    def act(self, out, in_, func, R=None, W=None, bias=None, scale=1.0, accum=None):
        kw = {}
        if bias is not None:
            kw["bias"] = bias
        if accum is not None:
            kw["accum_out"] = accum
        self.op("act", lambda e: e.activation(out=out, in_=in_, func=func, scale=scale, **kw), R, W)

    def cp(self, eng, out, in_, R=None, W=None):
        if eng == "act":
            self.op("act", lambda e: e.activation(out=out, in_=in_, func=AF.Copy), R, W)
        else:
            self.op(eng, lambda e: e.tensor_copy(out=out, in_=in_), R, W)

    def mm(self, out, lhsT, rhs, start, stop, R=None, W=None, tp=None):
        if tp is None:
            self.op("pe", lambda e: e.matmul(out, lhsT=lhsT, rhs=rhs, start=start, stop=stop), R, W)
        else:
            self.op("pe", lambda e: e.matmul(out, lhsT=lhsT, rhs=rhs, start=start, stop=stop, tile_position=tp), R, W)

    def tr(self, out, in_, ident, R=None, W=None):
        self.op("pe", lambda e: e.transpose(out, in_, ident), R, W)

    def memset(self, eng, ap, v, W=None):
        self.op(eng, lambda e: e.memset(ap, v), None, W)


def bc(ap, shape):
    return ap.to_broadcast(list(shape))


def rms_to_hT(ph, G, xt, P, gcol, hT, c0, tag, gkey, hkey="hT"):
    sq, ss, xn, pT = G["sq"], G["ss"], G["xn"], G["pT"]
    x_ = G.get("sx", "")
    ksq, kss, kxn, kpT = "sq" + x_, "ss" + x_, "xn" + x_, "pT" + x_
    ph.act(sq[:P, :], xt[:P, :], AF.Square, R="xt" + tag, W=[ksq, kss], accum=ss[:P, 0:1])
    ph.act(ss[:P, 1:2], ss[:P, 0:1], AF.Sqrt, R=[kss, "eps"], W=kss, bias=G["eps"][:P, 0:1], scale=1.0 / D)
    ph.op("dve", lambda e: e.reciprocal(out=ss[:P, 2:3], in_=ss[:P, 1:2]), R=kss, W=kss)
    ph.ts("dve", xn[:P, :], xt[:P, :], ss[:P, 2:3], ALU.mult, R=["xt" + tag, kss], W=kxn)
    for k in range(8):
        ph.tr(pT[:, k, :P], xn[:P, k * 128:(k + 1) * 128], G["identb"][:P, :P], R=[kxn, "identb"], W=kpT)
    ph.tt("dve", hT[:, :, c0:c0 + P], pT[:, :, :P], bc(gcol[:, :].unsqueeze(2), [128, 8, P]), ALU.mult,
          R=[kpT, gkey], W=hkey)


def load_col(ph, dst, src1d, n, key):
    ph.dma("sp", dst, src1d.rearrange("(k p) -> p k", p=128), W=key, slow=True)


def norm_scratch(ph, G0, sx="", eps=None):
    G = dict(G0)
    G["sx"] = sx
    G["sq"] = ph.sb("sq" + sx, [128, D], F32)
    G["ss"] = ph.sb("ss" + sx, [128, 4], F32)
    G["xn"] = ph.sb("xn" + sx, [128, D], BF16)
    G["pT"] = ph.ps("pT" + sx, [128, 8, 128], BF16)
    if eps is None:
        G["eps"] = ph.sb("eps", [128, 1], F32)
        ph.memset("dve", G["eps"][:], 1e-6, W="eps")
    else:
        G["eps"] = eps
    return G


def build_program(upto=9, debug=False):
    nc = bass.Bass("TRN2", target_bir_lowering=False)
    I = {}

    def inp(name, shape, dt=F32):
        I[name] = nc.dram_tensor(name, list(shape), dt, kind="ExternalInput").ap()

    def outp(name, shape):
        I[name] = nc.dram_tensor(name, list(shape), F32, kind="ExternalOutput").ap()

    def scratch(name, shape, dt):
        if debug:
            I[name] = nc.dram_tensor(name, list(shape), dt, kind="ExternalOutput").ap()
        else:
            I[name] = nc.dram_tensor(name, list(shape), dt).ap()
    if debug:
        scratch("d_BwT", [128, 4 * CS * 2 * 128], BF16); scratch("d_Kmat", [128, 4 * CS * 128], BF16)
        scratch("d_CwT", [128, CS * 2 * 16 * 32], BF16); scratch("d_Abar", [128, 64], F32)

    inp("xall", [NT, D]); inp("pall", [NT, 256])
    inp("st_shift", [NS, 1792]); inp("st_wkv", [128, 4096]); inp("st_re", [NS, 2048]); inp("st_im", [NS, 2048])
    inp("st_conv", [NS, 2, 2816])
    inp("ln1_g", [D]); inp("w_in", [D, 4352]); inp("mu_shift", [1792]); inp("w0", [512]); inp("w2", [64, 512])
    inp("a0", [512]); inp("a2", [64, 512]); inp("g2", [128, 512]); inp("k_k", [512]); inp("k_a", [512])
    inp("r_k", [512]); inp("lnx_g", [512]); inp("lnx_b", [512]); inp("w_rw_out", [512, D])
    inp("A_re", [32, 64]); inp("A_im", [32, 64]); inp("log_dt", [32]); inp("B_re", [32, 64, 16]); inp("B_im", [32, 64, 16])
    inp("C_re", [512, 64]); inp("C_im", [512, 64]); inp("D_skip", [512]); inp("w_glu", [512, 2048]); inp("w_out", [D, D])
    inp("ln2_g", [D]); inp("w_ffn_in", [D, 5632]); inp("conv_w", [3, 2816]); inp("conv_b", [2816]); inp("w_ffn_out", [2816, D])
    inp("ln3_g", [D]); inp("w_ple_gate", [D, D]); inp("w_ple", [256, D]); inp("final_g", [D])
    inp("c_ident", [128, 128]); inp("c_msl", [128, 128]); inp("c_msu", [128, 128]); inp("c_mui", [128, 128])
    inp("c_blk64", [128, 128]); inp("c_blk32", [128, 128]); inp("c_rowgp", [128, 128])
    outp("y", [NT, D]); outp("p_shift", [1792]); outp("p_wkv", [512, 64]); outp("p_re", [2048]); outp("p_im", [2048])
    outp("p_conv", [2, 2816]); outp("s_shift", [NS, 1792]); outp("s_wkv", [128, 4096]); outp("s_re", [NS, 2048])
    outp("s_im", [NS, 2048]); outp("s_conv", [NS, 2, 2816])
    scratch("PRW", [1792, NT], F32); scratch("UU", [512, NT], F32); scratch("GT", [2048, NT], BF16)
    scratch("YF", [512, NT], BF16); scratch("ZZ", [512, NT], BF16); scratch("X1", [NT, D], F32); scratch("X2", [NT, D], F32)
    scratch("SW", [6, NS, 512], F32); scratch("SY", [128, 64], F32)

    with contextlib.ExitStack() as gst:
        def gsb(name, shape, dt):
            return gst.enter_context(nc.sbuf_tensor("g_" + name, list(shape), dt))
        G0 = {}
        G0["identb"] = gsb("identb", [128, 128], BF16)
        G0["identf"] = gsb("identf", [128, 128], F32)
        with contextlib.ExitStack() as g2:
            def g2sb(name, shape, dt):
                return g2.enter_context(nc.sbuf_tensor("g_" + name, list(shape), dt))
            G0["BwT"] = g2sb("BwT", [128, 4, CS, 2, 128], BF16)
            G0["Kmat"] = g2sb("Kmat", [128, 4, CS, 128], BF16)
            G0["CwT"] = g2sb("CwT", [128, CS, 2, 16, 32], BF16)
            G0["Abar"] = g2sb("Abar", [128, 2, 2, 16], F32)
            if upto >= 1:
                phase1(nc, I, G0, debug)
            else:
                phase0(nc, I, G0, debug)
            if upto >= 2:
                phase2(nc, I, G0, True)
            if upto >= 2.5:
                phase2(nc, I, G0, False)
        g4 = contextlib.ExitStack()
        WFI = g4.enter_context(nc.sbuf_tensor("g_wfi", [128, 8, 5632], BF16))
        if upto >= 3:
            phase3(nc, I, G0, None, WFI)
        if upto >= 4:
            phase4(nc, I, G0, WFI)
        g4.close()
        if upto >= 5:
            phase5(nc, I, G0)
    return nc


def phase0(nc, I, G0, debug=False, ph=None):
    own = ph is None
    if own:
        ph = Ph(nc, "p0")
        ph.dma("pool", G0["identb"][:], I["c_ident"], W="identb")
        ph.dma("sp", G0["identf"][:], I["c_ident"], W="identf")
    sb = ph.sb
    lr = sb("lr", [128, 16], F32); li = sb("li", [128, 16], F32); dtl = sb("dtl", [128, 16], F32)
    Bre = sb("Bre", [128, 16, 16], F32); Bim = sb("Bim", [128, 16, 16], F32)
    ph.dma("sp", lr[:], I["A_re"].rearrange("(P gp) n -> (gp n) P", gp=2), W="lr", slow=True)
    ph.dma("sp", li[:], I["A_im"].rearrange("(P gp) n -> (gp n) P", gp=2), W="li", slow=True)
    ldt2 = I["log_dt"].rearrange("(P gp) -> gp P", gp=2)
    for gp in range(2):
        ph.dma("sp", dtl[64 * gp:64 * gp + 64, :], ldt2[gp].partition_broadcast(64), W="dtl", slow=True)
    ph.dma("sp", Bre[:], I["B_re"].rearrange("(P gp) n c -> (gp n) P c", gp=2), W="Bre")
    ph.dma("sp", Bim[:], I["B_im"].rearrange("(P gp) n c -> (gp n) P c", gp=2), W="Bim")
    rowgp = sb("rowgp", [128, 128], F32); blk32 = sb("blk32", [128, 128], F32)
    ph.dma("sp", rowgp[:], I["c_rowgp"], W="rowgp"); ph.dma("sp", blk32[:], I["c_blk32"], W="blk32")
    CT = [sb("CTr", [128, 4, 128], F32), sb("CTi", [128, 4, 128], F32)]
    c2 = sb("c2", [128, 128], F32)
    pA = ph.ps("pA", [128, 4, 128], F32)
    for ri, nm in enumerate(("C_re", "C_im")):
        for k in range(4):
            src = I[nm][k * 128:(k + 1) * 128, :]
            ph.dma("sp", c2[:, 0:64], src, W="c2"); ph.dma("sp", c2[:, 64:128], src, W="c2")
            ph.tt("dve", c2[:], c2[:], rowgp[:], ALU.mult, R=["c2", "rowgp"], W="c2")
            ph.tr(pA[:, k, :], c2[:], G0["identf"][:], R=["c2", "identf"], W="pA")
        ph.cp("dve", CT[ri][:], pA[:], R="pA", W="CT%d" % ri)
    t = {n: sb(n, [128, 16], F32) for n in ("dt", "e1", "mag", "ang", "sa", "ca", "sinv", "cosv", "ar", "ai", "den",
                                             "rden", "am1", "fr", "fi", "t1", "t2")}
    V = "dve"
    K = lambda *n: list(n)
    hpi = sb("hpi", [128, 1], F32)
    ph.memset(V, hpi[:], math.pi / 2, W="hpi")
    ph.act(t["dt"][:], dtl[:], AF.Exp, R="dtl", W="dt")
    ph.tt(V, t["e1"][:], lr[:], t["dt"][:], ALU.mult, R=K("lr", "dt"), W="e1")
    ph.act(t["mag"][:], t["e1"][:], AF.Exp, R="e1", W="mag")
    ph.tt(V, t["ang"][:], li[:], t["dt"][:], ALU.mult, R=K("li", "dt"), W="ang")
    ph.ts(V, t["sa"][:], t["ang"][:], 1.0 / 64, ALU.mult, R="ang", W="sa")
    ph.act(t["sinv"][:], t["sa"][:], AF.Sin, R="sa", W="sinv")
    ph.act(t["cosv"][:], t["sa"][:], AF.Sin, R=["sa", "hpi"], W="cosv", bias=hpi[:, 0:1])
    for _ in range(6):
        ph.tt(V, t["t1"][:], t["cosv"][:], t["cosv"][:], ALU.mult, R="cosv", W="t1")
        ph.tt(V, t["t2"][:], t["sinv"][:], t["sinv"][:], ALU.mult, R="sinv", W="t2")
        ph.stt(t["sinv"][:], t["cosv"][:], 2.0, t["sinv"][:], ALU.mult, ALU.mult, R=["cosv", "sinv", "t2"], W="sinv")
        ph.tt(V, t["cosv"][:], t["t1"][:], t["t2"][:], ALU.subtract, R=["t1", "t2", "sinv"], W="cosv")
    ph.tt(V, t["ar"][:], t["mag"][:], t["cosv"][:], ALU.mult, R=K("mag", "cosv"), W="ar")
    ph.tt(V, t["ai"][:], t["mag"][:], t["sinv"][:], ALU.mult, R=K("mag", "sinv"), W="ai")
    ph.tt(V, t["den"][:], lr[:], lr[:], ALU.mult, R="lr", W="den")
    ph.tt(V, t["t1"][:], li[:], li[:], ALU.mult, R="li", W="t1")
    ph.tt(V, t["den"][:], t["den"][:], t["t1"][:], ALU.add, R=K("den", "t1"), W="den")
    ph.op(V, lambda e: e.reciprocal(out=t["rden"][:], in_=t["den"][:]), R="den", W="rden")
    ph.ts(V, t["am1"][:], t["ar"][:], -1.0, ALU.add, R="ar", W="am1")
    ph.tt(V, t["t1"][:], t["am1"][:], lr[:], ALU.mult, R=K("am1", "lr", "den"), W="t1")
    ph.tt(V, t["t2"][:], t["ai"][:], li[:], ALU.mult, R=K("ai", "li"), W="t2")
    ph.tt(V, t["t1"][:], t["t1"][:], t["t2"][:], ALU.add, R=K("t1", "t2"), W="t1")
    ph.tt(V, t["fr"][:], t["t1"][:], t["rden"][:], ALU.mult, R=K("t1", "rden"), W="fr")
    ph.tt(V, t["t1"][:], t["ai"][:], lr[:], ALU.mult, R=K("ai", "lr", "fr"), W="t1")
    ph.tt(V, t["t2"][:], t["am1"][:], li[:], ALU.mult, R=K("am1", "li"), W="t2")
    ph.tt(V, t["t1"][:], t["t1"][:], t["t2"][:], ALU.subtract, R=K("t1", "t2"), W="t1")
    ph.tt(V, t["fi"][:], t["t1"][:], t["rden"][:], ALU.mult, R=K("t1", "rden"), W="fi")
    pwr = sb("pwr", [128, CS + 1, 16], F32); pwi = sb("pwi", [128, CS + 1, 16], F32)
    ph.memset(V, pwr[:, 0, :], 1.0, W="pw"); ph.memset(V, pwi[:, 0, :], 0.0, W="pw")
    for e in range(CS):
        ph.tt(V, t["t1"][:], pwr[:, e, :], t["ar"][:], ALU.mult, R=K("pw", "ar", "fi"), W="t1")
        ph.tt(V, t["t2"][:], pwi[:, e, :], t["ai"][:], ALU.mult, R=K("pw", "ai"), W="t2")
        ph.tt(V, pwr[:, e + 1, :], t["t1"][:], t["t2"][:], ALU.subtract, R=K("t1", "t2"), W="pw")
        ph.tt(V, t["t1"][:], pwr[:, e, :], t["ai"][:], ALU.mult, R=K("pw", "ai"), W="t1")
        ph.tt(V, t["t2"][:], pwi[:, e, :], t["ar"][:], ALU.mult, R=K("pw", "ar"), W="t2")
        ph.tt(V, pwi[:, e + 1, :], t["t1"][:], t["t2"][:], ALU.add, R=K("t1", "t2"), W="pw")
    Ab = G0["Abar"]
    ph.cp(V, Ab[:, 0, 0, :], pwr[:, CS, :], R="pw", W="Abar"); ph.cp(V, Ab[:, 0, 1, :], pwi[:, CS, :], R="pw", W="Abar")
    ph.cp(V, Ab[:, 1, 0, :], pwr[:, 1, :], R="pw", W="Abar"); ph.cp(V, Ab[:, 1, 1, :], pwi[:, 1, :], R="pw", W="Abar")
    bbr = sb("bbr", [128, 16, 16], F32); bbi = sb("bbi", [128, 16, 16], F32)
    u1 = sb("u1", [128, 16, 16], F32); u2 = sb("u2", [128, 16, 16], F32)
    frb = bc(t["fr"][:, :].unsqueeze(2), [128, 16, 16]); fib = bc(t["fi"][:, :].unsqueeze(2), [128, 16, 16])
    ph.tt(V, u1[:], Bre[:], frb, ALU.mult, R=K("Bre", "fr"), W="u1")
    ph.tt(V, u2[:], Bim[:], fib, ALU.mult, R=K("Bim", "fi"), W="u2")
    ph.tt(V, bbr[:], u1[:], u2[:], ALU.subtract, R=K("u1", "u2"), W="bbr")
    ph.tt(V, u1[:], Bim[:], frb, ALU.mult, R=K("Bim", "fr", "bbr"), W="u1")
    ph.tt(V, u2[:], Bre[:], fib, ALU.mult, R=K("Bre", "fi", "bbr"), W="u2")
    ph.tt(V, bbi[:], u1[:], u2[:], ALU.add, R=K("u1", "u2"), W="bbi")
    Ew = sb("Ew", [128, CS, 2, 16, 2, 16], F32)
    ph.memset(V, Ew[:].rearrange("p a b c d e -> p (a b c d e)"), 0.0, W="Ew")
    for e in range(CS):
        pr = bc(pwr[:, e, :].unsqueeze(2), [128, 16, 16]); pi = bc(pwi[:, e, :].unsqueeze(2), [128, 16, 16])
        ph.tt(V, u1[:], bbr[:], pr, ALU.mult, R=K("bbr", "pw", "Ew"), W="u1")
        ph.tt(V, u2[:], bbi[:], pi, ALU.mult, R=K("bbi", "pw", "Ew"), W="u2")
        ph.tt(V, u1[:], u1[:], u2[:], ALU.subtract, R=K("u1", "u2"), W="u1")
        for gp in range(2):
            ph.cp(V, Ew[64 * gp:64 * gp + 64, e, 0, :, gp, :], u1[64 * gp:64 * gp + 64, :, :], R="u1", W="Ew")
        ph.tt(V, u1[:], bbr[:], pi, ALU.mult, R=K("bbr", "pw", "Ew"), W="u1")
        ph.tt(V, u2[:], bbi[:], pr, ALU.mult, R=K("bbi", "pw", "Ew"), W="u2")
        ph.tt(V, u1[:], u1[:], u2[:], ALU.add, R=K("u1", "u2"), W="u1")
        for gp in range(2):
            ph.cp(V, Ew[64 * gp:64 * gp + 64, e, 1, :, gp, :], u1[64 * gp:64 * gp + 64, :, :], R="u1", W="Ew")
    CTin = sb("CTin", [128, 4, 128], F32)
    ph.ts(V, CTin[:], CT[1][:], -1.0, ALU.mult, R="CT1", W="CTin")
    pB = [ph.ps("pB%d" % i, [128, 4, 128], F32) for i in range(2)]
    n = 0
    for j in range(CS):
        e = CS - 1 - j
        for ri in range(2):
            pb = pB[n % 2]; n += 1
            for k in range(4):
                src = Ew[:, e, ri, 4 * k:4 * k + 4, :, :].rearrange("p a b c -> p (a b c)")
                ph.tr(pb[:, k, :], src, G0["identf"][:], R=["Ew", "identf"], W="pB%d" % ((n - 1) % 2))
            ph.cp("act" if n % 2 else "dve", G0["BwT"][:, :, j, ri, :], pb[:], R="pB%d" % ((n - 1) % 2), W="BwT")
    for tau in range(CS):
        pb = pB[n % 2]; key = "pB%d" % (n % 2); n += 1
        for k in range(4):
            lr_ = Ew[:, tau, 0, 4 * k:4 * k + 4, :, :].rearrange("p a b c -> p (a b c)")
            li_ = Ew[:, tau, 1, 4 * k:4 * k + 4, :, :].rearrange("p a b c -> p (a b c)")
            ph.mm(pb[:, k, :], lr_, CT[0][:, k, :], True, False, R=["Ew", "CT0"], W=key)
            ph.mm(pb[:, k, :], li_, CTin[:, k, :], False, True, R=["Ew", "CTin"], W=key)
        ph.tt(V, G0["Kmat"][:, :, tau, :], pb[:], bc(blk32[:, :].unsqueeze(1), [128, 4, 128]), ALU.mult,
              R=[key, "blk32"], W="Kmat")
    w1 = sb("w1", [128, 16, 32], F32); w2_ = sb("w2", [128, 16, 32], F32)
    CTr3 = CT[0][:].rearrange("p k (a b) -> p (k a) b", a=4); CTi3 = CT[1][:].rearrange("p k (a b) -> p (k a) b", a=4)
    for i in range(CS):
        pr = bc(pwr[:, i + 1, :].unsqueeze(2), [128, 16, 32]); pi = bc(pwi[:, i + 1, :].unsqueeze(2), [128, 16, 32])
        ph.tt(V, w1[:], CTr3, pr, ALU.mult, R=K("CT0", "pw", "CwT"), W="w1")
        ph.tt(V, w2_[:], CTi3, pi, ALU.mult, R=K("CT1", "pw", "CwT"), W="w2")
        ph.tt(V, G0["CwT"][:, i, 0, :, :], w1[:], w2_[:], ALU.subtract, R=K("w1", "w2"), W="CwT")
        ph.tt(V, w1[:], CTr3, pi, ALU.mult, R=K("CT0", "pw", "CwT"), W="w1")
        ph.tt(V, w2_[:], CTi3, pr, ALU.mult, R=K("CT1", "pw", "CwT"), W="w2")
        ph.tt(V, w1[:], w1[:], w2_[:], ALU.add, R=K("w1", "w2"), W="w1")
        ph.ts(V, G0["CwT"][:, i, 1, :, :], w1[:], -1.0, ALU.mult, R="w1", W="CwT")
    if debug:
        ph.dma("sp", I["d_BwT"], G0["BwT"][:].rearrange("p a b c d -> p (a b c d)"), R="BwT")
        ph.dma("sp", I["d_Kmat"], G0["Kmat"][:].rearrange("p a b c -> p (a b c)"), R="Kmat")
        ph.dma("sp", I["d_CwT"], G0["CwT"][:].rearrange("p a b c d -> p (a b c d)"), R="CwT")
        ph.dma("sp", I["d_Abar"], G0["Abar"][:].rearrange("p a b c -> p (a b c)"), R="Abar")
    if own:
        ph.finish()


def phase1(nc, I, G0, debug=False):
    ph = Ph(nc, "p1")
    win = ph.sb("win", [128, 8, 4352], BF16)
    for k in range(8):
        ph.dma("pool", win[:, k, :], I["w_in"][k * 128:(k + 1) * 128, :], W="win%d" % k)
    ph.dma("pool", G0["identb"][:], I["c_ident"], W="identb")
    ph.dma("sp", G0["identf"][:], I["c_ident"], W="identf")
    ph.rec_begin()
    phase0(nc, I, G0, debug, ph=ph)
    s0 = ph.rec_end()
    ph.rec_begin()
    G = norm_scratch(ph, G0)
    g1c = ph.sb("g1c", [128, 8], F32)
    load_col(ph, g1c[:], I["ln1_g"], 8, "g1c")
    hTs = [ph.sb("hT%d" % i, [128, 8, 512], BF16) for i in range(2)]
    xts = [ph.sb("xt%d" % i, [128, D], F32) for i in range(2)]
    pm = [ph.ps("pm%d" % i, [128, 512], F32) for i in range(4)]
    stf = [ph.sb("stf%d" % i, [128, 512], F32) for i in range(4)]
    stb = [ph.sb("stb%d" % i, [128, 512], BF16) for i in range(3)]
    WK = ["win%d" % k for k in range(8)]
    nx = nf = nb = npm = 0
    pre1 = ph.rec_end()
    NR1, MM1 = [], []
    for bi_, (t0, nt) in enumerate(BLOCKS):
        P = min(128, nt)
        hT = hTs[bi_ % 2]; hk = "hT%d" % (bi_ % 2)
        ph.rec_begin()
        for s in range((nt + 127) // 128):
            xt = xts[nx % 2]; tg = str(nx % 2); nx += 1
            ph.dma("sp", xt[:P, :], I["xall"][t0 + s * 128:t0 + s * 128 + P, :], W="xt" + tg)
            rms_to_hT(ph, G, xt, P, g1c, hT, s * 128, tg, "g1c", hk)
        NR1.append(ph.rec_end())
        ph.rec_begin()
        for m in range(34):
            pb = pm[npm % 4]; pk = "pm%d" % (npm % 4); npm += 1
            for k in range(8):
                ph.mm(pb[:, :nt], win[:, k, m * 128:(m + 1) * 128], hT[:, k, :nt], k == 0, k == 7,
                      R=["win%d" % k, hk], W=pk)
            if m < 18:
                sf = stf[nf % 4]; sk = "stf%d" % (nf % 4); nf += 1
                ph.cp("dve" if m % 2 else "act", sf[:, :nt], pb[:, :nt], R=pk, W=sk)
                if m < 14:
                    ph.dma("pool", I["PRW"][m * 128:(m + 1) * 128, t0:t0 + nt], sf[:, :nt], R=sk)
                else:
                    ph.dma("pool", I["UU"][(m - 14) * 128:(m - 13) * 128, t0:t0 + nt], sf[:, :nt], R=sk)
            else:
                sbf = stb[nb % 3]; sk = "stb%d" % (nb % 3); nb += 1
                ph.act(sbf[:, :nt], pb[:, :nt], AF.Sigmoid, R=pk, W=sk)
                ph.dma("act", I["GT"][(m - 18) * 128:(m - 17) * 128, t0:t0 + nt], sbf[:, :nt], R=sk)
        MM1.append(ph.rec_end())
    s1 = pre1 + NR1[0]
    for b_ in range(len(BLOCKS)):
        if STRICT1:
            s1 = s1 + (NR1[b_ + 1] if b_ + 1 < len(BLOCKS) else []) + MM1[b_]
        else:
            s1 = s1 + ph.merge(MM1[b_], NR1[b_ + 1] if b_ + 1 < len(BLOCKS) else [])
    ph.play(s1, s0)
    ph.finish()


def alloc_w3(nc, st):
    t = lambda n, shp: st.enter_context(nc.sbuf_tensor("w3_" + n, shp, BF16))
    return {"rwo": t("rwo", [128, 4, D]), "glu": t("glu", [128, 4, 2048]), "wo": t("wo", [128, 8, D])}


def load_w3(ph, I, W3):
    for k in range(4):
        ph.dma("pool", W3["rwo"][:, k, :], I["w_rw_out"][k * 128:(k + 1) * 128, :], W="rwo")
        ph.dma("pool", W3["glu"][:, k, :], I["w_glu"][k * 128:(k + 1) * 128, :], W="glu")
    for k in range(8):
        ph.dma("pool", W3["wo"][:, k, :], I["w_out"][k * 128:(k + 1) * 128, :], W="wo")


def phase2(nc, I, G0, prompt, W3=None):
    ph = Ph(nc, "p2a" if prompt else "p2b")
    sb, ps = ph.sb, ph.ps
    V = "dve"
    if W3 is not None:
        load_w3(ph, I, W3)
    ph._s5tmp = [sb("s5a", [128, 2, 16], F32), sb("s5b", [128, 2, 16], F32)]
    ph._s5xb = sb("Xb", [128, 2, 16, 64], BF16)
    ph._s5du = sb("s5du", [128, 512], F32)
    if prompt:
        msl = sb("msl", [128, 128], BF16); msu = sb("msu", [128, 128], BF16); mui = sb("mui", [128, 128], BF16)
        ph.dma("pool", msl[:], I["c_msl"], W="msl"); ph.dma("pool", msu[:], I["c_msu"], W="msu")
        ph.dma("pool", mui[:], I["c_mui"], W="mui")
    blk64 = sb("blk64", [128, 128], F32); ph.dma("sp", blk64[:], I["c_blk64"], W="blk64")
    w2a2 = sb("w2a2", [128, 512], BF16); g2b = sb("g2b", [128, 512], BF16)
    ph.dma("pool", w2a2[0:64, :], I["w2"], W="w2a2"); ph.dma("pool", w2a2[64:128, :], I["a2"], W="w2a2")
    ph.dma("pool", g2b[:], I["g2"], W="g2b")
    pc = {}
    for nm, n in (("mu_shift", 14), ("w0", 4), ("a0", 4), ("k_k", 4), ("k_a", 4), ("r_k", 4), ("lnx_g", 4),
                  ("lnx_b", 4), ("D_skip", 4)):
        pc[nm] = sb("c_" + nm, [128, n], F32)
        load_col(ph, pc[nm][:], I[nm], n, "c_" + nm)
    PK = ["c_" + k for k in pc]
    scm = sb("scm", [128, 4, 128], F32)
    ph.memset(V, scm[:].rearrange("p a b -> p (a b)"), 1.0, W="scm"); ph.memset(V, scm[:, :, 0:1], 0.0, W="scm")
    eps_gn = sb("eps_gn", [128, 1], F32); ph.memset(V, eps_gn[:], 64e-5, W="eps_gn")
    if prompt:
        Sst = sb("Sst", [128, 4, 64], F32); Sbd = sb("Sbd", [128, 4, 128], BF16)
        ph.memset(V, Sst[:].rearrange("p a b -> p (a b)"), 0.0, W="Sst")
        ph.memset(V, Sbd[:].rearrange("p a b -> p (a b)"), 0.0, W="Sbd")
        Xs = sb("Xs", [128, 2, 16, 65], F32)
        ph.memset(V, Xs[:].rearrange("p a b c -> p (a b c)"), 0.0, W="Xs")
        Pf = sb("Pf", [128, 14, 513], F32)
        ph.memset(V, Pf[:, :, 0:1], 0.0, W="Pf")
    WB = 512 if prompt else NS
    WC = 128 if prompt else NS
    uf = sb("uf", [128, 4, WB], F32); ub = sb("ub", [128, 4, WB], BF16)
    YFb = sb("YFb", [128, 4, WB], BF16); ZZb = sb("ZZb", [128, 4, WB], BF16)
    f4 = lambda n: sb(n, [128, 4, WC], F32)
    b4 = lambda n: sb(n, [128, 4, WC], BF16)
    XS = sb("XS", [128, 14, WC], F32); dd = sb("dd", [128, 14, WC], F32)
    lin = sb("lin", [128, WC], BF16); sgx = sb("sgx", [128, WC], BF16)
    sig = f4("sig"); aa = f4("aa"); gg = f4("gg"); kk0 = f4("kk0"); tq = f4("tq"); rn = f4("rn"); kkn = f4("kkn")
    bb = f4("bb"); kmod = f4("kmod"); bon = f4("bon"); cs = f4("cs"); ex1 = f4("ex1"); ex2 = f4("ex2"); ex3 = f4("ex3")
    nbias = sb("nbias", [128, 4], F32); PCt = sb("PCt", [128, 4], F32)
    gns = f4("gns")
    KX = {n_: n_ for n_ in ("rT", "kT", "bT", "aT", "khT", "bhT", "vT", "PCt", "bon", "gg")}
    if prompt:
        rT = b4("rT"); kT = b4("kT"); bT = b4("bT"); aT = b4("aT"); khT = b4("khT"); bhT = b4("bhT"); vT = b4("vT")
        alt = {"rT": b4("rT1"), "kT": b4("kT1"), "bT": b4("bT1"), "aT": b4("aT1"), "khT": b4("khT1"),
               "bhT": b4("bhT1"), "vT": b4("vT1"), "PCt": sb("PCt1", [128, 4], F32), "bon": f4("bon1"), "gg": f4("gg1")}
        Vtok = sb("Vtok", [128, 512], BF16); Khtok = sb("Khtok", [128, 512], BF16); Bhtok = sb("Bhtok", [128, 512], BF16)
        h8 = lambda n: sb(n, [128, 8, 128], BF16)
        Nb = [h8("Nb0"), h8("Nb1")]; Lb = [h8("Lb0"), h8("Lb1")]; Mt = [h8("Mt0"), h8("Mt1")]
        LKb = h8("LKb"); Arb = h8("Arb"); Ark = h8("Ark")
        Wbf = sb("Wbf", [128, 512], BF16); Ubf = sb("Ubf", [128, 512], BF16)
        tS = sb("tS", [128, 4, 64], F32)
    Ysb = sb("Ysb", [128, 8, 64], F32); Ysq = sb("Ysq", [128, 8, 64], F32); ynb = sb("ynb", [128, 8, 64], BF16)
    gn = sb("gn", [128, 6, 8], F32)
    pF = [ps("pF%d" % i, [128, 4, 128], F32) for i in range(6)]
    pT = [ps("pTb%d" % i, [128, 8, 128], BF16) for i in range(2)]
    cnt = {"f": 0, "t": 0}

    def getF():
        i = cnt["f"] % 6; cnt["f"] += 1
        return pF[i], "pF%d" % i

    def mkpool(base):
        st_ = {"n": 0}

        def get():
            i = base + st_["n"] % 2; st_["n"] += 1
            return pF[i], "pF%d" % i
        return get
    getF_prep, getF_core, getFs = mkpool(0), mkpool(2), mkpool(4)

    def getT():
        i = cnt["t"] % 2; cnt["t"] += 1
        return pT[i], "pTb%d" % i

    ib = G0["identb"]

    if not prompt:
        sample_mixer(ph, I, G0, locals())
        ph.finish()
        return
    Lbase = dict(locals())
    Lpar = [dict(Lbase), dict(Lbase)]
    Lpar[1].update(alt)
    Lpar[1]["KX"] = {n_: n_ + "1" for n_ in KX}
    REC = []
    for bi, (t0, nt) in enumerate(BLOCKS[:4]):
        ph.rec_begin()
        if bi > 0:
            ph.cp(V, Pf[:, :, 0:1], Pf[:, :, 512:513], R="Pf", W="Pf")
        ph.dma("sp", Pf[:, :, 1:513], I["PRW"][:, t0:t0 + nt].rearrange("(m p) t -> p m t", p=128), W="Pf")
        if bi == 3:
            ph.dma("sp", I["p_shift"].rearrange("(m p) -> p m", p=128), Pf[:, :, 512], R="Pf", slow=True)
        hdr_pf = ph.rec_end()
        ph.rec_begin()
        ph.dma("act", uf[:], I["UU"][:, t0:t0 + nt].rearrange("(m p) t -> p m t", p=128), W="uf")
        ph.cp("act", ub[:].rearrange("p a b -> p (a b)"), uf[:].rearrange("p a b -> p (a b)"), R="uf", W="ub")
        hdr_ub = ph.rec_end()
        ph.rec_begin()
        s5_block(ph, I, G0, pc, Xs, ub, ZZb, getFs, nchunk=64, which=0, ncol=512)
        ph.dma("act", I["ZZ"][:, t0:t0 + nt].rearrange("(m p) t -> p m t", p=128), ZZb[:], R="ZZb")
        s5s = ph.rec_end()
        m0, m1 = ph._s5marks
        preps, cores = [], []
        for c in range(4):
            c0 = c * 128
            Lc = dict(Lpar[c % 2]); Lc["getF"] = getF_prep
            Lk = dict(Lpar[c % 2]); Lk["getF"] = getF_core
            ph.rec_begin()
            ph.tt(V, dd[:], Pf[:, :, c0:c0 + 128], Pf[:, :, c0 + 1:c0 + 129], ALU.subtract, R="Pf", W="dd")
            ph.tt(V, dd[:], dd[:], bc(pc["mu_shift"][:, :].unsqueeze(2), [128, 14, 128]), ALU.mult,
                  R=["dd", "c_mu_shift"], W="dd")
            ph.tt(V, XS[:], dd[:], Pf[:, :, c0 + 1:c0 + 129], ALU.add, R=["dd", "Pf"], W="XS")
            rwkv_prep_and_core(ph, Lc, c, c0)
            preps.append(ph.rec_end())
            ph.rec_begin()
            wkv_core(ph, Lk, c, c0)
            cores.append(ph.rec_end())
        ph.rec_begin()
        ph.dma("pool", I["YF"][:, t0:t0 + nt].rearrange("(m p) t -> p m t", p=128), YFb[:], R="YFb")
        yfst = ph.rec_end()
        hs = (m1 - m0) // 2
        REC.append(dict(hdr_pf=hdr_pf, hdr_ub=hdr_ub, SG=s5s[:m0], SS1=s5s[m0:m0 + hs], SS2=s5s[m0 + hs:m1],
                        SY=s5s[m1:], preps=preps, cores=cores, yfst=yfst))
    ph.play(REC[0]["hdr_pf"])
    ph.play(REC[0]["preps"][0])
    for bi in range(4):
        Rb = REC[bi]
        ph.play(Rb["hdr_ub"])
        ph.play(Rb["cores"][0], Rb["preps"][1], Rb["SG"])
        ph.play(Rb["cores"][1], Rb["preps"][2], Rb["SS1"])
        ph.play(Rb["cores"][2], Rb["preps"][3], Rb["SS2"])
        if bi < 3:
            ph.play(REC[bi + 1]["hdr_pf"])
            ph.play(Rb["cores"][3], Rb["SY"], REC[bi + 1]["preps"][0], spans=[(0.0, 1.0), (SYO, 1.0 - SYO), (0.0, 1.0)])
        else:
            ph.play(Rb["cores"][3], Rb["SY"], spans=[(0.0, 1.0), (SYO, 1.0 - SYO)])
        ph.play(Rb["yfst"])
    ph.dma("sp", I["p_wkv"].rearrange("(m p) v -> p m v", p=128), Sst[:], R="Sst")
    ph.dma("sp", I["p_re"].rearrange("(P p) -> p P", p=128), Xs[:, 0, :, 0], R="Xs", slow=True)
    ph.dma("sp", I["p_im"].rearrange("(P p) -> p P", p=128), Xs[:, 1, :, 0], R="Xs", slow=True)
    ph.finish()


def rwkv_prep_and_core(ph, L, c, c0):
    V = "dve"
    PV = L.get("PV", "dve")
    KX = L["KX"]
    pc = L["pc"]; XS = L["XS"]; getF = L["getF"]; getT = L["getT"]; ib = L["ib"]
    sig, aa, gg, kk0, tq, rn, kkn = L["sig"], L["aa"], L["gg"], L["kk0"], L["tq"], L["rn"], L["kkn"]
    bb, kmod, bon, cs, ex1, ex2, ex3 = L["bb"], L["kmod"], L["bon"], L["cs"], L["ex1"], L["ex2"], L["ex3"]
    rT, kT, bT, aT, khT, bhT, vT = L["rT"], L["kT"], L["bT"], L["aT"], L["khT"], L["bhT"], L["vT"]
    lin, sgx, w2a2, g2b, blk64 = L["lin"], L["sgx"], L["w2a2"], L["g2b"], L["blk64"]
    nbias, PCt, scm = L["nbias"], L["PCt"], L["scm"]
    r_ = XS[:, 0:4, :]; k_ = XS[:, 4:8, :]; v_ = XS[:, 8:12, :]
    B4 = lambda t: bc(t[:, :].unsqueeze(2), [128, 4, 128])
    fl = lambda t: t[:].rearrange("p a b -> p (a b)")
    ph.act(lin[0:64, :], XS[0:64, 12, :], AF.Tanh, R="XS", W="lin")
    ph.cp("act", lin[64:128, :], XS[64:128, 12, :], R="XS", W="lin")
    ph.act(sgx[:], XS[:, 13, :], AF.Sigmoid, R="XS", W="sgx")
    pw_, kw_ = getF()
    for m in range(4):
        ph.mm(pw_[:, m, :], w2a2[0:64, m * 128:(m + 1) * 128], lin[0:64, :], True, True, R=["w2a2", "lin"], W=kw_)
    for m in range(4):
        ph.act(sig[:, m, :], pw_[:, m, :], AF.Sigmoid, R=[kw_, "c_w0"], W="sig", bias=pc["w0"][:, m:m + 1])
    pa_, ka_ = getF()
    for m in range(4):
        ph.mm(pa_[:, m, :], w2a2[64:128, m * 128:(m + 1) * 128], lin[64:128, :], True, True, R=["w2a2", "lin"], W=ka_)
    for m in range(4):
        ph.act(aa[:, m, :], pa_[:, m, :], AF.Sigmoid, R=[ka_, "c_a0"], W="aa", bias=pc["a0"][:, m:m + 1])
    pg_, kg_ = getF()
    for m in range(4):
        ph.mm(pg_[:, m, :], g2b[:, m * 128:(m + 1) * 128], sgx[:], True, True, R=["g2b", "sgx"], W=kg_)
    ph.cp("act", gg[:], pg_[:], R=kg_, W=KX["gg"])
    ph.tt(PV, kk0[:], k_, B4(pc["k_k"]), ALU.mult, R=["XS", "c_k_k"], W="kk0")
    ph.tt(PV, tq[:], kk0[:], kk0[:], ALU.mult, R="kk0", W="tq")
    pq, kq = getF()
    for m in range(4):
        ph.mm(pq[:, m, :], blk64[:], tq[:, m, :], True, True, R=["blk64", "tq"], W=kq)
    ph.act(rn[:], pq[:], AF.Sqrt, R=kq, W="rn")
    ph.ts(V, rn[:], rn[:], 1e-12, ALU.max, R="rn", W="rn")
    ph.op(V, lambda e: e.reciprocal(out=fl(rn), in_=fl(rn)), R="rn", W="rn")
    ph.tt(PV, kkn[:], kk0[:], rn[:], ALU.mult, R=["kk0", "rn"], W="kkn")
    ph.tt(PV, bb[:], kkn[:], aa[:], ALU.mult, R=["kkn", "aa"], W="bb")
    ph.tt(PV, tq[:], aa[:], B4(pc["k_a"]), ALU.mult, R=["aa", "c_k_a", kq], W="tq")
    ph.tt(PV, tq[:], tq[:], B4(pc["k_a"]), ALU.subtract, R=["tq", "c_k_a"], W="tq")
    ph.stt(kmod[:], tq[:], 1.0, k_, ALU.add, ALU.mult, R=["tq", "XS"], W="kmod")
    ph.tt(PV, tq[:], r_, kmod[:], ALU.mult, R=["XS", "kmod"], W="tq")
    ph.tt(PV, tq[:], tq[:], B4(pc["r_k"]), ALU.mult, R=["tq", "c_r_k"], W="tq")
    pq2, kq2 = getF()
    for m in range(4):
        ph.mm(pq2[:, m, :], blk64[:], tq[:, m, :], True, True, R=["blk64", "tq"], W=kq2)
    ph.tt(V, bon[:], pq2[:], v_, ALU.mult, R=[kq2, "XS"], W=KX["bon"])
    ph.op(V, lambda e: e.tensor_tensor_scan(out=fl(cs), data0=fl(scm), data1=fl(sig), initial=0.0, op0=ALU.mult,
                                             op1=ALU.add), R=["scm", "sig"], W="cs")
    ph.ts(V, nbias[:], cs[:, :, 127], -C1, ALU.mult, R="cs", W="nbias")
    ph.act(PCt[:], nbias[:], AF.Exp, R="nbias", W=KX["PCt"])
    ph.act(ex1[:], cs[:], AF.Exp, R="cs", W="ex1", scale=-C1)
    ph.tt(PV, rT[:], r_, ex1[:], ALU.mult, R=["XS", "ex1"], W=KX["rT"])
    ph.act(ex2[:], cs[:], AF.Exp, R="cs", W="ex2", scale=C1)
    ph.tt(PV, kT[:], kmod[:], ex2[:], ALU.mult, R=["kmod", "ex2"], W=KX["kT"])
    ph.tt(PV, bT[:], bb[:], ex2[:], ALU.mult, R=["bb", "ex2"], W=KX["bT"])
    ph.tt(PV, ex3[:], cs[:], sig[:], ALU.subtract, R=["cs", "sig"], W="ex3")
    ph.act(ex3[:], ex3[:], AF.Exp, R="ex3", W="ex3", scale=-C1)
    ph.stt(aT[:], kkn[:], -1.0, ex3[:], ALU.mult, ALU.mult, R=["kkn", "ex3"], W=KX["aT"])
    for m in range(4):
        ph.act(ex1[:, m, :], cs[:, m, :], AF.Exp, R=["cs", "nbias", KX["rT"]], W="ex1", bias=nbias[:, m:m + 1], scale=C1)
    ph.tt(PV, khT[:], kmod[:], ex1[:], ALU.mult, R=["kmod", "ex1"], W=KX["khT"])
    ph.tt(PV, bhT[:], bb[:], ex1[:], ALU.mult, R=["bb", "ex1"], W=KX["bhT"])
    ph.cp("act", vT[:], v_, R="XS", W=KX["vT"])


def wkv_core(ph, L, c, c0):
    V = "dve"
    KX = L["KX"]
    getF = L["getF"]; getT = L["getT"]; ib = L["ib"]
    rT, kT, bT, aT, khT, bhT, vT = L["rT"], L["kT"], L["bT"], L["aT"], L["khT"], L["bhT"], L["vT"]
    Vtok, Khtok, Bhtok = L["Vtok"], L["Khtok"], L["Bhtok"]
    Nb, Lb, Mt, LKb, Arb, Ark = L["Nb"], L["Lb"], L["Mt"], L["LKb"], L["Arb"], L["Ark"]
    msl, msu, mui = L["msl"], L["msu"], L["mui"]
    Wbf, Ubf, Ysb, Ysq, ynb, gn = L["Wbf"], L["Ubf"], L["Ysb"], L["Ysq"], L["ynb"], L["gn"]
    Sst, Sbd, PCt, tS = L["Sst"], L["Sbd"], L["PCt"], L["tS"]
    pc = L["pc"]; bon, gg, YFb = L["bon"], L["gg"], L["YFb"]
    M4 = lambda m_: bc(m_[:, :].unsqueeze(1), [128, 4, 128])
    pt, kt = getT()
    for m in range(4):
        ph.tr(pt[:, m, :], vT[:, m, :], ib[:], R=KX["vT"], W=kt)
    for m in range(4):
        ph.tr(pt[:, 4 + m, :], khT[:, m, :], ib[:], R=KX["khT"], W=kt)
    ph.cp("act", Vtok[:], pt[:, 0:4, :].rearrange("p a b -> p (a b)"), R=kt, W="Vtok")
    ph.cp(V, Khtok[:], pt[:, 4:8, :].rearrange("p a b -> p (a b)"), R=kt, W="Khtok")
    pt2, kt2 = getT()
    for m in range(4):
        ph.tr(pt2[:, m, :], bhT[:, m, :], ib[:], R=KX["bhT"], W=kt2)
    ph.cp("act", Bhtok[:], pt2[:, 0:4, :].rearrange("p a b -> p (a b)"), R=kt2, W="Bhtok")

    def hsl(t, h):
        return t[64 * (h % 2):64 * (h % 2) + 64, h // 2, :]

    def amat(dst, dkey, lhs, lkey, rhs, rkey, mask, mkey):
        for par in range(2):
            pb, pk = getF()
            for q in range(4):
                h = 2 * q + par
                ph.mm(pb[:, q, :], hsl(lhs, h), hsl(rhs, h), True, True, R=[lkey, rkey], W=pk)
            ph.tt(V, dst[:, par:8:2, :], pb[:], M4(mask), ALU.mult, R=[pk, mkey], W=dkey)

    amat(Nb[0], "Nb0", aT, KX["aT"], bT, KX["bT"], msl, "msl")
    amat(Lb[0], "Lb0", bT, KX["bT"], aT, KX["aT"], msu, "msu")
    amat(LKb, "LKb", kT, KX["kT"], aT, KX["aT"], msu, "msu")
    amat(Arb, "Arb", bT, KX["bT"], rT, KX["rT"], mui, "mui")
    amat(Ark, "Ark", kT, KX["kT"], rT, KX["rT"], mui, "mui")
    for half in range(2):
        ph.tt(V, Mt[0][:, half * 4:half * 4 + 4, :], Lb[0][:, half * 4:half * 4 + 4, :], M4(ib), ALU.add,
              R=["Lb0", "identb"], W="Mt0")
    cur = 0
    for lvl in range(6):
        nxt = 1 - cur
        for half in range(2):
            pb, pk = getF()
            for q in range(4):
                h = half * 4 + q
                ph.mm(pb[:, q, :], Lb[cur][:, h, :], Nb[cur][:, h, :], True, True, R=["Lb%d" % cur, "Nb%d" % cur], W=pk)
            ph.cp("act", Nb[nxt][:, half * 4:half * 4 + 4, :], pb[:], R=pk, W="Nb%d" % nxt)
        if BUB2:
            ph.bubble(BUB2)
        if lvl < 5:
            for half in range(2):
                pb, pk = getF()
                for q in range(4):
                    h = half * 4 + q
                    ph.mm(pb[:, q, :], Nb[cur][:, h, :], Lb[cur][:, h, :], True, True,
                          R=["Lb%d" % cur, "Nb%d" % cur], W=pk)
                ph.cp("act", Lb[nxt][:, half * 4:half * 4 + 4, :], pb[:], R=pk, W="Lb%d" % nxt)
        for half in range(2):
            pb, pk = getF()
            for q in range(4):
                h = half * 4 + q
                ph.mm(pb[:, q, :], Nb[nxt][:, h, :], Mt[cur][:, h, :], True, True, R=["Nb%d" % nxt, "Mt%d" % cur], W=pk)
            ph.tt(V, Mt[nxt][:, half * 4:half * 4 + 4, :], pb[:], Mt[cur][:, half * 4:half * 4 + 4, :], ALU.add,
                  R=[pk, "Mt%d" % cur], W="Mt%d" % nxt)
        cur = nxt
    MtF = Mt[cur]; mk = "Mt%d" % cur
    def hcols(pb, h):
        return pb[:].rearrange("p a b -> p (a b)")[:, h * 64:h * 64 + 64]

    def pcols(pb, m):
        return pb[:].rearrange("p a b -> p (a b)")[:, m * 128:m * 128 + 128]

    pb, pk = getF()
    for m in range(4):
        ph.mm(pcols(pb, m), aT[:, m, :], Sbd[:, m, :], True, False, R=[KX["aT"], "Sbd"], W=pk)
        for hh in range(2):
            h = 2 * m + hh
            ph.mm(hcols(pb, h), LKb[:, h, :], Vtok[:, h * 64:h * 64 + 64], False, hh == 1, R=["LKb", "Vtok"], W=pk)
    ph.cp("act", Wbf[:], pb[:].rearrange("p a b -> p (a b)"), R=pk, W="Wbf")
    ph.bubble(BUB)
    pb, pk = getF()
    for h in range(8):
        ph.mm(hcols(pb, h), MtF[:, h, :], Wbf[:, h * 64:h * 64 + 64], True, True, R=[mk, "Wbf"], W=pk)
    ph.cp("act", Ubf[:], pb[:].rearrange("p a b -> p (a b)"), R=pk, W="Ubf")
    ph.bubble(BUB)
    pb, pk = getF()
    for m in range(4):
        ph.mm(pcols(pb, m), rT[:, m, :], Sbd[:, m, :], True, False, R=[KX["rT"], "Sbd"], W=pk)
        for hh in range(2):
            h = 2 * m + hh
            ph.mm(hcols(pb, h), Arb[:, h, :], Ubf[:, h * 64:h * 64 + 64], False, False, R=["Arb", "Ubf"], W=pk)
            ph.mm(hcols(pb, h), Ark[:, h, :], Vtok[:, h * 64:h * 64 + 64], False, hh == 1, R=["Ark", "Vtok"], W=pk)
    ph.cp("act", Ysb[:].rearrange("p a b -> p (a b)"), pb[:].rearrange("p a b -> p (a b)"), R=pk, W="Ysb")
    pS, kS = getF()
    for m in range(4):
        ph.mm(pS[:, m, :], Bhtok[:, m * 128:(m + 1) * 128], Ubf[:, m * 128:(m + 1) * 128], True, False,
              R=["Bhtok", "Ubf"], W=kS)
        ph.mm(pS[:, m, :], Khtok[:, m * 128:(m + 1) * 128], Vtok[:, m * 128:(m + 1) * 128], False, True,
              R=["Khtok", "Vtok"], W=kS)
    ph.tt(V, tS[:], Sst[:], bc(PCt[:, :].unsqueeze(2), [128, 4, 64]), ALU.mult, R=["Sst", KX["PCt"]], W="tS")
    for hh in range(2):
        rs = slice(64 * hh, 64 * hh + 64)
        ph.tt(V, Sst[rs, :, :], tS[rs, :, :], pS[rs, :, 64 * hh:64 * hh + 64], ALU.add, R=["tS", kS], W="Sst")
        ph.cp(V, Sbd[rs, :, 64 * hh:64 * hh + 64], Sst[rs, :, :], R="Sst", W="Sbd")
    groupnorm_out(ph, L, c0, 128)


def groupnorm_out(ph, L, c0, P):
    V = "dve"
    KX = L["KX"]
    Ysb, Ysq, ynb, gn = L["Ysb"], L["Ysq"], L["ynb"], L["gn"]
    pc = L["pc"]; bon, gg, YFb = L["bon"], L["gg"], L["YFb"]; getT = L["getT"]; ib = L["ib"]
    eps_gn = L["eps_gn"]; ex2 = L["gns"]
    ph.op(V, lambda e: e.tensor_reduce(out=gn[:P, 0, :], in_=Ysb[:P], axis=AX.X, op=ALU.add), R="Ysb", W="gn")
    ph.act(Ysq[:P].rearrange("p a b -> p (a b)"), Ysb[:P].rearrange("p a b -> p (a b)"), AF.Square, R="Ysb", W="Ysq")
    ph.op(V, lambda e: e.tensor_reduce(out=gn[:P, 1, :], in_=Ysq[:P], axis=AX.X, op=ALU.add), R="Ysq", W="gn")
    ph.ts(V, gn[:P, 2, :], gn[:P, 0, :], 1.0 / 64, ALU.mult, R="gn", W="gn")
    ph.tt(V, gn[:P, 3, :], gn[:P, 2, :], gn[:P, 2, :], ALU.mult, R="gn", W="gn")
    ph.stt(gn[:P, 4, :], gn[:P, 1, :], 1.0 / 64, gn[:P, 3, :], ALU.mult, ALU.subtract, R="gn", W="gn")
    ph.act(gn[:P, 4, :], gn[:P, 4, :], AF.Sqrt, R=["gn", "eps_gn"], W="gn", bias=eps_gn[:P, 0:1])
    ph.op(V, lambda e: e.reciprocal(out=gn[:P, 5, :], in_=gn[:P, 4, :]), R="gn", W="gn")
    ph.tt(V, Ysq[:P], Ysb[:P], bc(gn[:P, 2, :].unsqueeze(2), [P, 8, 64]), ALU.subtract, R=["Ysb", "gn"], W="Ysq")
    ph.tt(V, ynb[:P], Ysq[:P], bc(gn[:P, 5, :].unsqueeze(2), [P, 8, 64]), ALU.mult, R=["Ysq", "gn"], W="ynb")
    pt, kt = getT()
    for m in range(4):
        ph.tr(pt[:, m, :P], ynb[:P, 2 * m:2 * m + 2, :].rearrange("p a b -> p (a b)"), ib[:P, :P], R="ynb", W=kt)
    B4 = lambda t: bc(t[:, :].unsqueeze(2), [128, 4, P])
    t1 = ex2
    ph.tt(V, t1[:, :, :P], pt[:, 0:4, :P], B4(pc["lnx_g"]), ALU.mult, R=[kt, "c_lnx_g"], W="gns")
    ph.tt(V, t1[:, :, :P], t1[:, :, :P], B4(pc["lnx_b"]), ALU.add, R=["gns", "c_lnx_b"], W="gns")
    ph.tt(V, t1[:, :, :P], t1[:, :, :P], bon[:, :, :P], ALU.add, R=["gns", KX["bon"]], W="gns")
    ph.tt(V, YFb[:, :, c0:c0 + P], t1[:, :, :P], gg[:, :, :P], ALU.mult, R=["gns", KX["gg"]], W="YFb")


def s5_block(ph, I, G0, pc, Xs, ub, ZZb, getF, nchunk, which, ncol, step=CS, npos=CS):
    V = "dve"
    BwT, Kmat, CwT, Ab = G0["BwT"], G0["Kmat"], G0["CwT"], G0["Abar"]
    nm = nchunk
    assert nm * 8 <= 512
    for Pl in range(4):
        pb, pk = getF()
        flat = pb[:].rearrange("p a b -> p (a b)")
        for ri in range(2):
            for k in range(4):
                q = ri * 4 + k
                dst = flat[:, q * nm:(q + 1) * nm]
                for j in range(npos):
                    jj = (CS - npos) + j
                    rhs = ub[32 * Pl:32 * Pl + 32, k, j:j + (nm - 1) * step + 1:step]
                    ph.mm(dst, BwT[32 * Pl:32 * Pl + 32, k, jj, ri, :], rhs, j == 0, j == npos - 1,
                          R=["BwT", "ub"], W=pk, tp=((96, 0) if Pl == 3 else None))
        for ri in range(2):
            ph.cp(V, Xs[:, ri, Pl:16:4, 1:1 + nm],
                  flat[:, ri * 4 * nm:(ri + 1) * 4 * nm].rearrange("p (q m) -> p q m", m=nm), R=[pk], W="Xs")
    A_r = bc(Ab[:, which, 0, :].unsqueeze(1), [128, 2, 16]); A_i = bc(Ab[:, which, 1, :].unsqueeze(1), [128, 2, 16])
    ph._s5marks = [len(ph._rec) if ph._rec is not None else 0]
    tmpa = ph._s5tmp[0]; tmpb = ph._s5tmp[1]
    for m in range(nm):
        ph.tt(SCAN_ENG, tmpa[:], Xs[:, :, :, m], A_r, ALU.mult, R=["Xs", "Abar"], W="s5a")
        ph.tt(SCAN_ENG, tmpb[:], Xs[:, :, :, m], A_i, ALU.mult, R=["Xs", "Abar"], W="s5b")
        ph.tt(SCAN_ENG, Xs[:, :, :, m + 1], Xs[:, :, :, m + 1], tmpa[:], ALU.add, R=["Xs", "s5a"], W="Xs")
        ph.tt(SCAN_ENG, Xs[:, 0, :, m + 1], Xs[:, 0, :, m + 1], tmpb[:, 1, :], ALU.subtract, R=["Xs", "s5b"], W="Xs")
        ph.tt(SCAN_ENG, Xs[:, 1, :, m + 1], Xs[:, 1, :, m + 1], tmpb[:, 0, :], ALU.add, R=["Xs", "s5b"], W="Xs")
    ph._s5marks.append(len(ph._rec) if ph._rec is not None else 0)
    Xb = ph._s5xb
    ph.cp("act", Xb[:, :, :, 0:nm], Xs[:, :, :, 0:nm], R="Xs", W="Xb")
    for k in range(4):
        pb, pk = getF()
        flat = pb[:].rearrange("p a b -> p (a b)")
        for i in range(npos):
            dst = flat[:, i * nm:(i + 1) * nm]
            for tau in range(i + 1):
                rhs = ub[:, k, (i - tau):(i - tau) + (nm - 1) * step + 1:step]
                ph.mm(dst, Kmat[:, k, tau, :], rhs, tau == 0, False, R=["Kmat", "ub"], W=pk)
            for Pl in range(4):
                P_ = 4 * k + Pl
                for ri in range(2):
                    ph.mm(flat[32 * Pl:32 * Pl + 32, i * nm:(i + 1) * nm], CwT[:, i, ri, P_, :], Xb[:, ri, P_, 0:nm],
                          False, ri == 1, R=["CwT", "Xb"], W=pk, tp=(0, 32 * Pl))
        du = ph._s5du
        ph.ts(V, du[:, 0:ncol], ub[:, k, 0:ncol], pc["D_skip"][:, k:k + 1], ALU.mult, R=["ub", "c_D_skip", "s5z"], W="s5du")
        if npos == 1:
            ph.tt(V, du[:, 0:ncol], du[:, 0:ncol], flat[:, 0:nm], ALU.add, R=["s5du", pk], W="s5du")
        else:
            ph.tt(V, du[:, 0:ncol].rearrange("p (m i) -> p m i", i=npos), du[:, 0:ncol].rearrange("p (m i) -> p m i", i=npos),
                  flat[:, 0:npos * nm].rearrange("p (i m) -> p m i", m=nm), ALU.add, R=["s5du", pk], W="s5du")
        ph.act(ZZb[:, k, 0:ncol], du[:, 0:ncol], AF.Gelu_apprx_tanh, R="s5du", W=["ZZb", "s5z"])
    ph.cp(V, Xs[:, :, :, 0], Xs[:, :, :, nm], R="Xs", W="Xs")


def sample_mixer(ph, I, G0, L):
    V = "dve"
    sb = ph.sb
    pc = L["pc"]; getF, getT, ib = L["getF"], L["getT"], L["ib"]
    identf = G0["identf"]
    XS = L["XS"]; dd = L["dd"]
    t0 = T
    n = NS
    cur = sb("s_cur", [128, 14, NS], F32); prv = sb("s_prv", [128, 14, NS], F32)
    ph.dma("sp", cur[:], I["PRW"][:, t0:t0 + n].rearrange("(m p) t -> p m t", p=128), W="s_cur")
    sst = sb("s_sst", [NS, 1792], F32)
    ph.dma("sp", sst[:], I["st_shift"], W="s_sst")
    for half in range(4):
        pb, pk = getF()
        flat = pb[:].rearrange("p a b -> p (a b)")
        ms = list(range(half * 4, min(14, half * 4 + 4)))
        for q, m in enumerate(ms):
            ph.tr(flat[:, q * NS:(q + 1) * NS], sst[:, m * 128:(m + 1) * 128], identf[:NS, :NS], R=["s_sst"], W=pk)
        ph.cp(V, prv[:, ms[0]:ms[-1] + 1, :], flat[:, 0:len(ms) * NS].rearrange("p (a b) -> p a b", b=NS), R=pk, W="s_prv")
    ph.dbg("cur", cur[:], [128, 14, NS], "s_cur")
    ph.dbg("prv", prv[:], [128, 14, NS], "s_prv")
    so = sst
    for half in range(4):
        pb, pk = getF()
        flat = pb[:].rearrange("p a b -> p (a b)")
        ms = list(range(half * 4, min(14, half * 4 + 4)))
        for q, m in enumerate(ms):
            ph.tr(flat[:NS, q * 128:(q + 1) * 128], cur[:, m, :], identf[:], R=["s_cur"], W=pk)
        ph.cp(V, so[:, ms[0] * 128:(ms[-1] + 1) * 128], flat[:NS, 0:len(ms) * 128], R=pk, W="s_sst")
    ph.dma("sp", I["s_shift"], so[:], R="s_sst")
    xs = XS[:, :, 0:NS]
    ph.tt(V, dd[:, :, 0:NS], prv[:], cur[:], ALU.subtract, R=["s_prv", "s_cur"], W="dd")
    ph.tt(V, dd[:, :, 0:NS], dd[:, :, 0:NS], bc(pc["mu_shift"][:, :].unsqueeze(2), [128, 14, NS]), ALU.mult,
          R=["dd", "c_mu_shift"], W="dd")
    ph.tt(V, xs, dd[:, :, 0:NS], cur[:], ALU.add, R=["dd", "s_cur"], W="XS")
    uf = L["uf"]; ub = L["ub"]; ZZb = L["ZZb"]
    ph.dma("act", uf[:, :, 0:NS], I["UU"][:, t0:t0 + n].rearrange("(m p) t -> p m t", p=128), W="uf")
    ph.cp("act", ub[:, :, 0:NS], uf[:, :, 0:NS], R="uf", W="ub")
    stx = [sb("s_stre", [NS, 2048], F32), sb("s_stim", [NS, 2048], F32)]
    ph.dma("sp", stx[0][:], I["st_re"], W="s_stx0"); ph.dma("sp", stx[1][:], I["st_im"], W="s_stx1")
    Xsm = sb("s_Xsm", [128, 2, 16, NS], F32)
    for ri in range(2):
        for q4 in range(4):
            pb, pk = getF()
            flat = pb[:].rearrange("p a b -> p (a b)")
            for q in range(4):
                P_ = q4 * 4 + q
                ph.tr(flat[:, q * NS:(q + 1) * NS], stx[ri][:, P_ * 128:(P_ + 1) * 128], identf[:NS, :NS],
                      R="s_stx%d" % ri, W=pk)
            ph.cp(V, Xsm[:, ri, q4 * 4:q4 * 4 + 4, :], flat[:, 0:4 * NS].rearrange("p (a b) -> p a b", b=NS), R=pk, W="s_Xsm")
    s5_sample(ph, I, G0, pc, Xsm, ub, ZZb, getF, stx)
    ph.dma("act", I["ZZ"][:, t0:t0 + n].rearrange("(m p) t -> p m t", p=128), ZZb[:, :, 0:NS], R="ZZb")
    rwkv_sample(ph, I, G0, L)
    ph.dma("sp", I["YF"][:, t0:t0 + n].rearrange("(m p) t -> p m t", p=128), L["YFb"][:, :, 0:NS], R="YFb")


def s5_sample(ph, I, G0, pc, Xsm, ub, ZZb, getF, stx):
    V = "dve"
    BwT, Kmat, CwT, Ab = G0["BwT"], G0["Kmat"], G0["CwT"], G0["Abar"]
    identf = G0["identf"]
    Xb = ph._s5xb
    ph.cp("act", Xb[:, :, :, 0:NS], Xsm[:], R="s_Xsm", W="Xb")
    du = ph._s5du
    for k in range(4):
        pb, pk = getF()
        flat = pb[:].rearrange("p a b -> p (a b)")
        ph.mm(flat[:, 0:NS], Kmat[:, k, 0, :], ub[:, k, 0:NS], True, False, R=["Kmat", "ub"], W=pk)
        for Pl in range(4):
            P_ = 4 * k + Pl
            for ri in range(2):
                ph.mm(flat[32 * Pl:32 * Pl + 32, 0:NS], CwT[:, 0, ri, P_, :], Xb[:, ri, P_, 0:NS], False,
                      ri == 1, R=["CwT", "Xb"], W=pk, tp=(0, 32 * Pl))
        ph.ts(V, du[:, 0:NS], ub[:, k, 0:NS], pc["D_skip"][:, k:k + 1], ALU.mult, R=["ub", "c_D_skip", "s5z"], W="s5du")
        ph.tt(V, du[:, 0:NS], du[:, 0:NS], flat[:, 0:NS], ALU.add, R=["s5du", pk], W="s5du")
        ph.act(ZZb[:, k, 0:NS], du[:, 0:NS], AF.Gelu_apprx_tanh, R="s5du", W=["ZZb", "s5z"])
    Gs = ph.sb("s_Gs", [128, 2, 16, NS], F32)
    for Pl in range(4):
        pb, pk = getF()
        flat = pb[:].rearrange("p a b -> p (a b)")
        for ri in range(2):
            for k in range(4):
                q = ri * 4 + k
                ph.mm(flat[:, q * NS:(q + 1) * NS], BwT[32 * Pl:32 * Pl + 32, k, CS - 1, ri, :],
                      ub[32 * Pl:32 * Pl + 32, k, 0:NS], True, True, R=["BwT", "ub"], W=pk,
                      tp=((96, 0) if Pl == 3 else None))
        for ri in range(2):
            ph.cp(V, Gs[:, ri, Pl:16:4, :], flat[:, ri * 4 * NS:(ri + 1) * 4 * NS].rearrange("p (q m) -> p q m", m=NS),
                  R=pk, W="s_Gs")
    A_r = bc(Ab[:, 1, 0, :].unsqueeze(2), [128, 16, NS]); A_i = bc(Ab[:, 1, 1, :].unsqueeze(2), [128, 16, NS])
    ta = ph.sb("s_ta", [128, 16, NS], F32)
    ph.tt(V, ta[:], Xsm[:, 0], A_r, ALU.mult, R=["s_Xsm", "Abar"], W="s_ta")
    ph.tt(V, Gs[:, 0], Gs[:, 0], ta[:], ALU.add, R=["s_Gs", "s_ta"], W="s_Gs")
    ph.tt(V, ta[:], Xsm[:, 1], A_i, ALU.mult, R=["s_Xsm", "Abar", "s_Gs"], W="s_ta")
    ph.tt(V, Gs[:, 0], Gs[:, 0], ta[:], ALU.subtract, R=["s_Gs", "s_ta"], W="s_Gs")
    ph.tt(V, ta[:], Xsm[:, 1], A_r, ALU.mult, R=["s_Xsm", "Abar", "s_Gs"], W="s_ta")
    ph.tt(V, Gs[:, 1], Gs[:, 1], ta[:], ALU.add, R=["s_Gs", "s_ta"], W="s_Gs")
    ph.tt(V, ta[:], Xsm[:, 0], A_i, ALU.mult, R=["s_Xsm", "Abar", "s_Gs"], W="s_ta")
    ph.tt(V, Gs[:, 1], Gs[:, 1], ta[:], ALU.add, R=["s_Gs", "s_ta"], W="s_Gs")
    for ri, nm in enumerate(("s_re", "s_im")):
        xo = stx[ri]
        for q4 in range(4):
            pb, pk = getF()
            flat = pb[:].rearrange("p a b -> p (a b)")
            for q in range(4):
                P_ = q4 * 4 + q
                ph.tr(flat[:NS, q * 128:(q + 1) * 128], Gs[:, ri, P_, :], identf[:], R="s_Gs", W=pk)
            ph.cp(V, xo[:, q4 * 512:(q4 + 1) * 512], flat[:NS, 0:512], R=pk, W="s_stx%d" % ri)
        ph.dma("sp", I[nm], xo[:], R="s_stx%d" % ri)


def rwkv_sample(ph, I, G0, L):
    V = "dve"
    sb = ph.sb
    pc = L["pc"]; getF, getT, ib = L["getF"], L["getT"], L["ib"]
    identf = G0["identf"]
    XS = L["XS"]
    sig, aa, gg, kk0, tq, rn, kkn = L["sig"], L["aa"], L["gg"], L["kk0"], L["tq"], L["rn"], L["kkn"]
    bb, kmod, bon = L["bb"], L["kmod"], L["bon"]
    lin, sgx, w2a2, g2b, blk64 = L["lin"], L["sgx"], L["w2a2"], L["g2b"], L["blk64"]
    n = NS
    r_ = XS[:, 0:4, 0:n]; k_ = XS[:, 4:8, 0:n]; v_ = XS[:, 8:12, 0:n]
    B4 = lambda t: bc(t[:, :].unsqueeze(2), [128, 4, n])
    S4 = lambda t: t[:, :, 0:n]
    ph.act(lin[0:64, 0:n], XS[0:64, 12, 0:n], AF.Tanh, R="XS", W="lin")
    ph.cp("act", lin[64:128, 0:n], XS[64:128, 12, 0:n], R="XS", W="lin")
    ph.act(sgx[:, 0:n], XS[:, 13, 0:n], AF.Sigmoid, R="XS", W="sgx")
    pw_, kw_ = getF(); pa_, ka_ = getF(); pg_, kg_ = getF()
    for m in range(4):
        ph.mm(pw_[:, m, 0:n], w2a2[0:64, m * 128:(m + 1) * 128], lin[0:64, 0:n], True, True, R=["w2a2", "lin"], W=kw_)
        ph.mm(pa_[:, m, 0:n], w2a2[64:128, m * 128:(m + 1) * 128], lin[64:128, 0:n], True, True, R=["w2a2", "lin"], W=ka_)
        ph.mm(pg_[:, m, 0:n], g2b[:, m * 128:(m + 1) * 128], sgx[:, 0:n], True, True, R=["g2b", "sgx"], W=kg_)
    for m in range(4):
        ph.act(sig[:, m, 0:n], pw_[:, m, 0:n], AF.Sigmoid, R=[kw_, "c_w0"], W="sig", bias=pc["w0"][:, m:m + 1])
        ph.act(aa[:, m, 0:n], pa_[:, m, 0:n], AF.Sigmoid, R=[ka_, "c_a0"], W="aa", bias=pc["a0"][:, m:m + 1])
    ph.cp("act", S4(gg), pg_[:, :, 0:n], R=kg_, W="gg")
    ph.tt(V, S4(kk0), k_, B4(pc["k_k"]), ALU.mult, R=["XS", "c_k_k"], W="kk0")
    ph.tt(V, S4(tq), S4(kk0), S4(kk0), ALU.mult, R="kk0", W="tq")
    pq, kq = getF()
    for m in range(4):
        ph.mm(pq[:, m, 0:n], blk64[:], tq[:, m, 0:n], True, True, R=["blk64", "tq"], W=kq)
    ph.act(S4(rn), pq[:, :, 0:n], AF.Sqrt, R=kq, W="rn")
    ph.ts(V, S4(rn), S4(rn), 1e-12, ALU.max, R="rn", W="rn")
    ph.op(V, lambda e: e.reciprocal(out=S4(rn), in_=S4(rn)), R="rn", W="rn")
    ph.tt(V, S4(kkn), S4(kk0), S4(rn), ALU.mult, R=["kk0", "rn"], W="kkn")
    ph.tt(V, S4(bb), S4(kkn), S4(aa), ALU.mult, R=["kkn", "aa"], W="bb")
    ph.tt(V, S4(tq), S4(aa), B4(pc["k_a"]), ALU.mult, R=["aa", "c_k_a", kq], W="tq")
    ph.tt(V, S4(tq), S4(tq), B4(pc["k_a"]), ALU.subtract, R=["tq", "c_k_a"], W="tq")
    ph.stt(S4(kmod), S4(tq), 1.0, k_, ALU.add, ALU.mult, R=["tq", "XS"], W="kmod")
    ph.tt(V, S4(tq), r_, S4(kmod), ALU.mult, R=["XS", "kmod"], W="tq")
    ph.tt(V, S4(tq), S4(tq), B4(pc["r_k"]), ALU.mult, R=["tq", "c_r_k"], W="tq")
    pq2, kq2 = getF()
    for m in range(4):
        ph.mm(pq2[:, m, 0:n], blk64[:], tq[:, m, 0:n], True, True, R=["blk64", "tq"], W=kq2)
    ph.tt(V, S4(bon), pq2[:, :, 0:n], v_, ALU.mult, R=[kq2, "XS"], W="bon")
    wdec = L["ex1"]
    ph.act(S4(wdec), S4(sig), AF.Exp, R="sig", W="ex1", scale=-C1)
    srcs = [r_, S4(wdec), S4(kmod), v_, S4(kkn), S4(bb)]
    keys = ["XS", "ex1", "kmod", "XS", "kkn", "bb"]
    tok = sb("s_tok", [NS, 6, 512], F32)
    for i, (src, kkey) in enumerate(zip(srcs, keys)):
        pb, pk = getF()
        flat = pb[:].rearrange("p a b -> p (a b)")
        for m in range(4):
            ph.tr(flat[:NS, m * 128:(m + 1) * 128], src[:, m, :], identf[:], R=kkey, W=pk)
        ph.cp(V if i % 2 else "act", tok[:, i, :], flat[:NS, 0:512], R=pk, W="s_tok")
    ph.dma("sp", I["SW"].rearrange("i b f -> b i f"), tok[:], R="s_tok", W="SWd")
    vec = sb("s_vec", [128, 6, 64], F32)
    ph.dma("sp", vec[:], I["SW"].rearrange("i b (h k) -> (b h) i k", h=8), R="SWd", W="s_vec")
    S0 = sb("s_S0", [128, 64, 64], F32)
    ph.dma("act", S0[:].rearrange("p a b -> p (a b)"), I["st_wkv"], W="s_S0")
    tmp = sb("s_tmp", [128, 64, 64], F32)
    sa = sb("s_sa", [128, 64], F32); yv = sb("s_yv", [128, 64], F32); kka = sb("s_kka", [128, 64], F32)
    kB = lambda i: bc(vec[:, i, :].unsqueeze(1), [128, 64, 64])
    ph.tt(V, tmp[:], S0[:], kB(4), ALU.mult, R=["s_S0", "s_vec"], W="s_tmp")
    ph.op(V, lambda e: e.tensor_reduce(out=sa[:], in_=tmp[:], axis=AX.X, op=ALU.add), R="s_tmp", W="s_sa")
    ph.tt(V, S0[:], S0[:], kB(1), ALU.mult, R=["s_S0", "s_vec", "s_tmp"], W="s_S0")
    ph.tt(V, tmp[:], bc(sa[:, :].unsqueeze(2), [128, 64, 64]), kB(5), ALU.mult, R=["s_sa", "s_vec"], W="s_tmp")
    ph.tt(V, S0[:], S0[:], tmp[:], ALU.subtract, R=["s_S0", "s_tmp"], W="s_S0")
    ph.tt(V, tmp[:], bc(vec[:, 3, :].unsqueeze(2), [128, 64, 64]), kB(2), ALU.mult, R=["s_vec", "s_S0"], W="s_tmp")
    ph.tt(V, S0[:], S0[:], tmp[:], ALU.add, R=["s_S0", "s_tmp"], W="s_S0")
    ph.dma("act", I["s_wkv"], S0[:].rearrange("p a b -> p (a b)"), R="s_S0")
    ph.tt(V, tmp[:], S0[:], kB(0), ALU.mult, R=["s_S0", "s_vec"], W="s_tmp")
    ph.op(V, lambda e: e.tensor_reduce(out=yv[:], in_=tmp[:], axis=AX.X, op=ALU.add), R="s_tmp", W="s_yv")
    ph.dma("sp", I["SY"], yv[:], R="s_yv", W="SYd")
    Ysb = L["Ysb"]
    ph.dma("sp", Ysb[:NS].rearrange("p a b -> p (a b)"), I["SY"].rearrange("(b h) v -> b (h v)", h=8), R="SYd", W="Ysb")
    groupnorm_out(ph, L, 0, NS)


def phase3(nc, I, G0, W3, WFI):
    ph = Ph(nc, "p3")
    V = "dve"
    W3 = alloc_w3(nc, ph.st)
    load_w3(ph, I, W3)
    rwo, glu, wo = W3["rwo"], W3["glu"], W3["wo"]
    for k in range(8):
        ph.dma("pool", WFI[:, k, :], I["w_ffn_in"][k * 128:(k + 1) * 128, :], W="wfi_pre")
    yf = ph.sb("yf", [128, 4, 512], BF16); zz = ph.sb("zz", [128, 4, 512], BF16); gt = ph.sb("gt", [128, 16, 512], BF16)
    trw = ph.sb("trw", [128, 8, 512], F32)
    mgs = [ph.sb("mg%d" % i, [128, 8, 512], BF16) for i in range(2)]
    sgb = [ph.sb("sgb%d" % i, [128, 512], F32) for i in range(2)]
    s5t = [ph.sb("s5t%d" % i, [128, 512], F32) for i in range(2)]
    xts = [ph.sb("xt%d" % i, [128, D], F32) for i in range(2)]
    pm = [ph.ps("pm%d" % i, [128, 512], F32) for i in range(6)]
    npm = nx = ns = 0
    SA, SB = [], []
    for bi_, (t0, nt) in enumerate(BLOCKS):
        P = min(128, nt)
        mg = mgs[bi_ % 2]; mk_ = "mg%d" % (bi_ % 2)
        r3 = lambda name: I[name][:, t0:t0 + nt].rearrange("(m p) t -> p m t", p=128)
        ph.rec_begin()
        ph.dma("sp", yf[:, :, :nt], r3("YF"), W="yf"); ph.dma("sp", zz[:, :, :nt], r3("ZZ"), W="zz")
        ph.dma("act", gt[:, :, :nt], r3("GT"), W="gt")
        for m in range(8):
            pb = pm[npm % 6]; pk = "pm%d" % (npm % 6); npm += 1
            for k in range(4):
                ph.mm(pb[:, :nt], rwo[:, k, m * 128:(m + 1) * 128], yf[:, k, :nt], k == 0, k == 3, R=["rwo", "yf"], W=pk)
            ph.tt(V, trw[:, m, :nt], pb[:, :nt], gt[:, m, :nt], ALU.mult, R=[pk, "gt"], W="trw%d" % m)
        for m in range(8):
            pa = pm[npm % 6]; pka = "pm%d" % (npm % 6); npm += 1
            pb = pm[npm % 6]; pkb = "pm%d" % (npm % 6); npm += 1
            for k in range(4):
                ph.mm(pa[:, :nt], glu[:, k, m * 128:(m + 1) * 128], zz[:, k, :nt], k == 0, k == 3, R=["glu", "zz"], W=pka)
            for k in range(4):
                ph.mm(pb[:, :nt], glu[:, k, D + m * 128:D + (m + 1) * 128], zz[:, k, :nt], k == 0, k == 3,
                      R=["glu", "zz"], W=pkb)
            sg = sgb[ns % 2]; sk = "sgb%d" % (ns % 2); s5 = s5t[ns % 2]; s5k = "s5t%d" % (ns % 2); ns += 1
            ph.act(sg[:, :nt], pb[:, :nt], AF.Sigmoid, R=pkb, W=sk)
            ph.tt(V, s5[:, :nt], pa[:, :nt], sg[:, :nt], ALU.mult, R=[pka, sk], W=s5k)
            ph.tt(V, s5[:, :nt], s5[:, :nt], gt[:, 8 + m, :nt], ALU.mult, R=[s5k, "gt"], W=s5k)
            ph.tt(V, mg[:, m, :nt], s5[:, :nt], trw[:, m, :nt], ALU.add, R=[s5k, "trw%d" % m], W=mk_)
        SA.append(ph.rec_end())
        ph.rec_begin()
        for s in range((nt + 127) // 128):
            xt = xts[nx % 2]; xk = "xt%d" % (nx % 2); nx += 1
            rows = slice(t0 + s * 128, t0 + s * 128 + P)
            ph.dma("sp", xt[:P, :], I["xall"][rows, :], W=xk)
            for half in range(2):
                pb = pm[npm % 6]; pk = "pm%d" % (npm % 6); npm += 1
                for k in range(8):
                    ph.mm(pb[:P, :], mg[:, k, s * 128:s * 128 + P], wo[:, k, half * 512:(half + 1) * 512], k == 0, k == 7,
                          R=[mk_, "wo"], W=pk)
                ph.tt(V, xt[:P, half * 512:(half + 1) * 512], xt[:P, half * 512:(half + 1) * 512], pb[:P, :], ALU.add,
                      R=[pk, xk], W=xk)
            ph.dma("pool", I["X1"][rows, :], xt[:P, :], R=xk)
        SB.append(ph.rec_end())
    ph.play(SA[0])
    for b_ in range(len(BLOCKS)):
        if b_ + 1 < len(BLOCKS):
            ph.play(SA[b_ + 1])
        ph.play(SB[b_])
    ph.finish()


def phase4(nc, I, G0, WFI):
    ph = Ph(nc, "p4")
    V = "dve"
    G = norm_scratch(ph, G0)
    identf = G0["identf"]
    wfi = WFI; wfo = ph.sb("wfo", [128, 22, D], BF16)
    for k in range(22):
        ph.dma("pool", wfo[:, k, :], I["w_ffn_out"][k * 128:(k + 1) * 128, :], W="wfo")
    g2c = ph.sb("g2c", [128, 8], F32); load_col(ph, g2c[:], I["ln2_g"], 8, "g2c")
    cw = ph.sb("cw", [128, 3, 22], F32); cb = ph.sb("cb", [128, 22], F32)
    ph.dma("sp", cw[:], I["conv_w"].rearrange("t (f p) -> p t f", p=128), W="cw", slow=True)
    load_col(ph, cb[:], I["conv_b"], 22, "cb")
    hTs = [ph.sb("hT%d" % i, [128, 8, 512], BF16) for i in range(2)]
    hid = ph.sb("hid", [128, 22, 512], BF16)
    xts = [ph.sb("xt%d" % i, [128, D], F32) for i in range(2)]
    At = [ph.sb("At%d" % i, [128, 514], F32) for i in range(2)]
    acc = [ph.sb("acc%d" % i, [128, 512], F32) for i in range(2)]
    cc = ph.sb("cc", [128, 22, 2], F32)
    ph.memset(V, cc[:].rearrange("p a b -> p (a b)"), 0.0, W="cc")
    pm = [ph.ps("pm%d" % i, [128, 512], F32) for i in range(6)]
    scs = ph.sb("scs", [NS, 2816], F32)
    scT = ph.sb("scT", [128, 22, 2, NS], F32)
    aout = scs
    npm = na = 0
    NR, FI, FO = [], [], []
    for bi_, (t0, nt) in enumerate(BLOCKS):
        P = min(128, nt)
        hT = hTs[bi_ % 2]; hk = "hT%d" % (bi_ % 2)
        sample = nt < 128
        nsub = (nt + 127) // 128
        ph.rec_begin()
        for s in range(nsub):
            rows = slice(t0 + s * 128, t0 + s * 128 + P)
            ph.dma("sp", xts[s % 2][:P, :], I["X1"][rows, :], W="xt%d" % (s % 2))
            rms_to_hT(ph, G, xts[s % 2], P, g2c, hT, s * 128, str(s % 2), "g2c", hk)
        NR.append(ph.rec_end())
        ph.rec_begin()
        if sample:
            for tt_ in range(2):
                ph.dma("sp", scs[:], I["st_conv"][:, tt_, :], W="scs")
                for q in range(6):
                    pb = pm[npm % 6]; pk = "pm%d" % (npm % 6); npm += 1
                    fs = list(range(q * 4, min(22, q * 4 + 4)))
                    for j, f_ in enumerate(fs):
                        ph.tr(pb[:, j * NS:(j + 1) * NS], scs[:, f_ * 128:(f_ + 1) * 128], identf[:NS, :NS], R="scs", W=pk)
                    ph.cp(V, scT[:, fs[0]:fs[-1] + 1, tt_, :], pb[:, 0:len(fs) * NS].rearrange("p (a b) -> p a b", b=NS),
                          R=pk, W="scT")
        for f in range(22):
            pa = pm[npm % 6]; pka = "pm%d" % (npm % 6); npm += 1
            pb = pm[npm % 6]; pkb = "pm%d" % (npm % 6); npm += 1
            for k in range(8):
                ph.mm(pa[:, :nt], wfi[:, k, f * 128:(f + 1) * 128], hT[:, k, :nt], k == 0, k == 7, R=["wfi", hk], W=pka)
            for k in range(8):
                ph.mm(pb[:, :nt], wfi[:, k, 2816 + f * 128:2816 + (f + 1) * 128], hT[:, k, :nt], k == 0, k == 7,
                      R=["wfi", hk], W=pkb)
            A = At[na % 2]; ak = "At%d" % (na % 2); ac = acc[na % 2]; ck = "acc%d" % (na % 2); na += 1
            ph.cp("act", A[:, 2:2 + nt], pa[:, :nt], R=pka, W=ak)
            if not sample:
                ph.cp(V, A[:, 0:2], cc[:, f, :], R="cc", W=ak)
                a0, a1, a2 = A[:, 0:nt], A[:, 1:1 + nt], A[:, 2:2 + nt]
            else:
                a0, a1, a2 = scT[:, f, 0, :], scT[:, f, 1, :], A[:, 2:2 + nt]
            ph.ts(V, ac[:, :nt], a0, cw[:, 0, f:f + 1], ALU.mult, cb[:, f:f + 1], ALU.add, R=[ak, "scT", "cw", "cb"], W=ck)
            ph.stt(ac[:, :nt], a1, cw[:, 1, f:f + 1], ac[:, :nt], ALU.mult, ALU.add, R=[ak, "scT", "cw", ck], W=ck)
            ph.stt(ac[:, :nt], a2, cw[:, 2, f:f + 1], ac[:, :nt], ALU.mult, ALU.add, R=[ak, "cw", ck], W=ck)
            ph.act(ac[:, :nt], ac[:, :nt], AF.Gelu_apprx_tanh, R=ck, W=ck)
            ph.tt(V, hid[:, f, :nt], ac[:, :nt], pb[:, :nt], ALU.mult, R=[ck, pkb], W="hid")
            if not sample:
                ph.cp(V, cc[:, f, :], A[:, nt:nt + 2], R=ak, W="cc")
            else:
                po = pm[npm % 6]; pko = "pm%d" % (npm % 6); npm += 1
                ph.tr(po[:NS, 0:128], A[:, 2:2 + NS], identf[:], R=ak, W=pko)
                ph.cp(V, aout[:, f * 128:(f + 1) * 128], po[:NS, 0:128], R=pko, W="scs")
        if t0 + nt == T:
            for tt_ in range(2):
                ph.dma("sp", I["p_conv"][tt_].rearrange("(f p) -> p f", p=128), cc[:, :, tt_], R="cc", slow=True)
        if sample:
            ph.dma("sp", I["s_conv"][:, 1, :], aout[:], R="scs")
            ph.dma("act", I["s_conv"][:, 0, :], I["st_conv"][:, 1, :])
        FI.append(ph.rec_end())
        ph.rec_begin()
        for s in range(nsub):
            rows = slice(t0 + s * 128, t0 + s * 128 + P)
            xt = xts[s % 2]; xk = "xt%d" % (s % 2)
            ph.dma("sp", xt[:P, :], I["X1"][rows, :], W=xk)
            for half in range(2):
                pb = pm[npm % 6]; pk = "pm%d" % (npm % 6); npm += 1
                for f in range(22):
                    ph.mm(pb[:P, :], hid[:, f, s * 128:s * 128 + P], wfo[:, f, half * 512:(half + 1) * 512], f == 0, f == 21,
                          R=["hid", "wfo"], W=pk)
                ph.tt(V, xt[:P, half * 512:(half + 1) * 512], xt[:P, half * 512:(half + 1) * 512], pb[:P, :],
                      ALU.add, R=[pk, xk], W=xk)
            ph.dma("pool", I["X2"][rows, :], xt[:P, :], R=xk)
        FO.append(ph.rec_end())
    ph.play(NR[0])
    for b_ in range(len(BLOCKS)):
        if STRICT4:
            if b_ + 1 < len(BLOCKS):
                ph.play(NR[b_ + 1])
            ph.play(FI[b_])
        else:
            ph.play(FI[b_], NR[b_ + 1] if b_ + 1 < len(BLOCKS) else [])
        ph.play(FO[b_])
    ph.finish()


def phase5(nc, I, G0):
    ph = Ph(nc, "p5")
    V = "dve"
    NB = 4
    Gs = [norm_scratch(ph, G0, "a")]
    for i in range(1, NB):
        Gs.append(norm_scratch(ph, G0, "abcd"[i], eps=Gs[0]["eps"]))
    wpg = ph.sb("wpg", [128, 8, D], BF16); wpl = ph.sb("wpl", [128, 2, D], BF16)
    for k in range(8):
        ph.dma("pool", wpg[:, k, :], I["w_ple_gate"][k * 128:(k + 1) * 128, :], W="wpg")
    for k in range(2):
        ph.dma("pool", wpl[:, k, :], I["w_ple"][k * 128:(k + 1) * 128, :], W="wpl")
    g3c = ph.sb("g3c", [128, 8], F32); load_col(ph, g3c[:], I["ln3_g"], 8, "g3c")
    fg = ph.sb("fg", [128, D], F32)
    ph.dma("sp", fg[:], I["final_g"].partition_broadcast(128), W="fg")
    hTs = [ph.sb("hT%d" % i, [128, 8, 128], BF16) for i in range(NB)]
    xts = [ph.sb("xt%d" % i, [128, D], F32) for i in range(NB)]
    sg = [ph.sb("sg%d" % i, [128, 512], F32) for i in range(NB)]
    yo = [ph.sb("yo%d" % i, [128, D], F32) for i in range(NB)]
    pm = [ph.ps("pm%d" % i, [128, 512], F32) for i in range(3)]
    pq = ph.ps("pq", [128, 8, 128], BF16)
    subs = []
    for (t0, nt) in BLOCKS:
        P = min(128, nt)
        for s in range((nt + 127) // 128):
            subs.append((t0 + s * 128, P))
    NSUB = len(subs)
    pball = ph.sb("pball", [128, NSUB, 256], BF16)
    pTall = ph.sb("pTall", [128, NSUB, 2, 128], BF16)
    for i, (r0, P) in enumerate(subs):
        ph.dma("pool", pball[:P, i, :], I["pall"][r0:r0 + P, :], W="pb%d" % i)
    for i0_ in range(0, NSUB, 4):
        grp = list(range(i0_, min(NSUB, i0_ + 4)))
        for j, i in enumerate(grp):
            P = subs[i][1]
            for k in range(2):
                ph.tr(pq[:, 2 * j + k, :P], pball[:P, i, k * 128:(k + 1) * 128], G0["identb"][:P, :P],
                      R=["pb%d" % i, "identb"], W="pq")
        for j, i in enumerate(grp):
            P = subs[i][1]
            ph.cp("act", pTall[:, i, :, :P], pq[:, 2 * j:2 * j + 2, :P], R="pq", W="pT%d" % i)
    npm = nsg = 0
    FR, BK = [], []
    for i, (r0, P) in enumerate(subs):
        rows = slice(r0, r0 + P)
        i2 = i % NB
        xt = xts[i2]; xk = "xt%d" % i2
        G = Gs[i2]; hT = hTs[i2]; hk = "hT%d" % i2
        ph.rec_begin()
        ph.dma("sp", xt[:P, :], I["X2"][rows, :], W=xk)
        rms_to_hT(ph, G, xt, P, g3c, hT, 0, str(i2), "g3c", hk)
        FR.append(ph.rec_end())
        ph.rec_begin()
        for half in range(2):
            cs_ = slice(half * 512, (half + 1) * 512)
            pg = pm[npm % 3]; pgk = "pm%d" % (npm % 3); npm += 1
            pe = pm[npm % 3]; pek = "pm%d" % (npm % 3); npm += 1
            for k in range(8):
                ph.mm(pg[:P, :], hT[:, k, :P], wpg[:, k, cs_], k == 0, k == 7, R=[hk, "wpg"], W=pgk)
            for k in range(2):
                ph.mm(pe[:P, :], pTall[:, i, k, :P], wpl[:, k, cs_], k == 0, k == 1, R=["pT%d" % i, "wpl"], W=pek)
            sgt = sg[nsg % NB]; sgk = "sg%d" % (nsg % NB); nsg += 1
            ph.act(sgt[:P, :], pg[:P, :], AF.Sigmoid, R=pgk, W=sgk)
            ph.tt(V, sgt[:P, :], sgt[:P, :], pe[:P, :], ALU.mult, R=[sgk, pek], W=sgk)
            ph.tt(V, xt[:P, cs_], xt[:P, cs_], sgt[:P, :], ALU.add, R=[sgk, xk, "xn" + G["sx"]], W=xk)
        ss = G["ss"]; sq = G["sq"]; kss = "ss" + G["sx"]; ksq = "sq" + G["sx"]
        ph.act(sq[:P, :], xt[:P, :], AF.Square, R=xk, W=[ksq, kss], accum=ss[:P, 0:1])
        ph.act(ss[:P, 1:2], ss[:P, 0:1], AF.Sqrt, R=[kss, "eps"], W=kss, bias=G["eps"][:P, 0:1], scale=1.0 / D)
        ph.op(V, lambda e, ss=ss, P=P: e.reciprocal(out=ss[:P, 3:4], in_=ss[:P, 1:2]), R=kss, W=kss + "3")
        y = yo[i2]; yk = "yo%d" % i2
        ph.stt(y[:P, :], xt[:P, :], ss[:P, 3:4], fg[:P, :], ALU.mult, ALU.mult, R=[xk, kss + "3", "fg"], W=yk)
        ph.dma("pool", I["y"][rows, :], y[:P, :], R=yk)
        BK.append(ph.rec_end())
    AHEAD = 2
    for i in range(min(AHEAD, NSUB)):
        ph.play(FR[i])
    for i in range(NSUB):
        if i + AHEAD < NSUB:
            ph.play(FR[i + AHEAD])
        ph.play(BK[i])
    ph.finish()


_CACHE = {}


def _consts():
    i = np.arange(128)
    c = {}
    c["c_ident"] = np.eye(128, dtype=np.float32)
    c["c_msl"] = (i[None, :] < i[:, None]).astype(np.float32)
    c["c_msu"] = (i[:, None] < i[None, :]).astype(np.float32)
    c["c_mui"] = (i[:, None] <= i[None, :]).astype(np.float32)
    c["c_blk64"] = ((i[:, None] // 64) == (i[None, :] // 64)).astype(np.float32)
    c["c_blk32"] = ((i[:, None] // 32) == (i[None, :] // 32)).astype(np.float32)
    c["c_rowgp"] = (((i[:, None] // 16) % 2) == (i[None, :] // 64)).astype(np.float32)
    return c


def make_in_maps(inp):
    f = lambda a: np.ascontiguousarray(np.asarray(a, dtype=np.float32))
    cst = _consts()
    shared = {}
    for k in ("ln1_g", "w_in", "mu_shift", "w0", "w2", "a0", "a2", "g2", "k_k", "k_a", "lnx_g", "lnx_b", "w_rw_out",
              "A_re", "A_im", "log_dt", "B_re", "B_im", "D_skip", "w_glu", "w_out", "ln2_g", "w_ffn_in", "conv_w",
              "conv_b", "w_ffn_out", "ln3_g", "w_ple_gate", "w_ple"):
        shared[k] = f(inp[k])[0]
    shared["r_k"] = f(inp["r_k"])[0].reshape(512)
    shared["C_re"] = f(inp["C_re"])[0].reshape(512, 64)
    shared["C_im"] = f(inp["C_im"])[0].reshape(512, 64)
    shared["final_g"] = f(inp["final_g"])
    shared.update(cst)
    xp, xs = f(inp["x_prompt"]), f(inp["x_sample"])
    pp, psm = f(inp["p_prompt"])[0], f(inp["p_sample"])[0]
    in_maps = []
    for c in range(8):
        sl = slice(NS * c, NS * c + NS)
        m = dict(shared)
        m["xall"] = np.concatenate([xp[c], xs[sl, 0]], 0)
        m["pall"] = np.concatenate([pp[c], psm[sl, 0]], 0)
        m["st_shift"] = f(inp["state_shift"])[0, sl]
        m["st_wkv"] = f(inp["state_wkv"])[0, sl].reshape(128, 4096)
        m["st_re"] = f(inp["state_ssm_re"])[0, sl].reshape(NS, 2048)
        m["st_im"] = f(inp["state_ssm_im"])[0, sl].reshape(NS, 2048)
        m["st_conv"] = f(inp["state_conv"])[0, sl]
        in_maps.append({k: np.ascontiguousarray(v) for k, v in m.items()})
    return in_maps


def kernel(**inp):
    f = lambda a: np.ascontiguousarray(np.asarray(a, dtype=np.float32))
    if "nc" not in _CACHE:
        _CACHE["nc"] = build_program()
    nc = _CACHE["nc"]
    in_maps = make_in_maps(inp)
    res = run_bass_kernel_spmd(nc, in_maps, core_ids=list(range(8)))
    R = res.results
    cat = lambda fn: np.stack([fn(r) for r in R], 0)
    y_prompt = cat(lambda r: r["y"][:T])
    y_sample = np.concatenate([r["y"][T:] for r in R], 0)[:, None, :]
    p_shift = cat(lambda r: r["p_shift"])[None]
    p_wkv = cat(lambda r: r["p_wkv"].reshape(8, 64, 64).transpose(0, 2, 1))[None]
    p_re = cat(lambda r: r["p_re"].reshape(32, 64))[None]
    p_im = cat(lambda r: r["p_im"].reshape(32, 64))[None]
    p_conv = cat(lambda r: r["p_conv"])[None]
    s_shift = np.concatenate([r["s_shift"] for r in R], 0)[None]
    s_wkv = np.concatenate([r["s_wkv"].reshape(NS, 8, 64, 64) for r in R], 0)[None]
    s_re = np.concatenate([r["s_re"].reshape(NS, 32, 64) for r in R], 0)[None]
    s_im = np.concatenate([r["s_im"].reshape(NS, 32, 64) for r in R], 0)[None]
    s_conv = np.concatenate([r["s_conv"] for r in R], 0)[None]
    outs = (y_prompt, y_sample, p_shift, p_wkv, p_re, p_im, p_conv, s_shift, s_wkv, s_re, s_im, s_conv)
    return tuple(np.ascontiguousarray(o.astype(np.float32)) for o in outs)
```

```python
import contextlib
import math
import numpy as np
import concourse.bass as bass
import concourse.mybir as mybir
from concourse.bass_utils import run_bass_kernel_spmd

F32 = mybir.dt.float32
BF16 = mybir.dt.bfloat16
AF = mybir.ActivationFunctionType
ALU = mybir.AluOpType
AX = mybir.AxisListType

T = 2048
NS = 16
NT = T + NS
D = 1024
CS = 8
SCAN_ENG = "pool"
import os as _os
BUB = int(_os.environ.get("K_BUB", "48"))
BUB2 = int(_os.environ.get("K_BUB2", "0"))
SYO = float(_os.environ.get("K_SYO", "0.5"))
STRICT1 = int(_os.environ.get("K_ST1", "0"))
STRICT4 = int(_os.environ.get("K_ST4", "1"))
C1 = math.exp(-0.5)
BLOCKS = [(0, 512), (512, 512), (1024, 512), (1536, 512), (2048, 16)]

ENGS = ("pe", "act", "dve", "pool", "sp")
NDSEM = 12


class _Op:
    __slots__ = ("eng", "fn", "deps", "dma", "observed", "tok", "idx", "dslot")

    def __init__(self, eng, fn, dma):
        self.eng, self.fn, self.dma = eng, fn, dma
        self.deps = set()
        self.observed = False
        self.tok = None
        self.dslot = None


class Sched:
    def __init__(self, nc):
        self.nc = nc
        self.ops = []
        self.last_w = {}
        self.readers = {}
        self.dma_rr = {e: 0 for e in ENGS}
        self.dma_prev = {}
        self.excl = set()

    def _add(self, eng, fn, reads, writes, dma):
        op = _Op(eng, fn, dma)
        op.idx = len(self.ops)
        if self.excl:
            ex = tuple(b for b in reads if b in self.excl)
            if ex:
                writes = tuple(writes) + ex
        for b in reads:
            w = self.last_w.get(b)
            if w is not None:
                op.deps.add(w)
        for b in writes:
            w = self.last_w.get(b)
            if w is not None:
                op.deps.add(w)
            for r in self.readers.get(b, ()):
                op.deps.add(r)
        if dma:
            slot = (eng, self.dma_rr[eng] % NDSEM)
            self.dma_rr[eng] += 1
            op.dslot = slot
            prev = self.dma_prev.get(slot)
            if prev is not None:
                op.deps.add(prev)
            self.dma_prev[slot] = op.idx
        op.deps.discard(op.idx)
        self.ops.append(op)
        for b in writes:
            self.last_w[b] = op.idx
            self.readers[b] = []
        for b in reads:
            if b not in writes:
                self.readers.setdefault(b, []).append(op.idx)
        return op.idx

    def emit(self):
        nc = self.nc
        ops = self.ops
        need = []
        for op in ops:
            nd = []
            for d in op.deps:
                p = ops[d]
                if (not p.dma) and (not op.dma) and p.eng == op.eng == "pe":
                    continue
                nd.append(d)
                p.observed = True
            need.append(nd)
        last = {}
        for op in ops:
            key = op.dslot if op.dma else op.eng
            last[key] = op.idx
        for i in last.values():
            ops[i].observed = True
        g = getattr(nc, "_gsem", None)
        if g is None:
            g = {"sems": {}, "cnt": {e: 0 for e in ENGS}, "dcnt": {}}
            nc._gsem = g
        cnt = g["cnt"]
        dcnt = g["dcnt"]
        for op in ops:
            if op.dma:
                dcnt[op.dslot] = dcnt.get(op.dslot, 0) + 16
                op.tok = (op.dslot, dcnt[op.dslot])
            elif op.observed:
                cnt[op.eng] += 1
                op.tok = (op.eng, cnt[op.eng])
        sems = g["sems"]
        for k in list(ENGS) + sorted(set(o.dslot for o in ops if o.dma)):
            if k not in sems:
                nm = k if isinstance(k, str) else "d_%s_%d" % k
                sems[k] = nc.alloc_semaphore(name="s_" + nm)
        with contextlib.ExitStack() as st:
            block = st.enter_context(nc.Block())
            per = {e: [o for o in ops if o.eng == e] for e in ENGS}
            hw = {"pe": block.tensor, "act": block.scalar, "dve": block.vector,
                  "pool": block.gpsimd, "sp": block.sync}

            def make(e):
                def body(eng):
                    seen = {}
                    for op in per[e]:
                        waits = {}
                        for d in need[op.idx]:
                            k, v = ops[d].tok
                            if v > waits.get(k, 0):
                                waits[k] = v
                        for k, v in waits.items():
                            if seen.get(k, 0) >= v:
                                continue
                            seen[k] = v
                            eng.wait_ge(sems[k], v)
                        ins = op.fn(eng)
                        if op.dma:
                            ins.then_inc(sems[op.tok[0]], 16)
                        elif op.observed:
                            ins.then_inc(sems[e], 1)
                    if e == "sp":
                        for key, i in last.items():
                            k, v = ops[i].tok
                            if seen.get(k, 0) < v:
                                eng.wait_ge(sems[k], v)
                return body

            for e in ENGS:
                hw[e](make(e))


def _L(x):
    if x is None:
        return ()
    if isinstance(x, str):
        return (x,)
    return tuple(x)


class Ph:
    _uid = [0]

    def __init__(self, nc, tag):
        self.nc = nc
        self.tag = tag
        self.st = contextlib.ExitStack()
        self.S = Sched(nc)

    def sb(self, name, shape, dt):
        return self.st.enter_context(self.nc.sbuf_tensor(self.tag + "_" + name, list(shape), dt))

    def ps(self, name, shape, dt):
        self.S.excl.add(name)
        return self.st.enter_context(self.nc.psum_tensor(self.tag + "_" + name, list(shape), dt))

    def finish(self):
        self.S.emit()
        self.st.close()

    def dbg(self, name, ap, shape, key, dt=F32):
        import os
        if os.environ.get("K_DBG_DUMP", "") == "":
            return
        t = self.nc.dram_tensor("dbg_" + name, list(shape), dt, kind="ExternalOutput").ap()
        self.dma("sp", t, ap, R=key)

    _rec = None

    def rec_begin(self):
        self._rec = []

    def rec_end(self):
        r, self._rec = self._rec, None
        return r

    def bubble(self, k):
        if self._rec is not None:
            self._rec.append(("bubble", k))

    def merge(self, *streams, spans=None):
        if spans is None:
            spans = [(0.0, 1.0)] * len(streams)
        keep = [i for i, st_ in enumerate(streams) if st_]
        spans = [spans[i] for i in keep]
        streams = [streams[i] for i in keep]
        pos = [0] * len(streams)
        out = []
        while True:
            best, bi = None, -1
            for i, st_ in enumerate(streams):
                if pos[i] < len(st_):
                    f = spans[i][0] + spans[i][1] * (pos[i] + 1.0) / len(st_)
                    if best is None or f < best:
                        best, bi = f, i
            if bi < 0:
                break
            item = streams[bi][pos[bi]]
            pos[bi] += 1
            if item[0] == "bubble":
                left = item[1]
                prog = True
                while left > 0 and prog:
                    prog = False
                    for j in range(len(streams)):
                        if j != bi and pos[j] < len(streams[j]) and left > 0:
                            it2 = streams[j][pos[j]]
                            pos[j] += 1
                            prog = True
                            if it2[0] != "bubble":
                                out.append(it2)
                                left -= 1
                continue
            out.append(item)
        return out

    def play(self, *streams, spans=None):
        for item in self.merge(*streams, spans=spans):
            if item[0] == "bubble":
                continue
            eng, fn, R, W, dma = item
            self.S._add(eng, fn, R, W, dma)

    def op(self, eng, fn, R=None, W=None):
        if self._rec is not None:
            self._rec.append((eng, fn, _L(R), _L(W), False))
        else:
            self.S._add(eng, fn, _L(R), _L(W), False)

    def dma(self, q, out, in_, R=None, W=None, slow=False):
        if slow:
            fn = lambda e: e.dma_start(out=out, in_=in_, allow_slow_non_contiguous=True)
        else:
            fn = lambda e: e.dma_start(out=out, in_=in_)
        if self._rec is not None:
            self._rec.append((q, fn, _L(R), _L(W), True))
        else:
            self.S._add(q, fn, _L(R), _L(W), True)

    def tt(self, eng, out, in0, in1, op, R=None, W=None):
        self.op(eng, lambda e: e.tensor_tensor(out=out, in0=in0, in1=in1, op=op), R, W)

    def ts(self, eng, out, in0, s1, op0, s2=None, op1=None, R=None, W=None):
        if op1 is None:
            self.op(eng, lambda e: e.tensor_scalar(out=out, in0=in0, scalar1=s1, scalar2=None, op0=op0), R, W)
        else:
            self.op(eng, lambda e: e.tensor_scalar(out=out, in0=in0, scalar1=s1, scalar2=s2, op0=op0, op1=op1), R, W)

    def stt(self, out, in0, scalar, in1, op0, op1, R=None, W=None):
        self.op("dve", lambda e: e.scalar_tensor_tensor(out=out, in0=in0, scalar=scalar, in1=in1, op0=op0, op1=op1), R, W)

    def act(self, out, in_, func, R=None, W=None, bias=None, scale=1.0, accum=None):
        kw = {}
        if bias is not None:
            kw["bias"] = bias
        if accum is not None:
            kw["accum_out"] = accum
        self.op("act", lambda e: e.activation(out=out, in_=in_, func=func, scale=scale, **kw), R, W)

    def cp(self, eng, out, in_, R=None, W=None):
        if eng == "act":
            self.op("act", lambda e: e.activation(out=out, in_=in_, func=AF.Copy), R, W)
        else:
            self.op(eng, lambda e: e.tensor_copy(out=out, in_=in_), R, W)

    def mm(self, out, lhsT, rhs, start, stop, R=None, W=None, tp=None):
        if tp is None:
            self.op("pe", lambda e: e.matmul(out, lhsT=lhsT, rhs=rhs, start=start, stop=stop), R, W)
        else:
            self.op("pe", lambda e: e.matmul(out, lhsT=lhsT, rhs=rhs, start=start, stop=stop, tile_position=tp), R, W)

    def tr(self, out, in_, ident, R=None, W=None):
        self.op("pe", lambda e: e.transpose(out, in_, ident), R, W)

    def memset(self, eng, ap, v, W=None):
        self.op(eng, lambda e: e.memset(ap, v), None, W)


def bc(ap, shape):
    return ap.to_broadcast(list(shape))


def rms_to_hT(ph, G, xt, P, gcol, hT, c0, tag, gkey, hkey="hT"):
    sq, ss, xn, pT = G["sq"], G["ss"], G["xn"], G["pT"]
    x_ = G.get("sx", "")
    ksq, kss, kxn, kpT = "sq" + x_, "ss" + x_, "xn" + x_, "pT" + x_
    ph.act(sq[:P, :], xt[:P, :], AF.Square, R="xt" + tag, W=[ksq, kss], accum=ss[:P, 0:1])
    ph.act(ss[:P, 1:2], ss[:P, 0:1], AF.Sqrt, R=[kss, "eps"], W=kss, bias=G["eps"][:P, 0:1], scale=1.0 / D)
    ph.op("dve", lambda e: e.reciprocal(out=ss[:P, 2:3], in_=ss[:P, 1:2]), R=kss, W=kss)
    ph.ts("dve", xn[:P, :], xt[:P, :], ss[:P, 2:3], ALU.mult, R=["xt" + tag, kss], W=kxn)
    for k in range(8):
        ph.tr(pT[:, k, :P], xn[:P, k * 128:(k + 1) * 128], G["identb"][:P, :P], R=[kxn, "identb"], W=kpT)
    ph.tt("dve", hT[:, :, c0:c0 + P], pT[:, :, :P], bc(gcol[:, :].unsqueeze(2), [128, 8, P]), ALU.mult,
          R=[kpT, gkey], W=hkey)


def load_col(ph, dst, src1d, n, key):
    ph.dma("sp", dst, src1d.rearrange("(k p) -> p k", p=128), W=key, slow=True)


def norm_scratch(ph, G0, sx="", eps=None):
    G = dict(G0)
    G["sx"] = sx
    G["sq"] = ph.sb("sq" + sx, [128, D], F32)
    G["ss"] = ph.sb("ss" + sx, [128, 4], F32)
    G["xn"] = ph.sb("xn" + sx, [128, D], BF16)
    G["pT"] = ph.ps("pT" + sx, [128, 8, 128], BF16)
    if eps is None:
        G["eps"] = ph.sb("eps", [128, 1], F32)
        ph.memset("dve", G["eps"][:], 1e-6, W="eps")
    else:
        G["eps"] = eps
    return G


def build_program(upto=9, debug=False):
    nc = bass.Bass("TRN2", target_bir_lowering=False)
    I = {}

    def inp(name, shape, dt=F32):
        I[name] = nc.dram_tensor(name, list(shape), dt, kind="ExternalInput").ap()

    def outp(name, shape):
        I[name] = nc.dram_tensor(name, list(shape), F32, kind="ExternalOutput").ap()

    def scratch(name, shape, dt):
        if debug:
            I[name] = nc.dram_tensor(name, list(shape), dt, kind="ExternalOutput").ap()
        else:
            I[name] = nc.dram_tensor(name, list(shape), dt).ap()
    if debug:
        scratch("d_BwT", [128, 4 * CS * 2 * 128], BF16); scratch("d_Kmat", [128, 4 * CS * 128], BF16)
        scratch("d_CwT", [128, CS * 2 * 16 * 32], BF16); scratch("d_Abar", [128, 64], F32)

    inp("xall", [NT, D]); inp("pall", [NT, 256])
    inp("st_shift", [NS, 1792]); inp("st_wkv", [128, 4096]); inp("st_re", [NS, 2048]); inp("st_im", [NS, 2048])
    inp("st_conv", [NS, 2, 2816])
    inp("ln1_g", [D]); inp("w_in", [D, 4352]); inp("mu_shift", [1792]); inp("w0", [512]); inp("w2", [64, 512])
    inp("a0", [512]); inp("a2", [64, 512]); inp("g2", [128, 512]); inp("k_k", [512]); inp("k_a", [512])
    inp("r_k", [512]); inp("lnx_g", [512]); inp("lnx_b", [512]); inp("w_rw_out", [512, D])
    inp("A_re", [32, 64]); inp("A_im", [32, 64]); inp("log_dt", [32]); inp("B_re", [32, 64, 16]); inp("B_im", [32, 64, 16])
    inp("C_re", [512, 64]); inp("C_im", [512, 64]); inp("D_skip", [512]); inp("w_glu", [512, 2048]); inp("w_out", [D, D])
    inp("ln2_g", [D]); inp("w_ffn_in", [D, 5632]); inp("conv_w", [3, 2816]); inp("conv_b", [2816]); inp("w_ffn_out", [2816, D])
    inp("ln3_g", [D]); inp("w_ple_gate", [D, D]); inp("w_ple", [256, D]); inp("final_g", [D])
    inp("c_ident", [128, 128]); inp("c_msl", [128, 128]); inp("c_msu", [128, 128]); inp("c_mui", [128, 128])
    inp("c_blk64", [128, 128]); inp("c_blk32", [128, 128]); inp("c_rowgp", [128, 128])
    outp("y", [NT, D]); outp("p_shift", [1792]); outp("p_wkv", [512, 64]); outp("p_re", [2048]); outp("p_im", [2048])
    outp("p_conv", [2, 2816]); outp("s_shift", [NS, 1792]); outp("s_wkv", [128, 4096]); outp("s_re", [NS, 2048])
    outp("s_im", [NS, 2048]); outp("s_conv", [NS, 2, 2816])
    scratch("PRW", [1792, NT], F32); scratch("UU", [512, NT], F32); scratch("GT", [2048, NT], BF16)
    scratch("YF", [512, NT], BF16); scratch("ZZ", [512, NT], BF16); scratch("X1", [NT, D], F32); scratch("X2", [NT, D], F32)
    scratch("SW", [6, NS, 512], F32); scratch("SY", [128, 64], F32)

    with contextlib.ExitStack() as gst:
        def gsb(name, shape, dt):
            return gst.enter_context(nc.sbuf_tensor("g_" + name, list(shape), dt))
        G0 = {}
        G0["identb"] = gsb("identb", [128, 128], BF16)
        G0["identf"] = gsb("identf", [128, 128], F32)
        with contextlib.ExitStack() as g2:
            def g2sb(name, shape, dt):
                return g2.enter_context(nc.sbuf_tensor("g_" + name, list(shape), dt))
            G0["BwT"] = g2sb("BwT", [128, 4, CS, 2, 128], BF16)
            G0["Kmat"] = g2sb("Kmat", [128, 4, CS, 128], BF16)
            G0["CwT"] = g2sb("CwT", [128, CS, 2, 16, 32], BF16)
            G0["Abar"] = g2sb("Abar", [128, 2, 2, 16], F32)
            if upto >= 1:
                phase1(nc, I, G0, debug)
            else:
                phase0(nc, I, G0, debug)
            if upto >= 2:
                phase2(nc, I, G0, True)
            if upto >= 2.5:
                phase2(nc, I, G0, False)
        g4 = contextlib.ExitStack()
        WFI = g4.enter_context(nc.sbuf_tensor("g_wfi", [128, 8, 5632], BF16))
        if upto >= 3:
            phase3(nc, I, G0, None, WFI)
        if upto >= 4:
            phase4(nc, I, G0, WFI)
        g4.close()
        if upto >= 5:
            phase5(nc, I, G0)
    return nc


def phase0(nc, I, G0, debug=False, ph=None):
    own = ph is None
    if own:
        ph = Ph(nc, "p0")
        ph.dma("pool", G0["identb"][:], I["c_ident"], W="identb")
        ph.dma("sp", G0["identf"][:], I["c_ident"], W="identf")
    sb = ph.sb
    lr = sb("lr", [128, 16], F32); li = sb("li", [128, 16], F32); dtl = sb("dtl", [128, 16], F32)
    Bre = sb("Bre", [128, 16, 16], F32); Bim = sb("Bim", [128, 16, 16], F32)
    ph.dma("sp", lr[:], I["A_re"].rearrange("(P gp) n -> (gp n) P", gp=2), W="lr", slow=True)
    ph.dma("sp", li[:], I["A_im"].rearrange("(P gp) n -> (gp n) P", gp=2), W="li", slow=True)
    ldt2 = I["log_dt"].rearrange("(P gp) -> gp P", gp=2)
    for gp in range(2):
        ph.dma("sp", dtl[64 * gp:64 * gp + 64, :], ldt2[gp].partition_broadcast(64), W="dtl", slow=True)
    ph.dma("sp", Bre[:], I["B_re"].rearrange("(P gp) n c -> (gp n) P c", gp=2), W="Bre")
    ph.dma("sp", Bim[:], I["B_im"].rearrange("(P gp) n c -> (gp n) P c", gp=2), W="Bim")
    rowgp = sb("rowgp", [128, 128], F32); blk32 = sb("blk32", [128, 128], F32)
    ph.dma("sp", rowgp[:], I["c_rowgp"], W="rowgp"); ph.dma("sp", blk32[:], I["c_blk32"], W="blk32")
    CT = [sb("CTr", [128, 4, 128], F32), sb("CTi", [128, 4, 128], F32)]
    c2 = sb("c2", [128, 128], F32)
    pA = ph.ps("pA", [128, 4, 128], F32)
    for ri, nm in enumerate(("C_re", "C_im")):
        for k in range(4):
            src = I[nm][k * 128:(k + 1) * 128, :]
            ph.dma("sp", c2[:, 0:64], src, W="c2"); ph.dma("sp", c2[:, 64:128], src, W="c2")
            ph.tt("dve", c2[:], c2[:], rowgp[:], ALU.mult, R=["c2", "rowgp"], W="c2")
            ph.tr(pA[:, k, :], c2[:], G0["identf"][:], R=["c2", "identf"], W="pA")
        ph.cp("dve", CT[ri][:], pA[:], R="pA", W="CT%d" % ri)
    t = {n: sb(n, [128, 16], F32) for n in ("dt", "e1", "mag", "ang", "sa", "ca", "sinv", "cosv", "ar", "ai", "den",
                                             "rden", "am1", "fr", "fi", "t1", "t2")}
    V = "dve"
    K = lambda *n: list(n)
    hpi = sb("hpi", [128, 1], F32)
    ph.memset(V, hpi[:], math.pi / 2, W="hpi")
    ph.act(t["dt"][:], dtl[:], AF.Exp, R="dtl", W="dt")
    ph.tt(V, t["e1"][:], lr[:], t["dt"][:], ALU.mult, R=K("lr", "dt"), W="e1")
    ph.act(t["mag"][:], t["e1"][:], AF.Exp, R="e1", W="mag")
    ph.tt(V, t["ang"][:], li[:], t["dt"][:], ALU.mult, R=K("li", "dt"), W="ang")
    ph.ts(V, t["sa"][:], t["ang"][:], 1.0 / 64, ALU.mult, R="ang", W="sa")
    ph.act(t["sinv"][:], t["sa"][:], AF.Sin, R="sa", W="sinv")
    ph.act(t["cosv"][:], t["sa"][:], AF.Sin, R=["sa", "hpi"], W="cosv", bias=hpi[:, 0:1])
    for _ in range(6):
        ph.tt(V, t["t1"][:], t["cosv"][:], t["cosv"][:], ALU.mult, R="cosv", W="t1")
        ph.tt(V, t["t2"][:], t["sinv"][:], t["sinv"][:], ALU.mult, R="sinv", W="t2")
        ph.stt(t["sinv"][:], t["cosv"][:], 2.0, t["sinv"][:], ALU.mult, ALU.mult, R=["cosv", "sinv", "t2"], W="sinv")
        ph.tt(V, t["cosv"][:], t["t1"][:], t["t2"][:], ALU.subtract, R=["t1", "t2", "sinv"], W="cosv")
    ph.tt(V, t["ar"][:], t["mag"][:], t["cosv"][:], ALU.mult, R=K("mag", "cosv"), W="ar")
    ph.tt(V, t["ai"][:], t["mag"][:], t["sinv"][:], ALU.mult, R=K("mag", "sinv"), W="ai")
    ph.tt(V, t["den"][:], lr[:], lr[:], ALU.mult, R="lr", W="den")
    ph.tt(V, t["t1"][:], li[:], li[:], ALU.mult, R="li", W="t1")
    ph.tt(V, t["den"][:], t["den"][:], t["t1"][:], ALU.add, R=K("den", "t1"), W="den")
    ph.op(V, lambda e: e.reciprocal(out=t["rden"][:], in_=t["den"][:]), R="den", W="rden")
    ph.ts(V, t["am1"][:], t["ar"][:], -1.0, ALU.add, R="ar", W="am1")
    ph.tt(V, t["t1"][:], t["am1"][:], lr[:], ALU.mult, R=K("am1", "lr", "den"), W="t1")
    ph.tt(V, t["t2"][:], t["ai"][:], li[:], ALU.mult, R=K("ai", "li"), W="t2")
    ph.tt(V, t["t1"][:], t["t1"][:], t["t2"][:], ALU.add, R=K("t1", "t2"), W="t1")
    ph.tt(V, t["fr"][:], t["t1"][:], t["rden"][:], ALU.mult, R=K("t1", "rden"), W="fr")
    ph.tt(V, t["t1"][:], t["ai"][:], lr[:], ALU.mult, R=K("ai", "lr", "fr"), W="t1")
    ph.tt(V, t["t2"][:], t["am1"][:], li[:], ALU.mult, R=K("am1", "li"), W="t2")
    ph.tt(V, t["t1"][:], t["t1"][:], t["t2"][:], ALU.subtract, R=K("t1", "t2"), W="t1")
    ph.tt(V, t["fi"][:], t["t1"][:], t["rden"][:], ALU.mult, R=K("t1", "rden"), W="fi")
    pwr = sb("pwr", [128, CS + 1, 16], F32); pwi = sb("pwi", [128, CS + 1, 16], F32)
    ph.memset(V, pwr[:, 0, :], 1.0, W="pw"); ph.memset(V, pwi[:, 0, :], 0.0, W="pw")
    for e in range(CS):
        ph.tt(V, t["t1"][:], pwr[:, e, :], t["ar"][:], ALU.mult, R=K("pw", "ar", "fi"), W="t1")
        ph.tt(V, t["t2"][:], pwi[:, e, :], t["ai"][:], ALU.mult, R=K("pw", "ai"), W="t2")
        ph.tt(V, pwr[:, e + 1, :], t["t1"][:], t["t2"][:], ALU.subtract, R=K("t1", "t2"), W="pw")
        ph.tt(V, t["t1"][:], pwr[:, e, :], t["ai"][:], ALU.mult, R=K("pw", "ai"), W="t1")
        ph.tt(V, t["t2"][:], pwi[:, e, :], t["ar"][:], ALU.mult, R=K("pw", "ar"), W="t2")
        ph.tt(V, pwi[:, e + 1, :], t["t1"][:], t["t2"][:], ALU.add, R=K("t1", "t2"), W="pw")
    Ab = G0["Abar"]
    ph.cp(V, Ab[:, 0, 0, :], pwr[:, CS, :], R="pw", W="Abar"); ph.cp(V, Ab[:, 0, 1, :], pwi[:, CS, :], R="pw", W="Abar")
    ph.cp(V, Ab[:, 1, 0, :], pwr[:, 1, :], R="pw", W="Abar"); ph.cp(V, Ab[:, 1, 1, :], pwi[:, 1, :], R="pw", W="Abar")
    bbr = sb("bbr", [128, 16, 16], F32); bbi = sb("bbi", [128, 16, 16], F32)
    u1 = sb("u1", [128, 16, 16], F32); u2 = sb("u2", [128, 16, 16], F32)
    frb = bc(t["fr"][:, :].unsqueeze(2), [128, 16, 16]); fib = bc(t["fi"][:, :].unsqueeze(2), [128, 16, 16])
    ph.tt(V, u1[:], Bre[:], frb, ALU.mult, R=K("Bre", "fr"), W="u1")
    ph.tt(V, u2[:], Bim[:], fib, ALU.mult, R=K("Bim", "fi"), W="u2")
    ph.tt(V, bbr[:], u1[:], u2[:], ALU.subtract, R=K("u1", "u2"), W="bbr")
    ph.tt(V, u1[:], Bim[:], frb, ALU.mult, R=K("Bim", "fr", "bbr"), W="u1")
    ph.tt(V, u2[:], Bre[:], fib, ALU.mult, R=K("Bre", "fi", "bbr"), W="u2")
    ph.tt(V, bbi[:], u1[:], u2[:], ALU.add, R=K("u1", "u2"), W="bbi")
    Ew = sb("Ew", [128, CS, 2, 16, 2, 16], F32)
    ph.memset(V, Ew[:].rearrange("p a b c d e -> p (a b c d e)"), 0.0, W="Ew")
    for e in range(CS):
        pr = bc(pwr[:, e, :].unsqueeze(2), [128, 16, 16]); pi = bc(pwi[:, e, :].unsqueeze(2), [128, 16, 16])
        ph.tt(V, u1[:], bbr[:], pr, ALU.mult, R=K("bbr", "pw", "Ew"), W="u1")
        ph.tt(V, u2[:], bbi[:], pi, ALU.mult, R=K("bbi", "pw", "Ew"), W="u2")
        ph.tt(V, u1[:], u1[:], u2[:], ALU.subtract, R=K("u1", "u2"), W="u1")
        for gp in range(2):
            ph.cp(V, Ew[64 * gp:64 * gp + 64, e, 0, :, gp, :], u1[64 * gp:64 * gp + 64, :, :], R="u1", W="Ew")
        ph.tt(V, u1[:], bbr[:], pi, ALU.mult, R=K("bbr", "pw", "Ew"), W="u1")
        ph.tt(V, u2[:], bbi[:], pr, ALU.mult, R=K("bbi", "pw", "Ew"), W="u2")
        ph.tt(V, u1[:], u1[:], u2[:], ALU.add, R=K("u1", "u2"), W="u1")
        for gp in range(2):
            ph.cp(V, Ew[64 * gp:64 * gp + 64, e, 1, :, gp, :], u1[64 * gp:64 * gp + 64, :, :], R="u1", W="Ew")
    CTin = sb("CTin", [128, 4, 128], F32)
    ph.ts(V, CTin[:], CT[1][:], -1.0, ALU.mult, R="CT1", W="CTin")
    pB = [ph.ps("pB%d" % i, [128, 4, 128], F32) for i in range(2)]
    n = 0
    for j in range(CS):
        e = CS - 1 - j
        for ri in range(2):
            pb = pB[n % 2]; n += 1
            for k in range(4):
                src = Ew[:, e, ri, 4 * k:4 * k + 4, :, :].rearrange("p a b c -> p (a b c)")
                ph.tr(pb[:, k, :], src, G0["identf"][:], R=["Ew", "identf"], W="pB%d" % ((n - 1) % 2))
            ph.cp("act" if n % 2 else "dve", G0["BwT"][:, :, j, ri, :], pb[:], R="pB%d" % ((n - 1) % 2), W="BwT")
    for tau in range(CS):
        pb = pB[n % 2]; key = "pB%d" % (n % 2); n += 1
        for k in range(4):
            lr_ = Ew[:, tau, 0, 4 * k:4 * k + 4, :, :].rearrange("p a b c -> p (a b c)")
            li_ = Ew[:, tau, 1, 4 * k:4 * k + 4, :, :].rearrange("p a b c -> p (a b c)")
            ph.mm(pb[:, k, :], lr_, CT[0][:, k, :], True, False, R=["Ew", "CT0"], W=key)
            ph.mm(pb[:, k, :], li_, CTin[:, k, :], False, True, R=["Ew", "CTin"], W=key)
        ph.tt(V, G0["Kmat"][:, :, tau, :], pb[:], bc(blk32[:, :].unsqueeze(1), [128, 4, 128]), ALU.mult,
              R=[key, "blk32"], W="Kmat")
    w1 = sb("w1", [128, 16, 32], F32); w2_ = sb("w2", [128, 16, 32], F32)
    CTr3 = CT[0][:].rearrange("p k (a b) -> p (k a) b", a=4); CTi3 = CT[1][:].rearrange("p k (a b) -> p (k a) b", a=4)
    for i in range(CS):
        pr = bc(pwr[:, i + 1, :].unsqueeze(2), [128, 16, 32]); pi = bc(pwi[:, i + 1, :].unsqueeze(2), [128, 16, 32])
        ph.tt(V, w1[:], CTr3, pr, ALU.mult, R=K("CT0", "pw", "CwT"), W="w1")
        ph.tt(V, w2_[:], CTi3, pi, ALU.mult, R=K("CT1", "pw", "CwT"), W="w2")
        ph.tt(V, G0["CwT"][:, i, 0, :, :], w1[:], w2_[:], ALU.subtract, R=K("w1", "w2"), W="CwT")
        ph.tt(V, w1[:], CTr3, pi, ALU.mult, R=K("CT0", "pw", "CwT"), W="w1")
        ph.tt(V, w2_[:], CTi3, pr, ALU.mult, R=K("CT1", "pw", "CwT"), W="w2")
        ph.tt(V, w1[:], w1[:], w2_[:], ALU.add, R=K("w1", "w2"), W="w1")
        ph.ts(V, G0["CwT"][:, i, 1, :, :], w1[:], -1.0, ALU.mult, R="w1", W="CwT")
    if debug:
        ph.dma("sp", I["d_BwT"], G0["BwT"][:].rearrange("p a b c d -> p (a b c d)"), R="BwT")
        ph.dma("sp", I["d_Kmat"], G0["Kmat"][:].rearrange("p a b c -> p (a b c)"), R="Kmat")
        ph.dma("sp", I["d_CwT"], G0["CwT"][:].rearrange("p a b c d -> p (a b c d)"), R="CwT")
        ph.dma("sp", I["d_Abar"], G0["Abar"][:].rearrange("p a b c -> p (a b c)"), R="Abar")
    if own:
        ph.finish()


def phase1(nc, I, G0, debug=False):
    ph = Ph(nc, "p1")
    win = ph.sb("win", [128, 8, 4352], BF16)
    for k in range(8):
        ph.dma("pool", win[:, k, :], I["w_in"][k * 128:(k + 1) * 128, :], W="win%d" % k)
    ph.dma("pool", G0["identb"][:], I["c_ident"], W="identb")
    ph.dma("sp", G0["identf"][:], I["c_ident"], W="identf")
    ph.rec_begin()
    phase0(nc, I, G0, debug, ph=ph)
    s0 = ph.rec_end()
    ph.rec_begin()
    G = norm_scratch(ph, G0)
    g1c = ph.sb("g1c", [128, 8], F32)
    load_col(ph, g1c[:], I["ln1_g"], 8, "g1c")
    hTs = [ph.sb("hT%d" % i, [128, 8, 512], BF16) for i in range(2)]
    xts = [ph.sb("xt%d" % i, [128, D], F32) for i in range(2)]
    pm = [ph.ps("pm%d" % i, [128, 512], F32) for i in range(4)]
    stf = [ph.sb("stf%d" % i, [128, 512], F32) for i in range(4)]
    stb = [ph.sb("stb%d" % i, [128, 512], BF16) for i in range(3)]
    WK = ["win%d" % k for k in range(8)]
    nx = nf = nb = npm = 0
    pre1 = ph.rec_end()
    NR1, MM1 = [], []
    for bi_, (t0, nt) in enumerate(BLOCKS):
        P = min(128, nt)
        hT = hTs[bi_ % 2]; hk = "hT%d" % (bi_ % 2)
        ph.rec_begin()
        for s in range((nt + 127) // 128):
            xt = xts[nx % 2]; tg = str(nx % 2); nx += 1
            ph.dma("sp", xt[:P, :], I["xall"][t0 + s * 128:t0 + s * 128 + P, :], W="xt" + tg)
            rms_to_hT(ph, G, xt, P, g1c, hT, s * 128, tg, "g1c", hk)
        NR1.append(ph.rec_end())
        ph.rec_begin()
        for m in range(34):
            pb = pm[npm % 4]; pk = "pm%d" % (npm % 4); npm += 1
            for k in range(8):
                ph.mm(pb[:, :nt], win[:, k, m * 128:(m + 1) * 128], hT[:, k, :nt], k == 0, k == 7,
                      R=["win%d" % k, hk], W=pk)
            if m < 18:
                sf = stf[nf % 4]; sk = "stf%d" % (nf % 4); nf += 1
                ph.cp("dve" if m % 2 else "act", sf[:, :nt], pb[:, :nt], R=pk, W=sk)
                if m < 14:
                    ph.dma("pool", I["PRW"][m * 128:(m + 1) * 128, t0:t0 + nt], sf[:, :nt], R=sk)
                else:
                    ph.dma("pool", I["UU"][(m - 14) * 128:(m - 13) * 128, t0:t0 + nt], sf[:, :nt], R=sk)
            else:
                sbf = stb[nb % 3]; sk = "stb%d" % (nb % 3); nb += 1
                ph.act(sbf[:, :nt], pb[:, :nt], AF.Sigmoid, R=pk, W=sk)
                ph.dma("act", I["GT"][(m - 18) * 128:(m - 17) * 128, t0:t0 + nt], sbf[:, :nt], R=sk)
        MM1.append(ph.rec_end())
    s1 = pre1 + NR1[0]
    for b_ in range(len(BLOCKS)):
        if STRICT1:
            s1 = s1 + (NR1[b_ + 1] if b_ + 1 < len(BLOCKS) else []) + MM1[b_]
        else:
            s1 = s1 + ph.merge(MM1[b_], NR1[b_ + 1] if b_ + 1 < len(BLOCKS) else [])
    ph.play(s1, s0)
    ph.finish()


def alloc_w3(nc, st):
    t = lambda n, shp: st.enter_context(nc.sbuf_tensor("w3_" + n, shp, BF16))
    return {"rwo": t("rwo", [128, 4, D]), "glu": t("glu", [128, 4, 2048]), "wo": t("wo", [128, 8, D])}


def load_w3(ph, I, W3):
    for k in range(4):
        ph.dma("pool", W3["rwo"][:, k, :], I["w_rw_out"][k * 128:(k + 1) * 128, :], W="rwo")
        ph.dma("pool", W3["glu"][:, k, :], I["w_glu"][k * 128:(k + 1) * 128, :], W="glu")
    for k in range(8):
        ph.dma("pool", W3["wo"][:, k, :], I["w_out"][k * 128:(k + 1) * 128, :], W="wo")


def phase2(nc, I, G0, prompt, W3=None):
    ph = Ph(nc, "p2a" if prompt else "p2b")
    sb, ps = ph.sb, ph.ps
    V = "dve"
    if W3 is not None:
        load_w3(ph, I, W3)
    ph._s5tmp = [sb("s5a", [128, 2, 16], F32), sb("s5b", [128, 2, 16], F32)]
    ph._s5xb = sb("Xb", [128, 2, 16, 64], BF16)
    ph._s5du = sb("s5du", [128, 512], F32)
    if prompt:
        msl = sb("msl", [128, 128], BF16); msu = sb("msu", [128, 128], BF16); mui = sb("mui", [128, 128], BF16)
        ph.dma("pool", msl[:], I["c_msl"], W="msl"); ph.dma("pool", msu[:], I["c_msu"], W="msu")
        ph.dma("pool", mui[:], I["c_mui"], W="mui")
    blk64 = sb("blk64", [128, 128], F32); ph.dma("sp", blk64[:], I["c_blk64"], W="blk64")
    w2a2 = sb("w2a2", [128, 512], BF16); g2b = sb("g2b", [128, 512], BF16)
    ph.dma("pool", w2a2[0:64, :], I["w2"], W="w2a2"); ph.dma("pool", w2a2[64:128, :], I["a2"], W="w2a2")
    ph.dma("pool", g2b[:], I["g2"], W="g2b")
    pc = {}
    for nm, n in (("mu_shift", 14), ("w0", 4), ("a0", 4), ("k_k", 4), ("k_a", 4), ("r_k", 4), ("lnx_g", 4),
                  ("lnx_b", 4), ("D_skip", 4)):
        pc[nm] = sb("c_" + nm, [128, n], F32)
        load_col(ph, pc[nm][:], I[nm], n, "c_" + nm)
    PK = ["c_" + k for k in pc]
    scm = sb("scm", [128, 4, 128], F32)
    ph.memset(V, scm[:].rearrange("p a b -> p (a b)"), 1.0, W="scm"); ph.memset(V, scm[:, :, 0:1], 0.0, W="scm")
    eps_gn = sb("eps_gn", [128, 1], F32); ph.memset(V, eps_gn[:], 64e-5, W="eps_gn")
    if prompt:
        Sst = sb("Sst", [128, 4, 64], F32); Sbd = sb("Sbd", [128, 4, 128], BF16)
        ph.memset(V, Sst[:].rearrange("p a b -> p (a b)"), 0.0, W="Sst")
        ph.memset(V, Sbd[:].rearrange("p a b -> p (a b)"), 0.0, W="Sbd")
        Xs = sb("Xs", [128, 2, 16, 65], F32)
        ph.memset(V, Xs[:].rearrange("p a b c -> p (a b c)"), 0.0, W="Xs")
        Pf = sb("Pf", [128, 14, 513], F32)
        ph.memset(V, Pf[:, :, 0:1], 0.0, W="Pf")
    WB = 512 if prompt else NS
    WC = 128 if prompt else NS
    uf = sb("uf", [128, 4, WB], F32); ub = sb("ub", [128, 4, WB], BF16)
    YFb = sb("YFb", [128, 4, WB], BF16); ZZb = sb("ZZb", [128, 4, WB], BF16)
    f4 = lambda n: sb(n, [128, 4, WC], F32)
    b4 = lambda n: sb(n, [128, 4, WC], BF16)
    XS = sb("XS", [128, 14, WC], F32); dd = sb("dd", [128, 14, WC], F32)
    lin = sb("lin", [128, WC], BF16); sgx = sb("sgx", [128, WC], BF16)
    sig = f4("sig"); aa = f4("aa"); gg = f4("gg"); kk0 = f4("kk0"); tq = f4("tq"); rn = f4("rn"); kkn = f4("kkn")
    bb = f4("bb"); kmod = f4("kmod"); bon = f4("bon"); cs = f4("cs"); ex1 = f4("ex1"); ex2 = f4("ex2"); ex3 = f4("ex3")
    nbias = sb("nbias", [128, 4], F32); PCt = sb("PCt", [128, 4], F32)
    gns = f4("gns")
    KX = {n_: n_ for n_ in ("rT", "kT", "bT", "aT", "khT", "bhT", "vT", "PCt", "bon", "gg")}
    if prompt:
        rT = b4("rT"); kT = b4("kT"); bT = b4("bT"); aT = b4("aT"); khT = b4("khT"); bhT = b4("bhT"); vT = b4("vT")
        alt = {"rT": b4("rT1"), "kT": b4("kT1"), "bT": b4("bT1"), "aT": b4("aT1"), "khT": b4("khT1"),
               "bhT": b4("bhT1"), "vT": b4("vT1"), "PCt": sb("PCt1", [128, 4], F32), "bon": f4("bon1"), "gg": f4("gg1")}
        Vtok = sb("Vtok", [128, 512], BF16); Khtok = sb("Khtok", [128, 512], BF16); Bhtok = sb("Bhtok", [128, 512], BF16)
        h8 = lambda n: sb(n, [128, 8, 128], BF16)
        Nb = [h8("Nb0"), h8("Nb1")]; Lb = [h8("Lb0"), h8("Lb1")]; Mt = [h8("Mt0"), h8("Mt1")]
        LKb = h8("LKb"); Arb = h8("Arb"); Ark = h8("Ark")
        Wbf = sb("Wbf", [128, 512], BF16); Ubf = sb("Ubf", [128, 512], BF16)
        tS = sb("tS", [128, 4, 64], F32)
    Ysb = sb("Ysb", [128, 8, 64], F32); Ysq = sb("Ysq", [128, 8, 64], F32); ynb = sb("ynb", [128, 8, 64], BF16)
    gn = sb("gn", [128, 6, 8], F32)
    pF = [ps("pF%d" % i, [128, 4, 128], F32) for i in range(6)]
    pT = [ps("pTb%d" % i, [128, 8, 128], BF16) for i in range(2)]
    cnt = {"f": 0, "t": 0}

    def getF():
        i = cnt["f"] % 6; cnt["f"] += 1
        return pF[i], "pF%d" % i

    def mkpool(base):
        st_ = {"n": 0}

        def get():
            i = base + st_["n"] % 2; st_["n"] += 1
            return pF[i], "pF%d" % i
        return get
    getF_prep, getF_core, getFs = mkpool(0), mkpool(2), mkpool(4)

    def getT():
        i = cnt["t"] % 2; cnt["t"] += 1
        return pT[i], "pTb%d" % i

    ib = G0["identb"]

    if not prompt:
        sample_mixer(ph, I, G0, locals())
        ph.finish()
        return
    Lbase = dict(locals())
    Lpar = [dict(Lbase), dict(Lbase)]
    Lpar[1].update(alt)
    Lpar[1]["KX"] = {n_: n_ + "1" for n_ in KX}
    REC = []
    for bi, (t0, nt) in enumerate(BLOCKS[:4]):
        ph.rec_begin()
        if bi > 0:
            ph.cp(V, Pf[:, :, 0:1], Pf[:, :, 512:513], R="Pf", W="Pf")
        ph.dma("sp", Pf[:, :, 1:513], I["PRW"][:, t0:t0 + nt].rearrange("(m p) t -> p m t", p=128), W="Pf")
        if bi == 3:
            ph.dma("sp", I["p_shift"].rearrange("(m p) -> p m", p=128), Pf[:, :, 512], R="Pf", slow=True)
        hdr_pf = ph.rec_end()
        ph.rec_begin()
        ph.dma("act", uf[:], I["UU"][:, t0:t0 + nt].rearrange("(m p) t -> p m t", p=128), W="uf")
        ph.cp("act", ub[:].rearrange("p a b -> p (a b)"), uf[:].rearrange("p a b -> p (a b)"), R="uf", W="ub")
        hdr_ub = ph.rec_end()
        ph.rec_begin()
        s5_block(ph, I, G0, pc, Xs, ub, ZZb, getFs, nchunk=64, which=0, ncol=512)
        ph.dma("act", I["ZZ"][:, t0:t0 + nt].rearrange("(m p) t -> p m t", p=128), ZZb[:], R="ZZb")
        s5s = ph.rec_end()
        m0, m1 = ph._s5marks
        preps, cores = [], []
        for c in range(4):
            c0 = c * 128
            Lc = dict(Lpar[c % 2]); Lc["getF"] = getF_prep
            Lk = dict(Lpar[c % 2]); Lk["getF"] = getF_core
            ph.rec_begin()
            ph.tt(V, dd[:], Pf[:, :, c0:c0 + 128], Pf[:, :, c0 + 1:c0 + 129], ALU.subtract, R="Pf", W="dd")
            ph.tt(V, dd[:], dd[:], bc(pc["mu_shift"][:, :].unsqueeze(2), [128, 14, 128]), ALU.mult,
                  R=["dd", "c_mu_shift"], W="dd")
            ph.tt(V, XS[:], dd[:], Pf[:, :, c0 + 1:c0 + 129], ALU.add, R=["dd", "Pf"], W="XS")
            rwkv_prep_and_core(ph, Lc, c, c0)
            preps.append(ph.rec_end())
            ph.rec_begin()
            wkv_core(ph, Lk, c, c0)
            cores.append(ph.rec_end())
        ph.rec_begin()
        ph.dma("pool", I["YF"][:, t0:t0 + nt].rearrange("(m p) t -> p m t", p=128), YFb[:], R="YFb")
        yfst = ph.rec_end()
        hs = (m1 - m0) // 2
        REC.append(dict(hdr_pf=hdr_pf, hdr_ub=hdr_ub, SG=s5s[:m0], SS1=s5s[m0:m0 + hs], SS2=s5s[m0 + hs:m1],
                        SY=s5s[m1:], preps=preps, cores=cores, yfst=yfst))
    ph.play(REC[0]["hdr_pf"])
    ph.play(REC[0]["preps"][0])
    for bi in range(4):
        Rb = REC[bi]
        ph.play(Rb["hdr_ub"])
        ph.play(Rb["cores"][0], Rb["preps"][1], Rb["SG"])
        ph.play(Rb["cores"][1], Rb["preps"][2], Rb["SS1"])
        ph.play(Rb["cores"][2], Rb["preps"][3], Rb["SS2"])
        if bi < 3:
            ph.play(REC[bi + 1]["hdr_pf"])
            ph.play(Rb["cores"][3], Rb["SY"], REC[bi + 1]["preps"][0], spans=[(0.0, 1.0), (SYO, 1.0 - SYO), (0.0, 1.0)])
        else:
            ph.play(Rb["cores"][3], Rb["SY"], spans=[(0.0, 1.0), (SYO, 1.0 - SYO)])
        ph.play(Rb["yfst"])
    ph.dma("sp", I["p_wkv"].rearrange("(m p) v -> p m v", p=128), Sst[:], R="Sst")
    ph.dma("sp", I["p_re"].rearrange("(P p) -> p P", p=128), Xs[:, 0, :, 0], R="Xs", slow=True)
    ph.dma("sp", I["p_im"].rearrange("(P p) -> p P", p=128), Xs[:, 1, :, 0], R="Xs", slow=True)
    ph.finish()


def rwkv_prep_and_core(ph, L, c, c0):
    V = "dve"
    PV = L.get("PV", "dve")
    KX = L["KX"]
    pc = L["pc"]; XS = L["XS"]; getF = L["getF"]; getT = L["getT"]; ib = L["ib"]
    sig, aa, gg, kk0, tq, rn, kkn = L["sig"], L["aa"], L["gg"], L["kk0"], L["tq"], L["rn"], L["kkn"]
    bb, kmod, bon, cs, ex1, ex2, ex3 = L["bb"], L["kmod"], L["bon"], L["cs"], L["ex1"], L["ex2"], L["ex3"]
    rT, kT, bT, aT, khT, bhT, vT = L["rT"], L["kT"], L["bT"], L["aT"], L["khT"], L["bhT"], L["vT"]
    lin, sgx, w2a2, g2b, blk64 = L["lin"], L["sgx"], L["w2a2"], L["g2b"], L["blk64"]
    nbias, PCt, scm = L["nbias"], L["PCt"], L["scm"]
    r_ = XS[:, 0:4, :]; k_ = XS[:, 4:8, :]; v_ = XS[:, 8:12, :]
    B4 = lambda t: bc(t[:, :].unsqueeze(2), [128, 4, 128])
    fl = lambda t: t[:].rearrange("p a b -> p (a b)")
    ph.act(lin[0:64, :], XS[0:64, 12, :], AF.Tanh, R="XS", W="lin")
    ph.cp("act", lin[64:128, :], XS[64:128, 12, :], R="XS", W="lin")
    ph.act(sgx[:], XS[:, 13, :], AF.Sigmoid, R="XS", W="sgx")
    pw_, kw_ = getF()
    for m in range(4):
        ph.mm(pw_[:, m, :], w2a2[0:64, m * 128:(m + 1) * 128], lin[0:64, :], True, True, R=["w2a2", "lin"], W=kw_)
    for m in range(4):
        ph.act(sig[:, m, :], pw_[:, m, :], AF.Sigmoid, R=[kw_, "c_w0"], W="sig", bias=pc["w0"][:, m:m + 1])
    pa_, ka_ = getF()
    for m in range(4):
        ph.mm(pa_[:, m, :], w2a2[64:128, m * 128:(m + 1) * 128], lin[64:128, :], True, True, R=["w2a2", "lin"], W=ka_)
    for m in range(4):
        ph.act(aa[:, m, :], pa_[:, m, :], AF.Sigmoid, R=[ka_, "c_a0"], W="aa", bias=pc["a0"][:, m:m + 1])
    pg_, kg_ = getF()
    for m in range(4):
        ph.mm(pg_[:, m, :], g2b[:, m * 128:(m + 1) * 128], sgx[:], True, True, R=["g2b", "sgx"], W=kg_)
    ph.cp("act", gg[:], pg_[:], R=kg_, W=KX["gg"])
    ph.tt(PV, kk0[:], k_, B4(pc["k_k"]), ALU.mult, R=["XS", "c_k_k"], W="kk0")
    ph.tt(PV, tq[:], kk0[:], kk0[:], ALU.mult, R="kk0", W="tq")
    pq, kq = getF()
    for m in range(4):
        ph.mm(pq[:, m, :], blk64[:], tq[:, m, :], True, True, R=["blk64", "tq"], W=kq)
    ph.act(rn[:], pq[:], AF.Sqrt, R=kq, W="rn")
    ph.ts(V, rn[:], rn[:], 1e-12, ALU.max, R="rn", W="rn")
    ph.op(V, lambda e: e.reciprocal(out=fl(rn), in_=fl(rn)), R="rn", W="rn")
    ph.tt(PV, kkn[:], kk0[:], rn[:], ALU.mult, R=["kk0", "rn"], W="kkn")
    ph.tt(PV, bb[:], kkn[:], aa[:], ALU.mult, R=["kkn", "aa"], W="bb")
    ph.tt(PV, tq[:], aa[:], B4(pc["k_a"]), ALU.mult, R=["aa", "c_k_a", kq], W="tq")
    ph.tt(PV, tq[:], tq[:], B4(pc["k_a"]), ALU.subtract, R=["tq", "c_k_a"], W="tq")
    ph.stt(kmod[:], tq[:], 1.0, k_, ALU.add, ALU.mult, R=["tq", "XS"], W="kmod")
    ph.tt(PV, tq[:], r_, kmod[:], ALU.mult, R=["XS", "kmod"], W="tq")
    ph.tt(PV, tq[:], tq[:], B4(pc["r_k"]), ALU.mult, R=["tq", "c_r_k"], W="tq")
    pq2, kq2 = getF()
    for m in range(4):
        ph.mm(pq2[:, m, :], blk64[:], tq[:, m, :], True, True, R=["blk64", "tq"], W=kq2)
    ph.tt(V, bon[:], pq2[:], v_, ALU.mult, R=[kq2, "XS"], W=KX["bon"])
    ph.op(V, lambda e: e.tensor_tensor_scan(out=fl(cs), data0=fl(scm), data1=fl(sig), initial=0.0, op0=ALU.mult,
                                             op1=ALU.add), R=["scm", "sig"], W="cs")
    ph.ts(V, nbias[:], cs[:, :, 127], -C1, ALU.mult, R="cs", W="nbias")
    ph.act(PCt[:], nbias[:], AF.Exp, R="nbias", W=KX["PCt"])
    ph.act(ex1[:], cs[:], AF.Exp, R="cs", W="ex1", scale=-C1)
    ph.tt(PV, rT[:], r_, ex1[:], ALU.mult, R=["XS", "ex1"], W=KX["rT"])
    ph.act(ex2[:], cs[:], AF.Exp, R="cs", W="ex2", scale=C1)
    ph.tt(PV, kT[:], kmod[:], ex2[:], ALU.mult, R=["kmod", "ex2"], W=KX["kT"])
    ph.tt(PV, bT[:], bb[:], ex2[:], ALU.mult, R=["bb", "ex2"], W=KX["bT"])
    ph.tt(PV, ex3[:], cs[:], sig[:], ALU.subtract, R=["cs", "sig"], W="ex3")
    ph.act(ex3[:], ex3[:], AF.Exp, R="ex3", W="ex3", scale=-C1)
    ph.stt(aT[:], kkn[:], -1.0, ex3[:], ALU.mult, ALU.mult, R=["kkn", "ex3"], W=KX["aT"])
    for m in range(4):
        ph.act(ex1[:, m, :], cs[:, m, :], AF.Exp, R=["cs", "nbias", KX["rT"]], W="ex1", bias=nbias[:, m:m + 1], scale=C1)
    ph.tt(PV, khT[:], kmod[:], ex1[:], ALU.mult, R=["kmod", "ex1"], W=KX["khT"])
    ph.tt(PV, bhT[:], bb[:], ex1[:], ALU.mult, R=["bb", "ex1"], W=KX["bhT"])
    ph.cp("act", vT[:], v_, R="XS", W=KX["vT"])


def wkv_core(ph, L, c, c0):
    V = "dve"
    KX = L["KX"]
    getF = L["getF"]; getT = L["getT"]; ib = L["ib"]
    rT, kT, bT, aT, khT, bhT, vT = L["rT"], L["kT"], L["bT"], L["aT"], L["khT"], L["bhT"], L["vT"]
    Vtok, Khtok, Bhtok = L["Vtok"], L["Khtok"], L["Bhtok"]
    Nb, Lb, Mt, LKb, Arb, Ark = L["Nb"], L["Lb"], L["Mt"], L["LKb"], L["Arb"], L["Ark"]
    msl, msu, mui = L["msl"], L["msu"], L["mui"]
    Wbf, Ubf, Ysb, Ysq, ynb, gn = L["Wbf"], L["Ubf"], L["Ysb"], L["Ysq"], L["ynb"], L["gn"]
    Sst, Sbd, PCt, tS = L["Sst"], L["Sbd"], L["PCt"], L["tS"]
    pc = L["pc"]; bon, gg, YFb = L["bon"], L["gg"], L["YFb"]
    M4 = lambda m_: bc(m_[:, :].unsqueeze(1), [128, 4, 128])
    pt, kt = getT()
    for m in range(4):
        ph.tr(pt[:, m, :], vT[:, m, :], ib[:], R=KX["vT"], W=kt)
    for m in range(4):
        ph.tr(pt[:, 4 + m, :], khT[:, m, :], ib[:], R=KX["khT"], W=kt)
    ph.cp("act", Vtok[:], pt[:, 0:4, :].rearrange("p a b -> p (a b)"), R=kt, W="Vtok")
    ph.cp(V, Khtok[:], pt[:, 4:8, :].rearrange("p a b -> p (a b)"), R=kt, W="Khtok")
    pt2, kt2 = getT()
    for m in range(4):
        ph.tr(pt2[:, m, :], bhT[:, m, :], ib[:], R=KX["bhT"], W=kt2)
    ph.cp("act", Bhtok[:], pt2[:, 0:4, :].rearrange("p a b -> p (a b)"), R=kt2, W="Bhtok")

    def hsl(t, h):
        return t[64 * (h % 2):64 * (h % 2) + 64, h // 2, :]

    def amat(dst, dkey, lhs, lkey, rhs, rkey, mask, mkey):
        for par in range(2):
            pb, pk = getF()
            for q in range(4):
                h = 2 * q + par
                ph.mm(pb[:, q, :], hsl(lhs, h), hsl(rhs, h), True, True, R=[lkey, rkey], W=pk)
            ph.tt(V, dst[:, par:8:2, :], pb[:], M4(mask), ALU.mult, R=[pk, mkey], W=dkey)

    amat(Nb[0], "Nb0", aT, KX["aT"], bT, KX["bT"], msl, "msl")
    amat(Lb[0], "Lb0", bT, KX["bT"], aT, KX["aT"], msu, "msu")
    amat(LKb, "LKb", kT, KX["kT"], aT, KX["aT"], msu, "msu")
    amat(Arb, "Arb", bT, KX["bT"], rT, KX["rT"], mui, "mui")
    amat(Ark, "Ark", kT, KX["kT"], rT, KX["rT"], mui, "mui")
    for half in range(2):
        ph.tt(V, Mt[0][:, half * 4:half * 4 + 4, :], Lb[0][:, half * 4:half * 4 + 4, :], M4(ib), ALU.add,
              R=["Lb0", "identb"], W="Mt0")
    cur = 0
    for lvl in range(6):
        nxt = 1 - cur
        for half in range(2):
            pb, pk = getF()
            for q in range(4):
                h = half * 4 + q
                ph.mm(pb[:, q, :], Lb[cur][:, h, :], Nb[cur][:, h, :], True, True, R=["Lb%d" % cur, "Nb%d" % cur], W=pk)
            ph.cp("act", Nb[nxt][:, half * 4:half * 4 + 4, :], pb[:], R=pk, W="Nb%d" % nxt)
        if BUB2:
            ph.bubble(BUB2)
        if lvl < 5:
            for half in range(2):
                pb, pk = getF()
                for q in range(4):
                    h = half * 4 + q
                    ph.mm(pb[:, q, :], Nb[cur][:, h, :], Lb[cur][:, h, :], True, True,
                          R=["Lb%d" % cur, "Nb%d" % cur], W=pk)
                ph.cp("act", Lb[nxt][:, half * 4:half * 4 + 4, :], pb[:], R=pk, W="Lb%d" % nxt)
        for half in range(2):
            pb, pk = getF()
            for q in range(4):
                h = half * 4 + q
                ph.mm(pb[:, q, :], Nb[nxt][:, h, :], Mt[cur][:, h, :], True, True, R=["Nb%d" % nxt, "Mt%d" % cur], W=pk)
            ph.tt(V, Mt[nxt][:, half * 4:half * 4 + 4, :], pb[:], Mt[cur][:, half * 4:half * 4 + 4, :], ALU.add,
                  R=[pk, "Mt%d" % cur], W="Mt%d" % nxt)
        cur = nxt
    MtF = Mt[cur]; mk = "Mt%d" % cur
    def hcols(pb, h):
        return pb[:].rearrange("p a b -> p (a b)")[:, h * 64:h * 64 + 64]

    def pcols(pb, m):
        return pb[:].rearrange("p a b -> p (a b)")[:, m * 128:m * 128 + 128]

    pb, pk = getF()
    for m in range(4):
        ph.mm(pcols(pb, m), aT[:, m, :], Sbd[:, m, :], True, False, R=[KX["aT"], "Sbd"], W=pk)
        for hh in range(2):
            h = 2 * m + hh
            ph.mm(hcols(pb, h), LKb[:, h, :], Vtok[:, h * 64:h * 64 + 64], False, hh == 1, R=["LKb", "Vtok"], W=pk)
    ph.cp("act", Wbf[:], pb[:].rearrange("p a b -> p (a b)"), R=pk, W="Wbf")
    ph.bubble(BUB)
    pb, pk = getF()
    for h in range(8):
        ph.mm(hcols(pb, h), MtF[:, h, :], Wbf[:, h * 64:h * 64 + 64], True, True, R=[mk, "Wbf"], W=pk)
    ph.cp("act", Ubf[:], pb[:].rearrange("p a b -> p (a b)"), R=pk, W="Ubf")
    ph.bubble(BUB)
    pb, pk = getF()
    for m in range(4):
        ph.mm(pcols(pb, m), rT[:, m, :], Sbd[:, m, :], True, False, R=[KX["rT"], "Sbd"], W=pk)
        for hh in range(2):
            h = 2 * m + hh
            ph.mm(hcols(pb, h), Arb[:, h, :], Ubf[:, h * 64:h * 64 + 64], False, False, R=["Arb", "Ubf"], W=pk)
            ph.mm(hcols(pb, h), Ark[:, h, :], Vtok[:, h * 64:h * 64 + 64], False, hh == 1, R=["Ark", "Vtok"], W=pk)
    ph.cp("act", Ysb[:].rearrange("p a b -> p (a b)"), pb[:].rearrange("p a b -> p (a b)"), R=pk, W="Ysb")
    pS, kS = getF()
    for m in range(4):
        ph.mm(pS[:, m, :], Bhtok[:, m * 128:(m + 1) * 128], Ubf[:, m * 128:(m + 1) * 128], True, False,
              R=["Bhtok", "Ubf"], W=kS)
        ph.mm(pS[:, m, :], Khtok[:, m * 128:(m + 1) * 128], Vtok[:, m * 128:(m + 1) * 128], False, True,
              R=["Khtok", "Vtok"], W=kS)
    ph.tt(V, tS[:], Sst[:], bc(PCt[:, :].unsqueeze(2), [128, 4, 64]), ALU.mult, R=["Sst", KX["PCt"]], W="tS")
    for hh in range(2):
        rs = slice(64 * hh, 64 * hh + 64)
        ph.tt(V, Sst[rs, :, :], tS[rs, :, :], pS[rs, :, 64 * hh:64 * hh + 64], ALU.add, R=["tS", kS], W="Sst")
        ph.cp(V, Sbd[rs, :, 64 * hh:64 * hh + 64], Sst[rs, :, :], R="Sst", W="Sbd")
    groupnorm_out(ph, L, c0, 128)


def groupnorm_out(ph, L, c0, P):
    V = "dve"
    KX = L["KX"]
    Ysb, Ysq, ynb, gn = L["Ysb"], L["Ysq"], L["ynb"], L["gn"]
    pc = L["pc"]; bon, gg, YFb = L["bon"], L["gg"], L["YFb"]; getT = L["getT"]; ib = L["ib"]
    eps_gn = L["eps_gn"]; ex2 = L["gns"]
    ph.op(V, lambda e: e.tensor_reduce(out=gn[:P, 0, :], in_=Ysb[:P], axis=AX.X, op=ALU.add), R="Ysb", W="gn")
    ph.act(Ysq[:P].rearrange("p a b -> p (a b)"), Ysb[:P].rearrange("p a b -> p (a b)"), AF.Square, R="Ysb", W="Ysq")
    ph.op(V, lambda e: e.tensor_reduce(out=gn[:P, 1, :], in_=Ysq[:P], axis=AX.X, op=ALU.add), R="Ysq", W="gn")
    ph.ts(V, gn[:P, 2, :], gn[:P, 0, :], 1.0 / 64, ALU.mult, R="gn", W="gn")
    ph.tt(V, gn[:P, 3, :], gn[:P, 2, :], gn[:P, 2, :], ALU.mult, R="gn", W="gn")
    ph.stt(gn[:P, 4, :], gn[:P, 1, :], 1.0 / 64, gn[:P, 3, :], ALU.mult, ALU.subtract, R="gn", W="gn")
    ph.act(gn[:P, 4, :], gn[:P, 4, :], AF.Sqrt, R=["gn", "eps_gn"], W="gn", bias=eps_gn[:P, 0:1])
    ph.op(V, lambda e: e.reciprocal(out=gn[:P, 5, :], in_=gn[:P, 4, :]), R="gn", W="gn")
    ph.tt(V, Ysq[:P], Ysb[:P], bc(gn[:P, 2, :].unsqueeze(2), [P, 8, 64]), ALU.subtract, R=["Ysb", "gn"], W="Ysq")
    ph.tt(V, ynb[:P], Ysq[:P], bc(gn[:P, 5, :].unsqueeze(2), [P, 8, 64]), ALU.mult, R=["Ysq", "gn"], W="ynb")
    pt, kt = getT()
    for m in range(4):
        ph.tr(pt[:, m, :P], ynb[:P, 2 * m:2 * m + 2, :].rearrange("p a b -> p (a b)"), ib[:P, :P], R="ynb", W=kt)
    B4 = lambda t: bc(t[:, :].unsqueeze(2), [128, 4, P])
    t1 = ex2
    ph.tt(V, t1[:, :, :P], pt[:, 0:4, :P], B4(pc["lnx_g"]), ALU.mult, R=[kt, "c_lnx_g"], W="gns")
    ph.tt(V, t1[:, :, :P], t1[:, :, :P], B4(pc["lnx_b"]), ALU.add, R=["gns", "c_lnx_b"], W="gns")
    ph.tt(V, t1[:, :, :P], t1[:, :, :P], bon[:, :, :P], ALU.add, R=["gns", KX["bon"]], W="gns")
    ph.tt(V, YFb[:, :, c0:c0 + P], t1[:, :, :P], gg[:, :, :P], ALU.mult, R=["gns", KX["gg"]], W="YFb")


def s5_block(ph, I, G0, pc, Xs, ub, ZZb, getF, nchunk, which, ncol, step=CS, npos=CS):
    V = "dve"
    BwT, Kmat, CwT, Ab = G0["BwT"], G0["Kmat"], G0["CwT"], G0["Abar"]
    nm = nchunk
    assert nm * 8 <= 512
    for Pl in range(4):
        pb, pk = getF()
        flat = pb[:].rearrange("p a b -> p (a b)")
        for ri in range(2):
            for k in range(4):
                q = ri * 4 + k
                dst = flat[:, q * nm:(q + 1) * nm]
                for j in range(npos):
                    jj = (CS - npos) + j
                    rhs = ub[32 * Pl:32 * Pl + 32, k, j:j + (nm - 1) * step + 1:step]
                    ph.mm(dst, BwT[32 * Pl:32 * Pl + 32, k, jj, ri, :], rhs, j == 0, j == npos - 1,
                          R=["BwT", "ub"], W=pk, tp=((96, 0) if Pl == 3 else None))
        for ri in range(2):
            ph.cp(V, Xs[:, ri, Pl:16:4, 1:1 + nm],
                  flat[:, ri * 4 * nm:(ri + 1) * 4 * nm].rearrange("p (q m) -> p q m", m=nm), R=[pk], W="Xs")
    A_r = bc(Ab[:, which, 0, :].unsqueeze(1), [128, 2, 16]); A_i = bc(Ab[:, which, 1, :].unsqueeze(1), [128, 2, 16])
    ph._s5marks = [len(ph._rec) if ph._rec is not None else 0]
    tmpa = ph._s5tmp[0]; tmpb = ph._s5tmp[1]
    for m in range(nm):
        ph.tt(SCAN_ENG, tmpa[:], Xs[:, :, :, m], A_r, ALU.mult, R=["Xs", "Abar"], W="s5a")
        ph.tt(SCAN_ENG, tmpb[:], Xs[:, :, :, m], A_i, ALU.mult, R=["Xs", "Abar"], W="s5b")
        ph.tt(SCAN_ENG, Xs[:, :, :, m + 1], Xs[:, :, :, m + 1], tmpa[:], ALU.add, R=["Xs", "s5a"], W="Xs")
        ph.tt(SCAN_ENG, Xs[:, 0, :, m + 1], Xs[:, 0, :, m + 1], tmpb[:, 1, :], ALU.subtract, R=["Xs", "s5b"], W="Xs")
        ph.tt(SCAN_ENG, Xs[:, 1, :, m + 1], Xs[:, 1, :, m + 1], tmpb[:, 0, :], ALU.add, R=["Xs", "s5b"], W="Xs")
    ph._s5marks.append(len(ph._rec) if ph._rec is not None else 0)
    Xb = ph._s5xb
    ph.cp("act", Xb[:, :, :, 0:nm], Xs[:, :, :, 0:nm], R="Xs", W="Xb")
    for k in range(4):
        pb, pk = getF()
        flat = pb[:].rearrange("p a b -> p (a b)")
        for i in range(npos):
            dst = flat[:, i * nm:(i + 1) * nm]
            for tau in range(i + 1):
                rhs = ub[:, k, (i - tau):(i - tau) + (nm - 1) * step + 1:step]
                ph.mm(dst, Kmat[:, k, tau, :], rhs, tau == 0, False, R=["Kmat", "ub"], W=pk)
            for Pl in range(4):
                P_ = 4 * k + Pl
                for ri in range(2):
                    ph.mm(flat[32 * Pl:32 * Pl + 32, i * nm:(i + 1) * nm], CwT[:, i, ri, P_, :], Xb[:, ri, P_, 0:nm],
                          False, ri == 1, R=["CwT", "Xb"], W=pk, tp=(0, 32 * Pl))
        du = ph._s5du
        ph.ts(V, du[:, 0:ncol], ub[:, k, 0:ncol], pc["D_skip"][:, k:k + 1], ALU.mult, R=["ub", "c_D_skip", "s5z"], W="s5du")
        if npos == 1:
            ph.tt(V, du[:, 0:ncol], du[:, 0:ncol], flat[:, 0:nm], ALU.add, R=["s5du", pk], W="s5du")
        else:
            ph.tt(V, du[:, 0:ncol].rearrange("p (m i) -> p m i", i=npos), du[:, 0:ncol].rearrange("p (m i) -> p m i", i=npos),
                  flat[:, 0:npos * nm].rearrange("p (i m) -> p m i", m=nm), ALU.add, R=["s5du", pk], W="s5du")
        ph.act(ZZb[:, k, 0:ncol], du[:, 0:ncol], AF.Gelu_apprx_tanh, R="s5du", W=["ZZb", "s5z"])
    ph.cp(V, Xs[:, :, :, 0], Xs[:, :, :, nm], R="Xs", W="Xs")


def sample_mixer(ph, I, G0, L):
    V = "dve"
    sb = ph.sb
    pc = L["pc"]; getF, getT, ib = L["getF"], L["getT"], L["ib"]
    identf = G0["identf"]
    XS = L["XS"]; dd = L["dd"]
    t0 = T
    n = NS
    cur = sb("s_cur", [128, 14, NS], F32); prv = sb("s_prv", [128, 14, NS], F32)
    ph.dma("sp", cur[:], I["PRW"][:, t0:t0 + n].rearrange("(m p) t -> p m t", p=128), W="s_cur")
    sst = sb("s_sst", [NS, 1792], F32)
    ph.dma("sp", sst[:], I["st_shift"], W="s_sst")
    for half in range(4):
        pb, pk = getF()
        flat = pb[:].rearrange("p a b -> p (a b)")
        ms = list(range(half * 4, min(14, half * 4 + 4)))
        for q, m in enumerate(ms):
            ph.tr(flat[:, q * NS:(q + 1) * NS], sst[:, m * 128:(m + 1) * 128], identf[:NS, :NS], R=["s_sst"], W=pk)
        ph.cp(V, prv[:, ms[0]:ms[-1] + 1, :], flat[:, 0:len(ms) * NS].rearrange("p (a b) -> p a b", b=NS), R=pk, W="s_prv")
    ph.dbg("cur", cur[:], [128, 14, NS], "s_cur")
    ph.dbg("prv", prv[:], [128, 14, NS], "s_prv")
    so = sst
    for half in range(4):
        pb, pk = getF()
        flat = pb[:].rearrange("p a b -> p (a b)")
        ms = list(range(half * 4, min(14, half * 4 + 4)))
        for q, m in enumerate(ms):
            ph.tr(flat[:NS, q * 128:(q + 1) * 128], cur[:, m, :], identf[:], R=["s_cur"], W=pk)
        ph.cp(V, so[:, ms[0] * 128:(ms[-1] + 1) * 128], flat[:NS, 0:len(ms) * 128], R=pk, W="s_sst")
    ph.dma("sp", I["s_shift"], so[:], R="s_sst")
    xs = XS[:, :, 0:NS]
    ph.tt(V, dd[:, :, 0:NS], prv[:], cur[:], ALU.subtract, R=["s_prv", "s_cur"], W="dd")
    ph.tt(V, dd[:, :, 0:NS], dd[:, :, 0:NS], bc(pc["mu_shift"][:, :].unsqueeze(2), [128, 14, NS]), ALU.mult,
          R=["dd", "c_mu_shift"], W="dd")
    ph.tt(V, xs, dd[:, :, 0:NS], cur[:], ALU.add, R=["dd", "s_cur"], W="XS")
    uf = L["uf"]; ub = L["ub"]; ZZb = L["ZZb"]
    ph.dma("act", uf[:, :, 0:NS], I["UU"][:, t0:t0 + n].rearrange("(m p) t -> p m t", p=128), W="uf")
    ph.cp("act", ub[:, :, 0:NS], uf[:, :, 0:NS], R="uf", W="ub")
    stx = [sb("s_stre", [NS, 2048], F32), sb("s_stim", [NS, 2048], F32)]
    ph.dma("sp", stx[0][:], I["st_re"], W="s_stx0"); ph.dma("sp", stx[1][:], I["st_im"], W="s_stx1")
    Xsm = sb("s_Xsm", [128, 2, 16, NS], F32)
    for ri in range(2):
        for q4 in range(4):
            pb, pk = getF()
            flat = pb[:].rearrange("p a b -> p (a b)")
            for q in range(4):
                P_ = q4 * 4 + q
                ph.tr(flat[:, q * NS:(q + 1) * NS], stx[ri][:, P_ * 128:(P_ + 1) * 128], identf[:NS, :NS],
                      R="s_stx%d" % ri, W=pk)
            ph.cp(V, Xsm[:, ri, q4 * 4:q4 * 4 + 4, :], flat[:, 0:4 * NS].rearrange("p (a b) -> p a b", b=NS), R=pk, W="s_Xsm")
    s5_sample(ph, I, G0, pc, Xsm, ub, ZZb, getF, stx)
    ph.dma("act", I["ZZ"][:, t0:t0 + n].rearrange("(m p) t -> p m t", p=128), ZZb[:, :, 0:NS], R="ZZb")
    rwkv_sample(ph, I, G0, L)
    ph.dma("sp", I["YF"][:, t0:t0 + n].rearrange("(m p) t -> p m t", p=128), L["YFb"][:, :, 0:NS], R="YFb")


def s5_sample(ph, I, G0, pc, Xsm, ub, ZZb, getF, stx):
    V = "dve"
    BwT, Kmat, CwT, Ab = G0["BwT"], G0["Kmat"], G0["CwT"], G0["Abar"]
    identf = G0["identf"]
    Xb = ph._s5xb
    ph.cp("act", Xb[:, :, :, 0:NS], Xsm[:], R="s_Xsm", W="Xb")
    du = ph._s5du
    for k in range(4):
        pb, pk = getF()
        flat = pb[:].rearrange("p a b -> p (a b)")
        ph.mm(flat[:, 0:NS], Kmat[:, k, 0, :], ub[:, k, 0:NS], True, False, R=["Kmat", "ub"], W=pk)
        for Pl in range(4):
            P_ = 4 * k + Pl
            for ri in range(2):
                ph.mm(flat[32 * Pl:32 * Pl + 32, 0:NS], CwT[:, 0, ri, P_, :], Xb[:, ri, P_, 0:NS], False,
                      ri == 1, R=["CwT", "Xb"], W=pk, tp=(0, 32 * Pl))
        ph.ts(V, du[:, 0:NS], ub[:, k, 0:NS], pc["D_skip"][:, k:k + 1], ALU.mult, R=["ub", "c_D_skip", "s5z"], W="s5du")
        ph.tt(V, du[:, 0:NS], du[:, 0:NS], flat[:, 0:NS], ALU.add, R=["s5du", pk], W="s5du")
        ph.act(ZZb[:, k, 0:NS], du[:, 0:NS], AF.Gelu_apprx_tanh, R="s5du", W=["ZZb", "s5z"])
    Gs = ph.sb("s_Gs", [128, 2, 16, NS], F32)
    for Pl in range(4):
        pb, pk = getF()
        flat = pb[:].rearrange("p a b -> p (a b)")
        for ri in range(2):
            for k in range(4):
                q = ri * 4 + k
                ph.mm(flat[:, q * NS:(q + 1) * NS], BwT[32 * Pl:32 * Pl + 32, k, CS - 1, ri, :],
                      ub[32 * Pl:32 * Pl + 32, k, 0:NS], True, True, R=["BwT", "ub"], W=pk,
                      tp=((96, 0) if Pl == 3 else None))
        for ri in range(2):
            ph.cp(V, Gs[:, ri, Pl:16:4, :], flat[:, ri * 4 * NS:(ri + 1) * 4 * NS].rearrange("p (q m) -> p q m", m=NS),
                  R=pk, W="s_Gs")
    A_r = bc(Ab[:, 1, 0, :].unsqueeze(2), [128, 16, NS]); A_i = bc(Ab[:, 1, 1, :].unsqueeze(2), [128, 16, NS])
    ta = ph.sb("s_ta", [128, 16, NS], F32)
    ph.tt(V, ta[:], Xsm[:, 0], A_r, ALU.mult, R=["s_Xsm", "Abar"], W="s_ta")
    ph.tt(V, Gs[:, 0], Gs[:, 0], ta[:], ALU.add, R=["s_Gs", "s_ta"], W="s_Gs")
    ph.tt(V, ta[:], Xsm[:, 1], A_i, ALU.mult, R=["s_Xsm", "Abar", "s_Gs"], W="s_ta")
    ph.tt(V, Gs[:, 0], Gs[:, 0], ta[:], ALU.subtract, R=["s_Gs", "s_ta"], W="s_Gs")
    ph.tt(V, ta[:], Xsm[:, 1], A_r, ALU.mult, R=["s_Xsm", "Abar", "s_Gs"], W="s_ta")
    ph.tt(V, Gs[:, 1], Gs[:, 1], ta[:], ALU.add, R=["s_Gs", "s_ta"], W="s_Gs")
    ph.tt(V, ta[:], Xsm[:, 0], A_i, ALU.mult, R=["s_Xsm", "Abar", "s_Gs"], W="s_ta")
    ph.tt(V, Gs[:, 1], Gs[:, 1], ta[:], ALU.add, R=["s_Gs", "s_ta"], W="s_Gs")
    for ri, nm in enumerate(("s_re", "s_im")):
        xo = stx[ri]
        for q4 in range(4):
            pb, pk = getF()
            flat = pb[:].rearrange("p a b -> p (a b)")
            for q in range(4):
                P_ = q4 * 4 + q
                ph.tr(flat[:NS, q * 128:(q + 1) * 128], Gs[:, ri, P_, :], identf[:], R="s_Gs", W=pk)
            ph.cp(V, xo[:, q4 * 512:(q4 + 1) * 512], flat[:NS, 0:512], R=pk, W="s_stx%d" % ri)
        ph.dma("sp", I[nm], xo[:], R="s_stx%d" % ri)


def rwkv_sample(ph, I, G0, L):
    V = "dve"
    sb = ph.sb
    pc = L["pc"]; getF, getT, ib = L["getF"], L["getT"], L["ib"]
    identf = G0["identf"]
    XS = L["XS"]
    sig, aa, gg, kk0, tq, rn, kkn = L["sig"], L["aa"], L["gg"], L["kk0"], L["tq"], L["rn"], L["kkn"]
    bb, kmod, bon = L["bb"], L["kmod"], L["bon"]
    lin, sgx, w2a2, g2b, blk64 = L["lin"], L["sgx"], L["w2a2"], L["g2b"], L["blk64"]
    n = NS
    r_ = XS[:, 0:4, 0:n]; k_ = XS[:, 4:8, 0:n]; v_ = XS[:, 8:12, 0:n]
    B4 = lambda t: bc(t[:, :].unsqueeze(2), [128, 4, n])
    S4 = lambda t: t[:, :, 0:n]
    ph.act(lin[0:64, 0:n], XS[0:64, 12, 0:n], AF.Tanh, R="XS", W="lin")
    ph.cp("act", lin[64:128, 0:n], XS[64:128, 12, 0:n], R="XS", W="lin")
    ph.act(sgx[:, 0:n], XS[:, 13, 0:n], AF.Sigmoid, R="XS", W="sgx")
    pw_, kw_ = getF(); pa_, ka_ = getF(); pg_, kg_ = getF()
    for m in range(4):
        ph.mm(pw_[:, m, 0:n], w2a2[0:64, m * 128:(m + 1) * 128], lin[0:64, 0:n], True, True, R=["w2a2", "lin"], W=kw_)
        ph.mm(pa_[:, m, 0:n], w2a2[64:128, m * 128:(m + 1) * 128], lin[64:128, 0:n], True, True, R=["w2a2", "lin"], W=ka_)
        ph.mm(pg_[:, m, 0:n], g2b[:, m * 128:(m + 1) * 128], sgx[:, 0:n], True, True, R=["g2b", "sgx"], W=kg_)
    for m in range(4):
        ph.act(sig[:, m, 0:n], pw_[:, m, 0:n], AF.Sigmoid, R=[kw_, "c_w0"], W="sig", bias=pc["w0"][:, m:m + 1])
        ph.act(aa[:, m, 0:n], pa_[:, m, 0:n], AF.Sigmoid, R=[ka_, "c_a0"], W="aa", bias=pc["a0"][:, m:m + 1])
    ph.cp("act", S4(gg), pg_[:, :, 0:n], R=kg_, W="gg")
    ph.tt(V, S4(kk0), k_, B4(pc["k_k"]), ALU.mult, R=["XS", "c_k_k"], W="kk0")
    ph.tt(V, S4(tq), S4(kk0), S4(kk0), ALU.mult, R="kk0", W="tq")
    pq, kq = getF()
    for m in range(4):
        ph.mm(pq[:, m, 0:n], blk64[:], tq[:, m, 0:n], True, True, R=["blk64", "tq"], W=kq)
    ph.act(S4(rn), pq[:, :, 0:n], AF.Sqrt, R=kq, W="rn")
    ph.ts(V, S4(rn), S4(rn), 1e-12, ALU.max, R="rn", W="rn")
    ph.op(V, lambda e: e.reciprocal(out=S4(rn), in_=S4(rn)), R="rn", W="rn")
    ph.tt(V, S4(kkn), S4(kk0), S4(rn), ALU.mult, R=["kk0", "rn"], W="kkn")
    ph.tt(V, S4(bb), S4(kkn), S4(aa), ALU.mult, R=["kkn", "aa"], W="bb")
    ph.tt(V, S4(tq), S4(aa), B4(pc["k_a"]), ALU.mult, R=["aa", "c_k_a", kq], W="tq")
    ph.tt(V, S4(tq), S4(tq), B4(pc["k_a"]), ALU.subtract, R=["tq", "c_k_a"], W="tq")
    ph.stt(S4(kmod), S4(tq), 1.0, k_, ALU.add, ALU.mult, R=["tq", "XS"], W="kmod")
    ph.tt(V, S4(tq), r_, S4(kmod), ALU.mult, R=["XS", "kmod"], W="tq")
    ph.tt(V, S4(tq), S4(tq), B4(pc["r_k"]), ALU.mult, R=["tq", "c_r_k"], W="tq")
    pq2, kq2 = getF()
    for m in range(4):
        ph.mm(pq2[:, m, 0:n], blk64[:], tq[:, m, 0:n], True, True, R=["blk64", "tq"], W=kq2)
    ph.tt(V, S4(bon), pq2[:, :, 0:n], v_, ALU.mult, R=[kq2, "XS"], W="bon")
    wdec = L["ex1"]
    ph.act(S4(wdec), S4(sig), AF.Exp, R="sig", W="ex1", scale=-C1)
    srcs = [r_, S4(wdec), S4(kmod), v_, S4(kkn), S4(bb)]
    keys = ["XS", "ex1", "kmod", "XS", "kkn", "bb"]
    tok = sb("s_tok", [NS, 6, 512], F32)
    for i, (src, kkey) in enumerate(zip(srcs, keys)):
        pb, pk = getF()
        flat = pb[:].rearrange("p a b -> p (a b)")
        for m in range(4):
            ph.tr(flat[:NS, m * 128:(m + 1) * 128], src[:, m, :], identf[:], R=kkey, W=pk)
        ph.cp(V if i % 2 else "act", tok[:, i, :], flat[:NS, 0:512], R=pk, W="s_tok")
    ph.dma("sp", I["SW"].rearrange("i b f -> b i f"), tok[:], R="s_tok", W="SWd")
    vec = sb("s_vec", [128, 6, 64], F32)
    ph.dma("sp", vec[:], I["SW"].rearrange("i b (h k) -> (b h) i k", h=8), R="SWd", W="s_vec")
    S0 = sb("s_S0", [128, 64, 64], F32)
    ph.dma("act", S0[:].rearrange("p a b -> p (a b)"), I["st_wkv"], W="s_S0")
    tmp = sb("s_tmp", [128, 64, 64], F32)
    sa = sb("s_sa", [128, 64], F32); yv = sb("s_yv", [128, 64], F32); kka = sb("s_kka", [128, 64], F32)
    kB = lambda i: bc(vec[:, i, :].unsqueeze(1), [128, 64, 64])
    ph.tt(V, tmp[:], S0[:], kB(4), ALU.mult, R=["s_S0", "s_vec"], W="s_tmp")
    ph.op(V, lambda e: e.tensor_reduce(out=sa[:], in_=tmp[:], axis=AX.X, op=ALU.add), R="s_tmp", W="s_sa")
    ph.tt(V, S0[:], S0[:], kB(1), ALU.mult, R=["s_S0", "s_vec", "s_tmp"], W="s_S0")
    ph.tt(V, tmp[:], bc(sa[:, :].unsqueeze(2), [128, 64, 64]), kB(5), ALU.mult, R=["s_sa", "s_vec"], W="s_tmp")
    ph.tt(V, S0[:], S0[:], tmp[:], ALU.subtract, R=["s_S0", "s_tmp"], W="s_S0")
    ph.tt(V, tmp[:], bc(vec[:, 3, :].unsqueeze(2), [128, 64, 64]), kB(2), ALU.mult, R=["s_vec", "s_S0"], W="s_tmp")
    ph.tt(V, S0[:], S0[:], tmp[:], ALU.add, R=["s_S0", "s_tmp"], W="s_S0")
    ph.dma("act", I["s_wkv"], S0[:].rearrange("p a b -> p (a b)"), R="s_S0")
    ph.tt(V, tmp[:], S0[:], kB(0), ALU.mult, R=["s_S0", "s_vec"], W="s_tmp")
    ph.op(V, lambda e: e.tensor_reduce(out=yv[:], in_=tmp[:], axis=AX.X, op=ALU.add), R="s_tmp", W="s_yv")
    ph.dma("sp", I["SY"], yv[:], R="s_yv", W="SYd")
    Ysb = L["Ysb"]
    ph.dma("sp", Ysb[:NS].rearrange("p a b -> p (a b)"), I["SY"].rearrange("(b h) v -> b (h v)", h=8), R="SYd", W="Ysb")
    groupnorm_out(ph, L, 0, NS)


def phase3(nc, I, G0, W3, WFI):
    ph = Ph(nc, "p3")
    V = "dve"
    W3 = alloc_w3(nc, ph.st)
    load_w3(ph, I, W3)
    rwo, glu, wo = W3["rwo"], W3["glu"], W3["wo"]
    for k in range(8):
        ph.dma("pool", WFI[:, k, :], I["w_ffn_in"][k * 128:(k + 1) * 128, :], W="wfi_pre")
    yf = ph.sb("yf", [128, 4, 512], BF16); zz = ph.sb("zz", [128, 4, 512], BF16); gt = ph.sb("gt", [128, 16, 512], BF16)
    trw = ph.sb("trw", [128, 8, 512], F32)
    mgs = [ph.sb("mg%d" % i, [128, 8, 512], BF16) for i in range(2)]
    sgb = [ph.sb("sgb%d" % i, [128, 512], F32) for i in range(2)]
    s5t = [ph.sb("s5t%d" % i, [128, 512], F32) for i in range(2)]
    xts = [ph.sb("xt%d" % i, [128, D], F32) for i in range(2)]
    pm = [ph.ps("pm%d" % i, [128, 512], F32) for i in range(6)]
    npm = nx = ns = 0
    SA, SB = [], []
    for bi_, (t0, nt) in enumerate(BLOCKS):
        P = min(128, nt)
        mg = mgs[bi_ % 2]; mk_ = "mg%d" % (bi_ % 2)
        r3 = lambda name: I[name][:, t0:t0 + nt].rearrange("(m p) t -> p m t", p=128)
        ph.rec_begin()
        ph.dma("sp", yf[:, :, :nt], r3("YF"), W="yf"); ph.dma("sp", zz[:, :, :nt], r3("ZZ"), W="zz")
        ph.dma("act", gt[:, :, :nt], r3("GT"), W="gt")
        for m in range(8):
            pb = pm[npm % 6]; pk = "pm%d" % (npm % 6); npm += 1
            for k in range(4):
                ph.mm(pb[:, :nt], rwo[:, k, m * 128:(m + 1) * 128], yf[:, k, :nt], k == 0, k == 3, R=["rwo", "yf"], W=pk)
            ph.tt(V, trw[:, m, :nt], pb[:, :nt], gt[:, m, :nt], ALU.mult, R=[pk, "gt"], W="trw%d" % m)
        late = []
        for m in range(8):
            pa = pm[npm % 6]; pka = "pm%d" % (npm % 6); npm += 1
            pb = pm[npm % 6]; pkb = "pm%d" % (npm % 6); npm += 1
            for k in range(4):
                ph.mm(pa[:, :nt], glu[:, k, m * 128:(m + 1) * 128], zz[:, k, :nt], k == 0, k == 3, R=["glu", "zz"], W=pka)
            for k in range(4):
                ph.mm(pb[:, :nt], glu[:, k, D + m * 128:D + (m + 1) * 128], zz[:, k, :nt], k == 0, k == 3,
                      R=["glu", "zz"], W=pkb)
            sg = sgb[ns % 2]; sk = "sgb%d" % (ns % 2); s5 = s5t[ns % 2]; s5k = "s5t%d" % (ns % 2); ns += 1
            ph.act(sg[:, :nt], pb[:, :nt], AF.Sigmoid, R=pkb, W=sk)
            for fn_ in late:
                fn_()

            def _ep(m=m, s5=s5, pa=pa, sg=sg, pka=pka, sk=sk, s5k=s5k, nt=nt, mg=mg, mk_=mk_):
                ph.tt(V, s5[:, :nt], pa[:, :nt], sg[:, :nt], ALU.mult, R=[pka, sk], W=s5k)
                ph.tt(V, s5[:, :nt], s5[:, :nt], gt[:, 8 + m, :nt], ALU.mult, R=[s5k, "gt"], W=s5k)
                ph.tt(V, mg[:, m, :nt], s5[:, :nt], trw[:, m, :nt], ALU.add, R=[s5k, "trw%d" % m], W=mk_)
            late = [_ep]
        for fn_ in late:
            fn_()
        late = []
        SA.append(ph.rec_end())
        ph.rec_begin()
        for s in range((nt + 127) // 128):
            xt = xts[nx % 2]; xk = "xt%d" % (nx % 2); nx += 1
            rows = slice(t0 + s * 128, t0 + s * 128 + P)
            ph.dma("sp", xt[:P, :], I["xall"][rows, :], W=xk)
            for half in range(2):
                pb = pm[npm % 6]; pk = "pm%d" % (npm % 6); npm += 1
                for k in range(8):
                    ph.mm(pb[:P, :], mg[:, k, s * 128:s * 128 + P], wo[:, k, half * 512:(half + 1) * 512], k == 0, k == 7,
                          R=[mk_, "wo"], W=pk)
                ph.tt(V, xt[:P, half * 512:(half + 1) * 512], xt[:P, half * 512:(half + 1) * 512], pb[:P, :], ALU.add,
                      R=[pk, xk], W=xk)
            ph.dma("pool", I["X1"][rows, :], xt[:P, :], R=xk)
        SB.append(ph.rec_end())
    ph.play(SA[0])
    for b_ in range(len(BLOCKS)):
        if b_ + 1 < len(BLOCKS):
            ph.play(SA[b_ + 1])
        ph.play(SB[b_])
    ph.finish()


def phase4(nc, I, G0, WFI):
    ph = Ph(nc, "p4")
    V = "dve"
    G = norm_scratch(ph, G0)
    identf = G0["identf"]
    wfi = WFI; wfo = ph.sb("wfo", [128, 22, D], BF16)
    for k in range(22):
        ph.dma("pool", wfo[:, k, :], I["w_ffn_out"][k * 128:(k + 1) * 128, :], W="wfo")
    g2c = ph.sb("g2c", [128, 8], F32); load_col(ph, g2c[:], I["ln2_g"], 8, "g2c")
    cw = ph.sb("cw", [128, 3, 22], F32); cb = ph.sb("cb", [128, 22], F32)
    ph.dma("sp", cw[:], I["conv_w"].rearrange("t (f p) -> p t f", p=128), W="cw", slow=True)
    load_col(ph, cb[:], I["conv_b"], 22, "cb")
    hTs = [ph.sb("hT%d" % i, [128, 8, 512], BF16) for i in range(2)]
    hid = ph.sb("hid", [128, 22, 512], BF16)
    xts = [ph.sb("xt%d" % i, [128, D], F32) for i in range(2)]
    At = [ph.sb("At%d" % i, [128, 514], F32) for i in range(2)]
    acc = [ph.sb("acc%d" % i, [128, 512], F32) for i in range(2)]
    cc = ph.sb("cc", [128, 22, 2], F32)
    ph.memset(V, cc[:].rearrange("p a b -> p (a b)"), 0.0, W="cc")
    pm = [ph.ps("pm%d" % i, [128, 512], F32) for i in range(6)]
    scs = ph.sb("scs", [NS, 2816], F32)
    scT = ph.sb("scT", [128, 22, 2, NS], F32)
    aout = scs
    npm = na = 0
    NR, FI, FO = [], [], []
    for bi_, (t0, nt) in enumerate(BLOCKS):
        P = min(128, nt)
        hT = hTs[bi_ % 2]; hk = "hT%d" % (bi_ % 2)
        sample = nt < 128
        nsub = (nt + 127) // 128
        ph.rec_begin()
        for s in range(nsub):
            rows = slice(t0 + s * 128, t0 + s * 128 + P)
            ph.dma("sp", xts[s % 2][:P, :], I["X1"][rows, :], W="xt%d" % (s % 2))
            rms_to_hT(ph, G, xts[s % 2], P, g2c, hT, s * 128, str(s % 2), "g2c", hk)
        NR.append(ph.rec_end())
        ph.rec_begin()
        if sample:
            for tt_ in range(2):
                ph.dma("sp", scs[:], I["st_conv"][:, tt_, :], W="scs")
                for q in range(6):
                    pb = pm[npm % 6]; pk = "pm%d" % (npm % 6); npm += 1
                    fs = list(range(q * 4, min(22, q * 4 + 4)))
                    for j, f_ in enumerate(fs):
                        ph.tr(pb[:, j * NS:(j + 1) * NS], scs[:, f_ * 128:(f_ + 1) * 128], identf[:NS, :NS], R="scs", W=pk)
                    ph.cp(V, scT[:, fs[0]:fs[-1] + 1, tt_, :], pb[:, 0:len(fs) * NS].rearrange("p (a b) -> p a b", b=NS),
                          R=pk, W="scT")
        late = []
        for f in range(22):
            pa = pm[npm % 6]; pka = "pm%d" % (npm % 6); npm += 1
            pb = pm[npm % 6]; pkb = "pm%d" % (npm % 6); npm += 1
            for k in range(8):
                ph.mm(pa[:, :nt], wfi[:, k, f * 128:(f + 1) * 128], hT[:, k, :nt], k == 0, k == 7, R=["wfi", hk], W=pka)
            for k in range(8):
                ph.mm(pb[:, :nt], wfi[:, k, 2816 + f * 128:2816 + (f + 1) * 128], hT[:, k, :nt], k == 0, k == 7,
                      R=["wfi", hk], W=pkb)
            A = At[na % 2]; ak = "At%d" % (na % 2); ac = acc[na % 2]; ck = "acc%d" % (na % 2); na += 1
            ph.cp("act", A[:, 2:2 + nt], pa[:, :nt], R=pka, W=ak)
            if not sample:
                ph.cp(V, A[:, 0:2], cc[:, f, :], R="cc", W=ak)
                a0, a1, a2 = A[:, 0:nt], A[:, 1:1 + nt], A[:, 2:2 + nt]
            else:
                a0, a1, a2 = scT[:, f, 0, :], scT[:, f, 1, :], A[:, 2:2 + nt]
            ph.ts(V, ac[:, :nt], a0, cw[:, 0, f:f + 1], ALU.mult, cb[:, f:f + 1], ALU.add, R=[ak, "scT", "cw", "cb"], W=ck)
            ph.stt(ac[:, :nt], a1, cw[:, 1, f:f + 1], ac[:, :nt], ALU.mult, ALU.add, R=[ak, "scT", "cw", ck], W=ck)
            ph.stt(ac[:, :nt], a2, cw[:, 2, f:f + 1], ac[:, :nt], ALU.mult, ALU.add, R=[ak, "cw", ck], W=ck)
            ph.act(ac[:, :nt], ac[:, :nt], AF.Gelu_apprx_tanh, R=ck, W=ck)
            for fn_ in late:
                fn_()
            late = [(lambda f=f, ac=ac, pb=pb, ck=ck, pkb=pkb, nt=nt:
                     ph.tt(V, hid[:, f, :nt], ac[:, :nt], pb[:, :nt], ALU.mult, R=[ck, pkb], W="hid"))]
            if not sample:
                ph.cp(V, cc[:, f, :], A[:, nt:nt + 2], R=ak, W="cc")
            else:
                po = pm[npm % 6]; pko = "pm%d" % (npm % 6); npm += 1
                ph.tr(po[:NS, 0:128], A[:, 2:2 + NS], identf[:], R=ak, W=pko)
                ph.cp(V, aout[:, f * 128:(f + 1) * 128], po[:NS, 0:128], R=pko, W="scs")
        for fn_ in late:
            fn_()
        late = []
        if t0 + nt == T:
            for tt_ in range(2):
                ph.dma("sp", I["p_conv"][tt_].rearrange("(f p) -> p f", p=128), cc[:, :, tt_], R="cc", slow=True)
        if sample:
            ph.dma("sp", I["s_conv"][:, 1, :], aout[:], R="scs")
            ph.dma("act", I["s_conv"][:, 0, :], I["st_conv"][:, 1, :])
        FI.append(ph.rec_end())
        ph.rec_begin()
        for s in range(nsub):
            rows = slice(t0 + s * 128, t0 + s * 128 + P)
            xt = xts[s % 2]; xk = "xt%d" % (s % 2)
            ph.dma("sp", xt[:P, :], I["X1"][rows, :], W=xk)
            for half in range(2):
                pb = pm[npm % 6]; pk = "pm%d" % (npm % 6); npm += 1
                for f in range(22):
                    ph.mm(pb[:P, :], hid[:, f, s * 128:s * 128 + P], wfo[:, f, half * 512:(half + 1) * 512], f == 0, f == 21,
                          R=["hid", "wfo"], W=pk)
                ph.tt(V, xt[:P, half * 512:(half + 1) * 512], xt[:P, half * 512:(half + 1) * 512], pb[:P, :],
                      ALU.add, R=[pk, xk], W=xk)
            ph.dma("pool", I["X2"][rows, :], xt[:P, :], R=xk)
        FO.append(ph.rec_end())
    ph.play(NR[0])
    for b_ in range(len(BLOCKS)):
        if STRICT4:
            if b_ + 1 < len(BLOCKS):
                ph.play(NR[b_ + 1])
            ph.play(FI[b_])
        else:
            ph.play(FI[b_], NR[b_ + 1] if b_ + 1 < len(BLOCKS) else [])
        ph.play(FO[b_])
    ph.finish()


def phase5(nc, I, G0):
    ph = Ph(nc, "p5")
    V = "dve"
    NB = 4
    Gs = [norm_scratch(ph, G0, "a")]
    for i in range(1, NB):
        Gs.append(norm_scratch(ph, G0, "abcd"[i], eps=Gs[0]["eps"]))
    wpg = ph.sb("wpg", [128, 8, D], BF16); wpl = ph.sb("wpl", [128, 2, D], BF16)
    for k in range(8):
        ph.dma("pool", wpg[:, k, :], I["w_ple_gate"][k * 128:(k + 1) * 128, :], W="wpg")
    for k in range(2):
        ph.dma("pool", wpl[:, k, :], I["w_ple"][k * 128:(k + 1) * 128, :], W="wpl")
    g3c = ph.sb("g3c", [128, 8], F32); load_col(ph, g3c[:], I["ln3_g"], 8, "g3c")
    fg = ph.sb("fg", [128, D], F32)
    ph.dma("sp", fg[:], I["final_g"].partition_broadcast(128), W="fg")
    hTs = [ph.sb("hT%d" % i, [128, 8, 128], BF16) for i in range(NB)]
    xts = [ph.sb("xt%d" % i, [128, D], F32) for i in range(NB)]
    sg = [ph.sb("sg%d" % i, [128, 512], F32) for i in range(NB)]
    yo = [ph.sb("yo%d" % i, [128, D], F32) for i in range(NB)]
    pm = [ph.ps("pm%d" % i, [128, 512], F32) for i in range(3)]
    pq = ph.ps("pq", [128, 8, 128], BF16)
    subs = []
    for (t0, nt) in BLOCKS:
        P = min(128, nt)
        for s in range((nt + 127) // 128):
            subs.append((t0 + s * 128, P))
    NSUB = len(subs)
    pball = ph.sb("pball", [128, NSUB, 256], BF16)
    pTall = ph.sb("pTall", [128, NSUB, 2, 128], BF16)
    for i, (r0, P) in enumerate(subs):
        ph.dma("pool", pball[:P, i, :], I["pall"][r0:r0 + P, :], W="pb%d" % i)
    for i0_ in range(0, NSUB, 4):
        grp = list(range(i0_, min(NSUB, i0_ + 4)))
        for j, i in enumerate(grp):
            P = subs[i][1]
            for k in range(2):
                ph.tr(pq[:, 2 * j + k, :P], pball[:P, i, k * 128:(k + 1) * 128], G0["identb"][:P, :P],
                      R=["pb%d" % i, "identb"], W="pq")
        for j, i in enumerate(grp):
            P = subs[i][1]
            ph.cp("act", pTall[:, i, :, :P], pq[:, 2 * j:2 * j + 2, :P], R="pq", W="pT%d" % i)
    npm = nsg = 0
    FR, BK = [], []
    for i, (r0, P) in enumerate(subs):
        rows = slice(r0, r0 + P)
        i2 = i % NB
        xt = xts[i2]; xk = "xt%d" % i2
        G = Gs[i2]; hT = hTs[i2]; hk = "hT%d" % i2
        ph.rec_begin()
        ph.dma("sp", xt[:P, :], I["X2"][rows, :], W=xk)
        rms_to_hT(ph, G, xt, P, g3c, hT, 0, str(i2), "g3c", hk)
        FR.append(ph.rec_end())
        ph.rec_begin()
        for half in range(2):
            cs_ = slice(half * 512, (half + 1) * 512)
            pg = pm[npm % 3]; pgk = "pm%d" % (npm % 3); npm += 1
            pe = pm[npm % 3]; pek = "pm%d" % (npm % 3); npm += 1
            for k in range(8):
                ph.mm(pg[:P, :], hT[:, k, :P], wpg[:, k, cs_], k == 0, k == 7, R=[hk, "wpg"], W=pgk)
            for k in range(2):
                ph.mm(pe[:P, :], pTall[:, i, k, :P], wpl[:, k, cs_], k == 0, k == 1, R=["pT%d" % i, "wpl"], W=pek)
            sgt = sg[nsg % NB]; sgk = "sg%d" % (nsg % NB); nsg += 1
            ph.act(sgt[:P, :], pg[:P, :], AF.Sigmoid, R=pgk, W=sgk)
            ph.tt(V, sgt[:P, :], sgt[:P, :], pe[:P, :], ALU.mult, R=[sgk, pek], W=sgk)
            ph.tt(V, xt[:P, cs_], xt[:P, cs_], sgt[:P, :], ALU.add, R=[sgk, xk, "xn" + G["sx"]], W=xk)
        ss = G["ss"]; sq = G["sq"]; kss = "ss" + G["sx"]; ksq = "sq" + G["sx"]
        ph.act(sq[:P, :], xt[:P, :], AF.Square, R=xk, W=[ksq, kss], accum=ss[:P, 0:1])
        ph.act(ss[:P, 1:2], ss[:P, 0:1], AF.Sqrt, R=[kss, "eps"], W=kss, bias=G["eps"][:P, 0:1], scale=1.0 / D)
        ph.op(V, lambda e, ss=ss, P=P: e.reciprocal(out=ss[:P, 3:4], in_=ss[:P, 1:2]), R=kss, W=kss + "3")
        y = yo[i2]; yk = "yo%d" % i2
        ph.stt(y[:P, :], xt[:P, :], ss[:P, 3:4], fg[:P, :], ALU.mult, ALU.mult, R=[xk, kss + "3", "fg"], W=yk)
        ph.dma("pool", I["y"][rows, :], y[:P, :], R=yk)
        BK.append(ph.rec_end())
    AHEAD = 2
    for i in range(min(AHEAD, NSUB)):
        ph.play(FR[i])
    for i in range(NSUB):
        if i + AHEAD < NSUB:
            ph.play(FR[i + AHEAD])
        ph.play(BK[i])
    ph.finish()


_CACHE = {}


def _consts():
    i = np.arange(128)
    c = {}
    c["c_ident"] = np.eye(128, dtype=np.float32)
    c["c_msl"] = (i[None, :] < i[:, None]).astype(np.float32)
    c["c_msu"] = (i[:, None] < i[None, :]).astype(np.float32)
    c["c_mui"] = (i[:, None] <= i[None, :]).astype(np.float32)
    c["c_blk64"] = ((i[:, None] // 64) == (i[None, :] // 64)).astype(np.float32)
    c["c_blk32"] = ((i[:, None] // 32) == (i[None, :] // 32)).astype(np.float32)
    c["c_rowgp"] = (((i[:, None] // 16) % 2) == (i[None, :] // 64)).astype(np.float32)
    return c


def make_in_maps(inp):
    f = lambda a: np.ascontiguousarray(np.asarray(a, dtype=np.float32))
    cst = _consts()
    shared = {}
    for k in ("ln1_g", "w_in", "mu_shift", "w0", "w2", "a0", "a2", "g2", "k_k", "k_a", "lnx_g", "lnx_b", "w_rw_out",
              "A_re", "A_im", "log_dt", "B_re", "B_im", "D_skip", "w_glu", "w_out", "ln2_g", "w_ffn_in", "conv_w",
              "conv_b", "w_ffn_out", "ln3_g", "w_ple_gate", "w_ple"):
        shared[k] = f(inp[k])[0]
    shared["r_k"] = f(inp["r_k"])[0].reshape(512)
    shared["C_re"] = f(inp["C_re"])[0].reshape(512, 64)
    shared["C_im"] = f(inp["C_im"])[0].reshape(512, 64)
    shared["final_g"] = f(inp["final_g"])
    shared.update(cst)
    xp, xs = f(inp["x_prompt"]), f(inp["x_sample"])
    pp, psm = f(inp["p_prompt"])[0], f(inp["p_sample"])[0]
    in_maps = []
    for c in range(8):
        sl = slice(NS * c, NS * c + NS)
        m = dict(shared)
        m["xall"] = np.concatenate([xp[c], xs[sl, 0]], 0)
        m["pall"] = np.concatenate([pp[c], psm[sl, 0]], 0)
        m["st_shift"] = f(inp["state_shift"])[0, sl]
        m["st_wkv"] = f(inp["state_wkv"])[0, sl].reshape(128, 4096)
        m["st_re"] = f(inp["state_ssm_re"])[0, sl].reshape(NS, 2048)
        m["st_im"] = f(inp["state_ssm_im"])[0, sl].reshape(NS, 2048)
        m["st_conv"] = f(inp["state_conv"])[0, sl]
        in_maps.append({k: np.ascontiguousarray(v) for k, v in m.items()})
    return in_maps


def kernel(**inp):
    f = lambda a: np.ascontiguousarray(np.asarray(a, dtype=np.float32))
    if "nc" not in _CACHE:
        _CACHE["nc"] = build_program()
    nc = _CACHE["nc"]
    in_maps = make_in_maps(inp)
    res = run_bass_kernel_spmd(nc, in_maps, core_ids=list(range(8)))
    R = res.results
    cat = lambda fn: np.stack([fn(r) for r in R], 0)
    y_prompt = cat(lambda r: r["y"][:T])
    y_sample = np.concatenate([r["y"][T:] for r in R], 0)[:, None, :]
    p_shift = cat(lambda r: r["p_shift"])[None]
    p_wkv = cat(lambda r: r["p_wkv"].reshape(8, 64, 64).transpose(0, 2, 1))[None]
    p_re = cat(lambda r: r["p_re"].reshape(32, 64))[None]
    p_im = cat(lambda r: r["p_im"].reshape(32, 64))[None]
    p_conv = cat(lambda r: r["p_conv"])[None]
    s_shift = np.concatenate([r["s_shift"] for r in R], 0)[None]
    s_wkv = np.concatenate([r["s_wkv"].reshape(NS, 8, 64, 64) for r in R], 0)[None]
    s_re = np.concatenate([r["s_re"].reshape(NS, 32, 64) for r in R], 0)[None]
    s_im = np.concatenate([r["s_im"].reshape(NS, 32, 64) for r in R], 0)[None]
    s_conv = np.concatenate([r["s_conv"] for r in R], 0)[None]
    outs = (y_prompt, y_sample, p_shift, p_wkv, p_re, p_im, p_conv, s_shift, s_wkv, s_re, s_im, s_conv)
    return tuple(np.ascontiguousarray(o.astype(np.float32)) for o in outs)
```

```python
import contextlib
import math
import numpy as np
import concourse.bass as bass
import concourse.mybir as mybir
from concourse.bass_utils import run_bass_kernel_spmd

F32 = mybir.dt.float32
BF16 = mybir.dt.bfloat16
AF = mybir.ActivationFunctionType
ALU = mybir.AluOpType
AX = mybir.AxisListType

T = 2048
NS = 16
NT = T + NS
D = 1024
CS = 8
SCAN_ENG = "pool"
import os as _os
BUB = int(_os.environ.get("K_BUB", "48"))
BUB2 = int(_os.environ.get("K_BUB2", "0"))
SYO = float(_os.environ.get("K_SYO", "0.5"))
STRICT1 = int(_os.environ.get("K_ST1", "0"))
STRICT4 = int(_os.environ.get("K_ST4", "1"))
C1 = math.exp(-0.5)
BLOCKS = [(0, 512), (512, 512), (1024, 512), (1536, 512), (2048, 16)]

ENGS = ("pe", "act", "dve", "pool", "sp")
NDSEM = 12


class _Op:
    __slots__ = ("eng", "fn", "deps", "dma", "observed", "tok", "idx", "dslot")

    def __init__(self, eng, fn, dma):
        self.eng, self.fn, self.dma = eng, fn, dma
        self.deps = set()
        self.observed = False
        self.tok = None
        self.dslot = None


class Sched:
    def __init__(self, nc):
        self.nc = nc
        self.ops = []
        self.last_w = {}
        self.readers = {}
        self.dma_rr = {e: 0 for e in ENGS}
        self.dma_prev = {}
        self.excl = set()

    def _add(self, eng, fn, reads, writes, dma):
        op = _Op(eng, fn, dma)
        op.idx = len(self.ops)
        if self.excl:
            ex = tuple(b for b in reads if b in self.excl)
            if ex:
                writes = tuple(writes) + ex
        for b in reads:
            w = self.last_w.get(b)
            if w is not None:
                op.deps.add(w)
        for b in writes:
            w = self.last_w.get(b)
            if w is not None:
                op.deps.add(w)
            for r in self.readers.get(b, ()):
                op.deps.add(r)
        if dma:
            slot = (eng, self.dma_rr[eng] % NDSEM)
            self.dma_rr[eng] += 1
            op.dslot = slot
            prev = self.dma_prev.get(slot)
            if prev is not None:
                op.deps.add(prev)
            self.dma_prev[slot] = op.idx
        op.deps.discard(op.idx)
        self.ops.append(op)
        for b in writes:
            self.last_w[b] = op.idx
            self.readers[b] = []
        for b in reads:
            if b not in writes:
                self.readers.setdefault(b, []).append(op.idx)
        return op.idx

    def emit(self):
        nc = self.nc
        ops = self.ops
        need = []
        for op in ops:
            nd = []
            for d in op.deps:
                p = ops[d]
                if (not p.dma) and (not op.dma) and p.eng == op.eng == "pe":
                    continue
                nd.append(d)
                p.observed = True
            need.append(nd)
        last = {}
        for op in ops:
            key = op.dslot if op.dma else op.eng
            last[key] = op.idx
        for i in last.values():
            ops[i].observed = True
        g = getattr(nc, "_gsem", None)
        if g is None:
            g = {"sems": {}, "cnt": {e: 0 for e in ENGS}, "dcnt": {}}
            nc._gsem = g
        cnt = g["cnt"]
        dcnt = g["dcnt"]
        for op in ops:
            if op.dma:
                dcnt[op.dslot] = dcnt.get(op.dslot, 0) + 16
                op.tok = (op.dslot, dcnt[op.dslot])
            elif op.observed:
                cnt[op.eng] += 1
                op.tok = (op.eng, cnt[op.eng])
        sems = g["sems"]
        for k in list(ENGS) + sorted(set(o.dslot for o in ops if o.dma)):
            if k not in sems:
                nm = k if isinstance(k, str) else "d_%s_%d" % k
                sems[k] = nc.alloc_semaphore(name="s_" + nm)
        with contextlib.ExitStack() as st:
            block = st.enter_context(nc.Block())
            per = {e: [o for o in ops if o.eng == e] for e in ENGS}
            hw = {"pe": block.tensor, "act": block.scalar, "dve": block.vector,
                  "pool": block.gpsimd, "sp": block.sync}

            def make(e):
                def body(eng):
                    seen = {}
                    for op in per[e]:
                        waits = {}
                        for d in need[op.idx]:
                            k, v = ops[d].tok
                            if v > waits.get(k, 0):
                                waits[k] = v
                        for k, v in waits.items():
                            if seen.get(k, 0) >= v:
                                continue
                            seen[k] = v
                            eng.wait_ge(sems[k], v)
                        ins = op.fn(eng)
                        if op.dma:
                            ins.then_inc(sems[op.tok[0]], 16)
                        elif op.observed:
                            ins.then_inc(sems[e], 1)
                    if e == "sp":
                        for key, i in last.items():
                            k, v = ops[i].tok
                            if seen.get(k, 0) < v:
                                eng.wait_ge(sems[k], v)
                return body

            for e in ENGS:
                hw[e](make(e))


def _L(x):
    if x is None:
        return ()
    if isinstance(x, str):
        return (x,)
    return tuple(x)


class Ph:
    _uid = [0]

    def __init__(self, nc, tag):
        self.nc = nc
        self.tag = tag
        self.st = contextlib.ExitStack()
        self.S = Sched(nc)

    def sb(self, name, shape, dt):
        return self.st.enter_context(self.nc.sbuf_tensor(self.tag + "_" + name, list(shape), dt))

    def ps(self, name, shape, dt):
        self.S.excl.add(name)
        return self.st.enter_context(self.nc.psum_tensor(self.tag + "_" + name, list(shape), dt))

    def finish(self):
        self.S.emit()
        self.st.close()

    def dbg(self, name, ap, shape, key, dt=F32):
        import os
        if os.environ.get("K_DBG_DUMP", "") == "":
            return
        t = self.nc.dram_tensor("dbg_" + name, list(shape), dt, kind="ExternalOutput").ap()
        self.dma("sp", t, ap, R=key)

    _rec = None

    def rec_begin(self):
        self._rec = []

    def rec_end(self):
        r, self._rec = self._rec, None
        return r

    def bubble(self, k):
        if self._rec is not None:
            self._rec.append(("bubble", k))

    def merge(self, *streams, spans=None):
        if spans is None:
            spans = [(0.0, 1.0)] * len(streams)
        keep = [i for i, st_ in enumerate(streams) if st_]
        spans = [spans[i] for i in keep]
        streams = [streams[i] for i in keep]
        pos = [0] * len(streams)
        out = []
        while True:
            best, bi = None, -1
            for i, st_ in enumerate(streams):
                if pos[i] < len(st_):
                    f = spans[i][0] + spans[i][1] * (pos[i] + 1.0) / len(st_)
                    if best is None or f < best:
                        best, bi = f, i
            if bi < 0:
                break
            item = streams[bi][pos[bi]]
            pos[bi] += 1
            if item[0] == "bubble":
                left = item[1]
                prog = True
                while left > 0 and prog:
                    prog = False
                    for j in range(len(streams)):
                        if j != bi and pos[j] < len(streams[j]) and left > 0:
                            it2 = streams[j][pos[j]]
                            pos[j] += 1
                            prog = True
                            if it2[0] != "bubble":
                                out.append(it2)
                                left -= 1
                continue
            out.append(item)
        return out

    def play(self, *streams, spans=None):
        for item in self.merge(*streams, spans=spans):
            if item[0] == "bubble":
                continue
            eng, fn, R, W, dma = item
            self.S._add(eng, fn, R, W, dma)

    def op(self, eng, fn, R=None, W=None):
        if self._rec is not None:
            self._rec.append((eng, fn, _L(R), _L(W), False))
        else:
            self.S._add(eng, fn, _L(R), _L(W), False)

    def dma(self, q, out, in_, R=None, W=None, slow=False):
        if slow:
            fn = lambda e: e.dma_start(out=out, in_=in_, allow_slow_non_contiguous=True)
        else:
            fn = lambda e: e.dma_start(out=out, in_=in_)
        if self._rec is not None:
            self._rec.append((q, fn, _L(R), _L(W), True))
        else:
            self.S._add(q, fn, _L(R), _L(W), True)

    def tt(self, eng, out, in0, in1, op, R=None, W=None):
        self.op(eng, lambda e: e.tensor_tensor(out=out, in0=in0, in1=in1, op=op), R, W)

    def ts(self, eng, out, in0, s1, op0, s2=None, op1=None, R=None, W=None):
        if op1 is None:
            self.op(eng, lambda e: e.tensor_scalar(out=out, in0=in0, scalar1=s1, scalar2=None, op0=op0), R, W)
        else:
            self.op(eng, lambda e: e.tensor_scalar(out=out, in0=in0, scalar1=s1, scalar2=s2, op0=op0, op1=op1), R, W)

    def stt(self, out, in0, scalar, in1, op0, op1, R=None, W=None):
        self.op("dve", lambda e: e.scalar_tensor_tensor(out=out, in0=in0, scalar=scalar, in1=in1, op0=op0, op1=op1), R, W)

    def act(self, out, in_, func, R=None, W=None, bias=None, scale=1.0, accum=None):
        kw = {}
        if bias is not None:
            kw["bias"] = bias
        if accum is not None:
            kw["accum_out"] = accum
        self.op("act", lambda e: e.activation(out=out, in_=in_, func=func, scale=scale, **kw), R, W)

    def cp(self, eng, out, in_, R=None, W=None):
        if eng == "act":
            self.op("act", lambda e: e.activation(out=out, in_=in_, func=AF.Copy), R, W)
        else:
            self.op(eng, lambda e: e.tensor_copy(out=out, in_=in_), R, W)

    def mm(self, out, lhsT, rhs, start, stop, R=None, W=None, tp=None):
        if tp is None:
            self.op("pe", lambda e: e.matmul(out, lhsT=lhsT, rhs=rhs, start=start, stop=stop), R, W)
        else:
            self.op("pe", lambda e: e.matmul(out, lhsT=lhsT, rhs=rhs, start=start, stop=stop, tile_position=tp), R, W)

    def tr(self, out, in_, ident, R=None, W=None):
        self.op("pe", lambda e: e.transpose(out, in_, ident), R, W)

    def memset(self, eng, ap, v, W=None):
        self.op(eng, lambda e: e.memset(ap, v), None, W)


def bc(ap, shape):
    return ap.to_broadcast(list(shape))


def rms_to_hT(ph, G, xt, P, gcol, hT, c0, tag, gkey, hkey="hT"):
    sq, ss, xn, pT = G["sq"], G["ss"], G["xn"], G["pT"]
    x_ = G.get("sx", "")
    ksq, kss, kxn, kpT = "sq" + x_, "ss" + x_, "xn" + x_, "pT" + x_
    ph.act(sq[:P, :], xt[:P, :], AF.Square, R="xt" + tag, W=[ksq, kss], accum=ss[:P, 0:1])
    ph.act(ss[:P, 1:2], ss[:P, 0:1], AF.Sqrt, R=[kss, "eps"], W=kss, bias=G["eps"][:P, 0:1], scale=1.0 / D)
    ph.op("dve", lambda e: e.reciprocal(out=ss[:P, 2:3], in_=ss[:P, 1:2]), R=kss, W=kss)
    ph.ts("dve", xn[:P, :], xt[:P, :], ss[:P, 2:3], ALU.mult, R=["xt" + tag, kss], W=kxn)
    for k in range(8):
        ph.tr(pT[:, k, :P], xn[:P, k * 128:(k + 1) * 128], G["identb"][:P, :P], R=[kxn, "identb"], W=kpT)
    ph.tt("dve", hT[:, :, c0:c0 + P], pT[:, :, :P], bc(gcol[:, :].unsqueeze(2), [128, 8, P]), ALU.mult,
          R=[kpT, gkey], W=hkey)


def load_col(ph, dst, src1d, n, key):
    ph.dma("sp", dst, src1d.rearrange("(k p) -> p k", p=128), W=key, slow=True)


def norm_scratch(ph, G0, sx="", eps=None):
    G = dict(G0)
    G["sx"] = sx
    G["sq"] = ph.sb("sq" + sx, [128, D], F32)
    G["ss"] = ph.sb("ss" + sx, [128, 4], F32)
    G["xn"] = ph.sb("xn" + sx, [128, D], BF16)
    G["pT"] = ph.ps("pT" + sx, [128, 8, 128], BF16)
    if eps is None:
        G["eps"] = ph.sb("eps", [128, 1], F32)
        ph.memset("dve", G["eps"][:], 1e-6, W="eps")
    else:
        G["eps"] = eps
    return G


def build_program(upto=9, debug=False):
    nc = bass.Bass("TRN2", target_bir_lowering=False)
    I = {}

    def inp(name, shape, dt=F32):
        I[name] = nc.dram_tensor(name, list(shape), dt, kind="ExternalInput").ap()

    def outp(name, shape):
        I[name] = nc.dram_tensor(name, list(shape), F32, kind="ExternalOutput").ap()

    def scratch(name, shape, dt):
        if debug:
            I[name] = nc.dram_tensor(name, list(shape), dt, kind="ExternalOutput").ap()
        else:
            I[name] = nc.dram_tensor(name, list(shape), dt).ap()
    if debug:
        scratch("d_BwT", [128, 4 * CS * 2 * 128], BF16); scratch("d_Kmat", [128, 4 * CS * 128], BF16)
        scratch("d_CwT", [128, CS * 2 * 16 * 32], BF16); scratch("d_Abar", [128, 64], F32)

    inp("xall", [NT, D]); inp("pall", [NT, 256])
    inp("st_shift", [NS, 1792]); inp("st_wkv", [128, 4096]); inp("st_re", [NS, 2048]); inp("st_im", [NS, 2048])
    inp("st_conv", [NS, 2, 2816])
    inp("ln1_g", [D]); inp("w_in", [D, 4352]); inp("mu_shift", [1792]); inp("w0", [512]); inp("w2", [64, 512])
    inp("a0", [512]); inp("a2", [64, 512]); inp("g2", [128, 512]); inp("k_k", [512]); inp("k_a", [512])
    inp("r_k", [512]); inp("lnx_g", [512]); inp("lnx_b", [512]); inp("w_rw_out", [512, D])
    inp("A_re", [32, 64]); inp("A_im", [32, 64]); inp("log_dt", [32]); inp("B_re", [32, 64, 16]); inp("B_im", [32, 64, 16])
    inp("C_re", [512, 64]); inp("C_im", [512, 64]); inp("D_skip", [512]); inp("w_glu", [512, 2048]); inp("w_out", [D, D])
    inp("ln2_g", [D]); inp("w_ffn_in", [D, 5632]); inp("conv_w", [3, 2816]); inp("conv_b", [2816]); inp("w_ffn_out", [2816, D])
    inp("ln3_g", [D]); inp("w_ple_gate", [D, D]); inp("w_ple", [256, D]); inp("final_g", [D])
    inp("c_ident", [128, 128]); inp("c_msl", [128, 128]); inp("c_msu", [128, 128]); inp("c_mui", [128, 128])
    inp("c_blk64", [128, 128]); inp("c_blk32", [128, 128]); inp("c_rowgp", [128, 128])
    outp("y", [NT, D]); outp("p_shift", [1792]); outp("p_wkv", [512, 64]); outp("p_re", [2048]); outp("p_im", [2048])
    outp("p_conv", [2, 2816]); outp("s_shift", [NS, 1792]); outp("s_wkv", [128, 4096]); outp("s_re", [NS, 2048])
    outp("s_im", [NS, 2048]); outp("s_conv", [NS, 2, 2816])
    scratch("PRW", [1792, NT], F32); scratch("UU", [512, NT], F32); scratch("GT", [2048, NT], BF16)
    scratch("YF", [512, NT], BF16); scratch("ZZ", [512, NT], BF16); scratch("X1", [NT, D], F32); scratch("X2", [NT, D], F32)
    scratch("SW", [6, NS, 512], F32); scratch("SY", [128, 64], F32)

    with contextlib.ExitStack() as gst:
        def gsb(name, shape, dt):
            return gst.enter_context(nc.sbuf_tensor("g_" + name, list(shape), dt))
        G0 = {}
        G0["identb"] = gsb("identb", [128, 128], BF16)
        G0["identf"] = gsb("identf", [128, 128], F32)
        with contextlib.ExitStack() as g2:
            def g2sb(name, shape, dt):
                return g2.enter_context(nc.sbuf_tensor("g_" + name, list(shape), dt))
            G0["BwT"] = g2sb("BwT", [128, 4, CS, 2, 128], BF16)
            G0["Kmat"] = g2sb("Kmat", [128, 4, CS, 128], BF16)
            G0["CwT"] = g2sb("CwT", [128, CS, 2, 16, 32], BF16)
            G0["Abar"] = g2sb("Abar", [128, 2, 2, 16], F32)
            if upto >= 1:
                phase1(nc, I, G0, debug)
            else:
                phase0(nc, I, G0, debug)
            if upto >= 2:
                phase2(nc, I, G0, True)
            if upto >= 2.5:
                phase2(nc, I, G0, False)
        g4 = contextlib.ExitStack()
        WFI = g4.enter_context(nc.sbuf_tensor("g_wfi", [128, 8, 5632], BF16))
        if upto >= 3:
            phase3(nc, I, G0, None, WFI)
        if upto >= 4:
            phase4(nc, I, G0, WFI)
        g4.close()
        if upto >= 5:
            phase5(nc, I, G0)
    return nc


def phase0(nc, I, G0, debug=False, ph=None):
    own = ph is None
    if own:
        ph = Ph(nc, "p0")
        ph.dma("pool", G0["identb"][:], I["c_ident"], W="identb")
        ph.dma("sp", G0["identf"][:], I["c_ident"], W="identf")
    sb = ph.sb
    lr = sb("lr", [128, 16], F32); li = sb("li", [128, 16], F32); dtl = sb("dtl", [128, 16], F32)
    Bre = sb("Bre", [128, 16, 16], F32); Bim = sb("Bim", [128, 16, 16], F32)
    ph.dma("sp", lr[:], I["A_re"].rearrange("(P gp) n -> (gp n) P", gp=2), W="lr", slow=True)
    ph.dma("sp", li[:], I["A_im"].rearrange("(P gp) n -> (gp n) P", gp=2), W="li", slow=True)
    ldt2 = I["log_dt"].rearrange("(P gp) -> gp P", gp=2)
    for gp in range(2):
        ph.dma("sp", dtl[64 * gp:64 * gp + 64, :], ldt2[gp].partition_broadcast(64), W="dtl", slow=True)
    ph.dma("sp", Bre[:], I["B_re"].rearrange("(P gp) n c -> (gp n) P c", gp=2), W="Bre")
    ph.dma("sp", Bim[:], I["B_im"].rearrange("(P gp) n c -> (gp n) P c", gp=2), W="Bim")
    rowgp = sb("rowgp", [128, 128], F32); blk32 = sb("blk32", [128, 128], F32)
    ph.dma("sp", rowgp[:], I["c_rowgp"], W="rowgp"); ph.dma("sp", blk32[:], I["c_blk32"], W="blk32")
    CT = [sb("CTr", [128, 4, 128], F32), sb("CTi", [128, 4, 128], F32)]
    c2 = sb("c2", [128, 128], F32)
    pA = ph.ps("pA", [128, 4, 128], F32)
    for ri, nm in enumerate(("C_re", "C_im")):
        for k in range(4):
            src = I[nm][k * 128:(k + 1) * 128, :]
            ph.dma("sp", c2[:, 0:64], src, W="c2"); ph.dma("sp", c2[:, 64:128], src, W="c2")
            ph.tt("dve", c2[:], c2[:], rowgp[:], ALU.mult, R=["c2", "rowgp"], W="c2")
            ph.tr(pA[:, k, :], c2[:], G0["identf"][:], R=["c2", "identf"], W="pA")
        ph.cp("dve", CT[ri][:], pA[:], R="pA", W="CT%d" % ri)
    t = {n: sb(n, [128, 16], F32) for n in ("dt", "e1", "mag", "ang", "sa", "ca", "sinv", "cosv", "ar", "ai", "den",
                                             "rden", "am1", "fr", "fi", "t1", "t2")}
    V = "dve"
    K = lambda *n: list(n)
    hpi = sb("hpi", [128, 1], F32)
    ph.memset(V, hpi[:], math.pi / 2, W="hpi")
    ph.act(t["dt"][:], dtl[:], AF.Exp, R="dtl", W="dt")
    ph.tt(V, t["e1"][:], lr[:], t["dt"][:], ALU.mult, R=K("lr", "dt"), W="e1")
    ph.act(t["mag"][:], t["e1"][:], AF.Exp, R="e1", W="mag")
    ph.tt(V, t["ang"][:], li[:], t["dt"][:], ALU.mult, R=K("li", "dt"), W="ang")
    ph.ts(V, t["sa"][:], t["ang"][:], 1.0 / 64, ALU.mult, R="ang", W="sa")
    ph.act(t["sinv"][:], t["sa"][:], AF.Sin, R="sa", W="sinv")
    ph.act(t["cosv"][:], t["sa"][:], AF.Sin, R=["sa", "hpi"], W="cosv", bias=hpi[:, 0:1])
    for _ in range(6):
        ph.tt(V, t["t1"][:], t["cosv"][:], t["cosv"][:], ALU.mult, R="cosv", W="t1")
        ph.tt(V, t["t2"][:], t["sinv"][:], t["sinv"][:], ALU.mult, R="sinv", W="t2")
        ph.stt(t["sinv"][:], t["cosv"][:], 2.0, t["sinv"][:], ALU.mult, ALU.mult, R=["cosv", "sinv", "t2"], W="sinv")
        ph.tt(V, t["cosv"][:], t["t1"][:], t["t2"][:], ALU.subtract, R=["t1", "t2", "sinv"], W="cosv")
    ph.tt(V, t["ar"][:], t["mag"][:], t["cosv"][:], ALU.mult, R=K("mag", "cosv"), W="ar")
    ph.tt(V, t["ai"][:], t["mag"][:], t["sinv"][:], ALU.mult, R=K("mag", "sinv"), W="ai")
    ph.tt(V, t["den"][:], lr[:], lr[:], ALU.mult, R="lr", W="den")
    ph.tt(V, t["t1"][:], li[:], li[:], ALU.mult, R="li", W="t1")
    ph.tt(V, t["den"][:], t["den"][:], t["t1"][:], ALU.add, R=K("den", "t1"), W="den")
    ph.op(V, lambda e: e.reciprocal(out=t["rden"][:], in_=t["den"][:]), R="den", W="rden")
    ph.ts(V, t["am1"][:], t["ar"][:], -1.0, ALU.add, R="ar", W="am1")
    ph.tt(V, t["t1"][:], t["am1"][:], lr[:], ALU.mult, R=K("am1", "lr", "den"), W="t1")
    ph.tt(V, t["t2"][:], t["ai"][:], li[:], ALU.mult, R=K("ai", "li"), W="t2")
    ph.tt(V, t["t1"][:], t["t1"][:], t["t2"][:], ALU.add, R=K("t1", "t2"), W="t1")
    ph.tt(V, t["fr"][:], t["t1"][:], t["rden"][:], ALU.mult, R=K("t1", "rden"), W="fr")
    ph.tt(V, t["t1"][:], t["ai"][:], lr[:], ALU.mult, R=K("ai", "lr", "fr"), W="t1")
    ph.tt(V, t["t2"][:], t["am1"][:], li[:], ALU.mult, R=K("am1", "li"), W="t2")
    ph.tt(V, t["t1"][:], t["t1"][:], t["t2"][:], ALU.subtract, R=K("t1", "t2"), W="t1")
    ph.tt(V, t["fi"][:], t["t1"][:], t["rden"][:], ALU.mult, R=K("t1", "rden"), W="fi")
    pwr = sb("pwr", [128, CS + 1, 16], F32); pwi = sb("pwi", [128, CS + 1, 16], F32)
    ph.memset(V, pwr[:, 0, :], 1.0, W="pw"); ph.memset(V, pwi[:, 0, :], 0.0, W="pw")
    for e in range(CS):
        ph.tt(V, t["t1"][:], pwr[:, e, :], t["ar"][:], ALU.mult, R=K("pw", "ar", "fi"), W="t1")
        ph.tt(V, t["t2"][:], pwi[:, e, :], t["ai"][:], ALU.mult, R=K("pw", "ai"), W="t2")
        ph.tt(V, pwr[:, e + 1, :], t["t1"][:], t["t2"][:], ALU.subtract, R=K("t1", "t2"), W="pw")
        ph.tt(V, t["t1"][:], pwr[:, e, :], t["ai"][:], ALU.mult, R=K("pw", "ai"), W="t1")
        ph.tt(V, t["t2"][:], pwi[:, e, :], t["ar"][:], ALU.mult, R=K("pw", "ar"), W="t2")
        ph.tt(V, pwi[:, e + 1, :], t["t1"][:], t["t2"][:], ALU.add, R=K("t1", "t2"), W="pw")
    Ab = G0["Abar"]
    ph.cp(V, Ab[:, 0, 0, :], pwr[:, CS, :], R="pw", W="Abar"); ph.cp(V, Ab[:, 0, 1, :], pwi[:, CS, :], R="pw", W="Abar")
    ph.cp(V, Ab[:, 1, 0, :], pwr[:, 1, :], R="pw", W="Abar"); ph.cp(V, Ab[:, 1, 1, :], pwi[:, 1, :], R="pw", W="Abar")
    bbr = sb("bbr", [128, 16, 16], F32); bbi = sb("bbi", [128, 16, 16], F32)
    u1 = sb("u1", [128, 16, 16], F32); u2 = sb("u2", [128, 16, 16], F32)
    frb = bc(t["fr"][:, :].unsqueeze(2), [128, 16, 16]); fib = bc(t["fi"][:, :].unsqueeze(2), [128, 16, 16])
    ph.tt(V, u1[:], Bre[:], frb, ALU.mult, R=K("Bre", "fr"), W="u1")
    ph.tt(V, u2[:], Bim[:], fib, ALU.mult, R=K("Bim", "fi"), W="u2")
    ph.tt(V, bbr[:], u1[:], u2[:], ALU.subtract, R=K("u1", "u2"), W="bbr")
    ph.tt(V, u1[:], Bim[:], frb, ALU.mult, R=K("Bim", "fr", "bbr"), W="u1")
    ph.tt(V, u2[:], Bre[:], fib, ALU.mult, R=K("Bre", "fi", "bbr"), W="u2")
    ph.tt(V, bbi[:], u1[:], u2[:], ALU.add, R=K("u1", "u2"), W="bbi")
    Ew = sb("Ew", [128, CS, 2, 16, 2, 16], F32)
    ph.memset(V, Ew[:].rearrange("p a b c d e -> p (a b c d e)"), 0.0, W="Ew")
    for e in range(CS):
        pr = bc(pwr[:, e, :].unsqueeze(2), [128, 16, 16]); pi = bc(pwi[:, e, :].unsqueeze(2), [128, 16, 16])
        ph.tt(V, u1[:], bbr[:], pr, ALU.mult, R=K("bbr", "pw", "Ew"), W="u1")
        ph.tt(V, u2[:], bbi[:], pi, ALU.mult, R=K("bbi", "pw", "Ew"), W="u2")
        ph.tt(V, u1[:], u1[:], u2[:], ALU.subtract, R=K("u1", "u2"), W="u1")
        for gp in range(2):
            ph.cp(V, Ew[64 * gp:64 * gp + 64, e, 0, :, gp, :], u1[64 * gp:64 * gp + 64, :, :], R="u1", W="Ew")
        ph.tt(V, u1[:], bbr[:], pi, ALU.mult, R=K("bbr", "pw", "Ew"), W="u1")
        ph.tt(V, u2[:], bbi[:], pr, ALU.mult, R=K("bbi", "pw", "Ew"), W="u2")
        ph.tt(V, u1[:], u1[:], u2[:], ALU.add, R=K("u1", "u2"), W="u1")
        for gp in range(2):
            ph.cp(V, Ew[64 * gp:64 * gp + 64, e, 1, :, gp, :], u1[64 * gp:64 * gp + 64, :, :], R="u1", W="Ew")
    CTin = sb("CTin", [128, 4, 128], F32)
    ph.ts(V, CTin[:], CT[1][:], -1.0, ALU.mult, R="CT1", W="CTin")
    pB = [ph.ps("pB%d" % i, [128, 4, 128], F32) for i in range(2)]
    n = 0
    for j in range(CS):
        e = CS - 1 - j
        for ri in range(2):
            pb = pB[n % 2]; n += 1
            for k in range(4):
                src = Ew[:, e, ri, 4 * k:4 * k + 4, :, :].rearrange("p a b c -> p (a b c)")
                ph.tr(pb[:, k, :], src, G0["identf"][:], R=["Ew", "identf"], W="pB%d" % ((n - 1) % 2))
            ph.cp("act" if n % 2 else "dve", G0["BwT"][:, :, j, ri, :], pb[:], R="pB%d" % ((n - 1) % 2), W="BwT")
    for tau in range(CS):
        pb = pB[n % 2]; key = "pB%d" % (n % 2); n += 1
        for k in range(4):
            lr_ = Ew[:, tau, 0, 4 * k:4 * k + 4, :, :].rearrange("p a b c -> p (a b c)")
            li_ = Ew[:, tau, 1, 4 * k:4 * k + 4, :, :].rearrange("p a b c -> p (a b c)")
            ph.mm(pb[:, k, :], lr_, CT[0][:, k, :], True, False, R=["Ew", "CT0"], W=key)
            ph.mm(pb[:, k, :], li_, CTin[:, k, :], False, True, R=["Ew", "CTin"], W=key)
        ph.tt(V, G0["Kmat"][:, :, tau, :], pb[:], bc(blk32[:, :].unsqueeze(1), [128, 4, 128]), ALU.mult,
              R=[key, "blk32"], W="Kmat")
    w1 = sb("w1", [128, 16, 32], F32); w2_ = sb("w2", [128, 16, 32], F32)
    CTr3 = CT[0][:].rearrange("p k (a b) -> p (k a) b", a=4); CTi3 = CT[1][:].rearrange("p k (a b) -> p (k a) b", a=4)
    for i in range(CS):
        pr = bc(pwr[:, i + 1, :].unsqueeze(2), [128, 16, 32]); pi = bc(pwi[:, i + 1, :].unsqueeze(2), [128, 16, 32])
        ph.tt(V, w1[:], CTr3, pr, ALU.mult, R=K("CT0", "pw", "CwT"), W="w1")
        ph.tt(V, w2_[:], CTi3, pi, ALU.mult, R=K("CT1", "pw", "CwT"), W="w2")
        ph.tt(V, G0["CwT"][:, i, 0, :, :], w1[:], w2_[:], ALU.subtract, R=K("w1", "w2"), W="CwT")
        ph.tt(V, w1[:], CTr3, pi, ALU.mult, R=K("CT0", "pw", "CwT"), W="w1")
        ph.tt(V, w2_[:], CTi3, pr, ALU.mult, R=K("CT1", "pw", "CwT"), W="w2")
        ph.tt(V, w1[:], w1[:], w2_[:], ALU.add, R=K("w1", "w2"), W="w1")
        ph.ts(V, G0["CwT"][:, i, 1, :, :], w1[:], -1.0, ALU.mult, R="w1", W="CwT")
    if debug:
        ph.dma("sp", I["d_BwT"], G0["BwT"][:].rearrange("p a b c d -> p (a b c d)"), R="BwT")
        ph.dma("sp", I["d_Kmat"], G0["Kmat"][:].rearrange("p a b c -> p (a b c)"), R="Kmat")
        ph.dma("sp", I["d_CwT"], G0["CwT"][:].rearrange("p a b c d -> p (a b c d)"), R="CwT")
        ph.dma("sp", I["d_Abar"], G0["Abar"][:].rearrange("p a b c -> p (a b c)"), R="Abar")
    if own:
        ph.finish()


def phase1(nc, I, G0, debug=False):
    ph = Ph(nc, "p1")
    win = ph.sb("win", [128, 8, 4352], BF16)
    for k in range(8):
        ph.dma("pool", win[:, k, :], I["w_in"][k * 128:(k + 1) * 128, :], W="win%d" % k)
    ph.dma("pool", G0["identb"][:], I["c_ident"], W="identb")
    ph.dma("sp", G0["identf"][:], I["c_ident"], W="identf")
    ph.rec_begin()
    phase0(nc, I, G0, debug, ph=ph)
    s0 = ph.rec_end()
    ph.rec_begin()
    G = norm_scratch(ph, G0)
    g1c = ph.sb("g1c", [128, 8], F32)
    load_col(ph, g1c[:], I["ln1_g"], 8, "g1c")
    hTs = [ph.sb("hT%d" % i, [128, 8, 512], BF16) for i in range(2)]
    xts = [ph.sb("xt%d" % i, [128, D], F32) for i in range(2)]
    pm = [ph.ps("pm%d" % i, [128, 512], F32) for i in range(4)]
    stf = [ph.sb("stf%d" % i, [128, 512], F32) for i in range(4)]
    stb = [ph.sb("stb%d" % i, [128, 512], BF16) for i in range(3)]
    WK = ["win%d" % k for k in range(8)]
    nx = nf = nb = npm = 0
    pre1 = ph.rec_end()
    NR1, MM1 = [], []
    for bi_, (t0, nt) in enumerate(BLOCKS):
        P = min(128, nt)
        hT = hTs[bi_ % 2]; hk = "hT%d" % (bi_ % 2)
        ph.rec_begin()
        for s in range((nt + 127) // 128):
            xt = xts[nx % 2]; tg = str(nx % 2); nx += 1
            ph.dma("sp", xt[:P, :], I["xall"][t0 + s * 128:t0 + s * 128 + P, :], W="xt" + tg)
            rms_to_hT(ph, G, xt, P, g1c, hT, s * 128, tg, "g1c", hk)
        NR1.append(ph.rec_end())
        ph.rec_begin()
        for m in range(34):
            pb = pm[npm % 4]; pk = "pm%d" % (npm % 4); npm += 1
            for k in range(8):
                ph.mm(pb[:, :nt], win[:, k, m * 128:(m + 1) * 128], hT[:, k, :nt], k == 0, k == 7,
                      R=["win%d" % k, hk], W=pk)
            if m < 18:
                sf = stf[nf % 4]; sk = "stf%d" % (nf % 4); nf += 1
                ph.cp("dve" if m % 2 else "act", sf[:, :nt], pb[:, :nt], R=pk, W=sk)
                if m < 14:
                    ph.dma("pool", I["PRW"][m * 128:(m + 1) * 128, t0:t0 + nt], sf[:, :nt], R=sk)
                else:
                    ph.dma("pool", I["UU"][(m - 14) * 128:(m - 13) * 128, t0:t0 + nt], sf[:, :nt], R=sk)
            else:
                sbf = stb[nb % 3]; sk = "stb%d" % (nb % 3); nb += 1
                ph.act(sbf[:, :nt], pb[:, :nt], AF.Sigmoid, R=pk, W=sk)
                ph.dma("act", I["GT"][(m - 18) * 128:(m - 17) * 128, t0:t0 + nt], sbf[:, :nt], R=sk)
        MM1.append(ph.rec_end())
    s1 = pre1 + NR1[0]
    for b_ in range(len(BLOCKS)):
        if STRICT1:
            s1 = s1 + (NR1[b_ + 1] if b_ + 1 < len(BLOCKS) else []) + MM1[b_]
        else:
            s1 = s1 + ph.merge(MM1[b_], NR1[b_ + 1] if b_ + 1 < len(BLOCKS) else [])
    ph.play(s1, s0)
    ph.finish()


def alloc_w3(nc, st):
    t = lambda n, shp: st.enter_context(nc.sbuf_tensor("w3_" + n, shp, BF16))
    return {"rwo": t("rwo", [128, 4, D]), "glu": t("glu", [128, 4, 2048]), "wo": t("wo", [128, 8, D])}


def load_w3(ph, I, W3):
    for k in range(4):
        ph.dma("pool", W3["rwo"][:, k, :], I["w_rw_out"][k * 128:(k + 1) * 128, :], W="rwo")
        ph.dma("pool", W3["glu"][:, k, :], I["w_glu"][k * 128:(k + 1) * 128, :], W="glu")
    for k in range(8):
        ph.dma("pool", W3["wo"][:, k, :], I["w_out"][k * 128:(k + 1) * 128, :], W="wo")


def phase2(nc, I, G0, prompt, W3=None):
    ph = Ph(nc, "p2a" if prompt else "p2b")
    sb, ps = ph.sb, ph.ps
    V = "dve"
    if W3 is not None:
        load_w3(ph, I, W3)
    ph._s5tmp = [sb("s5a", [128, 2, 16], F32), sb("s5b", [128, 2, 16], F32)]
    ph._s5xb = sb("Xb", [128, 2, 16, 64], BF16)
    ph._s5du = sb("s5du", [128, 512], F32)
    if prompt:
        msl = sb("msl", [128, 128], BF16); msu = sb("msu", [128, 128], BF16); mui = sb("mui", [128, 128], BF16)
        ph.dma("pool", msl[:], I["c_msl"], W="msl"); ph.dma("pool", msu[:], I["c_msu"], W="msu")
        ph.dma("pool", mui[:], I["c_mui"], W="mui")
    blk64 = sb("blk64", [128, 128], F32); ph.dma("sp", blk64[:], I["c_blk64"], W="blk64")
    w2a2 = sb("w2a2", [128, 512], BF16); g2b = sb("g2b", [128, 512], BF16)
    ph.dma("pool", w2a2[0:64, :], I["w2"], W="w2a2"); ph.dma("pool", w2a2[64:128, :], I["a2"], W="w2a2")
    ph.dma("pool", g2b[:], I["g2"], W="g2b")
    pc = {}
    for nm, n in (("mu_shift", 14), ("w0", 4), ("a0", 4), ("k_k", 4), ("k_a", 4), ("r_k", 4), ("lnx_g", 4),
                  ("lnx_b", 4), ("D_skip", 4)):
        pc[nm] = sb("c_" + nm, [128, n], F32)
        load_col(ph, pc[nm][:], I[nm], n, "c_" + nm)
    PK = ["c_" + k for k in pc]
    scm = sb("scm", [128, 4, 128], F32)
    ph.memset(V, scm[:].rearrange("p a b -> p (a b)"), 1.0, W="scm"); ph.memset(V, scm[:, :, 0:1], 0.0, W="scm")
    eps_gn = sb("eps_gn", [128, 1], F32); ph.memset(V, eps_gn[:], 64e-5, W="eps_gn")
    if prompt:
        Sst = sb("Sst", [128, 4, 64], F32); Sbd = sb("Sbd", [128, 4, 128], BF16)
        ph.memset(V, Sst[:].rearrange("p a b -> p (a b)"), 0.0, W="Sst")
        ph.memset(V, Sbd[:].rearrange("p a b -> p (a b)"), 0.0, W="Sbd")
        Xs = sb("Xs", [128, 2, 16, 65], F32)
        ph.memset(V, Xs[:].rearrange("p a b c -> p (a b c)"), 0.0, W="Xs")
        Pf = sb("Pf", [128, 14, 513], F32)
        ph.memset(V, Pf[:, :, 0:1], 0.0, W="Pf")
    WB = 512 if prompt else NS
    WC = 128 if prompt else NS
    uf = sb("uf", [128, 4, WB], F32); ub = sb("ub", [128, 4, WB], BF16)
    YFb = sb("YFb", [128, 4, WB], BF16); ZZb = sb("ZZb", [128, 4, WB], BF16)
    f4 = lambda n: sb(n, [128, 4, WC], F32)
    b4 = lambda n: sb(n, [128, 4, WC], BF16)
    XS = sb("XS", [128, 14, WC], F32); dd = sb("dd", [128, 14, WC], F32)
    lin = sb("lin", [128, WC], BF16); sgx = sb("sgx", [128, WC], BF16)
    sig = f4("sig"); aa = f4("aa"); gg = f4("gg"); kk0 = f4("kk0"); tq = f4("tq"); rn = f4("rn"); kkn = f4("kkn")
    bb = f4("bb"); kmod = f4("kmod"); bon = f4("bon"); cs = f4("cs"); ex1 = f4("ex1"); ex2 = f4("ex2"); ex3 = f4("ex3")
    nbias = sb("nbias", [128, 4], F32); PCt = sb("PCt", [128, 4], F32)
    gns = f4("gns")
    KX = {n_: n_ for n_ in ("rT", "kT", "bT", "aT", "khT", "bhT", "vT", "PCt", "bon", "gg")}
    if prompt:
        rT = b4("rT"); kT = b4("kT"); bT = b4("bT"); aT = b4("aT"); khT = b4("khT"); bhT = b4("bhT"); vT = b4("vT")
        alt = {"rT": b4("rT1"), "kT": b4("kT1"), "bT": b4("bT1"), "aT": b4("aT1"), "khT": b4("khT1"),
               "bhT": b4("bhT1"), "vT": b4("vT1"), "PCt": sb("PCt1", [128, 4], F32), "bon": f4("bon1"), "gg": f4("gg1")}
        Vtok = sb("Vtok", [128, 512], BF16); Khtok = sb("Khtok", [128, 512], BF16); Bhtok = sb("Bhtok", [128, 512], BF16)
        h8 = lambda n: sb(n, [128, 8, 128], BF16)
        Nb = [h8("Nb0"), h8("Nb1")]; Lb = [h8("Lb0"), h8("Lb1")]; Mt = [h8("Mt0"), h8("Mt1")]
        LKb = h8("LKb"); Arb = h8("Arb"); Ark = h8("Ark")
        Wbf = sb("Wbf", [128, 512], BF16); Ubf = sb("Ubf", [128, 512], BF16)
        tS = sb("tS", [128, 4, 64], F32)
    Ysb = sb("Ysb", [128, 8, 64], F32); Ysq = sb("Ysq", [128, 8, 64], F32); ynb = sb("ynb", [128, 8, 64], BF16)
    gn = sb("gn", [128, 6, 8], F32)
    pF = [ps("pF%d" % i, [128, 4, 128], F32) for i in range(6)]
    pT = [ps("pTb%d" % i, [128, 8, 128], BF16) for i in range(2)]
    cnt = {"f": 0, "t": 0}

    def getF():
        i = cnt["f"] % 6; cnt["f"] += 1
        return pF[i], "pF%d" % i

    def mkpool(base):
        st_ = {"n": 0}

        def get():
            i = base + st_["n"] % 2; st_["n"] += 1
            return pF[i], "pF%d" % i
        return get
    getF_prep, getF_core, getFs = mkpool(0), mkpool(2), mkpool(4)

    def getT():
        i = cnt["t"] % 2; cnt["t"] += 1
        return pT[i], "pTb%d" % i

    ib = G0["identb"]

    if not prompt:
        sample_mixer(ph, I, G0, locals())
        ph.finish()
        return
    Lbase = dict(locals())
    Lpar = [dict(Lbase), dict(Lbase)]
    Lpar[1].update(alt)
    Lpar[1]["KX"] = {n_: n_ + "1" for n_ in KX}
    REC = []
    for bi, (t0, nt) in enumerate(BLOCKS[:4]):
        ph.rec_begin()
        if bi > 0:
            ph.cp(V, Pf[:, :, 0:1], Pf[:, :, 512:513], R="Pf", W="Pf")
        ph.dma("sp", Pf[:, :, 1:513], I["PRW"][:, t0:t0 + nt].rearrange("(m p) t -> p m t", p=128), W="Pf")
        if bi == 3:
            ph.dma("sp", I["p_shift"].rearrange("(m p) -> p m", p=128), Pf[:, :, 512], R="Pf", slow=True)
        hdr_pf = ph.rec_end()
        ph.rec_begin()
        ph.dma("act", uf[:], I["UU"][:, t0:t0 + nt].rearrange("(m p) t -> p m t", p=128), W="uf")
        ph.cp("act", ub[:].rearrange("p a b -> p (a b)"), uf[:].rearrange("p a b -> p (a b)"), R="uf", W="ub")
        hdr_ub = ph.rec_end()
        ph.rec_begin()
        s5_block(ph, I, G0, pc, Xs, ub, ZZb, getFs, nchunk=64, which=0, ncol=512)
        ph.dma("act", I["ZZ"][:, t0:t0 + nt].rearrange("(m p) t -> p m t", p=128), ZZb[:], R="ZZb")
        s5s = ph.rec_end()
        m0, m1 = ph._s5marks
        preps, cores = [], []
        for c in range(4):
            c0 = c * 128
            Lc = dict(Lpar[c % 2]); Lc["getF"] = getF_prep
            Lk = dict(Lpar[c % 2]); Lk["getF"] = getF_core
            ph.rec_begin()
            ph.tt(V, dd[:], Pf[:, :, c0:c0 + 128], Pf[:, :, c0 + 1:c0 + 129], ALU.subtract, R="Pf", W="dd")
            ph.tt(V, dd[:], dd[:], bc(pc["mu_shift"][:, :].unsqueeze(2), [128, 14, 128]), ALU.mult,
                  R=["dd", "c_mu_shift"], W="dd")
            ph.tt(V, XS[:], dd[:], Pf[:, :, c0 + 1:c0 + 129], ALU.add, R=["dd", "Pf"], W="XS")
            rwkv_prep_and_core(ph, Lc, c, c0)
            preps.append(ph.rec_end())
            ph.rec_begin()
            wkv_core(ph, Lk, c, c0)
            cores.append(ph.rec_end())
        ph.rec_begin()
        ph.dma("pool", I["YF"][:, t0:t0 + nt].rearrange("(m p) t -> p m t", p=128), YFb[:], R="YFb")
        yfst = ph.rec_end()
        hs = (m1 - m0) // 2
        REC.append(dict(hdr_pf=hdr_pf, hdr_ub=hdr_ub, SG=s5s[:m0], SS1=s5s[m0:m0 + hs], SS2=s5s[m0 + hs:m1],
                        SY=s5s[m1:], preps=preps, cores=cores, yfst=yfst))
    ph.play(REC[0]["hdr_pf"])
    ph.play(REC[0]["preps"][0])
    for bi in range(4):
        Rb = REC[bi]
        ph.play(Rb["hdr_ub"])
        ph.play(Rb["cores"][0], Rb["preps"][1], Rb["SG"])
        ph.play(Rb["cores"][1], Rb["preps"][2], Rb["SS1"])
        ph.play(Rb["cores"][2], Rb["preps"][3], Rb["SS2"])
        if bi < 3:
            ph.play(REC[bi + 1]["hdr_pf"])
            ph.play(Rb["cores"][3], Rb["SY"], REC[bi + 1]["preps"][0], spans=[(0.0, 1.0), (SYO, 1.0 - SYO), (0.0, 1.0)])
        else:
            ph.play(Rb["cores"][3], Rb["SY"], spans=[(0.0, 1.0), (SYO, 1.0 - SYO)])
        ph.play(Rb["yfst"])
    ph.dma("sp", I["p_wkv"].rearrange("(m p) v -> p m v", p=128), Sst[:], R="Sst")
    ph.dma("sp", I["p_re"].rearrange("(P p) -> p P", p=128), Xs[:, 0, :, 0], R="Xs", slow=True)
    ph.dma("sp", I["p_im"].rearrange("(P p) -> p P", p=128), Xs[:, 1, :, 0], R="Xs", slow=True)
    ph.finish()


def rwkv_prep_and_core(ph, L, c, c0):
    V = "dve"
    PV = L.get("PV", "dve")
    KX = L["KX"]
    pc = L["pc"]; XS = L["XS"]; getF = L["getF"]; getT = L["getT"]; ib = L["ib"]
    sig, aa, gg, kk0, tq, rn, kkn = L["sig"], L["aa"], L["gg"], L["kk0"], L["tq"], L["rn"], L["kkn"]
    bb, kmod, bon, cs, ex1, ex2, ex3 = L["bb"], L["kmod"], L["bon"], L["cs"], L["ex1"], L["ex2"], L["ex3"]
    rT, kT, bT, aT, khT, bhT, vT = L["rT"], L["kT"], L["bT"], L["aT"], L["khT"], L["bhT"], L["vT"]
    lin, sgx, w2a2, g2b, blk64 = L["lin"], L["sgx"], L["w2a2"], L["g2b"], L["blk64"]
    nbias, PCt, scm = L["nbias"], L["PCt"], L["scm"]
    r_ = XS[:, 0:4, :]; k_ = XS[:, 4:8, :]; v_ = XS[:, 8:12, :]
    B4 = lambda t: bc(t[:, :].unsqueeze(2), [128, 4, 128])
    fl = lambda t: t[:].rearrange("p a b -> p (a b)")
    ph.act(lin[0:64, :], XS[0:64, 12, :], AF.Tanh, R="XS", W="lin")
    ph.cp("act", lin[64:128, :], XS[64:128, 12, :], R="XS", W="lin")
    ph.act(sgx[:], XS[:, 13, :], AF.Sigmoid, R="XS", W="sgx")
    pw_, kw_ = getF()
    for m in range(4):
        ph.mm(pw_[:, m, :], w2a2[0:64, m * 128:(m + 1) * 128], lin[0:64, :], True, True, R=["w2a2", "lin"], W=kw_)
    for m in range(4):
        ph.act(sig[:, m, :], pw_[:, m, :], AF.Sigmoid, R=[kw_, "c_w0"], W="sig", bias=pc["w0"][:, m:m + 1])
    pa_, ka_ = getF()
    for m in range(4):
        ph.mm(pa_[:, m, :], w2a2[64:128, m * 128:(m + 1) * 128], lin[64:128, :], True, True, R=["w2a2", "lin"], W=ka_)
    for m in range(4):
        ph.act(aa[:, m, :], pa_[:, m, :], AF.Sigmoid, R=[ka_, "c_a0"], W="aa", bias=pc["a0"][:, m:m + 1])
    pg_, kg_ = getF()
    for m in range(4):
        ph.mm(pg_[:, m, :], g2b[:, m * 128:(m + 1) * 128], sgx[:], True, True, R=["g2b", "sgx"], W=kg_)
    ph.cp("act", gg[:], pg_[:], R=kg_, W=KX["gg"])
    ph.tt(PV, kk0[:], k_, B4(pc["k_k"]), ALU.mult, R=["XS", "c_k_k"], W="kk0")
    ph.tt(PV, tq[:], kk0[:], kk0[:], ALU.mult, R="kk0", W="tq")
    pq, kq = getF()
    for m in range(4):
        ph.mm(pq[:, m, :], blk64[:], tq[:, m, :], True, True, R=["blk64", "tq"], W=kq)
    ph.act(rn[:], pq[:], AF.Sqrt, R=kq, W="rn")
    ph.ts(V, rn[:], rn[:], 1e-12, ALU.max, R="rn", W="rn")
    ph.op(V, lambda e: e.reciprocal(out=fl(rn), in_=fl(rn)), R="rn", W="rn")
    ph.tt(PV, kkn[:], kk0[:], rn[:], ALU.mult, R=["kk0", "rn"], W="kkn")
    ph.tt(PV, bb[:], kkn[:], aa[:], ALU.mult, R=["kkn", "aa"], W="bb")
    ph.tt(PV, tq[:], aa[:], B4(pc["k_a"]), ALU.mult, R=["aa", "c_k_a", kq], W="tq")
    ph.tt(PV, tq[:], tq[:], B4(pc["k_a"]), ALU.subtract, R=["tq", "c_k_a"], W="tq")
    ph.stt(kmod[:], tq[:], 1.0, k_, ALU.add, ALU.mult, R=["tq", "XS"], W="kmod")
    ph.tt(PV, tq[:], r_, kmod[:], ALU.mult, R=["XS", "kmod"], W="tq")
    ph.tt(PV, tq[:], tq[:], B4(pc["r_k"]), ALU.mult, R=["tq", "c_r_k"], W="tq")
    pq2, kq2 = getF()
    for m in range(4):
        ph.mm(pq2[:, m, :], blk64[:], tq[:, m, :], True, True, R=["blk64", "tq"], W=kq2)
    ph.tt(V, bon[:], pq2[:], v_, ALU.mult, R=[kq2, "XS"], W=KX["bon"])
    ph.op(V, lambda e: e.tensor_tensor_scan(out=fl(cs), data0=fl(scm), data1=fl(sig), initial=0.0, op0=ALU.mult,
                                             op1=ALU.add), R=["scm", "sig"], W="cs")
    ph.ts(V, nbias[:], cs[:, :, 127], -C1, ALU.mult, R="cs", W="nbias")
    ph.act(PCt[:], nbias[:], AF.Exp, R="nbias", W=KX["PCt"])
    ph.act(ex1[:], cs[:], AF.Exp, R="cs", W="ex1", scale=-C1)
    ph.tt(PV, rT[:], r_, ex1[:], ALU.mult, R=["XS", "ex1"], W=KX["rT"])
    ph.act(ex2[:], cs[:], AF.Exp, R="cs", W="ex2", scale=C1)
    ph.tt(PV, kT[:], kmod[:], ex2[:], ALU.mult, R=["kmod", "ex2"], W=KX["kT"])
    ph.tt(PV, bT[:], bb[:], ex2[:], ALU.mult, R=["bb", "ex2"], W=KX["bT"])
    ph.tt(PV, ex3[:], cs[:], sig[:], ALU.subtract, R=["cs", "sig"], W="ex3")
    ph.act(ex3[:], ex3[:], AF.Exp, R="ex3", W="ex3", scale=-C1)
    ph.stt(aT[:], kkn[:], -1.0, ex3[:], ALU.mult, ALU.mult, R=["kkn", "ex3"], W=KX["aT"])
    for m in range(4):
        ph.act(ex1[:, m, :], cs[:, m, :], AF.Exp, R=["cs", "nbias", KX["rT"]], W="ex1", bias=nbias[:, m:m + 1], scale=C1)
    ph.tt(PV, khT[:], kmod[:], ex1[:], ALU.mult, R=["kmod", "ex1"], W=KX["khT"])
    ph.tt(PV, bhT[:], bb[:], ex1[:], ALU.mult, R=["bb", "ex1"], W=KX["bhT"])
    ph.cp("act", vT[:], v_, R="XS", W=KX["vT"])


def wkv_core(ph, L, c, c0):
    V = "dve"
    KX = L["KX"]
    getF = L["getF"]; getT = L["getT"]; ib = L["ib"]
    rT, kT, bT, aT, khT, bhT, vT = L["rT"], L["kT"], L["bT"], L["aT"], L["khT"], L["bhT"], L["vT"]
    Vtok, Khtok, Bhtok = L["Vtok"], L["Khtok"], L["Bhtok"]
    Nb, Lb, Mt, LKb, Arb, Ark = L["Nb"], L["Lb"], L["Mt"], L["LKb"], L["Arb"], L["Ark"]
    msl, msu, mui = L["msl"], L["msu"], L["mui"]
    Wbf, Ubf, Ysb, Ysq, ynb, gn = L["Wbf"], L["Ubf"], L["Ysb"], L["Ysq"], L["ynb"], L["gn"]
    Sst, Sbd, PCt, tS = L["Sst"], L["Sbd"], L["PCt"], L["tS"]
    pc = L["pc"]; bon, gg, YFb = L["bon"], L["gg"], L["YFb"]
    M4 = lambda m_: bc(m_[:, :].unsqueeze(1), [128, 4, 128])
    pt, kt = getT()
    for m in range(4):
        ph.tr(pt[:, m, :], vT[:, m, :], ib[:], R=KX["vT"], W=kt)
    for m in range(4):
        ph.tr(pt[:, 4 + m, :], khT[:, m, :], ib[:], R=KX["khT"], W=kt)
    ph.cp("act", Vtok[:], pt[:, 0:4, :].rearrange("p a b -> p (a b)"), R=kt, W="Vtok")
    ph.cp("act", Khtok[:], pt[:, 4:8, :].rearrange("p a b -> p (a b)"), R=kt, W="Khtok")
    pt2, kt2 = getT()
    for m in range(4):
        ph.tr(pt2[:, m, :], bhT[:, m, :], ib[:], R=KX["bhT"], W=kt2)
    ph.cp("act", Bhtok[:], pt2[:, 0:4, :].rearrange("p a b -> p (a b)"), R=kt2, W="Bhtok")

    def hsl(t, h):
        return t[64 * (h % 2):64 * (h % 2) + 64, h // 2, :]

    def amat(dst, dkey, lhs, lkey, rhs, rkey, mask, mkey):
        for par in range(2):
            pb, pk = getF()
            for q in range(4):
                h = 2 * q + par
                ph.mm(pb[:, q, :], hsl(lhs, h), hsl(rhs, h), True, True, R=[lkey, rkey], W=pk)
            ph.tt(V, dst[:, par:8:2, :], pb[:], M4(mask), ALU.mult, R=[pk, mkey], W=dkey)

    amat(Nb[0], "Nb0", aT, KX["aT"], bT, KX["bT"], msl, "msl")
    amat(Lb[0], "Lb0", bT, KX["bT"], aT, KX["aT"], msu, "msu")
    amat(LKb, "LKb", kT, KX["kT"], aT, KX["aT"], msu, "msu")
    amat(Arb, "Arb", bT, KX["bT"], rT, KX["rT"], mui, "mui")
    amat(Ark, "Ark", kT, KX["kT"], rT, KX["rT"], mui, "mui")
    for half in range(2):
        ph.tt(V, Mt[0][:, half * 4:half * 4 + 4, :], Lb[0][:, half * 4:half * 4 + 4, :], M4(ib), ALU.add,
              R=["Lb0", "identb"], W="Mt0")
    cur = 0
    for lvl in range(6):
        nxt = 1 - cur
        for half in range(2):
            pb, pk = getF()
            for q in range(4):
                h = half * 4 + q
                ph.mm(pb[:, q, :], Lb[cur][:, h, :], Nb[cur][:, h, :], True, True, R=["Lb%d" % cur, "Nb%d" % cur], W=pk)
            ph.cp("act", Nb[nxt][:, half * 4:half * 4 + 4, :], pb[:], R=pk, W="Nb%d" % nxt)
        if BUB2:
            ph.bubble(BUB2)
        if lvl < 5:
            for half in range(2):
                pb, pk = getF()
                for q in range(4):
                    h = half * 4 + q
                    ph.mm(pb[:, q, :], Nb[cur][:, h, :], Lb[cur][:, h, :], True, True,
                          R=["Lb%d" % cur, "Nb%d" % cur], W=pk)
                ph.cp("act", Lb[nxt][:, half * 4:half * 4 + 4, :], pb[:], R=pk, W="Lb%d" % nxt)
        for half in range(2):
            pb, pk = getF()
            for q in range(4):
                h = half * 4 + q
                ph.mm(pb[:, q, :], Nb[nxt][:, h, :], Mt[cur][:, h, :], True, True, R=["Nb%d" % nxt, "Mt%d" % cur], W=pk)
            ph.tt(V, Mt[nxt][:, half * 4:half * 4 + 4, :], pb[:], Mt[cur][:, half * 4:half * 4 + 4, :], ALU.add,
                  R=[pk, "Mt%d" % cur], W="Mt%d" % nxt)
        cur = nxt
    MtF = Mt[cur]; mk = "Mt%d" % cur
    def hcols(pb, h):
        return pb[:].rearrange("p a b -> p (a b)")[:, h * 64:h * 64 + 64]

    def pcols(pb, m):
        return pb[:].rearrange("p a b -> p (a b)")[:, m * 128:m * 128 + 128]

    pb, pk = getF()
    for m in range(4):
        ph.mm(pcols(pb, m), aT[:, m, :], Sbd[:, m, :], True, False, R=[KX["aT"], "Sbd"], W=pk)
        for hh in range(2):
            h = 2 * m + hh
            ph.mm(hcols(pb, h), LKb[:, h, :], Vtok[:, h * 64:h * 64 + 64], False, hh == 1, R=["LKb", "Vtok"], W=pk)
    ph.cp("act", Wbf[:], pb[:].rearrange("p a b -> p (a b)"), R=pk, W="Wbf")
    ph.bubble(BUB)
    pb, pk = getF()
    for h in range(8):
        ph.mm(hcols(pb, h), MtF[:, h, :], Wbf[:, h * 64:h * 64 + 64], True, True, R=[mk, "Wbf"], W=pk)
    ph.cp("act", Ubf[:], pb[:].rearrange("p a b -> p (a b)"), R=pk, W="Ubf")
    ph.bubble(BUB)
    pb, pk = getF()
    for m in range(4):
        ph.mm(pcols(pb, m), rT[:, m, :], Sbd[:, m, :], True, False, R=[KX["rT"], "Sbd"], W=pk)
        for hh in range(2):
            h = 2 * m + hh
            ph.mm(hcols(pb, h), Arb[:, h, :], Ubf[:, h * 64:h * 64 + 64], False, False, R=["Arb", "Ubf"], W=pk)
            ph.mm(hcols(pb, h), Ark[:, h, :], Vtok[:, h * 64:h * 64 + 64], False, hh == 1, R=["Ark", "Vtok"], W=pk)
    ph.cp("act", Ysb[:].rearrange("p a b -> p (a b)"), pb[:].rearrange("p a b -> p (a b)"), R=pk, W="Ysb")
    pS, kS = getF()
    for m in range(4):
        ph.mm(pS[:, m, :], Bhtok[:, m * 128:(m + 1) * 128], Ubf[:, m * 128:(m + 1) * 128], True, False,
              R=["Bhtok", "Ubf"], W=kS)
        ph.mm(pS[:, m, :], Khtok[:, m * 128:(m + 1) * 128], Vtok[:, m * 128:(m + 1) * 128], False, True,
              R=["Khtok", "Vtok"], W=kS)
    ph.tt(V, tS[:], Sst[:], bc(PCt[:, :].unsqueeze(2), [128, 4, 64]), ALU.mult, R=["Sst", KX["PCt"]], W="tS")
    for hh in range(2):
        rs = slice(64 * hh, 64 * hh + 64)
        ph.tt(V, Sst[rs, :, :], tS[rs, :, :], pS[rs, :, 64 * hh:64 * hh + 64], ALU.add, R=["tS", kS], W="Sst")
        ph.cp(V, Sbd[rs, :, 64 * hh:64 * hh + 64], Sst[rs, :, :], R="Sst", W="Sbd")
    groupnorm_out(ph, L, c0, 128)


def groupnorm_out(ph, L, c0, P):
    V = "dve"
    KX = L["KX"]
    Ysb, Ysq, ynb, gn = L["Ysb"], L["Ysq"], L["ynb"], L["gn"]
    pc = L["pc"]; bon, gg, YFb = L["bon"], L["gg"], L["YFb"]; getT = L["getT"]; ib = L["ib"]
    eps_gn = L["eps_gn"]; ex2 = L["gns"]
    ph.op(V, lambda e: e.tensor_reduce(out=gn[:P, 0, :], in_=Ysb[:P], axis=AX.X, op=ALU.add), R="Ysb", W="gn")
    ph.act(Ysq[:P].rearrange("p a b -> p (a b)"), Ysb[:P].rearrange("p a b -> p (a b)"), AF.Square, R="Ysb", W="Ysq")
    ph.op(V, lambda e: e.tensor_reduce(out=gn[:P, 1, :], in_=Ysq[:P], axis=AX.X, op=ALU.add), R="Ysq", W="gn")
    ph.ts(V, gn[:P, 2, :], gn[:P, 0, :], 1.0 / 64, ALU.mult, R="gn", W="gn")
    ph.tt(V, gn[:P, 3, :], gn[:P, 2, :], gn[:P, 2, :], ALU.mult, R="gn", W="gn")
    ph.stt(gn[:P, 4, :], gn[:P, 1, :], 1.0 / 64, gn[:P, 3, :], ALU.mult, ALU.subtract, R="gn", W="gn")
    ph.act(gn[:P, 4, :], gn[:P, 4, :], AF.Sqrt, R=["gn", "eps_gn"], W="gn", bias=eps_gn[:P, 0:1])
    ph.op(V, lambda e: e.reciprocal(out=gn[:P, 5, :], in_=gn[:P, 4, :]), R="gn", W="gn")
    ph.tt(V, Ysq[:P], Ysb[:P], bc(gn[:P, 2, :].unsqueeze(2), [P, 8, 64]), ALU.subtract, R=["Ysb", "gn"], W="Ysq")
    ph.tt(V, ynb[:P], Ysq[:P], bc(gn[:P, 5, :].unsqueeze(2), [P, 8, 64]), ALU.mult, R=["Ysq", "gn"], W="ynb")
    pt, kt = getT()
    for m in range(4):
        ph.tr(pt[:, m, :P], ynb[:P, 2 * m:2 * m + 2, :].rearrange("p a b -> p (a b)"), ib[:P, :P], R="ynb", W=kt)
    B4 = lambda t: bc(t[:, :].unsqueeze(2), [128, 4, P])
    t1 = ex2
    ph.tt(V, t1[:, :, :P], pt[:, 0:4, :P], B4(pc["lnx_g"]), ALU.mult, R=[kt, "c_lnx_g"], W="gns")
    ph.tt(V, t1[:, :, :P], t1[:, :, :P], B4(pc["lnx_b"]), ALU.add, R=["gns", "c_lnx_b"], W="gns")
    ph.tt(V, t1[:, :, :P], t1[:, :, :P], bon[:, :, :P], ALU.add, R=["gns", KX["bon"]], W="gns")
    ph.tt(V, YFb[:, :, c0:c0 + P], t1[:, :, :P], gg[:, :, :P], ALU.mult, R=["gns", KX["gg"]], W="YFb")


def s5_block(ph, I, G0, pc, Xs, ub, ZZb, getF, nchunk, which, ncol, step=CS, npos=CS):
    V = "dve"
    BwT, Kmat, CwT, Ab = G0["BwT"], G0["Kmat"], G0["CwT"], G0["Abar"]
    nm = nchunk
    assert nm * 8 <= 512
    for Pl in range(4):
        pb, pk = getF()
        flat = pb[:].rearrange("p a b -> p (a b)")
        for ri in range(2):
            for k in range(4):
                q = ri * 4 + k
                dst = flat[:, q * nm:(q + 1) * nm]
                for j in range(npos):
                    jj = (CS - npos) + j
                    rhs = ub[32 * Pl:32 * Pl + 32, k, j:j + (nm - 1) * step + 1:step]
                    ph.mm(dst, BwT[32 * Pl:32 * Pl + 32, k, jj, ri, :], rhs, j == 0, j == npos - 1,
                          R=["BwT", "ub"], W=pk, tp=((96, 0) if Pl == 3 else None))
        for ri in range(2):
            ph.cp(V, Xs[:, ri, Pl:16:4, 1:1 + nm],
                  flat[:, ri * 4 * nm:(ri + 1) * 4 * nm].rearrange("p (q m) -> p q m", m=nm), R=[pk], W="Xs")
    A_r = bc(Ab[:, which, 0, :].unsqueeze(1), [128, 2, 16]); A_i = bc(Ab[:, which, 1, :].unsqueeze(1), [128, 2, 16])
    ph._s5marks = [len(ph._rec) if ph._rec is not None else 0]
    tmpa = ph._s5tmp[0]; tmpb = ph._s5tmp[1]
    for m in range(nm):
        ph.tt(SCAN_ENG, tmpa[:], Xs[:, :, :, m], A_r, ALU.mult, R=["Xs", "Abar"], W="s5a")
        ph.tt(SCAN_ENG, tmpb[:], Xs[:, :, :, m], A_i, ALU.mult, R=["Xs", "Abar"], W="s5b")
        ph.tt(SCAN_ENG, Xs[:, :, :, m + 1], Xs[:, :, :, m + 1], tmpa[:], ALU.add, R=["Xs", "s5a"], W="Xs")
        ph.tt(SCAN_ENG, Xs[:, 0, :, m + 1], Xs[:, 0, :, m + 1], tmpb[:, 1, :], ALU.subtract, R=["Xs", "s5b"], W="Xs")
        ph.tt(SCAN_ENG, Xs[:, 1, :, m + 1], Xs[:, 1, :, m + 1], tmpb[:, 0, :], ALU.add, R=["Xs", "s5b"], W="Xs")
    ph._s5marks.append(len(ph._rec) if ph._rec is not None else 0)
    Xb = ph._s5xb
    ph.cp("act", Xb[:, :, :, 0:nm], Xs[:, :, :, 0:nm], R="Xs", W="Xb")
    for k in range(4):
        pb, pk = getF()
        flat = pb[:].rearrange("p a b -> p (a b)")
        for i in range(npos):
            dst = flat[:, i * nm:(i + 1) * nm]
            for tau in range(i + 1):
                rhs = ub[:, k, (i - tau):(i - tau) + (nm - 1) * step + 1:step]
                ph.mm(dst, Kmat[:, k, tau, :], rhs, tau == 0, False, R=["Kmat", "ub"], W=pk)
            for Pl in range(4):
                P_ = 4 * k + Pl
                for ri in range(2):
                    ph.mm(flat[32 * Pl:32 * Pl + 32, i * nm:(i + 1) * nm], CwT[:, i, ri, P_, :], Xb[:, ri, P_, 0:nm],
                          False, ri == 1, R=["CwT", "Xb"], W=pk, tp=(0, 32 * Pl))
        du = ph._s5du
        ph.ts(V, du[:, 0:ncol], ub[:, k, 0:ncol], pc["D_skip"][:, k:k + 1], ALU.mult, R=["ub", "c_D_skip", "s5z"], W="s5du")
        if npos == 1:
            ph.tt(V, du[:, 0:ncol], du[:, 0:ncol], flat[:, 0:nm], ALU.add, R=["s5du", pk], W="s5du")
        else:
            ph.tt(V, du[:, 0:ncol].rearrange("p (m i) -> p m i", i=npos), du[:, 0:ncol].rearrange("p (m i) -> p m i", i=npos),
                  flat[:, 0:npos * nm].rearrange("p (i m) -> p m i", m=nm), ALU.add, R=["s5du", pk], W="s5du")
        ph.act(ZZb[:, k, 0:ncol], du[:, 0:ncol], AF.Gelu_apprx_tanh, R="s5du", W=["ZZb", "s5z"])
    ph.cp(V, Xs[:, :, :, 0], Xs[:, :, :, nm], R="Xs", W="Xs")


def sample_mixer(ph, I, G0, L):
    V = "dve"
    sb = ph.sb
    pc = L["pc"]; getF, getT, ib = L["getF"], L["getT"], L["ib"]
    identf = G0["identf"]
    XS = L["XS"]; dd = L["dd"]
    t0 = T
    n = NS
    cur = sb("s_cur", [128, 14, NS], F32); prv = sb("s_prv", [128, 14, NS], F32)
    ph.dma("sp", cur[:], I["PRW"][:, t0:t0 + n].rearrange("(m p) t -> p m t", p=128), W="s_cur")
    sst = sb("s_sst", [NS, 1792], F32)
    ph.dma("sp", sst[:], I["st_shift"], W="s_sst")
    for half in range(4):
        pb, pk = getF()
        flat = pb[:].rearrange("p a b -> p (a b)")
        ms = list(range(half * 4, min(14, half * 4 + 4)))
        for q, m in enumerate(ms):
            ph.tr(flat[:, q * NS:(q + 1) * NS], sst[:, m * 128:(m + 1) * 128], identf[:NS, :NS], R=["s_sst"], W=pk)
        ph.cp(V, prv[:, ms[0]:ms[-1] + 1, :], flat[:, 0:len(ms) * NS].rearrange("p (a b) -> p a b", b=NS), R=pk, W="s_prv")
    ph.dbg("cur", cur[:], [128, 14, NS], "s_cur")
    ph.dbg("prv", prv[:], [128, 14, NS], "s_prv")
    so = sst
    for half in range(4):
        pb, pk = getF()
        flat = pb[:].rearrange("p a b -> p (a b)")
        ms = list(range(half * 4, min(14, half * 4 + 4)))
        for q, m in enumerate(ms):
            ph.tr(flat[:NS, q * 128:(q + 1) * 128], cur[:, m, :], identf[:], R=["s_cur"], W=pk)
        ph.cp(V, so[:, ms[0] * 128:(ms[-1] + 1) * 128], flat[:NS, 0:len(ms) * 128], R=pk, W="s_sst")
    ph.dma("sp", I["s_shift"], so[:], R="s_sst")
    xs = XS[:, :, 0:NS]
    ph.tt(V, dd[:, :, 0:NS], prv[:], cur[:], ALU.subtract, R=["s_prv", "s_cur"], W="dd")
    ph.tt(V, dd[:, :, 0:NS], dd[:, :, 0:NS], bc(pc["mu_shift"][:, :].unsqueeze(2), [128, 14, NS]), ALU.mult,
          R=["dd", "c_mu_shift"], W="dd")
    ph.tt(V, xs, dd[:, :, 0:NS], cur[:], ALU.add, R=["dd", "s_cur"], W="XS")
    uf = L["uf"]; ub = L["ub"]; ZZb = L["ZZb"]
    ph.dma("act", uf[:, :, 0:NS], I["UU"][:, t0:t0 + n].rearrange("(m p) t -> p m t", p=128), W="uf")
    ph.cp("act", ub[:, :, 0:NS], uf[:, :, 0:NS], R="uf", W="ub")
    stx = [sb("s_stre", [NS, 2048], F32), sb("s_stim", [NS, 2048], F32)]
    ph.dma("sp", stx[0][:], I["st_re"], W="s_stx0"); ph.dma("sp", stx[1][:], I["st_im"], W="s_stx1")
    Xsm = sb("s_Xsm", [128, 2, 16, NS], F32)
    for ri in range(2):
        for q4 in range(4):
            pb, pk = getF()
            flat = pb[:].rearrange("p a b -> p (a b)")
            for q in range(4):
                P_ = q4 * 4 + q
                ph.tr(flat[:, q * NS:(q + 1) * NS], stx[ri][:, P_ * 128:(P_ + 1) * 128], identf[:NS, :NS],
                      R="s_stx%d" % ri, W=pk)
            ph.cp(V, Xsm[:, ri, q4 * 4:q4 * 4 + 4, :], flat[:, 0:4 * NS].rearrange("p (a b) -> p a b", b=NS), R=pk, W="s_Xsm")
    s5_sample(ph, I, G0, pc, Xsm, ub, ZZb, getF, stx)
    ph.dma("act", I["ZZ"][:, t0:t0 + n].rearrange("(m p) t -> p m t", p=128), ZZb[:, :, 0:NS], R="ZZb")
    rwkv_sample(ph, I, G0, L)
    ph.dma("sp", I["YF"][:, t0:t0 + n].rearrange("(m p) t -> p m t", p=128), L["YFb"][:, :, 0:NS], R="YFb")


def s5_sample(ph, I, G0, pc, Xsm, ub, ZZb, getF, stx):
    V = "dve"
    BwT, Kmat, CwT, Ab = G0["BwT"], G0["Kmat"], G0["CwT"], G0["Abar"]
    identf = G0["identf"]
    Xb = ph._s5xb
    ph.cp("act", Xb[:, :, :, 0:NS], Xsm[:], R="s_Xsm", W="Xb")
    du = ph._s5du
    for k in range(4):
        pb, pk = getF()
        flat = pb[:].rearrange("p a b -> p (a b)")
        ph.mm(flat[:, 0:NS], Kmat[:, k, 0, :], ub[:, k, 0:NS], True, False, R=["Kmat", "ub"], W=pk)
        for Pl in range(4):
            P_ = 4 * k + Pl
            for ri in range(2):
                ph.mm(flat[32 * Pl:32 * Pl + 32, 0:NS], CwT[:, 0, ri, P_, :], Xb[:, ri, P_, 0:NS], False,
                      ri == 1, R=["CwT", "Xb"], W=pk, tp=(0, 32 * Pl))
        ph.ts(V, du[:, 0:NS], ub[:, k, 0:NS], pc["D_skip"][:, k:k + 1], ALU.mult, R=["ub", "c_D_skip", "s5z"], W="s5du")
        ph.tt(V, du[:, 0:NS], du[:, 0:NS], flat[:, 0:NS], ALU.add, R=["s5du", pk], W="s5du")
        ph.act(ZZb[:, k, 0:NS], du[:, 0:NS], AF.Gelu_apprx_tanh, R="s5du", W=["ZZb", "s5z"])
    Gs = ph.sb("s_Gs", [128, 2, 16, NS], F32)
    for Pl in range(4):
        pb, pk = getF()
        flat = pb[:].rearrange("p a b -> p (a b)")
        for ri in range(2):
            for k in range(4):
                q = ri * 4 + k
                ph.mm(flat[:, q * NS:(q + 1) * NS], BwT[32 * Pl:32 * Pl + 32, k, CS - 1, ri, :],
                      ub[32 * Pl:32 * Pl + 32, k, 0:NS], True, True, R=["BwT", "ub"], W=pk,
                      tp=((96, 0) if Pl == 3 else None))
        for ri in range(2):
            ph.cp(V, Gs[:, ri, Pl:16:4, :], flat[:, ri * 4 * NS:(ri + 1) * 4 * NS].rearrange("p (q m) -> p q m", m=NS),
                  R=pk, W="s_Gs")
    A_r = bc(Ab[:, 1, 0, :].unsqueeze(2), [128, 16, NS]); A_i = bc(Ab[:, 1, 1, :].unsqueeze(2), [128, 16, NS])
    ta = ph.sb("s_ta", [128, 16, NS], F32)
    ph.tt(V, ta[:], Xsm[:, 0], A_r, ALU.mult, R=["s_Xsm", "Abar"], W="s_ta")
    ph.tt(V, Gs[:, 0], Gs[:, 0], ta[:], ALU.add, R=["s_Gs", "s_ta"], W="s_Gs")
    ph.tt(V, ta[:], Xsm[:, 1], A_i, ALU.mult, R=["s_Xsm", "Abar", "s_Gs"], W="s_ta")
    ph.tt(V, Gs[:, 0], Gs[:, 0], ta[:], ALU.subtract, R=["s_Gs", "s_ta"], W="s_Gs")
    ph.tt(V, ta[:], Xsm[:, 1], A_r, ALU.mult, R=["s_Xsm", "Abar", "s_Gs"], W="s_ta")
    ph.tt(V, Gs[:, 1], Gs[:, 1], ta[:], ALU.add, R=["s_Gs", "s_ta"], W="s_Gs")
    ph.tt(V, ta[:], Xsm[:, 0], A_i, ALU.mult, R=["s_Xsm", "Abar", "s_Gs"], W="s_ta")
    ph.tt(V, Gs[:, 1], Gs[:, 1], ta[:], ALU.add, R=["s_Gs", "s_ta"], W="s_Gs")
    for ri, nm in enumerate(("s_re", "s_im")):
        xo = stx[ri]
        for q4 in range(4):
            pb, pk = getF()
            flat = pb[:].rearrange("p a b -> p (a b)")
            for q in range(4):
                P_ = q4 * 4 + q
                ph.tr(flat[:NS, q * 128:(q + 1) * 128], Gs[:, ri, P_, :], identf[:], R="s_Gs", W=pk)
            ph.cp(V, xo[:, q4 * 512:(q4 + 1) * 512], flat[:NS, 0:512], R=pk, W="s_stx%d" % ri)
        ph.dma("sp", I[nm], xo[:], R="s_stx%d" % ri)


def rwkv_sample(ph, I, G0, L):
    V = "dve"
    sb = ph.sb
    pc = L["pc"]; getF, getT, ib = L["getF"], L["getT"], L["ib"]
    identf = G0["identf"]
    XS = L["XS"]
    sig, aa, gg, kk0, tq, rn, kkn = L["sig"], L["aa"], L["gg"], L["kk0"], L["tq"], L["rn"], L["kkn"]
    bb, kmod, bon = L["bb"], L["kmod"], L["bon"]
    lin, sgx, w2a2, g2b, blk64 = L["lin"], L["sgx"], L["w2a2"], L["g2b"], L["blk64"]
    n = NS
    r_ = XS[:, 0:4, 0:n]; k_ = XS[:, 4:8, 0:n]; v_ = XS[:, 8:12, 0:n]
    B4 = lambda t: bc(t[:, :].unsqueeze(2), [128, 4, n])
    S4 = lambda t: t[:, :, 0:n]
    ph.act(lin[0:64, 0:n], XS[0:64, 12, 0:n], AF.Tanh, R="XS", W="lin")
    ph.cp("act", lin[64:128, 0:n], XS[64:128, 12, 0:n], R="XS", W="lin")
    ph.act(sgx[:, 0:n], XS[:, 13, 0:n], AF.Sigmoid, R="XS", W="sgx")
    pw_, kw_ = getF(); pa_, ka_ = getF(); pg_, kg_ = getF()
    for m in range(4):
        ph.mm(pw_[:, m, 0:n], w2a2[0:64, m * 128:(m + 1) * 128], lin[0:64, 0:n], True, True, R=["w2a2", "lin"], W=kw_)
        ph.mm(pa_[:, m, 0:n], w2a2[64:128, m * 128:(m + 1) * 128], lin[64:128, 0:n], True, True, R=["w2a2", "lin"], W=ka_)
        ph.mm(pg_[:, m, 0:n], g2b[:, m * 128:(m + 1) * 128], sgx[:, 0:n], True, True, R=["g2b", "sgx"], W=kg_)
    for m in range(4):
        ph.act(sig[:, m, 0:n], pw_[:, m, 0:n], AF.Sigmoid, R=[kw_, "c_w0"], W="sig", bias=pc["w0"][:, m:m + 1])
        ph.act(aa[:, m, 0:n], pa_[:, m, 0:n], AF.Sigmoid, R=[ka_, "c_a0"], W="aa", bias=pc["a0"][:, m:m + 1])
    ph.cp("act", S4(gg), pg_[:, :, 0:n], R=kg_, W="gg")
    ph.tt(V, S4(kk0), k_, B4(pc["k_k"]), ALU.mult, R=["XS", "c_k_k"], W="kk0")
    ph.tt(V, S4(tq), S4(kk0), S4(kk0), ALU.mult, R="kk0", W="tq")
    pq, kq = getF()
    for m in range(4):
        ph.mm(pq[:, m, 0:n], blk64[:], tq[:, m, 0:n], True, True, R=["blk64", "tq"], W=kq)
    ph.act(S4(rn), pq[:, :, 0:n], AF.Sqrt, R=kq, W="rn")
    ph.ts(V, S4(rn), S4(rn), 1e-12, ALU.max, R="rn", W="rn")
    ph.op(V, lambda e: e.reciprocal(out=S4(rn), in_=S4(rn)), R="rn", W="rn")
    ph.tt(V, S4(kkn), S4(kk0), S4(rn), ALU.mult, R=["kk0", "rn"], W="kkn")
    ph.tt(V, S4(bb), S4(kkn), S4(aa), ALU.mult, R=["kkn", "aa"], W="bb")
    ph.tt(V, S4(tq), S4(aa), B4(pc["k_a"]), ALU.mult, R=["aa", "c_k_a", kq], W="tq")
    ph.tt(V, S4(tq), S4(tq), B4(pc["k_a"]), ALU.subtract, R=["tq", "c_k_a"], W="tq")
    ph.stt(S4(kmod), S4(tq), 1.0, k_, ALU.add, ALU.mult, R=["tq", "XS"], W="kmod")
    ph.tt(V, S4(tq), r_, S4(kmod), ALU.mult, R=["XS", "kmod"], W="tq")
    ph.tt(V, S4(tq), S4(tq), B4(pc["r_k"]), ALU.mult, R=["tq", "c_r_k"], W="tq")
    pq2, kq2 = getF()
    for m in range(4):
        ph.mm(pq2[:, m, 0:n], blk64[:], tq[:, m, 0:n], True, True, R=["blk64", "tq"], W=kq2)
    ph.tt(V, S4(bon), pq2[:, :, 0:n], v_, ALU.mult, R=[kq2, "XS"], W="bon")
    wdec = L["ex1"]
    ph.act(S4(wdec), S4(sig), AF.Exp, R="sig", W="ex1", scale=-C1)
    srcs = [r_, S4(wdec), S4(kmod), v_, S4(kkn), S4(bb)]
    keys = ["XS", "ex1", "kmod", "XS", "kkn", "bb"]
    tok = sb("s_tok", [NS, 6, 512], F32)
    for i, (src, kkey) in enumerate(zip(srcs, keys)):
        pb, pk = getF()
        flat = pb[:].rearrange("p a b -> p (a b)")
        for m in range(4):
            ph.tr(flat[:NS, m * 128:(m + 1) * 128], src[:, m, :], identf[:], R=kkey, W=pk)
        ph.cp(V if i % 2 else "act", tok[:, i, :], flat[:NS, 0:512], R=pk, W="s_tok")
    ph.dma("sp", I["SW"].rearrange("i b f -> b i f"), tok[:], R="s_tok", W="SWd")
    vec = sb("s_vec", [128, 6, 64], F32)
    ph.dma("sp", vec[:], I["SW"].rearrange("i b (h k) -> (b h) i k", h=8), R="SWd", W="s_vec")
    S0 = sb("s_S0", [128, 64, 64], F32)
    ph.dma("act", S0[:].rearrange("p a b -> p (a b)"), I["st_wkv"], W="s_S0")
    tmp = sb("s_tmp", [128, 64, 64], F32)
    sa = sb("s_sa", [128, 64], F32); yv = sb("s_yv", [128, 64], F32); kka = sb("s_kka", [128, 64], F32)
    kB = lambda i: bc(vec[:, i, :].unsqueeze(1), [128, 64, 64])
    ph.tt(V, tmp[:], S0[:], kB(4), ALU.mult, R=["s_S0", "s_vec"], W="s_tmp")
    ph.op(V, lambda e: e.tensor_reduce(out=sa[:], in_=tmp[:], axis=AX.X, op=ALU.add), R="s_tmp", W="s_sa")
    ph.tt(V, S0[:], S0[:], kB(1), ALU.mult, R=["s_S0", "s_vec", "s_tmp"], W="s_S0")
    ph.tt(V, tmp[:], bc(sa[:, :].unsqueeze(2), [128, 64, 64]), kB(5), ALU.mult, R=["s_sa", "s_vec"], W="s_tmp")
    ph.tt(V, S0[:], S0[:], tmp[:], ALU.subtract, R=["s_S0", "s_tmp"], W="s_S0")
    ph.tt(V, tmp[:], bc(vec[:, 3, :].unsqueeze(2), [128, 64, 64]), kB(2), ALU.mult, R=["s_vec", "s_S0"], W="s_tmp")
    ph.tt(V, S0[:], S0[:], tmp[:], ALU.add, R=["s_S0", "s_tmp"], W="s_S0")
    ph.dma("act", I["s_wkv"], S0[:].rearrange("p a b -> p (a b)"), R="s_S0")
    ph.tt(V, tmp[:], S0[:], kB(0), ALU.mult, R=["s_S0", "s_vec"], W="s_tmp")
    ph.op(V, lambda e: e.tensor_reduce(out=yv[:], in_=tmp[:], axis=AX.X, op=ALU.add), R="s_tmp", W="s_yv")
    ph.dma("sp", I["SY"], yv[:], R="s_yv", W="SYd")
    Ysb = L["Ysb"]
    ph.dma("sp", Ysb[:NS].rearrange("p a b -> p (a b)"), I["SY"].rearrange("(b h) v -> b (h v)", h=8), R="SYd", W="Ysb")
    groupnorm_out(ph, L, 0, NS)


def phase3(nc, I, G0, W3, WFI):
    ph = Ph(nc, "p3")
    V = "dve"
    W3 = alloc_w3(nc, ph.st)
    load_w3(ph, I, W3)
    rwo, glu, wo = W3["rwo"], W3["glu"], W3["wo"]
    for k in range(8):
        ph.dma("pool", WFI[:, k, :], I["w_ffn_in"][k * 128:(k + 1) * 128, :], W="wfi_pre")
    yf = ph.sb("yf", [128, 4, 512], BF16); zz = ph.sb("zz", [128, 4, 512], BF16); gt = ph.sb("gt", [128, 16, 512], BF16)
    trw = ph.sb("trw", [128, 8, 512], F32)
    mgs = [ph.sb("mg%d" % i, [128, 8, 512], BF16) for i in range(2)]
    sgb = [ph.sb("sgb%d" % i, [128, 512], F32) for i in range(2)]
    s5t = [ph.sb("s5t%d" % i, [128, 512], F32) for i in range(2)]
    xts = [ph.sb("xt%d" % i, [128, D], F32) for i in range(2)]
    pm = [ph.ps("pm%d" % i, [128, 512], F32) for i in range(6)]
    npm = nx = ns = 0
    SA, SB = [], []
    for bi_, (t0, nt) in enumerate(BLOCKS):
        P = min(128, nt)
        mg = mgs[bi_ % 2]; mk_ = "mg%d" % (bi_ % 2)
        r3 = lambda name: I[name][:, t0:t0 + nt].rearrange("(m p) t -> p m t", p=128)
        ph.rec_begin()
        ph.dma("sp", yf[:, :, :nt], r3("YF"), W="yf"); ph.dma("sp", zz[:, :, :nt], r3("ZZ"), W="zz")
        ph.dma("act", gt[:, :, :nt], r3("GT"), W="gt")
        for m in range(8):
            pb = pm[npm % 6]; pk = "pm%d" % (npm % 6); npm += 1
            for k in range(4):
                ph.mm(pb[:, :nt], rwo[:, k, m * 128:(m + 1) * 128], yf[:, k, :nt], k == 0, k == 3, R=["rwo", "yf"], W=pk)
            ph.tt(V, trw[:, m, :nt], pb[:, :nt], gt[:, m, :nt], ALU.mult, R=[pk, "gt"], W="trw%d" % m)
        late = []
        for m in range(8):
            pa = pm[npm % 6]; pka = "pm%d" % (npm % 6); npm += 1
            pb = pm[npm % 6]; pkb = "pm%d" % (npm % 6); npm += 1
            for k in range(4):
                ph.mm(pa[:, :nt], glu[:, k, m * 128:(m + 1) * 128], zz[:, k, :nt], k == 0, k == 3, R=["glu", "zz"], W=pka)
            for k in range(4):
                ph.mm(pb[:, :nt], glu[:, k, D + m * 128:D + (m + 1) * 128], zz[:, k, :nt], k == 0, k == 3,
                      R=["glu", "zz"], W=pkb)
            sg = sgb[ns % 2]; sk = "sgb%d" % (ns % 2); s5 = s5t[ns % 2]; s5k = "s5t%d" % (ns % 2); ns += 1
            ph.act(sg[:, :nt], pb[:, :nt], AF.Sigmoid, R=pkb, W=sk)
            for fn_ in late:
                fn_()

            def _ep(m=m, s5=s5, pa=pa, sg=sg, pka=pka, sk=sk, s5k=s5k, nt=nt, mg=mg, mk_=mk_):
                ph.tt(V, s5[:, :nt], pa[:, :nt], sg[:, :nt], ALU.mult, R=[pka, sk], W=s5k)
                ph.tt(V, s5[:, :nt], s5[:, :nt], gt[:, 8 + m, :nt], ALU.mult, R=[s5k, "gt"], W=s5k)
                ph.tt(V, mg[:, m, :nt], s5[:, :nt], trw[:, m, :nt], ALU.add, R=[s5k, "trw%d" % m], W=mk_)
            late = [_ep]
        for fn_ in late:
            fn_()
        late = []
        SA.append(ph.rec_end())
        ph.rec_begin()
        for s in range((nt + 127) // 128):
            xt = xts[nx % 2]; xk = "xt%d" % (nx % 2); nx += 1
            rows = slice(t0 + s * 128, t0 + s * 128 + P)
            ph.dma("sp", xt[:P, :], I["xall"][rows, :], W=xk)
            for half in range(2):
                pb = pm[npm % 6]; pk = "pm%d" % (npm % 6); npm += 1
                for k in range(8):
                    ph.mm(pb[:P, :], mg[:, k, s * 128:s * 128 + P], wo[:, k, half * 512:(half + 1) * 512], k == 0, k == 7,
                          R=[mk_, "wo"], W=pk)
                ph.tt(V, xt[:P, half * 512:(half + 1) * 512], xt[:P, half * 512:(half + 1) * 512], pb[:P, :], ALU.add,
                      R=[pk, xk], W=xk)
            ph.dma("pool", I["X1"][rows, :], xt[:P, :], R=xk)
        SB.append(ph.rec_end())
    ph.play(SA[0])
    for b_ in range(len(BLOCKS)):
        if b_ + 1 < len(BLOCKS):
            ph.play(SA[b_ + 1])
        ph.play(SB[b_])
    ph.finish()


def phase4(nc, I, G0, WFI):
    ph = Ph(nc, "p4")
    V = "dve"
    G = norm_scratch(ph, G0)
    identf = G0["identf"]
    wfi = WFI; wfo = ph.sb("wfo", [128, 22, D], BF16)
    for k in range(22):
        ph.dma("pool", wfo[:, k, :], I["w_ffn_out"][k * 128:(k + 1) * 128, :], W="wfo")
    g2c = ph.sb("g2c", [128, 8], F32); load_col(ph, g2c[:], I["ln2_g"], 8, "g2c")
    cw = ph.sb("cw", [128, 3, 22], F32); cb = ph.sb("cb", [128, 22], F32)
    ph.dma("sp", cw[:], I["conv_w"].rearrange("t (f p) -> p t f", p=128), W="cw", slow=True)
    load_col(ph, cb[:], I["conv_b"], 22, "cb")
    hTs = [ph.sb("hT%d" % i, [128, 8, 512], BF16) for i in range(2)]
    hid = ph.sb("hid", [128, 22, 512], BF16)
    xts = [ph.sb("xt%d" % i, [128, D], F32) for i in range(2)]
    At = [ph.sb("At%d" % i, [128, 514], F32) for i in range(2)]
    acc = [ph.sb("acc%d" % i, [128, 512], F32) for i in range(2)]
    cc = ph.sb("cc", [128, 22, 2], F32)
    ph.memset(V, cc[:].rearrange("p a b -> p (a b)"), 0.0, W="cc")
    pm = [ph.ps("pm%d" % i, [128, 512], F32) for i in range(6)]
    scs = ph.sb("scs", [NS, 2816], F32)
    scT = ph.sb("scT", [128, 22, 2, NS], F32)
    aout = scs
    npm = na = 0
    NR, FI, FO = [], [], []
    for bi_, (t0, nt) in enumerate(BLOCKS):
        P = min(128, nt)
        hT = hTs[bi_ % 2]; hk = "hT%d" % (bi_ % 2)
        sample = nt < 128
        nsub = (nt + 127) // 128
        ph.rec_begin()
        for s in range(nsub):
            rows = slice(t0 + s * 128, t0 + s * 128 + P)
            ph.dma("sp", xts[s % 2][:P, :], I["X1"][rows, :], W="xt%d" % (s % 2))
            rms_to_hT(ph, G, xts[s % 2], P, g2c, hT, s * 128, str(s % 2), "g2c", hk)
        NR.append(ph.rec_end())
        ph.rec_begin()
        if sample:
            for tt_ in range(2):
                ph.dma("sp", scs[:], I["st_conv"][:, tt_, :], W="scs")
                for q in range(6):
                    pb = pm[npm % 6]; pk = "pm%d" % (npm % 6); npm += 1
                    fs = list(range(q * 4, min(22, q * 4 + 4)))
                    for j, f_ in enumerate(fs):
                        ph.tr(pb[:, j * NS:(j + 1) * NS], scs[:, f_ * 128:(f_ + 1) * 128], identf[:NS, :NS], R="scs", W=pk)
                    ph.cp(V, scT[:, fs[0]:fs[-1] + 1, tt_, :], pb[:, 0:len(fs) * NS].rearrange("p (a b) -> p a b", b=NS),
                          R=pk, W="scT")
        late = []
        for f in range(22):
            pa = pm[npm % 6]; pka = "pm%d" % (npm % 6); npm += 1
            pb = pm[npm % 6]; pkb = "pm%d" % (npm % 6); npm += 1
            for k in range(8):
                ph.mm(pa[:, :nt], wfi[:, k, f * 128:(f + 1) * 128], hT[:, k, :nt], k == 0, k == 7, R=["wfi", hk], W=pka)
            for k in range(8):
                ph.mm(pb[:, :nt], wfi[:, k, 2816 + f * 128:2816 + (f + 1) * 128], hT[:, k, :nt], k == 0, k == 7,
                      R=["wfi", hk], W=pkb)
            A = At[na % 2]; ak = "At%d" % (na % 2); ac = acc[na % 2]; ck = "acc%d" % (na % 2); na += 1
            ph.cp("act", A[:, 2:2 + nt], pa[:, :nt], R=pka, W=ak)
            if not sample:
                ph.cp(V, A[:, 0:2], cc[:, f, :], R="cc", W=ak)
                a0, a1, a2 = A[:, 0:nt], A[:, 1:1 + nt], A[:, 2:2 + nt]
            else:
                a0, a1, a2 = scT[:, f, 0, :], scT[:, f, 1, :], A[:, 2:2 + nt]
            ph.ts(V, ac[:, :nt], a0, cw[:, 0, f:f + 1], ALU.mult, cb[:, f:f + 1], ALU.add, R=[ak, "scT", "cw", "cb"], W=ck)
            ph.stt(ac[:, :nt], a1, cw[:, 1, f:f + 1], ac[:, :nt], ALU.mult, ALU.add, R=[ak, "scT", "cw", ck], W=ck)
            ph.stt(ac[:, :nt], a2, cw[:, 2, f:f + 1], ac[:, :nt], ALU.mult, ALU.add, R=[ak, "cw", ck], W=ck)
            ph.act(ac[:, :nt], ac[:, :nt], AF.Gelu_apprx_tanh, R=ck, W=ck)
            for fn_ in late:
                fn_()
            late = [(lambda f=f, ac=ac, pb=pb, ck=ck, pkb=pkb, nt=nt:
                     ph.tt(V, hid[:, f, :nt], ac[:, :nt], pb[:, :nt], ALU.mult, R=[ck, pkb], W="hid"))]
            if not sample:
                ph.cp(V, cc[:, f, :], A[:, nt:nt + 2], R=ak, W="cc")
            else:
                po = pm[npm % 6]; pko = "pm%d" % (npm % 6); npm += 1
                ph.tr(po[:NS, 0:128], A[:, 2:2 + NS], identf[:], R=ak, W=pko)
                ph.cp(V, aout[:, f * 128:(f + 1) * 128], po[:NS, 0:128], R=pko, W="scs")
        for fn_ in late:
            fn_()
        late = []
        if t0 + nt == T:
            for tt_ in range(2):
                ph.dma("sp", I["p_conv"][tt_].rearrange("(f p) -> p f", p=128), cc[:, :, tt_], R="cc", slow=True)
        if sample:
            ph.dma("sp", I["s_conv"][:, 1, :], aout[:], R="scs")
            ph.dma("act", I["s_conv"][:, 0, :], I["st_conv"][:, 1, :])
        FI.append(ph.rec_end())
        ph.rec_begin()
        for s in range(nsub):
            rows = slice(t0 + s * 128, t0 + s * 128 + P)
            xt = xts[s % 2]; xk = "xt%d" % (s % 2)
            ph.dma("sp", xt[:P, :], I["X1"][rows, :], W=xk)
            for half in range(2):
                pb = pm[npm % 6]; pk = "pm%d" % (npm % 6); npm += 1
                for f in range(22):
                    ph.mm(pb[:P, :], hid[:, f, s * 128:s * 128 + P], wfo[:, f, half * 512:(half + 1) * 512], f == 0, f == 21,
                          R=["hid", "wfo"], W=pk)
                ph.tt(V, xt[:P, half * 512:(half + 1) * 512], xt[:P, half * 512:(half + 1) * 512], pb[:P, :],
                      ALU.add, R=[pk, xk], W=xk)
            ph.dma("pool", I["X2"][rows, :], xt[:P, :], R=xk)
        FO.append(ph.rec_end())
    ph.play(NR[0])
    for b_ in range(len(BLOCKS)):
        if STRICT4:
            if b_ + 1 < len(BLOCKS):
                ph.play(NR[b_ + 1])
            ph.play(FI[b_])
        else:
            ph.play(FI[b_], NR[b_ + 1] if b_ + 1 < len(BLOCKS) else [])
        ph.play(FO[b_])
    ph.finish()


def phase5(nc, I, G0):
    ph = Ph(nc, "p5")
    V = "dve"
    NB = 4
    Gs = [norm_scratch(ph, G0, "a")]
    for i in range(1, NB):
        Gs.append(norm_scratch(ph, G0, "abcd"[i], eps=Gs[0]["eps"]))
    wpg = ph.sb("wpg", [128, 8, D], BF16); wpl = ph.sb("wpl", [128, 2, D], BF16)
    for k in range(8):
        ph.dma("pool", wpg[:, k, :], I["w_ple_gate"][k * 128:(k + 1) * 128, :], W="wpg")
    for k in range(2):
        ph.dma("pool", wpl[:, k, :], I["w_ple"][k * 128:(k + 1) * 128, :], W="wpl")
    g3c = ph.sb("g3c", [128, 8], F32); load_col(ph, g3c[:], I["ln3_g"], 8, "g3c")
    fg = ph.sb("fg", [128, D], F32)
    ph.dma("sp", fg[:], I["final_g"].partition_broadcast(128), W="fg")
    hTs = [ph.sb("hT%d" % i, [128, 8, 128], BF16) for i in range(NB)]
    xts = [ph.sb("xt%d" % i, [128, D], F32) for i in range(NB)]
    sg = [ph.sb("sg%d" % i, [128, 512], F32) for i in range(NB)]
    yo = [ph.sb("yo%d" % i, [128, D], F32) for i in range(NB)]
    pm = [ph.ps("pm%d" % i, [128, 512], F32) for i in range(3)]
    pq = ph.ps("pq", [128, 8, 128], BF16)
    subs = []
    for (t0, nt) in BLOCKS:
        P = min(128, nt)
        for s in range((nt + 127) // 128):
            subs.append((t0 + s * 128, P))
    NSUB = len(subs)
    pball = ph.sb("pball", [128, NSUB, 256], BF16)
    pTall = ph.sb("pTall", [128, NSUB, 2, 128], BF16)
    for i, (r0, P) in enumerate(subs):
        ph.dma("pool", pball[:P, i, :], I["pall"][r0:r0 + P, :], W="pb%d" % i)
    for i0_ in range(0, NSUB, 4):
        grp = list(range(i0_, min(NSUB, i0_ + 4)))
        for j, i in enumerate(grp):
            P = subs[i][1]
            for k in range(2):
                ph.tr(pq[:, 2 * j + k, :P], pball[:P, i, k * 128:(k + 1) * 128], G0["identb"][:P, :P],
                      R=["pb%d" % i, "identb"], W="pq")
        for j, i in enumerate(grp):
            P = subs[i][1]
            ph.cp("act", pTall[:, i, :, :P], pq[:, 2 * j:2 * j + 2, :P], R="pq", W="pT%d" % i)
    npm = nsg = 0
    FR, BK = [], []
    for i, (r0, P) in enumerate(subs):
        rows = slice(r0, r0 + P)
        i2 = i % NB
        xt = xts[i2]; xk = "xt%d" % i2
        G = Gs[i2]; hT = hTs[i2]; hk = "hT%d" % i2
        ph.rec_begin()
        ph.dma("sp", xt[:P, :], I["X2"][rows, :], W=xk)
        rms_to_hT(ph, G, xt, P, g3c, hT, 0, str(i2), "g3c", hk)
        FR.append(ph.rec_end())
        ph.rec_begin()
        late5 = []
        for half in range(2):
            cs_ = slice(half * 512, (half + 1) * 512)
            pg = pm[npm % 3]; pgk = "pm%d" % (npm % 3); npm += 1
            pe = pm[npm % 3]; pek = "pm%d" % (npm % 3); npm += 1
            for k in range(8):
                ph.mm(pg[:P, :], hT[:, k, :P], wpg[:, k, cs_], k == 0, k == 7, R=[hk, "wpg"], W=pgk)
            for k in range(2):
                ph.mm(pe[:P, :], pTall[:, i, k, :P], wpl[:, k, cs_], k == 0, k == 1, R=["pT%d" % i, "wpl"], W=pek)
            sgt = sg[nsg % NB]; sgk = "sg%d" % (nsg % NB); nsg += 1
            ph.act(sgt[:P, :], pg[:P, :], AF.Sigmoid, R=pgk, W=sgk)
            for fn_ in late5:
                fn_()

            def _ep5(sgt=sgt, pe=pe, sgk=sgk, pek=pek, cs_=cs_, xt=xt, xk=xk, P=P, G=G):
                ph.tt(V, sgt[:P, :], sgt[:P, :], pe[:P, :], ALU.mult, R=[sgk, pek], W=sgk)
                ph.tt(V, xt[:P, cs_], xt[:P, cs_], sgt[:P, :], ALU.add, R=[sgk, xk, "xn" + G["sx"]], W=xk)
            late5 = [_ep5]
        for fn_ in late5:
            fn_()
        late5 = []
        ss = G["ss"]; sq = G["sq"]; kss = "ss" + G["sx"]; ksq = "sq" + G["sx"]
        ph.act(sq[:P, :], xt[:P, :], AF.Square, R=xk, W=[ksq, kss], accum=ss[:P, 0:1])
        ph.act(ss[:P, 1:2], ss[:P, 0:1], AF.Sqrt, R=[kss, "eps"], W=kss, bias=G["eps"][:P, 0:1], scale=1.0 / D)
        ph.op(V, lambda e, ss=ss, P=P: e.reciprocal(out=ss[:P, 3:4], in_=ss[:P, 1:2]), R=kss, W=kss + "3")
        y = yo[i2]; yk = "yo%d" % i2
        ph.stt(y[:P, :], xt[:P, :], ss[:P, 3:4], fg[:P, :], ALU.mult, ALU.mult, R=[xk, kss + "3", "fg"], W=yk)
        ph.dma("pool", I["y"][rows, :], y[:P, :], R=yk)
        BK.append(ph.rec_end())
    AHEAD = 2
    for i in range(min(AHEAD, NSUB)):
        ph.play(FR[i])
    for i in range(NSUB):
        if i + AHEAD < NSUB:
            ph.play(FR[i + AHEAD])
        ph.play(BK[i])
    ph.finish()


_CACHE = {}


def _consts():
    i = np.arange(128)
    c = {}
    c["c_ident"] = np.eye(128, dtype=np.float32)
    c["c_msl"] = (i[None, :] < i[:, None]).astype(np.float32)
    c["c_msu"] = (i[:, None] < i[None, :]).astype(np.float32)
    c["c_mui"] = (i[:, None] <= i[None, :]).astype(np.float32)
    c["c_blk64"] = ((i[:, None] // 64) == (i[None, :] // 64)).astype(np.float32)
    c["c_blk32"] = ((i[:, None] // 32) == (i[None, :] // 32)).astype(np.float32)
    c["c_rowgp"] = (((i[:, None] // 16) % 2) == (i[None, :] // 64)).astype(np.float32)
    return c


def make_in_maps(inp):
    f = lambda a: np.ascontiguousarray(np.asarray(a, dtype=np.float32))
    cst = _consts()
    shared = {}
    for k in ("ln1_g", "w_in", "mu_shift", "w0", "w2", "a0", "a2", "g2", "k_k", "k_a", "lnx_g", "lnx_b", "w_rw_out",
              "A_re", "A_im", "log_dt", "B_re", "B_im", "D_skip", "w_glu", "w_out", "ln2_g", "w_ffn_in", "conv_w",
              "conv_b", "w_ffn_out", "ln3_g", "w_ple_gate", "w_ple"):
        shared[k] = f(inp[k])[0]
    shared["r_k"] = f(inp["r_k"])[0].reshape(512)
    shared["C_re"] = f(inp["C_re"])[0].reshape(512, 64)
    shared["C_im"] = f(inp["C_im"])[0].reshape(512, 64)
    shared["final_g"] = f(inp["final_g"])
    shared.update(cst)
    xp, xs = f(inp["x_prompt"]), f(inp["x_sample"])
    pp, psm = f(inp["p_prompt"])[0], f(inp["p_sample"])[0]
    in_maps = []
    for c in range(8):
        sl = slice(NS * c, NS * c + NS)
        m = dict(shared)
        m["xall"] = np.concatenate([xp[c], xs[sl, 0]], 0)
        m["pall"] = np.concatenate([pp[c], psm[sl, 0]], 0)
        m["st_shift"] = f(inp["state_shift"])[0, sl]
        m["st_wkv"] = f(inp["state_wkv"])[0, sl].reshape(128, 4096)
        m["st_re"] = f(inp["state_ssm_re"])[0, sl].reshape(NS, 2048)
        m["st_im"] = f(inp["state_ssm_im"])[0, sl].reshape(NS, 2048)
        m["st_conv"] = f(inp["state_conv"])[0, sl]
        in_maps.append({k: np.ascontiguousarray(v) for k, v in m.items()})
    return in_maps


def kernel(**inp):
    f = lambda a: np.ascontiguousarray(np.asarray(a, dtype=np.float32))
    if "nc" not in _CACHE:
        _CACHE["nc"] = build_program()
    nc = _CACHE["nc"]
    in_maps = make_in_maps(inp)
    res = run_bass_kernel_spmd(nc, in_maps, core_ids=list(range(8)))
    R = res.results
    cat = lambda fn: np.stack([fn(r) for r in R], 0)
    y_prompt = cat(lambda r: r["y"][:T])
    y_sample = np.concatenate([r["y"][T:] for r in R], 0)[:, None, :]
    p_shift = cat(lambda r: r["p_shift"])[None]
    p_wkv = cat(lambda r: r["p_wkv"].reshape(8, 64, 64).transpose(0, 2, 1))[None]
    p_re = cat(lambda r: r["p_re"].reshape(32, 64))[None]
    p_im = cat(lambda r: r["p_im"].reshape(32, 64))[None]
    p_conv = cat(lambda r: r["p_conv"])[None]
    s_shift = np.concatenate([r["s_shift"] for r in R], 0)[None]
    s_wkv = np.concatenate([r["s_wkv"].reshape(NS, 8, 64, 64) for r in R], 0)[None]
    s_re = np.concatenate([r["s_re"].reshape(NS, 32, 64) for r in R], 0)[None]
    s_im = np.concatenate([r["s_im"].reshape(NS, 32, 64) for r in R], 0)[None]
    s_conv = np.concatenate([r["s_conv"] for r in R], 0)[None]
    outs = (y_prompt, y_sample, p_shift, p_wkv, p_re, p_im, p_conv, s_shift, s_wkv, s_re, s_im, s_conv)
    return tuple(np.ascontiguousarray(o.astype(np.float32)) for o in outs)
```
